# Optimizing a Trainium2 kernel written in Bass

```python
import math
import jax
import jax.numpy as jnp
from jax import lax
import numpy as np

D_MODEL = 1024
BATCH = 4
SEQ = 4096
DEPTH = 1
DEC_BATCH = 128
DEC_SEQ = 8
PAST_LEN = 8192
PAGE_SIZE = 128

HEAD_DIM = 64
ATT_WIDTH = D_MODEL // 2
ATT_HEADS = ATT_WIDTH // HEAD_DIM
ATT_KV_HEADS = ATT_HEADS // 4
ATT_GROUP = ATT_HEADS // ATT_KV_HEADS
KV_WIDTH = ATT_KV_HEADS * HEAD_DIM
WINDOW = 128
MLSTM_WIDTH = D_MODEL - ATT_WIDTH
MLSTM_DV = 128
MLSTM_HEADS = MLSTM_WIDTH // MLSTM_DV
MLSTM_DK = MLSTM_DV // 2
MQK_WIDTH = MLSTM_HEADS * MLSTM_DK
MLSTM_CHUNK = 64
MIX_WIDTH = ATT_WIDTH + MLSTM_WIDTH
IN_WIDTH = ATT_WIDTH + 2 * KV_WIDTH + 2 * MQK_WIDTH + 2 * MLSTM_WIDTH + 2 * MLSTM_HEADS
MEM_TOKENS = 256
CROSS_HEADS = 4
CROSS_HEAD_DIM = 64
CROSS_WIDTH = CROSS_HEADS * CROSS_HEAD_DIM
FFN_HIDDEN = -(-(8 * D_MODEL) // (3 * 256)) * 256
FORGET_BIAS = 3.0
EPS = 1e-6

kernel_name = "hymba_swa_sink_mlstm_decoder_step"


def rmsnorm(x, g):
    xf = x.astype(jnp.float32)
    y = xf * lax.rsqrt(jnp.mean(xf * xf, axis=-1, keepdims=True) + EPS)
    return (y * g.astype(jnp.float32)).astype(x.dtype)


def project_mixers(xn, w_in, b_igate, b_fgate):
    B, T, _ = xn.shape
    f32 = jnp.float32
    z = jnp.einsum('btd,de->bte', xn, w_in)
    sizes = (ATT_WIDTH, KV_WIDTH, KV_WIDTH, MQK_WIDTH, MQK_WIDTH, MLSTM_WIDTH, MLSTM_WIDTH,
             MLSTM_HEADS, MLSTM_HEADS)
    points = []
    acc = 0
    for s in sizes[:-1]:
        acc += s
        points.append(acc)
    q, k, v, mq, mk, mv, og, ig, fg = jnp.split(z, points, axis=-1)
    q = q.reshape(B, T, ATT_KV_HEADS, ATT_GROUP, HEAD_DIM)
    k = k.reshape(B, T, ATT_KV_HEADS, HEAD_DIM)
    v = v.reshape(B, T, ATT_KV_HEADS, HEAD_DIM)
    mq = mq.astype(f32).reshape(B, T, MLSTM_HEADS, MLSTM_DK)
    mk = mk.astype(f32).reshape(B, T, MLSTM_HEADS, MLSTM_DK) * (MLSTM_DK ** -0.5)
    mv = mv.astype(f32).reshape(B, T, MLSTM_HEADS, MLSTM_DV)
    ig = ig.astype(f32) + b_igate.astype(f32)
    lf = jax.nn.log_sigmoid(fg.astype(f32) + b_fgate.astype(f32))
    return q, k, v, mq, mk, mv, og, ig, lf


def window_mask(qpos, kpos):
    d = qpos[:, :, None] - kpos[:, None, :]
    return (d >= 0) & (d <= WINDOW)


def attend_with_sinks(q, k, v, mask, sinks):
    s = jnp.einsum('bnqhgd,bnkhd->bnhgqk', q, k).astype(jnp.float32) * (HEAD_DIM ** -0.5)
    s = jnp.where(mask[None, :, None, None], s, -jnp.inf)
    sink = sinks.astype(jnp.float32).reshape(1, 1, ATT_KV_HEADS, ATT_GROUP, 1, 1)
    m = jnp.maximum(jnp.max(s, axis=-1, keepdims=True), sink)
    p = jnp.exp(s - m)
    p = p / (jnp.sum(p, axis=-1, keepdims=True) + jnp.exp(sink - m))
    return jnp.einsum('bnhgqk,bnkhd->bnqhgd', p.astype(v.dtype), v)


def swa_prompt(q, k, v, sinks):
    B, T = q.shape[:2]
    nb = T // WINDOW
    qb = q.reshape(B, nb, WINDOW, ATT_KV_HEADS, ATT_GROUP, HEAD_DIM)

    def band(a):
        ab = a.reshape(B, nb, WINDOW, ATT_KV_HEADS, HEAD_DIM)
        prev = jnp.pad(ab[:, :-1], ((0, 0), (1, 0), (0, 0), (0, 0), (0, 0)))
        return jnp.concatenate([prev, ab], axis=2)

    start = jnp.arange(nb)[:, None] * WINDOW
    qpos = start + jnp.arange(WINDOW)[None, :]
    kpos = start - WINDOW + jnp.arange(2 * WINDOW)[None, :]
    mask = window_mask(qpos, kpos) & (kpos >= 0)[:, None, :]
    o = attend_with_sinks(qb, band(k), band(v), mask, sinks)
    return o.reshape(B, T, ATT_WIDTH)


def swa_sample(q, k, v, buf_k, buf_v, sinks):
    B, T = q.shape[:2]
    P = buf_k.shape[1]
    kc = jnp.concatenate([buf_k.astype(k.dtype), k], axis=1)
    vc = jnp.concatenate([buf_v.astype(v.dtype), v], axis=1)
    qpos = jnp.arange(T)[None, :]
    kpos = (jnp.arange(P + T) - P)[None, :]
    mask = window_mask(qpos, kpos)
    o = attend_with_sinks(q[:, None], kc[:, None], vc[:, None], mask, sinks)
    return o.reshape(B, T, ATT_WIDTH), kc[:, T:], vc[:, T:]


def mlstm_chunk(carry, inp):
    C, n, m = carry
    q, k, v, ig, lf = inp
    L = q.shape[1]
    bt = jnp.swapaxes(jnp.cumsum(lf, axis=1), 1, 2)
    it = jnp.swapaxes(ig, 1, 2)
    causal = jnp.tril(jnp.ones((L, L), dtype=bool))
    D = jnp.where(causal, bt[..., :, None] - bt[..., None, :] + it[..., None, :], -jnp.inf)
    inter = bt + m[..., None]
    mt = jnp.maximum(jnp.max(D, axis=-1), inter)
    S = jnp.einsum('blhd,bshd->bhls', q, k) * jnp.exp(D - mt[..., None])
    wi = jnp.exp(inter - mt)
    num = (jnp.einsum('bhls,bshe->blhe', S, v)
           + jnp.swapaxes(wi, 1, 2)[..., None] * jnp.einsum('blhd,bhed->blhe', q, C))
    den = jnp.sum(S, axis=-1) + wi * jnp.einsum('blhd,bhd->bhl', q, n)
    lower = jnp.maximum(jnp.abs(den), jnp.exp(-mt))
    h = num / jnp.swapaxes(lower, 1, 2)[..., None]
    bL = bt[..., -1]
    m_new = mt[..., -1]
    wk = jnp.exp(bL[..., None] - bt + it - m_new[..., None])
    wc = jnp.exp(bL + m - m_new)
    C_new = wc[..., None, None] * C + jnp.einsum('bhs,bshe,bshd->bhed', wk, v, k)
    n_new = wc[..., None] * n + jnp.einsum('bhs,bshd->bhd', wk, k)
    return (C_new, n_new, m_new), h


def mlstm_prompt(mq, mk, mv, ig, lf):
    B, T = mq.shape[:2]
    nc = T // MLSTM_CHUNK

    def chunks(a):
        return jnp.moveaxis(a.reshape(B, nc, MLSTM_CHUNK, *a.shape[2:]), 1, 0)

    init = (jnp.zeros((B, MLSTM_HEADS, MLSTM_DV, MLSTM_DK), jnp.float32),
            jnp.zeros((B, MLSTM_HEADS, MLSTM_DK), jnp.float32),
            jnp.zeros((B, MLSTM_HEADS), jnp.float32))
    (C, n, m), h = lax.scan(mlstm_chunk, init,
                            (chunks(mq), chunks(mk), chunks(mv), chunks(ig), chunks(lf)))
    h = jnp.moveaxis(h, 0, 1).reshape(B, T, MLSTM_HEADS, MLSTM_DV)
    return h, C, n, m


def merge_mixers(att, h, og, g_head, w_out):
    B, T = h.shape[:2]
    hn = h * lax.rsqrt(jnp.mean(h * h, axis=-1, keepdims=True) + EPS)
    hm = hn.reshape(B, T, MLSTM_WIDTH) * g_head.astype(jnp.float32) * jax.nn.sigmoid(og.astype(jnp.float32))
    cat = jnp.concatenate([att, hm.astype(att.dtype)], axis=-1)
    return jnp.einsum('bte,ed->btd', cat, w_out)


def memory_kv(mem, g_mem, w_ck, w_cv):
    B, M, _ = mem.shape
    mn = rmsnorm(mem, g_mem)
    k = jnp.einsum('bmd,de->bme', mn, w_ck).reshape(B, M, CROSS_HEADS, CROSS_HEAD_DIM)
    v = jnp.einsum('bmd,de->bme', mn, w_cv).reshape(B, M, CROSS_HEADS, CROSS_HEAD_DIM)
    return k, v


def cross_attend(xn, mk, mv, w_cq, w_co):
    B, T, _ = xn.shape
    q = jnp.einsum('btd,de->bte', xn, w_cq).reshape(B, T, CROSS_HEADS, CROSS_HEAD_DIM)
    s = jnp.einsum('bthd,bmhd->bhtm', q, mk.astype(q.dtype)).astype(jnp.float32) * (CROSS_HEAD_DIM ** -0.5)
    p = jax.nn.softmax(s, axis=-1)
    o = jnp.einsum('bhtm,bmhd->bthd', p.astype(q.dtype), mv.astype(q.dtype))
    return jnp.einsum('bte,ed->btd', o.reshape(B, T, CROSS_WIDTH), w_co)


def swiglu(xn, w_gate, w_up, w_down):
    g = jnp.einsum('btd,df->btf', xn, w_gate)
    u = jnp.einsum('btd,df->btf', xn, w_up)
    return jnp.einsum('btf,fd->btd', jax.nn.silu(g) * u, w_down)


def setup_inputs(seed: int = 0) -> dict:
    key = jax.random.key(seed)
    ks = jax.random.split(key, 32)
    f32 = jnp.float32

    def nrm(k, shape, scale):
        return jax.random.normal(k, shape, f32) * scale

    buf = min(WINDOW, PAST_LEN)
    D = D_MODEL
    return {
        "x_prompt": nrm(ks[0], (BATCH, SEQ, D), 1.0),
        "x_sample": nrm(ks[1], (DEC_BATCH, DEC_SEQ, D), 1.0),
        "mem_prompt": nrm(ks[2], (BATCH, MEM_TOKENS, D), 1.0),
        "cache_swa_k": nrm(ks[3], (DEPTH, DEC_BATCH, buf, ATT_KV_HEADS, HEAD_DIM), 1.0),
        "cache_swa_v": nrm(ks[4], (DEPTH, DEC_BATCH, buf, ATT_KV_HEADS, HEAD_DIM), 1.0),
        "state_mlstm_C": nrm(ks[5], (DEPTH, DEC_BATCH, MLSTM_HEADS, MLSTM_DV, MLSTM_DK), 0.3),
        "state_mlstm_n": nrm(ks[6], (DEPTH, DEC_BATCH, MLSTM_HEADS, MLSTM_DK), 0.3),
        "state_mlstm_m": nrm(ks[7], (DEPTH, DEC_BATCH, MLSTM_HEADS), 1.0),
        "cache_mem_k": nrm(ks[8], (DEPTH, DEC_BATCH, MEM_TOKENS, CROSS_HEADS, CROSS_HEAD_DIM), 1.0),
        "cache_mem_v": nrm(ks[9], (DEPTH, DEC_BATCH, MEM_TOKENS, CROSS_HEADS, CROSS_HEAD_DIM), 1.0),
        "w_in": nrm(ks[10], (DEPTH, D, IN_WIDTH), D ** -0.5),
        "b_igate": nrm(ks[11], (DEPTH, MLSTM_HEADS), 0.1),
        "b_fgate": FORGET_BIAS + nrm(ks[12], (DEPTH, MLSTM_HEADS), 0.1),
        "attn_sinks": nrm(ks[13], (DEPTH, ATT_HEADS), 0.5),
        "g_mlstm_head": 1.0 + nrm(ks[14], (DEPTH, MLSTM_WIDTH), 0.05),
        "w_out": nrm(ks[15], (DEPTH, MIX_WIDTH, D), MIX_WIDTH ** -0.5),
        "g_mix": 1.0 + nrm(ks[16], (DEPTH, D), 0.05),
        "g_cross": 1.0 + nrm(ks[17], (DEPTH, D), 0.05),
        "g_mem": 1.0 + nrm(ks[18], (DEPTH, D), 0.05),
        "w_cq": nrm(ks[19], (DEPTH, D, CROSS_WIDTH), D ** -0.5),
        "w_ck": nrm(ks[20], (DEPTH, D, CROSS_WIDTH), D ** -0.5),
        "w_cv": nrm(ks[21], (DEPTH, D, CROSS_WIDTH), D ** -0.5),
        "w_co": nrm(ks[22], (DEPTH, CROSS_WIDTH, D), CROSS_WIDTH ** -0.5),
        "g_ffn": 1.0 + nrm(ks[23], (DEPTH, D), 0.05),
        "w_gate": nrm(ks[24], (DEPTH, D, FFN_HIDDEN), D ** -0.5),
        "w_up": nrm(ks[25], (DEPTH, D, FFN_HIDDEN), D ** -0.5),
        "w_down": nrm(ks[26], (DEPTH, FFN_HIDDEN, D), FFN_HIDDEN ** -0.5),
        "g_final": 1.0 + nrm(ks[27], (D,), 0.05),
    }


def reference(x_prompt, x_sample, mem_prompt, cache_swa_k, cache_swa_v, state_mlstm_C,
              state_mlstm_n, state_mlstm_m, cache_mem_k, cache_mem_v, w_in, b_igate, b_fgate,
              attn_sinks, g_mlstm_head, w_out, g_mix, g_cross, g_mem, w_cq, w_ck, w_cv, w_co,
              g_ffn, w_gate, w_up, w_down, g_final):
    f32 = jnp.float32
    yp, ys = x_prompt, x_sample
    skp, svp, Cp, np_, mp, mkp, mvp = [], [], [], [], [], [], []
    sks, svs, Cs, ns, ms = [], [], [], [], []
    for l in range(DEPTH):
        q, k, v, mq, mk, mv, og, ig, lf = project_mixers(rmsnorm(yp, g_mix[l]), w_in[l], b_igate[l], b_fgate[l])
        att = swa_prompt(q, k, v, attn_sinks[l])
        h, C, n, m = mlstm_prompt(mq, mk, mv, ig, lf)
        yp = yp + merge_mixers(att, h, og, g_mlstm_head[l], w_out[l])
        skp.append(k[:, -WINDOW:])
        svp.append(v[:, -WINDOW:])
        Cp.append(C)
        np_.append(n)
        mp.append(m)
        q, k, v, mq, mk, mv, og, ig, lf = project_mixers(rmsnorm(ys, g_mix[l]), w_in[l], b_igate[l], b_fgate[l])
        att, kbuf, vbuf = swa_sample(q, k, v, cache_swa_k[l], cache_swa_v[l], attn_sinks[l])
        carry = (state_mlstm_C[l].astype(f32), state_mlstm_n[l].astype(f32), state_mlstm_m[l].astype(f32))
        (C, n, m), h = mlstm_chunk(carry, (mq, mk, mv, ig, lf))
        ys = ys + merge_mixers(att, h, og, g_mlstm_head[l], w_out[l])
        sks.append(kbuf)
        svs.append(vbuf)
        Cs.append(C)
        ns.append(n)
        ms.append(m)
        memk, memv = memory_kv(mem_prompt, g_mem[l], w_ck[l], w_cv[l])
        yp = yp + cross_attend(rmsnorm(yp, g_cross[l]), memk, memv, w_cq[l], w_co[l])
        ys = ys + cross_attend(rmsnorm(ys, g_cross[l]), cache_mem_k[l], cache_mem_v[l], w_cq[l], w_co[l])
        mkp.append(memk)
        mvp.append(memv)
        yp = yp + swiglu(rmsnorm(yp, g_ffn[l]), w_gate[l], w_up[l], w_down[l])
        ys = ys + swiglu(rmsnorm(ys, g_ffn[l]), w_gate[l], w_up[l], w_down[l])
    y_prompt = rmsnorm(yp, g_final)
    y_sample = rmsnorm(ys, g_final)
    return (y_prompt, y_sample,
            jnp.stack(skp), jnp.stack(svp), jnp.stack(Cp), jnp.stack(np_), jnp.stack(mp),
            jnp.stack(mkp), jnp.stack(mvp),
            jnp.stack(sks), jnp.stack(svs), jnp.stack(Cs), jnp.stack(ns), jnp.stack(ms))
```

```python
import contextlib
from concourse.bass_utils import run_bass_kernel_spmd
import numpy as np
import concourse.bass as bass
import concourse.mybir as mybir

F32 = mybir.dt.float32
BF16 = mybir.dt.bfloat16
I32 = mybir.dt.int32
AF = mybir.ActivationFunctionType
ALU = mybir.AluOpType
AX = mybir.AxisListType

ENGS = ("pe", "act", "dve", "pool", "sp")


class Res:
    __slots__ = ("name", "last_w", "readers", "sem", "dcount", "excl")

    def __init__(self, name):
        self.name = name
        self.last_w = None
        self.readers = []
        self.sem = None
        self.dcount = 0
        self.excl = False


class Op:
    __slots__ = ("eng", "fn", "deps", "dma_res", "sig", "cnt", "k", "group")

    def __init__(self, eng, fn, dma_res):
        self.eng = eng
        self.fn = fn
        self.deps = set()
        self.dma_res = dma_res
        self.sig = False
        self.cnt = 0
        self.k = 0


class Prog:
    def __init__(self, nc):
        self.nc = nc
        self.ops = []
        self.nres = 0
        self.inherit = []
        self.phase_res = []

    def res(self, name=None, arena=False):
        self.nres += 1
        r = Res(name or f"r{self.nres}")
        if arena:
            r.readers = list(self.inherit)
            self.phase_res.append(r)
        return r

    def new_phase(self):
        inh = set(self.inherit)
        for r in self.phase_res:
            if r.last_w is not None:
                inh.add(r.last_w)
            inh.update(r.readers)
        self.inherit = sorted(inh)
        self.phase_res = []

    def op(self, eng, fn, reads=(), writes=(), dma_res=None, accum=False, group=False):
        i = len(self.ops)
        o = Op(eng, fn, dma_res)
        for r in reads:
            if r.last_w is not None:
                o.deps.add(r.last_w)
            if r.excl:
                for q in r.readers:
                    if self.ops[q].eng != eng:
                        o.deps.add(q)
            r.readers.append(i)
        for r in writes:
            if r.last_w is not None:
                lw = self.ops[r.last_w]
                if group and lw.dma_res is not None and lw.dma_res is dma_res:
                    o.deps |= lw.deps
                elif not (accum and lw.eng == "pe" and eng == "pe"):
                    o.deps.add(r.last_w)
            latest = {}
            for q in r.readers:
                if q == i:
                    continue
                oq = self.ops[q]
                if oq.dma_res is not None:
                    o.deps.add(q)
                elif latest.get(oq.eng, -1) < q:
                    latest[oq.eng] = q
            o.deps.update(latest.values())
            r.last_w = i
            r.readers = []
        if eng == "pe":
            o.deps = {d for d in o.deps if self.ops[d].eng != "pe" or self.ops[d].dma_res is not None}
        self.ops.append(o)
        return i

    def dma(self, eng, out, in_, res, reads=(), writes=(), group=False, **kw):
        kw = dict(kw); kw["out"] = out; kw["in_"] = in_
        return self.op(eng, ("dma_start", kw), reads=reads, writes=writes, dma_res=res, group=group)

    def I(self, eng, name, reads=(), writes=(), **kw):
        return self.op(eng, (name, kw), reads=reads, writes=writes)

    def emit(self, final_wait_all=True):
        nc = self.nc
        ops = self.ops
        for o in ops:
            for d in o.deps:
                ops[d].sig = True
        per_eng = {e: [] for e in ENGS}
        for i, o in enumerate(ops):
            per_eng[o.eng].append(i)
        import contextlib
        with contextlib.ExitStack() as st:
            esem = {e: st.enter_context(nc.semaphore(f"s_{e}")) for e in ENGS}
            ecount = {e: 0 for e in ENGS}
            dma_sems = []
            for i, o in enumerate(ops):
                if o.dma_res is not None:
                    r = o.dma_res
                    if r.sem is None:
                        r.sem = st.enter_context(nc.semaphore(f"d{len(dma_sems)}_{r.name}"))
                        dma_sems.append(r)
                    r.dcount += 1
                    o.cnt = 16 * r.dcount
                elif o.sig:
                    ecount[o.eng] += 1
                    o.cnt = ecount[o.eng]
            self.n_dma_sems = len(dma_sems)
            know = {e: {} for e in ENGS}
            know_issue = [None] * len(ops)

            def key_of(o):
                return ("d", id(o.dma_res)) if o.dma_res is not None else ("e", o.eng)

            block = st.enter_context(nc.Block())
            handles = {}

            plan = [None] * len(ops)
            for i, o in enumerate(ops):
                kn = know[o.eng]
                need = {}
                for d in o.deps:
                    p = ops[d]
                    k = key_of(p)
                    if kn.get(k, 0) >= p.cnt:
                        continue
                    if need.get(k, (0, None))[0] < p.cnt:
                        need[k] = (p.cnt, d)
                waits = []
                for k, (cnt, d) in need.items():
                    p = ops[d]
                    sem = p.dma_res.sem if p.dma_res is not None else esem[p.eng]
                    waits.append((sem, cnt))
                    kn[k] = max(kn.get(k, 0), cnt)
                    ki = know_issue[d]
                    for kk, vv in ki.items():
                        if kn.get(kk, 0) < vv:
                            kn[kk] = vv
                know_issue[i] = dict(kn)
                plan[i] = waits
            self.n_waits = sum(len(w) for w in plan)

            def make(ename):
                def body(eh):
                    for i in per_eng[ename]:
                        o = ops[i]
                        for sem, cnt in plan[i]:
                            eh.wait_ge(sem, cnt)
                        ins = getattr(eh, o.fn[0])(**o.fn[1])
                        if o.dma_res is not None:
                            ins.then_inc(o.dma_res.sem, 16)
                        elif o.sig:
                            ins.then_inc(esem[o.eng], 1)
                    if ename == "sp" and final_wait_all:
                        for r in dma_sems:
                            eh.wait_ge(r.sem, 16 * r.dcount)
                        for e in ("pe", "act", "dve", "pool"):
                            if ecount[e]:
                                eh.wait_ge(esem[e], ecount[e])
                return body

            block.tensor(make("pe"))
            block.scalar(make("act"))
            block.vector(make("dve"))
            block.gpsimd(make("pool"))
            block.sync(make("sp"))


D = 1024
FH = 2816
EPS = 1e-6
NTP = 16
NT = 17
GT = 2
NEG = -30000.0
DBG_G0 = 2


def build_program(stage=3, debug=False):
    nc = bass.Bass("TRN2", target_bir_lowering=False)
    P = Prog(nc)

    def din(name, shape, dt=F32):
        return nc.dram_tensor(name, list(shape), dt, kind="ExternalInput").ap()

    def dout(name, shape):
        return nc.dram_tensor(name, list(shape), F32, kind="ExternalOutput").ap()

    xp_d = din("xp", [2048, D]); xpre_d = din("xpre", [2048, D]); xs_d = din("xs", [128, D])
    mem_d = din("mem", [256, D])
    csk_d = din("csk", [16, 128, 128]); csv_d = din("csv", [16, 128, 128])
    sC_d = din("sC", [16, 4, 128, 64]); sn_d = din("sn", [16, 4, 64]); sm_d = din("sm", [16, 4])
    cmk_d = din("cmk", [16, 256, 256]); cmv_d = din("cmv", [16, 256, 256])
    w_in_d = din("w_in", [D, 2312]); b_i_d = din("b_igate", [4]); b_f_d = din("b_fgate", [4])
    sinks_d = din("attn_sinks", [8]); ghead_d = din("g_mlstm_head", [512]); w_out_d = din("w_out", [D, D])
    g_mix_d = din("g_mix", [D]); g_cross_d = din("g_cross", [D]); g_mem_d = din("g_mem", [D])
    w_cq_d = din("w_cq", [D, 256]); w_ck_d = din("w_ck", [D, 256]); w_cv_d = din("w_cv", [D, 256])
    w_co_d = din("w_co", [256, D]); g_ffn_d = din("g_ffn", [D])
    w_gate_d = din("w_gate", [D, FH]); w_up_d = din("w_up", [D, FH]); w_down_d = din("w_down", [FH, D])
    g_final_d = din("g_final", [D])
    ident_d = din("c_ident", [128, 128]); mb_band_d = din("c_mb_band", [128, 256]); mb_first_d = din("c_mb_first", [128, 256])
    mb_caus_d = din("c_mb_caus", [128, 128]); mb_causs_d = din("c_mb_causs", [128, 128])
    sel_d = din("c_sel", [4, 1024]); pmask_d = din("c_pmask", [4, 2])
    smc_d = din("c_smc", [32, 128]); smn_d = din("c_smn", [32, 16, 128]); sinkcol_d = din("c_sinkcol", [32, 2])
    bt_d = din("c_bt", [128, 128]); eseq_d = din("c_eseq", [128, 16])

    yp_o = dout("yp", [2048, D]); ys_o = dout("ys", [128, D])
    swak_o = dout("swak", [128, 128]); swav_o = dout("swav", [128, 128])
    Cp_o = dout("Cp", [4, 128, 64]); np_o = dout("np", [4, 64]); mp_o = dout("mp", [4, 1])
    memk_o = dout("memk", [256, 256]); memv_o = dout("memv", [256, 256])
    sks_o = dout("sks", [16, 128, 128]); svs_o = dout("svs", [16, 128, 128])
    Cs_o = dout("Cs", [16, 4, 128, 64]); ns_o = dout("ns", [16, 4, 64]); ms_o = dout("ms", [16, 4])

    st = contextlib.ExitStack()
    with st:
        def sb(name, shape, dt):
            return st.enter_context(nc.sbuf_tensor(name, list(shape), dt))

        def ps(name, shape, dt):
            return st.enter_context(nc.psum_tensor(name, list(shape), dt))

        banks = [ps(f"bk{i}", [128, 512], F32) for i in range(7)]
        bres = [P.res(f"bk{i}") for i in range(7)]
        for r_ in bres:
            r_.excl = True
        tb = ps("tb", [128, 1024], BF16)
        tbr = P.res("tb")
        tbr.excl = True
        bki = [0]

        NROT = 5

        def bank():
            i = bki[0] % NROT
            bki[0] += 1
            return banks[i], bres[i]

        Y = sb("Y", [128, NT, D], F32)
        Yr = [P.res(f"Y{t}") for t in range(NT)]
        identb = sb("identb", [128, 128], BF16); identr = P.res("identb")
        identf = sb("identf", [128, 128], F32); identfr = P.res("identf")
        onesb = sb("onesb", [128, 128], BF16); onesbr = P.res("onesb")
        onesf = sb("onesf", [128, 256], F32); onesfr = P.res("onesf")
        SEL = sb("SEL", [4, 1024], F32); selr = P.res("SEL")
        gcols = sb("gcols", [128, 4, 8], F32); gcolsr = P.res("gcols")
        gheadc = sb("gheadc", [128, 4], F32); gheadr = P.res("ghead")
        sinkb = sb("sinkb", [128, 16], F32); sinkbr = P.res("sinkb")
        gb4 = sb("gb4", [4, 4], F32); gb4r = P.res("gb4")
        mbband = sb("mbband", [128, 256], BF16); mbbandr = P.res("mbband")
        mbfirst = sb("mbfirst", [128, 256], BF16); mbfirstr = P.res("mbfirst")
        mbcaus = sb("mbcaus", [128, 128], BF16); mbcausr = P.res("mbcaus")
        stat = sb("stat", [128, 8, 4], F32)
        statr = [P.res(f"stat{i}") for i in range(8)]
        stati = [0]
        xsb = sb("xsb", [128, 2, D], BF16); xsbr = [P.res("xsb0"), P.res("xsb1")]
        Cst = sb("Cst", [64, 4, 129], F32); Cstr = [P.res(f"Cst{h}") for h in range(4)]
        ARN = 64400
        arena = sb("arena", [128, ARN], BF16)
        aoff = [0]

        def A(shape, dt, parts=128, name=None):
            n = int(np.prod(shape))
            nb = n * (4 if dt == F32 else 2)
            n16 = (nb + 1) // 2
            n16 = (n16 + 15) // 16 * 16
            assert aoff[0] + n16 <= ARN, f"arena overflow {aoff[0]}+{n16} ({name})"
            v = arena[0:parts, aoff[0]:aoff[0] + n16]
            aoff[0] += n16
            if dt == F32:
                v = v.bitcast(F32)
            v = v[:, 0:n]
            if len(shape) == 2:
                v = v.rearrange("p (a b) -> p a b", a=shape[0])
            elif len(shape) == 3:
                v = v.rearrange("p (a b c) -> p a b c", a=shape[0], b=shape[1])
            return v

        def new_phase():
            P.new_phase()
            aoff[0] = 0

        def AR(name):
            return P.res(name, arena=True)

        def A_at(off, shape, dt, parts=128):
            n = int(np.prod(shape))
            nb = n * (4 if dt == F32 else 2)
            n16 = ((nb + 1) // 2 + 15) // 16 * 16
            v = arena[0:parts, off:off + n16]
            if dt == F32:
                v = v.bitcast(F32)
            v = v[:, 0:n]
            if len(shape) == 2:
                v = v.rearrange("p (a b) -> p a b", a=shape[0])
            elif len(shape) == 3:
                v = v.rearrange("p (a b c) -> p a b c", a=shape[0], b=shape[1])
            return v, off + n16

        def ARalias(name, olds):
            r = P.res(name, arena=True)
            dd = set(r.readers)
            for o_ in olds:
                if o_.last_w is not None:
                    dd.add(o_.last_w)
                dd.update(o_.readers)
            r.readers = sorted(dd)
            return r

        I = P.I

        def mm(out, lhsT, rhs, start, stop, reads, wres):
            I("pe", "matmul", reads, [wres], out=out, lhsT=lhsT, rhs=rhs, start=start, stop=stop)

        P.dma("pool", identb[:], ident_d, identr, writes=[identr])
        P.dma("sp", identf[:], ident_d, identfr, writes=[identfr])
        I("dve", "memset", [], [onesbr], ap=onesb[:], constant=1.0)
        I("dve", "memset", [], [onesfr], ap=onesf[:], constant=1.0)
        P.dma("sp", SEL[:], sel_d, selr, writes=[selr])
        for i, g in enumerate((g_mix_d, g_cross_d, g_mem_d, g_ffn_d)):
            P.dma("sp", gcols[:, i, :], g.rearrange("(k p) -> p k", p=128), gcolsr, writes=[gcolsr], group=True, allow_slow_non_contiguous=True)
        P.dma("sp", gheadc[:], ghead_d.rearrange("(h p) -> p h", p=128), gheadr, writes=[gheadr], allow_slow_non_contiguous=True)
        P.dma("sp", gb4[:, 0:1], b_i_d.rearrange("(h o) -> h o", o=1), gb4r, writes=[gb4r], group=True, allow_slow_non_contiguous=True)
        P.dma("sp", gb4[:, 1:2], b_f_d.rearrange("(h o) -> h o", o=1), gb4r, writes=[gb4r], group=True, allow_slow_non_contiguous=True)
        P.dma("sp", gb4[:, 2:4], pmask_d, gb4r, writes=[gb4r], group=True, allow_slow_non_contiguous=True)
        P.dma("sp", sinkb[:, 0:8], sinks_d.partition_broadcast(128), sinkbr, writes=[sinkbr])
        I("dve", "tensor_scalar", [sinkbr], [sinkbr], out=sinkb[:, 8:16], in0=sinkb[:, 0:8], scalar1=-1.0, scalar2=None,
          op0=ALU.mult)
        I("dve", "tensor_scalar", [gb4r], [gb4r], out=gb4[:, 1:2], in0=gb4[:, 1:2], scalar1=-1.0, scalar2=None, op0=ALU.mult)
        P.dma("pool", mbband[:], mb_band_d, mbbandr, writes=[mbbandr])
        P.dma("pool", mbfirst[:], mb_first_d, mbfirstr, writes=[mbfirstr])
        P.dma("pool", mbcaus[:], mb_caus_d, mbcausr, writes=[mbcausr])
        for h in range(4):
            I("dve", "memset", [], [Cstr[h]], ap=Cst[:, h, :], constant=0.0)

        def SELh(h, n=128):
            return SEL[:, h * 128:h * 128 + n]

        def NSELh(h, n=128):
            return SEL[:, 512 + h * 128:512 + h * 128 + n]

        def norm_stats(src, sres, jb=0):
            i = stati[0] % 8
            stati[0] += 1
            sr = statr[i]
            I("act", "activation", [sres], [xsbr[jb], sr], out=xsb[:, jb, :], in_=src, func=AF.Square, accum_out=stat[:, i, 0:1])
            I("dve", "tensor_scalar", [sr], [sr], out=stat[:, i, 1:2], in0=stat[:, i, 0:1], scalar1=1.0 / D, scalar2=EPS,
              op0=ALU.mult, op1=ALU.add)
            I("act", "activation", [sr], [sr], out=stat[:, i, 2:3], in_=stat[:, i, 1:2], func=AF.Sqrt)
            I("dve", "reciprocal", [sr], [sr], out=stat[:, i, 3:4], in_=stat[:, i, 2:3])
            return stat[:, i, 3:4], sr

        xsi = [0]

        def norm_T(src, sres, gi, dst, dres):
            b = xsi[0] % 2
            xsi[0] += 1
            rstd, sr = norm_stats(src, sres, b)
            I("dve", "tensor_scalar", [sres, sr], [xsbr[b]], out=xsb[:, b, :], in0=src, scalar1=rstd, scalar2=None, op0=ALU.mult)
            for k in range(8):
                I("pe", "transpose", [xsbr[b], identr], [tbr], out=tb[:, k * 128:(k + 1) * 128],
                  in_=xsb[:, b, k * 128:(k + 1) * 128], identity=identb[:])
            for k in range(8):
                I("act", "activation", [tbr, gcolsr], dres, out=dst[:, k, :], in_=tb[:, k * 128:(k + 1) * 128],
                  func=AF.Copy, scale=gcols[:, gi, k:k + 1])

        class NS:
            pass

        def alloc_mixer(gt, nkt, nvt):
            M = NS()
            M.WQ = A([8, 512], BF16); M.WQr = AR("WQ")
            M.WTOK = A([8, 1024], BF16); M.WTOKr = AR("WTOK")
            M.WK = M.WTOK[:, :, 0:128]; M.WKr = M.WTOKr
            M.WMQ = A([8, 256], BF16); M.WMQr = AR("WMQ")
            M.WMK = M.WTOK[:, :, 256:512]; M.WMKr = M.WTOKr
            M.WOG = A([8, 512], BF16); M.WOGr = AR("WOG")
            M.WGT = A([8, 8], BF16); M.WGTr = AR("WGT")
            M.WOA = A([8, 1024], BF16, parts=64); M.WOAr = AR("WOA")
            M.WOM = A([4, 1024], BF16); M.WOMr = AR("WOM")

            def wload(dst, res, src, **kw):
                P.dma("pool", dst, src, res, writes=[res], **kw)

            def wcols(a_, b_):
                return w_in_d[:, a_:b_].rearrange("(k p) n -> p k n", p=128)
            wload(M.WTOK[:, :, 0:256], M.WTOKr, wcols(512, 768), group=True)
            wload(M.WTOK[:, :, 256:1024], M.WTOKr, wcols(1024, 1792), group=True)
            wload(M.WGT[:], M.WGTr, wcols(2304, 2312), allow_slow_non_contiguous=True)
            wload(M.WQ[:], M.WQr, wcols(0, 512))
            wload(M.WMQ[:], M.WMQr, wcols(768, 1024))
            wload(M.WOG[:], M.WOGr, wcols(1792, 2304))
            wload(M.WOA[:], M.WOAr, w_out_d[0:512, :].rearrange("(g d) n -> d g n", d=64))
            wload(M.WOM[:], M.WOMr, w_out_d[512:1024, :].rearrange("(h p) n -> p h n", p=128))
            M.KT = A([2, nkt * 128], BF16, parts=64); M.KTr = [AR(f"KT{i}") for i in range(nkt)]
            M.Vt = A([nvt, 128], BF16); M.Vtr = [AR(f"Vt{i}") for i in range(nvt)]
            M.XNTg = A([8, gt * 128], BF16); M.XNTgr = [AR(f"XNTg{i}") for i in range(gt)]
            M.QT = A([8, gt * 128], BF16, parts=64); M.QTr = AR("QT")
            M.MQT = A([4, gt * 128], BF16, parts=64); M.MQTr = AR("MQT")
            M.MKT = A([4, gt * 128], BF16, parts=64); M.MKTr = AR("MKT")
            M.SGT = A([4, gt * 128], BF16); M.SGTr = AR("SGT")
            M.MKtok = A([gt, 256], BF16); M.MKtokr = [AR(f"MKtok{i}") for i in range(gt)]
            M.MVaug = A([gt, 4, 129], BF16); M.MVaugr = [AR(f"MVaug{i}") for i in range(gt)]
            M.ATTT = A([8, gt * 128], BF16, parts=64); M.ATTTr = [AR(f"ATTT{i}") for i in range(gt)]
            M.HMT = A([4, gt * 128], BF16); M.HMTr = [AR(f"HMT{i}") for i in range(gt)]
            M.NG = gt * 128
            NG_ = M.NG
            M.G_IG = A([1, NG_ + 1], F32, parts=4)[:, 0, :]; M.G_E = A([1, NG_], F32, parts=4)[:, 0, :]
            M.G_L1 = A([1, NG_], F32, parts=4)[:, 0, :]; M.G_B = A([1, NG_ + 1], F32, parts=4)[:, 0, :]
            M.G_A = A([1, NG_], F32, parts=4)[:, 0, :]; M.G_M = A([1, NG_ + 1], F32, parts=4)[:, 0, :]
            M.G_BM = A([1, NG_], F32, parts=4)[:, 0, :]; M.G_DM = A([1, NG_], F32, parts=4)[:, 0, :]
            M.Gr = AR("G_IG"); M.G_Br = AR("G_B"); M.G_Ar = AR("G_A"); M.G_Mr = AR("G_M"); M.G_BMr = AR("G_BM"); M.G_DMr = AR("G_DM")
            M.SKV = A([1, 256], F32)[:, 0, :]; M.SKVr = AR("SKV")
            M.Ebuf = A([4, 256], BF16); M.Er = AR("E")
            M.PTs = A([1, 1024], BF16); M.PTsr = [AR("PTs0")] * 2
            M.sm_st = A([1, 32], F32)[:, 0, :]; M.smr = AR("sm_st")
            M.WKC = A([1, 8], F32)[:, 0, :]; M.WKCr = AR("WKC")
            M.VW = A([4, 129], BF16); M.VWr = [AR(f"VW{h}") for h in range(4)]
            M.Cb = A([4, 257], BF16, parts=64); M.Cbr = [AR(f"Cb{h}") for h in range(4)]
            M.WT = A([4, 128], BF16); M.WTr = AR("WT")
            M.ST = A([4, 128], BF16); M.STr = AR("ST")
            M.WI = A([4, 128], BF16); M.WIr = AR("WI")
            M.QW = A([4, 128], BF16, parts=64); M.QWr = AR("QW")
            M.LOWB = A([4, 128], F32); M.LOWBr = AR("LOWB")
            M.T1 = A([4, 128], F32); M.T1r = AR("T1")
            M.T2 = A([4, 128], F32); M.T2r = AR("T2")
            M.USQ = A([4, 128], BF16); M.USQr = AR("USQ")
            for i in range(gt):
                I("dve", "memset", [], [M.MVaugr[i]], ap=M.MVaug[:, i, :, 128:129], constant=1.0)
            return M

        M = alloc_mixer(GT, NTP + 1, NTP + 1)
        I("dve", "memset", [], [M.G_Br], ap=M.G_B[:, 0:1], constant=0.0)
        I("dve", "memset", [], [M.G_Mr], ap=M.G_M[:, 0:1], constant=0.0)

        def tok_major(ti, xcols, xres, vslot, want_kv_out=None):
            b0, b0r = bank()
            b1, b1r = bank()
            for k in range(8):
                mm(b0[:, :], M.XNTg[:, k, xcols], M.WTOK[:, k, 0:512], k == 0, k == 7, [xres, M.WTOKr], b0r)
            for k in range(8):
                mm(b1[:, :], M.XNTg[:, k, xcols], M.WTOK[:, k, 512:1024], k == 0, k == 7, [xres, M.WTOKr], b1r)
            I("act", "activation", [b0r], [M.Vtr[vslot]], out=M.Vt[:, vslot, :], in_=b0[:, 128:256], func=AF.Copy)
            I("act", "activation", [b0r], [M.MKtokr[ti]], out=M.MKtok[:, ti, :], in_=b0[:, 256:512], func=AF.Copy, scale=0.125)
            I("dve", "tensor_copy", [b1r], [M.MVaugr[ti]], out=M.MVaug[:, ti, :, 0:128],
              in_=b1[:, :].rearrange("p (h d) -> p h d", h=4))
            if want_kv_out is not None:
                I("dve", "tensor_copy", [b0r], [M.SKVr], out=M.SKV[:, :], in_=b0[:, 0:256])
                if want_kv_out == "sample":
                    P.dma("sp", sks_o[:, 120:128, :], M.SKV[:, 0:128], M.SKVr, reads=[M.SKVr], group=True)
                    P.dma("sp", svs_o[:, 120:128, :], M.SKV[:, 128:256], M.SKVr, reads=[M.SKVr], group=True)
                else:
                    P.dma("sp", swak_o, M.SKV[:, 0:128], M.SKVr, reads=[M.SKVr], group=True)
                    P.dma("sp", swav_o, M.SKV[:, 128:256], M.SKVr, reads=[M.SKVr], group=True)

        def feat64(W, Wr, nh, dst, dres, ntok, xres, scale=None, dcol0=0):
            for h0 in range(0, nh, 2):
                bk, bkr = bank()
                for hh in range(2):
                    h = h0 + hh
                    for k in range(8):
                        mm(bk[0:64, hh * 256:hh * 256 + ntok], W[:, k, h * 64:(h + 1) * 64], M.XNTg[:, k, 0:ntok],
                           k == 0, k == 7, [Wr] + xres, bkr)
                src = bk[0:64, :].rearrange("p (a b) -> p a b", a=2)[:, :, 0:ntok]
                kw = {} if scale is None else {"scale": scale}
                I("act", "activation", [bkr], dres, out=dst[:, h0:h0 + 2, dcol0:dcol0 + ntok], in_=src, func=AF.Copy, **kw)

        def gates(ntok, xres, prefix):
            pg, pgr = bank()
            for k in range(8):
                mm(pg[0:4, 0:ntok], M.WGT[:, k, 0:4], M.XNTg[:, k, 0:ntok], k == 0, k == 7, [M.WGTr] + xres, pgr)
            for k in range(8):
                mm(pg[0:4, 256:256 + ntok], M.WGT[:, k, 4:8], M.XNTg[:, k, 0:ntok], k == 0, k == 7, [M.WGTr] + xres, pgr)
            I("act", "activation", [pgr, gb4r], [M.Gr], out=M.G_IG[:, 1:ntok + 1], in_=pg[0:4, 0:ntok], func=AF.Identity,
              bias=gb4[:, 0:1])
            I("act", "activation", [pgr, gb4r], [M.Gr], out=M.G_E[:, 0:ntok], in_=pg[0:4, 256:256 + ntok], func=AF.Exp,
              bias=gb4[:, 1:2], scale=-1.0)
            I("act", "activation", [M.Gr], [M.Gr], out=M.G_L1[:, 0:ntok], in_=M.G_E[:, 0:ntok], func=AF.Ln, bias=1.0)
            if prefix == "sample":
                return
            if prefix:
                I("dve", "tensor_scalar", [M.Gr, gb4r], [M.Gr], out=M.G_L1[:, 0:ntok], in0=M.G_L1[:, 0:ntok], scalar1=gb4[:, 2:3],
                  scalar2=None, op0=ALU.mult)
            I("dve", "tensor_tensor_scan", [M.Gr, M.G_Br, onesfr], [M.G_Br], out=M.G_B[:, 1:ntok + 1], data0=onesf[0:4, 0:ntok],
              data1=M.G_L1[:, 0:ntok], initial=M.G_B[:, 0:1], op0=ALU.mult, op1=ALU.subtract)
            I("dve", "scalar_tensor_tensor", [M.Gr, M.G_Br, gb4r], [M.G_Ar], out=M.G_A[:, 0:ntok], in0=M.G_IG[:, 1:ntok + 1],
              scalar=(gb4[:, 3:4] if prefix else 0.0), in1=M.G_B[:, 1:ntok + 1], op0=ALU.add, op1=ALU.subtract)
            I("dve", "tensor_tensor_scan", [M.G_Ar, M.G_Mr, onesfr], [M.G_Mr], out=M.G_M[:, 1:ntok + 1], data0=onesf[0:4, 0:ntok],
              data1=M.G_A[:, 0:ntok], initial=M.G_M[:, 0:1], op0=ALU.mult, op1=ALU.max)
            I("dve", "tensor_tensor", [M.G_Br, M.G_Mr], [M.G_BMr], out=M.G_BM[:, 0:ntok], in0=M.G_B[:, 1:ntok + 1],
              in1=M.G_M[:, 1:ntok + 1], op=ALU.add)
            for ci in range(ntok // 128):
                I("dve", "tensor_scalar", [M.G_Mr], [M.G_DMr], out=M.G_DM[:, ci * 128:(ci + 1) * 128],
                  in0=M.G_M[:, 1 + ci * 128:1 + (ci + 1) * 128], scalar1=M.G_M[:, ci * 128:ci * 128 + 1], scalar2=None,
                  op0=ALU.subtract)

        def gates_carry(ntok):
            I("dve", "tensor_copy", [M.G_Br], [M.G_Br], out=M.G_B[:, 0:1], in_=M.G_B[:, ntok:ntok + 1])
            I("dve", "tensor_copy", [M.G_Mr], [M.G_Mr], out=M.G_M[:, 0:1], in_=M.G_M[:, ntok:ntok + 1])

        def state_update(ti, c0, refresh_cb):
            pw, pwr = bank()
            for h in range(4):
                mm(pw[:, h:h + 1], M.G_A[:, c0:c0 + 128], SELh(h, 1), True, False, [M.G_Ar, selr], pwr)
                mm(pw[:, h:h + 1], NSELh(h), M.G_M[:, c0 + 128:c0 + 129], False, True, [M.G_Mr, selr], pwr)
            for h in range(4):
                mm(pw[:, 4 + h:5 + h], NSELh(h), M.G_DM[:, c0 + 127:c0 + 128], True, True, [M.G_DMr, selr], pwr)
            I("act", "activation", [pwr], [M.WKCr], out=M.WKC[:, 0:8], in_=pw[:, 0:8], func=AF.Exp)
            for h in range(4):
                I("dve", "tensor_scalar", [M.MVaugr[ti], M.WKCr], [M.VWr[h]], out=M.VW[:, h, :], in0=M.MVaug[:, ti, h, :],
                  scalar1=M.WKC[:, h:h + 1], scalar2=None, op0=ALU.mult)
            for h0 in (0, 2):
                dc, dcr = bank()
                for hh in range(2):
                    h = h0 + hh
                    mm(dc[0:64, hh * 129:(hh + 1) * 129], M.MKtok[:, ti, h * 64:(h + 1) * 64], M.VW[:, h, :], True, True,
                       [M.MKtokr[ti], M.VWr[h]], dcr)
                for hh in range(2):
                    h = h0 + hh
                    I("dve", "scalar_tensor_tensor", [Cstr[h], M.WKCr, dcr], [Cstr[h]], out=Cst[:, h, :], in0=Cst[:, h, :],
                      scalar=M.WKC[0:64, 4 + h:5 + h], in1=dc[0:64, hh * 129:(hh + 1) * 129], op0=ALU.mult, op1=ALU.add)
            if refresh_cb:
                for h in range(4):
                    I("act", "activation", [Cstr[h]], [M.Cbr[h]], out=M.Cb[:, h, 0:129], in_=Cst[:, h, :], func=AF.Copy)
                    I("act", "activation", [Cstr[h]], [M.Cbr[h]], out=M.Cb[:, h, 129:257],
                      in_=Cst[:, h, 128:129].broadcast_to([64, 128]), func=AF.Copy)

        def mlstm_chunk(ti, c0, mbias, mbiasr, inter=True, inter_fn=None):
            cs = slice(c0, c0 + 128)
            pwt, pwtr = bank()
            for h in range(4):
                o = pwt[:, h * 128:(h + 1) * 128]
                mm(o, M.G_A[:, cs], SELh(h), True, False, [M.G_Ar, selr], pwtr)
                mm(o, NSELh(h), M.G_M[:, c0 + 1:c0 + 129], False, False, [M.G_Mr, selr], pwtr)
                mm(o, identb[:], mbias, False, True, [identr, mbiasr], pwtr)
            I("act", "activation", [pwtr], [M.WTr], out=M.WT[:, :, :], in_=pwt[:, :].rearrange("p (h t) -> p h t", h=4), func=AF.Exp)
            pqk, pqkr = bank()
            for h in range(4):
                mm(pqk[:, h * 128:(h + 1) * 128], M.MKT[:, h, cs], M.MQT[:, h, cs], True, True, [M.MKTr, M.MQTr], pqkr)
            I("dve", "tensor_tensor", [pqkr, M.WTr], [M.STr], out=M.ST[:, :, :], in0=pqk[:, :].rearrange("p (h t) -> p h t", h=4),
              in1=M.WT[:, :, :], op=ALU.mult)
            pwi, pwir = bank()
            for h in range(4):
                mm(pwi[:, h * 128:(h + 1) * 128], NSELh(h), M.G_DM[:, cs], True, True, [M.G_DMr, selr], pwir)
            I("act", "activation", [pwir], [M.WIr], out=M.WI[:, :, :], in_=pwi[:, :].rearrange("p (h t) -> p h t", h=4), func=AF.Exp)
            I("dve", "tensor_tensor", [M.MQTr, M.WIr], [M.QWr], out=M.QW[:, :, :], in0=M.MQT[:, :, cs], in1=M.WI[0:64, :, :], op=ALU.mult)
            plb, plbr = bank()
            for h in range(4):
                mm(plb[:, h * 128:(h + 1) * 128], NSELh(h), M.G_BM[:, cs], True, True, [M.G_BMr, selr], plbr)
            I("act", "activation", [plbr], [M.LOWBr], out=M.LOWB[:, :, :], in_=plb[:, :].rearrange("p (h t) -> p h t", h=4), func=AF.Exp)
            pnum, pnumr = banks[5], bres[5]
            pden, pdenr = banks[6], bres[6]
            if inter_fn is not None:
                inter_fn("pre")
            for h in range(4):
                o = pnum[:, h * 128:(h + 1) * 128]
                mm(o, M.MVaug[:, ti, h, 0:128], M.ST[:, h, :], True, False, [M.MVaugr[ti], M.STr], pnumr)
                if inter_fn is not None:
                    inter_fn("num", h, pnum, pnumr)
                else:
                    mm(o, M.Cb[:, h, 0:128], M.QW[:, h, :], False, True, [M.Cbr[h], M.QWr], pnumr)
            for h in range(4):
                o = pden[:, h * 128:(h + 1) * 128]
                mm(o, onesb[:], M.ST[:, h, :], True, False, [onesbr, M.STr], pdenr)
                if inter_fn is not None:
                    inter_fn("den", h, pden, pdenr)
                else:
                    mm(o, M.Cb[:, h, 129:257], M.QW[:, h, :], False, True, [M.Cbr[h], M.QWr], pdenr)
            return pnum, pnumr, pden, pdenr

        def mlstm_finish(pnum, pnumr, pden, pdenr, c0, hres):
            cs = slice(c0, c0 + 128)
            v4 = lambda b: b[:, :].rearrange("p (h t) -> p h t", h=4)
            I("act", "activation", [pdenr], [M.T1r], out=M.T1[:, :, :], in_=v4(pden), func=AF.Abs)
            I("dve", "tensor_tensor", [M.T1r, M.LOWBr], [M.T1r], out=M.T1[:, :, :], in0=M.T1[:, :, :], in1=M.LOWB[:, :, :], op=ALU.max)
            I("act", "activation", [M.T1r], [M.T1r], out=M.T1[:, :, :], in_=M.T1[:, :, :], func=AF.Square, scale=float(np.sqrt(EPS)))
            I("act", "activation", [pnumr], [M.USQr], out=M.USQ[:, :, :], in_=v4(pnum), func=AF.Square)
            pss, pssr = bank()
            mm(pss[:, :], onesb[:], M.USQ[:, :, :], True, True, [onesbr, M.USQr], pssr)
            I("dve", "scalar_tensor_tensor", [pssr, M.T1r], [M.T2r], out=M.T2[:, :, :], in0=v4(pss), scalar=1.0 / 128, in1=M.T1[:, :, :],
              op0=ALU.mult, op1=ALU.add)
            I("act", "activation", [M.T2r], [M.T2r], out=M.T2[:, :, :], in_=M.T2[:, :, :], func=AF.Sqrt)
            I("dve", "reciprocal", [M.T2r], [M.T2r], out=M.T2[:, :, :], in_=M.T2[:, :, :])
            I("dve", "tensor_tensor", [pnumr, M.T2r], [M.T1r], out=M.T1[:, :, :], in0=v4(pnum), in1=M.T2[:, :, :], op=ALU.mult)
            for h in range(4):
                I("dve", "scalar_tensor_tensor", [M.T1r, gheadr, M.SGTr], [hres], out=M.HMT[:, h, cs], in0=M.T1[:, h, :],
                  scalar=gheadc[:, h:h + 1], in1=M.SGT[:, h, cs], op0=ALU.mult, op1=ALU.mult)

        def swa_tile(ti, kcol0, vslots, mb, mbr, ktres):
            qs = slice(ti * 128, (ti + 1) * 128)
            for h in range(2):
                bks = [bank(), bank()]
                for g in range(4):
                    bk, bkr = bks[g // 2]
                    o = bk[:, (g % 2) * 256:(g % 2 + 1) * 256]
                    mm(o, M.QT[:, 4 * h + g, qs], M.KT[:, h, kcol0:kcol0 + 256], True, False, [M.QTr] + ktres, bkr)
                    mm(o, identb[:], mb, False, True, [identr, mbr], bkr)
                for j in range(2):
                    I("dve", "reduce_max", [bks[j][1]], [M.smr], out=M.sm_st[:, 2 * j:2 * j + 2],
                      in_=bks[j][0][:, :].rearrange("p (a b) -> p a b", a=2), axis=AX.X)
                I("dve", "tensor_scalar", [M.smr], [M.smr], out=M.sm_st[:, 0:4], in0=M.sm_st[:, 0:4], scalar1=-0.125, scalar2=None,
                  op0=ALU.mult)
                I("dve", "tensor_tensor", [M.smr, sinkbr], [M.smr], out=M.sm_st[:, 0:4], in0=M.sm_st[:, 0:4],
                  in1=sinkb[:, 8 + 4 * h:12 + 4 * h], op=ALU.min)
                for g in range(4):
                    bk, bkr = bks[g // 2]
                    I("act", "activation", [bkr, M.smr], [M.Er, M.smr], out=M.Ebuf[:, g, :], in_=bk[:, (g % 2) * 256:(g % 2 + 1) * 256],
                      func=AF.Exp, bias=M.sm_st[:, g:g + 1], scale=0.125, accum_out=M.sm_st[:, 4 + g:5 + g])
                I("dve", "tensor_tensor", [M.smr, sinkbr], [M.smr], out=M.sm_st[:, 8:12], in0=M.sm_st[:, 0:4],
                  in1=sinkb[:, 4 * h:4 * h + 4], op=ALU.add)
                I("act", "activation", [M.smr], [M.smr], out=M.sm_st[:, 8:12], in_=M.sm_st[:, 8:12], func=AF.Exp)
                I("dve", "tensor_tensor", [M.smr], [M.smr], out=M.sm_st[:, 8:12], in0=M.sm_st[:, 8:12], in1=M.sm_st[:, 4:8], op=ALU.add)
                I("dve", "reciprocal", [M.smr], [M.smr], out=M.sm_st[:, 12:16], in_=M.sm_st[:, 8:12])
                for g in range(4):
                    if g % 2 == 0:
                        I("act", "activation", [M.Er, M.smr], [M.Er], out=M.Ebuf[:, g, :], in_=M.Ebuf[:, g, :], func=AF.Copy,
                          scale=M.sm_st[:, 12 + g:13 + g])
                    else:
                        I("dve", "tensor_scalar", [M.Er, M.smr], [M.Er], out=M.Ebuf[:, g, :], in0=M.Ebuf[:, g, :],
                          scalar1=M.sm_st[:, 12 + g:13 + g], scalar2=None, op0=ALU.mult)
                for kb in range(2):
                    for g in range(4):
                        I("pe", "transpose", [M.Er, identr], [tbr], out=tb[:, (kb * 4 + g) * 128:(kb * 4 + g + 1) * 128],
                          in_=M.Ebuf[:, g, kb * 128:(kb + 1) * 128], identity=identb[:])
                pb = 0
                if h == 0:
                    I("dve", "tensor_copy", [tbr], [M.PTsr[pb]], out=M.PTs[:, pb, :], in_=tb[:, :])
                else:
                    I("act", "activation", [tbr], [M.PTsr[pb]], out=M.PTs[:, pb, :], in_=tb[:, :], func=AF.Copy)
                po, por = bank()
                mm(po[0:64, :], M.Vt[:, vslots[0], h * 64:(h + 1) * 64], M.PTs[:, pb, 0:512], True, False, [M.Vtr[vslots[0]], M.PTsr[pb]], por)
                mm(po[0:64, :], M.Vt[:, vslots[1], h * 64:(h + 1) * 64], M.PTs[:, pb, 512:1024], False, True, [M.Vtr[vslots[1]], M.PTsr[pb]], por)
                I("act", "activation", [por], [M.ATTTr[ti]], out=M.ATTT[:, 4 * h:4 * h + 4, qs],
                  in_=po[0:64, :].rearrange("p (g q) -> p g q", g=4), func=AF.Copy)

        def wout_tile(ti, t):
            qs = slice(ti * 128, (ti + 1) * 128)
            for c in range(2):
                bk, bkr = bank()
                cc = slice(c * 512, (c + 1) * 512)
                for hg in range(8):
                    mm(bk[:, :], M.ATTT[:, hg, qs], M.WOA[:, hg, cc], hg == 0, False, [M.ATTTr[ti], M.WOAr], bkr)
                for h in range(4):
                    mm(bk[:, :], M.HMT[:, h, qs], M.WOM[:, h, cc], False, h == 3, [M.HMTr[ti], M.WOMr], bkr)
                I("dve", "tensor_tensor", [Yr[t], bkr], [Yr[t]], out=Y[:, t, cc], in0=Y[:, t, cc], in1=bk[:, :], op=ALU.add)

        xpre_t = xpre_d.rearrange("(t p) d -> t p d", p=128)
        xp_t = xp_d.rearrange("(t p) d -> t p d", p=128)
        for t in range(NTP):
            P.dma("sp", Y[:, t, :], xpre_t[t], Yr[t], writes=[Yr[t]])
        for g0 in range(0, NTP, GT):
            for ti in range(GT):
                t = g0 + ti
                norm_T(Y[:, t, :], Yr[t], 0, M.XNTg[:, :, ti * 128:(ti + 1) * 128], [M.XNTgr[ti]])
                P.dma("sp", Y[:, t, :], xp_t[t], Yr[t], writes=[Yr[t]])
            for ti in range(GT):
                tok_major(ti, slice(ti * 128, (ti + 1) * 128), M.XNTgr[ti], 0)
            gates(GT * 128, M.XNTgr, True)
            if g0 + GT == NTP:
                bk, bkr = bank()
                for h in range(2):
                    for k in range(8):
                        mm(bk[0:64, h * 128:(h + 1) * 128], M.WK[:, k, h * 64:(h + 1) * 64], M.XNTg[:, k, (GT - 1) * 128:GT * 128],
                           k == 0, k == 7, [M.WKr, M.XNTgr[GT - 1]], bkr)
                I("act", "activation", [bkr], [M.KTr[0]], out=M.KT[:, :, 0:128],
                  in_=bk[0:64, 0:256].rearrange("p (a b) -> p a b", a=2), func=AF.Copy)
            for ti in range(GT):
                last = (g0 + ti == NTP - 1)
                state_update(ti, ti * 128, last)
            gates_carry(GT * 128)

        for g0 in range(0, NTP, GT):
            for ti in range(GT):
                t = g0 + ti
                norm_T(Y[:, t, :], Yr[t], 0, M.XNTg[:, :, ti * 128:(ti + 1) * 128], [M.XNTgr[ti]])
            for ti in range(GT):
                t = g0 + ti
                tok_major(ti, slice(ti * 128, (ti + 1) * 128), M.XNTgr[ti], 1 + t, want_kv_out=(True if t == NTP - 1 else None))
            gates(M.NG, M.XNTgr, False)
            feat64(M.WK, M.WKr, 2, M.KT, [M.KTr[1 + g0 + i] for i in range(GT)], M.NG, M.XNTgr, dcol0=128 + g0 * 128)
            feat64(M.WQ, M.WQr, 8, M.QT, [M.QTr], M.NG, M.XNTgr)
            feat64(M.WMQ, M.WMQr, 4, M.MQT, [M.MQTr], M.NG, M.XNTgr)
            feat64(M.WMK, M.WMKr, 4, M.MKT, [M.MKTr], M.NG, M.XNTgr, scale=0.125)
            for h0 in (0, 2):
                bk, bkr = bank()
                for hh in range(2):
                    h = h0 + hh
                    for k in range(8):
                        mm(bk[:, hh * 256:hh * 256 + M.NG], M.WOG[:, k, h * 128:(h + 1) * 128], M.XNTg[:, k, 0:M.NG], k == 0, k == 7,
                           [M.WOGr] + M.XNTgr, bkr)
                I("act", "activation", [bkr], [M.SGTr], out=M.SGT[:, h0:h0 + 2, :],
                  in_=bk[:, :].rearrange("p (a b) -> p a b", a=2)[:, :, 0:M.NG], func=AF.Sigmoid)
            for ti in range(GT):
                t = g0 + ti
                pn = mlstm_chunk(ti, ti * 128, mbcaus[:], mbcausr)
                swa_tile(ti, t * 128, (t, t + 1), (mbfirst[:] if t == 0 else mbband[:]), (mbfirstr if t == 0 else mbbandr),
                         [M.KTr[t], M.KTr[t + 1]])
                mlstm_finish(*pn, ti * 128, M.HMTr[ti])
                state_update(ti, ti * 128, True)
                wout_tile(ti, t)
            if debug and g0 == DBG_G0:
                dA = dout("dbg_att", [64, 8, M.NG]); dH = dout("dbg_hm", [128, 4, M.NG])
                P.dma("pool", dA, M.ATTT[:, :, :], M.ATTTr[0], reads=M.ATTTr)
                P.dma("pool", dH, M.HMT[:, :, :], M.HMTr[0], reads=M.HMTr)
            gates_carry(M.NG)

        CO = A([4, 64], F32); COr = AR("CO")
        for h in range(4):
            bk, bkr = bank()
            mm(bk[:, 0:64], Cst[:, h, 0:128], identf[0:64, 0:64], True, True, [Cstr[h], identfr], bkr)
            I("act", "activation", [bkr], [COr], out=CO[:, h, :], in_=bk[:, 0:64], func=AF.Copy)
        P.dma("sp", Cp_o.rearrange("h p k -> p h k"), CO[:, :, :], COr, reads=[COr])
        for h in range(4):
            P.dma("sp", np_o[h, :].rearrange("(k o) -> k o", o=1), Cst[:, h, 128:129], Cstr[h], reads=[Cstr[h]], allow_slow_non_contiguous=True)
        P.dma("sp", mp_o, M.G_BM[:, M.NG - 1:M.NG], M.G_BMr, reads=[M.G_BMr], allow_slow_non_contiguous=True)

        new_phase()
        MA = M
        M = alloc_mixer(1, 1, 1)
        R0_olds = [M.WQr, M.WTOKr, M.WMQr, M.WOGr, M.WGTr]
        TS = NTP
        P.dma("sp", Y[:, TS, :], xs_d, Yr[TS], writes=[Yr[TS]])
        shk = P.res("shk"); shv = P.res("shv")
        P.dma("sp", sks_o[:, 0:120, :], csk_d[:, 8:128, :], shk, writes=[shk])
        P.dma("sp", svs_o[:, 0:120, :], csv_d[:, 8:128, :], shv, writes=[shv])
        CKn = A([16, 128], BF16); CKnr = AR("CKn")
        CV = A([16, 128], BF16); CVr = AR("CV")
        CKT = A([16, 128], BF16, parts=64); CKTr = AR("CKT")
        SMC = A([1, 128], BF16, parts=32)[:, 0, :]; SMCr = AR("SMC")
        SMN = A([16, 128], BF16, parts=32); SMNr = AR("SMN")
        SINKC = A([1, 4], F32, parts=32)[:, 0, :]; SINKCr = AR("SINKC")
        mbcs = A([1, 128], BF16)[:, 0, :]; mbcsr = AR("mbcs")
        PNs = A([2, 256], BF16, parts=32); PNsr = [AR("PNs0"), AR("PNs1")]
        sms = A([2, 8], F32, parts=32); smsr = [AR("sms0"), AR("sms1")]
        PTS = A([1, 1024], BF16)[:, 0, :]; PTSr = AR("PTS")
        M0 = A([1, 16], F32, parts=4)[:, 0, :]; M0r = AR("M0")
        MTe = A([1, 128], F32, parts=4)[:, 0, :]; MTer = AR("MTe")
        DMT = A([1, 16], F32, parts=4)[:, 0, :]; DMTr = AR("DMT")
        E16 = A([1, 16], F32)[:, 0, :]; E16r = AR("E16")
        EW = A([4, 16], BF16); EWr = AR("EW")
        WCB = A([4, 16], F32); WCBr = AR("WCB")
        SNn = A([1, 64], F32, parts=64)[:, 0, :]; SNnr = AR("SNn")
        SNT = A([1, 64], F32, parts=64)[:, 0, :]; SNTr = AR("SNT")
        NNT = A([1, 64], F32, parts=64)[:, 0, :]; NNTr = AR("NNT")
        NNo = A([1, 64], F32, parts=64)[:, 0, :]; NNor = AR("NNo")
        BTf = A([1, 128], F32)[:, 0, :]; BTfr = AR("BTf")
        P.dma("pool", CKn[:, :, :], csk_d.rearrange("j p c -> p j c"), CKnr, writes=[CKnr])
        P.dma("pool", CV[:, :, :], csv_d.rearrange("j p c -> p j c"), CVr, writes=[CVr])
        P.dma("pool", SMC, smc_d, SMCr, writes=[SMCr])
        P.dma("pool", SMN[:, :, :], smn_d, SMNr, writes=[SMNr])
        P.dma("sp", SINKC[:, 0:2], sinkcol_d, SINKCr, writes=[SINKCr])
        I("dve", "tensor_scalar", [SINKCr], [SINKCr], out=SINKC[:, 2:4], in0=SINKC[:, 0:2], scalar1=-1.0, scalar2=None, op0=ALU.mult)
        P.dma("pool", mbcs, mb_causs_d, mbcsr, writes=[mbcsr])
        P.dma("sp", M0, sm_d.rearrange("j h -> h j"), M0r, writes=[M0r], allow_slow_non_contiguous=True)
        P.dma("sp", E16, eseq_d, E16r, writes=[E16r])
        P.dma("sp", SNn, sn_d.rearrange("j h k -> (j h) k"), SNnr, writes=[SNnr])

        norm_T(Y[:, TS, :], Yr[TS], 0, M.XNTg[:, :, 0:128], [M.XNTgr[0]])
        tok_major(0, slice(0, 128), M.XNTgr[0], 0, want_kv_out="sample")
        gates(128, M.XNTgr, "sample")
        feat64(M.WK, M.WKr, 2, M.KT, [M.KTr[0]], 128, M.XNTgr, dcol0=0)
        feat64(M.WQ, M.WQr, 8, M.QT, [M.QTr], 128, M.XNTgr)
        feat64(M.WMQ, M.WMQr, 4, M.MQT, [M.MQTr], 128, M.XNTgr)
        feat64(M.WMK, M.WMKr, 4, M.MKT, [M.MKTr], 128, M.XNTgr, scale=0.125)
        for h0 in (0, 2):
            bk, bkr = bank()
            for hh in range(2):
                h = h0 + hh
                for k in range(8):
                    mm(bk[:, hh * 256:hh * 256 + 128], M.WOG[:, k, h * 128:(h + 1) * 128], M.XNTg[:, k, 0:128], k == 0, k == 7,
                       [M.WOGr] + M.XNTgr, bkr)
            I("act", "activation", [bkr], [M.SGTr], out=M.SGT[:, h0:h0 + 2, :],
              in_=bk[:, :].rearrange("p (a b) -> p a b", a=2)[:, :, 0:128], func=AF.Sigmoid)
        for j in range(16):
            I("dve", "tensor_tensor_scan", [M.Gr, M.G_Br, onesfr], [M.G_Br], out=M.G_B[:, 1 + 8 * j:9 + 8 * j],
              data0=onesf[0:4, 0:8], data1=M.G_L1[:, 8 * j:8 * j + 8], initial=0.0, op0=ALU.mult, op1=ALU.subtract)
        I("dve", "tensor_tensor", [M.Gr, M.G_Br], [M.G_Ar], out=M.G_A[:, 0:128], in0=M.G_IG[:, 1:129], in1=M.G_B[:, 1:129],
          op=ALU.subtract)
        for j in range(16):
            I("dve", "tensor_tensor_scan", [M.G_Ar, M.G_Mr, onesfr, M0r], [M.G_Mr], out=M.G_M[:, 1 + 8 * j:9 + 8 * j],
              data0=onesf[0:4, 0:8], data1=M.G_A[:, 8 * j:8 * j + 8], initial=M0[:, j:j + 1], op0=ALU.mult, op1=ALU.max)
        I("dve", "tensor_tensor", [M.G_Br, M.G_Mr], [M.G_BMr], out=M.G_BM[:, 0:128], in0=M.G_B[:, 1:129], in1=M.G_M[:, 1:129],
          op=ALU.add)
        GM3 = M.G_M[:, 1:129].rearrange("p (j i) -> p j i", i=8)
        I("dve", "tensor_tensor", [M.G_Mr, M0r], [M.G_DMr], out=M.G_DM[:, 0:128].rearrange("p (j i) -> p j i", i=8), in0=GM3,
          in1=M0[:, :].unsqueeze(2).broadcast_to([4, 16, 8]), op=ALU.subtract)
        I("dve", "tensor_copy", [M.G_Mr], [MTer], out=MTe[:, :].rearrange("p (j i) -> p j i", i=8),
          in_=GM3[:, :, 7:8].broadcast_to([4, 16, 8]))
        I("dve", "tensor_tensor", [M.G_Mr, M0r], [DMTr], out=DMT[:, :].unsqueeze(2), in0=GM3[:, :, 7:8], in1=M0[:, :].unsqueeze(2),
          op=ALU.subtract)
        P.dma("sp", ms_o.rearrange("j h -> h j"), M.G_BM[:, 0:128].rearrange("p (j i) -> p j i", i=8)[:, :, 7], M.G_BMr,
              reads=[M.G_BMr], allow_slow_non_contiguous=True)

        pair_i = [0]
        QS = A([2, 16, 32], BF16, parts=64); QSr = AR("QS")
        for h in range(2):
            I("act", "activation", [M.QTr], [QSr], out=QS[:, h, :, :].rearrange("p j (g i) -> p j g i", i=8),
              in_=M.QT[:, 4 * h:4 * h + 4, :].rearrange("p g (j i) -> p j g i", i=8), func=AF.Copy)
        for h in range(2):
            for q4 in range(2):
                for jj in range(8):
                    j = q4 * 8 + jj
                    I("pe", "transpose", [CKnr, identr], [tbr], out=tb[0:64, jj * 128:(jj + 1) * 128],
                      in_=CKn[:, j, h * 64:(h + 1) * 64], identity=identb[:])
                I("act", "activation", [tbr], [CKTr], out=CKT[:, q4 * 8:(q4 + 1) * 8, :],
                  in_=tb[0:64, :].rearrange("p (a b) -> p a b", a=8), func=AF.Copy)
            for half in range(2):
                po, por = bank()
                pend = []
                for jj in range(8):
                    j = half * 8 + jj
                    b = pair_i[0] % 2
                    pair_i[0] += 1
                    bk, bkr = bank()
                    lq = QS[:, h, j, :]
                    mm(bk[0:32, 0:128], lq, CKT[:, j, :], True, False, [QSr, CKTr], bkr)
                    mm(bk[0:32, 0:128], identb[0:32, 0:32], SMC, False, True, [identr, SMCr], bkr)
                    mm(bk[0:32, 128:256], lq, M.KT[:, h, 0:128], True, False, [QSr, M.KTr[0]], bkr)
                    mm(bk[0:32, 128:256], identb[0:32, 0:32], SMN[:, j, :], False, True, [identr, SMNr], bkr)
                    st_ = sms[:, b, :]
                    I("dve", "reduce_max", [bkr], [smsr[b]], out=st_[:, 0:1], in_=bk[0:32, 0:256], axis=AX.X)
                    I("dve", "tensor_scalar", [smsr[b]], [smsr[b]], out=st_[:, 0:1], in0=st_[:, 0:1], scalar1=-0.125, scalar2=None,
                      op0=ALU.mult)
                    I("dve", "tensor_tensor", [smsr[b], SINKCr], [smsr[b]], out=st_[:, 0:1], in0=st_[:, 0:1],
                      in1=SINKC[:, 2 + h:3 + h], op=ALU.min)
                    I("act", "activation", [bkr, smsr[b]], [PNsr[b], smsr[b]], out=PNs[:, b, :], in_=bk[0:32, 0:256], func=AF.Exp,
                      bias=st_[:, 0:1], scale=0.125, accum_out=st_[:, 1:2])
                    I("act", "activation", [SINKCr, smsr[b]], [smsr[b]], out=st_[:, 2:3], in_=SINKC[:, h:h + 1], func=AF.Exp,
                      bias=st_[:, 0:1])
                    I("dve", "tensor_tensor", [smsr[b]], [smsr[b]], out=st_[:, 2:3], in0=st_[:, 2:3], in1=st_[:, 1:2], op=ALU.add)
                    I("dve", "reciprocal", [smsr[b]], [smsr[b]], out=st_[:, 3:4], in_=st_[:, 2:3])
                    I("dve", "tensor_scalar", [PNsr[b], smsr[b]], [PNsr[b]], out=PNs[:, b, :], in0=PNs[:, b, :],
                      scalar1=st_[:, 3:4], scalar2=None, op0=ALU.mult)
                    for c2 in range(2):
                        I("pe", "transpose", [PNsr[b], identr], [tbr], out=tb[:, jj * 64 + c2 * 32:jj * 64 + (c2 + 1) * 32],
                          in_=PNs[:, b, c2 * 128:(c2 + 1) * 128], identity=identb[0:32, 0:32])
                I("dve", "tensor_copy", [tbr], [PTSr], out=PTS[:, 0:512], in_=tb[:, 0:512])
                for jj in range(8):
                    j = half * 8 + jj
                    o = po[0:64, jj * 32:(jj + 1) * 32]
                    mm(o, CV[:, j, h * 64:(h + 1) * 64], PTS[:, jj * 64:jj * 64 + 32], True, False, [CVr, PTSr], por)
                    mm(o, M.Vt[:, 0, h * 64:(h + 1) * 64], PTS[:, jj * 64 + 32:jj * 64 + 64], False, True, [M.Vtr[0], PTSr], por)
                I("act", "activation", [por], [M.ATTTr[0]],
                  out=M.ATTT[:, 4 * h:4 * h + 4, half * 64:(half + 1) * 64].rearrange("p g (j i) -> p j g i", i=8),
                  in_=po[0:64, 0:256].rearrange("p (j g i) -> p j g i", g=4, i=8), func=AF.Copy)

        off = 0
        SCf, off = A_at(off, [64, 64], F32); SCfr = ARalias("SCf", R0_olds)
        SCT, off = A_at(off, [64, 128], BF16, parts=64); SCTr = ARalias("SCT", R0_olds)
        QN, off = A_at(off, [4, 128], BF16, parts=64); QNr = ARalias("QN", R0_olds)
        KJ, off = A_at(off, [16, 64], BF16); KJr = ARalias("KJ", R0_olds)
        assert off <= 18496
        P.dma("sp", SCf[:, :, :], sC_d.rearrange("j h p k -> p (j h) k"), SCfr, writes=[SCfr])
        for p4 in range(16):
            bk, bkr = bank()
            for q_ in range(4):
                pr = p4 * 4 + q_
                mm(bk[0:64, q_ * 128:(q_ + 1) * 128], SCf[:, pr, :], identf[:], True, True, [SCfr, identfr], bkr)
            I("act", "activation", [bkr], [SCTr], out=SCT[:, p4 * 4:(p4 + 1) * 4, :],
              in_=bk[0:64, :].rearrange("p (a b) -> p a b", a=4), func=AF.Copy)
        bk, bkr = bank()
        mm(bk[0:64, 0:64], SNn, identf[0:64, 0:64], True, True, [SNnr, identfr], bkr)
        I("act", "activation", [bkr], [SNTr], out=SNT, in_=bk[0:64, 0:64], func=AF.Copy)

        def sample_inter(kind, h=None, pb=None, pbr=None):
            if kind == "pre":
                I("dve", "tensor_tensor", [M.QWr, SNTr], [QNr], out=QN[:, :, :].rearrange("p h (j i) -> p h j i", i=8),
                  in0=M.QW[:, :, :].rearrange("p h (j i) -> p h j i", i=8),
                  in1=SNT.rearrange("p (j h) -> p h j", h=4).unsqueeze(3).broadcast_to([64, 4, 16, 8]), op=ALU.mult)
            elif kind == "num":
                for j in range(16):
                    mm(pb[:, h * 128 + 8 * j:h * 128 + 8 * j + 8], SCT[:, j * 4 + h, :], M.QW[:, h, 8 * j:8 * j + 8], False, j == 15,
                       [SCTr, M.QWr], pbr)
            else:
                mm(pb[:, h * 128:(h + 1) * 128], onesb[0:64, :], QN[:, h, :], False, True, [onesbr, QNr], pbr)

        pn = mlstm_chunk(0, 0, mbcs, mbcsr, inter_fn=sample_inter)
        mlstm_finish(*pn, 0, M.HMTr[0])
        wout_tile(0, TS)

        pw, pwr = bank()
        for h in range(4):
            mm(pw[:, h:h + 1], M.G_A[:, 0:128], SELh(h, 1), True, False, [M.G_Ar, selr], pwr)
            mm(pw[:, h:h + 1], MTe, SEL[:, 512 + h * 128:512 + h * 128 + 1], False, True, [MTer, selr], pwr)
        for h in range(4):
            mm(pw[:, 8 + 16 * h:8 + 16 * (h + 1)], NSELh(h), DMT, True, True, [DMTr, selr], pwr)
        I("act", "activation", [pwr], [M.WKCr], out=M.WKC[:, 0:4], in_=pw[:, 0:4], func=AF.Exp)
        I("act", "activation", [pwr], [WCBr], out=WCB[:, :, :], in_=pw[:, 8:72].rearrange("p (h j) -> p h j", h=4), func=AF.Exp)
        for h in range(4):
            I("dve", "tensor_scalar", [M.MVaugr[0], M.WKCr], [M.VWr[h]], out=M.VW[:, h, :], in0=M.MVaug[:, 0, h, :],
              scalar1=M.WKC[:, h:h + 1], scalar2=None, op0=ALU.mult)
            I("dve", "tensor_scalar", [E16r, M.WKCr], [EWr], out=EW[:, h, :], in0=E16, scalar1=M.WKC[:, h:h + 1], scalar2=None,
              op0=ALU.mult)
        bk, bkr = bank()
        for h in range(4):
            mm(bk[0:64, h * 16:(h + 1) * 16], M.MKtok[:, 0, h * 64:(h + 1) * 64], EW[:, h, :], True, True, [M.MKtokr[0], EWr], bkr)
        I("dve", "tensor_tensor", [SNTr, WCBr], [NNTr], out=NNT.rearrange("p (j h) -> p h j", h=4),
          in0=SNT.rearrange("p (j h) -> p h j", h=4), in1=WCB[0:64, :, :], op=ALU.mult)
        I("dve", "tensor_tensor", [NNTr, bkr], [NNTr], out=NNT.rearrange("p (j h) -> p h j", h=4),
          in0=NNT.rearrange("p (j h) -> p h j", h=4), in1=bk[0:64, 0:64].rearrange("p (h j) -> p h j", h=4), op=ALU.add)
        bk2, bk2r = bank()
        mm(bk2[0:64, 0:64], NNT, identf[0:64, 0:64], True, True, [NNTr, identfr], bk2r)
        I("act", "activation", [bk2r], [NNor], out=NNo, in_=bk2[0:64, 0:64], func=AF.Copy)
        P.dma("sp", ns_o.rearrange("j h k -> (j h) k"), NNo, NNor, reads=[NNor])
        for h in range(4):
            I("dve", "tensor_tensor", [M.MKtokr[0], E16r], [KJr], out=KJ[:, :, :],
              in0=M.MKtok[:, 0, h * 64:(h + 1) * 64].unsqueeze(1).broadcast_to([128, 16, 64]),
              in1=E16.unsqueeze(2).broadcast_to([128, 16, 64]), op=ALU.mult)
            for half in range(2):
                bk, bkr = bank()
                mm(bk[:, :], M.VW[:, h, 0:128], KJ[:, half * 8:(half + 1) * 8, :], True, True, [M.VWr[h], KJr], bkr)
                scv = SCf[:, :, :].rearrange("p (j h) k -> p h j k", h=4)[:, h, half * 8:(half + 1) * 8, :]
                I("dve", "tensor_tensor", [SCfr, WCBr], [SCfr], out=scv, in0=scv,
                  in1=WCB[:, h, half * 8:(half + 1) * 8].unsqueeze(2).broadcast_to([128, 8, 64]), op=ALU.mult)
                I("dve", "tensor_tensor", [SCfr, bkr], [SCfr], out=scv, in0=scv,
                  in1=bk[:, :].rearrange("p (j k) -> p j k", k=64), op=ALU.add)
        P.dma("sp", Cs_o.rearrange("j h p k -> p (j h) k"), SCf[:, :, :], SCfr, reads=[SCfr])

        def dump_y(tiles):
            yo = yp_o.rearrange("(t p) d -> t p d", p=128)
            for t in tiles:
                if Yr[t].last_w is None:
                    continue
                if t < NTP:
                    P.dma("sp", yo[t], Y[:, t, :], Yr[t], reads=[Yr[t]])
                else:
                    P.dma("sp", ys_o, Y[:, t, :], Yr[t], reads=[Yr[t]])

        if stage <= 1:
            dump_y(range(NT))
            P.emit()
            return nc, P

        new_phase()
        WCQ = A([8, 256], BF16); WCQr = AR("WCQ")
        WCKV = A([8, 512], BF16); WCKVr = AR("WCKV")
        WCO = A([4, 1024], BF16, parts=64); WCOr = AR("WCO")
        P.dma("pool", WCQ[:], w_cq_d.rearrange("(k p) n -> p k n", p=128), WCQr, writes=[WCQr])
        P.dma("pool", WCKV[:, :, 0:256], w_ck_d.rearrange("(k p) n -> p k n", p=128), WCKVr, writes=[WCKVr], group=True)
        P.dma("pool", WCKV[:, :, 256:512], w_cv_d.rearrange("(k p) n -> p k n", p=128), WCKVr, writes=[WCKVr], group=True)
        P.dma("pool", WCO[:], w_co_d.rearrange("(h d) n -> d h n", d=64), WCOr, writes=[WCOr])
        MEMX = A([2, D], F32); MEMXr = [AR("MEMX0"), AR("MEMX1")]
        MNT = A([8, 256], BF16); MNTr = [AR("MNT0"), AR("MNT1")]
        MKTm = A([4, 256], BF16, parts=64); MKTmr = AR("MKTm")
        MVm = A([2, 256], BF16); MVmr = AR("MVm")
        MKVo = A([2, 512], F32); MKVor = [AR("MKVo0"), AR("MKVo1")]
        GB = 4
        XNTb = A([8, GB * 128], BF16); XNTbr = [AR(f"XNTb{i}") for i in range(GB)]
        QcT = A([4, GB * 128], BF16, parts=64); QcTr = AR("QcT")
        OcT = A([4, GB * 128], BF16, parts=64); OcTr = [AR(f"OcT{i}") for i in range(GB)]
        Eb2 = A([4, 256], BF16); Eb2r = AR("Eb2")
        PT2 = A([1, 1024], BF16); PT2r = AR("PT2")
        sm2 = A([1, 32], F32)[:, 0, :]; sm2r = AR("sm2")
        mem_t = mem_d.rearrange("(t p) d -> t p d", p=128)
        import os
        SK = os.environ.get("SKIP", "")
        for mt in range(2):
            P.dma("sp", MEMX[:, mt, :], mem_t[mt], MEMXr[mt], writes=[MEMXr[mt]])
        for mt in (range(2) if "noBnorm" not in SK else []):
            norm_T(MEMX[:, mt, :], MEMXr[mt], 2, MNT[:, :, mt * 128:(mt + 1) * 128], [MNTr[mt]])
        for mt in (range(2) if "noBkv" not in SK else []):
            bk, bkr = bank()
            for k in range(8):
                mm(bk[:, :], MNT[:, k, mt * 128:(mt + 1) * 128], WCKV[:, k, :], k == 0, k == 7, [MNTr[mt], WCKVr], bkr)
            if "noBcp1" not in SK:
                I("act", "activation", [bkr], [MKVor[mt]], out=MKVo[:, mt, :], in_=bk[:, :], func=AF.Copy)
            if "noBcp2" not in SK:
                I("act", "activation", [bkr], [MVmr], out=MVm[:, mt, :], in_=bk[:, 256:512], func=AF.Copy)
            if "noBdma" not in SK:
                P.dma("sp", memk_o[mt * 128:(mt + 1) * 128, :], MKVo[:, mt, 0:256], MKVor[mt], reads=[MKVor[mt]], group=True)
                P.dma("sp", memv_o[mt * 128:(mt + 1) * 128, :], MKVo[:, mt, 256:512], MKVor[mt], reads=[MKVor[mt]], group=True)
        for h0 in ((0, 2) if "noBkt" not in SK else []):
            bk, bkr = bank()
            for hh in range(2):
                h = h0 + hh
                for k in range(8):
                    mm(bk[0:64, hh * 256:(hh + 1) * 256], WCKV[:, k, h * 64:(h + 1) * 64], MNT[:, k, :], k == 0, k == 7,
                       [WCKVr] + MNTr, bkr)
            I("act", "activation", [bkr], [MKTmr], out=MKTm[:, h0:h0 + 2, :],
              in_=bk[0:64, :].rearrange("p (a b) -> p a b", a=2), func=AF.Copy)

        def cross_q(ntok, xres):
            for h in range(4):
                bk, bkr = bank()
                for k in range(8):
                    mm(bk[0:64, 0:ntok], WCQ[:, k, h * 64:(h + 1) * 64], XNTb[:, k, 0:ntok], k == 0, k == 7, [WCQr] + xres, bkr)
                I("act", "activation", [bkr], [QcTr], out=QcT[:, h, 0:ntok], in_=bk[0:64, 0:ntok], func=AF.Copy)

        def cross_tile_prompt(ti):
            qs = slice(ti * 128, (ti + 1) * 128)
            bks = [bank(), bank()]
            for h in range(4):
                bk, bkr = bks[h // 2]
                mm(bk[:, (h % 2) * 256:(h % 2 + 1) * 256], QcT[:, h, qs], MKTm[:, h, :], True, True, [QcTr, MKTmr], bkr)
            for j in range(2):
                I("dve", "reduce_max", [bks[j][1]], [sm2r], out=sm2[:, 2 * j:2 * j + 2],
                  in_=bks[j][0][:, :].rearrange("p (a b) -> p a b", a=2), axis=AX.X)
            I("dve", "tensor_scalar", [sm2r], [sm2r], out=sm2[:, 0:4], in0=sm2[:, 0:4], scalar1=-0.125, scalar2=None, op0=ALU.mult)
            for h in range(4):
                bk, bkr = bks[h // 2]
                I("act", "activation", [bkr, sm2r], [Eb2r, sm2r], out=Eb2[:, h, :], in_=bk[:, (h % 2) * 256:(h % 2 + 1) * 256],
                  func=AF.Exp, bias=sm2[:, h:h + 1], scale=0.125, accum_out=sm2[:, 4 + h:5 + h])
            I("dve", "reciprocal", [sm2r], [sm2r], out=sm2[:, 8:12], in_=sm2[:, 4:8])
            for h in range(4):
                if h % 2 == 0:
                    I("act", "activation", [Eb2r, sm2r], [Eb2r], out=Eb2[:, h, :], in_=Eb2[:, h, :], func=AF.Copy,
                      scale=sm2[:, 8 + h:9 + h])
                else:
                    I("dve", "tensor_scalar", [Eb2r, sm2r], [Eb2r], out=Eb2[:, h, :], in0=Eb2[:, h, :],
                      scalar1=sm2[:, 8 + h:9 + h], scalar2=None, op0=ALU.mult)
            for mc in range(2):
                for h in range(4):
                    I("pe", "transpose", [Eb2r, identr], [tbr], out=tb[:, (mc * 4 + h) * 128:(mc * 4 + h + 1) * 128],
                      in_=Eb2[:, h, mc * 128:(mc + 1) * 128], identity=identb[:])
            I("dve", "tensor_copy", [tbr], [PT2r], out=PT2[:, 0, :], in_=tb[:, :])
            po, por = bank()
            for h in range(4):
                for mc in range(2):
                    mm(po[0:64, h * 128:(h + 1) * 128], MVm[:, mc, h * 64:(h + 1) * 64],
                       PT2[:, 0, (mc * 4 + h) * 128:(mc * 4 + h + 1) * 128], mc == 0, mc == 1, [MVmr, PT2r], por)
            I("act", "activation", [por], [OcTr[ti]], out=OcT[:, :, qs], in_=po[0:64, :].rearrange("p (h q) -> p h q", h=4),
              func=AF.Copy)

        def wco_tile(ti, t):
            qs = slice(ti * 128, (ti + 1) * 128)
            for c in range(2):
                bk, bkr = bank()
                cc = slice(c * 512, (c + 1) * 512)
                for h in range(4):
                    mm(bk[:, :], OcT[:, h, qs], WCO[:, h, cc], h == 0, h == 3, [OcTr[ti], WCOr], bkr)
                I("dve", "tensor_tensor", [Yr[t], bkr], [Yr[t]], out=Y[:, t, cc], in0=Y[:, t, cc], in1=bk[:, :], op=ALU.add)

        import os
        for g0 in (range(0, NTP, GB) if "noBloop" not in os.environ.get("SKIP", "") else []):
            for ti in range(GB):
                norm_T(Y[:, g0 + ti, :], Yr[g0 + ti], 1, XNTb[:, :, ti * 128:(ti + 1) * 128], [XNTbr[ti]])
            cross_q(GB * 128, XNTbr)
            for ti in range(GB):
                cross_tile_prompt(ti)
                wco_tile(ti, g0 + ti)
        TS = NTP
        CMn = A([8, 2, 256], BF16); CMnr = AR("CMn")
        CMV = A([16, 2, 256], BF16); CMVr = AR("CMV")
        CMKT = A([8, 4, 256], BF16, parts=64); CMKTr = AR("CMKT")
        Es = A([2, 4, 256], BF16, parts=8); Esr = [AR("Es0"), AR("Es1")]
        sm3 = A([2, 16], F32, parts=8); sm3r = [AR("sm30"), AR("sm31")]
        PT3 = A([1, 1024], BF16)[:, 0, :]; PT3r = AR("PT3")
        P.dma("pool", CMV[:, :, :, :], cmv_d.rearrange("j (c p) f -> p j c f", p=128), CMVr, writes=[CMVr])
        norm_T(Y[:, TS, :], Yr[TS], 1, XNTb[:, :, 0:128], [XNTbr[0]])
        cross_q(128, [XNTbr[0]])
        po3, po3r = banks[5], bres[5]
        for half in range(2):
            P.dma("pool", CMn[:, :, :, :], cmk_d[half * 8:(half + 1) * 8].rearrange("j (c p) f -> p j c f", p=128), CMnr,
                  writes=[CMnr])
            for jj in range(8):
                for h in range(4):
                    for mc in range(2):
                        I("pe", "transpose", [CMnr, identr], [tbr], out=tb[0:64, (h * 2 + mc) * 128:(h * 2 + mc + 1) * 128],
                          in_=CMn[:, jj, mc, h * 64:(h + 1) * 64], identity=identb[:])
                I("act", "activation", [tbr], [CMKTr], out=CMKT[:, jj, :, :],
                  in_=tb[0:64, :].rearrange("p (h m) -> p h m", h=4), func=AF.Copy)
            for jj in range(8):
                j = half * 8 + jj
                b = j % 2
                bks = [bank(), bank()]
                for h in range(4):
                    bk, bkr = bks[h // 2]
                    mm(bk[0:8, (h % 2) * 256:(h % 2 + 1) * 256], QcT[:, h, 8 * j:8 * j + 8], CMKT[:, jj, h, :], True, True,
                       [QcTr, CMKTr], bkr)
                st_ = sm3[:, b, :]
                for q_ in range(2):
                    I("dve", "reduce_max", [bks[q_][1]], [sm3r[b]], out=st_[:, 2 * q_:2 * q_ + 2],
                      in_=bks[q_][0][0:8, :].rearrange("p (a b) -> p a b", a=2), axis=AX.X)
                I("dve", "tensor_scalar", [sm3r[b]], [sm3r[b]], out=st_[:, 0:4], in0=st_[:, 0:4], scalar1=-0.125, scalar2=None,
                  op0=ALU.mult)
                for h in range(4):
                    bk, bkr = bks[h // 2]
                    I("act", "activation", [bkr, sm3r[b]], [Esr[b], sm3r[b]], out=Es[:, b, h, :],
                      in_=bk[0:8, (h % 2) * 256:(h % 2 + 1) * 256], func=AF.Exp, bias=st_[:, h:h + 1], scale=0.125,
                      accum_out=st_[:, 4 + h:5 + h])
                I("dve", "reciprocal", [sm3r[b]], [sm3r[b]], out=st_[:, 8:12], in_=st_[:, 4:8])
                I("dve", "tensor_tensor", [Esr[b], sm3r[b]], [Esr[b]], out=Es[:, b, :, :], in0=Es[:, b, :, :],
                  in1=st_[:, 8:12].unsqueeze(2).broadcast_to([8, 4, 256]), op=ALU.mult)
                for mc in range(2):
                    for h in range(4):
                        c0_ = j * 64 + (mc * 4 + h) * 8
                        I("pe", "transpose", [Esr[b], identr], [tbr], out=tb[:, c0_:c0_ + 8],
                          in_=Es[:, b, h, mc * 128:(mc + 1) * 128], identity=identb[0:8, 0:8])
            I("dve", "tensor_copy", [tbr], [PT3r], out=PT3[:, half * 512:(half + 1) * 512], in_=tb[:, half * 512:(half + 1) * 512])
        for j in range(16):
            for h in range(4):
                for mc in range(2):
                    c0_ = j * 64 + (mc * 4 + h) * 8
                    mm(po3[0:64, (j * 4 + h) * 8:(j * 4 + h) * 8 + 8], CMV[:, j, mc, h * 64:(h + 1) * 64], PT3[:, c0_:c0_ + 8],
                       mc == 0, mc == 1, [CMVr, PT3r], po3r)
        I("act", "activation", [po3r], [OcTr[0]], out=OcT[:, :, 0:128].rearrange("p h (j i) -> p j h i", i=8),
          in_=po3[0:64, :].rearrange("p (j h i) -> p j h i", h=4, i=8), func=AF.Copy)
        wco_tile(0, TS)

        if stage <= 2:
            dump_y(range(NT))
            P.emit()
            return nc, P

        new_phase()
        NF = FH // 128
        XNTa = A([8, NT * 128], BF16); XNTar = [AR(f"XNTa{t}") for t in range(NT)]
        NSLOT = 12
        WG = A([NSLOT, 8, 128], BF16); WU = A([NSLOT, 8, 128], BF16); WD = A([NSLOT, D], BF16)
        Wsr = [AR(f"Ws{s_}") for s_ in range(NSLOT)]
        Hh = A([6, 512], BF16); Hr = [AR(f"H{j}") for j in range(6)]
        SG = A([2, 512], BF16); SGr = [AR("SG0"), AR("SG1")]
        OUT = A([1, D], F32)[:, 0, :]; OUTr = AR("OUT")
        gfin = A([1, D], F32)[:, 0, :]; gfinr = AR("gfin")
        P.dma("sp", gfin, g_final_d.partition_broadcast(128), gfinr, writes=[gfinr])
        passes = [list(range(0, 6)), list(range(6, 12)), list(range(12, 17)), list(range(17, 22))]
        groups = [(0, 4), (4, 4), (8, 4), (12, 4), (16, 1)]
        wd_v = w_down_d.rearrange("(f p) n -> f p n", p=128)
        wslot = {}
        nload = [0]

        def load_w(f):
            s_ = nload[0] % NSLOT
            nload[0] += 1
            wslot[f] = s_
            P.dma("pool", WG[:, s_], w_gate_d[:, f * 128:(f + 1) * 128].rearrange("(k p) n -> p k n", p=128), Wsr[s_],
                  writes=[Wsr[s_]], group=True)
            P.dma("pool", WU[:, s_], w_up_d[:, f * 128:(f + 1) * 128].rearrange("(k p) n -> p k n", p=128), Wsr[s_],
                  writes=[Wsr[s_]], group=True)
            P.dma("pool", WD[:, s_], wd_v[f], Wsr[s_], writes=[Wsr[s_]], group=True)

        for f in passes[0]:
            load_w(f)
        gcnt = [0]
        for pi, fl in enumerate(passes):
            for gi, (t0, n) in enumerate(groups):
                if pi == 0:
                    for t in range(t0, t0 + n):
                        norm_T(Y[:, t, :], Yr[t], 3, XNTa[:, :, t * 128:(t + 1) * 128], [XNTar[t]])
                if pi + 1 < len(passes) and gi == 0:
                    for f in passes[pi + 1]:
                        load_w(f)
                ntok = n * 128
                tok = slice(t0 * 128, t0 * 128 + ntok)
                xr = [XNTar[t] for t in range(t0, t0 + n)]
                for j, f in enumerate(fl):
                    s_ = wslot[f]
                    b = gcnt[0] % 2
                    gcnt[0] += 1
                    pg, pgr = bank()
                    pu, pur = bank()
                    for k in range(8):
                        mm(pg[:, 0:ntok], WG[:, s_, k, :], XNTa[:, k, tok], k == 0, k == 7, [Wsr[s_]] + xr, pgr)
                    for k in range(8):
                        mm(pu[:, 0:ntok], WU[:, s_, k, :], XNTa[:, k, tok], k == 0, k == 7, [Wsr[s_]] + xr, pur)
                    I("act", "activation", [pgr], [SGr[b]], out=SG[:, b, 0:ntok], in_=pg[:, 0:ntok], func=AF.Silu)
                    I("dve", "tensor_tensor", [SGr[b], pur], [Hr[j]], out=Hh[:, j, 0:ntok], in0=SG[:, b, 0:ntok],
                      in1=pu[:, 0:ntok], op=ALU.mult)
                for ti in range(n):
                    t = t0 + ti
                    for c in range(2):
                        pd, pdr = bank()
                        for j, f in enumerate(fl):
                            s_ = wslot[f]
                            mm(pd[:, :], Hh[:, j, ti * 128:(ti + 1) * 128], WD[:, s_, c * 512:(c + 1) * 512], j == 0,
                               j == len(fl) - 1, [Hr[j], Wsr[s_]], pdr)
                        I("dve", "tensor_tensor", [Yr[t], pdr], [Yr[t]], out=Y[:, t, c * 512:(c + 1) * 512],
                          in0=Y[:, t, c * 512:(c + 1) * 512], in1=pd[:, :], op=ALU.add)
                if pi == len(passes) - 1:
                    yo = yp_o.rearrange("(t p) d -> t p d", p=128)
                    for t in range(t0, t0 + n):
                        rstd, sr = norm_stats(Y[:, t, :], Yr[t], 0)
                        I("dve", "scalar_tensor_tensor", [Yr[t], sr, gfinr], [OUTr], out=OUT, in0=Y[:, t, :], scalar=rstd,
                          in1=gfin, op0=ALU.mult, op1=ALU.mult)
                        P.dma("sp", (yo[t] if t < NTP else ys_o), OUT, OUTr, reads=[OUTr])
        P.emit()
        return nc, P


def make_consts(hf):
    c = {}
    c["c_ident"] = np.eye(128, dtype=np.float32)
    i = np.arange(128)[:, None]; j = np.arange(256)[None, :]
    band = np.where((j >= i) & (j <= i + 128), 0.0, NEG).astype(np.float32)
    first = band.copy()
    if hf == 0:
        first[:, :128] = NEG
    c["c_mb_band"] = band; c["c_mb_first"] = first
    s = np.arange(128)[:, None]; t = np.arange(128)[None, :]
    c["c_mb_caus"] = np.where(s <= t, 0.0, NEG).astype(np.float32)
    c["c_mb_causs"] = np.where((s <= t) & (s // 8 == t // 8), 0.0, NEG).astype(np.float32)
    sel = np.zeros((4, 1024), np.float32)
    for h in range(4):
        sel[h, h * 128:(h + 1) * 128] = 1.0
        sel[h, 512 + h * 128:512 + (h + 1) * 128] = -1.0
    c["c_sel"] = sel
    pm = np.zeros((4, 2), np.float32)
    pm[:, 0] = 1.0 if hf else 0.0
    pm[:, 1] = 0.0 if hf else NEG
    c["c_pmask"] = pm
    r = np.arange(32)[:, None] % 8
    p = np.arange(128)[None, :]
    c["c_smc"] = np.where(p >= r, 0.0, NEG).astype(np.float32)
    smn = np.full((32, 16, 128), NEG, np.float32)
    for jq in range(16):
        for ii in range(8):
            smn[(np.arange(32) % 8) >= ii, jq, jq * 8 + ii] = 0.0
    c["c_smn"] = smn
    c["c_bt"] = np.where((s <= t) & (s // 8 == t // 8), 1.0, 0.0).astype(np.float32)
    e = np.zeros((128, 16), np.float32); e[np.arange(128), np.arange(128) // 8] = 1.0
    c["c_eseq"] = e
    return c

def shard_inputs(inp):
    maps = []
    W = ["w_in", "b_igate", "b_fgate", "attn_sinks", "g_mlstm_head", "w_out", "g_mix", "g_cross", "g_mem",
         "w_cq", "w_ck", "w_cv", "w_co", "g_ffn", "w_gate", "w_up", "w_down"]
    wd = {k: np.ascontiguousarray(np.asarray(inp[k], np.float32)[0]) for k in W}
    wd["g_final"] = np.ascontiguousarray(np.asarray(inp["g_final"], np.float32))
    xp = np.asarray(inp["x_prompt"], np.float32); xs = np.asarray(inp["x_sample"], np.float32)
    for c in range(8):
        b, hf = c // 2, c % 2
        m = dict(wd)
        m["xp"] = np.ascontiguousarray(xp[b, hf * 2048:(hf + 1) * 2048])
        m["xpre"] = np.ascontiguousarray(xp[b, 0:2048]) if hf else np.zeros((2048, 1024), np.float32)
        m["xs"] = np.ascontiguousarray(xs[16 * c:16 * c + 16].reshape(128, 1024))
        m["mem"] = np.ascontiguousarray(np.asarray(inp["mem_prompt"], np.float32)[b])
        sl = slice(16 * c, 16 * c + 16)
        m["csk"] = np.ascontiguousarray(np.asarray(inp["cache_swa_k"], np.float32)[0, sl].reshape(16, 128, 128))
        m["csv"] = np.ascontiguousarray(np.asarray(inp["cache_swa_v"], np.float32)[0, sl].reshape(16, 128, 128))
        m["sC"] = np.ascontiguousarray(np.asarray(inp["state_mlstm_C"], np.float32)[0, sl])
        m["sn"] = np.ascontiguousarray(np.asarray(inp["state_mlstm_n"], np.float32)[0, sl])
        m["sm"] = np.ascontiguousarray(np.asarray(inp["state_mlstm_m"], np.float32)[0, sl])
        m["cmk"] = np.ascontiguousarray(np.asarray(inp["cache_mem_k"], np.float32)[0, sl].reshape(16, 256, 256))
        m["cmv"] = np.ascontiguousarray(np.asarray(inp["cache_mem_v"], np.float32)[0, sl].reshape(16, 256, 256))
        m.update(make_consts(hf))
        sk = wd["attn_sinks"]
        sc = np.zeros((32, 2), np.float32)
        for h in range(2):
            sc[:, h] = sk[4 * h + np.arange(32) // 8]
        m["c_sinkcol"] = sc
        maps.append(m)
    return maps

def gather(res):
    f = np.float32
    yp = np.zeros((4, 4096, 1024), f); ys = np.zeros((128, 8, 1024), f)
    skp = np.zeros((1, 4, 128, 2, 64), f); svp = np.zeros_like(skp)
    Cp = np.zeros((1, 4, 4, 128, 64), f); npp = np.zeros((1, 4, 4, 64), f); mp = np.zeros((1, 4, 4), f)
    mkp = np.zeros((1, 4, 256, 4, 64), f); mvp = np.zeros_like(mkp)
    sks = np.zeros((1, 128, 128, 2, 64), f); svs = np.zeros_like(sks)
    Cs = np.zeros((1, 128, 4, 128, 64), f); ns = np.zeros((1, 128, 4, 64), f); ms = np.zeros((1, 128, 4), f)
    for c in range(8):
        r = res[c]; b, hf = c // 2, c % 2
        yp[b, hf * 2048:(hf + 1) * 2048] = r["yp"]
        ys[16 * c:16 * c + 16] = r["ys"].reshape(16, 8, 1024)
        if hf == 1:
            skp[0, b] = r["swak"].reshape(128, 2, 64); svp[0, b] = r["swav"].reshape(128, 2, 64)
            Cp[0, b] = r["Cp"]; npp[0, b] = r["np"]; mp[0, b] = r["mp"].reshape(4)
        else:
            mkp[0, b] = r["memk"].reshape(256, 4, 64); mvp[0, b] = r["memv"].reshape(256, 4, 64)
        sl = slice(16 * c, 16 * c + 16)
        sks[0, sl] = r["sks"].reshape(16, 128, 2, 64); svs[0, sl] = r["svs"].reshape(16, 128, 2, 64)
        Cs[0, sl] = r["Cs"]; ns[0, sl] = r["ns"]; ms[0, sl] = r["ms"]
    return (yp, ys, skp, svp, Cp, npp, mp, mkp, mvp, sks, svs, Cs, ns, ms)


_CACHE = {}


def kernel(**inputs):
    if "nc" not in _CACHE:
        _CACHE["nc"] = build_program(3)[0]
    nc = _CACHE["nc"]
    maps = shard_inputs(inputs)
    res = run_bass_kernel_spmd(nc, maps, core_ids=list(range(8)))
    return gather(res.results)
```

```python
import contextlib
from concourse.bass_utils import run_bass_kernel_spmd
import numpy as np
import concourse.bass as bass
import concourse.mybir as mybir

F32 = mybir.dt.float32
BF16 = mybir.dt.bfloat16
I32 = mybir.dt.int32
AF = mybir.ActivationFunctionType
ALU = mybir.AluOpType
AX = mybir.AxisListType

ENGS = ("pe", "act", "dve", "pool", "sp")


class Res:
    __slots__ = ("name", "last_w", "readers", "sem", "dcount", "excl")

    def __init__(self, name):
        self.name = name
        self.last_w = None
        self.readers = []
        self.sem = None
        self.dcount = 0
        self.excl = False


class Op:
    __slots__ = ("eng", "fn", "deps", "dma_res", "sig", "cnt", "k", "group")

    def __init__(self, eng, fn, dma_res):
        self.eng = eng
        self.fn = fn
        self.deps = set()
        self.dma_res = dma_res
        self.sig = False
        self.cnt = 0
        self.k = 0


class Prog:
    def __init__(self, nc):
        self.nc = nc
        self.ops = []
        self.nres = 0
        self.inherit = []
        self.phase_res = []

    def res(self, name=None, arena=False):
        self.nres += 1
        r = Res(name or f"r{self.nres}")
        if arena:
            r.readers = list(self.inherit)
            self.phase_res.append(r)
        return r

    def new_phase(self):
        inh = set(self.inherit)
        for r in self.phase_res:
            if r.last_w is not None:
                inh.add(r.last_w)
            inh.update(r.readers)
        self.inherit = sorted(inh)
        self.phase_res = []

    def rec_begin(self):
        self._rec = []

    def rec_end(self):
        r = self._rec
        self._rec = None
        return r

    def merge(self, streams):
        streams = [s_ for s_ in streams if s_]
        pos = [0] * len(streams)
        while True:
            best = None
            for k, s_ in enumerate(streams):
                if pos[k] < len(s_):
                    f = pos[k] / len(s_)
                    if best is None or f < best[0]:
                        best = (f, k)
            if best is None:
                break
            k = best[1]
            a, kw = streams[k][pos[k]]
            pos[k] += 1
            self.op(*a, **kw)

    def op(self, eng, fn, reads=(), writes=(), dma_res=None, accum=False, group=False):
        if getattr(self, "_rec", None) is not None:
            self._rec.append(((eng, fn, tuple(reads), tuple(writes)), dict(dma_res=dma_res, accum=accum, group=group)))
            return None
        i = len(self.ops)
        o = Op(eng, fn, dma_res)
        for r in reads:
            if r.last_w is not None:
                o.deps.add(r.last_w)
            if r.excl:
                for q in r.readers:
                    if self.ops[q].eng != eng:
                        o.deps.add(q)
            r.readers.append(i)
        for r in writes:
            if r.last_w is not None:
                lw = self.ops[r.last_w]
                if group and lw.dma_res is not None and lw.dma_res is dma_res:
                    o.deps |= lw.deps
                elif not (accum and lw.eng == "pe" and eng == "pe"):
                    o.deps.add(r.last_w)
            latest = {}
            for q in r.readers:
                if q == i:
                    continue
                oq = self.ops[q]
                if oq.dma_res is not None:
                    o.deps.add(q)
                elif latest.get(oq.eng, -1) < q:
                    latest[oq.eng] = q
            o.deps.update(latest.values())
            r.last_w = i
            r.readers = []
        if eng == "pe":
            o.deps = {d for d in o.deps if self.ops[d].eng != "pe" or self.ops[d].dma_res is not None}
        self.ops.append(o)
        return i

    def dma(self, eng, out, in_, res, reads=(), writes=(), group=False, **kw):
        kw = dict(kw); kw["out"] = out; kw["in_"] = in_
        return self.op(eng, ("dma_start", kw), reads=reads, writes=writes, dma_res=res, group=group)

    def I(self, eng, name, reads=(), writes=(), **kw):
        return self.op(eng, (name, kw), reads=reads, writes=writes)

    def emit(self, final_wait_all=True):
        nc = self.nc
        ops = self.ops
        for o in ops:
            for d in o.deps:
                ops[d].sig = True
        per_eng = {e: [] for e in ENGS}
        for i, o in enumerate(ops):
            per_eng[o.eng].append(i)
        import contextlib
        with contextlib.ExitStack() as st:
            esem = {e: st.enter_context(nc.semaphore(f"s_{e}")) for e in ENGS}
            ecount = {e: 0 for e in ENGS}
            dma_sems = []
            for i, o in enumerate(ops):
                if o.dma_res is not None:
                    r = o.dma_res
                    if r.sem is None:
                        r.sem = st.enter_context(nc.semaphore(f"d{len(dma_sems)}_{r.name}"))
                        dma_sems.append(r)
                    r.dcount += 1
                    o.cnt = 16 * r.dcount
                elif o.sig:
                    ecount[o.eng] += 1
                    o.cnt = ecount[o.eng]
            self.n_dma_sems = len(dma_sems)
            know = {e: {} for e in ENGS}
            know_issue = [None] * len(ops)

            def key_of(o):
                return ("d", id(o.dma_res)) if o.dma_res is not None else ("e", o.eng)

            block = st.enter_context(nc.Block())
            handles = {}

            plan = [None] * len(ops)
            for i, o in enumerate(ops):
                kn = know[o.eng]
                need = {}
                for d in o.deps:
                    p = ops[d]
                    k = key_of(p)
                    if kn.get(k, 0) >= p.cnt:
                        continue
                    if need.get(k, (0, None))[0] < p.cnt:
                        need[k] = (p.cnt, d)
                waits = []
                for k, (cnt, d) in need.items():
                    p = ops[d]
                    sem = p.dma_res.sem if p.dma_res is not None else esem[p.eng]
                    waits.append((sem, cnt))
                    kn[k] = max(kn.get(k, 0), cnt)
                    ki = know_issue[d]
                    for kk, vv in ki.items():
                        if kn.get(kk, 0) < vv:
                            kn[kk] = vv
                know_issue[i] = dict(kn)
                plan[i] = waits
            self.n_waits = sum(len(w) for w in plan)

            def make(ename):
                def body(eh):
                    for i in per_eng[ename]:
                        o = ops[i]
                        for sem, cnt in plan[i]:
                            eh.wait_ge(sem, cnt)
                        ins = getattr(eh, o.fn[0])(**o.fn[1])
                        if o.dma_res is not None:
                            ins.then_inc(o.dma_res.sem, 16)
                        elif o.sig:
                            ins.then_inc(esem[o.eng], 1)
                    if ename == "sp" and final_wait_all:
                        for r in dma_sems:
                            eh.wait_ge(r.sem, 16 * r.dcount)
                        for e in ("pe", "act", "dve", "pool"):
                            if ecount[e]:
                                eh.wait_ge(esem[e], ecount[e])
                return body

            block.tensor(make("pe"))
            block.scalar(make("act"))
            block.vector(make("dve"))
            block.gpsimd(make("pool"))
            block.sync(make("sp"))


D = 1024
FH = 2816
EPS = 1e-6
NTP = 16
NT = 17
GT = 2
NEG = -30000.0
DBG_G0 = 2


def build_program(stage=3, debug=False):
    nc = bass.Bass("TRN2", target_bir_lowering=False)
    P = Prog(nc)

    def din(name, shape, dt=F32):
        return nc.dram_tensor(name, list(shape), dt, kind="ExternalInput").ap()

    def dout(name, shape):
        return nc.dram_tensor(name, list(shape), F32, kind="ExternalOutput").ap()

    xp_d = din("xp", [2048, D]); xpre_d = din("xpre", [2048, D]); xs_d = din("xs", [128, D])
    mem_d = din("mem", [256, D])
    csk_d = din("csk", [16, 128, 128]); csv_d = din("csv", [16, 128, 128])
    sC_d = din("sC", [16, 4, 128, 64]); sn_d = din("sn", [16, 4, 64]); sm_d = din("sm", [16, 4])
    cmk_d = din("cmk", [16, 256, 256]); cmv_d = din("cmv", [16, 256, 256])
    w_in_d = din("w_in", [D, 2312]); b_i_d = din("b_igate", [4]); b_f_d = din("b_fgate", [4])
    sinks_d = din("attn_sinks", [8]); ghead_d = din("g_mlstm_head", [512]); w_out_d = din("w_out", [D, D])
    g_mix_d = din("g_mix", [D]); g_cross_d = din("g_cross", [D]); g_mem_d = din("g_mem", [D])
    w_cq_d = din("w_cq", [D, 256]); w_ck_d = din("w_ck", [D, 256]); w_cv_d = din("w_cv", [D, 256])
    w_co_d = din("w_co", [256, D]); g_ffn_d = din("g_ffn", [D])
    w_gate_d = din("w_gate", [D, FH]); w_up_d = din("w_up", [D, FH]); w_down_d = din("w_down", [FH, D])
    g_final_d = din("g_final", [D])
    ident_d = din("c_ident", [128, 128]); mb_band_d = din("c_mb_band", [128, 256]); mb_first_d = din("c_mb_first", [128, 256])
    mb_caus_d = din("c_mb_caus", [128, 128]); mb_causs_d = din("c_mb_causs", [128, 128])
    sel_d = din("c_sel", [4, 1024]); pmask_d = din("c_pmask", [4, 2])
    smc_d = din("c_smc", [32, 128]); smn_d = din("c_smn", [32, 16, 128]); sinkcol_d = din("c_sinkcol", [32, 2])
    bt_d = din("c_bt", [128, 128]); eseq_d = din("c_eseq", [128, 16])

    yp_o = dout("yp", [2048, D]); ys_o = dout("ys", [128, D])
    swak_o = dout("swak", [128, 128]); swav_o = dout("swav", [128, 128])
    Cp_o = dout("Cp", [4, 128, 64]); np_o = dout("np", [4, 64]); mp_o = dout("mp", [4, 1])
    memk_o = dout("memk", [256, 256]); memv_o = dout("memv", [256, 256])
    sks_o = dout("sks", [16, 128, 128]); svs_o = dout("svs", [16, 128, 128])
    Cs_o = dout("Cs", [16, 4, 128, 64]); ns_o = dout("ns", [16, 4, 64]); ms_o = dout("ms", [16, 4])

    st = contextlib.ExitStack()
    with st:
        def sb(name, shape, dt):
            return st.enter_context(nc.sbuf_tensor(name, list(shape), dt))

        def ps(name, shape, dt):
            return st.enter_context(nc.psum_tensor(name, list(shape), dt))

        banks = [ps(f"bk{i}", [128, 512], F32) for i in range(7)]
        bres = [P.res(f"bk{i}") for i in range(7)]
        for r_ in bres:
            r_.excl = True
        tb = ps("tb", [128, 1024], BF16)
        tbr = P.res("tb")
        tbr.excl = True
        bki = [0]

        bset = [[0, 1, 2, 3, 4]]
        bcnt = {}

        def bank():
            key = tuple(bset[0])
            c = bcnt.get(key, 0)
            bcnt[key] = c + 1
            i = bset[0][c % len(key)]
            return banks[i], bres[i]

        Y = sb("Y", [128, NT, D], F32)
        Yr = [P.res(f"Y{t}") for t in range(NT)]
        identb = sb("identb", [128, 128], BF16); identr = P.res("identb")
        identf = sb("identf", [128, 128], F32); identfr = P.res("identf")
        onesb = sb("onesb", [128, 128], BF16); onesbr = P.res("onesb")
        onesf = sb("onesf", [128, 256], F32); onesfr = P.res("onesf")
        SEL = sb("SEL", [4, 1024], F32); selr = P.res("SEL")
        gcols = sb("gcols", [128, 4, 8], F32); gcolsr = P.res("gcols")
        gheadc = sb("gheadc", [128, 4], F32); gheadr = P.res("ghead")
        sinkb = sb("sinkb", [128, 16], F32); sinkbr = P.res("sinkb")
        gb4 = sb("gb4", [4, 4], F32); gb4r = P.res("gb4")
        mbband = sb("mbband", [128, 256], BF16); mbbandr = P.res("mbband")
        mbfirst = sb("mbfirst", [128, 256], BF16); mbfirstr = P.res("mbfirst")
        mbcaus = sb("mbcaus", [128, 128], BF16); mbcausr = P.res("mbcaus")
        stat = sb("stat", [128, 8, 4], F32)
        statr = [P.res(f"stat{i}") for i in range(8)]
        stati = [0]
        USE_SQRT = [False]
        xsb = sb("xsb", [128, 2, D], BF16); xsbr = [P.res("xsb0"), P.res("xsb1")]
        Cst = sb("Cst", [64, 4, 129], F32); Cstr = [P.res(f"Cst{h}") for h in range(4)]
        ARN = 64400
        arena = sb("arena", [128, ARN], BF16)
        aoff = [0]

        def A(shape, dt, parts=128, name=None):
            n = int(np.prod(shape))
            nb = n * (4 if dt == F32 else 2)
            n16 = (nb + 1) // 2
            n16 = (n16 + 15) // 16 * 16
            assert aoff[0] + n16 <= ARN, f"arena overflow {aoff[0]}+{n16} ({name})"
            v = arena[0:parts, aoff[0]:aoff[0] + n16]
            aoff[0] += n16
            if dt == F32:
                v = v.bitcast(F32)
            v = v[:, 0:n]
            if len(shape) == 2:
                v = v.rearrange("p (a b) -> p a b", a=shape[0])
            elif len(shape) == 3:
                v = v.rearrange("p (a b c) -> p a b c", a=shape[0], b=shape[1])
            return v

        def new_phase():
            P.new_phase()
            aoff[0] = 0

        def AR(name):
            return P.res(name, arena=True)

        def A_at(off, shape, dt, parts=128):
            n = int(np.prod(shape))
            nb = n * (4 if dt == F32 else 2)
            n16 = ((nb + 1) // 2 + 15) // 16 * 16
            v = arena[0:parts, off:off + n16]
            if dt == F32:
                v = v.bitcast(F32)
            v = v[:, 0:n]
            if len(shape) == 2:
                v = v.rearrange("p (a b) -> p a b", a=shape[0])
            elif len(shape) == 3:
                v = v.rearrange("p (a b c) -> p a b c", a=shape[0], b=shape[1])
            return v, off + n16

        def ARalias(name, olds):
            r = P.res(name, arena=True)
            dd = set(r.readers)
            for o_ in olds:
                if o_.last_w is not None:
                    dd.add(o_.last_w)
                dd.update(o_.readers)
            r.readers = sorted(dd)
            return r

        I = P.I

        def mm(out, lhsT, rhs, start, stop, reads, wres):
            I("pe", "matmul", reads, [wres], out=out, lhsT=lhsT, rhs=rhs, start=start, stop=stop)

        P.dma("pool", identb[:], ident_d, identr, writes=[identr])
        P.dma("sp", identf[:], ident_d, identfr, writes=[identfr])
        I("dve", "memset", [], [onesbr], ap=onesb[:], constant=1.0)
        I("dve", "memset", [], [onesfr], ap=onesf[:], constant=1.0)
        P.dma("sp", SEL[:], sel_d, selr, writes=[selr])
        for i, g in enumerate((g_mix_d, g_cross_d, g_mem_d, g_ffn_d)):
            P.dma("sp", gcols[:, i, :], g.rearrange("(k p) -> p k", p=128), gcolsr, writes=[gcolsr], group=True, allow_slow_non_contiguous=True)
        P.dma("sp", gheadc[:], ghead_d.rearrange("(h p) -> p h", p=128), gheadr, writes=[gheadr], allow_slow_non_contiguous=True)
        P.dma("sp", gb4[:, 0:1], b_i_d.rearrange("(h o) -> h o", o=1), gb4r, writes=[gb4r], group=True, allow_slow_non_contiguous=True)
        P.dma("sp", gb4[:, 1:2], b_f_d.rearrange("(h o) -> h o", o=1), gb4r, writes=[gb4r], group=True, allow_slow_non_contiguous=True)
        P.dma("sp", gb4[:, 2:4], pmask_d, gb4r, writes=[gb4r], group=True, allow_slow_non_contiguous=True)
        P.dma("sp", sinkb[:, 0:8], sinks_d.partition_broadcast(128), sinkbr, writes=[sinkbr])
        I("dve", "tensor_scalar", [sinkbr], [sinkbr], out=sinkb[:, 8:16], in0=sinkb[:, 0:8], scalar1=-1.0, scalar2=None,
          op0=ALU.mult)
        I("dve", "tensor_scalar", [gb4r], [gb4r], out=gb4[:, 1:2], in0=gb4[:, 1:2], scalar1=-1.0, scalar2=None, op0=ALU.mult)
        P.dma("pool", mbband[:], mb_band_d, mbbandr, writes=[mbbandr])
        P.dma("pool", mbfirst[:], mb_first_d, mbfirstr, writes=[mbfirstr])
        P.dma("pool", mbcaus[:], mb_caus_d, mbcausr, writes=[mbcausr])
        for h in range(4):
            I("dve", "memset", [], [Cstr[h]], ap=Cst[:, h, :], constant=0.0)

        def SELh(h, n=128):
            return SEL[:, h * 128:h * 128 + n]

        def NSELh(h, n=128):
            return SEL[:, 512 + h * 128:512 + h * 128 + n]

        def norm_stats(src, sres, jb=0):
            i = stati[0] % 8
            stati[0] += 1
            sr = statr[i]
            I("act", "activation", [sres], [xsbr[jb], sr], out=xsb[:, jb, :], in_=src, func=AF.Square, accum_out=stat[:, i, 0:1])
            I("dve", "tensor_scalar", [sr], [sr], out=stat[:, i, 1:2], in0=stat[:, i, 0:1], scalar1=1.0 / D, scalar2=EPS,
              op0=ALU.mult, op1=ALU.add)
            if USE_SQRT[0]:
                I("act", "activation", [sr], [sr], out=stat[:, i, 2:3], in_=stat[:, i, 1:2], func=AF.Sqrt)
                I("dve", "reciprocal", [sr], [sr], out=stat[:, i, 3:4], in_=stat[:, i, 2:3])
            else:
                I("act", "activation", [sr], [sr], out=stat[:, i, 2:3], in_=stat[:, i, 1:2], func=AF.Ln)
                I("act", "activation", [sr], [sr], out=stat[:, i, 3:4], in_=stat[:, i, 2:3], func=AF.Exp, scale=-0.5)
            return stat[:, i, 3:4], sr

        xsi = [0]

        def norm_T(src, sres, gi, dst, dres):
            b = xsi[0] % 2
            xsi[0] += 1
            rstd, sr = norm_stats(src, sres, b)
            I("dve", "tensor_scalar", [sres, sr], [xsbr[b]], out=xsb[:, b, :], in0=src, scalar1=rstd, scalar2=None, op0=ALU.mult)
            for k in range(8):
                I("pe", "transpose", [xsbr[b], identr], [tbr], out=tb[:, k * 128:(k + 1) * 128],
                  in_=xsb[:, b, k * 128:(k + 1) * 128], identity=identb[:])
            for k in range(8):
                I("act", "activation", [tbr, gcolsr], dres, out=dst[:, k, :], in_=tb[:, k * 128:(k + 1) * 128],
                  func=AF.Copy, scale=gcols[:, gi, k:k + 1])

        class NS:
            pass

        def alloc_mixer(gt, nkt, nvt):
            M = NS()
            M.WQ = A([8, 512], BF16); M.WQr = AR("WQ")
            M.WTOK = A([8, 1024], BF16); M.WTOKr = AR("WTOK")
            M.WK = M.WTOK[:, :, 0:128]; M.WKr = M.WTOKr
            M.WMQ = A([8, 256], BF16); M.WMQr = AR("WMQ")
            M.WMK = M.WTOK[:, :, 256:512]; M.WMKr = M.WTOKr
            M.WOG = A([8, 512], BF16); M.WOGr = AR("WOG")
            M.WGT = A([8, 8], BF16); M.WGTr = AR("WGT")
            M.WOA = A([8, 1024], BF16, parts=64); M.WOAr = AR("WOA")
            M.WOM = A([4, 1024], BF16); M.WOMr = AR("WOM")

            def wload(dst, res, src, **kw):
                P.dma("pool", dst, src, res, writes=[res], **kw)

            def wcols(a_, b_):
                return w_in_d[:, a_:b_].rearrange("(k p) n -> p k n", p=128)
            wload(M.WTOK[:, :, 0:256], M.WTOKr, wcols(512, 768), group=True)
            wload(M.WTOK[:, :, 256:1024], M.WTOKr, wcols(1024, 1792), group=True)
            wload(M.WGT[:], M.WGTr, wcols(2304, 2312), allow_slow_non_contiguous=True)
            wload(M.WQ[:], M.WQr, wcols(0, 512))
            wload(M.WMQ[:], M.WMQr, wcols(768, 1024))
            wload(M.WOG[:], M.WOGr, wcols(1792, 2304))
            wload(M.WOA[:], M.WOAr, w_out_d[0:512, :].rearrange("(g d) n -> d g n", d=64))
            wload(M.WOM[:], M.WOMr, w_out_d[512:1024, :].rearrange("(h p) n -> p h n", p=128))
            M.KT = A([2, nkt * 128], BF16, parts=64); M.KTr = [AR(f"KT{i}") for i in range(nkt)]
            M.Vt = A([nvt, 128], BF16); M.Vtr = [AR(f"Vt{i}") for i in range(nvt)]
            M.XNTg = A([8, gt * 128], BF16); M.XNTgr = [AR(f"XNTg{i}") for i in range(gt)]
            M.QT = A([8, gt * 128], BF16, parts=64); M.QTr = AR("QT")
            M.MQT = A([4, gt * 128], BF16, parts=64); M.MQTr = AR("MQT")
            M.MKT = A([4, gt * 128], BF16, parts=64); M.MKTr = AR("MKT")
            M.SGT = A([4, gt * 128], BF16); M.SGTr = AR("SGT")
            M.MKtok = A([gt, 256], BF16); M.MKtokr = [AR(f"MKtok{i}") for i in range(gt)]
            M.MVaug = A([gt, 4, 129], BF16); M.MVaugr = [AR(f"MVaug{i}") for i in range(gt)]
            M.ATTT = A([8, gt * 128], BF16, parts=64); M.ATTTr = [AR(f"ATTT{i}") for i in range(gt)]
            M.HMT = A([4, gt * 128], BF16); M.HMTr = [AR(f"HMT{i}") for i in range(gt)]
            M.NG = gt * 128
            NG_ = M.NG
            M.G_IG = A([1, NG_ + 1], F32, parts=4)[:, 0, :]; M.G_E = A([1, NG_], F32, parts=4)[:, 0, :]
            M.G_L1 = A([1, NG_], F32, parts=4)[:, 0, :]; M.G_B = A([1, NG_ + 1], F32, parts=4)[:, 0, :]
            M.G_A = A([1, NG_], F32, parts=4)[:, 0, :]; M.G_M = A([1, NG_ + 1], F32, parts=4)[:, 0, :]
            M.G_BM = A([1, NG_], F32, parts=4)[:, 0, :]; M.G_DM = A([1, NG_], F32, parts=4)[:, 0, :]
            M.Gr = AR("G_IG"); M.G_Br = AR("G_B"); M.G_Ar = AR("G_A"); M.G_Mr = AR("G_M"); M.G_BMr = AR("G_BM"); M.G_DMr = AR("G_DM")
            M.SKV = A([1, 256], F32)[:, 0, :]; M.SKVr = AR("SKV")
            M.Ebuf = A([4, 256], BF16); M.Er = AR("E")
            M.PTs = A([1, 1024], BF16); M.PTsr = [AR("PTs0")] * 2
            M.sm_st = A([1, 32], F32)[:, 0, :]; M.smr = AR("sm_st")
            M.WKC = A([1, 8], F32)[:, 0, :]; M.WKCr = AR("WKC")
            M.DG = A([1, 8], F32, parts=4)[:, 0, :]; M.DGr = AR("DG")
            M.VW = A([4, 129], BF16); M.VWr = [AR(f"VW{h}") for h in range(4)]
            M.Cb = A([4, 257], BF16, parts=64); M.Cbr = [AR(f"Cb{h}") for h in range(4)]
            M.WT = A([4, 128], BF16); M.WTr = AR("WT")
            M.ST = A([4, 128], BF16); M.STr = AR("ST")
            M.WI = A([4, 128], BF16); M.WIr = AR("WI")
            M.QW = A([4, 128], BF16, parts=64); M.QWr = AR("QW")
            M.LOWB = A([4, 128], F32); M.LOWBr = AR("LOWB")
            M.T1 = A([4, 128], F32); M.T1r = AR("T1")
            M.T2 = A([4, 128], F32); M.T2r = AR("T2")
            M.USQ = A([4, 128], BF16); M.USQr = AR("USQ")
            for i in range(gt):
                I("dve", "memset", [], [M.MVaugr[i]], ap=M.MVaug[:, i, :, 128:129], constant=1.0)
            return M

        M = alloc_mixer(GT, NTP + 1, NTP + 1)
        I("dve", "memset", [], [M.G_Br], ap=M.G_B[:, 0:1], constant=0.0)
        I("dve", "memset", [], [M.G_Mr], ap=M.G_M[:, 0:1], constant=0.0)

        def tok_major(ti, xcols, xres, vslot, want_kv_out=None):
            b0, b0r = bank()
            b1, b1r = bank()
            for k in range(8):
                mm(b0[:, :], M.XNTg[:, k, xcols], M.WTOK[:, k, 0:512], k == 0, k == 7, [xres, M.WTOKr], b0r)
            for k in range(8):
                mm(b1[:, :], M.XNTg[:, k, xcols], M.WTOK[:, k, 512:1024], k == 0, k == 7, [xres, M.WTOKr], b1r)
            I("act", "activation", [b0r], [M.Vtr[vslot]], out=M.Vt[:, vslot, :], in_=b0[:, 128:256], func=AF.Copy)
            I("act", "activation", [b0r], [M.MKtokr[ti]], out=M.MKtok[:, ti, :], in_=b0[:, 256:512], func=AF.Copy, scale=0.125)
            I("dve", "tensor_copy", [b1r], [M.MVaugr[ti]], out=M.MVaug[:, ti, :, 0:128],
              in_=b1[:, :].rearrange("p (h d) -> p h d", h=4))
            if want_kv_out is not None:
                I("dve", "tensor_copy", [b0r], [M.SKVr], out=M.SKV[:, :], in_=b0[:, 0:256])
                if want_kv_out == "sample":
                    P.dma("sp", sks_o[:, 120:128, :], M.SKV[:, 0:128], M.SKVr, reads=[M.SKVr], group=True)
                    P.dma("sp", svs_o[:, 120:128, :], M.SKV[:, 128:256], M.SKVr, reads=[M.SKVr], group=True)
                else:
                    P.dma("sp", swak_o, M.SKV[:, 0:128], M.SKVr, reads=[M.SKVr], group=True)
                    P.dma("sp", swav_o, M.SKV[:, 128:256], M.SKVr, reads=[M.SKVr], group=True)

        def feat64(W, Wr, nh, dst, dres, ntok, xres, scale=None, dcol0=0):
            for h0 in range(0, nh, 2):
                bk, bkr = bank()
                for hh in range(2):
                    h = h0 + hh
                    for k in range(8):
                        mm(bk[0:64, hh * 256:hh * 256 + ntok], W[:, k, h * 64:(h + 1) * 64], M.XNTg[:, k, 0:ntok],
                           k == 0, k == 7, [Wr] + xres, bkr)
                src = bk[0:64, :].rearrange("p (a b) -> p a b", a=2)[:, :, 0:ntok]
                kw = {} if scale is None else {"scale": scale}
                I("act", "activation", [bkr], dres, out=dst[:, h0:h0 + 2, dcol0:dcol0 + ntok], in_=src, func=AF.Copy, **kw)

        def gates(ntok, xres, prefix):
            pg, pgr = bank()
            for k in range(8):
                mm(pg[0:4, 0:ntok], M.WGT[:, k, 0:4], M.XNTg[:, k, 0:ntok], k == 0, k == 7, [M.WGTr] + xres, pgr)
            for k in range(8):
                mm(pg[0:4, 256:256 + ntok], M.WGT[:, k, 4:8], M.XNTg[:, k, 0:ntok], k == 0, k == 7, [M.WGTr] + xres, pgr)
            I("act", "activation", [pgr, gb4r], [M.Gr], out=M.G_IG[:, 1:ntok + 1], in_=pg[0:4, 0:ntok], func=AF.Identity,
              bias=gb4[:, 0:1])
            I("act", "activation", [pgr, gb4r], [M.Gr], out=M.G_E[:, 0:ntok], in_=pg[0:4, 256:256 + ntok], func=AF.Exp,
              bias=gb4[:, 1:2], scale=-1.0)
            I("act", "activation", [M.Gr], [M.Gr], out=M.G_L1[:, 0:ntok], in_=M.G_E[:, 0:ntok], func=AF.Ln, bias=1.0)
            if prefix == "sample":
                return
            if prefix:
                I("dve", "tensor_scalar", [M.Gr, gb4r], [M.Gr], out=M.G_L1[:, 0:ntok], in0=M.G_L1[:, 0:ntok], scalar1=gb4[:, 2:3],
                  scalar2=None, op0=ALU.mult)
            I("dve", "tensor_tensor_scan", [M.Gr, M.G_Br, onesfr], [M.G_Br], out=M.G_B[:, 1:ntok + 1], data0=onesf[0:4, 0:ntok],
              data1=M.G_L1[:, 0:ntok], initial=M.G_B[:, 0:1], op0=ALU.mult, op1=ALU.subtract)
            I("dve", "scalar_tensor_tensor", [M.Gr, M.G_Br, gb4r], [M.G_Ar], out=M.G_A[:, 0:ntok], in0=M.G_IG[:, 1:ntok + 1],
              scalar=(gb4[:, 3:4] if prefix else 0.0), in1=M.G_B[:, 1:ntok + 1], op0=ALU.add, op1=ALU.subtract)
            I("dve", "tensor_tensor_scan", [M.G_Ar, M.G_Mr, onesfr], [M.G_Mr], out=M.G_M[:, 1:ntok + 1], data0=onesf[0:4, 0:ntok],
              data1=M.G_A[:, 0:ntok], initial=M.G_M[:, 0:1], op0=ALU.mult, op1=ALU.max)
            I("dve", "tensor_tensor", [M.G_Br, M.G_Mr], [M.G_BMr], out=M.G_BM[:, 0:ntok], in0=M.G_B[:, 1:ntok + 1],
              in1=M.G_M[:, 1:ntok + 1], op=ALU.add)
            for ci in range(ntok // 128):
                I("dve", "tensor_scalar", [M.G_Mr], [M.G_DMr], out=M.G_DM[:, ci * 128:(ci + 1) * 128],
                  in0=M.G_M[:, 1 + ci * 128:1 + (ci + 1) * 128], scalar1=M.G_M[:, ci * 128:ci * 128 + 1], scalar2=None,
                  op0=ALU.subtract)

        def gates_carry(ntok):
            I("dve", "tensor_copy", [M.G_Br], [M.G_Br], out=M.G_B[:, 0:1], in_=M.G_B[:, ntok:ntok + 1])
            I("dve", "tensor_copy", [M.G_Mr], [M.G_Mr], out=M.G_M[:, 0:1], in_=M.G_M[:, ntok:ntok + 1])

        def state_update(ti, c0, refresh_cb):
            pw, pwr = bank()
            I4 = SEL[:, 0:512].rearrange("p (h t) -> p h t", t=128)[:, :, 0]
            I("dve", "tensor_scalar", [selr, M.G_Mr], [M.DGr], out=M.DG[:, 0:4], in0=I4, scalar1=M.G_M[:, c0 + 128:c0 + 129],
              scalar2=-1.0, op0=ALU.mult, op1=ALU.mult)
            I("dve", "tensor_scalar", [selr, M.G_DMr], [M.DGr], out=M.DG[:, 4:8], in0=I4, scalar1=M.G_DM[:, c0 + 127:c0 + 128],
              scalar2=-1.0, op0=ALU.mult, op1=ALU.mult)
            mm(pw[:, 0:4], M.G_A[:, c0:c0 + 128], I4, True, False, [M.G_Ar, selr], pwr)
            mm(pw[:, 0:4], onesf[0:4, 0:128], M.DG[:, 0:4], False, True, [onesfr, M.DGr], pwr)
            mm(pw[:, 4:8], onesf[0:4, 0:128], M.DG[:, 4:8], True, True, [onesfr, M.DGr], pwr)
            I("act", "activation", [pwr], [M.WKCr], out=M.WKC[:, 0:8], in_=pw[:, 0:8], func=AF.Exp)
            for h in range(4):
                I("dve", "tensor_scalar", [M.MVaugr[ti], M.WKCr], [M.VWr[h]], out=M.VW[:, h, :], in0=M.MVaug[:, ti, h, :],
                  scalar1=M.WKC[:, h:h + 1], scalar2=None, op0=ALU.mult)
            for h0 in (0, 2):
                dc, dcr = bank()
                for hh in range(2):
                    h = h0 + hh
                    mm(dc[0:64, hh * 129:(hh + 1) * 129], M.MKtok[:, ti, h * 64:(h + 1) * 64], M.VW[:, h, :], True, True,
                       [M.MKtokr[ti], M.VWr[h]], dcr)
                for hh in range(2):
                    h = h0 + hh
                    I("dve", "scalar_tensor_tensor", [Cstr[h], M.WKCr, dcr], [Cstr[h]], out=Cst[:, h, :], in0=Cst[:, h, :],
                      scalar=M.WKC[0:64, 4 + h:5 + h], in1=dc[0:64, hh * 129:(hh + 1) * 129], op0=ALU.mult, op1=ALU.add)
            if refresh_cb:
                for h in range(4):
                    I("act", "activation", [Cstr[h]], [M.Cbr[h]], out=M.Cb[:, h, 0:129], in_=Cst[:, h, :], func=AF.Copy)
                    I("act", "activation", [Cstr[h]], [M.Cbr[h]], out=M.Cb[:, h, 129:257],
                      in_=Cst[:, h, 128:129].broadcast_to([64, 128]), func=AF.Copy)

        def mlstm_chunk(ti, c0, mbias, mbiasr, inter=True, inter_fn=None):
            cs = slice(c0, c0 + 128)
            pwt, pwtr = bank()
            for h in range(4):
                o = pwt[:, h * 128:(h + 1) * 128]
                mm(o, M.G_A[:, cs], SELh(h), True, False, [M.G_Ar, selr], pwtr)
                mm(o, NSELh(h), M.G_M[:, c0 + 1:c0 + 129], False, False, [M.G_Mr, selr], pwtr)
                mm(o, identb[:], mbias, False, True, [identr, mbiasr], pwtr)
            I("act", "activation", [pwtr], [M.WTr], out=M.WT[:, :, :], in_=pwt[:, :].rearrange("p (h t) -> p h t", h=4), func=AF.Exp)
            pqk, pqkr = bank()
            for h in range(4):
                mm(pqk[:, h * 128:(h + 1) * 128], M.MKT[:, h, cs], M.MQT[:, h, cs], True, True, [M.MKTr, M.MQTr], pqkr)
            I("dve", "tensor_tensor", [pqkr, M.WTr], [M.STr], out=M.ST[:, :, :], in0=pqk[:, :].rearrange("p (h t) -> p h t", h=4),
              in1=M.WT[:, :, :], op=ALU.mult)
            pwi, pwir = bank()
            for h in range(4):
                mm(pwi[:, h * 128:(h + 1) * 128], NSELh(h), M.G_DM[:, cs], True, True, [M.G_DMr, selr], pwir)
            I("act", "activation", [pwir], [M.WIr], out=M.WI[:, :, :], in_=pwi[:, :].rearrange("p (h t) -> p h t", h=4), func=AF.Exp)
            I("dve", "tensor_tensor", [M.MQTr, M.WIr], [M.QWr], out=M.QW[:, :, :], in0=M.MQT[:, :, cs], in1=M.WI[0:64, :, :], op=ALU.mult)
            plb, plbr = bank()
            for h in range(4):
                mm(plb[:, h * 128:(h + 1) * 128], NSELh(h), M.G_BM[:, cs], True, True, [M.G_BMr, selr], plbr)
            I("act", "activation", [plbr], [M.LOWBr], out=M.LOWB[:, :, :], in_=plb[:, :].rearrange("p (h t) -> p h t", h=4), func=AF.Exp)
            pnum, pnumr = banks[5], bres[5]
            pden, pdenr = banks[6], bres[6]
            if inter_fn is not None:
                inter_fn("pre")
            for h in range(4):
                o = pnum[:, h * 128:(h + 1) * 128]
                mm(o, M.MVaug[:, ti, h, 0:128], M.ST[:, h, :], True, False, [M.MVaugr[ti], M.STr], pnumr)
                if inter_fn is not None:
                    inter_fn("num", h, pnum, pnumr)
                else:
                    mm(o, M.Cb[:, h, 0:128], M.QW[:, h, :], False, True, [M.Cbr[h], M.QWr], pnumr)
            for h in range(4):
                o = pden[:, h * 128:(h + 1) * 128]
                mm(o, onesb[:], M.ST[:, h, :], True, False, [onesbr, M.STr], pdenr)
                if inter_fn is not None:
                    inter_fn("den", h, pden, pdenr)
                else:
                    mm(o, M.Cb[:, h, 129:257], M.QW[:, h, :], False, True, [M.Cbr[h], M.QWr], pdenr)
            return pnum, pnumr, pden, pdenr

        def mlstm_finish(pnum, pnumr, pden, pdenr, c0, hres):
            cs = slice(c0, c0 + 128)
            v4 = lambda b: b[:, :].rearrange("p (h t) -> p h t", h=4)
            I("act", "activation", [pdenr], [M.T1r], out=M.T1[:, :, :], in_=v4(pden), func=AF.Abs)
            I("dve", "tensor_tensor", [M.T1r, M.LOWBr], [M.T1r], out=M.T1[:, :, :], in0=M.T1[:, :, :], in1=M.LOWB[:, :, :], op=ALU.max)
            I("act", "activation", [M.T1r], [M.T1r], out=M.T1[:, :, :], in_=M.T1[:, :, :], func=AF.Square, scale=float(np.sqrt(EPS)))
            I("act", "activation", [pnumr], [M.USQr], out=M.USQ[:, :, :], in_=v4(pnum), func=AF.Square)
            pss, pssr = bank()
            mm(pss[:, :], onesb[:], M.USQ[:, :, :], True, True, [onesbr, M.USQr], pssr)
            I("dve", "scalar_tensor_tensor", [pssr, M.T1r], [M.T2r], out=M.T2[:, :, :], in0=v4(pss), scalar=1.0 / 128, in1=M.T1[:, :, :],
              op0=ALU.mult, op1=ALU.add)
            I("act", "activation", [M.T2r], [M.T2r], out=M.T2[:, :, :], in_=M.T2[:, :, :], func=AF.Ln)
            I("act", "activation", [M.T2r], [M.T2r], out=M.T2[:, :, :], in_=M.T2[:, :, :], func=AF.Exp, scale=-0.5)
            I("dve", "tensor_tensor", [pnumr, M.T2r], [M.T1r], out=M.T1[:, :, :], in0=v4(pnum), in1=M.T2[:, :, :], op=ALU.mult)
            for h in range(4):
                I("dve", "scalar_tensor_tensor", [M.T1r, gheadr, M.SGTr], [hres], out=M.HMT[:, h, cs], in0=M.T1[:, h, :],
                  scalar=gheadc[:, h:h + 1], in1=M.SGT[:, h, cs], op0=ALU.mult, op1=ALU.mult)

        def swa_tile(ti, kcol0, vslots, mb, mbr, ktres):
            qs = slice(ti * 128, (ti + 1) * 128)
            for h in range(2):
                bks = [bank(), bank()]
                for g in range(4):
                    bk, bkr = bks[g // 2]
                    o = bk[:, (g % 2) * 256:(g % 2 + 1) * 256]
                    mm(o, M.QT[:, 4 * h + g, qs], M.KT[:, h, kcol0:kcol0 + 256], True, False, [M.QTr] + ktres, bkr)
                    mm(o, identb[:], mb, False, True, [identr, mbr], bkr)
                for j in range(2):
                    I("dve", "reduce_max", [bks[j][1]], [M.smr], out=M.sm_st[:, 2 * j:2 * j + 2],
                      in_=bks[j][0][:, :].rearrange("p (a b) -> p a b", a=2), axis=AX.X)
                I("dve", "tensor_scalar", [M.smr], [M.smr], out=M.sm_st[:, 0:4], in0=M.sm_st[:, 0:4], scalar1=-0.125, scalar2=None,
                  op0=ALU.mult)
                I("dve", "tensor_tensor", [M.smr, sinkbr], [M.smr], out=M.sm_st[:, 0:4], in0=M.sm_st[:, 0:4],
                  in1=sinkb[:, 8 + 4 * h:12 + 4 * h], op=ALU.min)
                for g in range(4):
                    bk, bkr = bks[g // 2]
                    I("act", "activation", [bkr, M.smr], [M.Er, M.smr], out=M.Ebuf[:, g, :], in_=bk[:, (g % 2) * 256:(g % 2 + 1) * 256],
                      func=AF.Exp, bias=M.sm_st[:, g:g + 1], scale=0.125, accum_out=M.sm_st[:, 4 + g:5 + g])
                I("dve", "tensor_tensor", [M.smr, sinkbr], [M.smr], out=M.sm_st[:, 8:12], in0=M.sm_st[:, 0:4],
                  in1=sinkb[:, 4 * h:4 * h + 4], op=ALU.add)
                I("act", "activation", [M.smr], [M.smr], out=M.sm_st[:, 8:12], in_=M.sm_st[:, 8:12], func=AF.Exp)
                I("dve", "tensor_tensor", [M.smr], [M.smr], out=M.sm_st[:, 8:12], in0=M.sm_st[:, 8:12], in1=M.sm_st[:, 4:8], op=ALU.add)
                I("dve", "reciprocal", [M.smr], [M.smr], out=M.sm_st[:, 12:16], in_=M.sm_st[:, 8:12])
                for g in range(4):
                    if g % 2 == 0:
                        I("act", "activation", [M.Er, M.smr], [M.Er], out=M.Ebuf[:, g, :], in_=M.Ebuf[:, g, :], func=AF.Copy,
                          scale=M.sm_st[:, 12 + g:13 + g])
                    else:
                        I("dve", "tensor_scalar", [M.Er, M.smr], [M.Er], out=M.Ebuf[:, g, :], in0=M.Ebuf[:, g, :],
                          scalar1=M.sm_st[:, 12 + g:13 + g], scalar2=None, op0=ALU.mult)
                for kb in range(2):
                    for g in range(4):
                        I("pe", "transpose", [M.Er, identr], [tbr], out=tb[:, (kb * 4 + g) * 128:(kb * 4 + g + 1) * 128],
                          in_=M.Ebuf[:, g, kb * 128:(kb + 1) * 128], identity=identb[:])
                pb = 0
                if h == 0:
                    I("dve", "tensor_copy", [tbr], [M.PTsr[pb]], out=M.PTs[:, pb, :], in_=tb[:, :])
                else:
                    I("act", "activation", [tbr], [M.PTsr[pb]], out=M.PTs[:, pb, :], in_=tb[:, :], func=AF.Copy)
                po, por = bank()
                mm(po[0:64, :], M.Vt[:, vslots[0], h * 64:(h + 1) * 64], M.PTs[:, pb, 0:512], True, False, [M.Vtr[vslots[0]], M.PTsr[pb]], por)
                mm(po[0:64, :], M.Vt[:, vslots[1], h * 64:(h + 1) * 64], M.PTs[:, pb, 512:1024], False, True, [M.Vtr[vslots[1]], M.PTsr[pb]], por)
                I("act", "activation", [por], [M.ATTTr[ti]], out=M.ATTT[:, 4 * h:4 * h + 4, qs],
                  in_=po[0:64, :].rearrange("p (g q) -> p g q", g=4), func=AF.Copy)

        def wout_tile(ti, t):
            qs = slice(ti * 128, (ti + 1) * 128)
            for c in range(2):
                bk, bkr = bank()
                cc = slice(c * 512, (c + 1) * 512)
                for hg in range(8):
                    mm(bk[:, :], M.ATTT[:, hg, qs], M.WOA[:, hg, cc], hg == 0, False, [M.ATTTr[ti], M.WOAr], bkr)
                for h in range(4):
                    mm(bk[:, :], M.HMT[:, h, qs], M.WOM[:, h, cc], False, h == 3, [M.HMTr[ti], M.WOMr], bkr)
                I("dve", "tensor_tensor", [Yr[t], bkr], [Yr[t]], out=Y[:, t, cc], in0=Y[:, t, cc], in1=bk[:, :], op=ALU.add)

        xpre_t = xpre_d.rearrange("(t p) d -> t p d", p=128)
        xp_t = xp_d.rearrange("(t p) d -> t p d", p=128)
        for t in range(NTP):
            P.dma("sp", Y[:, t, :], xpre_t[t], Yr[t], writes=[Yr[t]])
        for g0 in range(0, NTP, GT):
            for ti in range(GT):
                t = g0 + ti
                norm_T(Y[:, t, :], Yr[t], 0, M.XNTg[:, :, ti * 128:(ti + 1) * 128], [M.XNTgr[ti]])
                P.dma("sp", Y[:, t, :], xp_t[t], Yr[t], writes=[Yr[t]])
            for ti in range(GT):
                tok_major(ti, slice(ti * 128, (ti + 1) * 128), M.XNTgr[ti], 0)
            gates(GT * 128, M.XNTgr, True)
            if g0 + GT == NTP:
                bk, bkr = bank()
                for h in range(2):
                    for k in range(8):
                        mm(bk[0:64, h * 128:(h + 1) * 128], M.WK[:, k, h * 64:(h + 1) * 64], M.XNTg[:, k, (GT - 1) * 128:GT * 128],
                           k == 0, k == 7, [M.WKr, M.XNTgr[GT - 1]], bkr)
                I("act", "activation", [bkr], [M.KTr[0]], out=M.KT[:, :, 0:128],
                  in_=bk[0:64, 0:256].rearrange("p (a b) -> p a b", a=2), func=AF.Copy)
            for ti in range(GT):
                last = (g0 + ti == NTP - 1)
                state_update(ti, ti * 128, last)
            gates_carry(GT * 128)

        for g0 in range(0, NTP, GT):
            for ti in range(GT):
                t = g0 + ti
                norm_T(Y[:, t, :], Yr[t], 0, M.XNTg[:, :, ti * 128:(ti + 1) * 128], [M.XNTgr[ti]])
            for ti in range(GT):
                t = g0 + ti
                tok_major(ti, slice(ti * 128, (ti + 1) * 128), M.XNTgr[ti], 1 + t, want_kv_out=(True if t == NTP - 1 else None))
            gates(M.NG, M.XNTgr, False)
            feat64(M.WK, M.WKr, 2, M.KT, [M.KTr[1 + g0 + i] for i in range(GT)], M.NG, M.XNTgr, dcol0=128 + g0 * 128)
            feat64(M.WQ, M.WQr, 8, M.QT, [M.QTr], M.NG, M.XNTgr)
            feat64(M.WMQ, M.WMQr, 4, M.MQT, [M.MQTr], M.NG, M.XNTgr)
            feat64(M.WMK, M.WMKr, 4, M.MKT, [M.MKTr], M.NG, M.XNTgr, scale=0.125)
            for h0 in (0, 2):
                bk, bkr = bank()
                for hh in range(2):
                    h = h0 + hh
                    for k in range(8):
                        mm(bk[:, hh * 256:hh * 256 + M.NG], M.WOG[:, k, h * 128:(h + 1) * 128], M.XNTg[:, k, 0:M.NG], k == 0, k == 7,
                           [M.WOGr] + M.XNTgr, bkr)
                sgv = M.SGT[:, h0:h0 + 2, :]
                I("act", "activation", [bkr], [M.SGTr], out=sgv, in_=bk[:, :].rearrange("p (a b) -> p a b", a=2)[:, :, 0:M.NG],
                  func=AF.Exp, scale=-1.0)
                I("act", "activation", [M.SGTr], [M.SGTr], out=sgv, in_=sgv, func=AF.Ln, bias=1.0)
                I("act", "activation", [M.SGTr], [M.SGTr], out=sgv, in_=sgv, func=AF.Exp, scale=-1.0)
            pend = None
            for ti in range(GT):
                t = g0 + ti
                P.rec_begin(); bset[0] = [2, 3]
                pn = mlstm_chunk(ti, ti * 128, mbcaus[:], mbcausr)
                mlstm_finish(*pn, ti * 128, M.HMTr[ti])
                state_update(ti, ti * 128, True)
                s_ml = P.rec_end()
                P.rec_begin(); bset[0] = [0, 1]
                swa_tile(ti, t * 128, (t, t + 1), (mbfirst[:] if t == 0 else mbband[:]), (mbfirstr if t == 0 else mbbandr),
                         [M.KTr[t], M.KTr[t + 1]])
                s_sw = P.rec_end()
                strs = [s_ml, s_sw]
                if pend is not None:
                    P.rec_begin(); bset[0] = [4]
                    wout_tile(*pend)
                    strs.append(P.rec_end())
                P.merge(strs)
                pend = (ti, t)
            bset[0] = [0, 1, 2, 3, 4]
            wout_tile(*pend)
            if debug and g0 == DBG_G0:
                dA = dout("dbg_att", [64, 8, M.NG]); dH = dout("dbg_hm", [128, 4, M.NG])
                P.dma("pool", dA, M.ATTT[:, :, :], M.ATTTr[0], reads=M.ATTTr)
                P.dma("pool", dH, M.HMT[:, :, :], M.HMTr[0], reads=M.HMTr)
            gates_carry(M.NG)

        CO = A([4, 64], F32); COr = AR("CO")
        for h in range(4):
            bk, bkr = bank()
            mm(bk[:, 0:64], Cst[:, h, 0:128], identf[0:64, 0:64], True, True, [Cstr[h], identfr], bkr)
            I("act", "activation", [bkr], [COr], out=CO[:, h, :], in_=bk[:, 0:64], func=AF.Copy)
        P.dma("sp", Cp_o.rearrange("h p k -> p h k"), CO[:, :, :], COr, reads=[COr])
        for h in range(4):
            P.dma("sp", np_o[h, :].rearrange("(k o) -> k o", o=1), Cst[:, h, 128:129], Cstr[h], reads=[Cstr[h]], allow_slow_non_contiguous=True)
        P.dma("sp", mp_o, M.G_BM[:, M.NG - 1:M.NG], M.G_BMr, reads=[M.G_BMr], allow_slow_non_contiguous=True)

        new_phase()
        MA = M
        M = alloc_mixer(1, 1, 1)
        R0_olds = [M.WQr, M.WTOKr, M.WMQr, M.WOGr, M.WGTr]
        TS = NTP
        P.dma("sp", Y[:, TS, :], xs_d, Yr[TS], writes=[Yr[TS]])
        shk = P.res("shk"); shv = P.res("shv")
        P.dma("sp", sks_o[:, 0:120, :], csk_d[:, 8:128, :], shk, writes=[shk])
        P.dma("sp", svs_o[:, 0:120, :], csv_d[:, 8:128, :], shv, writes=[shv])
        CKn = A([16, 128], BF16); CKnr = AR("CKn")
        CV = A([16, 128], BF16); CVr = AR("CV")
        CKT = A([16, 128], BF16, parts=64); CKTr = AR("CKT")
        SMC = A([1, 128], BF16, parts=32)[:, 0, :]; SMCr = AR("SMC")
        SMN = A([16, 128], BF16, parts=32); SMNr = AR("SMN")
        SINKC = A([1, 4], F32, parts=32)[:, 0, :]; SINKCr = AR("SINKC")
        mbcs = A([1, 128], BF16)[:, 0, :]; mbcsr = AR("mbcs")
        PNs = A([4, 256], BF16, parts=32); PNsr = [AR(f"PNs{i}") for i in range(4)]
        sms = A([4, 8], F32, parts=32); smsr = [AR(f"sms{i}") for i in range(4)]
        PTS = A([1, 1024], BF16)[:, 0, :]; PTSr = AR("PTS")
        M0 = A([1, 16], F32, parts=4)[:, 0, :]; M0r = AR("M0")
        MTe = A([1, 128], F32, parts=4)[:, 0, :]; MTer = AR("MTe")
        DMT = A([1, 16], F32, parts=4)[:, 0, :]; DMTr = AR("DMT")
        E16 = A([1, 16], F32)[:, 0, :]; E16r = AR("E16")
        EW = A([4, 16], BF16); EWr = AR("EW")
        WCB = A([4, 16], F32); WCBr = AR("WCB")
        SNn = A([1, 64], F32, parts=64)[:, 0, :]; SNnr = AR("SNn")
        SNT = A([1, 64], F32, parts=64)[:, 0, :]; SNTr = AR("SNT")
        NNT = A([1, 64], F32, parts=64)[:, 0, :]; NNTr = AR("NNT")
        NNo = A([1, 64], F32, parts=64)[:, 0, :]; NNor = AR("NNo")
        BTf = A([1, 128], F32)[:, 0, :]; BTfr = AR("BTf")
        P.dma("pool", CKn[:, :, :], csk_d.rearrange("j p c -> p j c"), CKnr, writes=[CKnr])
        P.dma("pool", CV[:, :, :], csv_d.rearrange("j p c -> p j c"), CVr, writes=[CVr])
        P.dma("pool", SMC, smc_d, SMCr, writes=[SMCr])
        P.dma("pool", SMN[:, :, :], smn_d, SMNr, writes=[SMNr])
        P.dma("sp", SINKC[:, 0:2], sinkcol_d, SINKCr, writes=[SINKCr])
        I("dve", "tensor_scalar", [SINKCr], [SINKCr], out=SINKC[:, 2:4], in0=SINKC[:, 0:2], scalar1=-1.0, scalar2=None, op0=ALU.mult)
        P.dma("pool", mbcs, mb_causs_d, mbcsr, writes=[mbcsr])
        P.dma("sp", M0, sm_d.rearrange("j h -> h j"), M0r, writes=[M0r], allow_slow_non_contiguous=True)
        P.dma("sp", E16, eseq_d, E16r, writes=[E16r])
        P.dma("sp", SNn, sn_d.rearrange("j h k -> (j h) k"), SNnr, writes=[SNnr])

        norm_T(Y[:, TS, :], Yr[TS], 0, M.XNTg[:, :, 0:128], [M.XNTgr[0]])
        tok_major(0, slice(0, 128), M.XNTgr[0], 0, want_kv_out="sample")
        gates(128, M.XNTgr, "sample")
        feat64(M.WK, M.WKr, 2, M.KT, [M.KTr[0]], 128, M.XNTgr, dcol0=0)
        feat64(M.WQ, M.WQr, 8, M.QT, [M.QTr], 128, M.XNTgr)
        feat64(M.WMQ, M.WMQr, 4, M.MQT, [M.MQTr], 128, M.XNTgr)
        feat64(M.WMK, M.WMKr, 4, M.MKT, [M.MKTr], 128, M.XNTgr, scale=0.125)
        for h0 in (0, 2):
            bk, bkr = bank()
            for hh in range(2):
                h = h0 + hh
                for k in range(8):
                    mm(bk[:, hh * 256:hh * 256 + 128], M.WOG[:, k, h * 128:(h + 1) * 128], M.XNTg[:, k, 0:128], k == 0, k == 7,
                       [M.WOGr] + M.XNTgr, bkr)
            sgv = M.SGT[:, h0:h0 + 2, :]
            I("act", "activation", [bkr], [M.SGTr], out=sgv, in_=bk[:, :].rearrange("p (a b) -> p a b", a=2)[:, :, 0:128],
              func=AF.Exp, scale=-1.0)
            I("act", "activation", [M.SGTr], [M.SGTr], out=sgv, in_=sgv, func=AF.Ln, bias=1.0)
            I("act", "activation", [M.SGTr], [M.SGTr], out=sgv, in_=sgv, func=AF.Exp, scale=-1.0)
        for j in range(16):
            I("dve", "tensor_tensor_scan", [M.Gr, M.G_Br, onesfr], [M.G_Br], out=M.G_B[:, 1 + 8 * j:9 + 8 * j],
              data0=onesf[0:4, 0:8], data1=M.G_L1[:, 8 * j:8 * j + 8], initial=0.0, op0=ALU.mult, op1=ALU.subtract)
        I("dve", "tensor_tensor", [M.Gr, M.G_Br], [M.G_Ar], out=M.G_A[:, 0:128], in0=M.G_IG[:, 1:129], in1=M.G_B[:, 1:129],
          op=ALU.subtract)
        for j in range(16):
            I("dve", "tensor_tensor_scan", [M.G_Ar, M.G_Mr, onesfr, M0r], [M.G_Mr], out=M.G_M[:, 1 + 8 * j:9 + 8 * j],
              data0=onesf[0:4, 0:8], data1=M.G_A[:, 8 * j:8 * j + 8], initial=M0[:, j:j + 1], op0=ALU.mult, op1=ALU.max)
        I("dve", "tensor_tensor", [M.G_Br, M.G_Mr], [M.G_BMr], out=M.G_BM[:, 0:128], in0=M.G_B[:, 1:129], in1=M.G_M[:, 1:129],
          op=ALU.add)
        GM3 = M.G_M[:, 1:129].rearrange("p (j i) -> p j i", i=8)
        I("dve", "tensor_tensor", [M.G_Mr, M0r], [M.G_DMr], out=M.G_DM[:, 0:128].rearrange("p (j i) -> p j i", i=8), in0=GM3,
          in1=M0[:, :].unsqueeze(2).broadcast_to([4, 16, 8]), op=ALU.subtract)
        I("dve", "tensor_copy", [M.G_Mr], [MTer], out=MTe[:, :].rearrange("p (j i) -> p j i", i=8),
          in_=GM3[:, :, 7:8].broadcast_to([4, 16, 8]))
        I("dve", "tensor_tensor", [M.G_Mr, M0r], [DMTr], out=DMT[:, :].unsqueeze(2), in0=GM3[:, :, 7:8], in1=M0[:, :].unsqueeze(2),
          op=ALU.subtract)
        P.dma("sp", ms_o.rearrange("j h -> h j"), M.G_BM[:, 0:128].rearrange("p (j i) -> p j i", i=8)[:, :, 7], M.G_BMr,
              reads=[M.G_BMr], allow_slow_non_contiguous=True)

        pair_i = [0]
        QS = A([2, 16, 32], BF16, parts=64); QSr = AR("QS")
        for h in range(2):
            I("act", "activation", [M.QTr], [QSr], out=QS[:, h, :, :].rearrange("p j (g i) -> p j g i", i=8),
              in_=M.QT[:, 4 * h:4 * h + 4, :].rearrange("p g (j i) -> p j g i", i=8), func=AF.Copy)
        for h in range(2):
            for q4 in range(2):
                for jj in range(8):
                    j = q4 * 8 + jj
                    I("pe", "transpose", [CKnr, identr], [tbr], out=tb[0:64, jj * 128:(jj + 1) * 128],
                      in_=CKn[:, j, h * 64:(h + 1) * 64], identity=identb[:])
                I("act", "activation", [tbr], [CKTr], out=CKT[:, q4 * 8:(q4 + 1) * 8, :],
                  in_=tb[0:64, :].rearrange("p (a b) -> p a b", a=8), func=AF.Copy)
            for half in range(2):
                po, por = banks[4], bres[4]
                strs = []
                for sk in range(4):
                    P.rec_begin(); bset[0] = [sk]
                    for jj in (sk, sk + 4):
                        j = half * 8 + jj
                        b = sk
                        bk, bkr = bank()
                        lq = QS[:, h, j, :]
                        mm(bk[0:32, 0:128], lq, CKT[:, j, :], True, False, [QSr, CKTr], bkr)
                        mm(bk[0:32, 0:128], identb[0:32, 0:32], SMC, False, True, [identr, SMCr], bkr)
                        mm(bk[0:32, 128:256], lq, M.KT[:, h, 0:128], True, False, [QSr, M.KTr[0]], bkr)
                        mm(bk[0:32, 128:256], identb[0:32, 0:32], SMN[:, j, :], False, True, [identr, SMNr], bkr)
                        st_ = sms[:, b, :]
                        I("dve", "reduce_max", [bkr], [smsr[b]], out=st_[:, 0:1], in_=bk[0:32, 0:256], axis=AX.X)
                        I("dve", "tensor_scalar", [smsr[b]], [smsr[b]], out=st_[:, 0:1], in0=st_[:, 0:1], scalar1=-0.125,
                          scalar2=None, op0=ALU.mult)
                        I("dve", "tensor_tensor", [smsr[b], SINKCr], [smsr[b]], out=st_[:, 0:1], in0=st_[:, 0:1],
                          in1=SINKC[:, 2 + h:3 + h], op=ALU.min)
                        I("act", "activation", [bkr, smsr[b]], [PNsr[b], smsr[b]], out=PNs[:, b, :], in_=bk[0:32, 0:256],
                          func=AF.Exp, bias=st_[:, 0:1], scale=0.125, accum_out=st_[:, 1:2])
                        I("act", "activation", [SINKCr, smsr[b]], [smsr[b]], out=st_[:, 2:3], in_=SINKC[:, h:h + 1], func=AF.Exp,
                          bias=st_[:, 0:1])
                        I("dve", "tensor_tensor", [smsr[b]], [smsr[b]], out=st_[:, 2:3], in0=st_[:, 2:3], in1=st_[:, 1:2],
                          op=ALU.add)
                        I("dve", "reciprocal", [smsr[b]], [smsr[b]], out=st_[:, 3:4], in_=st_[:, 2:3])
                        I("dve", "tensor_scalar", [PNsr[b], smsr[b]], [PNsr[b]], out=PNs[:, b, :], in0=PNs[:, b, :],
                          scalar1=st_[:, 3:4], scalar2=None, op0=ALU.mult)
                        for c2 in range(2):
                            I("pe", "transpose", [PNsr[b], identr], [tbr], out=tb[:, jj * 64 + c2 * 32:jj * 64 + (c2 + 1) * 32],
                              in_=PNs[:, b, c2 * 128:(c2 + 1) * 128], identity=identb[0:32, 0:32])
                    strs.append(P.rec_end())
                P.merge(strs)
                bset[0] = [0, 1, 2, 3]
                I("dve", "tensor_copy", [tbr], [PTSr], out=PTS[:, 0:512], in_=tb[:, 0:512])
                for jj in range(8):
                    j = half * 8 + jj
                    o = po[0:64, jj * 32:(jj + 1) * 32]
                    mm(o, CV[:, j, h * 64:(h + 1) * 64], PTS[:, jj * 64:jj * 64 + 32], True, False, [CVr, PTSr], por)
                    mm(o, M.Vt[:, 0, h * 64:(h + 1) * 64], PTS[:, jj * 64 + 32:jj * 64 + 64], False, True, [M.Vtr[0], PTSr], por)
                I("act", "activation", [por], [M.ATTTr[0]],
                  out=M.ATTT[:, 4 * h:4 * h + 4, half * 64:(half + 1) * 64].rearrange("p g (j i) -> p j g i", i=8),
                  in_=po[0:64, 0:256].rearrange("p (j g i) -> p j g i", g=4, i=8), func=AF.Copy)

        bset[0] = [0, 1, 2, 3, 4]
        off = 0
        SCf, off = A_at(off, [64, 64], F32); SCfr = ARalias("SCf", R0_olds)
        SCT, off = A_at(off, [64, 128], BF16, parts=64); SCTr = ARalias("SCT", R0_olds)
        QN, off = A_at(off, [4, 128], BF16, parts=64); QNr = ARalias("QN", R0_olds)
        KJ, off = A_at(off, [16, 64], BF16); KJr = ARalias("KJ", R0_olds)
        assert off <= 18496
        P.dma("sp", SCf[:, :, :], sC_d.rearrange("j h p k -> p (j h) k"), SCfr, writes=[SCfr])
        for p4 in range(16):
            bk, bkr = bank()
            for q_ in range(4):
                pr = p4 * 4 + q_
                mm(bk[0:64, q_ * 128:(q_ + 1) * 128], SCf[:, pr, :], identf[:], True, True, [SCfr, identfr], bkr)
            I("act", "activation", [bkr], [SCTr], out=SCT[:, p4 * 4:(p4 + 1) * 4, :],
              in_=bk[0:64, :].rearrange("p (a b) -> p a b", a=4), func=AF.Copy)
        bk, bkr = bank()
        mm(bk[0:64, 0:64], SNn, identf[0:64, 0:64], True, True, [SNnr, identfr], bkr)
        I("act", "activation", [bkr], [SNTr], out=SNT, in_=bk[0:64, 0:64], func=AF.Copy)

        def sample_inter(kind, h=None, pb=None, pbr=None):
            if kind == "pre":
                I("dve", "tensor_tensor", [M.QWr, SNTr], [QNr], out=QN[:, :, :].rearrange("p h (j i) -> p h j i", i=8),
                  in0=M.QW[:, :, :].rearrange("p h (j i) -> p h j i", i=8),
                  in1=SNT.rearrange("p (j h) -> p h j", h=4).unsqueeze(3).broadcast_to([64, 4, 16, 8]), op=ALU.mult)
            elif kind == "num":
                for j in range(16):
                    mm(pb[:, h * 128 + 8 * j:h * 128 + 8 * j + 8], SCT[:, j * 4 + h, :], M.QW[:, h, 8 * j:8 * j + 8], False, j == 15,
                       [SCTr, M.QWr], pbr)
            else:
                mm(pb[:, h * 128:(h + 1) * 128], onesb[0:64, :], QN[:, h, :], False, True, [onesbr, QNr], pbr)

        pn = mlstm_chunk(0, 0, mbcs, mbcsr, inter_fn=sample_inter)
        mlstm_finish(*pn, 0, M.HMTr[0])
        wout_tile(0, TS)

        pw, pwr = bank()
        for h in range(4):
            mm(pw[:, h:h + 1], M.G_A[:, 0:128], SELh(h, 1), True, False, [M.G_Ar, selr], pwr)
            mm(pw[:, h:h + 1], MTe, SEL[:, 512 + h * 128:512 + h * 128 + 1], False, True, [MTer, selr], pwr)
        for h in range(4):
            mm(pw[:, 8 + 16 * h:8 + 16 * (h + 1)], NSELh(h), DMT, True, True, [DMTr, selr], pwr)
        I("act", "activation", [pwr], [M.WKCr], out=M.WKC[:, 0:4], in_=pw[:, 0:4], func=AF.Exp)
        I("act", "activation", [pwr], [WCBr], out=WCB[:, :, :], in_=pw[:, 8:72].rearrange("p (h j) -> p h j", h=4), func=AF.Exp)
        for h in range(4):
            I("dve", "tensor_scalar", [M.MVaugr[0], M.WKCr], [M.VWr[h]], out=M.VW[:, h, :], in0=M.MVaug[:, 0, h, :],
              scalar1=M.WKC[:, h:h + 1], scalar2=None, op0=ALU.mult)
            I("dve", "tensor_scalar", [E16r, M.WKCr], [EWr], out=EW[:, h, :], in0=E16, scalar1=M.WKC[:, h:h + 1], scalar2=None,
              op0=ALU.mult)
        bk, bkr = bank()
        for h in range(4):
            mm(bk[0:64, h * 16:(h + 1) * 16], M.MKtok[:, 0, h * 64:(h + 1) * 64], EW[:, h, :], True, True, [M.MKtokr[0], EWr], bkr)
        I("dve", "tensor_tensor", [SNTr, WCBr], [NNTr], out=NNT.rearrange("p (j h) -> p h j", h=4),
          in0=SNT.rearrange("p (j h) -> p h j", h=4), in1=WCB[0:64, :, :], op=ALU.mult)
        I("dve", "tensor_tensor", [NNTr, bkr], [NNTr], out=NNT.rearrange("p (j h) -> p h j", h=4),
          in0=NNT.rearrange("p (j h) -> p h j", h=4), in1=bk[0:64, 0:64].rearrange("p (h j) -> p h j", h=4), op=ALU.add)
        bk2, bk2r = bank()
        mm(bk2[0:64, 0:64], NNT, identf[0:64, 0:64], True, True, [NNTr, identfr], bk2r)
        I("act", "activation", [bk2r], [NNor], out=NNo, in_=bk2[0:64, 0:64], func=AF.Copy)
        P.dma("sp", ns_o.rearrange("j h k -> (j h) k"), NNo, NNor, reads=[NNor])
        for h in range(4):
            I("dve", "tensor_tensor", [M.MKtokr[0], E16r], [KJr], out=KJ[:, :, :],
              in0=M.MKtok[:, 0, h * 64:(h + 1) * 64].unsqueeze(1).broadcast_to([128, 16, 64]),
              in1=E16.unsqueeze(2).broadcast_to([128, 16, 64]), op=ALU.mult)
            for half in range(2):
                bk, bkr = bank()
                mm(bk[:, :], M.VW[:, h, 0:128], KJ[:, half * 8:(half + 1) * 8, :], True, True, [M.VWr[h], KJr], bkr)
                scv = SCf[:, :, :].rearrange("p (j h) k -> p h j k", h=4)[:, h, half * 8:(half + 1) * 8, :]
                I("dve", "tensor_tensor", [SCfr, WCBr], [SCfr], out=scv, in0=scv,
                  in1=WCB[:, h, half * 8:(half + 1) * 8].unsqueeze(2).broadcast_to([128, 8, 64]), op=ALU.mult)
                I("dve", "tensor_tensor", [SCfr, bkr], [SCfr], out=scv, in0=scv,
                  in1=bk[:, :].rearrange("p (j k) -> p j k", k=64), op=ALU.add)
        P.dma("sp", Cs_o.rearrange("j h p k -> p (j h) k"), SCf[:, :, :], SCfr, reads=[SCfr])

        def dump_y(tiles):
            yo = yp_o.rearrange("(t p) d -> t p d", p=128)
            for t in tiles:
                if Yr[t].last_w is None:
                    continue
                if t < NTP:
                    P.dma("sp", yo[t], Y[:, t, :], Yr[t], reads=[Yr[t]])
                else:
                    P.dma("sp", ys_o, Y[:, t, :], Yr[t], reads=[Yr[t]])

        if stage <= 1:
            dump_y(range(NT))
            P.emit()
            return nc, P

        new_phase()
        WCQ = A([8, 256], BF16); WCQr = AR("WCQ")
        WCKV = A([8, 512], BF16); WCKVr = AR("WCKV")
        WCO = A([4, 1024], BF16, parts=64); WCOr = AR("WCO")
        P.dma("pool", WCQ[:], w_cq_d.rearrange("(k p) n -> p k n", p=128), WCQr, writes=[WCQr])
        P.dma("pool", WCKV[:, :, 0:256], w_ck_d.rearrange("(k p) n -> p k n", p=128), WCKVr, writes=[WCKVr], group=True)
        P.dma("pool", WCKV[:, :, 256:512], w_cv_d.rearrange("(k p) n -> p k n", p=128), WCKVr, writes=[WCKVr], group=True)
        P.dma("pool", WCO[:], w_co_d.rearrange("(h d) n -> d h n", d=64), WCOr, writes=[WCOr])
        MEMX = A([2, D], F32); MEMXr = [AR("MEMX0"), AR("MEMX1")]
        MNT = A([8, 256], BF16); MNTr = [AR("MNT0"), AR("MNT1")]
        MKTm = A([4, 256], BF16, parts=64); MKTmr = AR("MKTm")
        MVm = A([2, 256], BF16); MVmr = AR("MVm")
        MKVo = A([2, 512], F32); MKVor = [AR("MKVo0"), AR("MKVo1")]
        GB = 4
        XNTb = A([8, GB * 128], BF16); XNTbr = [AR(f"XNTb{i}") for i in range(GB)]
        QcT = A([4, GB * 128], BF16, parts=64); QcTr = AR("QcT")
        OcT = A([4, GB * 128], BF16, parts=64); OcTr = [AR(f"OcT{i}") for i in range(GB)]
        Eb2s = [A([4, 256], BF16) for _ in range(2)]; Eb2rs = [AR("Eb2a"), AR("Eb2b")]
        PT2s = [A([1, 1024], BF16) for _ in range(2)]; PT2rs = [AR("PT2a"), AR("PT2b")]
        sm2s = [A([1, 32], F32)[:, 0, :] for _ in range(2)]; sm2rs = [AR("sm2a"), AR("sm2b")]
        tbh = [tb[:, 0:512], tb[:, 512:1024]]
        tbhr = [P.res("tbA"), P.res("tbB")]
        for r_ in tbhr:
            r_.excl = True
            r_.readers = [x for x in ([tbr.last_w] if tbr.last_w is not None else [])] + list(tbr.readers)
        mem_t = mem_d.rearrange("(t p) d -> t p d", p=128)
        import os
        SK = os.environ.get("SKIP", "")
        for mt in range(2):
            P.dma("sp", MEMX[:, mt, :], mem_t[mt], MEMXr[mt], writes=[MEMXr[mt]])
        for mt in (range(2) if "noBnorm" not in SK else []):
            norm_T(MEMX[:, mt, :], MEMXr[mt], 2, MNT[:, :, mt * 128:(mt + 1) * 128], [MNTr[mt]])
        for mt in (range(2) if "noBkv" not in SK else []):
            bk, bkr = bank()
            for k in range(8):
                mm(bk[:, :], MNT[:, k, mt * 128:(mt + 1) * 128], WCKV[:, k, :], k == 0, k == 7, [MNTr[mt], WCKVr], bkr)
            if "noBcp1" not in SK:
                I("act", "activation", [bkr], [MKVor[mt]], out=MKVo[:, mt, :], in_=bk[:, :], func=AF.Copy)
            if "noBcp2" not in SK:
                I("act", "activation", [bkr], [MVmr], out=MVm[:, mt, :], in_=bk[:, 256:512], func=AF.Copy)
            if "noBdma" not in SK:
                P.dma("sp", memk_o[mt * 128:(mt + 1) * 128, :], MKVo[:, mt, 0:256], MKVor[mt], reads=[MKVor[mt]], group=True)
                P.dma("sp", memv_o[mt * 128:(mt + 1) * 128, :], MKVo[:, mt, 256:512], MKVor[mt], reads=[MKVor[mt]], group=True)
        for h0 in ((0, 2) if "noBkt" not in SK else []):
            bk, bkr = bank()
            for hh in range(2):
                h = h0 + hh
                for k in range(8):
                    mm(bk[0:64, hh * 256:(hh + 1) * 256], WCKV[:, k, h * 64:(h + 1) * 64], MNT[:, k, :], k == 0, k == 7,
                       [WCKVr] + MNTr, bkr)
            I("act", "activation", [bkr], [MKTmr], out=MKTm[:, h0:h0 + 2, :],
              in_=bk[0:64, :].rearrange("p (a b) -> p a b", a=2), func=AF.Copy)

        def cross_q(ntok, xres):
            for h in range(4):
                bk, bkr = bank()
                for k in range(8):
                    mm(bk[0:64, 0:ntok], WCQ[:, k, h * 64:(h + 1) * 64], XNTb[:, k, 0:ntok], k == 0, k == 7, [WCQr] + xres, bkr)
                I("act", "activation", [bkr], [QcTr], out=QcT[:, h, 0:ntok], in_=bk[0:64, 0:ntok], func=AF.Copy)

        def cross_tile_prompt(ti, sx):
            Eb2, Eb2r, PT2, PT2r, sm2, sm2r, tbx, tbxr = Eb2s[sx], Eb2rs[sx], PT2s[sx], PT2rs[sx], sm2s[sx], sm2rs[sx], tbh[sx], tbhr[sx]
            qs = slice(ti * 128, (ti + 1) * 128)
            bks = [bank(), bank()]
            for h in range(4):
                bk, bkr = bks[h // 2]
                mm(bk[:, (h % 2) * 256:(h % 2 + 1) * 256], QcT[:, h, qs], MKTm[:, h, :], True, True, [QcTr, MKTmr], bkr)
            for j in range(2):
                I("dve", "reduce_max", [bks[j][1]], [sm2r], out=sm2[:, 2 * j:2 * j + 2],
                  in_=bks[j][0][:, :].rearrange("p (a b) -> p a b", a=2), axis=AX.X)
            I("dve", "tensor_scalar", [sm2r], [sm2r], out=sm2[:, 0:4], in0=sm2[:, 0:4], scalar1=-0.125, scalar2=None, op0=ALU.mult)
            for h in range(4):
                bk, bkr = bks[h // 2]
                I("act", "activation", [bkr, sm2r], [Eb2r, sm2r], out=Eb2[:, h, :], in_=bk[:, (h % 2) * 256:(h % 2 + 1) * 256],
                  func=AF.Exp, bias=sm2[:, h:h + 1], scale=0.125, accum_out=sm2[:, 4 + h:5 + h])
            I("dve", "reciprocal", [sm2r], [sm2r], out=sm2[:, 8:12], in_=sm2[:, 4:8])
            for h in range(4):
                if h % 2 == 0:
                    I("act", "activation", [Eb2r, sm2r], [Eb2r], out=Eb2[:, h, :], in_=Eb2[:, h, :], func=AF.Copy,
                      scale=sm2[:, 8 + h:9 + h])
                else:
                    I("dve", "tensor_scalar", [Eb2r, sm2r], [Eb2r], out=Eb2[:, h, :], in0=Eb2[:, h, :],
                      scalar1=sm2[:, 8 + h:9 + h], scalar2=None, op0=ALU.mult)
            po, por = bank()
            for mc in range(2):
                for h in range(4):
                    I("pe", "transpose", [Eb2r, identr], [tbxr], out=tbx[:, h * 128:(h + 1) * 128],
                      in_=Eb2[:, h, mc * 128:(mc + 1) * 128], identity=identb[:])
                if mc == 0:
                    I("dve", "tensor_copy", [tbxr], [PT2r], out=PT2[:, 0, 0:512], in_=tbx)
                else:
                    I("act", "activation", [tbxr], [PT2r], out=PT2[:, 0, 512:1024], in_=tbx, func=AF.Copy)
            for h in range(4):
                for mc in range(2):
                    mm(po[0:64, h * 128:(h + 1) * 128], MVm[:, mc, h * 64:(h + 1) * 64],
                       PT2[:, 0, (mc * 4 + h) * 128:(mc * 4 + h + 1) * 128], mc == 0, mc == 1, [MVmr, PT2r], por)
            I("act", "activation", [por], [OcTr[ti]], out=OcT[:, :, qs], in_=po[0:64, :].rearrange("p (h q) -> p h q", h=4),
              func=AF.Copy)

        def wco_tile(ti, t):
            qs = slice(ti * 128, (ti + 1) * 128)
            for c in range(2):
                bk, bkr = bank()
                cc = slice(c * 512, (c + 1) * 512)
                for h in range(4):
                    mm(bk[:, :], OcT[:, h, qs], WCO[:, h, cc], h == 0, h == 3, [OcTr[ti], WCOr], bkr)
                I("dve", "tensor_tensor", [Yr[t], bkr], [Yr[t]], out=Y[:, t, cc], in0=Y[:, t, cc], in1=bk[:, :], op=ALU.add)

        import os
        for g0 in (range(0, NTP, GB) if "noBloop" not in os.environ.get("SKIP", "") else []):
            for ti in range(GB):
                norm_T(Y[:, g0 + ti, :], Yr[g0 + ti], 1, XNTb[:, :, ti * 128:(ti + 1) * 128], [XNTbr[ti]])
            cross_q(GB * 128, XNTbr)
            for tp_ in range(0, GB, 2):
                strs = []
                for sx in range(2):
                    P.rec_begin(); bset[0] = [0, 1, 2] if sx == 0 else [3, 4, 5]
                    cross_tile_prompt(tp_ + sx, sx)
                    wco_tile(tp_ + sx, g0 + tp_ + sx)
                    strs.append(P.rec_end())
                P.merge(strs)
            bset[0] = [0, 1, 2, 3, 4]
        TS = NTP
        CMn = A([8, 2, 256], BF16); CMnr = AR("CMn")
        CMV = A([16, 2, 256], BF16); CMVr = AR("CMV")
        CMKT = A([8, 4, 256], BF16, parts=64); CMKTr = AR("CMKT")
        Es = A([2, 4, 256], BF16, parts=8); Esr = [AR("Es0"), AR("Es1")]
        sm3 = A([2, 16], F32, parts=8); sm3r = [AR("sm30"), AR("sm31")]
        PT3 = A([1, 1024], BF16)[:, 0, :]; PT3r = AR("PT3")
        P.dma("pool", CMV[:, :, :, :], cmv_d.rearrange("j (c p) f -> p j c f", p=128), CMVr, writes=[CMVr])
        norm_T(Y[:, TS, :], Yr[TS], 1, XNTb[:, :, 0:128], [XNTbr[0]])
        cross_q(128, [XNTbr[0]])
        po3, po3r = banks[5], bres[5]
        for half in range(2):
            P.dma("pool", CMn[:, :, :, :], cmk_d[half * 8:(half + 1) * 8].rearrange("j (c p) f -> p j c f", p=128), CMnr,
                  writes=[CMnr])
            for jj in range(8):
                for h in range(4):
                    for mc in range(2):
                        I("pe", "transpose", [CMnr, identr], [tbr], out=tb[0:64, (h * 2 + mc) * 128:(h * 2 + mc + 1) * 128],
                          in_=CMn[:, jj, mc, h * 64:(h + 1) * 64], identity=identb[:])
                I("act", "activation", [tbr], [CMKTr], out=CMKT[:, jj, :, :],
                  in_=tb[0:64, :].rearrange("p (h m) -> p h m", h=4), func=AF.Copy)
            for jj in range(8):
                j = half * 8 + jj
                b = j % 2
                bks = [bank(), bank()]
                for h in range(4):
                    bk, bkr = bks[h // 2]
                    mm(bk[0:8, (h % 2) * 256:(h % 2 + 1) * 256], QcT[:, h, 8 * j:8 * j + 8], CMKT[:, jj, h, :], True, True,
                       [QcTr, CMKTr], bkr)
                st_ = sm3[:, b, :]
                for q_ in range(2):
                    I("dve", "reduce_max", [bks[q_][1]], [sm3r[b]], out=st_[:, 2 * q_:2 * q_ + 2],
                      in_=bks[q_][0][0:8, :].rearrange("p (a b) -> p a b", a=2), axis=AX.X)
                I("dve", "tensor_scalar", [sm3r[b]], [sm3r[b]], out=st_[:, 0:4], in0=st_[:, 0:4], scalar1=-0.125, scalar2=None,
                  op0=ALU.mult)
                for h in range(4):
                    bk, bkr = bks[h // 2]
                    I("act", "activation", [bkr, sm3r[b]], [Esr[b], sm3r[b]], out=Es[:, b, h, :],
                      in_=bk[0:8, (h % 2) * 256:(h % 2 + 1) * 256], func=AF.Exp, bias=st_[:, h:h + 1], scale=0.125,
                      accum_out=st_[:, 4 + h:5 + h])
                I("dve", "reciprocal", [sm3r[b]], [sm3r[b]], out=st_[:, 8:12], in_=st_[:, 4:8])
                I("dve", "tensor_tensor", [Esr[b], sm3r[b]], [Esr[b]], out=Es[:, b, :, :], in0=Es[:, b, :, :],
                  in1=st_[:, 8:12].unsqueeze(2).broadcast_to([8, 4, 256]), op=ALU.mult)
                for mc in range(2):
                    for h in range(4):
                        c0_ = j * 64 + (mc * 4 + h) * 8
                        I("pe", "transpose", [Esr[b], identr], [tbr], out=tb[:, c0_:c0_ + 8],
                          in_=Es[:, b, h, mc * 128:(mc + 1) * 128], identity=identb[0:8, 0:8])
            I("dve", "tensor_copy", [tbr], [PT3r], out=PT3[:, half * 512:(half + 1) * 512], in_=tb[:, half * 512:(half + 1) * 512])
        for j in range(16):
            for h in range(4):
                for mc in range(2):
                    c0_ = j * 64 + (mc * 4 + h) * 8
                    mm(po3[0:64, (j * 4 + h) * 8:(j * 4 + h) * 8 + 8], CMV[:, j, mc, h * 64:(h + 1) * 64], PT3[:, c0_:c0_ + 8],
                       mc == 0, mc == 1, [CMVr, PT3r], po3r)
        I("act", "activation", [po3r], [OcTr[0]], out=OcT[:, :, 0:128].rearrange("p h (j i) -> p j h i", i=8),
          in_=po3[0:64, :].rearrange("p (j h i) -> p j h i", h=4, i=8), func=AF.Copy)
        wco_tile(0, TS)

        if stage <= 2:
            dump_y(range(NT))
            P.emit()
            return nc, P

        new_phase()
        USE_SQRT[0] = True
        NF = FH // 128
        XNTa = A([8, NT * 128], BF16); XNTar = [AR(f"XNTa{t}") for t in range(NT)]
        NSLOT = 12
        WG = A([NSLOT, 8, 128], BF16); WU = A([NSLOT, 8, 128], BF16); WD = A([NSLOT, D], BF16)
        Wsr = [AR(f"Ws{s_}") for s_ in range(NSLOT)]
        Hh = A([6, 512], BF16); Hr = [AR(f"H{j}") for j in range(6)]
        SG = A([2, 512], BF16); SGr = [AR("SG0"), AR("SG1")]
        OUT = A([1, D], F32)[:, 0, :]; OUTr = AR("OUT")
        gfin = A([1, D], F32)[:, 0, :]; gfinr = AR("gfin")
        P.dma("sp", gfin, g_final_d.partition_broadcast(128), gfinr, writes=[gfinr])
        passes = [list(range(0, 6)), list(range(6, 12)), list(range(12, 17)), list(range(17, 22))]
        groups = [(0, 4), (4, 4), (8, 4), (12, 4), (16, 1)]
        wd_v = w_down_d.rearrange("(f p) n -> f p n", p=128)
        wslot = {}
        nload = [0]

        def load_w(f):
            s_ = nload[0] % NSLOT
            nload[0] += 1
            wslot[f] = s_
            P.dma("pool", WG[:, s_], w_gate_d[:, f * 128:(f + 1) * 128].rearrange("(k p) n -> p k n", p=128), Wsr[s_],
                  writes=[Wsr[s_]], group=True)
            P.dma("pool", WU[:, s_], w_up_d[:, f * 128:(f + 1) * 128].rearrange("(k p) n -> p k n", p=128), Wsr[s_],
                  writes=[Wsr[s_]], group=True)
            P.dma("pool", WD[:, s_], wd_v[f], Wsr[s_], writes=[Wsr[s_]], group=True)

        for f in passes[0]:
            load_w(f)
        gcnt = [0]
        for pi, fl in enumerate(passes):
            for gi, (t0, n) in enumerate(groups):
                if pi == 0:
                    for t in range(t0, t0 + n):
                        norm_T(Y[:, t, :], Yr[t], 3, XNTa[:, :, t * 128:(t + 1) * 128], [XNTar[t]])
                if pi + 1 < len(passes) and gi == 0:
                    for f in passes[pi + 1]:
                        load_w(f)
                ntok = n * 128
                tok = slice(t0 * 128, t0 * 128 + ntok)
                xr = [XNTar[t] for t in range(t0, t0 + n)]
                for j, f in enumerate(fl):
                    s_ = wslot[f]
                    b = gcnt[0] % 2
                    gcnt[0] += 1
                    pg, pgr = bank()
                    pu, pur = bank()
                    for k in range(8):
                        mm(pg[:, 0:ntok], WG[:, s_, k, :], XNTa[:, k, tok], k == 0, k == 7, [Wsr[s_]] + xr, pgr)
                    for k in range(8):
                        mm(pu[:, 0:ntok], WU[:, s_, k, :], XNTa[:, k, tok], k == 0, k == 7, [Wsr[s_]] + xr, pur)
                    I("act", "activation", [pgr], [SGr[b]], out=SG[:, b, 0:ntok], in_=pg[:, 0:ntok], func=AF.Silu)
                    I("dve", "tensor_tensor", [SGr[b], pur], [Hr[j]], out=Hh[:, j, 0:ntok], in0=SG[:, b, 0:ntok],
                      in1=pu[:, 0:ntok], op=ALU.mult)
                for ti in range(n):
                    t = t0 + ti
                    for c in range(2):
                        pd, pdr = bank()
                        for j, f in enumerate(fl):
                            s_ = wslot[f]
                            mm(pd[:, :], Hh[:, j, ti * 128:(ti + 1) * 128], WD[:, s_, c * 512:(c + 1) * 512], j == 0,
                               j == len(fl) - 1, [Hr[j], Wsr[s_]], pdr)
                        I("dve", "tensor_tensor", [Yr[t], pdr], [Yr[t]], out=Y[:, t, c * 512:(c + 1) * 512],
                          in0=Y[:, t, c * 512:(c + 1) * 512], in1=pd[:, :], op=ALU.add)
                if pi == len(passes) - 1:
                    yo = yp_o.rearrange("(t p) d -> t p d", p=128)
                    for t in range(t0, t0 + n):
                        rstd, sr = norm_stats(Y[:, t, :], Yr[t], 0)
                        I("dve", "scalar_tensor_tensor", [Yr[t], sr, gfinr], [OUTr], out=OUT, in0=Y[:, t, :], scalar=rstd,
                          in1=gfin, op0=ALU.mult, op1=ALU.mult)
                        P.dma("sp", (yo[t] if t < NTP else ys_o), OUT, OUTr, reads=[OUTr])
        P.emit()
        return nc, P


def make_consts(hf):
    c = {}
    c["c_ident"] = np.eye(128, dtype=np.float32)
    i = np.arange(128)[:, None]; j = np.arange(256)[None, :]
    band = np.where((j >= i) & (j <= i + 128), 0.0, NEG).astype(np.float32)
    first = band.copy()
    if hf == 0:
        first[:, :128] = NEG
    c["c_mb_band"] = band; c["c_mb_first"] = first
    s = np.arange(128)[:, None]; t = np.arange(128)[None, :]
    c["c_mb_caus"] = np.where(s <= t, 0.0, NEG).astype(np.float32)
    c["c_mb_causs"] = np.where((s <= t) & (s // 8 == t // 8), 0.0, NEG).astype(np.float32)
    sel = np.zeros((4, 1024), np.float32)
    for h in range(4):
        sel[h, h * 128:(h + 1) * 128] = 1.0
        sel[h, 512 + h * 128:512 + (h + 1) * 128] = -1.0
    c["c_sel"] = sel
    pm = np.zeros((4, 2), np.float32)
    pm[:, 0] = 1.0 if hf else 0.0
    pm[:, 1] = 0.0 if hf else NEG
    c["c_pmask"] = pm
    r = np.arange(32)[:, None] % 8
    p = np.arange(128)[None, :]
    c["c_smc"] = np.where(p >= r, 0.0, NEG).astype(np.float32)
    smn = np.full((32, 16, 128), NEG, np.float32)
    for jq in range(16):
        for ii in range(8):
            smn[(np.arange(32) % 8) >= ii, jq, jq * 8 + ii] = 0.0
    c["c_smn"] = smn
    c["c_bt"] = np.where((s <= t) & (s // 8 == t // 8), 1.0, 0.0).astype(np.float32)
    e = np.zeros((128, 16), np.float32); e[np.arange(128), np.arange(128) // 8] = 1.0
    c["c_eseq"] = e
    return c

def shard_inputs(inp):
    maps = []
    W = ["w_in", "b_igate", "b_fgate", "attn_sinks", "g_mlstm_head", "w_out", "g_mix", "g_cross", "g_mem",
         "w_cq", "w_ck", "w_cv", "w_co", "g_ffn", "w_gate", "w_up", "w_down"]
    wd = {k: np.ascontiguousarray(np.asarray(inp[k], np.float32)[0]) for k in W}
    wd["g_final"] = np.ascontiguousarray(np.asarray(inp["g_final"], np.float32))
    xp = np.asarray(inp["x_prompt"], np.float32); xs = np.asarray(inp["x_sample"], np.float32)
    for c in range(8):
        b, hf = c // 2, c % 2
        m = dict(wd)
        m["xp"] = np.ascontiguousarray(xp[b, hf * 2048:(hf + 1) * 2048])
        m["xpre"] = np.ascontiguousarray(xp[b, 0:2048]) if hf else np.zeros((2048, 1024), np.float32)
        m["xs"] = np.ascontiguousarray(xs[16 * c:16 * c + 16].reshape(128, 1024))
        m["mem"] = np.ascontiguousarray(np.asarray(inp["mem_prompt"], np.float32)[b])
        sl = slice(16 * c, 16 * c + 16)
        m["csk"] = np.ascontiguousarray(np.asarray(inp["cache_swa_k"], np.float32)[0, sl].reshape(16, 128, 128))
        m["csv"] = np.ascontiguousarray(np.asarray(inp["cache_swa_v"], np.float32)[0, sl].reshape(16, 128, 128))
        m["sC"] = np.ascontiguousarray(np.asarray(inp["state_mlstm_C"], np.float32)[0, sl])
        m["sn"] = np.ascontiguousarray(np.asarray(inp["state_mlstm_n"], np.float32)[0, sl])
        m["sm"] = np.ascontiguousarray(np.asarray(inp["state_mlstm_m"], np.float32)[0, sl])
        m["cmk"] = np.ascontiguousarray(np.asarray(inp["cache_mem_k"], np.float32)[0, sl].reshape(16, 256, 256))
        m["cmv"] = np.ascontiguousarray(np.asarray(inp["cache_mem_v"], np.float32)[0, sl].reshape(16, 256, 256))
        m.update(make_consts(hf))
        sk = wd["attn_sinks"]
        sc = np.zeros((32, 2), np.float32)
        for h in range(2):
            sc[:, h] = sk[4 * h + np.arange(32) // 8]
        m["c_sinkcol"] = sc
        maps.append(m)
    return maps

def gather(res):
    f = np.float32
    yp = np.zeros((4, 4096, 1024), f); ys = np.zeros((128, 8, 1024), f)
    skp = np.zeros((1, 4, 128, 2, 64), f); svp = np.zeros_like(skp)
    Cp = np.zeros((1, 4, 4, 128, 64), f); npp = np.zeros((1, 4, 4, 64), f); mp = np.zeros((1, 4, 4), f)
    mkp = np.zeros((1, 4, 256, 4, 64), f); mvp = np.zeros_like(mkp)
    sks = np.zeros((1, 128, 128, 2, 64), f); svs = np.zeros_like(sks)
    Cs = np.zeros((1, 128, 4, 128, 64), f); ns = np.zeros((1, 128, 4, 64), f); ms = np.zeros((1, 128, 4), f)
    for c in range(8):
        r = res[c]; b, hf = c // 2, c % 2
        yp[b, hf * 2048:(hf + 1) * 2048] = r["yp"]
        ys[16 * c:16 * c + 16] = r["ys"].reshape(16, 8, 1024)
        if hf == 1:
            skp[0, b] = r["swak"].reshape(128, 2, 64); svp[0, b] = r["swav"].reshape(128, 2, 64)
            Cp[0, b] = r["Cp"]; npp[0, b] = r["np"]; mp[0, b] = r["mp"].reshape(4)
        else:
            mkp[0, b] = r["memk"].reshape(256, 4, 64); mvp[0, b] = r["memv"].reshape(256, 4, 64)
        sl = slice(16 * c, 16 * c + 16)
        sks[0, sl] = r["sks"].reshape(16, 128, 2, 64); svs[0, sl] = r["svs"].reshape(16, 128, 2, 64)
        Cs[0, sl] = r["Cs"]; ns[0, sl] = r["ns"]; ms[0, sl] = r["ms"]
    return (yp, ys, skp, svp, Cp, npp, mp, mkp, mvp, sks, svs, Cs, ns, ms)


_CACHE = {}


def kernel(**inputs):
    if "nc" not in _CACHE:
        _CACHE["nc"] = build_program(3)[0]
    nc = _CACHE["nc"]
    maps = shard_inputs(inputs)
    res = run_bass_kernel_spmd(nc, maps, core_ids=list(range(8)))
    return gather(res.results)
```

```python
import contextlib
from concourse.bass_utils import run_bass_kernel_spmd
import numpy as np
import concourse.bass as bass
import concourse.mybir as mybir

F32 = mybir.dt.float32
BF16 = mybir.dt.bfloat16
I32 = mybir.dt.int32
AF = mybir.ActivationFunctionType
ALU = mybir.AluOpType
AX = mybir.AxisListType

ENGS = ("pe", "act", "dve", "pool", "sp")


class Res:
    __slots__ = ("name", "last_w", "readers", "sem", "dcount", "excl")

    def __init__(self, name):
        self.name = name
        self.last_w = None
        self.readers = []
        self.sem = None
        self.dcount = 0
        self.excl = False


class Op:
    __slots__ = ("eng", "fn", "deps", "dma_res", "sig", "cnt", "k", "group")

    def __init__(self, eng, fn, dma_res):
        self.eng = eng
        self.fn = fn
        self.deps = set()
        self.dma_res = dma_res
        self.sig = False
        self.cnt = 0
        self.k = 0


class Prog:
    def __init__(self, nc):
        self.nc = nc
        self.ops = []
        self.nres = 0
        self.inherit = []
        self.phase_res = []

    def res(self, name=None, arena=False):
        self.nres += 1
        r = Res(name or f"r{self.nres}")
        if arena:
            r.readers = list(self.inherit)
            self.phase_res.append(r)
        return r

    def new_phase(self):
        inh = set(self.inherit)
        for r in self.phase_res:
            if r.last_w is not None:
                inh.add(r.last_w)
            inh.update(r.readers)
        self.inherit = sorted(inh)
        self.phase_res = []

    def rec_begin(self):
        self._rec = []

    def rec_end(self):
        r = self._rec
        self._rec = None
        return r

    def merge(self, streams):
        streams = [s_ for s_ in streams if s_]
        pos = [0] * len(streams)
        while True:
            best = None
            for k, s_ in enumerate(streams):
                if pos[k] < len(s_):
                    f = pos[k] / len(s_)
                    if best is None or f < best[0]:
                        best = (f, k)
            if best is None:
                break
            k = best[1]
            a, kw = streams[k][pos[k]]
            pos[k] += 1
            self.op(*a, **kw)

    def op(self, eng, fn, reads=(), writes=(), dma_res=None, accum=False, group=False):
        if getattr(self, "_rec", None) is not None:
            self._rec.append(((eng, fn, tuple(reads), tuple(writes)), dict(dma_res=dma_res, accum=accum, group=group)))
            return None
        i = len(self.ops)
        o = Op(eng, fn, dma_res)
        for r in reads:
            if r.last_w is not None:
                o.deps.add(r.last_w)
            if r.excl:
                for q in r.readers:
                    if self.ops[q].eng != eng:
                        o.deps.add(q)
            r.readers.append(i)
        for r in writes:
            if r.last_w is not None:
                lw = self.ops[r.last_w]
                if group and lw.dma_res is not None and lw.dma_res is dma_res:
                    o.deps |= lw.deps
                elif not (accum and lw.eng == "pe" and eng == "pe"):
                    o.deps.add(r.last_w)
            latest = {}
            for q in r.readers:
                if q == i:
                    continue
                oq = self.ops[q]
                if oq.dma_res is not None:
                    o.deps.add(q)
                elif latest.get(oq.eng, -1) < q:
                    latest[oq.eng] = q
            o.deps.update(latest.values())
            r.last_w = i
            r.readers = []
        if eng == "pe":
            o.deps = {d for d in o.deps if self.ops[d].eng != "pe" or self.ops[d].dma_res is not None}
        self.ops.append(o)
        return i

    def dma(self, eng, out, in_, res, reads=(), writes=(), group=False, **kw):
        kw = dict(kw); kw["out"] = out; kw["in_"] = in_
        return self.op(eng, ("dma_start", kw), reads=reads, writes=writes, dma_res=res, group=group)

    def I(self, eng, name, reads=(), writes=(), **kw):
        return self.op(eng, (name, kw), reads=reads, writes=writes)

    def emit(self, final_wait_all=True):
        nc = self.nc
        ops = self.ops
        for o in ops:
            for d in o.deps:
                ops[d].sig = True
        per_eng = {e: [] for e in ENGS}
        for i, o in enumerate(ops):
            per_eng[o.eng].append(i)
        import contextlib
        with contextlib.ExitStack() as st:
            esem = {e: st.enter_context(nc.semaphore(f"s_{e}")) for e in ENGS}
            ecount = {e: 0 for e in ENGS}
            dma_sems = []
            for i, o in enumerate(ops):
                if o.dma_res is not None:
                    r = o.dma_res
                    if r.sem is None:
                        r.sem = st.enter_context(nc.semaphore(f"d{len(dma_sems)}_{r.name}"))
                        dma_sems.append(r)
                    r.dcount += 1
                    o.cnt = 16 * r.dcount
                elif o.sig:
                    ecount[o.eng] += 1
                    o.cnt = ecount[o.eng]
            self.n_dma_sems = len(dma_sems)
            know = {e: {} for e in ENGS}
            know_issue = [None] * len(ops)

            def key_of(o):
                return ("d", id(o.dma_res)) if o.dma_res is not None else ("e", o.eng)

            block = st.enter_context(nc.Block())
            handles = {}

            plan = [None] * len(ops)
            for i, o in enumerate(ops):
                kn = know[o.eng]
                need = {}
                for d in o.deps:
                    p = ops[d]
                    k = key_of(p)
                    if kn.get(k, 0) >= p.cnt:
                        continue
                    if need.get(k, (0, None))[0] < p.cnt:
                        need[k] = (p.cnt, d)
                waits = []
                for k, (cnt, d) in need.items():
                    p = ops[d]
                    sem = p.dma_res.sem if p.dma_res is not None else esem[p.eng]
                    waits.append((sem, cnt))
                    kn[k] = max(kn.get(k, 0), cnt)
                    ki = know_issue[d]
                    for kk, vv in ki.items():
                        if kn.get(kk, 0) < vv:
                            kn[kk] = vv
                know_issue[i] = dict(kn)
                plan[i] = waits
            self.n_waits = sum(len(w) for w in plan)

            def make(ename):
                def body(eh):
                    for i in per_eng[ename]:
                        o = ops[i]
                        for sem, cnt in plan[i]:
                            eh.wait_ge(sem, cnt)
                        ins = getattr(eh, o.fn[0])(**o.fn[1])
                        if o.dma_res is not None:
                            ins.then_inc(o.dma_res.sem, 16)
                        elif o.sig:
                            ins.then_inc(esem[o.eng], 1)
                    if ename == "sp" and final_wait_all:
                        for r in dma_sems:
                            eh.wait_ge(r.sem, 16 * r.dcount)
                        for e in ("pe", "act", "dve", "pool"):
                            if ecount[e]:
                                eh.wait_ge(esem[e], ecount[e])
                return body

            block.tensor(make("pe"))
            block.scalar(make("act"))
            block.vector(make("dve"))
            block.gpsimd(make("pool"))
            block.sync(make("sp"))


D = 1024
FH = 2816
EPS = 1e-6
NTP = 16
NT = 17
GT = 2
NEG = -30000.0
DBG_G0 = 2


def build_program(stage=3, debug=False):
    nc = bass.Bass("TRN2", target_bir_lowering=False)
    P = Prog(nc)

    def din(name, shape, dt=F32):
        return nc.dram_tensor(name, list(shape), dt, kind="ExternalInput").ap()

    def dout(name, shape):
        return nc.dram_tensor(name, list(shape), F32, kind="ExternalOutput").ap()

    xp_d = din("xp", [2048, D]); xpre_d = din("xpre", [2048, D]); xs_d = din("xs", [128, D])
    mem_d = din("mem", [256, D])
    csk_d = din("csk", [16, 128, 128]); csv_d = din("csv", [16, 128, 128])
    sC_d = din("sC", [16, 4, 128, 64]); sn_d = din("sn", [16, 4, 64]); sm_d = din("sm", [16, 4])
    cmk_d = din("cmk", [16, 256, 256]); cmv_d = din("cmv", [16, 256, 256])
    w_in_d = din("w_in", [D, 2312]); b_i_d = din("b_igate", [4]); b_f_d = din("b_fgate", [4])
    sinks_d = din("attn_sinks", [8]); ghead_d = din("g_mlstm_head", [512]); w_out_d = din("w_out", [D, D])
    g_mix_d = din("g_mix", [D]); g_cross_d = din("g_cross", [D]); g_mem_d = din("g_mem", [D])
    w_cq_d = din("w_cq", [D, 256]); w_ck_d = din("w_ck", [D, 256]); w_cv_d = din("w_cv", [D, 256])
    w_co_d = din("w_co", [256, D]); g_ffn_d = din("g_ffn", [D])
    w_gate_d = din("w_gate", [D, FH]); w_up_d = din("w_up", [D, FH]); w_down_d = din("w_down", [FH, D])
    g_final_d = din("g_final", [D])
    ident_d = din("c_ident", [128, 128]); mb_band_d = din("c_mb_band", [128, 256]); mb_first_d = din("c_mb_first", [128, 256])
    mb_caus_d = din("c_mb_caus", [128, 128]); mb_causs_d = din("c_mb_causs", [128, 128])
    sel_d = din("c_sel", [4, 1024]); pmask_d = din("c_pmask", [4, 2])
    smc_d = din("c_smc", [32, 128]); smn_d = din("c_smn", [32, 16, 128]); sinkcol_d = din("c_sinkcol", [32, 2])
    bt_d = din("c_bt", [128, 128]); eseq_d = din("c_eseq", [128, 16])

    yp_o = dout("yp", [2048, D]); ys_o = dout("ys", [128, D])
    swak_o = dout("swak", [128, 128]); swav_o = dout("swav", [128, 128])
    Cp_o = dout("Cp", [4, 128, 64]); np_o = dout("np", [4, 64]); mp_o = dout("mp", [4, 1])
    memk_o = dout("memk", [256, 256]); memv_o = dout("memv", [256, 256])
    sks_o = dout("sks", [16, 128, 128]); svs_o = dout("svs", [16, 128, 128])
    Cs_o = dout("Cs", [16, 4, 128, 64]); ns_o = dout("ns", [16, 4, 64]); ms_o = dout("ms", [16, 4])

    st = contextlib.ExitStack()
    with st:
        def sb(name, shape, dt):
            return st.enter_context(nc.sbuf_tensor(name, list(shape), dt))

        def ps(name, shape, dt):
            return st.enter_context(nc.psum_tensor(name, list(shape), dt))

        banks = [ps(f"bk{i}", [128, 512], F32) for i in range(7)]
        bres = [P.res(f"bk{i}") for i in range(7)]
        for r_ in bres:
            r_.excl = True
        tb = ps("tb", [128, 1024], BF16)
        tbr = P.res("tb")
        tbr.excl = True
        bki = [0]

        bset = [[0, 1, 2, 3, 4]]
        bcnt = {}

        def bank():
            key = tuple(bset[0])
            c = bcnt.get(key, 0)
            bcnt[key] = c + 1
            i = bset[0][c % len(key)]
            return banks[i], bres[i]

        Y = sb("Y", [128, NT, D], F32)
        Yr = [P.res(f"Y{t}") for t in range(NT)]
        identb = sb("identb", [128, 128], BF16); identr = P.res("identb")
        identf = sb("identf", [128, 128], F32); identfr = P.res("identf")
        onesb = sb("onesb", [128, 128], BF16); onesbr = P.res("onesb")
        onesf = sb("onesf", [128, 256], F32); onesfr = P.res("onesf")
        SEL = sb("SEL", [4, 1024], F32); selr = P.res("SEL")
        gcols = sb("gcols", [128, 4, 8], F32); gcolsr = P.res("gcols")
        gheadc = sb("gheadc", [128, 4], F32); gheadr = P.res("ghead")
        sinkb = sb("sinkb", [128, 16], F32); sinkbr = P.res("sinkb")
        gb4 = sb("gb4", [4, 4], F32); gb4r = P.res("gb4")
        mbband = sb("mbband", [128, 256], BF16); mbbandr = P.res("mbband")
        mbfirst = sb("mbfirst", [128, 256], BF16); mbfirstr = P.res("mbfirst")
        mbcaus = sb("mbcaus", [128, 128], BF16); mbcausr = P.res("mbcaus")
        stat = sb("stat", [128, 8, 4], F32)
        statr = [P.res(f"stat{i}") for i in range(8)]
        stati = [0]
        USE_SQRT = [False]
        xsb = sb("xsb", [128, 2, D], BF16); xsbr = [P.res("xsb0"), P.res("xsb1")]
        Cst = sb("Cst", [64, 4, 129], F32); Cstr = [P.res(f"Cst{h}") for h in range(4)]
        ARN = 64400
        arena = sb("arena", [128, ARN], BF16)
        aoff = [0]

        def A(shape, dt, parts=128, name=None):
            n = int(np.prod(shape))
            nb = n * (4 if dt == F32 else 2)
            n16 = (nb + 1) // 2
            n16 = (n16 + 15) // 16 * 16
            assert aoff[0] + n16 <= ARN, f"arena overflow {aoff[0]}+{n16} ({name})"
            v = arena[0:parts, aoff[0]:aoff[0] + n16]
            aoff[0] += n16
            if dt == F32:
                v = v.bitcast(F32)
            v = v[:, 0:n]
            if len(shape) == 2:
                v = v.rearrange("p (a b) -> p a b", a=shape[0])
            elif len(shape) == 3:
                v = v.rearrange("p (a b c) -> p a b c", a=shape[0], b=shape[1])
            return v

        def new_phase():
            P.new_phase()
            aoff[0] = 0

        def AR(name):
            return P.res(name, arena=True)

        def A_at(off, shape, dt, parts=128):
            n = int(np.prod(shape))
            nb = n * (4 if dt == F32 else 2)
            n16 = ((nb + 1) // 2 + 15) // 16 * 16
            v = arena[0:parts, off:off + n16]
            if dt == F32:
                v = v.bitcast(F32)
            v = v[:, 0:n]
            if len(shape) == 2:
                v = v.rearrange("p (a b) -> p a b", a=shape[0])
            elif len(shape) == 3:
                v = v.rearrange("p (a b c) -> p a b c", a=shape[0], b=shape[1])
            return v, off + n16

        def ARalias(name, olds):
            r = P.res(name, arena=True)
            dd = set(r.readers)
            for o_ in olds:
                if o_.last_w is not None:
                    dd.add(o_.last_w)
                dd.update(o_.readers)
            r.readers = sorted(dd)
            return r

        I = P.I

        def mm(out, lhsT, rhs, start, stop, reads, wres):
            I("pe", "matmul", reads, [wres], out=out, lhsT=lhsT, rhs=rhs, start=start, stop=stop)

        P.dma("pool", identb[:], ident_d, identr, writes=[identr])
        P.dma("sp", identf[:], ident_d, identfr, writes=[identfr])
        I("dve", "memset", [], [onesbr], ap=onesb[:], constant=1.0)
        I("dve", "memset", [], [onesfr], ap=onesf[:], constant=1.0)
        P.dma("sp", SEL[:], sel_d, selr, writes=[selr])
        for i, g in enumerate((g_mix_d, g_cross_d, g_mem_d, g_ffn_d)):
            P.dma("sp", gcols[:, i, :], g.rearrange("(k p) -> p k", p=128), gcolsr, writes=[gcolsr], group=True, allow_slow_non_contiguous=True)
        P.dma("sp", gheadc[:], ghead_d.rearrange("(h p) -> p h", p=128), gheadr, writes=[gheadr], allow_slow_non_contiguous=True)
        P.dma("sp", gb4[:, 0:1], b_i_d.rearrange("(h o) -> h o", o=1), gb4r, writes=[gb4r], group=True, allow_slow_non_contiguous=True)
        P.dma("sp", gb4[:, 1:2], b_f_d.rearrange("(h o) -> h o", o=1), gb4r, writes=[gb4r], group=True, allow_slow_non_contiguous=True)
        P.dma("sp", gb4[:, 2:4], pmask_d, gb4r, writes=[gb4r], group=True, allow_slow_non_contiguous=True)
        P.dma("sp", sinkb[:, 0:8], sinks_d.partition_broadcast(128), sinkbr, writes=[sinkbr])
        I("dve", "tensor_scalar", [sinkbr], [sinkbr], out=sinkb[:, 8:16], in0=sinkb[:, 0:8], scalar1=-1.0, scalar2=None,
          op0=ALU.mult)
        I("dve", "tensor_scalar", [gb4r], [gb4r], out=gb4[:, 1:2], in0=gb4[:, 1:2], scalar1=-1.0, scalar2=None, op0=ALU.mult)
        P.dma("pool", mbband[:], mb_band_d, mbbandr, writes=[mbbandr])
        P.dma("pool", mbfirst[:], mb_first_d, mbfirstr, writes=[mbfirstr])
        P.dma("pool", mbcaus[:], mb_caus_d, mbcausr, writes=[mbcausr])
        for h in range(4):
            I("dve", "memset", [], [Cstr[h]], ap=Cst[:, h, :], constant=0.0)

        def SELh(h, n=128):
            return SEL[:, h * 128:h * 128 + n]

        def NSELh(h, n=128):
            return SEL[:, 512 + h * 128:512 + h * 128 + n]

        def norm_stats(src, sres, jb=0):
            i = stati[0] % 8
            stati[0] += 1
            sr = statr[i]
            I("act", "activation", [sres], [xsbr[jb], sr], out=xsb[:, jb, :], in_=src, func=AF.Square, accum_out=stat[:, i, 0:1])
            I("dve", "tensor_scalar", [sr], [sr], out=stat[:, i, 1:2], in0=stat[:, i, 0:1], scalar1=1.0 / D, scalar2=EPS,
              op0=ALU.mult, op1=ALU.add)
            if USE_SQRT[0]:
                I("act", "activation", [sr], [sr], out=stat[:, i, 2:3], in_=stat[:, i, 1:2], func=AF.Sqrt)
                I("dve", "reciprocal", [sr], [sr], out=stat[:, i, 3:4], in_=stat[:, i, 2:3])
            else:
                I("act", "activation", [sr], [sr], out=stat[:, i, 2:3], in_=stat[:, i, 1:2], func=AF.Ln)
                I("act", "activation", [sr], [sr], out=stat[:, i, 3:4], in_=stat[:, i, 2:3], func=AF.Exp, scale=-0.5)
            return stat[:, i, 3:4], sr

        xsi = [0]

        def norm_T(src, sres, gi, dst, dres):
            b = xsi[0] % 2
            xsi[0] += 1
            rstd, sr = norm_stats(src, sres, b)
            I("dve", "tensor_scalar", [sres, sr], [xsbr[b]], out=xsb[:, b, :], in0=src, scalar1=rstd, scalar2=None, op0=ALU.mult)
            for k in range(8):
                I("pe", "transpose", [xsbr[b], identr], [tbr], out=tb[:, k * 128:(k + 1) * 128],
                  in_=xsb[:, b, k * 128:(k + 1) * 128], identity=identb[:])
            for k in range(8):
                I("act", "activation", [tbr, gcolsr], dres, out=dst[:, k, :], in_=tb[:, k * 128:(k + 1) * 128],
                  func=AF.Copy, scale=gcols[:, gi, k:k + 1])

        class NS:
            pass

        def alloc_mixer(gt, nkt, nvt):
            M = NS()
            M.WQ = A([8, 512], BF16); M.WQr = AR("WQ")
            M.WTOK = A([8, 1024], BF16); M.WTOKr = AR("WTOK")
            M.WK = M.WTOK[:, :, 0:128]; M.WKr = M.WTOKr
            M.WMQ = A([8, 256], BF16); M.WMQr = AR("WMQ")
            M.WMK = M.WTOK[:, :, 256:512]; M.WMKr = M.WTOKr
            M.WOG = A([8, 512], BF16); M.WOGr = AR("WOG")
            M.WGT = A([8, 8], BF16); M.WGTr = AR("WGT")
            M.WOA = A([4, 1024], BF16); M.WOAr = AR("WOA")
            M.WOM = A([4, 1024], BF16); M.WOMr = AR("WOM")

            def wload(dst, res, src, **kw):
                P.dma("pool", dst, src, res, writes=[res], **kw)

            def wcols(a_, b_):
                return w_in_d[:, a_:b_].rearrange("(k p) n -> p k n", p=128)
            wload(M.WTOK[:, :, 0:256], M.WTOKr, wcols(512, 768), group=True)
            wload(M.WTOK[:, :, 256:1024], M.WTOKr, wcols(1024, 1792), group=True)
            wload(M.WGT[:], M.WGTr, wcols(2304, 2312), allow_slow_non_contiguous=True)
            wload(M.WQ[:], M.WQr, wcols(0, 512))
            wload(M.WMQ[:], M.WMQr, wcols(768, 1024))
            wload(M.WOG[:], M.WOGr, wcols(1792, 2304))
            wload(M.WOA[:], M.WOAr, w_out_d[0:512, :].rearrange("(c p) n -> p c n", p=128))
            wload(M.WOM[:], M.WOMr, w_out_d[512:1024, :].rearrange("(h p) n -> p h n", p=128))
            M.KT = A([2, nkt * 128], BF16, parts=64); M.KTr = [AR(f"KT{i}") for i in range(nkt)]
            M.Vt = A([nvt, 128], BF16); M.Vtr = [AR(f"Vt{i}") for i in range(nvt)]
            M.XNTg = A([8, gt * 128], BF16); M.XNTgr = [AR(f"XNTg{i}") for i in range(gt)]
            M.QT = A([8, gt * 128], BF16, parts=64); M.QTr = AR("QT")
            M.MQT = A([4, gt * 128], BF16, parts=64); M.MQTr = AR("MQT")
            M.MKT = A([4, gt * 128], BF16, parts=64); M.MKTr = AR("MKT")
            M.SGT = A([4, gt * 128], BF16); M.SGTr = AR("SGT")
            M.MKtok = A([gt, 256], BF16); M.MKtokr = [AR(f"MKtok{i}") for i in range(gt)]
            M.MVaug = A([gt, 4, 129], BF16); M.MVaugr = [AR(f"MVaug{i}") for i in range(gt)]
            M.ATTT = A([4, gt * 128], BF16); M.ATTTr = [AR(f"ATTT{i}") for i in range(gt)]
            M.HMT = A([4, gt * 128], BF16); M.HMTr = [AR(f"HMT{i}") for i in range(gt)]
            M.NG = gt * 128
            NG_ = M.NG
            M.G_IG = A([1, NG_ + 1], F32, parts=4)[:, 0, :]; M.G_E = A([1, NG_], F32, parts=4)[:, 0, :]
            M.G_L1 = A([1, NG_], F32, parts=4)[:, 0, :]; M.G_B = A([1, NG_ + 1], F32, parts=4)[:, 0, :]
            M.G_A = A([1, NG_], F32, parts=4)[:, 0, :]; M.G_M = A([1, NG_ + 1], F32, parts=4)[:, 0, :]
            M.G_BM = A([1, NG_], F32, parts=4)[:, 0, :]; M.G_DM = A([1, NG_], F32, parts=4)[:, 0, :]
            M.Gr = AR("G_IG"); M.G_Br = AR("G_B"); M.G_Ar = AR("G_A"); M.G_Mr = AR("G_M"); M.G_BMr = AR("G_BM"); M.G_DMr = AR("G_DM")
            M.SKV = A([1, 256], F32)[:, 0, :]; M.SKVr = AR("SKV")
            M.Ebuf = A([4, 256], BF16); M.Er = AR("E")
            M.PTs = A([1, 1024], BF16); M.PTsr = [AR("PTs0")] * 2
            M.sm_st = A([1, 32], F32)[:, 0, :]; M.smr = AR("sm_st")
            M.WKC = A([1, 8], F32)[:, 0, :]; M.WKCr = AR("WKC")
            M.DG = A([1, 8], F32, parts=4)[:, 0, :]; M.DGr = AR("DG")
            M.VW = A([4, 129], BF16); M.VWr = [AR(f"VW{h}") for h in range(4)]
            M.Cb = A([4, 257], BF16, parts=64); M.Cbr = [AR(f"Cb{h}") for h in range(4)]
            M.WT = A([4, 128], BF16); M.WTr = AR("WT")
            M.ST = A([4, 128], BF16); M.STr = AR("ST")
            M.WI = A([4, 128], BF16); M.WIr = AR("WI")
            M.QW = A([4, 128], BF16, parts=64); M.QWr = AR("QW")
            M.LOWB = A([4, 128], F32); M.LOWBr = AR("LOWB")
            M.T1 = A([4, 128], F32); M.T1r = AR("T1")
            M.T2 = A([4, 128], F32); M.T2r = AR("T2")
            M.USQ = A([4, 128], BF16); M.USQr = AR("USQ")
            for i in range(gt):
                I("dve", "memset", [], [M.MVaugr[i]], ap=M.MVaug[:, i, :, 128:129], constant=1.0)
            return M

        M = alloc_mixer(GT, NTP + 1, NTP + 1)
        I("dve", "memset", [], [M.G_Br], ap=M.G_B[:, 0:1], constant=0.0)
        I("dve", "memset", [], [M.G_Mr], ap=M.G_M[:, 0:1], constant=0.0)

        def tok_major(ti, xcols, xres, vslot, want_kv_out=None):
            b0, b0r = bank()
            b1, b1r = bank()
            for k in range(8):
                mm(b0[:, :], M.XNTg[:, k, xcols], M.WTOK[:, k, 0:512], k == 0, k == 7, [xres, M.WTOKr], b0r)
            for k in range(8):
                mm(b1[:, :], M.XNTg[:, k, xcols], M.WTOK[:, k, 512:1024], k == 0, k == 7, [xres, M.WTOKr], b1r)
            I("act", "activation", [b0r], [M.Vtr[vslot]], out=M.Vt[:, vslot, :], in_=b0[:, 128:256], func=AF.Copy)
            I("act", "activation", [b0r], [M.MKtokr[ti]], out=M.MKtok[:, ti, :], in_=b0[:, 256:512], func=AF.Copy, scale=0.125)
            I("dve", "tensor_copy", [b1r], [M.MVaugr[ti]], out=M.MVaug[:, ti, :, 0:128],
              in_=b1[:, :].rearrange("p (h d) -> p h d", h=4))
            if want_kv_out is not None:
                I("dve", "tensor_copy", [b0r], [M.SKVr], out=M.SKV[:, :], in_=b0[:, 0:256])
                if want_kv_out == "sample":
                    P.dma("sp", sks_o[:, 120:128, :], M.SKV[:, 0:128], M.SKVr, reads=[M.SKVr], group=True)
                    P.dma("sp", svs_o[:, 120:128, :], M.SKV[:, 128:256], M.SKVr, reads=[M.SKVr], group=True)
                else:
                    P.dma("sp", swak_o, M.SKV[:, 0:128], M.SKVr, reads=[M.SKVr], group=True)
                    P.dma("sp", swav_o, M.SKV[:, 128:256], M.SKVr, reads=[M.SKVr], group=True)

        def feat64(W, Wr, nh, dst, dres, ntok, xres, scale=None, dcol0=0):
            for h0 in range(0, nh, 2):
                bk, bkr = bank()
                for hh in range(2):
                    h = h0 + hh
                    for k in range(8):
                        mm(bk[0:64, hh * 256:hh * 256 + ntok], W[:, k, h * 64:(h + 1) * 64], M.XNTg[:, k, 0:ntok],
                           k == 0, k == 7, [Wr] + xres, bkr)
                src = bk[0:64, :].rearrange("p (a b) -> p a b", a=2)[:, :, 0:ntok]
                kw = {} if scale is None else {"scale": scale}
                I("act", "activation", [bkr], dres, out=dst[:, h0:h0 + 2, dcol0:dcol0 + ntok], in_=src, func=AF.Copy, **kw)

        def gates(ntok, xres, prefix):
            pg, pgr = bank()
            for k in range(8):
                mm(pg[0:4, 0:ntok], M.WGT[:, k, 0:4], M.XNTg[:, k, 0:ntok], k == 0, k == 7, [M.WGTr] + xres, pgr)
            for k in range(8):
                mm(pg[0:4, 256:256 + ntok], M.WGT[:, k, 4:8], M.XNTg[:, k, 0:ntok], k == 0, k == 7, [M.WGTr] + xres, pgr)
            I("act", "activation", [pgr, gb4r], [M.Gr], out=M.G_IG[:, 1:ntok + 1], in_=pg[0:4, 0:ntok], func=AF.Identity,
              bias=gb4[:, 0:1])
            I("act", "activation", [pgr, gb4r], [M.Gr], out=M.G_E[:, 0:ntok], in_=pg[0:4, 256:256 + ntok], func=AF.Exp,
              bias=gb4[:, 1:2], scale=-1.0)
            I("act", "activation", [M.Gr], [M.Gr], out=M.G_L1[:, 0:ntok], in_=M.G_E[:, 0:ntok], func=AF.Ln, bias=1.0)
            if prefix == "sample":
                return
            if prefix:
                I("dve", "tensor_scalar", [M.Gr, gb4r], [M.Gr], out=M.G_L1[:, 0:ntok], in0=M.G_L1[:, 0:ntok], scalar1=gb4[:, 2:3],
                  scalar2=None, op0=ALU.mult)
            I("dve", "tensor_tensor_scan", [M.Gr, M.G_Br, onesfr], [M.G_Br], out=M.G_B[:, 1:ntok + 1], data0=onesf[0:4, 0:ntok],
              data1=M.G_L1[:, 0:ntok], initial=M.G_B[:, 0:1], op0=ALU.mult, op1=ALU.subtract)
            I("dve", "scalar_tensor_tensor", [M.Gr, M.G_Br, gb4r], [M.G_Ar], out=M.G_A[:, 0:ntok], in0=M.G_IG[:, 1:ntok + 1],
              scalar=(gb4[:, 3:4] if prefix else 0.0), in1=M.G_B[:, 1:ntok + 1], op0=ALU.add, op1=ALU.subtract)
            I("dve", "tensor_tensor_scan", [M.G_Ar, M.G_Mr, onesfr], [M.G_Mr], out=M.G_M[:, 1:ntok + 1], data0=onesf[0:4, 0:ntok],
              data1=M.G_A[:, 0:ntok], initial=M.G_M[:, 0:1], op0=ALU.mult, op1=ALU.max)
            I("dve", "tensor_tensor", [M.G_Br, M.G_Mr], [M.G_BMr], out=M.G_BM[:, 0:ntok], in0=M.G_B[:, 1:ntok + 1],
              in1=M.G_M[:, 1:ntok + 1], op=ALU.add)
            for ci in range(ntok // 128):
                I("dve", "tensor_scalar", [M.G_Mr], [M.G_DMr], out=M.G_DM[:, ci * 128:(ci + 1) * 128],
                  in0=M.G_M[:, 1 + ci * 128:1 + (ci + 1) * 128], scalar1=M.G_M[:, ci * 128:ci * 128 + 1], scalar2=None,
                  op0=ALU.subtract)

        def gates_carry(ntok):
            I("dve", "tensor_copy", [M.G_Br], [M.G_Br], out=M.G_B[:, 0:1], in_=M.G_B[:, ntok:ntok + 1])
            I("dve", "tensor_copy", [M.G_Mr], [M.G_Mr], out=M.G_M[:, 0:1], in_=M.G_M[:, ntok:ntok + 1])

        def state_update(ti, c0, refresh_cb):
            pw, pwr = bank()
            I4 = SEL[:, 0:512].rearrange("p (h t) -> p h t", t=128)[:, :, 0]
            I("dve", "tensor_scalar", [selr, M.G_Mr], [M.DGr], out=M.DG[:, 0:4], in0=I4, scalar1=M.G_M[:, c0 + 128:c0 + 129],
              scalar2=-1.0, op0=ALU.mult, op1=ALU.mult)
            I("dve", "tensor_scalar", [selr, M.G_DMr], [M.DGr], out=M.DG[:, 4:8], in0=I4, scalar1=M.G_DM[:, c0 + 127:c0 + 128],
              scalar2=-1.0, op0=ALU.mult, op1=ALU.mult)
            mm(pw[:, 0:4], M.G_A[:, c0:c0 + 128], I4, True, False, [M.G_Ar, selr], pwr)
            mm(pw[:, 0:4], onesf[0:4, 0:128], M.DG[:, 0:4], False, True, [onesfr, M.DGr], pwr)
            mm(pw[:, 4:8], onesf[0:4, 0:128], M.DG[:, 4:8], True, True, [onesfr, M.DGr], pwr)
            I("act", "activation", [pwr], [M.WKCr], out=M.WKC[:, 0:8], in_=pw[:, 0:8], func=AF.Exp)
            for h in range(4):
                I("dve", "tensor_scalar", [M.MVaugr[ti], M.WKCr], [M.VWr[h]], out=M.VW[:, h, :], in0=M.MVaug[:, ti, h, :],
                  scalar1=M.WKC[:, h:h + 1], scalar2=None, op0=ALU.mult)
            for h0 in (0, 2):
                dc, dcr = bank()
                for hh in range(2):
                    h = h0 + hh
                    mm(dc[0:64, hh * 129:(hh + 1) * 129], M.MKtok[:, ti, h * 64:(h + 1) * 64], M.VW[:, h, :], True, True,
                       [M.MKtokr[ti], M.VWr[h]], dcr)
                for hh in range(2):
                    h = h0 + hh
                    I("dve", "scalar_tensor_tensor", [Cstr[h], M.WKCr, dcr], [Cstr[h]], out=Cst[:, h, :], in0=Cst[:, h, :],
                      scalar=M.WKC[0:64, 4 + h:5 + h], in1=dc[0:64, hh * 129:(hh + 1) * 129], op0=ALU.mult, op1=ALU.add)
            if refresh_cb:
                for h in range(4):
                    I("act", "activation", [Cstr[h]], [M.Cbr[h]], out=M.Cb[:, h, 0:129], in_=Cst[:, h, :], func=AF.Copy)
                    I("act", "activation", [Cstr[h]], [M.Cbr[h]], out=M.Cb[:, h, 129:257],
                      in_=Cst[:, h, 128:129].broadcast_to([64, 128]), func=AF.Copy)

        def mlstm_chunk(ti, c0, mbias, mbiasr, inter=True, inter_fn=None):
            cs = slice(c0, c0 + 128)
            pwt, pwtr = bank()
            for h in range(4):
                o = pwt[:, h * 128:(h + 1) * 128]
                mm(o, M.G_A[:, cs], SELh(h), True, False, [M.G_Ar, selr], pwtr)
                mm(o, NSELh(h), M.G_M[:, c0 + 1:c0 + 129], False, False, [M.G_Mr, selr], pwtr)
                mm(o, identb[:], mbias, False, True, [identr, mbiasr], pwtr)
            I("act", "activation", [pwtr], [M.WTr], out=M.WT[:, :, :], in_=pwt[:, :].rearrange("p (h t) -> p h t", h=4), func=AF.Exp)
            pqk, pqkr = bank()
            for h in range(4):
                mm(pqk[:, h * 128:(h + 1) * 128], M.MKT[:, h, cs], M.MQT[:, h, cs], True, True, [M.MKTr, M.MQTr], pqkr)
            I("dve", "tensor_tensor", [pqkr, M.WTr], [M.STr], out=M.ST[:, :, :], in0=pqk[:, :].rearrange("p (h t) -> p h t", h=4),
              in1=M.WT[:, :, :], op=ALU.mult)
            pwi, pwir = bank()
            for h in range(4):
                mm(pwi[:, h * 128:(h + 1) * 128], NSELh(h), M.G_DM[:, cs], True, True, [M.G_DMr, selr], pwir)
            I("act", "activation", [pwir], [M.WIr], out=M.WI[:, :, :], in_=pwi[:, :].rearrange("p (h t) -> p h t", h=4), func=AF.Exp)
            I("dve", "tensor_tensor", [M.MQTr, M.WIr], [M.QWr], out=M.QW[:, :, :], in0=M.MQT[:, :, cs], in1=M.WI[0:64, :, :], op=ALU.mult)
            plb, plbr = bank()
            for h in range(4):
                mm(plb[:, h * 128:(h + 1) * 128], NSELh(h), M.G_BM[:, cs], True, True, [M.G_BMr, selr], plbr)
            I("act", "activation", [plbr], [M.LOWBr], out=M.LOWB[:, :, :], in_=plb[:, :].rearrange("p (h t) -> p h t", h=4), func=AF.Exp)
            pnum, pnumr = banks[5], bres[5]
            pden, pdenr = banks[6], bres[6]
            if inter_fn is not None:
                inter_fn("pre")
            for h in range(4):
                o = pnum[:, h * 128:(h + 1) * 128]
                mm(o, M.MVaug[:, ti, h, 0:128], M.ST[:, h, :], True, False, [M.MVaugr[ti], M.STr], pnumr)
                if inter_fn is not None:
                    inter_fn("num", h, pnum, pnumr)
                else:
                    mm(o, M.Cb[:, h, 0:128], M.QW[:, h, :], False, True, [M.Cbr[h], M.QWr], pnumr)
            for h in range(4):
                o = pden[:, h * 128:(h + 1) * 128]
                mm(o, onesb[:], M.ST[:, h, :], True, False, [onesbr, M.STr], pdenr)
                if inter_fn is not None:
                    inter_fn("den", h, pden, pdenr)
                else:
                    mm(o, M.Cb[:, h, 129:257], M.QW[:, h, :], False, True, [M.Cbr[h], M.QWr], pdenr)
            return pnum, pnumr, pden, pdenr

        def mlstm_finish(pnum, pnumr, pden, pdenr, c0, hres):
            cs = slice(c0, c0 + 128)
            v4 = lambda b: b[:, :].rearrange("p (h t) -> p h t", h=4)
            I("act", "activation", [pdenr], [M.T1r], out=M.T1[:, :, :], in_=v4(pden), func=AF.Abs)
            I("dve", "tensor_tensor", [M.T1r, M.LOWBr], [M.T1r], out=M.T1[:, :, :], in0=M.T1[:, :, :], in1=M.LOWB[:, :, :], op=ALU.max)
            I("act", "activation", [M.T1r], [M.T1r], out=M.T1[:, :, :], in_=M.T1[:, :, :], func=AF.Square, scale=float(np.sqrt(EPS)))
            I("act", "activation", [pnumr], [M.USQr], out=M.USQ[:, :, :], in_=v4(pnum), func=AF.Square)
            pss, pssr = bank()
            mm(pss[:, :], onesb[:], M.USQ[:, :, :], True, True, [onesbr, M.USQr], pssr)
            I("dve", "scalar_tensor_tensor", [pssr, M.T1r], [M.T2r], out=M.T2[:, :, :], in0=v4(pss), scalar=1.0 / 128, in1=M.T1[:, :, :],
              op0=ALU.mult, op1=ALU.add)
            I("act", "activation", [M.T2r], [M.T2r], out=M.T2[:, :, :], in_=M.T2[:, :, :], func=AF.Ln)
            I("act", "activation", [M.T2r], [M.T2r], out=M.T2[:, :, :], in_=M.T2[:, :, :], func=AF.Exp, scale=-0.5)
            I("dve", "tensor_tensor", [pnumr, M.T2r], [M.T1r], out=M.T1[:, :, :], in0=v4(pnum), in1=M.T2[:, :, :], op=ALU.mult)
            for h in range(4):
                I("dve", "scalar_tensor_tensor", [M.T1r, gheadr, M.SGTr], [hres], out=M.HMT[:, h, cs], in0=M.T1[:, h, :],
                  scalar=gheadc[:, h:h + 1], in1=M.SGT[:, h, cs], op0=ALU.mult, op1=ALU.mult)

        def swa_tile(ti, kcol0, vslots, mb, mbr, ktres):
            qs = slice(ti * 128, (ti + 1) * 128)
            for h in range(2):
                bks = [bank(), bank()]
                for g in range(4):
                    bk, bkr = bks[g // 2]
                    o = bk[:, (g % 2) * 256:(g % 2 + 1) * 256]
                    mm(o, M.QT[:, 4 * h + g, qs], M.KT[:, h, kcol0:kcol0 + 256], True, False, [M.QTr] + ktres, bkr)
                    mm(o, identb[:], mb, False, True, [identr, mbr], bkr)
                for j in range(2):
                    I("dve", "reduce_max", [bks[j][1]], [M.smr], out=M.sm_st[:, 2 * j:2 * j + 2],
                      in_=bks[j][0][:, :].rearrange("p (a b) -> p a b", a=2), axis=AX.X)
                I("dve", "tensor_scalar", [M.smr], [M.smr], out=M.sm_st[:, 0:4], in0=M.sm_st[:, 0:4], scalar1=-0.125, scalar2=None,
                  op0=ALU.mult)
                I("dve", "tensor_tensor", [M.smr, sinkbr], [M.smr], out=M.sm_st[:, 0:4], in0=M.sm_st[:, 0:4],
                  in1=sinkb[:, 8 + 4 * h:12 + 4 * h], op=ALU.min)
                for g in range(4):
                    bk, bkr = bks[g // 2]
                    I("act", "activation", [bkr, M.smr], [M.Er, M.smr], out=M.Ebuf[:, g, :], in_=bk[:, (g % 2) * 256:(g % 2 + 1) * 256],
                      func=AF.Exp, bias=M.sm_st[:, g:g + 1], scale=0.125, accum_out=M.sm_st[:, 4 + g:5 + g])
                I("dve", "tensor_tensor", [M.smr, sinkbr], [M.smr], out=M.sm_st[:, 8:12], in0=M.sm_st[:, 0:4],
                  in1=sinkb[:, 4 * h:4 * h + 4], op=ALU.add)
                I("act", "activation", [M.smr], [M.smr], out=M.sm_st[:, 8:12], in_=M.sm_st[:, 8:12], func=AF.Exp)
                I("dve", "tensor_tensor", [M.smr], [M.smr], out=M.sm_st[:, 8:12], in0=M.sm_st[:, 8:12], in1=M.sm_st[:, 4:8], op=ALU.add)
                I("dve", "reciprocal", [M.smr], [M.smr], out=M.sm_st[:, 12:16], in_=M.sm_st[:, 8:12])
                for g in range(4):
                    if g % 2 == 0:
                        I("act", "activation", [M.Er, M.smr], [M.Er], out=M.Ebuf[:, g, :], in_=M.Ebuf[:, g, :], func=AF.Copy,
                          scale=M.sm_st[:, 12 + g:13 + g])
                    else:
                        I("dve", "tensor_scalar", [M.Er, M.smr], [M.Er], out=M.Ebuf[:, g, :], in0=M.Ebuf[:, g, :],
                          scalar1=M.sm_st[:, 12 + g:13 + g], scalar2=None, op0=ALU.mult)
                for kb in range(2):
                    for g in range(4):
                        blk = kb * 4 + (g % 2) * 2 + g // 2
                        I("pe", "transpose", [M.Er, identr], [tbr], out=tb[:, blk * 128:(blk + 1) * 128],
                          in_=M.Ebuf[:, g, kb * 128:(kb + 1) * 128], identity=identb[:])
                pb = 0
                if h == 0:
                    I("dve", "tensor_copy", [tbr], [M.PTsr[pb]], out=M.PTs[:, pb, :], in_=tb[:, :])
                else:
                    I("act", "activation", [tbr], [M.PTsr[pb]], out=M.PTs[:, pb, :], in_=tb[:, :], func=AF.Copy)
                po, por = bank()
                for par in range(2):
                    for kb in range(2):
                        mm(po[par * 64:(par + 1) * 64, 0:256], M.Vt[:, vslots[kb], h * 64:(h + 1) * 64],
                           M.PTs[:, pb, kb * 512 + par * 256:kb * 512 + (par + 1) * 256], kb == 0, kb == 1,
                           [M.Vtr[vslots[kb]], M.PTsr[pb]], por)
                I("act", "activation", [por], [M.ATTTr[ti]], out=M.ATTT[:, 2 * h:2 * h + 2, qs],
                  in_=po[:, 0:256].rearrange("p (g q) -> p g q", g=2), func=AF.Copy)

        def wout_tile(ti, t):
            qs = slice(ti * 128, (ti + 1) * 128)
            for c in range(2):
                bk, bkr = bank()
                cc = slice(c * 512, (c + 1) * 512)
                for hg in range(4):
                    mm(bk[:, :], M.ATTT[:, hg, qs], M.WOA[:, hg, cc], hg == 0, False, [M.ATTTr[ti], M.WOAr], bkr)
                for h in range(4):
                    mm(bk[:, :], M.HMT[:, h, qs], M.WOM[:, h, cc], False, h == 3, [M.HMTr[ti], M.WOMr], bkr)
                I("dve", "tensor_tensor", [Yr[t], bkr], [Yr[t]], out=Y[:, t, cc], in0=Y[:, t, cc], in1=bk[:, :], op=ALU.add)

        xpre_t = xpre_d.rearrange("(t p) d -> t p d", p=128)
        xp_t = xp_d.rearrange("(t p) d -> t p d", p=128)
        for t in range(NTP):
            P.dma("sp", Y[:, t, :], xpre_t[t], Yr[t], writes=[Yr[t]])
        for g0 in range(0, NTP, GT):
            for ti in range(GT):
                t = g0 + ti
                norm_T(Y[:, t, :], Yr[t], 0, M.XNTg[:, :, ti * 128:(ti + 1) * 128], [M.XNTgr[ti]])
                P.dma("sp", Y[:, t, :], xp_t[t], Yr[t], writes=[Yr[t]])
            for ti in range(GT):
                tok_major(ti, slice(ti * 128, (ti + 1) * 128), M.XNTgr[ti], 0)
            gates(GT * 128, M.XNTgr, True)
            if g0 + GT == NTP:
                bk, bkr = bank()
                for h in range(2):
                    for k in range(8):
                        mm(bk[0:64, h * 128:(h + 1) * 128], M.WK[:, k, h * 64:(h + 1) * 64], M.XNTg[:, k, (GT - 1) * 128:GT * 128],
                           k == 0, k == 7, [M.WKr, M.XNTgr[GT - 1]], bkr)
                I("act", "activation", [bkr], [M.KTr[0]], out=M.KT[:, :, 0:128],
                  in_=bk[0:64, 0:256].rearrange("p (a b) -> p a b", a=2), func=AF.Copy)
            for ti in range(GT):
                last = (g0 + ti == NTP - 1)
                state_update(ti, ti * 128, last)
            gates_carry(GT * 128)

        for g0 in range(0, NTP, GT):
            for ti in range(GT):
                t = g0 + ti
                norm_T(Y[:, t, :], Yr[t], 0, M.XNTg[:, :, ti * 128:(ti + 1) * 128], [M.XNTgr[ti]])
            for ti in range(GT):
                t = g0 + ti
                tok_major(ti, slice(ti * 128, (ti + 1) * 128), M.XNTgr[ti], 1 + t, want_kv_out=(True if t == NTP - 1 else None))
            gates(M.NG, M.XNTgr, False)
            feat64(M.WK, M.WKr, 2, M.KT, [M.KTr[1 + g0 + i] for i in range(GT)], M.NG, M.XNTgr, dcol0=128 + g0 * 128)
            feat64(M.WQ, M.WQr, 8, M.QT, [M.QTr], M.NG, M.XNTgr)
            feat64(M.WMQ, M.WMQr, 4, M.MQT, [M.MQTr], M.NG, M.XNTgr)
            feat64(M.WMK, M.WMKr, 4, M.MKT, [M.MKTr], M.NG, M.XNTgr, scale=0.125)
            for h0 in (0, 2):
                bk, bkr = bank()
                for hh in range(2):
                    h = h0 + hh
                    for k in range(8):
                        mm(bk[:, hh * 256:hh * 256 + M.NG], M.WOG[:, k, h * 128:(h + 1) * 128], M.XNTg[:, k, 0:M.NG], k == 0, k == 7,
                           [M.WOGr] + M.XNTgr, bkr)
                sgv = M.SGT[:, h0:h0 + 2, :]
                I("act", "activation", [bkr], [M.SGTr], out=sgv, in_=bk[:, :].rearrange("p (a b) -> p a b", a=2)[:, :, 0:M.NG],
                  func=AF.Exp, scale=-1.0)
                I("act", "activation", [M.SGTr], [M.SGTr], out=sgv, in_=sgv, func=AF.Ln, bias=1.0)
                I("act", "activation", [M.SGTr], [M.SGTr], out=sgv, in_=sgv, func=AF.Exp, scale=-1.0)
            pend = None
            for ti in range(GT):
                t = g0 + ti
                P.rec_begin(); bset[0] = [2, 3]
                pn = mlstm_chunk(ti, ti * 128, mbcaus[:], mbcausr)
                mlstm_finish(*pn, ti * 128, M.HMTr[ti])
                state_update(ti, ti * 128, True)
                s_ml = P.rec_end()
                P.rec_begin(); bset[0] = [0, 1]
                swa_tile(ti, t * 128, (t, t + 1), (mbfirst[:] if t == 0 else mbband[:]), (mbfirstr if t == 0 else mbbandr),
                         [M.KTr[t], M.KTr[t + 1]])
                s_sw = P.rec_end()
                strs = [s_ml, s_sw]
                if pend is not None:
                    P.rec_begin(); bset[0] = [4]
                    wout_tile(*pend)
                    strs.append(P.rec_end())
                P.merge(strs)
                pend = (ti, t)
            bset[0] = [0, 1, 2, 3, 4]
            wout_tile(*pend)
            if debug and g0 == DBG_G0:
                dA = dout("dbg_att", [64, 8, M.NG]); dH = dout("dbg_hm", [128, 4, M.NG])
                P.dma("pool", dA, M.ATTT[:, :, :], M.ATTTr[0], reads=M.ATTTr)
                P.dma("pool", dH, M.HMT[:, :, :], M.HMTr[0], reads=M.HMTr)
            gates_carry(M.NG)

        CO = A([4, 64], F32); COr = AR("CO")
        for h in range(4):
            bk, bkr = bank()
            mm(bk[:, 0:64], Cst[:, h, 0:128], identf[0:64, 0:64], True, True, [Cstr[h], identfr], bkr)
            I("act", "activation", [bkr], [COr], out=CO[:, h, :], in_=bk[:, 0:64], func=AF.Copy)
        P.dma("sp", Cp_o.rearrange("h p k -> p h k"), CO[:, :, :], COr, reads=[COr])
        for h in range(4):
            P.dma("sp", np_o[h, :].rearrange("(k o) -> k o", o=1), Cst[:, h, 128:129], Cstr[h], reads=[Cstr[h]], allow_slow_non_contiguous=True)
        P.dma("sp", mp_o, M.G_BM[:, M.NG - 1:M.NG], M.G_BMr, reads=[M.G_BMr], allow_slow_non_contiguous=True)

        new_phase()
        MA = M
        M = alloc_mixer(1, 1, 1)
        R0_olds = [M.WQr, M.WTOKr, M.WMQr, M.WOGr, M.WGTr]
        TS = NTP
        P.dma("sp", Y[:, TS, :], xs_d, Yr[TS], writes=[Yr[TS]])
        shk = P.res("shk"); shv = P.res("shv")
        P.dma("sp", sks_o[:, 0:120, :], csk_d[:, 8:128, :], shk, writes=[shk])
        P.dma("sp", svs_o[:, 0:120, :], csv_d[:, 8:128, :], shv, writes=[shv])
        CKn = A([16, 128], BF16); CKnr = AR("CKn")
        CV = A([16, 128], BF16); CVr = AR("CV")
        CKT = A([16, 128], BF16, parts=64); CKTr = AR("CKT")
        SMC = A([1, 128], BF16, parts=32)[:, 0, :]; SMCr = AR("SMC")
        SMN = A([16, 128], BF16, parts=32); SMNr = AR("SMN")
        SINKC = A([1, 4], F32, parts=32)[:, 0, :]; SINKCr = AR("SINKC")
        mbcs = A([1, 128], BF16)[:, 0, :]; mbcsr = AR("mbcs")
        PNs = A([4, 256], BF16, parts=32); PNsr = [AR(f"PNs{i}") for i in range(4)]
        sms = A([4, 8], F32, parts=32); smsr = [AR(f"sms{i}") for i in range(4)]
        PTS = A([1, 1024], BF16)[:, 0, :]; PTSr = AR("PTS")
        M0 = A([1, 16], F32, parts=4)[:, 0, :]; M0r = AR("M0")
        MTe = A([1, 128], F32, parts=4)[:, 0, :]; MTer = AR("MTe")
        DMT = A([1, 16], F32, parts=4)[:, 0, :]; DMTr = AR("DMT")
        E16 = A([1, 16], F32)[:, 0, :]; E16r = AR("E16")
        EW = A([4, 16], BF16); EWr = AR("EW")
        WCB = A([4, 16], F32); WCBr = AR("WCB")
        SNn = A([1, 64], F32, parts=64)[:, 0, :]; SNnr = AR("SNn")
        SNT = A([1, 64], F32, parts=64)[:, 0, :]; SNTr = AR("SNT")
        NNT = A([1, 64], F32, parts=64)[:, 0, :]; NNTr = AR("NNT")
        NNo = A([1, 64], F32, parts=64)[:, 0, :]; NNor = AR("NNo")
        BTf = A([1, 128], F32)[:, 0, :]; BTfr = AR("BTf")
        P.dma("pool", CKn[:, :, :], csk_d.rearrange("j p c -> p j c"), CKnr, writes=[CKnr])
        P.dma("pool", CV[:, :, :], csv_d.rearrange("j p c -> p j c"), CVr, writes=[CVr])
        P.dma("pool", SMC, smc_d, SMCr, writes=[SMCr])
        P.dma("pool", SMN[:, :, :], smn_d, SMNr, writes=[SMNr])
        P.dma("sp", SINKC[:, 0:2], sinkcol_d, SINKCr, writes=[SINKCr])
        I("dve", "tensor_scalar", [SINKCr], [SINKCr], out=SINKC[:, 2:4], in0=SINKC[:, 0:2], scalar1=-1.0, scalar2=None, op0=ALU.mult)
        P.dma("pool", mbcs, mb_causs_d, mbcsr, writes=[mbcsr])
        P.dma("sp", M0, sm_d.rearrange("j h -> h j"), M0r, writes=[M0r], allow_slow_non_contiguous=True)
        P.dma("sp", E16, eseq_d, E16r, writes=[E16r])
        P.dma("sp", SNn, sn_d.rearrange("j h k -> (j h) k"), SNnr, writes=[SNnr])

        norm_T(Y[:, TS, :], Yr[TS], 0, M.XNTg[:, :, 0:128], [M.XNTgr[0]])
        tok_major(0, slice(0, 128), M.XNTgr[0], 0, want_kv_out="sample")
        gates(128, M.XNTgr, "sample")
        feat64(M.WK, M.WKr, 2, M.KT, [M.KTr[0]], 128, M.XNTgr, dcol0=0)
        feat64(M.WQ, M.WQr, 8, M.QT, [M.QTr], 128, M.XNTgr)
        feat64(M.WMQ, M.WMQr, 4, M.MQT, [M.MQTr], 128, M.XNTgr)
        feat64(M.WMK, M.WMKr, 4, M.MKT, [M.MKTr], 128, M.XNTgr, scale=0.125)
        for h0 in (0, 2):
            bk, bkr = bank()
            for hh in range(2):
                h = h0 + hh
                for k in range(8):
                    mm(bk[:, hh * 256:hh * 256 + 128], M.WOG[:, k, h * 128:(h + 1) * 128], M.XNTg[:, k, 0:128], k == 0, k == 7,
                       [M.WOGr] + M.XNTgr, bkr)
            sgv = M.SGT[:, h0:h0 + 2, :]
            I("act", "activation", [bkr], [M.SGTr], out=sgv, in_=bk[:, :].rearrange("p (a b) -> p a b", a=2)[:, :, 0:128],
              func=AF.Exp, scale=-1.0)
            I("act", "activation", [M.SGTr], [M.SGTr], out=sgv, in_=sgv, func=AF.Ln, bias=1.0)
            I("act", "activation", [M.SGTr], [M.SGTr], out=sgv, in_=sgv, func=AF.Exp, scale=-1.0)
        for j in range(16):
            I("dve", "tensor_tensor_scan", [M.Gr, M.G_Br, onesfr], [M.G_Br], out=M.G_B[:, 1 + 8 * j:9 + 8 * j],
              data0=onesf[0:4, 0:8], data1=M.G_L1[:, 8 * j:8 * j + 8], initial=0.0, op0=ALU.mult, op1=ALU.subtract)
        I("dve", "tensor_tensor", [M.Gr, M.G_Br], [M.G_Ar], out=M.G_A[:, 0:128], in0=M.G_IG[:, 1:129], in1=M.G_B[:, 1:129],
          op=ALU.subtract)
        for j in range(16):
            I("dve", "tensor_tensor_scan", [M.G_Ar, M.G_Mr, onesfr, M0r], [M.G_Mr], out=M.G_M[:, 1 + 8 * j:9 + 8 * j],
              data0=onesf[0:4, 0:8], data1=M.G_A[:, 8 * j:8 * j + 8], initial=M0[:, j:j + 1], op0=ALU.mult, op1=ALU.max)
        I("dve", "tensor_tensor", [M.G_Br, M.G_Mr], [M.G_BMr], out=M.G_BM[:, 0:128], in0=M.G_B[:, 1:129], in1=M.G_M[:, 1:129],
          op=ALU.add)
        GM3 = M.G_M[:, 1:129].rearrange("p (j i) -> p j i", i=8)
        I("dve", "tensor_tensor", [M.G_Mr, M0r], [M.G_DMr], out=M.G_DM[:, 0:128].rearrange("p (j i) -> p j i", i=8), in0=GM3,
          in1=M0[:, :].unsqueeze(2).broadcast_to([4, 16, 8]), op=ALU.subtract)
        I("dve", "tensor_copy", [M.G_Mr], [MTer], out=MTe[:, :].rearrange("p (j i) -> p j i", i=8),
          in_=GM3[:, :, 7:8].broadcast_to([4, 16, 8]))
        I("dve", "tensor_tensor", [M.G_Mr, M0r], [DMTr], out=DMT[:, :].unsqueeze(2), in0=GM3[:, :, 7:8], in1=M0[:, :].unsqueeze(2),
          op=ALU.subtract)
        P.dma("sp", ms_o.rearrange("j h -> h j"), M.G_BM[:, 0:128].rearrange("p (j i) -> p j i", i=8)[:, :, 7], M.G_BMr,
              reads=[M.G_BMr], allow_slow_non_contiguous=True)

        pair_i = [0]
        QS = A([2, 16, 32], BF16, parts=64); QSr = AR("QS")
        for h in range(2):
            for par in range(2):
                I("act", "activation", [M.QTr], [QSr],
                  out=QS[:, h, :, par * 16:(par + 1) * 16].rearrange("p j (gp i) -> p j gp i", i=8),
                  in_=M.QT[:, 4 * h:4 * h + 4, :].rearrange("p (gp two) t -> p two gp t", two=2)[:, par].rearrange(
                      "p gp (j i) -> p j gp i", i=8), func=AF.Copy)
        for h in range(2):
            for q4 in range(2):
                for jj in range(8):
                    j = q4 * 8 + jj
                    I("pe", "transpose", [CKnr, identr], [tbr], out=tb[0:64, jj * 128:(jj + 1) * 128],
                      in_=CKn[:, j, h * 64:(h + 1) * 64], identity=identb[:])
                I("act", "activation", [tbr], [CKTr], out=CKT[:, q4 * 8:(q4 + 1) * 8, :],
                  in_=tb[0:64, :].rearrange("p (a b) -> p a b", a=8), func=AF.Copy)
            for half in range(2):
                po, por = banks[4], bres[4]
                strs = []
                for sk in range(4):
                    P.rec_begin(); bset[0] = [sk]
                    for jj in (sk, sk + 4):
                        j = half * 8 + jj
                        b = sk
                        bk, bkr = bank()
                        lq = QS[:, h, j, :]
                        mm(bk[0:32, 0:128], lq, CKT[:, j, :], True, False, [QSr, CKTr], bkr)
                        mm(bk[0:32, 0:128], identb[0:32, 0:32], SMC, False, True, [identr, SMCr], bkr)
                        mm(bk[0:32, 128:256], lq, M.KT[:, h, 0:128], True, False, [QSr, M.KTr[0]], bkr)
                        mm(bk[0:32, 128:256], identb[0:32, 0:32], SMN[:, j, :], False, True, [identr, SMNr], bkr)
                        st_ = sms[:, b, :]
                        I("dve", "reduce_max", [bkr], [smsr[b]], out=st_[:, 0:1], in_=bk[0:32, 0:256], axis=AX.X)
                        I("dve", "tensor_scalar", [smsr[b]], [smsr[b]], out=st_[:, 0:1], in0=st_[:, 0:1], scalar1=-0.125,
                          scalar2=None, op0=ALU.mult)
                        I("dve", "tensor_tensor", [smsr[b], SINKCr], [smsr[b]], out=st_[:, 0:1], in0=st_[:, 0:1],
                          in1=SINKC[:, 2 + h:3 + h], op=ALU.min)
                        I("act", "activation", [bkr, smsr[b]], [PNsr[b], smsr[b]], out=PNs[:, b, :], in_=bk[0:32, 0:256],
                          func=AF.Exp, bias=st_[:, 0:1], scale=0.125, accum_out=st_[:, 1:2])
                        I("act", "activation", [SINKCr, smsr[b]], [smsr[b]], out=st_[:, 2:3], in_=SINKC[:, h:h + 1], func=AF.Exp,
                          bias=st_[:, 0:1])
                        I("dve", "tensor_tensor", [smsr[b]], [smsr[b]], out=st_[:, 2:3], in0=st_[:, 2:3], in1=st_[:, 1:2],
                          op=ALU.add)
                        I("dve", "reciprocal", [smsr[b]], [smsr[b]], out=st_[:, 3:4], in_=st_[:, 2:3])
                        I("dve", "tensor_scalar", [PNsr[b], smsr[b]], [PNsr[b]], out=PNs[:, b, :], in0=PNs[:, b, :],
                          scalar1=st_[:, 3:4], scalar2=None, op0=ALU.mult)
                        for c2 in range(2):
                            I("pe", "transpose", [PNsr[b], identr], [tbr], out=tb[:, jj * 64 + c2 * 32:jj * 64 + (c2 + 1) * 32],
                              in_=PNs[:, b, c2 * 128:(c2 + 1) * 128], identity=identb[0:32, 0:32])
                    strs.append(P.rec_end())
                P.merge(strs)
                bset[0] = [0, 1, 2, 3]
                I("dve", "tensor_copy", [tbr], [PTSr], out=PTS[:, 0:512], in_=tb[:, 0:512])
                for jj in range(8):
                    j = half * 8 + jj
                    for par in range(2):
                        o = po[par * 64:(par + 1) * 64, jj * 16:(jj + 1) * 16]
                        mm(o, CV[:, j, h * 64:(h + 1) * 64], PTS[:, jj * 64 + par * 16:jj * 64 + par * 16 + 16], True, False,
                           [CVr, PTSr], por)
                        mm(o, M.Vt[:, 0, h * 64:(h + 1) * 64], PTS[:, jj * 64 + 32 + par * 16:jj * 64 + 32 + par * 16 + 16], False,
                           True, [M.Vtr[0], PTSr], por)
                I("act", "activation", [por], [M.ATTTr[0]],
                  out=M.ATTT[:, 2 * h:2 * h + 2, half * 64:(half + 1) * 64].rearrange("p c (j i) -> p j c i", i=8),
                  in_=po[:, 0:128].rearrange("p (j c i) -> p j c i", c=2, i=8), func=AF.Copy)

        bset[0] = [0, 1, 2, 3, 4]
        off = 0
        SCf, off = A_at(off, [64, 64], F32); SCfr = ARalias("SCf", R0_olds)
        SCT, off = A_at(off, [64, 128], BF16, parts=64); SCTr = ARalias("SCT", R0_olds)
        QN, off = A_at(off, [4, 128], BF16, parts=64); QNr = ARalias("QN", R0_olds)
        KJ, off = A_at(off, [16, 64], BF16); KJr = ARalias("KJ", R0_olds)
        assert off <= 18496
        P.dma("sp", SCf[:, :, :], sC_d.rearrange("j h p k -> p (j h) k"), SCfr, writes=[SCfr])
        for p4 in range(16):
            bk, bkr = bank()
            for q_ in range(4):
                pr = p4 * 4 + q_
                mm(bk[0:64, q_ * 128:(q_ + 1) * 128], SCf[:, pr, :], identf[:], True, True, [SCfr, identfr], bkr)
            I("act", "activation", [bkr], [SCTr], out=SCT[:, p4 * 4:(p4 + 1) * 4, :],
              in_=bk[0:64, :].rearrange("p (a b) -> p a b", a=4), func=AF.Copy)
        bk, bkr = bank()
        mm(bk[0:64, 0:64], SNn, identf[0:64, 0:64], True, True, [SNnr, identfr], bkr)
        I("act", "activation", [bkr], [SNTr], out=SNT, in_=bk[0:64, 0:64], func=AF.Copy)

        def sample_inter(kind, h=None, pb=None, pbr=None):
            if kind == "pre":
                I("dve", "tensor_tensor", [M.QWr, SNTr], [QNr], out=QN[:, :, :].rearrange("p h (j i) -> p h j i", i=8),
                  in0=M.QW[:, :, :].rearrange("p h (j i) -> p h j i", i=8),
                  in1=SNT.rearrange("p (j h) -> p h j", h=4).unsqueeze(3).broadcast_to([64, 4, 16, 8]), op=ALU.mult)
            elif kind == "num":
                for j in range(16):
                    mm(pb[:, h * 128 + 8 * j:h * 128 + 8 * j + 8], SCT[:, j * 4 + h, :], M.QW[:, h, 8 * j:8 * j + 8], False, j == 15,
                       [SCTr, M.QWr], pbr)
            else:
                mm(pb[:, h * 128:(h + 1) * 128], onesb[0:64, :], QN[:, h, :], False, True, [onesbr, QNr], pbr)

        pn = mlstm_chunk(0, 0, mbcs, mbcsr, inter_fn=sample_inter)
        mlstm_finish(*pn, 0, M.HMTr[0])
        wout_tile(0, TS)

        pw, pwr = bank()
        for h in range(4):
            mm(pw[:, h:h + 1], M.G_A[:, 0:128], SELh(h, 1), True, False, [M.G_Ar, selr], pwr)
            mm(pw[:, h:h + 1], MTe, SEL[:, 512 + h * 128:512 + h * 128 + 1], False, True, [MTer, selr], pwr)
        for h in range(4):
            mm(pw[:, 8 + 16 * h:8 + 16 * (h + 1)], NSELh(h), DMT, True, True, [DMTr, selr], pwr)
        I("act", "activation", [pwr], [M.WKCr], out=M.WKC[:, 0:4], in_=pw[:, 0:4], func=AF.Exp)
        I("act", "activation", [pwr], [WCBr], out=WCB[:, :, :], in_=pw[:, 8:72].rearrange("p (h j) -> p h j", h=4), func=AF.Exp)
        for h in range(4):
            I("dve", "tensor_scalar", [M.MVaugr[0], M.WKCr], [M.VWr[h]], out=M.VW[:, h, :], in0=M.MVaug[:, 0, h, :],
              scalar1=M.WKC[:, h:h + 1], scalar2=None, op0=ALU.mult)
            I("dve", "tensor_scalar", [E16r, M.WKCr], [EWr], out=EW[:, h, :], in0=E16, scalar1=M.WKC[:, h:h + 1], scalar2=None,
              op0=ALU.mult)
        bk, bkr = bank()
        for h in range(4):
            mm(bk[0:64, h * 16:(h + 1) * 16], M.MKtok[:, 0, h * 64:(h + 1) * 64], EW[:, h, :], True, True, [M.MKtokr[0], EWr], bkr)
        I("dve", "tensor_tensor", [SNTr, WCBr], [NNTr], out=NNT.rearrange("p (j h) -> p h j", h=4),
          in0=SNT.rearrange("p (j h) -> p h j", h=4), in1=WCB[0:64, :, :], op=ALU.mult)
        I("dve", "tensor_tensor", [NNTr, bkr], [NNTr], out=NNT.rearrange("p (j h) -> p h j", h=4),
          in0=NNT.rearrange("p (j h) -> p h j", h=4), in1=bk[0:64, 0:64].rearrange("p (h j) -> p h j", h=4), op=ALU.add)
        bk2, bk2r = bank()
        mm(bk2[0:64, 0:64], NNT, identf[0:64, 0:64], True, True, [NNTr, identfr], bk2r)
        I("act", "activation", [bk2r], [NNor], out=NNo, in_=bk2[0:64, 0:64], func=AF.Copy)
        P.dma("sp", ns_o.rearrange("j h k -> (j h) k"), NNo, NNor, reads=[NNor])
        for h in range(4):
            I("dve", "tensor_tensor", [M.MKtokr[0], E16r], [KJr], out=KJ[:, :, :],
              in0=M.MKtok[:, 0, h * 64:(h + 1) * 64].unsqueeze(1).broadcast_to([128, 16, 64]),
              in1=E16.unsqueeze(2).broadcast_to([128, 16, 64]), op=ALU.mult)
            for half in range(2):
                bk, bkr = bank()
                mm(bk[:, :], M.VW[:, h, 0:128], KJ[:, half * 8:(half + 1) * 8, :], True, True, [M.VWr[h], KJr], bkr)
                scv = SCf[:, :, :].rearrange("p (j h) k -> p h j k", h=4)[:, h, half * 8:(half + 1) * 8, :]
                I("dve", "tensor_tensor", [SCfr, WCBr], [SCfr], out=scv, in0=scv,
                  in1=WCB[:, h, half * 8:(half + 1) * 8].unsqueeze(2).broadcast_to([128, 8, 64]), op=ALU.mult)
                I("dve", "tensor_tensor", [SCfr, bkr], [SCfr], out=scv, in0=scv,
                  in1=bk[:, :].rearrange("p (j k) -> p j k", k=64), op=ALU.add)
        P.dma("sp", Cs_o.rearrange("j h p k -> p (j h) k"), SCf[:, :, :], SCfr, reads=[SCfr])

        def dump_y(tiles):
            yo = yp_o.rearrange("(t p) d -> t p d", p=128)
            for t in tiles:
                if Yr[t].last_w is None:
                    continue
                if t < NTP:
                    P.dma("sp", yo[t], Y[:, t, :], Yr[t], reads=[Yr[t]])
                else:
                    P.dma("sp", ys_o, Y[:, t, :], Yr[t], reads=[Yr[t]])

        if stage <= 1:
            dump_y(range(NT))
            P.emit()
            return nc, P

        new_phase()
        WCQ = A([8, 256], BF16); WCQr = AR("WCQ")
        WCKV = A([8, 512], BF16); WCKVr = AR("WCKV")
        WCO = A([2, 1024], BF16); WCOr = AR("WCO")
        P.dma("pool", WCQ[:], w_cq_d.rearrange("(k p) n -> p k n", p=128), WCQr, writes=[WCQr])
        P.dma("pool", WCKV[:, :, 0:256], w_ck_d.rearrange("(k p) n -> p k n", p=128), WCKVr, writes=[WCKVr], group=True)
        P.dma("pool", WCKV[:, :, 256:512], w_cv_d.rearrange("(k p) n -> p k n", p=128), WCKVr, writes=[WCKVr], group=True)
        P.dma("pool", WCO[:], w_co_d.rearrange("(c p) n -> p c n", p=128), WCOr, writes=[WCOr])
        MEMX = A([2, D], F32); MEMXr = [AR("MEMX0"), AR("MEMX1")]
        MNT = A([8, 256], BF16); MNTr = [AR("MNT0"), AR("MNT1")]
        MKTm = A([4, 256], BF16, parts=64); MKTmr = AR("MKTm")
        MVm = A([2, 256], BF16); MVmr = AR("MVm")
        MKVo = A([2, 512], F32); MKVor = [AR("MKVo0"), AR("MKVo1")]
        GB = 4
        XNTb = A([8, GB * 128], BF16); XNTbr = [AR(f"XNTb{i}") for i in range(GB)]
        QcT = A([4, GB * 128], BF16, parts=64); QcTr = AR("QcT")
        OcT = A([2, GB * 128], BF16); OcTr = [AR(f"OcT{i}") for i in range(GB)]
        Eb2s = [A([4, 256], BF16) for _ in range(2)]; Eb2rs = [AR("Eb2a"), AR("Eb2b")]
        PT2s = [A([1, 1024], BF16) for _ in range(2)]; PT2rs = [AR("PT2a"), AR("PT2b")]
        sm2s = [A([1, 32], F32)[:, 0, :] for _ in range(2)]; sm2rs = [AR("sm2a"), AR("sm2b")]
        tbh = [tb[:, 0:512], tb[:, 512:1024]]
        tbhr = [P.res("tbA"), P.res("tbB")]
        for r_ in tbhr:
            r_.excl = True
            r_.readers = [x for x in ([tbr.last_w] if tbr.last_w is not None else [])] + list(tbr.readers)
        mem_t = mem_d.rearrange("(t p) d -> t p d", p=128)
        import os
        SK = os.environ.get("SKIP", "")
        for mt in range(2):
            P.dma("sp", MEMX[:, mt, :], mem_t[mt], MEMXr[mt], writes=[MEMXr[mt]])
        for mt in (range(2) if "noBnorm" not in SK else []):
            norm_T(MEMX[:, mt, :], MEMXr[mt], 2, MNT[:, :, mt * 128:(mt + 1) * 128], [MNTr[mt]])
        for mt in (range(2) if "noBkv" not in SK else []):
            bk, bkr = bank()
            for k in range(8):
                mm(bk[:, :], MNT[:, k, mt * 128:(mt + 1) * 128], WCKV[:, k, :], k == 0, k == 7, [MNTr[mt], WCKVr], bkr)
            if "noBcp1" not in SK:
                I("act", "activation", [bkr], [MKVor[mt]], out=MKVo[:, mt, :], in_=bk[:, :], func=AF.Copy)
            if "noBcp2" not in SK:
                I("act", "activation", [bkr], [MVmr], out=MVm[:, mt, :], in_=bk[:, 256:512], func=AF.Copy)
            if "noBdma" not in SK:
                P.dma("sp", memk_o[mt * 128:(mt + 1) * 128, :], MKVo[:, mt, 0:256], MKVor[mt], reads=[MKVor[mt]], group=True)
                P.dma("sp", memv_o[mt * 128:(mt + 1) * 128, :], MKVo[:, mt, 256:512], MKVor[mt], reads=[MKVor[mt]], group=True)
        for h0 in ((0, 2) if "noBkt" not in SK else []):
            bk, bkr = bank()
            for hh in range(2):
                h = h0 + hh
                for k in range(8):
                    mm(bk[0:64, hh * 256:(hh + 1) * 256], WCKV[:, k, h * 64:(h + 1) * 64], MNT[:, k, :], k == 0, k == 7,
                       [WCKVr] + MNTr, bkr)
            I("act", "activation", [bkr], [MKTmr], out=MKTm[:, h0:h0 + 2, :],
              in_=bk[0:64, :].rearrange("p (a b) -> p a b", a=2), func=AF.Copy)

        def cross_q(ntok, xres):
            for h in range(4):
                bk, bkr = bank()
                for k in range(8):
                    mm(bk[0:64, 0:ntok], WCQ[:, k, h * 64:(h + 1) * 64], XNTb[:, k, 0:ntok], k == 0, k == 7, [WCQr] + xres, bkr)
                I("act", "activation", [bkr], [QcTr], out=QcT[:, h, 0:ntok], in_=bk[0:64, 0:ntok], func=AF.Copy)

        def cross_tile_prompt(ti, sx):
            Eb2, Eb2r, PT2, PT2r, sm2, sm2r, tbx, tbxr = Eb2s[sx], Eb2rs[sx], PT2s[sx], PT2rs[sx], sm2s[sx], sm2rs[sx], tbh[sx], tbhr[sx]
            qs = slice(ti * 128, (ti + 1) * 128)
            bks = [bank(), bank()]
            for h in range(4):
                bk, bkr = bks[h // 2]
                mm(bk[:, (h % 2) * 256:(h % 2 + 1) * 256], QcT[:, h, qs], MKTm[:, h, :], True, True, [QcTr, MKTmr], bkr)
            for j in range(2):
                I("dve", "reduce_max", [bks[j][1]], [sm2r], out=sm2[:, 2 * j:2 * j + 2],
                  in_=bks[j][0][:, :].rearrange("p (a b) -> p a b", a=2), axis=AX.X)
            I("dve", "tensor_scalar", [sm2r], [sm2r], out=sm2[:, 0:4], in0=sm2[:, 0:4], scalar1=-0.125, scalar2=None, op0=ALU.mult)
            for h in range(4):
                bk, bkr = bks[h // 2]
                I("act", "activation", [bkr, sm2r], [Eb2r, sm2r], out=Eb2[:, h, :], in_=bk[:, (h % 2) * 256:(h % 2 + 1) * 256],
                  func=AF.Exp, bias=sm2[:, h:h + 1], scale=0.125, accum_out=sm2[:, 4 + h:5 + h])
            I("dve", "reciprocal", [sm2r], [sm2r], out=sm2[:, 8:12], in_=sm2[:, 4:8])
            for h in range(4):
                if h % 2 == 0:
                    I("act", "activation", [Eb2r, sm2r], [Eb2r], out=Eb2[:, h, :], in_=Eb2[:, h, :], func=AF.Copy,
                      scale=sm2[:, 8 + h:9 + h])
                else:
                    I("dve", "tensor_scalar", [Eb2r, sm2r], [Eb2r], out=Eb2[:, h, :], in0=Eb2[:, h, :],
                      scalar1=sm2[:, 8 + h:9 + h], scalar2=None, op0=ALU.mult)
            po, por = bank()
            for mc in range(2):
                for h in range(4):
                    I("pe", "transpose", [Eb2r, identr], [tbxr], out=tbx[:, h * 128:(h + 1) * 128],
                      in_=Eb2[:, h, mc * 128:(mc + 1) * 128], identity=identb[:])
                if mc == 0:
                    I("dve", "tensor_copy", [tbxr], [PT2r], out=PT2[:, 0, 0:512], in_=tbx)
                else:
                    I("act", "activation", [tbxr], [PT2r], out=PT2[:, 0, 512:1024], in_=tbx, func=AF.Copy)
            for h in range(4):
                for mc in range(2):
                    mm(po[(h % 2) * 64:(h % 2 + 1) * 64, (h // 2) * 128:(h // 2 + 1) * 128], MVm[:, mc, h * 64:(h + 1) * 64],
                       PT2[:, 0, (mc * 4 + h) * 128:(mc * 4 + h + 1) * 128], mc == 0, mc == 1, [MVmr, PT2r], por)
            I("act", "activation", [por], [OcTr[ti]], out=OcT[:, :, qs], in_=po[:, 0:256].rearrange("p (h q) -> p h q", h=2),
              func=AF.Copy)

        def wco_tile(ti, t):
            qs = slice(ti * 128, (ti + 1) * 128)
            for c in range(2):
                bk, bkr = bank()
                cc = slice(c * 512, (c + 1) * 512)
                for h in range(2):
                    mm(bk[:, :], OcT[:, h, qs], WCO[:, h, cc], h == 0, h == 1, [OcTr[ti], WCOr], bkr)
                I("dve", "tensor_tensor", [Yr[t], bkr], [Yr[t]], out=Y[:, t, cc], in0=Y[:, t, cc], in1=bk[:, :], op=ALU.add)

        import os
        for g0 in (range(0, NTP, GB) if "noBloop" not in os.environ.get("SKIP", "") else []):
            for ti in range(GB):
                norm_T(Y[:, g0 + ti, :], Yr[g0 + ti], 1, XNTb[:, :, ti * 128:(ti + 1) * 128], [XNTbr[ti]])
            cross_q(GB * 128, XNTbr)
            for tp_ in range(0, GB, 2):
                strs = []
                for sx in range(2):
                    P.rec_begin(); bset[0] = [0, 1, 2] if sx == 0 else [3, 4, 5]
                    cross_tile_prompt(tp_ + sx, sx)
                    wco_tile(tp_ + sx, g0 + tp_ + sx)
                    strs.append(P.rec_end())
                P.merge(strs)
            bset[0] = [0, 1, 2, 3, 4]
        TS = NTP
        CMn = A([8, 2, 256], BF16); CMnr = AR("CMn")
        CMV = A([16, 2, 256], BF16); CMVr = AR("CMV")
        CMKT = A([8, 4, 256], BF16, parts=64); CMKTr = AR("CMKT")
        Es = A([2, 4, 256], BF16, parts=8); Esr = [AR("Es0"), AR("Es1")]
        sm3 = A([2, 16], F32, parts=8); sm3r = [AR("sm30"), AR("sm31")]
        PT3 = A([1, 1024], BF16)[:, 0, :]; PT3r = AR("PT3")
        P.dma("pool", CMV[:, :, :, :], cmv_d.rearrange("j (c p) f -> p j c f", p=128), CMVr, writes=[CMVr])
        norm_T(Y[:, TS, :], Yr[TS], 1, XNTb[:, :, 0:128], [XNTbr[0]])
        cross_q(128, [XNTbr[0]])
        po3, po3r = banks[5], bres[5]
        for half in range(2):
            P.dma("pool", CMn[:, :, :, :], cmk_d[half * 8:(half + 1) * 8].rearrange("j (c p) f -> p j c f", p=128), CMnr,
                  writes=[CMnr])
            for jj in range(8):
                for h in range(4):
                    for mc in range(2):
                        I("pe", "transpose", [CMnr, identr], [tbr], out=tb[0:64, (h * 2 + mc) * 128:(h * 2 + mc + 1) * 128],
                          in_=CMn[:, jj, mc, h * 64:(h + 1) * 64], identity=identb[:])
                I("act", "activation", [tbr], [CMKTr], out=CMKT[:, jj, :, :],
                  in_=tb[0:64, :].rearrange("p (h m) -> p h m", h=4), func=AF.Copy)
            for jj in range(8):
                j = half * 8 + jj
                b = j % 2
                bks = [bank(), bank()]
                for h in range(4):
                    bk, bkr = bks[h // 2]
                    mm(bk[0:8, (h % 2) * 256:(h % 2 + 1) * 256], QcT[:, h, 8 * j:8 * j + 8], CMKT[:, jj, h, :], True, True,
                       [QcTr, CMKTr], bkr)
                st_ = sm3[:, b, :]
                for q_ in range(2):
                    I("dve", "reduce_max", [bks[q_][1]], [sm3r[b]], out=st_[:, 2 * q_:2 * q_ + 2],
                      in_=bks[q_][0][0:8, :].rearrange("p (a b) -> p a b", a=2), axis=AX.X)
                I("dve", "tensor_scalar", [sm3r[b]], [sm3r[b]], out=st_[:, 0:4], in0=st_[:, 0:4], scalar1=-0.125, scalar2=None,
                  op0=ALU.mult)
                for h in range(4):
                    bk, bkr = bks[h // 2]
                    I("act", "activation", [bkr, sm3r[b]], [Esr[b], sm3r[b]], out=Es[:, b, h, :],
                      in_=bk[0:8, (h % 2) * 256:(h % 2 + 1) * 256], func=AF.Exp, bias=st_[:, h:h + 1], scale=0.125,
                      accum_out=st_[:, 4 + h:5 + h])
                I("dve", "reciprocal", [sm3r[b]], [sm3r[b]], out=st_[:, 8:12], in_=st_[:, 4:8])
                I("dve", "tensor_tensor", [Esr[b], sm3r[b]], [Esr[b]], out=Es[:, b, :, :], in0=Es[:, b, :, :],
                  in1=st_[:, 8:12].unsqueeze(2).broadcast_to([8, 4, 256]), op=ALU.mult)
                for mc in range(2):
                    for h in range(4):
                        c0_ = j * 64 + (mc * 4 + h) * 8
                        I("pe", "transpose", [Esr[b], identr], [tbr], out=tb[:, c0_:c0_ + 8],
                          in_=Es[:, b, h, mc * 128:(mc + 1) * 128], identity=identb[0:8, 0:8])
            I("dve", "tensor_copy", [tbr], [PT3r], out=PT3[:, half * 512:(half + 1) * 512], in_=tb[:, half * 512:(half + 1) * 512])
        for j in range(16):
            for h in range(4):
                for mc in range(2):
                    c0_ = j * 64 + (mc * 4 + h) * 8
                    mm(po3[(h % 2) * 64:(h % 2 + 1) * 64, (j * 2 + h // 2) * 8:(j * 2 + h // 2) * 8 + 8],
                       CMV[:, j, mc, h * 64:(h + 1) * 64], PT3[:, c0_:c0_ + 8], mc == 0, mc == 1, [CMVr, PT3r], po3r)
        I("act", "activation", [po3r], [OcTr[0]], out=OcT[:, :, 0:128].rearrange("p c (j i) -> p j c i", i=8),
          in_=po3[:, 0:256].rearrange("p (j c i) -> p j c i", c=2, i=8), func=AF.Copy)
        wco_tile(0, TS)

        if stage <= 2:
            dump_y(range(NT))
            P.emit()
            return nc, P

        new_phase()
        USE_SQRT[0] = True
        NF = FH // 128
        XNTa = A([8, NT * 128], BF16); XNTar = [AR(f"XNTa{t}") for t in range(NT)]
        NSLOT = 12
        WG = A([NSLOT, 8, 128], BF16); WU = A([NSLOT, 8, 128], BF16); WD = A([NSLOT, D], BF16)
        Wsr = [AR(f"Ws{s_}") for s_ in range(NSLOT)]
        Hh = A([6, 512], BF16); Hr = [AR(f"H{j}") for j in range(6)]
        SG = A([2, 512], BF16); SGr = [AR("SG0"), AR("SG1")]
        OUT = A([1, D], F32)[:, 0, :]; OUTr = AR("OUT")
        gfin = A([1, D], F32)[:, 0, :]; gfinr = AR("gfin")
        P.dma("sp", gfin, g_final_d.partition_broadcast(128), gfinr, writes=[gfinr])
        passes = [list(range(0, 6)), list(range(6, 12)), list(range(12, 17)), list(range(17, 22))]
        groups = [(0, 4), (4, 4), (8, 4), (12, 4), (16, 1)]
        wd_v = w_down_d.rearrange("(f p) n -> f p n", p=128)
        wslot = {}
        nload = [0]

        def load_w(f):
            s_ = nload[0] % NSLOT
            nload[0] += 1
            wslot[f] = s_
            P.dma("pool", WG[:, s_], w_gate_d[:, f * 128:(f + 1) * 128].rearrange("(k p) n -> p k n", p=128), Wsr[s_],
                  writes=[Wsr[s_]], group=True)
            P.dma("pool", WU[:, s_], w_up_d[:, f * 128:(f + 1) * 128].rearrange("(k p) n -> p k n", p=128), Wsr[s_],
                  writes=[Wsr[s_]], group=True)
            P.dma("pool", WD[:, s_], wd_v[f], Wsr[s_], writes=[Wsr[s_]], group=True)

        for f in passes[0]:
            load_w(f)
        gcnt = [0]
        for pi, fl in enumerate(passes):
            for gi, (t0, n) in enumerate(groups):
                if pi == 0:
                    for t in range(t0, t0 + n):
                        norm_T(Y[:, t, :], Yr[t], 3, XNTa[:, :, t * 128:(t + 1) * 128], [XNTar[t]])
                if pi + 1 < len(passes) and gi == 0:
                    for f in passes[pi + 1]:
                        load_w(f)
                ntok = n * 128
                tok = slice(t0 * 128, t0 * 128 + ntok)
                xr = [XNTar[t] for t in range(t0, t0 + n)]
                for j, f in enumerate(fl):
                    s_ = wslot[f]
                    b = gcnt[0] % 2
                    gcnt[0] += 1
                    pg, pgr = bank()
                    pu, pur = bank()
                    for k in range(8):
                        mm(pg[:, 0:ntok], WG[:, s_, k, :], XNTa[:, k, tok], k == 0, k == 7, [Wsr[s_]] + xr, pgr)
                    for k in range(8):
                        mm(pu[:, 0:ntok], WU[:, s_, k, :], XNTa[:, k, tok], k == 0, k == 7, [Wsr[s_]] + xr, pur)
                    I("act", "activation", [pgr], [SGr[b]], out=SG[:, b, 0:ntok], in_=pg[:, 0:ntok], func=AF.Silu)
                    I("dve", "tensor_tensor", [SGr[b], pur], [Hr[j]], out=Hh[:, j, 0:ntok], in0=SG[:, b, 0:ntok],
                      in1=pu[:, 0:ntok], op=ALU.mult)
                for ti in range(n):
                    t = t0 + ti
                    for c in range(2):
                        pd, pdr = bank()
                        for j, f in enumerate(fl):
                            s_ = wslot[f]
                            mm(pd[:, :], Hh[:, j, ti * 128:(ti + 1) * 128], WD[:, s_, c * 512:(c + 1) * 512], j == 0,
                               j == len(fl) - 1, [Hr[j], Wsr[s_]], pdr)
                        I("dve", "tensor_tensor", [Yr[t], pdr], [Yr[t]], out=Y[:, t, c * 512:(c + 1) * 512],
                          in0=Y[:, t, c * 512:(c + 1) * 512], in1=pd[:, :], op=ALU.add)
                if pi == len(passes) - 1:
                    yo = yp_o.rearrange("(t p) d -> t p d", p=128)
                    for t in range(t0, t0 + n):
                        rstd, sr = norm_stats(Y[:, t, :], Yr[t], 0)
                        I("dve", "scalar_tensor_tensor", [Yr[t], sr, gfinr], [OUTr], out=OUT, in0=Y[:, t, :], scalar=rstd,
                          in1=gfin, op0=ALU.mult, op1=ALU.mult)
                        P.dma("sp", (yo[t] if t < NTP else ys_o), OUT, OUTr, reads=[OUTr])
        P.emit()
        return nc, P


def make_consts(hf):
    c = {}
    c["c_ident"] = np.eye(128, dtype=np.float32)
    i = np.arange(128)[:, None]; j = np.arange(256)[None, :]
    band = np.where((j >= i) & (j <= i + 128), 0.0, NEG).astype(np.float32)
    first = band.copy()
    if hf == 0:
        first[:, :128] = NEG
    c["c_mb_band"] = band; c["c_mb_first"] = first
    s = np.arange(128)[:, None]; t = np.arange(128)[None, :]
    c["c_mb_caus"] = np.where(s <= t, 0.0, NEG).astype(np.float32)
    c["c_mb_causs"] = np.where((s <= t) & (s // 8 == t // 8), 0.0, NEG).astype(np.float32)
    sel = np.zeros((4, 1024), np.float32)
    for h in range(4):
        sel[h, h * 128:(h + 1) * 128] = 1.0
        sel[h, 512 + h * 128:512 + (h + 1) * 128] = -1.0
    c["c_sel"] = sel
    pm = np.zeros((4, 2), np.float32)
    pm[:, 0] = 1.0 if hf else 0.0
    pm[:, 1] = 0.0 if hf else NEG
    c["c_pmask"] = pm
    r = np.arange(32)[:, None] % 8
    p = np.arange(128)[None, :]
    c["c_smc"] = np.where(p >= r, 0.0, NEG).astype(np.float32)
    smn = np.full((32, 16, 128), NEG, np.float32)
    for jq in range(16):
        for ii in range(8):
            smn[(np.arange(32) % 8) >= ii, jq, jq * 8 + ii] = 0.0
    c["c_smn"] = smn
    c["c_bt"] = np.where((s <= t) & (s // 8 == t // 8), 1.0, 0.0).astype(np.float32)
    e = np.zeros((128, 16), np.float32); e[np.arange(128), np.arange(128) // 8] = 1.0
    c["c_eseq"] = e
    return c

def shard_inputs(inp):
    maps = []
    W = ["w_in", "b_igate", "b_fgate", "attn_sinks", "g_mlstm_head", "w_out", "g_mix", "g_cross", "g_mem",
         "w_cq", "w_ck", "w_cv", "w_co", "g_ffn", "w_gate", "w_up", "w_down"]
    wd = {k: np.ascontiguousarray(np.asarray(inp[k], np.float32)[0]) for k in W}
    wd["g_final"] = np.ascontiguousarray(np.asarray(inp["g_final"], np.float32))
    xp = np.asarray(inp["x_prompt"], np.float32); xs = np.asarray(inp["x_sample"], np.float32)
    for c in range(8):
        b, hf = c // 2, c % 2
        m = dict(wd)
        m["xp"] = np.ascontiguousarray(xp[b, hf * 2048:(hf + 1) * 2048])
        m["xpre"] = np.ascontiguousarray(xp[b, 0:2048]) if hf else np.zeros((2048, 1024), np.float32)
        m["xs"] = np.ascontiguousarray(xs[16 * c:16 * c + 16].reshape(128, 1024))
        m["mem"] = np.ascontiguousarray(np.asarray(inp["mem_prompt"], np.float32)[b])
        sl = slice(16 * c, 16 * c + 16)
        m["csk"] = np.ascontiguousarray(np.asarray(inp["cache_swa_k"], np.float32)[0, sl].reshape(16, 128, 128))
        m["csv"] = np.ascontiguousarray(np.asarray(inp["cache_swa_v"], np.float32)[0, sl].reshape(16, 128, 128))
        m["sC"] = np.ascontiguousarray(np.asarray(inp["state_mlstm_C"], np.float32)[0, sl])
        m["sn"] = np.ascontiguousarray(np.asarray(inp["state_mlstm_n"], np.float32)[0, sl])
        m["sm"] = np.ascontiguousarray(np.asarray(inp["state_mlstm_m"], np.float32)[0, sl])
        m["cmk"] = np.ascontiguousarray(np.asarray(inp["cache_mem_k"], np.float32)[0, sl].reshape(16, 256, 256))
        m["cmv"] = np.ascontiguousarray(np.asarray(inp["cache_mem_v"], np.float32)[0, sl].reshape(16, 256, 256))
        m.update(make_consts(hf))
        sk = wd["attn_sinks"]
        sc = np.zeros((32, 2), np.float32)
        rr = np.arange(32)
        for h in range(2):
            sc[:, h] = sk[4 * h + 2 * ((rr % 16) // 8) + rr // 16]
        m["c_sinkcol"] = sc
        maps.append(m)
    return maps

def gather(res):
    f = np.float32
    yp = np.zeros((4, 4096, 1024), f); ys = np.zeros((128, 8, 1024), f)
    skp = np.zeros((1, 4, 128, 2, 64), f); svp = np.zeros_like(skp)
    Cp = np.zeros((1, 4, 4, 128, 64), f); npp = np.zeros((1, 4, 4, 64), f); mp = np.zeros((1, 4, 4), f)
    mkp = np.zeros((1, 4, 256, 4, 64), f); mvp = np.zeros_like(mkp)
    sks = np.zeros((1, 128, 128, 2, 64), f); svs = np.zeros_like(sks)
    Cs = np.zeros((1, 128, 4, 128, 64), f); ns = np.zeros((1, 128, 4, 64), f); ms = np.zeros((1, 128, 4), f)
    for c in range(8):
        r = res[c]; b, hf = c // 2, c % 2
        yp[b, hf * 2048:(hf + 1) * 2048] = r["yp"]
        ys[16 * c:16 * c + 16] = r["ys"].reshape(16, 8, 1024)
        if hf == 1:
            skp[0, b] = r["swak"].reshape(128, 2, 64); svp[0, b] = r["swav"].reshape(128, 2, 64)
            Cp[0, b] = r["Cp"]; npp[0, b] = r["np"]; mp[0, b] = r["mp"].reshape(4)
        else:
            mkp[0, b] = r["memk"].reshape(256, 4, 64); mvp[0, b] = r["memv"].reshape(256, 4, 64)
        sl = slice(16 * c, 16 * c + 16)
        sks[0, sl] = r["sks"].reshape(16, 128, 2, 64); svs[0, sl] = r["svs"].reshape(16, 128, 2, 64)
        Cs[0, sl] = r["Cs"]; ns[0, sl] = r["ns"]; ms[0, sl] = r["ms"]
    return (yp, ys, skp, svp, Cp, npp, mp, mkp, mvp, sks, svs, Cs, ns, ms)


_CACHE = {}


def kernel(**inputs):
    if "nc" not in _CACHE:
        _CACHE["nc"] = build_program(3)[0]
    nc = _CACHE["nc"]
    maps = shard_inputs(inputs)
    res = run_bass_kernel_spmd(nc, maps, core_ids=list(range(8)))
    return gather(res.results)
```

```python
import contextlib
from concourse.bass_utils import run_bass_kernel_spmd
import numpy as np
import concourse.bass as bass
import concourse.mybir as mybir

F32 = mybir.dt.float32
BF16 = mybir.dt.bfloat16
I32 = mybir.dt.int32
AF = mybir.ActivationFunctionType
ALU = mybir.AluOpType
AX = mybir.AxisListType

ENGS = ("pe", "act", "dve", "pool", "sp")


class Res:
    __slots__ = ("name", "last_w", "readers", "sem", "dcount", "excl")

    def __init__(self, name):
        self.name = name
        self.last_w = None
        self.readers = []
        self.sem = None
        self.dcount = 0
        self.excl = False


class Op:
    __slots__ = ("eng", "fn", "deps", "dma_res", "sig", "cnt", "k", "group")

    def __init__(self, eng, fn, dma_res):
        self.eng = eng
        self.fn = fn
        self.deps = set()
        self.dma_res = dma_res
        self.sig = False
        self.cnt = 0
        self.k = 0


class Prog:
    def __init__(self, nc):
        self.nc = nc
        self.ops = []
        self.nres = 0
        self.inherit = []
        self.phase_res = []

    def res(self, name=None, arena=False):
        self.nres += 1
        r = Res(name or f"r{self.nres}")
        if arena:
            r.readers = list(self.inherit)
            self.phase_res.append(r)
        return r

    def new_phase(self):
        inh = set(self.inherit)
        for r in self.phase_res:
            if r.last_w is not None:
                inh.add(r.last_w)
            inh.update(r.readers)
        self.inherit = sorted(inh)
        self.phase_res = []

    def rec_begin(self):
        self._rec = []

    def rec_end(self):
        r = self._rec
        self._rec = None
        return r

    def merge(self, streams):
        streams = [s_ for s_ in streams if s_]
        pos = [0] * len(streams)
        while True:
            best = None
            for k, s_ in enumerate(streams):
                if pos[k] < len(s_):
                    f = pos[k] / len(s_)
                    if best is None or f < best[0]:
                        best = (f, k)
            if best is None:
                break
            k = best[1]
            a, kw = streams[k][pos[k]]
            pos[k] += 1
            self.op(*a, **kw)

    def op(self, eng, fn, reads=(), writes=(), dma_res=None, accum=False, group=False):
        if getattr(self, "_rec", None) is not None:
            self._rec.append(((eng, fn, tuple(reads), tuple(writes)), dict(dma_res=dma_res, accum=accum, group=group)))
            return None
        i = len(self.ops)
        o = Op(eng, fn, dma_res)
        for r in reads:
            if r.last_w is not None:
                o.deps.add(r.last_w)
            if r.excl:
                for q in r.readers:
                    if self.ops[q].eng != eng:
                        o.deps.add(q)
            r.readers.append(i)
        for r in writes:
            if r.last_w is not None:
                lw = self.ops[r.last_w]
                if group and lw.dma_res is not None and lw.dma_res is dma_res:
                    o.deps |= lw.deps
                elif not (accum and lw.eng == "pe" and eng == "pe"):
                    o.deps.add(r.last_w)
            latest = {}
            for q in r.readers:
                if q == i:
                    continue
                oq = self.ops[q]
                if oq.dma_res is not None:
                    o.deps.add(q)
                elif latest.get(oq.eng, -1) < q:
                    latest[oq.eng] = q
            o.deps.update(latest.values())
            r.last_w = i
            r.readers = []
        if eng == "pe":
            o.deps = {d for d in o.deps if self.ops[d].eng != "pe" or self.ops[d].dma_res is not None}
        self.ops.append(o)
        return i

    def dma(self, eng, out, in_, res, reads=(), writes=(), group=False, **kw):
        kw = dict(kw); kw["out"] = out; kw["in_"] = in_
        return self.op(eng, ("dma_start", kw), reads=reads, writes=writes, dma_res=res, group=group)

    def I(self, eng, name, reads=(), writes=(), **kw):
        return self.op(eng, (name, kw), reads=reads, writes=writes)

    def emit(self, final_wait_all=True):
        nc = self.nc
        ops = self.ops
        for o in ops:
            for d in o.deps:
                ops[d].sig = True
        per_eng = {e: [] for e in ENGS}
        for i, o in enumerate(ops):
            per_eng[o.eng].append(i)
        import contextlib
        with contextlib.ExitStack() as st:
            esem = {e: st.enter_context(nc.semaphore(f"s_{e}")) for e in ENGS}
            ecount = {e: 0 for e in ENGS}
            dma_sems = []
            for i, o in enumerate(ops):
                if o.dma_res is not None:
                    r = o.dma_res
                    if r.sem is None:
                        r.sem = st.enter_context(nc.semaphore(f"d{len(dma_sems)}_{r.name}"))
                        dma_sems.append(r)
                    r.dcount += 1
                    o.cnt = 16 * r.dcount
                elif o.sig:
                    ecount[o.eng] += 1
                    o.cnt = ecount[o.eng]
            self.n_dma_sems = len(dma_sems)
            know = {e: {} for e in ENGS}
            know_issue = [None] * len(ops)

            def key_of(o):
                return ("d", id(o.dma_res)) if o.dma_res is not None else ("e", o.eng)

            block = st.enter_context(nc.Block())
            handles = {}

            plan = [None] * len(ops)
            for i, o in enumerate(ops):
                kn = know[o.eng]
                need = {}
                for d in o.deps:
                    p = ops[d]
                    k = key_of(p)
                    if kn.get(k, 0) >= p.cnt:
                        continue
                    if need.get(k, (0, None))[0] < p.cnt:
                        need[k] = (p.cnt, d)
                waits = []
                for k, (cnt, d) in need.items():
                    p = ops[d]
                    sem = p.dma_res.sem if p.dma_res is not None else esem[p.eng]
                    waits.append((sem, cnt))
                    kn[k] = max(kn.get(k, 0), cnt)
                    ki = know_issue[d]
                    for kk, vv in ki.items():
                        if kn.get(kk, 0) < vv:
                            kn[kk] = vv
                know_issue[i] = dict(kn)
                plan[i] = waits
            self.n_waits = sum(len(w) for w in plan)

            def make(ename):
                def body(eh):
                    for i in per_eng[ename]:
                        o = ops[i]
                        for sem, cnt in plan[i]:
                            eh.wait_ge(sem, cnt)
                        ins = getattr(eh, o.fn[0])(**o.fn[1])
                        if o.dma_res is not None:
                            ins.then_inc(o.dma_res.sem, 16)
                        elif o.sig:
                            ins.then_inc(esem[o.eng], 1)
                    if ename == "sp" and final_wait_all:
                        for r in dma_sems:
                            eh.wait_ge(r.sem, 16 * r.dcount)
                        for e in ("pe", "act", "dve", "pool"):
                            if ecount[e]:
                                eh.wait_ge(esem[e], ecount[e])
                return body

            block.tensor(make("pe"))
            block.scalar(make("act"))
            block.vector(make("dve"))
            block.gpsimd(make("pool"))
            block.sync(make("sp"))


D = 1024
FH = 2816
EPS = 1e-6
NTP = 16
NT = 17
GT = 2
NEG = -30000.0
DBG_G0 = 2


def build_program(stage=3, debug=False):
    nc = bass.Bass("TRN2", target_bir_lowering=False)
    P = Prog(nc)

    def din(name, shape, dt=F32):
        return nc.dram_tensor(name, list(shape), dt, kind="ExternalInput").ap()

    def dout(name, shape):
        return nc.dram_tensor(name, list(shape), F32, kind="ExternalOutput").ap()

    xp_d = din("xp", [2048, D]); xpre_d = din("xpre", [2048, D]); xs_d = din("xs", [128, D])
    mem_d = din("mem", [256, D])
    csk_d = din("csk", [16, 128, 128]); csv_d = din("csv", [16, 128, 128])
    sC_d = din("sC", [16, 4, 128, 64]); sn_d = din("sn", [16, 4, 64]); sm_d = din("sm", [16, 4])
    cmk_d = din("cmk", [16, 256, 256]); cmv_d = din("cmv", [16, 256, 256])
    w_in_d = din("w_in", [D, 2312]); b_i_d = din("b_igate", [4]); b_f_d = din("b_fgate", [4])
    sinks_d = din("attn_sinks", [8]); ghead_d = din("g_mlstm_head", [512]); w_out_d = din("w_out", [D, D])
    g_mix_d = din("g_mix", [D]); g_cross_d = din("g_cross", [D]); g_mem_d = din("g_mem", [D])
    w_cq_d = din("w_cq", [D, 256]); w_ck_d = din("w_ck", [D, 256]); w_cv_d = din("w_cv", [D, 256])
    w_co_d = din("w_co", [256, D]); g_ffn_d = din("g_ffn", [D])
    w_gate_d = din("w_gate", [D, FH]); w_up_d = din("w_up", [D, FH]); w_down_d = din("w_down", [FH, D])
    g_final_d = din("g_final", [D])
    ident_d = din("c_ident", [128, 128]); mb_band_d = din("c_mb_band", [128, 256]); mb_first_d = din("c_mb_first", [128, 256])
    mb_caus_d = din("c_mb_caus", [128, 128]); mb_causs_d = din("c_mb_causs", [128, 128])
    sel_d = din("c_sel", [4, 1024]); pmask_d = din("c_pmask", [4, 2])
    smc_d = din("c_smc", [32, 128]); smn_d = din("c_smn", [32, 16, 128]); sinkcol_d = din("c_sinkcol", [32, 2])
    bt_d = din("c_bt", [128, 128]); eseq_d = din("c_eseq", [128, 16])

    yp_o = dout("yp", [2048, D]); ys_o = dout("ys", [128, D])
    swak_o = dout("swak", [128, 128]); swav_o = dout("swav", [128, 128])
    Cp_o = dout("Cp", [4, 128, 64]); np_o = dout("np", [4, 64]); mp_o = dout("mp", [4, 1])
    memk_o = dout("memk", [256, 256]); memv_o = dout("memv", [256, 256])
    sks_o = dout("sks", [16, 128, 128]); svs_o = dout("svs", [16, 128, 128])
    Cs_o = dout("Cs", [16, 4, 128, 64]); ns_o = dout("ns", [16, 4, 64]); ms_o = dout("ms", [16, 4])

    st = contextlib.ExitStack()
    with st:
        def sb(name, shape, dt):
            return st.enter_context(nc.sbuf_tensor(name, list(shape), dt))

        def ps(name, shape, dt):
            return st.enter_context(nc.psum_tensor(name, list(shape), dt))

        banks = [ps(f"bk{i}", [128, 512], F32) for i in range(7)]
        bres = [P.res(f"bk{i}") for i in range(7)]
        for r_ in bres:
            r_.excl = True
        tb = ps("tb", [128, 1024], BF16)
        tbh = [tb[:, 0:512], tb[:, 512:1024]]
        tbhr = [P.res("tbA"), P.res("tbB")]
        for r_ in tbhr:
            r_.excl = True
        bki = [0]

        bset = [[0, 1, 2, 3, 4]]
        bcnt = {}

        def bank():
            key = tuple(bset[0])
            c = bcnt.get(key, 0)
            bcnt[key] = c + 1
            i = bset[0][c % len(key)]
            return banks[i], bres[i]

        Y = sb("Y", [128, NT, D], F32)
        Yr = [P.res(f"Y{t}") for t in range(NT)]
        identb = sb("identb", [128, 128], BF16); identr = P.res("identb")
        identf = sb("identf", [128, 128], F32); identfr = P.res("identf")
        onesb = sb("onesb", [128, 128], BF16); onesbr = P.res("onesb")
        onesf = sb("onesf", [128, 256], F32); onesfr = P.res("onesf")
        SEL = sb("SEL", [4, 1024], F32); selr = P.res("SEL")
        gcols = sb("gcols", [128, 4, 8], F32); gcolsr = P.res("gcols")
        gheadc = sb("gheadc", [128, 4], F32); gheadr = P.res("ghead")
        sinkb = sb("sinkb", [128, 16], F32); sinkbr = P.res("sinkb")
        gb4 = sb("gb4", [4, 4], F32); gb4r = P.res("gb4")
        mbband = sb("mbband", [128, 256], BF16); mbbandr = P.res("mbband")
        mbfirst = sb("mbfirst", [128, 256], BF16); mbfirstr = P.res("mbfirst")
        mbcaus = sb("mbcaus", [128, 128], BF16); mbcausr = P.res("mbcaus")
        stat = sb("stat", [128, 8, 4], F32)
        statr = [P.res(f"stat{i}") for i in range(8)]
        stati = [0]
        USE_SQRT = [False]
        xsb = sb("xsb", [128, 2, D], BF16); xsbr = [P.res("xsb0"), P.res("xsb1")]
        Cst = sb("Cst", [64, 4, 129], F32); Cstr = [P.res(f"Cst{h}") for h in range(4)]
        ARN = 64400
        arena = sb("arena", [128, ARN], BF16)
        aoff = [0]

        def A(shape, dt, parts=128, name=None):
            n = int(np.prod(shape))
            nb = n * (4 if dt == F32 else 2)
            n16 = (nb + 1) // 2
            n16 = (n16 + 15) // 16 * 16
            assert aoff[0] + n16 <= ARN, f"arena overflow {aoff[0]}+{n16} ({name})"
            v = arena[0:parts, aoff[0]:aoff[0] + n16]
            aoff[0] += n16
            if dt == F32:
                v = v.bitcast(F32)
            v = v[:, 0:n]
            if len(shape) == 2:
                v = v.rearrange("p (a b) -> p a b", a=shape[0])
            elif len(shape) == 3:
                v = v.rearrange("p (a b c) -> p a b c", a=shape[0], b=shape[1])
            return v

        def new_phase():
            P.new_phase()
            aoff[0] = 0

        def AR(name):
            return P.res(name, arena=True)

        def A_at(off, shape, dt, parts=128):
            n = int(np.prod(shape))
            nb = n * (4 if dt == F32 else 2)
            n16 = ((nb + 1) // 2 + 15) // 16 * 16
            v = arena[0:parts, off:off + n16]
            if dt == F32:
                v = v.bitcast(F32)
            v = v[:, 0:n]
            if len(shape) == 2:
                v = v.rearrange("p (a b) -> p a b", a=shape[0])
            elif len(shape) == 3:
                v = v.rearrange("p (a b c) -> p a b c", a=shape[0], b=shape[1])
            return v, off + n16

        def ARalias(name, olds):
            r = P.res(name, arena=True)
            dd = set(r.readers)
            for o_ in olds:
                if o_.last_w is not None:
                    dd.add(o_.last_w)
                dd.update(o_.readers)
            r.readers = sorted(dd)
            return r

        I = P.I

        def mm(out, lhsT, rhs, start, stop, reads, wres):
            I("pe", "matmul", reads, [wres], out=out, lhsT=lhsT, rhs=rhs, start=start, stop=stop)

        P.dma("pool", identb[:], ident_d, identr, writes=[identr])
        P.dma("sp", identf[:], ident_d, identfr, writes=[identfr])
        I("dve", "memset", [], [onesbr], ap=onesb[:], constant=1.0)
        I("dve", "memset", [], [onesfr], ap=onesf[:], constant=1.0)
        P.dma("sp", SEL[:], sel_d, selr, writes=[selr])
        for i, g in enumerate((g_mix_d, g_cross_d, g_mem_d, g_ffn_d)):
            P.dma("sp", gcols[:, i, :], g.rearrange("(k p) -> p k", p=128), gcolsr, writes=[gcolsr], group=True, allow_slow_non_contiguous=True)
        P.dma("sp", gheadc[:], ghead_d.rearrange("(h p) -> p h", p=128), gheadr, writes=[gheadr], allow_slow_non_contiguous=True)
        P.dma("sp", gb4[:, 0:1], b_i_d.rearrange("(h o) -> h o", o=1), gb4r, writes=[gb4r], group=True, allow_slow_non_contiguous=True)
        P.dma("sp", gb4[:, 1:2], b_f_d.rearrange("(h o) -> h o", o=1), gb4r, writes=[gb4r], group=True, allow_slow_non_contiguous=True)
        P.dma("sp", gb4[:, 2:4], pmask_d, gb4r, writes=[gb4r], group=True, allow_slow_non_contiguous=True)
        P.dma("sp", sinkb[:, 0:8], sinks_d.partition_broadcast(128), sinkbr, writes=[sinkbr])
        I("dve", "tensor_scalar", [sinkbr], [sinkbr], out=sinkb[:, 8:16], in0=sinkb[:, 0:8], scalar1=-1.0, scalar2=None,
          op0=ALU.mult)
        I("dve", "tensor_scalar", [gb4r], [gb4r], out=gb4[:, 1:2], in0=gb4[:, 1:2], scalar1=-1.0, scalar2=None, op0=ALU.mult)
        P.dma("pool", mbband[:], mb_band_d, mbbandr, writes=[mbbandr])
        P.dma("pool", mbfirst[:], mb_first_d, mbfirstr, writes=[mbfirstr])
        P.dma("pool", mbcaus[:], mb_caus_d, mbcausr, writes=[mbcausr])
        for h in range(4):
            I("dve", "memset", [], [Cstr[h]], ap=Cst[:, h, :], constant=0.0)

        def SELh(h, n=128):
            return SEL[:, h * 128:h * 128 + n]

        def NSELh(h, n=128):
            return SEL[:, 512 + h * 128:512 + h * 128 + n]

        def norm_stats(src, sres, jb=0):
            i = stati[0] % 8
            stati[0] += 1
            sr = statr[i]
            I("act", "activation", [sres], [xsbr[jb], sr], out=xsb[:, jb, :], in_=src, func=AF.Square, accum_out=stat[:, i, 0:1])
            I("dve", "tensor_scalar", [sr], [sr], out=stat[:, i, 1:2], in0=stat[:, i, 0:1], scalar1=1.0 / D, scalar2=EPS,
              op0=ALU.mult, op1=ALU.add)
            if USE_SQRT[0]:
                I("act", "activation", [sr], [sr], out=stat[:, i, 2:3], in_=stat[:, i, 1:2], func=AF.Sqrt)
                I("dve", "reciprocal", [sr], [sr], out=stat[:, i, 3:4], in_=stat[:, i, 2:3])
            else:
                I("act", "activation", [sr], [sr], out=stat[:, i, 2:3], in_=stat[:, i, 1:2], func=AF.Ln)
                I("act", "activation", [sr], [sr], out=stat[:, i, 3:4], in_=stat[:, i, 2:3], func=AF.Exp, scale=-0.5)
            return stat[:, i, 3:4], sr

        xsi = [0]

        def norm_T(src, sres, gi, dst, dres, half=None):
            b = xsi[0] % 2 if half is None else half
            xsi[0] += 1
            rstd, sr = norm_stats(src, sres, b)
            I("dve", "tensor_scalar", [sres, sr], [xsbr[b]], out=xsb[:, b, :], in0=src, scalar1=rstd, scalar2=None, op0=ALU.mult)
            if half is None:
                for k in range(8):
                    I("pe", "transpose", [xsbr[b], identr], [*tbhr], out=tb[:, k * 128:(k + 1) * 128],
                      in_=xsb[:, b, k * 128:(k + 1) * 128], identity=identb[:])
                for k in range(8):
                    I("act", "activation", [*tbhr, gcolsr], dres, out=dst[:, k, :], in_=tb[:, k * 128:(k + 1) * 128],
                      func=AF.Copy, scale=gcols[:, gi, k:k + 1])
            else:
                for kb in range(2):
                    for k4 in range(4):
                        k = kb * 4 + k4
                        I("pe", "transpose", [xsbr[b], identr], [tbhr[half]], out=tbh[half][:, k4 * 128:(k4 + 1) * 128],
                          in_=xsb[:, b, k * 128:(k + 1) * 128], identity=identb[:])
                    for k4 in range(4):
                        k = kb * 4 + k4
                        I("act", "activation", [tbhr[half], gcolsr], dres, out=dst[:, k, :],
                          in_=tbh[half][:, k4 * 128:(k4 + 1) * 128], func=AF.Copy, scale=gcols[:, gi, k:k + 1])

        class NS:
            pass

        def alloc_mixer(gt, nkt, nvt):
            M = NS()
            M.WQ = A([8, 512], BF16); M.WQr = AR("WQ")
            M.WTOK = A([8, 1024], BF16); M.WTOKr = AR("WTOK")
            M.WK = M.WTOK[:, :, 0:128]; M.WKr = M.WTOKr
            M.WMQ = A([8, 256], BF16); M.WMQr = AR("WMQ")
            M.WMK = M.WTOK[:, :, 256:512]; M.WMKr = M.WTOKr
            M.WOG = A([8, 512], BF16); M.WOGr = AR("WOG")
            M.WGT = A([8, 8], BF16); M.WGTr = AR("WGT")
            M.WOA = A([4, 1024], BF16); M.WOAr = AR("WOA")
            M.WOM = A([4, 1024], BF16); M.WOMr = AR("WOM")

            def wload(dst, res, src, **kw):
                P.dma("pool", dst, src, res, writes=[res], **kw)

            def wcols(a_, b_):
                return w_in_d[:, a_:b_].rearrange("(k p) n -> p k n", p=128)
            wload(M.WTOK[:, :, 0:256], M.WTOKr, wcols(512, 768), group=True)
            wload(M.WTOK[:, :, 256:1024], M.WTOKr, wcols(1024, 1792), group=True)
            wload(M.WGT[:], M.WGTr, wcols(2304, 2312), allow_slow_non_contiguous=True)
            wload(M.WQ[:], M.WQr, wcols(0, 512))
            wload(M.WMQ[:], M.WMQr, wcols(768, 1024))
            wload(M.WOG[:], M.WOGr, wcols(1792, 2304))
            wload(M.WOA[:], M.WOAr, w_out_d[0:512, :].rearrange("(c p) n -> p c n", p=128))
            wload(M.WOM[:], M.WOMr, w_out_d[512:1024, :].rearrange("(h p) n -> p h n", p=128))
            M.KT = A([2, nkt * 128], BF16, parts=64); M.KTr = [AR(f"KT{i}") for i in range(nkt)]
            M.Vt = A([nvt, 128], BF16); M.Vtr = [AR(f"Vt{i}") for i in range(nvt)]
            M.XNTg = A([8, gt * 128], BF16); M.XNTgr = [AR(f"XNTg{i}") for i in range(gt)]
            M.QT = A([8, gt * 128], BF16, parts=64); M.QTr = AR("QT")
            M.MQT = A([4, gt * 128], BF16, parts=64); M.MQTr = AR("MQT")
            M.MKT = A([4, gt * 128], BF16, parts=64); M.MKTr = AR("MKT")
            M.SGT = A([4, gt * 128], BF16); M.SGTr = AR("SGT")
            M.MKtok = A([gt, 256], BF16); M.MKtokr = [AR(f"MKtok{i}") for i in range(gt)]
            M.MVaug = A([gt, 4, 129], BF16); M.MVaugr = [AR(f"MVaug{i}") for i in range(gt)]
            M.ATTT = A([4, gt * 128], BF16); M.ATTTr = [AR(f"ATTT{i}") for i in range(gt)]
            M.HMT = A([4, gt * 128], BF16); M.HMTr = [AR(f"HMT{i}") for i in range(gt)]
            M.NG = gt * 128
            NG_ = M.NG
            M.G_IG = A([1, NG_ + 1], F32, parts=4)[:, 0, :]; M.G_E = A([1, NG_], F32, parts=4)[:, 0, :]
            M.G_L1 = A([1, NG_], F32, parts=4)[:, 0, :]; M.G_B = A([1, NG_ + 1], F32, parts=4)[:, 0, :]
            M.G_A = A([1, NG_], F32, parts=4)[:, 0, :]; M.G_M = A([1, NG_ + 1], F32, parts=4)[:, 0, :]
            M.G_BM = A([1, NG_], F32, parts=4)[:, 0, :]; M.G_DM = A([1, NG_], F32, parts=4)[:, 0, :]
            M.Gr = AR("G_IG"); M.G_Br = AR("G_B"); M.G_Ar = AR("G_A"); M.G_Mr = AR("G_M"); M.G_BMr = AR("G_BM"); M.G_DMr = AR("G_DM")
            M.SKV = A([1, 256], F32)[:, 0, :]; M.SKVr = AR("SKV")
            M.Ebuf = A([4, 256], BF16); M.Er = AR("E")
            M.PTs = A([1, 1024], BF16); M.PTsr = [AR("PTs0")] * 2
            M.sm_st = A([1, 32], F32)[:, 0, :]; M.smr = AR("sm_st")
            M.WKC = A([1, 8], F32)[:, 0, :]; M.WKCr = AR("WKC")
            M.DG = A([1, 8], F32, parts=4)[:, 0, :]; M.DGr = AR("DG")
            M.VW = A([4, 129], BF16); M.VWr = [AR(f"VW{h}") for h in range(4)]
            M.Cb = A([4, 257], BF16, parts=64); M.Cbr = [AR(f"Cb{h}") for h in range(4)]
            M.WT = A([4, 128], BF16); M.WTr = AR("WT")
            M.ST = A([4, 128], BF16); M.STr = AR("ST")
            M.WI = A([4, 128], BF16); M.WIr = AR("WI")
            M.QW = A([4, 128], BF16, parts=64); M.QWr = AR("QW")
            M.LOWB = A([4, 128], F32); M.LOWBr = AR("LOWB")
            M.T1 = A([4, 128], F32); M.T1r = AR("T1")
            M.T2 = A([4, 128], F32); M.T2r = AR("T2")
            M.USQ = A([4, 128], BF16); M.USQr = AR("USQ")
            for i in range(gt):
                I("dve", "memset", [], [M.MVaugr[i]], ap=M.MVaug[:, i, :, 128:129], constant=1.0)
            return M

        M = alloc_mixer(GT, NTP + 1, NTP + 1)
        I("dve", "memset", [], [M.G_Br], ap=M.G_B[:, 0:1], constant=0.0)
        I("dve", "memset", [], [M.G_Mr], ap=M.G_M[:, 0:1], constant=0.0)

        def tok_major(ti, xcols, xres, vslot, want_kv_out=None):
            b0, b0r = bank()
            b1, b1r = bank()
            for k in range(8):
                mm(b0[:, :], M.XNTg[:, k, xcols], M.WTOK[:, k, 0:512], k == 0, k == 7, [xres, M.WTOKr], b0r)
            for k in range(8):
                mm(b1[:, :], M.XNTg[:, k, xcols], M.WTOK[:, k, 512:1024], k == 0, k == 7, [xres, M.WTOKr], b1r)
            I("act", "activation", [b0r], [M.Vtr[vslot]], out=M.Vt[:, vslot, :], in_=b0[:, 128:256], func=AF.Copy)
            I("act", "activation", [b0r], [M.MKtokr[ti]], out=M.MKtok[:, ti, :], in_=b0[:, 256:512], func=AF.Copy, scale=0.125)
            I("dve", "tensor_copy", [b1r], [M.MVaugr[ti]], out=M.MVaug[:, ti, :, 0:128],
              in_=b1[:, :].rearrange("p (h d) -> p h d", h=4))
            if want_kv_out is not None:
                I("dve", "tensor_copy", [b0r], [M.SKVr], out=M.SKV[:, :], in_=b0[:, 0:256])
                if want_kv_out == "sample":
                    P.dma("sp", sks_o[:, 120:128, :], M.SKV[:, 0:128], M.SKVr, reads=[M.SKVr], group=True)
                    P.dma("sp", svs_o[:, 120:128, :], M.SKV[:, 128:256], M.SKVr, reads=[M.SKVr], group=True)
                else:
                    P.dma("sp", swak_o, M.SKV[:, 0:128], M.SKVr, reads=[M.SKVr], group=True)
                    P.dma("sp", swav_o, M.SKV[:, 128:256], M.SKVr, reads=[M.SKVr], group=True)

        def feat64(W, Wr, nh, dst, dres, ntok, xres, scale=None, dcol0=0):
            for h0 in range(0, nh, 2):
                bk, bkr = bank()
                for hh in range(2):
                    h = h0 + hh
                    for k in range(8):
                        mm(bk[0:64, hh * 256:hh * 256 + ntok], W[:, k, h * 64:(h + 1) * 64], M.XNTg[:, k, 0:ntok],
                           k == 0, k == 7, [Wr] + xres, bkr)
                src = bk[0:64, :].rearrange("p (a b) -> p a b", a=2)[:, :, 0:ntok]
                kw = {} if scale is None else {"scale": scale}
                I("act", "activation", [bkr], dres, out=dst[:, h0:h0 + 2, dcol0:dcol0 + ntok], in_=src, func=AF.Copy, **kw)

        def gates(ntok, xres, prefix):
            pg, pgr = bank()
            for k in range(8):
                mm(pg[0:4, 0:ntok], M.WGT[:, k, 0:4], M.XNTg[:, k, 0:ntok], k == 0, k == 7, [M.WGTr] + xres, pgr)
            for k in range(8):
                mm(pg[0:4, 256:256 + ntok], M.WGT[:, k, 4:8], M.XNTg[:, k, 0:ntok], k == 0, k == 7, [M.WGTr] + xres, pgr)
            I("act", "activation", [pgr, gb4r], [M.Gr], out=M.G_IG[:, 1:ntok + 1], in_=pg[0:4, 0:ntok], func=AF.Identity,
              bias=gb4[:, 0:1])
            I("act", "activation", [pgr, gb4r], [M.Gr], out=M.G_E[:, 0:ntok], in_=pg[0:4, 256:256 + ntok], func=AF.Exp,
              bias=gb4[:, 1:2], scale=-1.0)
            I("act", "activation", [M.Gr], [M.Gr], out=M.G_L1[:, 0:ntok], in_=M.G_E[:, 0:ntok], func=AF.Ln, bias=1.0)
            if prefix == "sample":
                return
            if prefix:
                I("dve", "tensor_scalar", [M.Gr, gb4r], [M.Gr], out=M.G_L1[:, 0:ntok], in0=M.G_L1[:, 0:ntok], scalar1=gb4[:, 2:3],
                  scalar2=None, op0=ALU.mult)
            I("dve", "tensor_tensor_scan", [M.Gr, M.G_Br, onesfr], [M.G_Br], out=M.G_B[:, 1:ntok + 1], data0=onesf[0:4, 0:ntok],
              data1=M.G_L1[:, 0:ntok], initial=M.G_B[:, 0:1], op0=ALU.mult, op1=ALU.subtract)
            I("dve", "scalar_tensor_tensor", [M.Gr, M.G_Br, gb4r], [M.G_Ar], out=M.G_A[:, 0:ntok], in0=M.G_IG[:, 1:ntok + 1],
              scalar=(gb4[:, 3:4] if prefix else 0.0), in1=M.G_B[:, 1:ntok + 1], op0=ALU.add, op1=ALU.subtract)
            I("dve", "tensor_tensor_scan", [M.G_Ar, M.G_Mr, onesfr], [M.G_Mr], out=M.G_M[:, 1:ntok + 1], data0=onesf[0:4, 0:ntok],
              data1=M.G_A[:, 0:ntok], initial=M.G_M[:, 0:1], op0=ALU.mult, op1=ALU.max)
            I("dve", "tensor_tensor", [M.G_Br, M.G_Mr], [M.G_BMr], out=M.G_BM[:, 0:ntok], in0=M.G_B[:, 1:ntok + 1],
              in1=M.G_M[:, 1:ntok + 1], op=ALU.add)
            for ci in range(ntok // 128):
                I("dve", "tensor_scalar", [M.G_Mr], [M.G_DMr], out=M.G_DM[:, ci * 128:(ci + 1) * 128],
                  in0=M.G_M[:, 1 + ci * 128:1 + (ci + 1) * 128], scalar1=M.G_M[:, ci * 128:ci * 128 + 1], scalar2=None,
                  op0=ALU.subtract)

        def gates_carry(ntok):
            I("dve", "tensor_copy", [M.G_Br], [M.G_Br], out=M.G_B[:, 0:1], in_=M.G_B[:, ntok:ntok + 1])
            I("dve", "tensor_copy", [M.G_Mr], [M.G_Mr], out=M.G_M[:, 0:1], in_=M.G_M[:, ntok:ntok + 1])

        def state_update(ti, c0, refresh_cb):
            pw, pwr = bank()
            I4 = SEL[:, 0:512].rearrange("p (h t) -> p h t", t=128)[:, :, 0]
            I("dve", "tensor_scalar", [selr, M.G_Mr], [M.DGr], out=M.DG[:, 0:4], in0=I4, scalar1=M.G_M[:, c0 + 128:c0 + 129],
              scalar2=-1.0, op0=ALU.mult, op1=ALU.mult)
            I("dve", "tensor_scalar", [selr, M.G_DMr], [M.DGr], out=M.DG[:, 4:8], in0=I4, scalar1=M.G_DM[:, c0 + 127:c0 + 128],
              scalar2=-1.0, op0=ALU.mult, op1=ALU.mult)
            mm(pw[:, 0:4], M.G_A[:, c0:c0 + 128], I4, True, False, [M.G_Ar, selr], pwr)
            mm(pw[:, 0:4], onesf[0:4, 0:128], M.DG[:, 0:4], False, True, [onesfr, M.DGr], pwr)
            mm(pw[:, 4:8], onesf[0:4, 0:128], M.DG[:, 4:8], True, True, [onesfr, M.DGr], pwr)
            I("act", "activation", [pwr], [M.WKCr], out=M.WKC[:, 0:8], in_=pw[:, 0:8], func=AF.Exp)
            for h in range(4):
                I("dve", "tensor_scalar", [M.MVaugr[ti], M.WKCr], [M.VWr[h]], out=M.VW[:, h, :], in0=M.MVaug[:, ti, h, :],
                  scalar1=M.WKC[:, h:h + 1], scalar2=None, op0=ALU.mult)
            for h0 in (0, 2):
                dc, dcr = bank()
                for hh in range(2):
                    h = h0 + hh
                    mm(dc[0:64, hh * 129:(hh + 1) * 129], M.MKtok[:, ti, h * 64:(h + 1) * 64], M.VW[:, h, :], True, True,
                       [M.MKtokr[ti], M.VWr[h]], dcr)
                for hh in range(2):
                    h = h0 + hh
                    I("dve", "scalar_tensor_tensor", [Cstr[h], M.WKCr, dcr], [Cstr[h]], out=Cst[:, h, :], in0=Cst[:, h, :],
                      scalar=M.WKC[0:64, 4 + h:5 + h], in1=dc[0:64, hh * 129:(hh + 1) * 129], op0=ALU.mult, op1=ALU.add)
            if refresh_cb:
                for h in range(4):
                    I("act", "activation", [Cstr[h]], [M.Cbr[h]], out=M.Cb[:, h, 0:129], in_=Cst[:, h, :], func=AF.Copy)
                    I("act", "activation", [Cstr[h]], [M.Cbr[h]], out=M.Cb[:, h, 129:257],
                      in_=Cst[:, h, 128:129].broadcast_to([64, 128]), func=AF.Copy)

        def mlstm_chunk(ti, c0, mbias, mbiasr, inter=True, inter_fn=None):
            cs = slice(c0, c0 + 128)
            pwt, pwtr = bank()
            for h in range(4):
                o = pwt[:, h * 128:(h + 1) * 128]
                mm(o, M.G_A[:, cs], SELh(h), True, False, [M.G_Ar, selr], pwtr)
                mm(o, NSELh(h), M.G_M[:, c0 + 1:c0 + 129], False, False, [M.G_Mr, selr], pwtr)
                mm(o, identb[:], mbias, False, True, [identr, mbiasr], pwtr)
            I("act", "activation", [pwtr], [M.WTr], out=M.WT[:, :, :], in_=pwt[:, :].rearrange("p (h t) -> p h t", h=4), func=AF.Exp)
            pqk, pqkr = bank()
            for h in range(4):
                mm(pqk[:, h * 128:(h + 1) * 128], M.MKT[:, h, cs], M.MQT[:, h, cs], True, True, [M.MKTr, M.MQTr], pqkr)
            I("dve", "tensor_tensor", [pqkr, M.WTr], [M.STr], out=M.ST[:, :, :], in0=pqk[:, :].rearrange("p (h t) -> p h t", h=4),
              in1=M.WT[:, :, :], op=ALU.mult)
            pwi, pwir = bank()
            for h in range(4):
                mm(pwi[:, h * 128:(h + 1) * 128], NSELh(h), M.G_DM[:, cs], True, True, [M.G_DMr, selr], pwir)
            I("act", "activation", [pwir], [M.WIr], out=M.WI[:, :, :], in_=pwi[:, :].rearrange("p (h t) -> p h t", h=4), func=AF.Exp)
            I("dve", "tensor_tensor", [M.MQTr, M.WIr], [M.QWr], out=M.QW[:, :, :], in0=M.MQT[:, :, cs], in1=M.WI[0:64, :, :], op=ALU.mult)
            plb, plbr = bank()
            for h in range(4):
                mm(plb[:, h * 128:(h + 1) * 128], NSELh(h), M.G_BM[:, cs], True, True, [M.G_BMr, selr], plbr)
            I("act", "activation", [plbr], [M.LOWBr], out=M.LOWB[:, :, :], in_=plb[:, :].rearrange("p (h t) -> p h t", h=4), func=AF.Exp)
            pnum, pnumr = banks[5], bres[5]
            pden, pdenr = banks[6], bres[6]
            if inter_fn is not None:
                inter_fn("pre")
            for h in range(4):
                o = pnum[:, h * 128:(h + 1) * 128]
                mm(o, M.MVaug[:, ti, h, 0:128], M.ST[:, h, :], True, False, [M.MVaugr[ti], M.STr], pnumr)
                if inter_fn is not None:
                    inter_fn("num", h, pnum, pnumr)
                else:
                    mm(o, M.Cb[:, h, 0:128], M.QW[:, h, :], False, True, [M.Cbr[h], M.QWr], pnumr)
            for h in range(4):
                o = pden[:, h * 128:(h + 1) * 128]
                mm(o, onesb[:], M.ST[:, h, :], True, False, [onesbr, M.STr], pdenr)
                if inter_fn is not None:
                    inter_fn("den", h, pden, pdenr)
                else:
                    mm(o, M.Cb[:, h, 129:257], M.QW[:, h, :], False, True, [M.Cbr[h], M.QWr], pdenr)
            return pnum, pnumr, pden, pdenr

        def mlstm_finish(pnum, pnumr, pden, pdenr, c0, hres):
            cs = slice(c0, c0 + 128)
            v4 = lambda b: b[:, :].rearrange("p (h t) -> p h t", h=4)
            I("act", "activation", [pdenr], [M.T1r], out=M.T1[:, :, :], in_=v4(pden), func=AF.Abs)
            I("dve", "tensor_tensor", [M.T1r, M.LOWBr], [M.T1r], out=M.T1[:, :, :], in0=M.T1[:, :, :], in1=M.LOWB[:, :, :], op=ALU.max)
            I("act", "activation", [M.T1r], [M.T1r], out=M.T1[:, :, :], in_=M.T1[:, :, :], func=AF.Square, scale=float(np.sqrt(EPS)))
            I("act", "activation", [pnumr], [M.USQr], out=M.USQ[:, :, :], in_=v4(pnum), func=AF.Square)
            pss, pssr = bank()
            mm(pss[:, :], onesb[:], M.USQ[:, :, :], True, True, [onesbr, M.USQr], pssr)
            I("dve", "scalar_tensor_tensor", [pssr, M.T1r], [M.T2r], out=M.T2[:, :, :], in0=v4(pss), scalar=1.0 / 128, in1=M.T1[:, :, :],
              op0=ALU.mult, op1=ALU.add)
            I("act", "activation", [M.T2r], [M.T2r], out=M.T2[:, :, :], in_=M.T2[:, :, :], func=AF.Ln)
            I("act", "activation", [M.T2r], [M.T2r], out=M.T2[:, :, :], in_=M.T2[:, :, :], func=AF.Exp, scale=-0.5)
            I("dve", "tensor_tensor", [pnumr, M.T2r], [M.T1r], out=M.T1[:, :, :], in0=v4(pnum), in1=M.T2[:, :, :], op=ALU.mult)
            for h in range(4):
                I("dve", "scalar_tensor_tensor", [M.T1r, gheadr, M.SGTr], [hres], out=M.HMT[:, h, cs], in0=M.T1[:, h, :],
                  scalar=gheadc[:, h:h + 1], in1=M.SGT[:, h, cs], op0=ALU.mult, op1=ALU.mult)

        def swa_tile(ti, kcol0, vslots, mb, mbr, ktres):
            qs = slice(ti * 128, (ti + 1) * 128)
            for h in range(2):
                bks = [bank(), bank()]
                for g in range(4):
                    bk, bkr = bks[g // 2]
                    o = bk[:, (g % 2) * 256:(g % 2 + 1) * 256]
                    mm(o, M.QT[:, 4 * h + g, qs], M.KT[:, h, kcol0:kcol0 + 256], True, False, [M.QTr] + ktres, bkr)
                    mm(o, identb[:], mb, False, True, [identr, mbr], bkr)
                for j in range(2):
                    I("dve", "reduce_max", [bks[j][1]], [M.smr], out=M.sm_st[:, 2 * j:2 * j + 2],
                      in_=bks[j][0][:, :].rearrange("p (a b) -> p a b", a=2), axis=AX.X)
                I("dve", "tensor_scalar", [M.smr], [M.smr], out=M.sm_st[:, 0:4], in0=M.sm_st[:, 0:4], scalar1=-0.125, scalar2=None,
                  op0=ALU.mult)
                I("dve", "tensor_tensor", [M.smr, sinkbr], [M.smr], out=M.sm_st[:, 0:4], in0=M.sm_st[:, 0:4],
                  in1=sinkb[:, 8 + 4 * h:12 + 4 * h], op=ALU.min)
                for g in range(4):
                    bk, bkr = bks[g // 2]
                    I("act", "activation", [bkr, M.smr], [M.Er, M.smr], out=M.Ebuf[:, g, :], in_=bk[:, (g % 2) * 256:(g % 2 + 1) * 256],
                      func=AF.Exp, bias=M.sm_st[:, g:g + 1], scale=0.125, accum_out=M.sm_st[:, 4 + g:5 + g])
                I("dve", "tensor_tensor", [M.smr, sinkbr], [M.smr], out=M.sm_st[:, 8:12], in0=M.sm_st[:, 0:4],
                  in1=sinkb[:, 4 * h:4 * h + 4], op=ALU.add)
                I("act", "activation", [M.smr], [M.smr], out=M.sm_st[:, 8:12], in_=M.sm_st[:, 8:12], func=AF.Exp)
                I("dve", "tensor_tensor", [M.smr], [M.smr], out=M.sm_st[:, 8:12], in0=M.sm_st[:, 8:12], in1=M.sm_st[:, 4:8], op=ALU.add)
                I("dve", "reciprocal", [M.smr], [M.smr], out=M.sm_st[:, 12:16], in_=M.sm_st[:, 8:12])
                for g in range(4):
                    if g % 2 == 0:
                        I("act", "activation", [M.Er, M.smr], [M.Er], out=M.Ebuf[:, g, :], in_=M.Ebuf[:, g, :], func=AF.Copy,
                          scale=M.sm_st[:, 12 + g:13 + g])
                    else:
                        I("dve", "tensor_scalar", [M.Er, M.smr], [M.Er], out=M.Ebuf[:, g, :], in0=M.Ebuf[:, g, :],
                          scalar1=M.sm_st[:, 12 + g:13 + g], scalar2=None, op0=ALU.mult)
                for kb in range(2):
                    for g in range(4):
                        blk = kb * 4 + (g % 2) * 2 + g // 2
                        I("pe", "transpose", [M.Er, identr], [*tbhr], out=tb[:, blk * 128:(blk + 1) * 128],
                          in_=M.Ebuf[:, g, kb * 128:(kb + 1) * 128], identity=identb[:])
                pb = 0
                if h == 0:
                    I("dve", "tensor_copy", [*tbhr], [M.PTsr[pb]], out=M.PTs[:, pb, :], in_=tb[:, :])
                else:
                    I("act", "activation", [*tbhr], [M.PTsr[pb]], out=M.PTs[:, pb, :], in_=tb[:, :], func=AF.Copy)
                po, por = bank()
                for par in range(2):
                    for kb in range(2):
                        mm(po[par * 64:(par + 1) * 64, 0:256], M.Vt[:, vslots[kb], h * 64:(h + 1) * 64],
                           M.PTs[:, pb, kb * 512 + par * 256:kb * 512 + (par + 1) * 256], kb == 0, kb == 1,
                           [M.Vtr[vslots[kb]], M.PTsr[pb]], por)
                I("act", "activation", [por], [M.ATTTr[ti]], out=M.ATTT[:, 2 * h:2 * h + 2, qs],
                  in_=po[:, 0:256].rearrange("p (g q) -> p g q", g=2), func=AF.Copy)

        def wout_tile(ti, t):
            qs = slice(ti * 128, (ti + 1) * 128)
            for c in range(2):
                bk, bkr = bank()
                cc = slice(c * 512, (c + 1) * 512)
                for hg in range(4):
                    mm(bk[:, :], M.ATTT[:, hg, qs], M.WOA[:, hg, cc], hg == 0, False, [M.ATTTr[ti], M.WOAr], bkr)
                for h in range(4):
                    mm(bk[:, :], M.HMT[:, h, qs], M.WOM[:, h, cc], False, h == 3, [M.HMTr[ti], M.WOMr], bkr)
                I("dve", "tensor_tensor", [Yr[t], bkr], [Yr[t]], out=Y[:, t, cc], in0=Y[:, t, cc], in1=bk[:, :], op=ALU.add)

        xpre_t = xpre_d.rearrange("(t p) d -> t p d", p=128)
        xp_t = xp_d.rearrange("(t p) d -> t p d", p=128)
        for t in range(NTP):
            P.dma("sp", Y[:, t, :], xpre_t[t], Yr[t], writes=[Yr[t]])
        for g0 in range(0, NTP, GT):
            for ti in range(GT):
                t = g0 + ti
                norm_T(Y[:, t, :], Yr[t], 0, M.XNTg[:, :, ti * 128:(ti + 1) * 128], [M.XNTgr[ti]])
                P.dma("sp", Y[:, t, :], xp_t[t], Yr[t], writes=[Yr[t]])
            for ti in range(GT):
                tok_major(ti, slice(ti * 128, (ti + 1) * 128), M.XNTgr[ti], 0)
            gates(GT * 128, M.XNTgr, True)
            if g0 + GT == NTP:
                bk, bkr = bank()
                for h in range(2):
                    for k in range(8):
                        mm(bk[0:64, h * 128:(h + 1) * 128], M.WK[:, k, h * 64:(h + 1) * 64], M.XNTg[:, k, (GT - 1) * 128:GT * 128],
                           k == 0, k == 7, [M.WKr, M.XNTgr[GT - 1]], bkr)
                I("act", "activation", [bkr], [M.KTr[0]], out=M.KT[:, :, 0:128],
                  in_=bk[0:64, 0:256].rearrange("p (a b) -> p a b", a=2), func=AF.Copy)
            for ti in range(GT):
                last = (g0 + ti == NTP - 1)
                state_update(ti, ti * 128, last)
            gates_carry(GT * 128)

        for g0 in range(0, NTP, GT):
            strs = []
            for ti in range(GT):
                t = g0 + ti
                P.rec_begin()
                norm_T(Y[:, t, :], Yr[t], 0, M.XNTg[:, :, ti * 128:(ti + 1) * 128], [M.XNTgr[ti]], half=ti)
                strs.append(P.rec_end())
            P.merge(strs)
            for ti in range(GT):
                t = g0 + ti
                tok_major(ti, slice(ti * 128, (ti + 1) * 128), M.XNTgr[ti], 1 + t, want_kv_out=(True if t == NTP - 1 else None))
            gates(M.NG, M.XNTgr, False)
            feat64(M.WK, M.WKr, 2, M.KT, [M.KTr[1 + g0 + i] for i in range(GT)], M.NG, M.XNTgr, dcol0=128 + g0 * 128)
            feat64(M.WQ, M.WQr, 8, M.QT, [M.QTr], M.NG, M.XNTgr)
            feat64(M.WMQ, M.WMQr, 4, M.MQT, [M.MQTr], M.NG, M.XNTgr)
            feat64(M.WMK, M.WMKr, 4, M.MKT, [M.MKTr], M.NG, M.XNTgr, scale=0.125)
            for h0 in (0, 2):
                bk, bkr = bank()
                for hh in range(2):
                    h = h0 + hh
                    for k in range(8):
                        mm(bk[:, hh * 256:hh * 256 + M.NG], M.WOG[:, k, h * 128:(h + 1) * 128], M.XNTg[:, k, 0:M.NG], k == 0, k == 7,
                           [M.WOGr] + M.XNTgr, bkr)
                sgv = M.SGT[:, h0:h0 + 2, :]
                I("act", "activation", [bkr], [M.SGTr], out=sgv, in_=bk[:, :].rearrange("p (a b) -> p a b", a=2)[:, :, 0:M.NG],
                  func=AF.Exp, scale=-1.0)
                I("act", "activation", [M.SGTr], [M.SGTr], out=sgv, in_=sgv, func=AF.Ln, bias=1.0)
                I("act", "activation", [M.SGTr], [M.SGTr], out=sgv, in_=sgv, func=AF.Exp, scale=-1.0)
            pend = None
            for ti in range(GT):
                t = g0 + ti
                P.rec_begin(); bset[0] = [2, 3]
                pn = mlstm_chunk(ti, ti * 128, mbcaus[:], mbcausr)
                mlstm_finish(*pn, ti * 128, M.HMTr[ti])
                state_update(ti, ti * 128, True)
                s_ml = P.rec_end()
                P.rec_begin(); bset[0] = [0, 1]
                swa_tile(ti, t * 128, (t, t + 1), (mbfirst[:] if t == 0 else mbband[:]), (mbfirstr if t == 0 else mbbandr),
                         [M.KTr[t], M.KTr[t + 1]])
                s_sw = P.rec_end()
                strs = [s_ml, s_sw]
                if pend is not None:
                    P.rec_begin(); bset[0] = [4]
                    wout_tile(*pend)
                    strs.append(P.rec_end())
                P.merge(strs)
                pend = (ti, t)
            bset[0] = [0, 1, 2, 3, 4]
            wout_tile(*pend)
            if debug and g0 == DBG_G0:
                dA = dout("dbg_att", [64, 8, M.NG]); dH = dout("dbg_hm", [128, 4, M.NG])
                P.dma("pool", dA, M.ATTT[:, :, :], M.ATTTr[0], reads=M.ATTTr)
                P.dma("pool", dH, M.HMT[:, :, :], M.HMTr[0], reads=M.HMTr)
            gates_carry(M.NG)

        CO = A([4, 64], F32); COr = AR("CO")
        for h in range(4):
            bk, bkr = bank()
            mm(bk[:, 0:64], Cst[:, h, 0:128], identf[0:64, 0:64], True, True, [Cstr[h], identfr], bkr)
            I("act", "activation", [bkr], [COr], out=CO[:, h, :], in_=bk[:, 0:64], func=AF.Copy)
        P.dma("sp", Cp_o.rearrange("h p k -> p h k"), CO[:, :, :], COr, reads=[COr])
        for h in range(4):
            P.dma("sp", np_o[h, :].rearrange("(k o) -> k o", o=1), Cst[:, h, 128:129], Cstr[h], reads=[Cstr[h]], allow_slow_non_contiguous=True)
        P.dma("sp", mp_o, M.G_BM[:, M.NG - 1:M.NG], M.G_BMr, reads=[M.G_BMr], allow_slow_non_contiguous=True)

        new_phase()
        MA = M
        M = alloc_mixer(1, 1, 1)
        R0_olds = [M.WQr, M.WTOKr, M.WMQr, M.WOGr, M.WGTr]
        TS = NTP
        P.dma("sp", Y[:, TS, :], xs_d, Yr[TS], writes=[Yr[TS]])
        shk = P.res("shk"); shv = P.res("shv")
        P.dma("sp", sks_o[:, 0:120, :], csk_d[:, 8:128, :], shk, writes=[shk])
        P.dma("sp", svs_o[:, 0:120, :], csv_d[:, 8:128, :], shv, writes=[shv])
        CKn = A([16, 128], BF16); CKnr = AR("CKn")
        CV = A([16, 128], BF16); CVr = AR("CV")
        CKT = A([16, 128], BF16, parts=64); CKTr = AR("CKT")
        SMC = A([1, 128], BF16, parts=32)[:, 0, :]; SMCr = AR("SMC")
        SMN = A([16, 128], BF16, parts=32); SMNr = AR("SMN")
        SINKC = A([1, 4], F32, parts=32)[:, 0, :]; SINKCr = AR("SINKC")
        mbcs = A([1, 128], BF16)[:, 0, :]; mbcsr = AR("mbcs")
        PNs = A([4, 256], BF16, parts=32); PNsr = [AR(f"PNs{i}") for i in range(4)]
        sms = A([4, 8], F32, parts=32); smsr = [AR(f"sms{i}") for i in range(4)]
        PTS = A([1, 1024], BF16)[:, 0, :]; PTSr = AR("PTS")
        M0 = A([1, 16], F32, parts=4)[:, 0, :]; M0r = AR("M0")
        MTe = A([1, 128], F32, parts=4)[:, 0, :]; MTer = AR("MTe")
        DMT = A([1, 16], F32, parts=4)[:, 0, :]; DMTr = AR("DMT")
        E16 = A([1, 16], F32)[:, 0, :]; E16r = AR("E16")
        EW = A([4, 16], BF16); EWr = AR("EW")
        WCB = A([4, 16], F32); WCBr = AR("WCB")
        SNn = A([1, 64], F32, parts=64)[:, 0, :]; SNnr = AR("SNn")
        SNT = A([1, 64], F32, parts=64)[:, 0, :]; SNTr = AR("SNT")
        NNT = A([1, 64], F32, parts=64)[:, 0, :]; NNTr = AR("NNT")
        NNo = A([1, 64], F32, parts=64)[:, 0, :]; NNor = AR("NNo")
        BTf = A([1, 128], F32)[:, 0, :]; BTfr = AR("BTf")
        P.dma("pool", CKn[:, :, :], csk_d.rearrange("j p c -> p j c"), CKnr, writes=[CKnr])
        P.dma("pool", CV[:, :, :], csv_d.rearrange("j p c -> p j c"), CVr, writes=[CVr])
        P.dma("pool", SMC, smc_d, SMCr, writes=[SMCr])
        P.dma("pool", SMN[:, :, :], smn_d, SMNr, writes=[SMNr])
        P.dma("sp", SINKC[:, 0:2], sinkcol_d, SINKCr, writes=[SINKCr])
        I("dve", "tensor_scalar", [SINKCr], [SINKCr], out=SINKC[:, 2:4], in0=SINKC[:, 0:2], scalar1=-1.0, scalar2=None, op0=ALU.mult)
        P.dma("pool", mbcs, mb_causs_d, mbcsr, writes=[mbcsr])
        P.dma("sp", M0, sm_d.rearrange("j h -> h j"), M0r, writes=[M0r], allow_slow_non_contiguous=True)
        P.dma("sp", E16, eseq_d, E16r, writes=[E16r])
        P.dma("sp", SNn, sn_d.rearrange("j h k -> (j h) k"), SNnr, writes=[SNnr])

        norm_T(Y[:, TS, :], Yr[TS], 0, M.XNTg[:, :, 0:128], [M.XNTgr[0]])
        tok_major(0, slice(0, 128), M.XNTgr[0], 0, want_kv_out="sample")
        gates(128, M.XNTgr, "sample")
        feat64(M.WK, M.WKr, 2, M.KT, [M.KTr[0]], 128, M.XNTgr, dcol0=0)
        feat64(M.WQ, M.WQr, 8, M.QT, [M.QTr], 128, M.XNTgr)
        feat64(M.WMQ, M.WMQr, 4, M.MQT, [M.MQTr], 128, M.XNTgr)
        feat64(M.WMK, M.WMKr, 4, M.MKT, [M.MKTr], 128, M.XNTgr, scale=0.125)
        for h0 in (0, 2):
            bk, bkr = bank()
            for hh in range(2):
                h = h0 + hh
                for k in range(8):
                    mm(bk[:, hh * 256:hh * 256 + 128], M.WOG[:, k, h * 128:(h + 1) * 128], M.XNTg[:, k, 0:128], k == 0, k == 7,
                       [M.WOGr] + M.XNTgr, bkr)
            sgv = M.SGT[:, h0:h0 + 2, :]
            I("act", "activation", [bkr], [M.SGTr], out=sgv, in_=bk[:, :].rearrange("p (a b) -> p a b", a=2)[:, :, 0:128],
              func=AF.Exp, scale=-1.0)
            I("act", "activation", [M.SGTr], [M.SGTr], out=sgv, in_=sgv, func=AF.Ln, bias=1.0)
            I("act", "activation", [M.SGTr], [M.SGTr], out=sgv, in_=sgv, func=AF.Exp, scale=-1.0)
        for j in range(16):
            I("dve", "tensor_tensor_scan", [M.Gr, M.G_Br, onesfr], [M.G_Br], out=M.G_B[:, 1 + 8 * j:9 + 8 * j],
              data0=onesf[0:4, 0:8], data1=M.G_L1[:, 8 * j:8 * j + 8], initial=0.0, op0=ALU.mult, op1=ALU.subtract)
        I("dve", "tensor_tensor", [M.Gr, M.G_Br], [M.G_Ar], out=M.G_A[:, 0:128], in0=M.G_IG[:, 1:129], in1=M.G_B[:, 1:129],
          op=ALU.subtract)
        for j in range(16):
            I("dve", "tensor_tensor_scan", [M.G_Ar, M.G_Mr, onesfr, M0r], [M.G_Mr], out=M.G_M[:, 1 + 8 * j:9 + 8 * j],
              data0=onesf[0:4, 0:8], data1=M.G_A[:, 8 * j:8 * j + 8], initial=M0[:, j:j + 1], op0=ALU.mult, op1=ALU.max)
        I("dve", "tensor_tensor", [M.G_Br, M.G_Mr], [M.G_BMr], out=M.G_BM[:, 0:128], in0=M.G_B[:, 1:129], in1=M.G_M[:, 1:129],
          op=ALU.add)
        GM3 = M.G_M[:, 1:129].rearrange("p (j i) -> p j i", i=8)
        I("dve", "tensor_tensor", [M.G_Mr, M0r], [M.G_DMr], out=M.G_DM[:, 0:128].rearrange("p (j i) -> p j i", i=8), in0=GM3,
          in1=M0[:, :].unsqueeze(2).broadcast_to([4, 16, 8]), op=ALU.subtract)
        I("dve", "tensor_copy", [M.G_Mr], [MTer], out=MTe[:, :].rearrange("p (j i) -> p j i", i=8),
          in_=GM3[:, :, 7:8].broadcast_to([4, 16, 8]))
        I("dve", "tensor_tensor", [M.G_Mr, M0r], [DMTr], out=DMT[:, :].unsqueeze(2), in0=GM3[:, :, 7:8], in1=M0[:, :].unsqueeze(2),
          op=ALU.subtract)
        P.dma("sp", ms_o.rearrange("j h -> h j"), M.G_BM[:, 0:128].rearrange("p (j i) -> p j i", i=8)[:, :, 7], M.G_BMr,
              reads=[M.G_BMr], allow_slow_non_contiguous=True)

        pair_i = [0]
        QS = A([2, 16, 32], BF16, parts=64); QSr = AR("QS")
        for h in range(2):
            for par in range(2):
                I("act", "activation", [M.QTr], [QSr],
                  out=QS[:, h, :, par * 16:(par + 1) * 16].rearrange("p j (gp i) -> p j gp i", i=8),
                  in_=M.QT[:, 4 * h:4 * h + 4, :].rearrange("p (gp two) t -> p two gp t", two=2)[:, par].rearrange(
                      "p gp (j i) -> p j gp i", i=8), func=AF.Copy)
        for h in range(2):
            for q4 in range(2):
                for jj in range(8):
                    j = q4 * 8 + jj
                    I("pe", "transpose", [CKnr, identr], [*tbhr], out=tb[0:64, jj * 128:(jj + 1) * 128],
                      in_=CKn[:, j, h * 64:(h + 1) * 64], identity=identb[:])
                I("act", "activation", [*tbhr], [CKTr], out=CKT[:, q4 * 8:(q4 + 1) * 8, :],
                  in_=tb[0:64, :].rearrange("p (a b) -> p a b", a=8), func=AF.Copy)
            for half in range(2):
                po, por = banks[4], bres[4]
                strs = []
                for sk in range(4):
                    P.rec_begin(); bset[0] = [sk]
                    for jj in (sk, sk + 4):
                        j = half * 8 + jj
                        b = sk
                        bk, bkr = bank()
                        lq = QS[:, h, j, :]
                        mm(bk[0:32, 0:128], lq, CKT[:, j, :], True, False, [QSr, CKTr], bkr)
                        mm(bk[0:32, 0:128], identb[0:32, 0:32], SMC, False, True, [identr, SMCr], bkr)
                        mm(bk[0:32, 128:256], lq, M.KT[:, h, 0:128], True, False, [QSr, M.KTr[0]], bkr)
                        mm(bk[0:32, 128:256], identb[0:32, 0:32], SMN[:, j, :], False, True, [identr, SMNr], bkr)
                        st_ = sms[:, b, :]
                        I("dve", "reduce_max", [bkr], [smsr[b]], out=st_[:, 0:1], in_=bk[0:32, 0:256], axis=AX.X)
                        I("dve", "tensor_scalar", [smsr[b]], [smsr[b]], out=st_[:, 0:1], in0=st_[:, 0:1], scalar1=-0.125,
                          scalar2=None, op0=ALU.mult)
                        I("dve", "tensor_tensor", [smsr[b], SINKCr], [smsr[b]], out=st_[:, 0:1], in0=st_[:, 0:1],
                          in1=SINKC[:, 2 + h:3 + h], op=ALU.min)
                        I("act", "activation", [bkr, smsr[b]], [PNsr[b], smsr[b]], out=PNs[:, b, :], in_=bk[0:32, 0:256],
                          func=AF.Exp, bias=st_[:, 0:1], scale=0.125, accum_out=st_[:, 1:2])
                        I("act", "activation", [SINKCr, smsr[b]], [smsr[b]], out=st_[:, 2:3], in_=SINKC[:, h:h + 1], func=AF.Exp,
                          bias=st_[:, 0:1])
                        I("dve", "tensor_tensor", [smsr[b]], [smsr[b]], out=st_[:, 2:3], in0=st_[:, 2:3], in1=st_[:, 1:2],
                          op=ALU.add)
                        I("dve", "reciprocal", [smsr[b]], [smsr[b]], out=st_[:, 3:4], in_=st_[:, 2:3])
                        I("dve", "tensor_scalar", [PNsr[b], smsr[b]], [PNsr[b]], out=PNs[:, b, :], in0=PNs[:, b, :],
                          scalar1=st_[:, 3:4], scalar2=None, op0=ALU.mult)
                        for c2 in range(2):
                            I("pe", "transpose", [PNsr[b], identr], [*tbhr], out=tb[:, jj * 64 + c2 * 32:jj * 64 + (c2 + 1) * 32],
                              in_=PNs[:, b, c2 * 128:(c2 + 1) * 128], identity=identb[0:32, 0:32])
                    strs.append(P.rec_end())
                P.merge(strs)
                bset[0] = [0, 1, 2, 3]
                I("dve", "tensor_copy", [*tbhr], [PTSr], out=PTS[:, 0:512], in_=tb[:, 0:512])
                for jj in range(8):
                    j = half * 8 + jj
                    for par in range(2):
                        o = po[par * 64:(par + 1) * 64, jj * 16:(jj + 1) * 16]
                        mm(o, CV[:, j, h * 64:(h + 1) * 64], PTS[:, jj * 64 + par * 16:jj * 64 + par * 16 + 16], True, False,
                           [CVr, PTSr], por)
                        mm(o, M.Vt[:, 0, h * 64:(h + 1) * 64], PTS[:, jj * 64 + 32 + par * 16:jj * 64 + 32 + par * 16 + 16], False,
                           True, [M.Vtr[0], PTSr], por)
                I("act", "activation", [por], [M.ATTTr[0]],
                  out=M.ATTT[:, 2 * h:2 * h + 2, half * 64:(half + 1) * 64].rearrange("p c (j i) -> p j c i", i=8),
                  in_=po[:, 0:128].rearrange("p (j c i) -> p j c i", c=2, i=8), func=AF.Copy)

        bset[0] = [0, 1, 2, 3, 4]
        off = 0
        SCf, off = A_at(off, [64, 64], F32); SCfr = ARalias("SCf", R0_olds)
        SCT, off = A_at(off, [64, 128], BF16, parts=64); SCTr = ARalias("SCT", R0_olds)
        QN, off = A_at(off, [4, 128], BF16, parts=64); QNr = ARalias("QN", R0_olds)
        KJ, off = A_at(off, [16, 64], BF16); KJr = ARalias("KJ", R0_olds)
        assert off <= 18496
        P.dma("sp", SCf[:, :, :], sC_d.rearrange("j h p k -> p (j h) k"), SCfr, writes=[SCfr])
        for p4 in range(16):
            bk, bkr = bank()
            for q_ in range(4):
                pr = p4 * 4 + q_
                mm(bk[0:64, q_ * 128:(q_ + 1) * 128], SCf[:, pr, :], identf[:], True, True, [SCfr, identfr], bkr)
            I("act", "activation", [bkr], [SCTr], out=SCT[:, p4 * 4:(p4 + 1) * 4, :],
              in_=bk[0:64, :].rearrange("p (a b) -> p a b", a=4), func=AF.Copy)
        bk, bkr = bank()
        mm(bk[0:64, 0:64], SNn, identf[0:64, 0:64], True, True, [SNnr, identfr], bkr)
        I("act", "activation", [bkr], [SNTr], out=SNT, in_=bk[0:64, 0:64], func=AF.Copy)

        def sample_inter(kind, h=None, pb=None, pbr=None):
            if kind == "pre":
                I("dve", "tensor_tensor", [M.QWr, SNTr], [QNr], out=QN[:, :, :].rearrange("p h (j i) -> p h j i", i=8),
                  in0=M.QW[:, :, :].rearrange("p h (j i) -> p h j i", i=8),
                  in1=SNT.rearrange("p (j h) -> p h j", h=4).unsqueeze(3).broadcast_to([64, 4, 16, 8]), op=ALU.mult)
            elif kind == "num":
                for j in range(16):
                    mm(pb[:, h * 128 + 8 * j:h * 128 + 8 * j + 8], SCT[:, j * 4 + h, :], M.QW[:, h, 8 * j:8 * j + 8], False, j == 15,
                       [SCTr, M.QWr], pbr)
            else:
                mm(pb[:, h * 128:(h + 1) * 128], onesb[0:64, :], QN[:, h, :], False, True, [onesbr, QNr], pbr)

        pn = mlstm_chunk(0, 0, mbcs, mbcsr, inter_fn=sample_inter)
        mlstm_finish(*pn, 0, M.HMTr[0])
        wout_tile(0, TS)

        pw, pwr = bank()
        for h in range(4):
            mm(pw[:, h:h + 1], M.G_A[:, 0:128], SELh(h, 1), True, False, [M.G_Ar, selr], pwr)
            mm(pw[:, h:h + 1], MTe, SEL[:, 512 + h * 128:512 + h * 128 + 1], False, True, [MTer, selr], pwr)
        for h in range(4):
            mm(pw[:, 8 + 16 * h:8 + 16 * (h + 1)], NSELh(h), DMT, True, True, [DMTr, selr], pwr)
        I("act", "activation", [pwr], [M.WKCr], out=M.WKC[:, 0:4], in_=pw[:, 0:4], func=AF.Exp)
        I("act", "activation", [pwr], [WCBr], out=WCB[:, :, :], in_=pw[:, 8:72].rearrange("p (h j) -> p h j", h=4), func=AF.Exp)
        for h in range(4):
            I("dve", "tensor_scalar", [M.MVaugr[0], M.WKCr], [M.VWr[h]], out=M.VW[:, h, :], in0=M.MVaug[:, 0, h, :],
              scalar1=M.WKC[:, h:h + 1], scalar2=None, op0=ALU.mult)
            I("dve", "tensor_scalar", [E16r, M.WKCr], [EWr], out=EW[:, h, :], in0=E16, scalar1=M.WKC[:, h:h + 1], scalar2=None,
              op0=ALU.mult)
        bk, bkr = bank()
        for h in range(4):
            mm(bk[0:64, h * 16:(h + 1) * 16], M.MKtok[:, 0, h * 64:(h + 1) * 64], EW[:, h, :], True, True, [M.MKtokr[0], EWr], bkr)
        I("dve", "tensor_tensor", [SNTr, WCBr], [NNTr], out=NNT.rearrange("p (j h) -> p h j", h=4),
          in0=SNT.rearrange("p (j h) -> p h j", h=4), in1=WCB[0:64, :, :], op=ALU.mult)
        I("dve", "tensor_tensor", [NNTr, bkr], [NNTr], out=NNT.rearrange("p (j h) -> p h j", h=4),
          in0=NNT.rearrange("p (j h) -> p h j", h=4), in1=bk[0:64, 0:64].rearrange("p (h j) -> p h j", h=4), op=ALU.add)
        bk2, bk2r = bank()
        mm(bk2[0:64, 0:64], NNT, identf[0:64, 0:64], True, True, [NNTr, identfr], bk2r)
        I("act", "activation", [bk2r], [NNor], out=NNo, in_=bk2[0:64, 0:64], func=AF.Copy)
        P.dma("sp", ns_o.rearrange("j h k -> (j h) k"), NNo, NNor, reads=[NNor])
        for h in range(4):
            I("dve", "tensor_tensor", [M.MKtokr[0], E16r], [KJr], out=KJ[:, :, :],
              in0=M.MKtok[:, 0, h * 64:(h + 1) * 64].unsqueeze(1).broadcast_to([128, 16, 64]),
              in1=E16.unsqueeze(2).broadcast_to([128, 16, 64]), op=ALU.mult)
            for half in range(2):
                bk, bkr = bank()
                mm(bk[:, :], M.VW[:, h, 0:128], KJ[:, half * 8:(half + 1) * 8, :], True, True, [M.VWr[h], KJr], bkr)
                scv = SCf[:, :, :].rearrange("p (j h) k -> p h j k", h=4)[:, h, half * 8:(half + 1) * 8, :]
                I("dve", "tensor_tensor", [SCfr, WCBr], [SCfr], out=scv, in0=scv,
                  in1=WCB[:, h, half * 8:(half + 1) * 8].unsqueeze(2).broadcast_to([128, 8, 64]), op=ALU.mult)
                I("dve", "tensor_tensor", [SCfr, bkr], [SCfr], out=scv, in0=scv,
                  in1=bk[:, :].rearrange("p (j k) -> p j k", k=64), op=ALU.add)
        P.dma("sp", Cs_o.rearrange("j h p k -> p (j h) k"), SCf[:, :, :], SCfr, reads=[SCfr])

        def dump_y(tiles):
            yo = yp_o.rearrange("(t p) d -> t p d", p=128)
            for t in tiles:
                if Yr[t].last_w is None:
                    continue
                if t < NTP:
                    P.dma("sp", yo[t], Y[:, t, :], Yr[t], reads=[Yr[t]])
                else:
                    P.dma("sp", ys_o, Y[:, t, :], Yr[t], reads=[Yr[t]])

        if stage <= 1:
            dump_y(range(NT))
            P.emit()
            return nc, P

        new_phase()
        WCQ = A([8, 256], BF16); WCQr = AR("WCQ")
        WCKV = A([8, 512], BF16); WCKVr = AR("WCKV")
        WCO = A([2, 1024], BF16); WCOr = AR("WCO")
        P.dma("pool", WCQ[:], w_cq_d.rearrange("(k p) n -> p k n", p=128), WCQr, writes=[WCQr])
        P.dma("pool", WCKV[:, :, 0:256], w_ck_d.rearrange("(k p) n -> p k n", p=128), WCKVr, writes=[WCKVr], group=True)
        P.dma("pool", WCKV[:, :, 256:512], w_cv_d.rearrange("(k p) n -> p k n", p=128), WCKVr, writes=[WCKVr], group=True)
        P.dma("pool", WCO[:], w_co_d.rearrange("(c p) n -> p c n", p=128), WCOr, writes=[WCOr])
        MEMX = A([2, D], F32); MEMXr = [AR("MEMX0"), AR("MEMX1")]
        MNT = A([8, 256], BF16); MNTr = [AR("MNT0"), AR("MNT1")]
        MKTm = A([4, 256], BF16, parts=64); MKTmr = AR("MKTm")
        MVm = A([2, 256], BF16); MVmr = AR("MVm")
        MKVo = A([2, 512], F32); MKVor = [AR("MKVo0"), AR("MKVo1")]
        GB = 4
        XNTb = A([8, GB * 128], BF16); XNTbr = [AR(f"XNTb{i}") for i in range(GB)]
        QcT = A([4, GB * 128], BF16, parts=64); QcTr = AR("QcT")
        OcT = A([2, GB * 128], BF16); OcTr = [AR(f"OcT{i}") for i in range(GB)]
        Eb2s = [A([4, 256], BF16) for _ in range(2)]; Eb2rs = [AR("Eb2a"), AR("Eb2b")]
        PT2s = [A([1, 1024], BF16) for _ in range(2)]; PT2rs = [AR("PT2a"), AR("PT2b")]
        sm2s = [A([1, 32], F32)[:, 0, :] for _ in range(2)]; sm2rs = [AR("sm2a"), AR("sm2b")]

        mem_t = mem_d.rearrange("(t p) d -> t p d", p=128)
        import os
        SK = os.environ.get("SKIP", "")
        for mt in range(2):
            P.dma("sp", MEMX[:, mt, :], mem_t[mt], MEMXr[mt], writes=[MEMXr[mt]])
        for mt in (range(2) if "noBnorm" not in SK else []):
            norm_T(MEMX[:, mt, :], MEMXr[mt], 2, MNT[:, :, mt * 128:(mt + 1) * 128], [MNTr[mt]])
        for mt in (range(2) if "noBkv" not in SK else []):
            bk, bkr = bank()
            for k in range(8):
                mm(bk[:, :], MNT[:, k, mt * 128:(mt + 1) * 128], WCKV[:, k, :], k == 0, k == 7, [MNTr[mt], WCKVr], bkr)
            if "noBcp1" not in SK:
                I("act", "activation", [bkr], [MKVor[mt]], out=MKVo[:, mt, :], in_=bk[:, :], func=AF.Copy)
            if "noBcp2" not in SK:
                I("act", "activation", [bkr], [MVmr], out=MVm[:, mt, :], in_=bk[:, 256:512], func=AF.Copy)
            if "noBdma" not in SK:
                P.dma("sp", memk_o[mt * 128:(mt + 1) * 128, :], MKVo[:, mt, 0:256], MKVor[mt], reads=[MKVor[mt]], group=True)
                P.dma("sp", memv_o[mt * 128:(mt + 1) * 128, :], MKVo[:, mt, 256:512], MKVor[mt], reads=[MKVor[mt]], group=True)
        for h0 in ((0, 2) if "noBkt" not in SK else []):
            bk, bkr = bank()
            for hh in range(2):
                h = h0 + hh
                for k in range(8):
                    mm(bk[0:64, hh * 256:(hh + 1) * 256], WCKV[:, k, h * 64:(h + 1) * 64], MNT[:, k, :], k == 0, k == 7,
                       [WCKVr] + MNTr, bkr)
            I("act", "activation", [bkr], [MKTmr], out=MKTm[:, h0:h0 + 2, :],
              in_=bk[0:64, :].rearrange("p (a b) -> p a b", a=2), func=AF.Copy)

        def cross_q(ntok, xres):
            for h in range(4):
                bk, bkr = bank()
                for k in range(8):
                    mm(bk[0:64, 0:ntok], WCQ[:, k, h * 64:(h + 1) * 64], XNTb[:, k, 0:ntok], k == 0, k == 7, [WCQr] + xres, bkr)
                I("act", "activation", [bkr], [QcTr], out=QcT[:, h, 0:ntok], in_=bk[0:64, 0:ntok], func=AF.Copy)

        def cross_tile_prompt(ti, sx):
            Eb2, Eb2r, PT2, PT2r, sm2, sm2r, tbx, tbxr = Eb2s[sx], Eb2rs[sx], PT2s[sx], PT2rs[sx], sm2s[sx], sm2rs[sx], tbh[sx], tbhr[sx]
            qs = slice(ti * 128, (ti + 1) * 128)
            bks = [bank(), bank()]
            for h in range(4):
                bk, bkr = bks[h // 2]
                mm(bk[:, (h % 2) * 256:(h % 2 + 1) * 256], QcT[:, h, qs], MKTm[:, h, :], True, True, [QcTr, MKTmr], bkr)
            for j in range(2):
                I("dve", "reduce_max", [bks[j][1]], [sm2r], out=sm2[:, 2 * j:2 * j + 2],
                  in_=bks[j][0][:, :].rearrange("p (a b) -> p a b", a=2), axis=AX.X)
            I("dve", "tensor_scalar", [sm2r], [sm2r], out=sm2[:, 0:4], in0=sm2[:, 0:4], scalar1=-0.125, scalar2=None, op0=ALU.mult)
            for h in range(4):
                bk, bkr = bks[h // 2]
                I("act", "activation", [bkr, sm2r], [Eb2r, sm2r], out=Eb2[:, h, :], in_=bk[:, (h % 2) * 256:(h % 2 + 1) * 256],
                  func=AF.Exp, bias=sm2[:, h:h + 1], scale=0.125, accum_out=sm2[:, 4 + h:5 + h])
            I("dve", "reciprocal", [sm2r], [sm2r], out=sm2[:, 8:12], in_=sm2[:, 4:8])
            for h in range(4):
                if h % 2 == 0:
                    I("act", "activation", [Eb2r, sm2r], [Eb2r], out=Eb2[:, h, :], in_=Eb2[:, h, :], func=AF.Copy,
                      scale=sm2[:, 8 + h:9 + h])
                else:
                    I("dve", "tensor_scalar", [Eb2r, sm2r], [Eb2r], out=Eb2[:, h, :], in0=Eb2[:, h, :],
                      scalar1=sm2[:, 8 + h:9 + h], scalar2=None, op0=ALU.mult)
            po, por = bank()
            for mc in range(2):
                for h in range(4):
                    I("pe", "transpose", [Eb2r, identr], [tbxr], out=tbx[:, h * 128:(h + 1) * 128],
                      in_=Eb2[:, h, mc * 128:(mc + 1) * 128], identity=identb[:])
                if mc == 0:
                    I("dve", "tensor_copy", [tbxr], [PT2r], out=PT2[:, 0, 0:512], in_=tbx)
                else:
                    I("act", "activation", [tbxr], [PT2r], out=PT2[:, 0, 512:1024], in_=tbx, func=AF.Copy)
            for h in range(4):
                for mc in range(2):
                    mm(po[(h % 2) * 64:(h % 2 + 1) * 64, (h // 2) * 128:(h // 2 + 1) * 128], MVm[:, mc, h * 64:(h + 1) * 64],
                       PT2[:, 0, (mc * 4 + h) * 128:(mc * 4 + h + 1) * 128], mc == 0, mc == 1, [MVmr, PT2r], por)
            I("act", "activation", [por], [OcTr[ti]], out=OcT[:, :, qs], in_=po[:, 0:256].rearrange("p (h q) -> p h q", h=2),
              func=AF.Copy)

        def wco_tile(ti, t):
            qs = slice(ti * 128, (ti + 1) * 128)
            for c in range(2):
                bk, bkr = bank()
                cc = slice(c * 512, (c + 1) * 512)
                for h in range(2):
                    mm(bk[:, :], OcT[:, h, qs], WCO[:, h, cc], h == 0, h == 1, [OcTr[ti], WCOr], bkr)
                I("dve", "tensor_tensor", [Yr[t], bkr], [Yr[t]], out=Y[:, t, cc], in0=Y[:, t, cc], in1=bk[:, :], op=ALU.add)

        import os
        for g0 in (range(0, NTP, GB) if "noBloop" not in os.environ.get("SKIP", "") else []):
            for tp_ in range(0, GB, 2):
                strs = []
                for sx in range(2):
                    ti = tp_ + sx
                    P.rec_begin()
                    norm_T(Y[:, g0 + ti, :], Yr[g0 + ti], 1, XNTb[:, :, ti * 128:(ti + 1) * 128], [XNTbr[ti]], half=sx)
                    strs.append(P.rec_end())
                P.merge(strs)
            cross_q(GB * 128, XNTbr)
            for tp_ in range(0, GB, 2):
                strs = []
                for sx in range(2):
                    P.rec_begin(); bset[0] = [0, 1, 2] if sx == 0 else [3, 4, 5]
                    cross_tile_prompt(tp_ + sx, sx)
                    wco_tile(tp_ + sx, g0 + tp_ + sx)
                    strs.append(P.rec_end())
                P.merge(strs)
            bset[0] = [0, 1, 2, 3, 4]
        TS = NTP
        CMn = A([8, 2, 256], BF16); CMnr = AR("CMn")
        CMV = A([16, 2, 256], BF16); CMVr = AR("CMV")
        CMKT = A([8, 4, 256], BF16, parts=64); CMKTr = AR("CMKT")
        Es = A([2, 4, 256], BF16, parts=8); Esr = [AR("Es0"), AR("Es1")]
        sm3 = A([2, 16], F32, parts=8); sm3r = [AR("sm30"), AR("sm31")]
        PT3 = A([1, 1024], BF16)[:, 0, :]; PT3r = AR("PT3")
        P.dma("pool", CMV[:, :, :, :], cmv_d.rearrange("j (c p) f -> p j c f", p=128), CMVr, writes=[CMVr])
        norm_T(Y[:, TS, :], Yr[TS], 1, XNTb[:, :, 0:128], [XNTbr[0]])
        cross_q(128, [XNTbr[0]])
        po3, po3r = banks[5], bres[5]
        for half in range(2):
            P.dma("pool", CMn[:, :, :, :], cmk_d[half * 8:(half + 1) * 8].rearrange("j (c p) f -> p j c f", p=128), CMnr,
                  writes=[CMnr])
            for jj in range(8):
                for h in range(4):
                    for mc in range(2):
                        I("pe", "transpose", [CMnr, identr], [*tbhr], out=tb[0:64, (h * 2 + mc) * 128:(h * 2 + mc + 1) * 128],
                          in_=CMn[:, jj, mc, h * 64:(h + 1) * 64], identity=identb[:])
                I("act", "activation", [*tbhr], [CMKTr], out=CMKT[:, jj, :, :],
                  in_=tb[0:64, :].rearrange("p (h m) -> p h m", h=4), func=AF.Copy)
            strs = []
            for sx in range(2):
                P.rec_begin(); bset[0] = [0, 1] if sx == 0 else [2, 3]
                for jj in range(sx, 8, 2):
                    j = half * 8 + jj
                    b = sx
                    bks = [bank(), bank()]
                    for h in range(4):
                        bk, bkr = bks[h // 2]
                        mm(bk[0:8, (h % 2) * 256:(h % 2 + 1) * 256], QcT[:, h, 8 * j:8 * j + 8], CMKT[:, jj, h, :], True, True,
                           [QcTr, CMKTr], bkr)
                    st_ = sm3[:, b, :]
                    for q_ in range(2):
                        I("dve", "reduce_max", [bks[q_][1]], [sm3r[b]], out=st_[:, 2 * q_:2 * q_ + 2],
                          in_=bks[q_][0][0:8, :].rearrange("p (a b) -> p a b", a=2), axis=AX.X)
                    I("dve", "tensor_scalar", [sm3r[b]], [sm3r[b]], out=st_[:, 0:4], in0=st_[:, 0:4], scalar1=-0.125, scalar2=None,
                      op0=ALU.mult)
                    for h in range(4):
                        bk, bkr = bks[h // 2]
                        I("act", "activation", [bkr, sm3r[b]], [Esr[b], sm3r[b]], out=Es[:, b, h, :],
                          in_=bk[0:8, (h % 2) * 256:(h % 2 + 1) * 256], func=AF.Exp, bias=st_[:, h:h + 1], scale=0.125,
                          accum_out=st_[:, 4 + h:5 + h])
                    I("dve", "reciprocal", [sm3r[b]], [sm3r[b]], out=st_[:, 8:12], in_=st_[:, 4:8])
                    I("dve", "tensor_tensor", [Esr[b], sm3r[b]], [Esr[b]], out=Es[:, b, :, :], in0=Es[:, b, :, :],
                      in1=st_[:, 8:12].unsqueeze(2).broadcast_to([8, 4, 256]), op=ALU.mult)
                    for mc in range(2):
                        for h in range(4):
                            c0_ = j * 64 + (mc * 4 + h) * 8
                            I("pe", "transpose", [Esr[b], identr], [*tbhr], out=tb[:, c0_:c0_ + 8],
                              in_=Es[:, b, h, mc * 128:(mc + 1) * 128], identity=identb[0:8, 0:8])

                strs.append(P.rec_end())
            P.merge(strs)
            bset[0] = [0, 1, 2, 3, 4]
            I("dve", "tensor_copy", [*tbhr], [PT3r], out=PT3[:, half * 512:(half + 1) * 512], in_=tb[:, half * 512:(half + 1) * 512])
        for j in range(16):
            for h in range(4):
                for mc in range(2):
                    c0_ = j * 64 + (mc * 4 + h) * 8
                    mm(po3[(h % 2) * 64:(h % 2 + 1) * 64, (j * 2 + h // 2) * 8:(j * 2 + h // 2) * 8 + 8],
                       CMV[:, j, mc, h * 64:(h + 1) * 64], PT3[:, c0_:c0_ + 8], mc == 0, mc == 1, [CMVr, PT3r], po3r)
        I("act", "activation", [po3r], [OcTr[0]], out=OcT[:, :, 0:128].rearrange("p c (j i) -> p j c i", i=8),
          in_=po3[:, 0:256].rearrange("p (j c i) -> p j c i", c=2, i=8), func=AF.Copy)
        wco_tile(0, TS)

        if stage <= 2:
            dump_y(range(NT))
            P.emit()
            return nc, P

        new_phase()
        USE_SQRT[0] = True
        NF = FH // 128
        XNTa = A([8, NT * 128], BF16); XNTar = [AR(f"XNTa{t}") for t in range(NT)]
        NSLOT = 12
        WG = A([NSLOT, 8, 128], BF16); WU = A([NSLOT, 8, 128], BF16); WD = A([NSLOT, D], BF16)
        Wsr = [AR(f"Ws{s_}") for s_ in range(NSLOT)]
        Hh = A([6, 512], BF16); Hr = [AR(f"H{j}") for j in range(6)]
        SG = A([2, 512], BF16); SGr = [AR("SG0"), AR("SG1")]
        OUT = A([1, D], F32)[:, 0, :]; OUTr = AR("OUT")
        gfin = A([1, D], F32)[:, 0, :]; gfinr = AR("gfin")
        P.dma("sp", gfin, g_final_d.partition_broadcast(128), gfinr, writes=[gfinr])
        passes = [list(range(0, 6)), list(range(6, 12)), list(range(12, 17)), list(range(17, 22))]
        groups = [(0, 4), (4, 4), (8, 4), (12, 4), (16, 1)]
        wd_v = w_down_d.rearrange("(f p) n -> f p n", p=128)
        wslot = {}
        nload = [0]

        def load_w(f):
            s_ = nload[0] % NSLOT
            nload[0] += 1
            wslot[f] = s_
            P.dma("pool", WG[:, s_], w_gate_d[:, f * 128:(f + 1) * 128].rearrange("(k p) n -> p k n", p=128), Wsr[s_],
                  writes=[Wsr[s_]], group=True)
            P.dma("pool", WU[:, s_], w_up_d[:, f * 128:(f + 1) * 128].rearrange("(k p) n -> p k n", p=128), Wsr[s_],
                  writes=[Wsr[s_]], group=True)
            P.dma("pool", WD[:, s_], wd_v[f], Wsr[s_], writes=[Wsr[s_]], group=True)

        for f in passes[0]:
            load_w(f)
        gcnt = [0]
        for pi, fl in enumerate(passes):
            for gi, (t0, n) in enumerate(groups):
                if pi == 0:
                    for t in range(t0, t0 + n):
                        norm_T(Y[:, t, :], Yr[t], 3, XNTa[:, :, t * 128:(t + 1) * 128], [XNTar[t]])
                if pi + 1 < len(passes) and gi == 0:
                    for f in passes[pi + 1]:
                        load_w(f)
                ntok = n * 128
                tok = slice(t0 * 128, t0 * 128 + ntok)
                xr = [XNTar[t] for t in range(t0, t0 + n)]
                for j, f in enumerate(fl):
                    s_ = wslot[f]
                    b = gcnt[0] % 2
                    gcnt[0] += 1
                    pg, pgr = bank()
                    pu, pur = bank()
                    for k in range(8):
                        mm(pg[:, 0:ntok], WG[:, s_, k, :], XNTa[:, k, tok], k == 0, k == 7, [Wsr[s_]] + xr, pgr)
                    for k in range(8):
                        mm(pu[:, 0:ntok], WU[:, s_, k, :], XNTa[:, k, tok], k == 0, k == 7, [Wsr[s_]] + xr, pur)
                    I("act", "activation", [pgr], [SGr[b]], out=SG[:, b, 0:ntok], in_=pg[:, 0:ntok], func=AF.Silu)
                    I("dve", "tensor_tensor", [SGr[b], pur], [Hr[j]], out=Hh[:, j, 0:ntok], in0=SG[:, b, 0:ntok],
                      in1=pu[:, 0:ntok], op=ALU.mult)
                for ti in range(n):
                    t = t0 + ti
                    for c in range(2):
                        pd, pdr = bank()
                        for j, f in enumerate(fl):
                            s_ = wslot[f]
                            mm(pd[:, :], Hh[:, j, ti * 128:(ti + 1) * 128], WD[:, s_, c * 512:(c + 1) * 512], j == 0,
                               j == len(fl) - 1, [Hr[j], Wsr[s_]], pdr)
                        I("dve", "tensor_tensor", [Yr[t], pdr], [Yr[t]], out=Y[:, t, c * 512:(c + 1) * 512],
                          in0=Y[:, t, c * 512:(c + 1) * 512], in1=pd[:, :], op=ALU.add)
                if pi == len(passes) - 1:
                    yo = yp_o.rearrange("(t p) d -> t p d", p=128)
                    for t in range(t0, t0 + n):
                        rstd, sr = norm_stats(Y[:, t, :], Yr[t], 0)
                        I("dve", "scalar_tensor_tensor", [Yr[t], sr, gfinr], [OUTr], out=OUT, in0=Y[:, t, :], scalar=rstd,
                          in1=gfin, op0=ALU.mult, op1=ALU.mult)
                        P.dma("sp", (yo[t] if t < NTP else ys_o), OUT, OUTr, reads=[OUTr])
        P.emit()
        return nc, P


def make_consts(hf):
    c = {}
    c["c_ident"] = np.eye(128, dtype=np.float32)
    i = np.arange(128)[:, None]; j = np.arange(256)[None, :]
    band = np.where((j >= i) & (j <= i + 128), 0.0, NEG).astype(np.float32)
    first = band.copy()
    if hf == 0:
        first[:, :128] = NEG
    c["c_mb_band"] = band; c["c_mb_first"] = first
    s = np.arange(128)[:, None]; t = np.arange(128)[None, :]
    c["c_mb_caus"] = np.where(s <= t, 0.0, NEG).astype(np.float32)
    c["c_mb_causs"] = np.where((s <= t) & (s // 8 == t // 8), 0.0, NEG).astype(np.float32)
    sel = np.zeros((4, 1024), np.float32)
    for h in range(4):
        sel[h, h * 128:(h + 1) * 128] = 1.0
        sel[h, 512 + h * 128:512 + (h + 1) * 128] = -1.0
    c["c_sel"] = sel
    pm = np.zeros((4, 2), np.float32)
    pm[:, 0] = 1.0 if hf else 0.0
    pm[:, 1] = 0.0 if hf else NEG
    c["c_pmask"] = pm
    r = np.arange(32)[:, None] % 8
    p = np.arange(128)[None, :]
    c["c_smc"] = np.where(p >= r, 0.0, NEG).astype(np.float32)
    smn = np.full((32, 16, 128), NEG, np.float32)
    for jq in range(16):
        for ii in range(8):
            smn[(np.arange(32) % 8) >= ii, jq, jq * 8 + ii] = 0.0
    c["c_smn"] = smn
    c["c_bt"] = np.where((s <= t) & (s // 8 == t // 8), 1.0, 0.0).astype(np.float32)
    e = np.zeros((128, 16), np.float32); e[np.arange(128), np.arange(128) // 8] = 1.0
    c["c_eseq"] = e
    return c

def shard_inputs(inp):
    maps = []
    W = ["w_in", "b_igate", "b_fgate", "attn_sinks", "g_mlstm_head", "w_out", "g_mix", "g_cross", "g_mem",
         "w_cq", "w_ck", "w_cv", "w_co", "g_ffn", "w_gate", "w_up", "w_down"]
    wd = {k: np.ascontiguousarray(np.asarray(inp[k], np.float32)[0]) for k in W}
    wd["g_final"] = np.ascontiguousarray(np.asarray(inp["g_final"], np.float32))
    xp = np.asarray(inp["x_prompt"], np.float32); xs = np.asarray(inp["x_sample"], np.float32)
    for c in range(8):
        b, hf = c // 2, c % 2
        m = dict(wd)
        m["xp"] = np.ascontiguousarray(xp[b, hf * 2048:(hf + 1) * 2048])
        m["xpre"] = np.ascontiguousarray(xp[b, 0:2048]) if hf else np.zeros((2048, 1024), np.float32)
        m["xs"] = np.ascontiguousarray(xs[16 * c:16 * c + 16].reshape(128, 1024))
        m["mem"] = np.ascontiguousarray(np.asarray(inp["mem_prompt"], np.float32)[b])
        sl = slice(16 * c, 16 * c + 16)
        m["csk"] = np.ascontiguousarray(np.asarray(inp["cache_swa_k"], np.float32)[0, sl].reshape(16, 128, 128))
        m["csv"] = np.ascontiguousarray(np.asarray(inp["cache_swa_v"], np.float32)[0, sl].reshape(16, 128, 128))
        m["sC"] = np.ascontiguousarray(np.asarray(inp["state_mlstm_C"], np.float32)[0, sl])
        m["sn"] = np.ascontiguousarray(np.asarray(inp["state_mlstm_n"], np.float32)[0, sl])
        m["sm"] = np.ascontiguousarray(np.asarray(inp["state_mlstm_m"], np.float32)[0, sl])
        m["cmk"] = np.ascontiguousarray(np.asarray(inp["cache_mem_k"], np.float32)[0, sl].reshape(16, 256, 256))
        m["cmv"] = np.ascontiguousarray(np.asarray(inp["cache_mem_v"], np.float32)[0, sl].reshape(16, 256, 256))
        m.update(make_consts(hf))
        sk = wd["attn_sinks"]
        sc = np.zeros((32, 2), np.float32)
        rr = np.arange(32)
        for h in range(2):
            sc[:, h] = sk[4 * h + 2 * ((rr % 16) // 8) + rr // 16]
        m["c_sinkcol"] = sc
        maps.append(m)
    return maps

def gather(res):
    f = np.float32
    yp = np.zeros((4, 4096, 1024), f); ys = np.zeros((128, 8, 1024), f)
    skp = np.zeros((1, 4, 128, 2, 64), f); svp = np.zeros_like(skp)
    Cp = np.zeros((1, 4, 4, 128, 64), f); npp = np.zeros((1, 4, 4, 64), f); mp = np.zeros((1, 4, 4), f)
    mkp = np.zeros((1, 4, 256, 4, 64), f); mvp = np.zeros_like(mkp)
    sks = np.zeros((1, 128, 128, 2, 64), f); svs = np.zeros_like(sks)
    Cs = np.zeros((1, 128, 4, 128, 64), f); ns = np.zeros((1, 128, 4, 64), f); ms = np.zeros((1, 128, 4), f)
    for c in range(8):
        r = res[c]; b, hf = c // 2, c % 2
        yp[b, hf * 2048:(hf + 1) * 2048] = r["yp"]
        ys[16 * c:16 * c + 16] = r["ys"].reshape(16, 8, 1024)
        if hf == 1:
            skp[0, b] = r["swak"].reshape(128, 2, 64); svp[0, b] = r["swav"].reshape(128, 2, 64)
            Cp[0, b] = r["Cp"]; npp[0, b] = r["np"]; mp[0, b] = r["mp"].reshape(4)
        else:
            mkp[0, b] = r["memk"].reshape(256, 4, 64); mvp[0, b] = r["memv"].reshape(256, 4, 64)
        sl = slice(16 * c, 16 * c + 16)
        sks[0, sl] = r["sks"].reshape(16, 128, 2, 64); svs[0, sl] = r["svs"].reshape(16, 128, 2, 64)
        Cs[0, sl] = r["Cs"]; ns[0, sl] = r["ns"]; ms[0, sl] = r["ms"]
    return (yp, ys, skp, svp, Cp, npp, mp, mkp, mvp, sks, svs, Cs, ns, ms)


_CACHE = {}


def kernel(**inputs):
    if "nc" not in _CACHE:
        _CACHE["nc"] = build_program(3)[0]
    nc = _CACHE["nc"]
    maps = shard_inputs(inputs)
    res = run_bass_kernel_spmd(nc, maps, core_ids=list(range(8)))
    return gather(res.results)
```

```python
import contextlib
from concourse.bass_utils import run_bass_kernel_spmd
import numpy as np
import concourse.bass as bass
import concourse.mybir as mybir

F32 = mybir.dt.float32
BF16 = mybir.dt.bfloat16
I32 = mybir.dt.int32
AF = mybir.ActivationFunctionType
ALU = mybir.AluOpType
AX = mybir.AxisListType

ENGS = ("pe", "act", "dve", "pool", "sp")


class Res:
    __slots__ = ("name", "last_w", "readers", "sem", "dcount", "excl")

    def __init__(self, name):
        self.name = name
        self.last_w = None
        self.readers = []
        self.sem = None
        self.dcount = 0
        self.excl = False


class Op:
    __slots__ = ("eng", "fn", "deps", "dma_res", "sig", "cnt", "k", "group")

    def __init__(self, eng, fn, dma_res):
        self.eng = eng
        self.fn = fn
        self.deps = set()
        self.dma_res = dma_res
        self.sig = False
        self.cnt = 0
        self.k = 0


class Prog:
    def __init__(self, nc):
        self.nc = nc
        self.ops = []
        self.nres = 0
        self.inherit = []
        self.phase_res = []

    def res(self, name=None, arena=False):
        self.nres += 1
        r = Res(name or f"r{self.nres}")
        if arena:
            r.readers = list(self.inherit)
            self.phase_res.append(r)
        return r

    def new_phase(self):
        inh = set(self.inherit)
        for r in self.phase_res:
            if r.last_w is not None:
                inh.add(r.last_w)
            inh.update(r.readers)
        self.inherit = sorted(inh)
        self.phase_res = []

    def rec_begin(self):
        self._rec = []

    def rec_end(self):
        r = self._rec
        self._rec = None
        return r

    def merge(self, streams):
        streams = [s_ for s_ in streams if s_]
        pos = [0] * len(streams)
        while True:
            best = None
            for k, s_ in enumerate(streams):
                if pos[k] < len(s_):
                    f = pos[k] / len(s_)
                    if best is None or f < best[0]:
                        best = (f, k)
            if best is None:
                break
            k = best[1]
            a, kw = streams[k][pos[k]]
            pos[k] += 1
            self.op(*a, **kw)

    def op(self, eng, fn, reads=(), writes=(), dma_res=None, accum=False, group=False):
        if getattr(self, "_rec", None) is not None:
            self._rec.append(((eng, fn, tuple(reads), tuple(writes)), dict(dma_res=dma_res, accum=accum, group=group)))
            return None
        i = len(self.ops)
        o = Op(eng, fn, dma_res)
        for r in reads:
            if r.last_w is not None:
                o.deps.add(r.last_w)
            if r.excl:
                for q in r.readers:
                    if self.ops[q].eng != eng:
                        o.deps.add(q)
            r.readers.append(i)
        for r in writes:
            if r.last_w is not None:
                lw = self.ops[r.last_w]
                if group and lw.dma_res is not None and lw.dma_res is dma_res:
                    o.deps |= lw.deps
                elif not (accum and lw.eng == "pe" and eng == "pe"):
                    o.deps.add(r.last_w)
            latest = {}
            for q in r.readers:
                if q == i:
                    continue
                oq = self.ops[q]
                if oq.dma_res is not None:
                    o.deps.add(q)
                elif latest.get(oq.eng, -1) < q:
                    latest[oq.eng] = q
            o.deps.update(latest.values())
            r.last_w = i
            r.readers = []
        if eng == "pe":
            o.deps = {d for d in o.deps if self.ops[d].eng != "pe" or self.ops[d].dma_res is not None}
        self.ops.append(o)
        return i

    def dma(self, eng, out, in_, res, reads=(), writes=(), group=False, **kw):
        kw = dict(kw); kw["out"] = out; kw["in_"] = in_
        return self.op(eng, ("dma_start", kw), reads=reads, writes=writes, dma_res=res, group=group)

    def I(self, eng, name, reads=(), writes=(), **kw):
        return self.op(eng, (name, kw), reads=reads, writes=writes)

    def emit(self, final_wait_all=True):
        nc = self.nc
        ops = self.ops
        for o in ops:
            for d in o.deps:
                ops[d].sig = True
        per_eng = {e: [] for e in ENGS}
        for i, o in enumerate(ops):
            per_eng[o.eng].append(i)
        import contextlib
        with contextlib.ExitStack() as st:
            esem = {e: st.enter_context(nc.semaphore(f"s_{e}")) for e in ENGS}
            ecount = {e: 0 for e in ENGS}
            dma_sems = []
            for i, o in enumerate(ops):
                if o.dma_res is not None:
                    r = o.dma_res
                    if r.sem is None:
                        r.sem = st.enter_context(nc.semaphore(f"d{len(dma_sems)}_{r.name}"))
                        dma_sems.append(r)
                    r.dcount += 1
                    o.cnt = 16 * r.dcount
                elif o.sig:
                    ecount[o.eng] += 1
                    o.cnt = ecount[o.eng]
            self.n_dma_sems = len(dma_sems)
            know = {e: {} for e in ENGS}
            know_issue = [None] * len(ops)

            def key_of(o):
                return ("d", id(o.dma_res)) if o.dma_res is not None else ("e", o.eng)

            block = st.enter_context(nc.Block())
            handles = {}

            plan = [None] * len(ops)
            for i, o in enumerate(ops):
                kn = know[o.eng]
                need = {}
                for d in o.deps:
                    p = ops[d]
                    k = key_of(p)
                    if kn.get(k, 0) >= p.cnt:
                        continue
                    if need.get(k, (0, None))[0] < p.cnt:
                        need[k] = (p.cnt, d)
                waits = []
                for k, (cnt, d) in need.items():
                    p = ops[d]
                    sem = p.dma_res.sem if p.dma_res is not None else esem[p.eng]
                    waits.append((sem, cnt))
                    kn[k] = max(kn.get(k, 0), cnt)
                    ki = know_issue[d]
                    for kk, vv in ki.items():
                        if kn.get(kk, 0) < vv:
                            kn[kk] = vv
                know_issue[i] = dict(kn)
                plan[i] = waits
            self.n_waits = sum(len(w) for w in plan)

            def make(ename):
                def body(eh):
                    for i in per_eng[ename]:
                        o = ops[i]
                        for sem, cnt in plan[i]:
                            eh.wait_ge(sem, cnt)
                        ins = getattr(eh, o.fn[0])(**o.fn[1])
                        if o.dma_res is not None:
                            ins.then_inc(o.dma_res.sem, 16)
                        elif o.sig:
                            ins.then_inc(esem[o.eng], 1)
                    if ename == "sp" and final_wait_all:
                        for r in dma_sems:
                            eh.wait_ge(r.sem, 16 * r.dcount)
                        for e in ("pe", "act", "dve", "pool"):
                            if ecount[e]:
                                eh.wait_ge(esem[e], ecount[e])
                return body

            block.tensor(make("pe"))
            block.scalar(make("act"))
            block.vector(make("dve"))
            block.gpsimd(make("pool"))
            block.sync(make("sp"))


D = 1024
FH = 2816
EPS = 1e-6
NTP = 16
NT = 17
GT = 2
NEG = -30000.0
DBG_G0 = 2


def build_program(stage=3, debug=False):
    nc = bass.Bass("TRN2", target_bir_lowering=False)
    P = Prog(nc)

    def din(name, shape, dt=F32):
        return nc.dram_tensor(name, list(shape), dt, kind="ExternalInput").ap()

    def dout(name, shape):
        return nc.dram_tensor(name, list(shape), F32, kind="ExternalOutput").ap()

    xp_d = din("xp", [2048, D]); xpre_d = din("xpre", [2048, D]); xs_d = din("xs", [128, D])
    mem_d = din("mem", [256, D])
    csk_d = din("csk", [16, 128, 128]); csv_d = din("csv", [16, 128, 128])
    sC_d = din("sC", [16, 4, 128, 64]); sn_d = din("sn", [16, 4, 64]); sm_d = din("sm", [16, 4])
    cmk_d = din("cmk", [16, 256, 256]); cmv_d = din("cmv", [16, 256, 256])
    w_in_d = din("w_in", [D, 2312]); b_i_d = din("b_igate", [4]); b_f_d = din("b_fgate", [4])
    sinks_d = din("attn_sinks", [8]); ghead_d = din("g_mlstm_head", [512]); w_out_d = din("w_out", [D, D])
    g_mix_d = din("g_mix", [D]); g_cross_d = din("g_cross", [D]); g_mem_d = din("g_mem", [D])
    w_cq_d = din("w_cq", [D, 256]); w_ck_d = din("w_ck", [D, 256]); w_cv_d = din("w_cv", [D, 256])
    w_co_d = din("w_co", [256, D]); g_ffn_d = din("g_ffn", [D])
    w_gate_d = din("w_gate", [D, FH]); w_up_d = din("w_up", [D, FH]); w_down_d = din("w_down", [FH, D])
    g_final_d = din("g_final", [D])
    ident_d = din("c_ident", [128, 128]); mb_band_d = din("c_mb_band", [128, 256]); mb_first_d = din("c_mb_first", [128, 256])
    mb_caus_d = din("c_mb_caus", [128, 128]); mb_causs_d = din("c_mb_causs", [128, 128])
    sel_d = din("c_sel", [4, 1024]); pmask_d = din("c_pmask", [4, 2])
    smc_d = din("c_smc", [32, 128]); smn_d = din("c_smn", [32, 16, 128]); sinkcol_d = din("c_sinkcol", [32, 2])
    bt_d = din("c_bt", [128, 128]); eseq_d = din("c_eseq", [128, 16])

    yp_o = dout("yp", [2048, D]); ys_o = dout("ys", [128, D])
    swak_o = dout("swak", [128, 128]); swav_o = dout("swav", [128, 128])
    Cp_o = dout("Cp", [4, 128, 64]); np_o = dout("np", [4, 64]); mp_o = dout("mp", [4, 1])
    memk_o = dout("memk", [256, 256]); memv_o = dout("memv", [256, 256])
    sks_o = dout("sks", [16, 128, 128]); svs_o = dout("svs", [16, 128, 128])
    Cs_o = dout("Cs", [16, 4, 128, 64]); ns_o = dout("ns", [16, 4, 64]); ms_o = dout("ms", [16, 4])

    st = contextlib.ExitStack()
    with st:
        def sb(name, shape, dt):
            return st.enter_context(nc.sbuf_tensor(name, list(shape), dt))

        def ps(name, shape, dt):
            return st.enter_context(nc.psum_tensor(name, list(shape), dt))

        banks = [ps(f"bk{i}", [128, 512], F32) for i in range(7)]
        bres = [P.res(f"bk{i}") for i in range(7)]
        for r_ in bres:
            r_.excl = True
        tb = ps("tb", [128, 1024], BF16)
        tbh = [tb[:, 0:512], tb[:, 512:1024]]
        tbhr = [P.res("tbA"), P.res("tbB")]
        for r_ in tbhr:
            r_.excl = True
        bki = [0]

        bset = [[0, 1, 2, 3, 4]]
        bcnt = {}

        def bank():
            key = tuple(bset[0])
            c = bcnt.get(key, 0)
            bcnt[key] = c + 1
            i = bset[0][c % len(key)]
            return banks[i], bres[i]

        Y = sb("Y", [128, NT, D], F32)
        Yr = [P.res(f"Y{t}") for t in range(NT)]
        identb = sb("identb", [128, 128], BF16); identr = P.res("identb")
        identf = sb("identf", [128, 128], F32); identfr = P.res("identf")
        onesb = sb("onesb", [128, 128], BF16); onesbr = P.res("onesb")
        onesf = sb("onesf", [128, 256], F32); onesfr = P.res("onesf")
        SEL = sb("SEL", [4, 1024], F32); selr = P.res("SEL")
        gcols = sb("gcols", [128, 4, 8], F32); gcolsr = P.res("gcols")
        gheadc = sb("gheadc", [128, 4], F32); gheadr = P.res("ghead")
        sinkb = sb("sinkb", [128, 16], F32); sinkbr = P.res("sinkb")
        gb4 = sb("gb4", [4, 4], F32); gb4r = P.res("gb4")
        mbband = sb("mbband", [128, 256], BF16); mbbandr = P.res("mbband")
        mbfirst = sb("mbfirst", [128, 256], BF16); mbfirstr = P.res("mbfirst")
        mbcaus = sb("mbcaus", [128, 128], BF16); mbcausr = P.res("mbcaus")
        stat = sb("stat", [128, 8, 4], F32)
        statr = [P.res(f"stat{i}") for i in range(8)]
        stati = [0]
        USE_SQRT = [False]
        xsb = sb("xsb", [128, 2, D], BF16); xsbr = [P.res("xsb0"), P.res("xsb1")]
        Cst = sb("Cst", [64, 4, 129], F32); Cstr = [P.res(f"Cst{h}") for h in range(4)]
        ARN = 64400
        arena = sb("arena", [128, ARN], BF16)
        aoff = [0]

        def A(shape, dt, parts=128, name=None):
            n = int(np.prod(shape))
            nb = n * (4 if dt == F32 else 2)
            n16 = (nb + 1) // 2
            n16 = (n16 + 15) // 16 * 16
            assert aoff[0] + n16 <= ARN, f"arena overflow {aoff[0]}+{n16} ({name})"
            v = arena[0:parts, aoff[0]:aoff[0] + n16]
            aoff[0] += n16
            if dt == F32:
                v = v.bitcast(F32)
            v = v[:, 0:n]
            if len(shape) == 2:
                v = v.rearrange("p (a b) -> p a b", a=shape[0])
            elif len(shape) == 3:
                v = v.rearrange("p (a b c) -> p a b c", a=shape[0], b=shape[1])
            return v

        def new_phase():
            P.new_phase()
            aoff[0] = 0

        def AR(name):
            return P.res(name, arena=True)

        def A_at(off, shape, dt, parts=128):
            n = int(np.prod(shape))
            nb = n * (4 if dt == F32 else 2)
            n16 = ((nb + 1) // 2 + 15) // 16 * 16
            v = arena[0:parts, off:off + n16]
            if dt == F32:
                v = v.bitcast(F32)
            v = v[:, 0:n]
            if len(shape) == 2:
                v = v.rearrange("p (a b) -> p a b", a=shape[0])
            elif len(shape) == 3:
                v = v.rearrange("p (a b c) -> p a b c", a=shape[0], b=shape[1])
            return v, off + n16

        def ARalias(name, olds):
            r = P.res(name, arena=True)
            dd = set(r.readers)
            for o_ in olds:
                if o_.last_w is not None:
                    dd.add(o_.last_w)
                dd.update(o_.readers)
            r.readers = sorted(dd)
            return r

        I = P.I

        def mm(out, lhsT, rhs, start, stop, reads, wres):
            I("pe", "matmul", reads, [wres], out=out, lhsT=lhsT, rhs=rhs, start=start, stop=stop)

        P.dma("pool", identb[:], ident_d, identr, writes=[identr])
        P.dma("sp", identf[:], ident_d, identfr, writes=[identfr])
        I("dve", "memset", [], [onesbr], ap=onesb[:], constant=1.0)
        I("dve", "memset", [], [onesfr], ap=onesf[:], constant=1.0)
        P.dma("sp", SEL[:], sel_d, selr, writes=[selr])
        for i, g in enumerate((g_mix_d, g_cross_d, g_mem_d, g_ffn_d)):
            P.dma("sp", gcols[:, i, :], g.rearrange("(k p) -> p k", p=128), gcolsr, writes=[gcolsr], group=True, allow_slow_non_contiguous=True)
        P.dma("sp", gheadc[:], ghead_d.rearrange("(h p) -> p h", p=128), gheadr, writes=[gheadr], allow_slow_non_contiguous=True)
        P.dma("sp", gb4[:, 0:1], b_i_d.rearrange("(h o) -> h o", o=1), gb4r, writes=[gb4r], group=True, allow_slow_non_contiguous=True)
        P.dma("sp", gb4[:, 1:2], b_f_d.rearrange("(h o) -> h o", o=1), gb4r, writes=[gb4r], group=True, allow_slow_non_contiguous=True)
        P.dma("sp", gb4[:, 2:4], pmask_d, gb4r, writes=[gb4r], group=True, allow_slow_non_contiguous=True)
        P.dma("sp", sinkb[:, 0:8], sinks_d.partition_broadcast(128), sinkbr, writes=[sinkbr])
        I("dve", "tensor_scalar", [sinkbr], [sinkbr], out=sinkb[:, 8:16], in0=sinkb[:, 0:8], scalar1=-1.0, scalar2=None,
          op0=ALU.mult)
        I("dve", "tensor_scalar", [gb4r], [gb4r], out=gb4[:, 1:2], in0=gb4[:, 1:2], scalar1=-1.0, scalar2=None, op0=ALU.mult)
        P.dma("pool", mbband[:], mb_band_d, mbbandr, writes=[mbbandr])
        P.dma("pool", mbfirst[:], mb_first_d, mbfirstr, writes=[mbfirstr])
        P.dma("pool", mbcaus[:], mb_caus_d, mbcausr, writes=[mbcausr])
        for h in range(4):
            I("dve", "memset", [], [Cstr[h]], ap=Cst[:, h, :], constant=0.0)

        def SELh(h, n=128):
            return SEL[:, h * 128:h * 128 + n]

        def NSELh(h, n=128):
            return SEL[:, 512 + h * 128:512 + h * 128 + n]

        def norm_stats(src, sres, jb=0):
            i = stati[0] % 8
            stati[0] += 1
            sr = statr[i]
            I("act", "activation", [sres], [xsbr[jb], sr], out=xsb[:, jb, :], in_=src, func=AF.Square, accum_out=stat[:, i, 0:1])
            I("dve", "tensor_scalar", [sr], [sr], out=stat[:, i, 1:2], in0=stat[:, i, 0:1], scalar1=1.0 / D, scalar2=EPS,
              op0=ALU.mult, op1=ALU.add)
            if USE_SQRT[0]:
                I("act", "activation", [sr], [sr], out=stat[:, i, 2:3], in_=stat[:, i, 1:2], func=AF.Sqrt)
                I("dve", "reciprocal", [sr], [sr], out=stat[:, i, 3:4], in_=stat[:, i, 2:3])
            else:
                I("act", "activation", [sr], [sr], out=stat[:, i, 2:3], in_=stat[:, i, 1:2], func=AF.Ln)
                I("act", "activation", [sr], [sr], out=stat[:, i, 3:4], in_=stat[:, i, 2:3], func=AF.Exp, scale=-0.5)
            return stat[:, i, 3:4], sr

        xsi = [0]

        def norm_T(src, sres, gi, dst, dres, half=None):
            b = xsi[0] % 2 if half is None else half
            xsi[0] += 1
            rstd, sr = norm_stats(src, sres, b)
            I("dve", "tensor_scalar", [sres, sr], [xsbr[b]], out=xsb[:, b, :], in0=src, scalar1=rstd, scalar2=None, op0=ALU.mult)
            if half is None:
                for k in range(8):
                    I("pe", "transpose", [xsbr[b], identr], [*tbhr], out=tb[:, k * 128:(k + 1) * 128],
                      in_=xsb[:, b, k * 128:(k + 1) * 128], identity=identb[:])
                for k in range(8):
                    I("act", "activation", [*tbhr, gcolsr], dres, out=dst[:, k, :], in_=tb[:, k * 128:(k + 1) * 128],
                      func=AF.Copy, scale=gcols[:, gi, k:k + 1])
            else:
                for kb in range(2):
                    for k4 in range(4):
                        k = kb * 4 + k4
                        I("pe", "transpose", [xsbr[b], identr], [tbhr[half]], out=tbh[half][:, k4 * 128:(k4 + 1) * 128],
                          in_=xsb[:, b, k * 128:(k + 1) * 128], identity=identb[:])
                    for k4 in range(4):
                        k = kb * 4 + k4
                        I("act", "activation", [tbhr[half], gcolsr], dres, out=dst[:, k, :],
                          in_=tbh[half][:, k4 * 128:(k4 + 1) * 128], func=AF.Copy, scale=gcols[:, gi, k:k + 1])

        class NS:
            pass

        def alloc_mixer(gt, nkt, nvt):
            M = NS()
            M.WQ = A([8, 512], BF16); M.WQr = AR("WQ")
            M.WTOK = A([8, 1024], BF16); M.WTOKr = AR("WTOK")
            M.WK = M.WTOK[:, :, 0:128]; M.WKr = M.WTOKr
            M.WMQ = A([8, 256], BF16); M.WMQr = AR("WMQ")
            M.WMK = M.WTOK[:, :, 256:512]; M.WMKr = M.WTOKr
            M.WOG = A([8, 512], BF16); M.WOGr = AR("WOG")
            M.WGT = A([8, 8], BF16); M.WGTr = AR("WGT")
            M.WOA = A([4, 1024], BF16); M.WOAr = AR("WOA")
            M.WOM = A([4, 1024], BF16); M.WOMr = AR("WOM")

            def wload(dst, res, src, **kw):
                P.dma("pool", dst, src, res, writes=[res], **kw)

            def wcols(a_, b_):
                return w_in_d[:, a_:b_].rearrange("(k p) n -> p k n", p=128)
            wload(M.WTOK[:, :, 0:256], M.WTOKr, wcols(512, 768), group=True)
            wload(M.WTOK[:, :, 256:1024], M.WTOKr, wcols(1024, 1792), group=True)
            wload(M.WGT[:], M.WGTr, wcols(2304, 2312), allow_slow_non_contiguous=True)
            wload(M.WQ[:], M.WQr, wcols(0, 512))
            wload(M.WMQ[:], M.WMQr, wcols(768, 1024))
            wload(M.WOG[:], M.WOGr, wcols(1792, 2304))
            wload(M.WOA[:], M.WOAr, w_out_d[0:512, :].rearrange("(c p) n -> p c n", p=128))
            wload(M.WOM[:], M.WOMr, w_out_d[512:1024, :].rearrange("(h p) n -> p h n", p=128))
            M.KT = A([2, nkt * 128], BF16, parts=64); M.KTr = [AR(f"KT{i}") for i in range(nkt)]
            M.Vt = A([nvt, 128], BF16); M.Vtr = [AR(f"Vt{i}") for i in range(nvt)]
            M.XNTg = A([8, gt * 128], BF16); M.XNTgr = [AR(f"XNTg{i}") for i in range(gt)]
            M.QT = A([8, gt * 128], BF16, parts=64); M.QTr = AR("QT")
            M.MQT = A([4, gt * 128], BF16, parts=64); M.MQTr = AR("MQT")
            M.MKT = A([4, gt * 128], BF16, parts=64); M.MKTr = AR("MKT")
            M.SGT = A([4, gt * 128], BF16); M.SGTr = AR("SGT")
            M.MKtok = A([gt, 256], BF16); M.MKtokr = [AR(f"MKtok{i}") for i in range(gt)]
            M.MVaug = A([gt, 4, 129], BF16); M.MVaugr = [AR(f"MVaug{i}") for i in range(gt)]
            M.ATTT = A([4, gt * 128], BF16); M.ATTTr = [AR(f"ATTT{i}") for i in range(gt)]
            M.HMT = A([4, gt * 128], BF16); M.HMTr = [AR(f"HMT{i}") for i in range(gt)]
            M.NG = gt * 128
            NG_ = M.NG
            M.G_IG = A([1, NG_ + 1], F32, parts=4)[:, 0, :]; M.G_E = A([1, NG_], F32, parts=4)[:, 0, :]
            M.G_L1 = A([1, NG_], F32, parts=4)[:, 0, :]; M.G_B = A([1, NG_ + 1], F32, parts=4)[:, 0, :]
            M.G_A = A([1, NG_], F32, parts=4)[:, 0, :]; M.G_M = A([1, NG_ + 1], F32, parts=4)[:, 0, :]
            M.G_BM = A([1, NG_], F32, parts=4)[:, 0, :]; M.G_DM = A([1, NG_], F32, parts=4)[:, 0, :]
            M.Gr = AR("G_IG"); M.G_Br = AR("G_B"); M.G_Ar = AR("G_A"); M.G_Mr = AR("G_M"); M.G_BMr = AR("G_BM"); M.G_DMr = AR("G_DM")
            M.SKV = A([1, 256], F32)[:, 0, :]; M.SKVr = AR("SKV")
            M.Ebuf = A([4, 256], BF16); M.Er = AR("E")
            M.PTs = A([1, 1024], BF16); M.PTsr = [AR("PTs0")] * 2
            M.sm_st = A([1, 32], F32)[:, 0, :]; M.smr = AR("sm_st")
            M.WKC = A([1, 8], F32)[:, 0, :]; M.WKCr = AR("WKC")
            M.DG = A([1, 8], F32, parts=4)[:, 0, :]; M.DGr = AR("DG")
            M.VW = A([4, 129], BF16); M.VWr = [AR(f"VW{h}") for h in range(4)]
            M.Cb = A([4, 257], BF16, parts=64); M.Cbr = [AR(f"Cb{h}") for h in range(4)]
            M.WT = A([4, 128], BF16); M.WTr = AR("WT")
            M.ST = A([4, 128], BF16); M.STr = AR("ST")
            M.WI = A([4, 128], BF16); M.WIr = AR("WI")
            M.QW = A([4, 128], BF16, parts=64); M.QWr = AR("QW")
            M.LOWB = A([4, 128], F32); M.LOWBr = AR("LOWB")
            M.T1 = A([4, 128], F32); M.T1r = AR("T1")
            M.T2 = A([4, 128], F32); M.T2r = AR("T2")
            M.USQ = A([4, 128], BF16); M.USQr = AR("USQ")
            for i in range(gt):
                I("dve", "memset", [], [M.MVaugr[i]], ap=M.MVaug[:, i, :, 128:129], constant=1.0)
            return M

        M = alloc_mixer(GT, NTP + 1, NTP + 1)
        I("dve", "memset", [], [M.G_Br], ap=M.G_B[:, 0:1], constant=0.0)
        I("dve", "memset", [], [M.G_Mr], ap=M.G_M[:, 0:1], constant=0.0)

        def tok_major(ti, xcols, xres, vslot, want_kv_out=None):
            b0, b0r = bank()
            b1, b1r = bank()
            for k in range(8):
                mm(b0[:, :], M.XNTg[:, k, xcols], M.WTOK[:, k, 0:512], k == 0, k == 7, [xres, M.WTOKr], b0r)
            for k in range(8):
                mm(b1[:, :], M.XNTg[:, k, xcols], M.WTOK[:, k, 512:1024], k == 0, k == 7, [xres, M.WTOKr], b1r)
            I("act", "activation", [b0r], [M.Vtr[vslot]], out=M.Vt[:, vslot, :], in_=b0[:, 128:256], func=AF.Copy)
            I("act", "activation", [b0r], [M.MKtokr[ti]], out=M.MKtok[:, ti, :], in_=b0[:, 256:512], func=AF.Copy, scale=0.125)
            I("dve", "tensor_copy", [b1r], [M.MVaugr[ti]], out=M.MVaug[:, ti, :, 0:128],
              in_=b1[:, :].rearrange("p (h d) -> p h d", h=4))
            if want_kv_out is not None:
                I("dve", "tensor_copy", [b0r], [M.SKVr], out=M.SKV[:, :], in_=b0[:, 0:256])
                if want_kv_out == "sample":
                    P.dma("sp", sks_o[:, 120:128, :], M.SKV[:, 0:128], M.SKVr, reads=[M.SKVr], group=True)
                    P.dma("sp", svs_o[:, 120:128, :], M.SKV[:, 128:256], M.SKVr, reads=[M.SKVr], group=True)
                else:
                    P.dma("sp", swak_o, M.SKV[:, 0:128], M.SKVr, reads=[M.SKVr], group=True)
                    P.dma("sp", swav_o, M.SKV[:, 128:256], M.SKVr, reads=[M.SKVr], group=True)

        def feat64(W, Wr, nh, dst, dres, ntok, xres, scale=None, dcol0=0):
            for h0 in range(0, nh, 2):
                bk, bkr = bank()
                for hh in range(2):
                    h = h0 + hh
                    for k in range(8):
                        mm(bk[0:64, hh * 256:hh * 256 + ntok], W[:, k, h * 64:(h + 1) * 64], M.XNTg[:, k, 0:ntok],
                           k == 0, k == 7, [Wr] + xres, bkr)
                src = bk[0:64, :].rearrange("p (a b) -> p a b", a=2)[:, :, 0:ntok]
                kw = {} if scale is None else {"scale": scale}
                I("act", "activation", [bkr], dres, out=dst[:, h0:h0 + 2, dcol0:dcol0 + ntok], in_=src, func=AF.Copy, **kw)

        def gates(ntok, xres, prefix):
            pg, pgr = bank()
            for k in range(8):
                mm(pg[0:4, 0:ntok], M.WGT[:, k, 0:4], M.XNTg[:, k, 0:ntok], k == 0, k == 7, [M.WGTr] + xres, pgr)
            for k in range(8):
                mm(pg[0:4, 256:256 + ntok], M.WGT[:, k, 4:8], M.XNTg[:, k, 0:ntok], k == 0, k == 7, [M.WGTr] + xres, pgr)
            I("act", "activation", [pgr, gb4r], [M.Gr], out=M.G_IG[:, 1:ntok + 1], in_=pg[0:4, 0:ntok], func=AF.Identity,
              bias=gb4[:, 0:1])
            I("act", "activation", [pgr, gb4r], [M.Gr], out=M.G_E[:, 0:ntok], in_=pg[0:4, 256:256 + ntok], func=AF.Exp,
              bias=gb4[:, 1:2], scale=-1.0)
            I("act", "activation", [M.Gr], [M.Gr], out=M.G_L1[:, 0:ntok], in_=M.G_E[:, 0:ntok], func=AF.Ln, bias=1.0)
            if prefix == "sample":
                return
            if prefix:
                I("dve", "tensor_scalar", [M.Gr, gb4r], [M.Gr], out=M.G_L1[:, 0:ntok], in0=M.G_L1[:, 0:ntok], scalar1=gb4[:, 2:3],
                  scalar2=None, op0=ALU.mult)
            I("dve", "tensor_tensor_scan", [M.Gr, M.G_Br, onesfr], [M.G_Br], out=M.G_B[:, 1:ntok + 1], data0=onesf[0:4, 0:ntok],
              data1=M.G_L1[:, 0:ntok], initial=M.G_B[:, 0:1], op0=ALU.mult, op1=ALU.subtract)
            I("dve", "scalar_tensor_tensor", [M.Gr, M.G_Br, gb4r], [M.G_Ar], out=M.G_A[:, 0:ntok], in0=M.G_IG[:, 1:ntok + 1],
              scalar=(gb4[:, 3:4] if prefix else 0.0), in1=M.G_B[:, 1:ntok + 1], op0=ALU.add, op1=ALU.subtract)
            I("dve", "tensor_tensor_scan", [M.G_Ar, M.G_Mr, onesfr], [M.G_Mr], out=M.G_M[:, 1:ntok + 1], data0=onesf[0:4, 0:ntok],
              data1=M.G_A[:, 0:ntok], initial=M.G_M[:, 0:1], op0=ALU.mult, op1=ALU.max)
            I("dve", "tensor_tensor", [M.G_Br, M.G_Mr], [M.G_BMr], out=M.G_BM[:, 0:ntok], in0=M.G_B[:, 1:ntok + 1],
              in1=M.G_M[:, 1:ntok + 1], op=ALU.add)
            for ci in range(ntok // 128):
                I("dve", "tensor_scalar", [M.G_Mr], [M.G_DMr], out=M.G_DM[:, ci * 128:(ci + 1) * 128],
                  in0=M.G_M[:, 1 + ci * 128:1 + (ci + 1) * 128], scalar1=M.G_M[:, ci * 128:ci * 128 + 1], scalar2=None,
                  op0=ALU.subtract)

        def gates_carry(ntok):
            I("dve", "tensor_copy", [M.G_Br], [M.G_Br], out=M.G_B[:, 0:1], in_=M.G_B[:, ntok:ntok + 1])
            I("dve", "tensor_copy", [M.G_Mr], [M.G_Mr], out=M.G_M[:, 0:1], in_=M.G_M[:, ntok:ntok + 1])

        def state_update(ti, c0, refresh_cb):
            pw, pwr = bank()
            I4 = SEL[:, 0:512].rearrange("p (h t) -> p h t", t=128)[:, :, 0]
            I("dve", "tensor_scalar", [selr, M.G_Mr], [M.DGr], out=M.DG[:, 0:4], in0=I4, scalar1=M.G_M[:, c0 + 128:c0 + 129],
              scalar2=-1.0, op0=ALU.mult, op1=ALU.mult)
            I("dve", "tensor_scalar", [selr, M.G_DMr], [M.DGr], out=M.DG[:, 4:8], in0=I4, scalar1=M.G_DM[:, c0 + 127:c0 + 128],
              scalar2=-1.0, op0=ALU.mult, op1=ALU.mult)
            mm(pw[:, 0:4], M.G_A[:, c0:c0 + 128], I4, True, False, [M.G_Ar, selr], pwr)
            mm(pw[:, 0:4], onesf[0:4, 0:128], M.DG[:, 0:4], False, True, [onesfr, M.DGr], pwr)
            mm(pw[:, 4:8], onesf[0:4, 0:128], M.DG[:, 4:8], True, True, [onesfr, M.DGr], pwr)
            I("act", "activation", [pwr], [M.WKCr], out=M.WKC[:, 0:8], in_=pw[:, 0:8], func=AF.Exp)
            for h in range(4):
                I("dve", "tensor_scalar", [M.MVaugr[ti], M.WKCr], [M.VWr[h]], out=M.VW[:, h, :], in0=M.MVaug[:, ti, h, :],
                  scalar1=M.WKC[:, h:h + 1], scalar2=None, op0=ALU.mult)
            for h0 in (0, 2):
                dc, dcr = bank()
                for hh in range(2):
                    h = h0 + hh
                    mm(dc[0:64, hh * 129:(hh + 1) * 129], M.MKtok[:, ti, h * 64:(h + 1) * 64], M.VW[:, h, :], True, True,
                       [M.MKtokr[ti], M.VWr[h]], dcr)
                for hh in range(2):
                    h = h0 + hh
                    I("dve", "scalar_tensor_tensor", [Cstr[h], M.WKCr, dcr], [Cstr[h]], out=Cst[:, h, :], in0=Cst[:, h, :],
                      scalar=M.WKC[0:64, 4 + h:5 + h], in1=dc[0:64, hh * 129:(hh + 1) * 129], op0=ALU.mult, op1=ALU.add)
            if refresh_cb:
                for h in range(4):
                    I("act", "activation", [Cstr[h]], [M.Cbr[h]], out=M.Cb[:, h, 0:129], in_=Cst[:, h, :], func=AF.Copy)
                    I("act", "activation", [Cstr[h]], [M.Cbr[h]], out=M.Cb[:, h, 129:257],
                      in_=Cst[:, h, 128:129].broadcast_to([64, 128]), func=AF.Copy)

        def mlstm_chunk(ti, c0, mbias, mbiasr, inter=True, inter_fn=None):
            cs = slice(c0, c0 + 128)
            pwt, pwtr = bank()
            for h in range(4):
                o = pwt[:, h * 128:(h + 1) * 128]
                mm(o, M.G_A[:, cs], SELh(h), True, False, [M.G_Ar, selr], pwtr)
                mm(o, NSELh(h), M.G_M[:, c0 + 1:c0 + 129], False, False, [M.G_Mr, selr], pwtr)
                mm(o, identb[:], mbias, False, True, [identr, mbiasr], pwtr)
            I("act", "activation", [pwtr], [M.WTr], out=M.WT[:, :, :], in_=pwt[:, :].rearrange("p (h t) -> p h t", h=4), func=AF.Exp)
            pqk, pqkr = bank()
            for h in range(4):
                mm(pqk[:, h * 128:(h + 1) * 128], M.MKT[:, h, cs], M.MQT[:, h, cs], True, True, [M.MKTr, M.MQTr], pqkr)
            I("dve", "tensor_tensor", [pqkr, M.WTr], [M.STr], out=M.ST[:, :, :], in0=pqk[:, :].rearrange("p (h t) -> p h t", h=4),
              in1=M.WT[:, :, :], op=ALU.mult)
            pwi, pwir = bank()
            for h in range(4):
                mm(pwi[:, h * 128:(h + 1) * 128], NSELh(h), M.G_DM[:, cs], True, True, [M.G_DMr, selr], pwir)
            I("act", "activation", [pwir], [M.WIr], out=M.WI[:, :, :], in_=pwi[:, :].rearrange("p (h t) -> p h t", h=4), func=AF.Exp)
            I("dve", "tensor_tensor", [M.MQTr, M.WIr], [M.QWr], out=M.QW[:, :, :], in0=M.MQT[:, :, cs], in1=M.WI[0:64, :, :], op=ALU.mult)
            plb, plbr = bank()
            for h in range(4):
                mm(plb[:, h * 128:(h + 1) * 128], NSELh(h), M.G_BM[:, cs], True, True, [M.G_BMr, selr], plbr)
            I("act", "activation", [plbr], [M.LOWBr], out=M.LOWB[:, :, :], in_=plb[:, :].rearrange("p (h t) -> p h t", h=4), func=AF.Exp)
            pnum, pnumr = banks[5], bres[5]
            pden, pdenr = banks[6], bres[6]
            if inter_fn is not None:
                inter_fn("pre")
            for h in range(4):
                o = pnum[:, h * 128:(h + 1) * 128]
                mm(o, M.MVaug[:, ti, h, 0:128], M.ST[:, h, :], True, False, [M.MVaugr[ti], M.STr], pnumr)
                if inter_fn is not None:
                    inter_fn("num", h, pnum, pnumr)
                else:
                    mm(o, M.Cb[:, h, 0:128], M.QW[:, h, :], False, True, [M.Cbr[h], M.QWr], pnumr)
            for h in range(4):
                o = pden[:, h * 128:(h + 1) * 128]
                mm(o, onesb[:], M.ST[:, h, :], True, False, [onesbr, M.STr], pdenr)
                if inter_fn is not None:
                    inter_fn("den", h, pden, pdenr)
                else:
                    mm(o, M.Cb[:, h, 129:257], M.QW[:, h, :], False, True, [M.Cbr[h], M.QWr], pdenr)
            return pnum, pnumr, pden, pdenr

        def mlstm_finish(pnum, pnumr, pden, pdenr, c0, hres):
            cs = slice(c0, c0 + 128)
            v4 = lambda b: b[:, :].rearrange("p (h t) -> p h t", h=4)
            I("act", "activation", [pdenr], [M.T1r], out=M.T1[:, :, :], in_=v4(pden), func=AF.Abs)
            I("dve", "tensor_tensor", [M.T1r, M.LOWBr], [M.T1r], out=M.T1[:, :, :], in0=M.T1[:, :, :], in1=M.LOWB[:, :, :], op=ALU.max)
            I("act", "activation", [M.T1r], [M.T1r], out=M.T1[:, :, :], in_=M.T1[:, :, :], func=AF.Square, scale=float(np.sqrt(EPS)))
            I("act", "activation", [pnumr], [M.USQr], out=M.USQ[:, :, :], in_=v4(pnum), func=AF.Square)
            pss, pssr = bank()
            mm(pss[:, :], onesb[:], M.USQ[:, :, :], True, True, [onesbr, M.USQr], pssr)
            I("dve", "scalar_tensor_tensor", [pssr, M.T1r], [M.T2r], out=M.T2[:, :, :], in0=v4(pss), scalar=1.0 / 128, in1=M.T1[:, :, :],
              op0=ALU.mult, op1=ALU.add)
            I("act", "activation", [M.T2r], [M.T2r], out=M.T2[:, :, :], in_=M.T2[:, :, :], func=AF.Ln)
            I("act", "activation", [M.T2r], [M.T2r], out=M.T2[:, :, :], in_=M.T2[:, :, :], func=AF.Exp, scale=-0.5)
            I("dve", "tensor_tensor", [pnumr, M.T2r], [M.T1r], out=M.T1[:, :, :], in0=v4(pnum), in1=M.T2[:, :, :], op=ALU.mult)
            for h in range(4):
                I("dve", "scalar_tensor_tensor", [M.T1r, gheadr, M.SGTr], [hres], out=M.HMT[:, h, cs], in0=M.T1[:, h, :],
                  scalar=gheadc[:, h:h + 1], in1=M.SGT[:, h, cs], op0=ALU.mult, op1=ALU.mult)

        def swa_tile(ti, kcol0, vslots, mb, mbr, ktres):
            qs = slice(ti * 128, (ti + 1) * 128)
            for h in range(2):
                bks = [bank(), bank()]
                for g in range(4):
                    bk, bkr = bks[g // 2]
                    o = bk[:, (g % 2) * 256:(g % 2 + 1) * 256]
                    mm(o, M.QT[:, 4 * h + g, qs], M.KT[:, h, kcol0:kcol0 + 256], True, False, [M.QTr] + ktres, bkr)
                    mm(o, identb[:], mb, False, True, [identr, mbr], bkr)
                for j in range(2):
                    I("dve", "reduce_max", [bks[j][1]], [M.smr], out=M.sm_st[:, 2 * j:2 * j + 2],
                      in_=bks[j][0][:, :].rearrange("p (a b) -> p a b", a=2), axis=AX.X)
                I("dve", "tensor_scalar", [M.smr], [M.smr], out=M.sm_st[:, 0:4], in0=M.sm_st[:, 0:4], scalar1=-0.125, scalar2=None,
                  op0=ALU.mult)
                I("dve", "tensor_tensor", [M.smr, sinkbr], [M.smr], out=M.sm_st[:, 0:4], in0=M.sm_st[:, 0:4],
                  in1=sinkb[:, 8 + 4 * h:12 + 4 * h], op=ALU.min)
                for g in range(4):
                    bk, bkr = bks[g // 2]
                    I("act", "activation", [bkr, M.smr], [M.Er, M.smr], out=M.Ebuf[:, g, :], in_=bk[:, (g % 2) * 256:(g % 2 + 1) * 256],
                      func=AF.Exp, bias=M.sm_st[:, g:g + 1], scale=0.125, accum_out=M.sm_st[:, 4 + g:5 + g])
                I("dve", "tensor_tensor", [M.smr, sinkbr], [M.smr], out=M.sm_st[:, 8:12], in0=M.sm_st[:, 0:4],
                  in1=sinkb[:, 4 * h:4 * h + 4], op=ALU.add)
                I("act", "activation", [M.smr], [M.smr], out=M.sm_st[:, 8:12], in_=M.sm_st[:, 8:12], func=AF.Exp)
                I("dve", "tensor_tensor", [M.smr], [M.smr], out=M.sm_st[:, 8:12], in0=M.sm_st[:, 8:12], in1=M.sm_st[:, 4:8], op=ALU.add)
                I("dve", "reciprocal", [M.smr], [M.smr], out=M.sm_st[:, 12:16], in_=M.sm_st[:, 8:12])
                for g in range(4):
                    if g % 2 == 0:
                        I("act", "activation", [M.Er, M.smr], [M.Er], out=M.Ebuf[:, g, :], in_=M.Ebuf[:, g, :], func=AF.Copy,
                          scale=M.sm_st[:, 12 + g:13 + g])
                    else:
                        I("dve", "tensor_scalar", [M.Er, M.smr], [M.Er], out=M.Ebuf[:, g, :], in0=M.Ebuf[:, g, :],
                          scalar1=M.sm_st[:, 12 + g:13 + g], scalar2=None, op0=ALU.mult)
                for kb in range(2):
                    for g in range(4):
                        blk = kb * 4 + (g % 2) * 2 + g // 2
                        I("pe", "transpose", [M.Er, identr], [*tbhr], out=tb[:, blk * 128:(blk + 1) * 128],
                          in_=M.Ebuf[:, g, kb * 128:(kb + 1) * 128], identity=identb[:])
                pb = 0
                if h == 0:
                    I("dve", "tensor_copy", [*tbhr], [M.PTsr[pb]], out=M.PTs[:, pb, :], in_=tb[:, :])
                else:
                    I("act", "activation", [*tbhr], [M.PTsr[pb]], out=M.PTs[:, pb, :], in_=tb[:, :], func=AF.Copy)
                po, por = bank()
                for par in range(2):
                    for kb in range(2):
                        mm(po[par * 64:(par + 1) * 64, 0:256], M.Vt[:, vslots[kb], h * 64:(h + 1) * 64],
                           M.PTs[:, pb, kb * 512 + par * 256:kb * 512 + (par + 1) * 256], kb == 0, kb == 1,
                           [M.Vtr[vslots[kb]], M.PTsr[pb]], por)
                I("act", "activation", [por], [M.ATTTr[ti]], out=M.ATTT[:, 2 * h:2 * h + 2, qs],
                  in_=po[:, 0:256].rearrange("p (g q) -> p g q", g=2), func=AF.Copy)

        def wout_tile(ti, t):
            qs = slice(ti * 128, (ti + 1) * 128)
            for c in range(2):
                bk, bkr = bank()
                cc = slice(c * 512, (c + 1) * 512)
                for hg in range(4):
                    mm(bk[:, :], M.ATTT[:, hg, qs], M.WOA[:, hg, cc], hg == 0, False, [M.ATTTr[ti], M.WOAr], bkr)
                for h in range(4):
                    mm(bk[:, :], M.HMT[:, h, qs], M.WOM[:, h, cc], False, h == 3, [M.HMTr[ti], M.WOMr], bkr)
                I("dve", "tensor_tensor", [Yr[t], bkr], [Yr[t]], out=Y[:, t, cc], in0=Y[:, t, cc], in1=bk[:, :], op=ALU.add)

        xpre_t = xpre_d.rearrange("(t p) d -> t p d", p=128)
        xp_t = xp_d.rearrange("(t p) d -> t p d", p=128)
        for t in range(NTP):
            P.dma("sp", Y[:, t, :], xpre_t[t], Yr[t], writes=[Yr[t]])
        for g0 in range(0, NTP, GT):
            for ti in range(GT):
                t = g0 + ti
                norm_T(Y[:, t, :], Yr[t], 0, M.XNTg[:, :, ti * 128:(ti + 1) * 128], [M.XNTgr[ti]])
                P.dma("sp", Y[:, t, :], xp_t[t], Yr[t], writes=[Yr[t]])
            for ti in range(GT):
                tok_major(ti, slice(ti * 128, (ti + 1) * 128), M.XNTgr[ti], 0)
            gates(GT * 128, M.XNTgr, True)
            if g0 + GT == NTP:
                bk, bkr = bank()
                for h in range(2):
                    for k in range(8):
                        mm(bk[0:64, h * 128:(h + 1) * 128], M.WK[:, k, h * 64:(h + 1) * 64], M.XNTg[:, k, (GT - 1) * 128:GT * 128],
                           k == 0, k == 7, [M.WKr, M.XNTgr[GT - 1]], bkr)
                I("act", "activation", [bkr], [M.KTr[0]], out=M.KT[:, :, 0:128],
                  in_=bk[0:64, 0:256].rearrange("p (a b) -> p a b", a=2), func=AF.Copy)
            for ti in range(GT):
                last = (g0 + ti == NTP - 1)
                state_update(ti, ti * 128, last)
            gates_carry(GT * 128)

        for g0 in range(0, NTP, GT):
            strs = []
            for ti in range(GT):
                t = g0 + ti
                P.rec_begin()
                norm_T(Y[:, t, :], Yr[t], 0, M.XNTg[:, :, ti * 128:(ti + 1) * 128], [M.XNTgr[ti]], half=ti)
                strs.append(P.rec_end())
            P.merge(strs)
            for ti in range(GT):
                t = g0 + ti
                tok_major(ti, slice(ti * 128, (ti + 1) * 128), M.XNTgr[ti], 1 + t, want_kv_out=(True if t == NTP - 1 else None))
            gates(M.NG, M.XNTgr, False)
            feat64(M.WK, M.WKr, 2, M.KT, [M.KTr[1 + g0 + i] for i in range(GT)], M.NG, M.XNTgr, dcol0=128 + g0 * 128)
            feat64(M.WQ, M.WQr, 8, M.QT, [M.QTr], M.NG, M.XNTgr)
            feat64(M.WMQ, M.WMQr, 4, M.MQT, [M.MQTr], M.NG, M.XNTgr)
            feat64(M.WMK, M.WMKr, 4, M.MKT, [M.MKTr], M.NG, M.XNTgr, scale=0.125)
            for h0 in (0, 2):
                bk, bkr = bank()
                for hh in range(2):
                    h = h0 + hh
                    for k in range(8):
                        mm(bk[:, hh * 256:hh * 256 + M.NG], M.WOG[:, k, h * 128:(h + 1) * 128], M.XNTg[:, k, 0:M.NG], k == 0, k == 7,
                           [M.WOGr] + M.XNTgr, bkr)
                sgv = M.SGT[:, h0:h0 + 2, :]
                I("act", "activation", [bkr], [M.SGTr], out=sgv, in_=bk[:, :].rearrange("p (a b) -> p a b", a=2)[:, :, 0:M.NG],
                  func=AF.Exp, scale=-1.0)
                I("act", "activation", [M.SGTr], [M.SGTr], out=sgv, in_=sgv, func=AF.Ln, bias=1.0)
                I("act", "activation", [M.SGTr], [M.SGTr], out=sgv, in_=sgv, func=AF.Exp, scale=-1.0)
            pend = None
            for ti in range(GT):
                t = g0 + ti
                P.rec_begin(); bset[0] = [2, 3]
                pn = mlstm_chunk(ti, ti * 128, mbcaus[:], mbcausr)
                mlstm_finish(*pn, ti * 128, M.HMTr[ti])
                state_update(ti, ti * 128, True)
                s_ml = P.rec_end()
                P.rec_begin(); bset[0] = [0, 1]
                swa_tile(ti, t * 128, (t, t + 1), (mbfirst[:] if t == 0 else mbband[:]), (mbfirstr if t == 0 else mbbandr),
                         [M.KTr[t], M.KTr[t + 1]])
                s_sw = P.rec_end()
                strs = [s_ml, s_sw]
                if pend is not None:
                    P.rec_begin(); bset[0] = [4]
                    wout_tile(*pend)
                    strs.append(P.rec_end())
                P.merge(strs)
                pend = (ti, t)
            bset[0] = [0, 1, 2, 3, 4]
            wout_tile(*pend)
            if debug and g0 == DBG_G0:
                dA = dout("dbg_att", [64, 8, M.NG]); dH = dout("dbg_hm", [128, 4, M.NG])
                P.dma("pool", dA, M.ATTT[:, :, :], M.ATTTr[0], reads=M.ATTTr)
                P.dma("pool", dH, M.HMT[:, :, :], M.HMTr[0], reads=M.HMTr)
            gates_carry(M.NG)

        CO = A([4, 64], F32); COr = AR("CO")
        for h in range(4):
            bk, bkr = bank()
            mm(bk[:, 0:64], Cst[:, h, 0:128], identf[0:64, 0:64], True, True, [Cstr[h], identfr], bkr)
            I("act", "activation", [bkr], [COr], out=CO[:, h, :], in_=bk[:, 0:64], func=AF.Copy)
        P.dma("sp", Cp_o.rearrange("h p k -> p h k"), CO[:, :, :], COr, reads=[COr])
        for h in range(4):
            P.dma("sp", np_o[h, :].rearrange("(k o) -> k o", o=1), Cst[:, h, 128:129], Cstr[h], reads=[Cstr[h]], allow_slow_non_contiguous=True)
        P.dma("sp", mp_o, M.G_BM[:, M.NG - 1:M.NG], M.G_BMr, reads=[M.G_BMr], allow_slow_non_contiguous=True)

        new_phase()
        MA = M
        M = alloc_mixer(1, 1, 1)
        R0_olds = [M.WQr, M.WTOKr, M.WMQr, M.WOGr, M.WGTr]
        TS = NTP
        P.dma("sp", Y[:, TS, :], xs_d, Yr[TS], writes=[Yr[TS]])
        shk = P.res("shk"); shv = P.res("shv")
        P.dma("sp", sks_o[:, 0:120, :], csk_d[:, 8:128, :], shk, writes=[shk])
        P.dma("sp", svs_o[:, 0:120, :], csv_d[:, 8:128, :], shv, writes=[shv])
        CKn = A([16, 128], BF16); CKnr = AR("CKn")
        CV = A([16, 128], BF16); CVr = AR("CV")
        CKT = A([16, 128], BF16, parts=64); CKTr = AR("CKT")
        SMC = A([1, 128], BF16, parts=32)[:, 0, :]; SMCr = AR("SMC")
        SMN = A([16, 128], BF16, parts=32); SMNr = AR("SMN")
        SINKC = A([1, 4], F32, parts=32)[:, 0, :]; SINKCr = AR("SINKC")
        mbcs = A([1, 128], BF16)[:, 0, :]; mbcsr = AR("mbcs")
        PNs = A([4, 256], BF16, parts=32); PNsr = [AR(f"PNs{i}") for i in range(4)]
        sms = A([4, 8], F32, parts=32); smsr = [AR(f"sms{i}") for i in range(4)]
        PTS = A([1, 1024], BF16)[:, 0, :]; PTSr = AR("PTS")
        M0 = A([1, 16], F32, parts=4)[:, 0, :]; M0r = AR("M0")
        MTe = A([1, 128], F32, parts=4)[:, 0, :]; MTer = AR("MTe")
        DMT = A([1, 16], F32, parts=4)[:, 0, :]; DMTr = AR("DMT")
        E16 = A([1, 16], F32)[:, 0, :]; E16r = AR("E16")
        EW = A([4, 16], BF16); EWr = AR("EW")
        WCB = A([4, 16], F32); WCBr = AR("WCB")
        SNn = A([1, 64], F32, parts=64)[:, 0, :]; SNnr = AR("SNn")
        SNT = A([1, 64], F32, parts=64)[:, 0, :]; SNTr = AR("SNT")
        NNT = A([1, 64], F32, parts=64)[:, 0, :]; NNTr = AR("NNT")
        NNo = A([1, 64], F32, parts=64)[:, 0, :]; NNor = AR("NNo")
        BTf = A([1, 128], F32)[:, 0, :]; BTfr = AR("BTf")
        P.dma("pool", CKn[:, :, :], csk_d.rearrange("j p c -> p j c"), CKnr, writes=[CKnr])
        P.dma("pool", CV[:, :, :], csv_d.rearrange("j p c -> p j c"), CVr, writes=[CVr])
        P.dma("pool", SMC, smc_d, SMCr, writes=[SMCr])
        P.dma("pool", SMN[:, :, :], smn_d, SMNr, writes=[SMNr])
        P.dma("sp", SINKC[:, 0:2], sinkcol_d, SINKCr, writes=[SINKCr])
        I("dve", "tensor_scalar", [SINKCr], [SINKCr], out=SINKC[:, 2:4], in0=SINKC[:, 0:2], scalar1=-1.0, scalar2=None, op0=ALU.mult)
        P.dma("pool", mbcs, mb_causs_d, mbcsr, writes=[mbcsr])
        P.dma("sp", M0, sm_d.rearrange("j h -> h j"), M0r, writes=[M0r], allow_slow_non_contiguous=True)
        P.dma("sp", E16, eseq_d, E16r, writes=[E16r])
        P.dma("sp", SNn, sn_d.rearrange("j h k -> (j h) k"), SNnr, writes=[SNnr])

        norm_T(Y[:, TS, :], Yr[TS], 0, M.XNTg[:, :, 0:128], [M.XNTgr[0]])
        tok_major(0, slice(0, 128), M.XNTgr[0], 0, want_kv_out="sample")
        gates(128, M.XNTgr, "sample")
        feat64(M.WK, M.WKr, 2, M.KT, [M.KTr[0]], 128, M.XNTgr, dcol0=0)
        feat64(M.WQ, M.WQr, 8, M.QT, [M.QTr], 128, M.XNTgr)
        feat64(M.WMQ, M.WMQr, 4, M.MQT, [M.MQTr], 128, M.XNTgr)
        feat64(M.WMK, M.WMKr, 4, M.MKT, [M.MKTr], 128, M.XNTgr, scale=0.125)
        for h0 in (0, 2):
            bk, bkr = bank()
            for hh in range(2):
                h = h0 + hh
                for k in range(8):
                    mm(bk[:, hh * 256:hh * 256 + 128], M.WOG[:, k, h * 128:(h + 1) * 128], M.XNTg[:, k, 0:128], k == 0, k == 7,
                       [M.WOGr] + M.XNTgr, bkr)
            sgv = M.SGT[:, h0:h0 + 2, :]
            I("act", "activation", [bkr], [M.SGTr], out=sgv, in_=bk[:, :].rearrange("p (a b) -> p a b", a=2)[:, :, 0:128],
              func=AF.Exp, scale=-1.0)
            I("act", "activation", [M.SGTr], [M.SGTr], out=sgv, in_=sgv, func=AF.Ln, bias=1.0)
            I("act", "activation", [M.SGTr], [M.SGTr], out=sgv, in_=sgv, func=AF.Exp, scale=-1.0)
        for j in range(16):
            I("dve", "tensor_tensor_scan", [M.Gr, M.G_Br, onesfr], [M.G_Br], out=M.G_B[:, 1 + 8 * j:9 + 8 * j],
              data0=onesf[0:4, 0:8], data1=M.G_L1[:, 8 * j:8 * j + 8], initial=0.0, op0=ALU.mult, op1=ALU.subtract)
        I("dve", "tensor_tensor", [M.Gr, M.G_Br], [M.G_Ar], out=M.G_A[:, 0:128], in0=M.G_IG[:, 1:129], in1=M.G_B[:, 1:129],
          op=ALU.subtract)
        for j in range(16):
            I("dve", "tensor_tensor_scan", [M.G_Ar, M.G_Mr, onesfr, M0r], [M.G_Mr], out=M.G_M[:, 1 + 8 * j:9 + 8 * j],
              data0=onesf[0:4, 0:8], data1=M.G_A[:, 8 * j:8 * j + 8], initial=M0[:, j:j + 1], op0=ALU.mult, op1=ALU.max)
        I("dve", "tensor_tensor", [M.G_Br, M.G_Mr], [M.G_BMr], out=M.G_BM[:, 0:128], in0=M.G_B[:, 1:129], in1=M.G_M[:, 1:129],
          op=ALU.add)
        GM3 = M.G_M[:, 1:129].rearrange("p (j i) -> p j i", i=8)
        I("dve", "tensor_tensor", [M.G_Mr, M0r], [M.G_DMr], out=M.G_DM[:, 0:128].rearrange("p (j i) -> p j i", i=8), in0=GM3,
          in1=M0[:, :].unsqueeze(2).broadcast_to([4, 16, 8]), op=ALU.subtract)
        I("dve", "tensor_copy", [M.G_Mr], [MTer], out=MTe[:, :].rearrange("p (j i) -> p j i", i=8),
          in_=GM3[:, :, 7:8].broadcast_to([4, 16, 8]))
        I("dve", "tensor_tensor", [M.G_Mr, M0r], [DMTr], out=DMT[:, :].unsqueeze(2), in0=GM3[:, :, 7:8], in1=M0[:, :].unsqueeze(2),
          op=ALU.subtract)
        P.dma("sp", ms_o.rearrange("j h -> h j"), M.G_BM[:, 0:128].rearrange("p (j i) -> p j i", i=8)[:, :, 7], M.G_BMr,
              reads=[M.G_BMr], allow_slow_non_contiguous=True)

        pair_i = [0]
        QS = A([2, 16, 32], BF16, parts=64); QSr = AR("QS")
        for h in range(2):
            for par in range(2):
                I("act", "activation", [M.QTr], [QSr],
                  out=QS[:, h, :, par * 16:(par + 1) * 16].rearrange("p j (gp i) -> p j gp i", i=8),
                  in_=M.QT[:, 4 * h:4 * h + 4, :].rearrange("p (gp two) t -> p two gp t", two=2)[:, par].rearrange(
                      "p gp (j i) -> p j gp i", i=8), func=AF.Copy)
        for h in range(2):
            for q4 in range(2):
                for jj in range(8):
                    j = q4 * 8 + jj
                    I("pe", "transpose", [CKnr, identr], [*tbhr], out=tb[0:64, jj * 128:(jj + 1) * 128],
                      in_=CKn[:, j, h * 64:(h + 1) * 64], identity=identb[:])
                I("act", "activation", [*tbhr], [CKTr], out=CKT[:, q4 * 8:(q4 + 1) * 8, :],
                  in_=tb[0:64, :].rearrange("p (a b) -> p a b", a=8), func=AF.Copy)
            for half in range(2):
                po, por = banks[4], bres[4]
                strs = []
                for sk in range(4):
                    P.rec_begin(); bset[0] = [sk]
                    for jj in (sk, sk + 4):
                        j = half * 8 + jj
                        b = sk
                        bk, bkr = bank()
                        lq = QS[:, h, j, :]
                        mm(bk[0:32, 0:128], lq, CKT[:, j, :], True, False, [QSr, CKTr], bkr)
                        mm(bk[0:32, 0:128], identb[0:32, 0:32], SMC, False, True, [identr, SMCr], bkr)
                        mm(bk[0:32, 128:256], lq, M.KT[:, h, 0:128], True, False, [QSr, M.KTr[0]], bkr)
                        mm(bk[0:32, 128:256], identb[0:32, 0:32], SMN[:, j, :], False, True, [identr, SMNr], bkr)
                        st_ = sms[:, b, :]
                        I("dve", "reduce_max", [bkr], [smsr[b]], out=st_[:, 0:1], in_=bk[0:32, 0:256], axis=AX.X)
                        I("dve", "tensor_scalar", [smsr[b]], [smsr[b]], out=st_[:, 0:1], in0=st_[:, 0:1], scalar1=-0.125,
                          scalar2=None, op0=ALU.mult)
                        I("dve", "tensor_tensor", [smsr[b], SINKCr], [smsr[b]], out=st_[:, 0:1], in0=st_[:, 0:1],
                          in1=SINKC[:, 2 + h:3 + h], op=ALU.min)
                        I("act", "activation", [bkr, smsr[b]], [PNsr[b], smsr[b]], out=PNs[:, b, :], in_=bk[0:32, 0:256],
                          func=AF.Exp, bias=st_[:, 0:1], scale=0.125, accum_out=st_[:, 1:2])
                        I("act", "activation", [SINKCr, smsr[b]], [smsr[b]], out=st_[:, 2:3], in_=SINKC[:, h:h + 1], func=AF.Exp,
                          bias=st_[:, 0:1])
                        I("dve", "tensor_tensor", [smsr[b]], [smsr[b]], out=st_[:, 2:3], in0=st_[:, 2:3], in1=st_[:, 1:2],
                          op=ALU.add)
                        I("dve", "reciprocal", [smsr[b]], [smsr[b]], out=st_[:, 3:4], in_=st_[:, 2:3])
                        I("dve", "tensor_scalar", [PNsr[b], smsr[b]], [PNsr[b]], out=PNs[:, b, :], in0=PNs[:, b, :],
                          scalar1=st_[:, 3:4], scalar2=None, op0=ALU.mult)
                        for c2 in range(2):
                            I("pe", "transpose", [PNsr[b], identr], [*tbhr], out=tb[:, jj * 64 + c2 * 32:jj * 64 + (c2 + 1) * 32],
                              in_=PNs[:, b, c2 * 128:(c2 + 1) * 128], identity=identb[0:32, 0:32])
                    strs.append(P.rec_end())
                P.merge(strs)
                bset[0] = [0, 1, 2, 3]
                I("dve", "tensor_copy", [*tbhr], [PTSr], out=PTS[:, 0:512], in_=tb[:, 0:512])
                for jj in range(8):
                    j = half * 8 + jj
                    for par in range(2):
                        o = po[par * 64:(par + 1) * 64, jj * 16:(jj + 1) * 16]
                        mm(o, CV[:, j, h * 64:(h + 1) * 64], PTS[:, jj * 64 + par * 16:jj * 64 + par * 16 + 16], True, False,
                           [CVr, PTSr], por)
                        mm(o, M.Vt[:, 0, h * 64:(h + 1) * 64], PTS[:, jj * 64 + 32 + par * 16:jj * 64 + 32 + par * 16 + 16], False,
                           True, [M.Vtr[0], PTSr], por)
                I("act", "activation", [por], [M.ATTTr[0]],
                  out=M.ATTT[:, 2 * h:2 * h + 2, half * 64:(half + 1) * 64].rearrange("p c (j i) -> p j c i", i=8),
                  in_=po[:, 0:128].rearrange("p (j c i) -> p j c i", c=2, i=8), func=AF.Copy)

        bset[0] = [0, 1, 2, 3, 4]
        off = 0
        SCf, off = A_at(off, [64, 64], F32); SCfr = ARalias("SCf", R0_olds)
        SCT, off = A_at(off, [64, 128], BF16, parts=64); SCTr = ARalias("SCT", R0_olds)
        QN, off = A_at(off, [4, 128], BF16, parts=64); QNr = ARalias("QN", R0_olds)
        KJ, off = A_at(off, [16, 64], BF16); KJr = ARalias("KJ", R0_olds)
        assert off <= 18496
        P.dma("sp", SCf[:, :, :], sC_d.rearrange("j h p k -> p (j h) k"), SCfr, writes=[SCfr])
        for p4 in range(16):
            bk, bkr = bank()
            for q_ in range(4):
                pr = p4 * 4 + q_
                mm(bk[0:64, q_ * 128:(q_ + 1) * 128], SCf[:, pr, :], identf[:], True, True, [SCfr, identfr], bkr)
            I("act", "activation", [bkr], [SCTr], out=SCT[:, p4 * 4:(p4 + 1) * 4, :],
              in_=bk[0:64, :].rearrange("p (a b) -> p a b", a=4), func=AF.Copy)
        bk, bkr = bank()
        mm(bk[0:64, 0:64], SNn, identf[0:64, 0:64], True, True, [SNnr, identfr], bkr)
        I("act", "activation", [bkr], [SNTr], out=SNT, in_=bk[0:64, 0:64], func=AF.Copy)

        def sample_inter(kind, h=None, pb=None, pbr=None):
            if kind == "pre":
                I("dve", "tensor_tensor", [M.QWr, SNTr], [QNr], out=QN[:, :, :].rearrange("p h (j i) -> p h j i", i=8),
                  in0=M.QW[:, :, :].rearrange("p h (j i) -> p h j i", i=8),
                  in1=SNT.rearrange("p (j h) -> p h j", h=4).unsqueeze(3).broadcast_to([64, 4, 16, 8]), op=ALU.mult)
            elif kind == "num":
                for j in range(16):
                    mm(pb[:, h * 128 + 8 * j:h * 128 + 8 * j + 8], SCT[:, j * 4 + h, :], M.QW[:, h, 8 * j:8 * j + 8], False, j == 15,
                       [SCTr, M.QWr], pbr)
            else:
                mm(pb[:, h * 128:(h + 1) * 128], onesb[0:64, :], QN[:, h, :], False, True, [onesbr, QNr], pbr)

        pn = mlstm_chunk(0, 0, mbcs, mbcsr, inter_fn=sample_inter)
        mlstm_finish(*pn, 0, M.HMTr[0])
        wout_tile(0, TS)

        pw, pwr = bank()
        for h in range(4):
            mm(pw[:, h:h + 1], M.G_A[:, 0:128], SELh(h, 1), True, False, [M.G_Ar, selr], pwr)
            mm(pw[:, h:h + 1], MTe, SEL[:, 512 + h * 128:512 + h * 128 + 1], False, True, [MTer, selr], pwr)
        for h in range(4):
            mm(pw[:, 8 + 16 * h:8 + 16 * (h + 1)], NSELh(h), DMT, True, True, [DMTr, selr], pwr)
        I("act", "activation", [pwr], [M.WKCr], out=M.WKC[:, 0:4], in_=pw[:, 0:4], func=AF.Exp)
        I("act", "activation", [pwr], [WCBr], out=WCB[:, :, :], in_=pw[:, 8:72].rearrange("p (h j) -> p h j", h=4), func=AF.Exp)
        for h in range(4):
            I("dve", "tensor_scalar", [M.MVaugr[0], M.WKCr], [M.VWr[h]], out=M.VW[:, h, :], in0=M.MVaug[:, 0, h, :],
              scalar1=M.WKC[:, h:h + 1], scalar2=None, op0=ALU.mult)
            I("dve", "tensor_scalar", [E16r, M.WKCr], [EWr], out=EW[:, h, :], in0=E16, scalar1=M.WKC[:, h:h + 1], scalar2=None,
              op0=ALU.mult)
        bk, bkr = bank()
        for h in range(4):
            mm(bk[0:64, h * 16:(h + 1) * 16], M.MKtok[:, 0, h * 64:(h + 1) * 64], EW[:, h, :], True, True, [M.MKtokr[0], EWr], bkr)
        I("dve", "tensor_tensor", [SNTr, WCBr], [NNTr], out=NNT.rearrange("p (j h) -> p h j", h=4),
          in0=SNT.rearrange("p (j h) -> p h j", h=4), in1=WCB[0:64, :, :], op=ALU.mult)
        I("dve", "tensor_tensor", [NNTr, bkr], [NNTr], out=NNT.rearrange("p (j h) -> p h j", h=4),
          in0=NNT.rearrange("p (j h) -> p h j", h=4), in1=bk[0:64, 0:64].rearrange("p (h j) -> p h j", h=4), op=ALU.add)
        bk2, bk2r = bank()
        mm(bk2[0:64, 0:64], NNT, identf[0:64, 0:64], True, True, [NNTr, identfr], bk2r)
        I("act", "activation", [bk2r], [NNor], out=NNo, in_=bk2[0:64, 0:64], func=AF.Copy)
        P.dma("sp", ns_o.rearrange("j h k -> (j h) k"), NNo, NNor, reads=[NNor])
        for h in range(4):
            I("dve", "tensor_tensor", [M.MKtokr[0], E16r], [KJr], out=KJ[:, :, :],
              in0=M.MKtok[:, 0, h * 64:(h + 1) * 64].unsqueeze(1).broadcast_to([128, 16, 64]),
              in1=E16.unsqueeze(2).broadcast_to([128, 16, 64]), op=ALU.mult)
            for half in range(2):
                bk, bkr = bank()
                mm(bk[:, :], M.VW[:, h, 0:128], KJ[:, half * 8:(half + 1) * 8, :], True, True, [M.VWr[h], KJr], bkr)
                scv = SCf[:, :, :].rearrange("p (j h) k -> p h j k", h=4)[:, h, half * 8:(half + 1) * 8, :]
                I("dve", "tensor_tensor", [SCfr, WCBr], [SCfr], out=scv, in0=scv,
                  in1=WCB[:, h, half * 8:(half + 1) * 8].unsqueeze(2).broadcast_to([128, 8, 64]), op=ALU.mult)
                I("dve", "tensor_tensor", [SCfr, bkr], [SCfr], out=scv, in0=scv,
                  in1=bk[:, :].rearrange("p (j k) -> p j k", k=64), op=ALU.add)
        P.dma("sp", Cs_o.rearrange("j h p k -> p (j h) k"), SCf[:, :, :], SCfr, reads=[SCfr])

        def dump_y(tiles):
            yo = yp_o.rearrange("(t p) d -> t p d", p=128)
            for t in tiles:
                if Yr[t].last_w is None:
                    continue
                if t < NTP:
                    P.dma("sp", yo[t], Y[:, t, :], Yr[t], reads=[Yr[t]])
                else:
                    P.dma("sp", ys_o, Y[:, t, :], Yr[t], reads=[Yr[t]])

        if stage <= 1:
            dump_y(range(NT))
            P.emit()
            return nc, P

        new_phase()
        WCQ = A([8, 256], BF16); WCQr = AR("WCQ")
        WCKV = A([8, 512], BF16); WCKVr = AR("WCKV")
        WCO = A([2, 1024], BF16); WCOr = AR("WCO")
        P.dma("pool", WCQ[:], w_cq_d.rearrange("(k p) n -> p k n", p=128), WCQr, writes=[WCQr])
        P.dma("pool", WCKV[:, :, 0:256], w_ck_d.rearrange("(k p) n -> p k n", p=128), WCKVr, writes=[WCKVr], group=True)
        P.dma("pool", WCKV[:, :, 256:512], w_cv_d.rearrange("(k p) n -> p k n", p=128), WCKVr, writes=[WCKVr], group=True)
        P.dma("pool", WCO[:], w_co_d.rearrange("(c p) n -> p c n", p=128), WCOr, writes=[WCOr])
        MEMX = A([2, D], F32); MEMXr = [AR("MEMX0"), AR("MEMX1")]
        MNT = A([8, 256], BF16); MNTr = [AR("MNT0"), AR("MNT1")]
        MKTm = A([4, 256], BF16, parts=64); MKTmr = AR("MKTm")
        MVm = A([2, 256], BF16); MVmr = AR("MVm")
        MKVo = A([2, 512], F32); MKVor = [AR("MKVo0"), AR("MKVo1")]
        GB = 4
        XNTb = A([8, GB * 128], BF16); XNTbr = [AR(f"XNTb{i}") for i in range(GB)]
        QcT = A([4, GB * 128], BF16, parts=64); QcTr = AR("QcT")
        OcT = A([2, GB * 128], BF16); OcTr = [AR(f"OcT{i}") for i in range(GB)]
        Eb2s = [A([4, 256], BF16) for _ in range(2)]; Eb2rs = [AR("Eb2a"), AR("Eb2b")]
        PT2s = [A([1, 1024], BF16) for _ in range(2)]; PT2rs = [AR("PT2a"), AR("PT2b")]
        sm2s = [A([1, 32], F32)[:, 0, :] for _ in range(2)]; sm2rs = [AR("sm2a"), AR("sm2b")]

        mem_t = mem_d.rearrange("(t p) d -> t p d", p=128)
        import os
        SK = os.environ.get("SKIP", "")
        for mt in range(2):
            P.dma("sp", MEMX[:, mt, :], mem_t[mt], MEMXr[mt], writes=[MEMXr[mt]])
        for mt in (range(2) if "noBnorm" not in SK else []):
            norm_T(MEMX[:, mt, :], MEMXr[mt], 2, MNT[:, :, mt * 128:(mt + 1) * 128], [MNTr[mt]])
        for mt in (range(2) if "noBkv" not in SK else []):
            bk, bkr = bank()
            for k in range(8):
                mm(bk[:, :], MNT[:, k, mt * 128:(mt + 1) * 128], WCKV[:, k, :], k == 0, k == 7, [MNTr[mt], WCKVr], bkr)
            if "noBcp1" not in SK:
                I("act", "activation", [bkr], [MKVor[mt]], out=MKVo[:, mt, :], in_=bk[:, :], func=AF.Copy)
            if "noBcp2" not in SK:
                I("act", "activation", [bkr], [MVmr], out=MVm[:, mt, :], in_=bk[:, 256:512], func=AF.Copy)
            if "noBdma" not in SK:
                P.dma("sp", memk_o[mt * 128:(mt + 1) * 128, :], MKVo[:, mt, 0:256], MKVor[mt], reads=[MKVor[mt]], group=True)
                P.dma("sp", memv_o[mt * 128:(mt + 1) * 128, :], MKVo[:, mt, 256:512], MKVor[mt], reads=[MKVor[mt]], group=True)
        for h0 in ((0, 2) if "noBkt" not in SK else []):
            bk, bkr = bank()
            for hh in range(2):
                h = h0 + hh
                for k in range(8):
                    mm(bk[0:64, hh * 256:(hh + 1) * 256], WCKV[:, k, h * 64:(h + 1) * 64], MNT[:, k, :], k == 0, k == 7,
                       [WCKVr] + MNTr, bkr)
            I("act", "activation", [bkr], [MKTmr], out=MKTm[:, h0:h0 + 2, :],
              in_=bk[0:64, :].rearrange("p (a b) -> p a b", a=2), func=AF.Copy)

        def cross_q(ntok, xres):
            for h in range(4):
                bk, bkr = bank()
                for k in range(8):
                    mm(bk[0:64, 0:ntok], WCQ[:, k, h * 64:(h + 1) * 64], XNTb[:, k, 0:ntok], k == 0, k == 7, [WCQr] + xres, bkr)
                I("act", "activation", [bkr], [QcTr], out=QcT[:, h, 0:ntok], in_=bk[0:64, 0:ntok], func=AF.Copy)

        def cross_tile_prompt(ti, sx):
            Eb2, Eb2r, PT2, PT2r, sm2, sm2r, tbx, tbxr = Eb2s[sx], Eb2rs[sx], PT2s[sx], PT2rs[sx], sm2s[sx], sm2rs[sx], tbh[sx], tbhr[sx]
            qs = slice(ti * 128, (ti + 1) * 128)
            bks = [bank(), bank()]
            for h in range(4):
                bk, bkr = bks[h // 2]
                mm(bk[:, (h % 2) * 256:(h % 2 + 1) * 256], QcT[:, h, qs], MKTm[:, h, :], True, True, [QcTr, MKTmr], bkr)
            for j in range(2):
                I("dve", "reduce_max", [bks[j][1]], [sm2r], out=sm2[:, 2 * j:2 * j + 2],
                  in_=bks[j][0][:, :].rearrange("p (a b) -> p a b", a=2), axis=AX.X)
            I("dve", "tensor_scalar", [sm2r], [sm2r], out=sm2[:, 0:4], in0=sm2[:, 0:4], scalar1=-0.125, scalar2=None, op0=ALU.mult)
            for h in range(4):
                bk, bkr = bks[h // 2]
                I("act", "activation", [bkr, sm2r], [Eb2r, sm2r], out=Eb2[:, h, :], in_=bk[:, (h % 2) * 256:(h % 2 + 1) * 256],
                  func=AF.Exp, bias=sm2[:, h:h + 1], scale=0.125, accum_out=sm2[:, 4 + h:5 + h])
            I("dve", "reciprocal", [sm2r], [sm2r], out=sm2[:, 8:12], in_=sm2[:, 4:8])
            for h in range(4):
                if h % 2 == 0:
                    I("act", "activation", [Eb2r, sm2r], [Eb2r], out=Eb2[:, h, :], in_=Eb2[:, h, :], func=AF.Copy,
                      scale=sm2[:, 8 + h:9 + h])
                else:
                    I("dve", "tensor_scalar", [Eb2r, sm2r], [Eb2r], out=Eb2[:, h, :], in0=Eb2[:, h, :],
                      scalar1=sm2[:, 8 + h:9 + h], scalar2=None, op0=ALU.mult)
            po, por = bank()
            for mc in range(2):
                for h in range(4):
                    I("pe", "transpose", [Eb2r, identr], [tbxr], out=tbx[:, h * 128:(h + 1) * 128],
                      in_=Eb2[:, h, mc * 128:(mc + 1) * 128], identity=identb[:])
                if mc == 0:
                    I("dve", "tensor_copy", [tbxr], [PT2r], out=PT2[:, 0, 0:512], in_=tbx)
                else:
                    I("act", "activation", [tbxr], [PT2r], out=PT2[:, 0, 512:1024], in_=tbx, func=AF.Copy)
            for h in range(4):
                for mc in range(2):
                    mm(po[(h % 2) * 64:(h % 2 + 1) * 64, (h // 2) * 128:(h // 2 + 1) * 128], MVm[:, mc, h * 64:(h + 1) * 64],
                       PT2[:, 0, (mc * 4 + h) * 128:(mc * 4 + h + 1) * 128], mc == 0, mc == 1, [MVmr, PT2r], por)
            I("act", "activation", [por], [OcTr[ti]], out=OcT[:, :, qs], in_=po[:, 0:256].rearrange("p (h q) -> p h q", h=2),
              func=AF.Copy)

        def wco_tile(ti, t):
            qs = slice(ti * 128, (ti + 1) * 128)
            for c in range(2):
                bk, bkr = bank()
                cc = slice(c * 512, (c + 1) * 512)
                for h in range(2):
                    mm(bk[:, :], OcT[:, h, qs], WCO[:, h, cc], h == 0, h == 1, [OcTr[ti], WCOr], bkr)
                I("dve", "tensor_tensor", [Yr[t], bkr], [Yr[t]], out=Y[:, t, cc], in0=Y[:, t, cc], in1=bk[:, :], op=ALU.add)

        import os
        for g0 in (range(0, NTP, GB) if "noBloop" not in os.environ.get("SKIP", "") else []):
            for tp_ in range(0, GB, 2):
                strs = []
                for sx in range(2):
                    ti = tp_ + sx
                    P.rec_begin()
                    norm_T(Y[:, g0 + ti, :], Yr[g0 + ti], 1, XNTb[:, :, ti * 128:(ti + 1) * 128], [XNTbr[ti]], half=sx)
                    strs.append(P.rec_end())
                P.merge(strs)
            cross_q(GB * 128, XNTbr)
            for tp_ in range(0, GB, 2):
                strs = []
                for sx in range(2):
                    P.rec_begin(); bset[0] = [0, 1, 2] if sx == 0 else [3, 4, 5]
                    cross_tile_prompt(tp_ + sx, sx)
                    wco_tile(tp_ + sx, g0 + tp_ + sx)
                    strs.append(P.rec_end())
                P.merge(strs)
            bset[0] = [0, 1, 2, 3, 4]
        TS = NTP
        CMn = A([8, 2, 256], BF16); CMnr = AR("CMn")
        CMV = A([16, 2, 256], BF16); CMVr = AR("CMV")
        CMKT = A([8, 4, 256], BF16, parts=64); CMKTr = AR("CMKT")
        Es = A([2, 4, 256], BF16, parts=8); Esr = [AR("Es0"), AR("Es1")]
        sm3 = A([2, 16], F32, parts=8); sm3r = [AR("sm30"), AR("sm31")]
        PT3 = A([1, 1024], BF16)[:, 0, :]; PT3r = AR("PT3")
        P.dma("pool", CMV[:, :, :, :], cmv_d.rearrange("j (c p) f -> p j c f", p=128), CMVr, writes=[CMVr])
        norm_T(Y[:, TS, :], Yr[TS], 1, XNTb[:, :, 0:128], [XNTbr[0]])
        cross_q(128, [XNTbr[0]])
        po3, po3r = banks[5], bres[5]
        for half in range(2):
            P.dma("pool", CMn[:, :, :, :], cmk_d[half * 8:(half + 1) * 8].rearrange("j (c p) f -> p j c f", p=128), CMnr,
                  writes=[CMnr])
            for jj in range(8):
                for h in range(4):
                    for mc in range(2):
                        I("pe", "transpose", [CMnr, identr], [*tbhr], out=tb[0:64, (h * 2 + mc) * 128:(h * 2 + mc + 1) * 128],
                          in_=CMn[:, jj, mc, h * 64:(h + 1) * 64], identity=identb[:])
                I("act", "activation", [*tbhr], [CMKTr], out=CMKT[:, jj, :, :],
                  in_=tb[0:64, :].rearrange("p (h m) -> p h m", h=4), func=AF.Copy)
            strs = []
            for sx in range(2):
                P.rec_begin(); bset[0] = [0, 1] if sx == 0 else [2, 3]
                for jj in range(sx, 8, 2):
                    j = half * 8 + jj
                    b = sx
                    bks = [bank(), bank()]
                    for h in range(4):
                        bk, bkr = bks[h // 2]
                        mm(bk[0:8, (h % 2) * 256:(h % 2 + 1) * 256], QcT[:, h, 8 * j:8 * j + 8], CMKT[:, jj, h, :], True, True,
                           [QcTr, CMKTr], bkr)
                    st_ = sm3[:, b, :]
                    for q_ in range(2):
                        I("dve", "reduce_max", [bks[q_][1]], [sm3r[b]], out=st_[:, 2 * q_:2 * q_ + 2],
                          in_=bks[q_][0][0:8, :].rearrange("p (a b) -> p a b", a=2), axis=AX.X)
                    I("dve", "tensor_scalar", [sm3r[b]], [sm3r[b]], out=st_[:, 0:4], in0=st_[:, 0:4], scalar1=-0.125, scalar2=None,
                      op0=ALU.mult)
                    for h in range(4):
                        bk, bkr = bks[h // 2]
                        I("act", "activation", [bkr, sm3r[b]], [Esr[b], sm3r[b]], out=Es[:, b, h, :],
                          in_=bk[0:8, (h % 2) * 256:(h % 2 + 1) * 256], func=AF.Exp, bias=st_[:, h:h + 1], scale=0.125,
                          accum_out=st_[:, 4 + h:5 + h])
                    I("dve", "reciprocal", [sm3r[b]], [sm3r[b]], out=st_[:, 8:12], in_=st_[:, 4:8])
                    I("dve", "tensor_tensor", [Esr[b], sm3r[b]], [Esr[b]], out=Es[:, b, :, :], in0=Es[:, b, :, :],
                      in1=st_[:, 8:12].unsqueeze(2).broadcast_to([8, 4, 256]), op=ALU.mult)
                    for mc in range(2):
                        for h in range(4):
                            c0_ = j * 64 + (mc * 4 + h) * 8
                            I("pe", "transpose", [Esr[b], identr], [*tbhr], out=tb[:, c0_:c0_ + 8],
                              in_=Es[:, b, h, mc * 128:(mc + 1) * 128], identity=identb[0:8, 0:8])

                strs.append(P.rec_end())
            P.merge(strs)
            bset[0] = [0, 1, 2, 3, 4]
            I("dve", "tensor_copy", [*tbhr], [PT3r], out=PT3[:, half * 512:(half + 1) * 512], in_=tb[:, half * 512:(half + 1) * 512])
        for j in range(16):
            for h in range(4):
                for mc in range(2):
                    c0_ = j * 64 + (mc * 4 + h) * 8
                    mm(po3[(h % 2) * 64:(h % 2 + 1) * 64, (j * 2 + h // 2) * 8:(j * 2 + h // 2) * 8 + 8],
                       CMV[:, j, mc, h * 64:(h + 1) * 64], PT3[:, c0_:c0_ + 8], mc == 0, mc == 1, [CMVr, PT3r], po3r)
        I("act", "activation", [po3r], [OcTr[0]], out=OcT[:, :, 0:128].rearrange("p c (j i) -> p j c i", i=8),
          in_=po3[:, 0:256].rearrange("p (j c i) -> p j c i", c=2, i=8), func=AF.Copy)
        wco_tile(0, TS)

        if stage <= 2:
            dump_y(range(NT))
            P.emit()
            return nc, P

        new_phase()
        USE_SQRT[0] = True
        NF = FH // 128
        XNTa = A([8, NT * 128], BF16); XNTar = [AR(f"XNTa{t}") for t in range(NT)]
        NSLOT = 12
        WG = A([NSLOT, 8, 128], BF16); WU = A([NSLOT, 8, 128], BF16); WD = A([NSLOT, D], BF16)
        Wsr = [AR(f"Ws{s_}") for s_ in range(NSLOT)]
        Hh = A([6, 512], BF16); Hr = [AR(f"H{j}") for j in range(6)]
        SG = A([2, 512], BF16); SGr = [AR("SG0"), AR("SG1")]
        OUT = A([1, D], F32)[:, 0, :]; OUTr = AR("OUT")
        gfin = A([1, D], F32)[:, 0, :]; gfinr = AR("gfin")
        P.dma("sp", gfin, g_final_d.partition_broadcast(128), gfinr, writes=[gfinr])
        passes = [list(range(0, 6)), list(range(6, 12)), list(range(12, 17)), list(range(17, 22))]
        groups = [(0, 4), (4, 4), (8, 4), (12, 4), (16, 1)]
        wd_v = w_down_d.rearrange("(f p) n -> f p n", p=128)
        wslot = {}
        nload = [0]

        def load_w(f):
            s_ = nload[0] % NSLOT
            nload[0] += 1
            wslot[f] = s_
            P.dma("pool", WG[:, s_], w_gate_d[:, f * 128:(f + 1) * 128].rearrange("(k p) n -> p k n", p=128), Wsr[s_],
                  writes=[Wsr[s_]], group=True)
            P.dma("pool", WU[:, s_], w_up_d[:, f * 128:(f + 1) * 128].rearrange("(k p) n -> p k n", p=128), Wsr[s_],
                  writes=[Wsr[s_]], group=True)
            P.dma("pool", WD[:, s_], wd_v[f], Wsr[s_], writes=[Wsr[s_]], group=True)

        for f in passes[0]:
            load_w(f)
        gcnt = [0]
        def ffn_norm_group(gi_):
            t0_, n_ = groups[gi_]
            for t in range(t0_, t0_ + n_):
                norm_T(Y[:, t, :], Yr[t], 3, XNTa[:, :, t * 128:(t + 1) * 128], [XNTar[t]], half=t % 2)

        ffn_norm_group(0)
        for pi, fl in enumerate(passes):
            for gi, (t0, n) in enumerate(groups):
                if pi + 1 < len(passes) and gi == 0:
                    for f in passes[pi + 1]:
                        load_w(f)
                merging = (pi == 0 and gi + 1 < len(groups))
                if merging:
                    P.rec_begin()
                    ffn_norm_group(gi + 1)
                    s_norm = P.rec_end()
                    P.rec_begin()
                ntok = n * 128
                tok = slice(t0 * 128, t0 * 128 + ntok)
                xr = [XNTar[t] for t in range(t0, t0 + n)]
                for j, f in enumerate(fl):
                    s_ = wslot[f]
                    b = gcnt[0] % 2
                    gcnt[0] += 1
                    pg, pgr = bank()
                    pu, pur = bank()
                    for k in range(8):
                        mm(pg[:, 0:ntok], WG[:, s_, k, :], XNTa[:, k, tok], k == 0, k == 7, [Wsr[s_]] + xr, pgr)
                    for k in range(8):
                        mm(pu[:, 0:ntok], WU[:, s_, k, :], XNTa[:, k, tok], k == 0, k == 7, [Wsr[s_]] + xr, pur)
                    I("act", "activation", [pgr], [SGr[b]], out=SG[:, b, 0:ntok], in_=pg[:, 0:ntok], func=AF.Silu)
                    I("dve", "tensor_tensor", [SGr[b], pur], [Hr[j]], out=Hh[:, j, 0:ntok], in0=SG[:, b, 0:ntok],
                      in1=pu[:, 0:ntok], op=ALU.mult)
                for ti in range(n):
                    t = t0 + ti
                    for c in range(2):
                        pd, pdr = bank()
                        for j, f in enumerate(fl):
                            s_ = wslot[f]
                            mm(pd[:, :], Hh[:, j, ti * 128:(ti + 1) * 128], WD[:, s_, c * 512:(c + 1) * 512], j == 0,
                               j == len(fl) - 1, [Hr[j], Wsr[s_]], pdr)
                        I("dve", "tensor_tensor", [Yr[t], pdr], [Yr[t]], out=Y[:, t, c * 512:(c + 1) * 512],
                          in0=Y[:, t, c * 512:(c + 1) * 512], in1=pd[:, :], op=ALU.add)
                if merging:
                    s_ffn = P.rec_end()
                    P.merge([s_ffn, s_norm])
                if pi == len(passes) - 1:
                    yo = yp_o.rearrange("(t p) d -> t p d", p=128)
                    for t in range(t0, t0 + n):
                        rstd, sr = norm_stats(Y[:, t, :], Yr[t], 0)
                        I("dve", "scalar_tensor_tensor", [Yr[t], sr, gfinr], [OUTr], out=OUT, in0=Y[:, t, :], scalar=rstd,
                          in1=gfin, op0=ALU.mult, op1=ALU.mult)
                        P.dma("sp", (yo[t] if t < NTP else ys_o), OUT, OUTr, reads=[OUTr])
        P.emit()
        return nc, P


def make_consts(hf):
    c = {}
    c["c_ident"] = np.eye(128, dtype=np.float32)
    i = np.arange(128)[:, None]; j = np.arange(256)[None, :]
    band = np.where((j >= i) & (j <= i + 128), 0.0, NEG).astype(np.float32)
    first = band.copy()
    if hf == 0:
        first[:, :128] = NEG
    c["c_mb_band"] = band; c["c_mb_first"] = first
    s = np.arange(128)[:, None]; t = np.arange(128)[None, :]
    c["c_mb_caus"] = np.where(s <= t, 0.0, NEG).astype(np.float32)
    c["c_mb_causs"] = np.where((s <= t) & (s // 8 == t // 8), 0.0, NEG).astype(np.float32)
    sel = np.zeros((4, 1024), np.float32)
    for h in range(4):
        sel[h, h * 128:(h + 1) * 128] = 1.0
        sel[h, 512 + h * 128:512 + (h + 1) * 128] = -1.0
    c["c_sel"] = sel
    pm = np.zeros((4, 2), np.float32)
    pm[:, 0] = 1.0 if hf else 0.0
    pm[:, 1] = 0.0 if hf else NEG
    c["c_pmask"] = pm
    r = np.arange(32)[:, None] % 8
    p = np.arange(128)[None, :]
    c["c_smc"] = np.where(p >= r, 0.0, NEG).astype(np.float32)
    smn = np.full((32, 16, 128), NEG, np.float32)
    for jq in range(16):
        for ii in range(8):
            smn[(np.arange(32) % 8) >= ii, jq, jq * 8 + ii] = 0.0
    c["c_smn"] = smn
    c["c_bt"] = np.where((s <= t) & (s // 8 == t // 8), 1.0, 0.0).astype(np.float32)
    e = np.zeros((128, 16), np.float32); e[np.arange(128), np.arange(128) // 8] = 1.0
    c["c_eseq"] = e
    return c

def shard_inputs(inp):
    maps = []
    W = ["w_in", "b_igate", "b_fgate", "attn_sinks", "g_mlstm_head", "w_out", "g_mix", "g_cross", "g_mem",
         "w_cq", "w_ck", "w_cv", "w_co", "g_ffn", "w_gate", "w_up", "w_down"]
    wd = {k: np.ascontiguousarray(np.asarray(inp[k], np.float32)[0]) for k in W}
    wd["g_final"] = np.ascontiguousarray(np.asarray(inp["g_final"], np.float32))
    xp = np.asarray(inp["x_prompt"], np.float32); xs = np.asarray(inp["x_sample"], np.float32)
    for c in range(8):
        b, hf = c // 2, c % 2
        m = dict(wd)
        m["xp"] = np.ascontiguousarray(xp[b, hf * 2048:(hf + 1) * 2048])
        m["xpre"] = np.ascontiguousarray(xp[b, 0:2048]) if hf else np.zeros((2048, 1024), np.float32)
        m["xs"] = np.ascontiguousarray(xs[16 * c:16 * c + 16].reshape(128, 1024))
        m["mem"] = np.ascontiguousarray(np.asarray(inp["mem_prompt"], np.float32)[b])
        sl = slice(16 * c, 16 * c + 16)
        m["csk"] = np.ascontiguousarray(np.asarray(inp["cache_swa_k"], np.float32)[0, sl].reshape(16, 128, 128))
        m["csv"] = np.ascontiguousarray(np.asarray(inp["cache_swa_v"], np.float32)[0, sl].reshape(16, 128, 128))
        m["sC"] = np.ascontiguousarray(np.asarray(inp["state_mlstm_C"], np.float32)[0, sl])
        m["sn"] = np.ascontiguousarray(np.asarray(inp["state_mlstm_n"], np.float32)[0, sl])
        m["sm"] = np.ascontiguousarray(np.asarray(inp["state_mlstm_m"], np.float32)[0, sl])
        m["cmk"] = np.ascontiguousarray(np.asarray(inp["cache_mem_k"], np.float32)[0, sl].reshape(16, 256, 256))
        m["cmv"] = np.ascontiguousarray(np.asarray(inp["cache_mem_v"], np.float32)[0, sl].reshape(16, 256, 256))
        m.update(make_consts(hf))
        sk = wd["attn_sinks"]
        sc = np.zeros((32, 2), np.float32)
        rr = np.arange(32)
        for h in range(2):
            sc[:, h] = sk[4 * h + 2 * ((rr % 16) // 8) + rr // 16]
        m["c_sinkcol"] = sc
        maps.append(m)
    return maps

def gather(res):
    f = np.float32
    yp = np.zeros((4, 4096, 1024), f); ys = np.zeros((128, 8, 1024), f)
    skp = np.zeros((1, 4, 128, 2, 64), f); svp = np.zeros_like(skp)
    Cp = np.zeros((1, 4, 4, 128, 64), f); npp = np.zeros((1, 4, 4, 64), f); mp = np.zeros((1, 4, 4), f)
    mkp = np.zeros((1, 4, 256, 4, 64), f); mvp = np.zeros_like(mkp)
    sks = np.zeros((1, 128, 128, 2, 64), f); svs = np.zeros_like(sks)
    Cs = np.zeros((1, 128, 4, 128, 64), f); ns = np.zeros((1, 128, 4, 64), f); ms = np.zeros((1, 128, 4), f)
    for c in range(8):
        r = res[c]; b, hf = c // 2, c % 2
        yp[b, hf * 2048:(hf + 1) * 2048] = r["yp"]
        ys[16 * c:16 * c + 16] = r["ys"].reshape(16, 8, 1024)
        if hf == 1:
            skp[0, b] = r["swak"].reshape(128, 2, 64); svp[0, b] = r["swav"].reshape(128, 2, 64)
            Cp[0, b] = r["Cp"]; npp[0, b] = r["np"]; mp[0, b] = r["mp"].reshape(4)
        else:
            mkp[0, b] = r["memk"].reshape(256, 4, 64); mvp[0, b] = r["memv"].reshape(256, 4, 64)
        sl = slice(16 * c, 16 * c + 16)
        sks[0, sl] = r["sks"].reshape(16, 128, 2, 64); svs[0, sl] = r["svs"].reshape(16, 128, 2, 64)
        Cs[0, sl] = r["Cs"]; ns[0, sl] = r["ns"]; ms[0, sl] = r["ms"]
    return (yp, ys, skp, svp, Cp, npp, mp, mkp, mvp, sks, svs, Cs, ns, ms)


_CACHE = {}


def kernel(**inputs):
    if "nc" not in _CACHE:
        _CACHE["nc"] = build_program(3)[0]
    nc = _CACHE["nc"]
    maps = shard_inputs(inputs)
    res = run_bass_kernel_spmd(nc, maps, core_ids=list(range(8)))
    return gather(res.results)
```

```python
import contextlib
from concourse.bass_utils import run_bass_kernel_spmd
import numpy as np
import concourse.bass as bass
import concourse.mybir as mybir

F32 = mybir.dt.float32
BF16 = mybir.dt.bfloat16
I32 = mybir.dt.int32
AF = mybir.ActivationFunctionType
ALU = mybir.AluOpType
AX = mybir.AxisListType

ENGS = ("pe", "act", "dve", "pool", "sp")


class Res:
    __slots__ = ("name", "last_w", "readers", "sem", "dcount", "excl")

    def __init__(self, name):
        self.name = name
        self.last_w = None
        self.readers = []
        self.sem = None
        self.dcount = 0
        self.excl = False


class Op:
    __slots__ = ("eng", "fn", "deps", "dma_res", "sig", "cnt", "k", "group")

    def __init__(self, eng, fn, dma_res):
        self.eng = eng
        self.fn = fn
        self.deps = set()
        self.dma_res = dma_res
        self.sig = False
        self.cnt = 0
        self.k = 0


class Prog:
    def __init__(self, nc):
        self.nc = nc
        self.ops = []
        self.nres = 0
        self.inherit = []
        self.phase_res = []

    def res(self, name=None, arena=False):
        self.nres += 1
        r = Res(name or f"r{self.nres}")
        if arena:
            r.readers = list(self.inherit)
            self.phase_res.append(r)
        return r

    def new_phase(self):
        inh = set(self.inherit)
        for r in self.phase_res:
            if r.last_w is not None:
                inh.add(r.last_w)
            inh.update(r.readers)
        self.inherit = sorted(inh)
        self.phase_res = []

    def rec_begin(self):
        self._rec = []

    def rec_end(self):
        r = self._rec
        self._rec = None
        return r

    def merge(self, streams):
        streams = [s_ for s_ in streams if s_]
        pos = [0] * len(streams)
        while True:
            best = None
            for k, s_ in enumerate(streams):
                if pos[k] < len(s_):
                    f = pos[k] / len(s_)
                    if best is None or f < best[0]:
                        best = (f, k)
            if best is None:
                break
            k = best[1]
            a, kw = streams[k][pos[k]]
            pos[k] += 1
            self.op(*a, **kw)

    def op(self, eng, fn, reads=(), writes=(), dma_res=None, accum=False, group=False):
        if getattr(self, "_rec", None) is not None:
            self._rec.append(((eng, fn, tuple(reads), tuple(writes)), dict(dma_res=dma_res, accum=accum, group=group)))
            return None
        i = len(self.ops)
        o = Op(eng, fn, dma_res)
        for r in reads:
            if r.last_w is not None:
                o.deps.add(r.last_w)
            if r.excl:
                for q in r.readers:
                    if self.ops[q].eng != eng:
                        o.deps.add(q)
            r.readers.append(i)
        for r in writes:
            if r.last_w is not None:
                lw = self.ops[r.last_w]
                if group and lw.dma_res is not None and lw.dma_res is dma_res:
                    o.deps |= lw.deps
                elif not (accum and lw.eng == "pe" and eng == "pe"):
                    o.deps.add(r.last_w)
            latest = {}
            for q in r.readers:
                if q == i:
                    continue
                oq = self.ops[q]
                if oq.dma_res is not None:
                    o.deps.add(q)
                elif latest.get(oq.eng, -1) < q:
                    latest[oq.eng] = q
            o.deps.update(latest.values())
            r.last_w = i
            r.readers = []
        if eng == "pe":
            o.deps = {d for d in o.deps if self.ops[d].eng != "pe" or self.ops[d].dma_res is not None}
        self.ops.append(o)
        return i

    def dma(self, eng, out, in_, res, reads=(), writes=(), group=False, **kw):
        kw = dict(kw); kw["out"] = out; kw["in_"] = in_
        return self.op(eng, ("dma_start", kw), reads=reads, writes=writes, dma_res=res, group=group)

    def I(self, eng, name, reads=(), writes=(), **kw):
        return self.op(eng, (name, kw), reads=reads, writes=writes)

    def emit(self, final_wait_all=True):
        nc = self.nc
        ops = self.ops
        for o in ops:
            for d in o.deps:
                ops[d].sig = True
        per_eng = {e: [] for e in ENGS}
        for i, o in enumerate(ops):
            per_eng[o.eng].append(i)
        import contextlib
        with contextlib.ExitStack() as st:
            esem = {e: st.enter_context(nc.semaphore(f"s_{e}")) for e in ENGS}
            ecount = {e: 0 for e in ENGS}
            dma_sems = []
            for i, o in enumerate(ops):
                if o.dma_res is not None:
                    r = o.dma_res
                    if r.sem is None:
                        r.sem = st.enter_context(nc.semaphore(f"d{len(dma_sems)}_{r.name}"))
                        dma_sems.append(r)
                    r.dcount += 1
                    o.cnt = 16 * r.dcount
                elif o.sig:
                    ecount[o.eng] += 1
                    o.cnt = ecount[o.eng]
            self.n_dma_sems = len(dma_sems)
            know = {e: {} for e in ENGS}
            know_issue = [None] * len(ops)

            def key_of(o):
                return ("d", id(o.dma_res)) if o.dma_res is not None else ("e", o.eng)

            block = st.enter_context(nc.Block())
            handles = {}

            plan = [None] * len(ops)
            for i, o in enumerate(ops):
                kn = know[o.eng]
                need = {}
                for d in o.deps:
                    p = ops[d]
                    k = key_of(p)
                    if kn.get(k, 0) >= p.cnt:
                        continue
                    if need.get(k, (0, None))[0] < p.cnt:
                        need[k] = (p.cnt, d)
                waits = []
                for k, (cnt, d) in need.items():
                    p = ops[d]
                    sem = p.dma_res.sem if p.dma_res is not None else esem[p.eng]
                    waits.append((sem, cnt))
                    kn[k] = max(kn.get(k, 0), cnt)
                    ki = know_issue[d]
                    for kk, vv in ki.items():
                        if kn.get(kk, 0) < vv:
                            kn[kk] = vv
                know_issue[i] = dict(kn)
                plan[i] = waits
            self.n_waits = sum(len(w) for w in plan)

            def make(ename):
                def body(eh):
                    for i in per_eng[ename]:
                        o = ops[i]
                        for sem, cnt in plan[i]:
                            eh.wait_ge(sem, cnt)
                        ins = getattr(eh, o.fn[0])(**o.fn[1])
                        if o.dma_res is not None:
                            ins.then_inc(o.dma_res.sem, 16)
                        elif o.sig:
                            ins.then_inc(esem[o.eng], 1)
                    if ename == "sp" and final_wait_all:
                        for r in dma_sems:
                            eh.wait_ge(r.sem, 16 * r.dcount)
                        for e in ("pe", "act", "dve", "pool"):
                            if ecount[e]:
                                eh.wait_ge(esem[e], ecount[e])
                return body

            block.tensor(make("pe"))
            block.scalar(make("act"))
            block.vector(make("dve"))
            block.gpsimd(make("pool"))
            block.sync(make("sp"))


D = 1024
FH = 2816
EPS = 1e-6
NTP = 16
NT = 17
GT = 2
NEG = -30000.0
DBG_G0 = 2


def build_program(stage=3, debug=False):
    nc = bass.Bass("TRN2", target_bir_lowering=False)
    P = Prog(nc)

    def din(name, shape, dt=F32):
        return nc.dram_tensor(name, list(shape), dt, kind="ExternalInput").ap()

    def dout(name, shape):
        return nc.dram_tensor(name, list(shape), F32, kind="ExternalOutput").ap()

    xp_d = din("xp", [2048, D]); xpre_d = din("xpre", [2048, D]); xs_d = din("xs", [128, D])
    mem_d = din("mem", [256, D])
    csk_d = din("csk", [16, 128, 128]); csv_d = din("csv", [16, 128, 128])
    sC_d = din("sC", [16, 4, 128, 64]); sn_d = din("sn", [16, 4, 64]); sm_d = din("sm", [16, 4])
    cmk_d = din("cmk", [16, 256, 256]); cmv_d = din("cmv", [16, 256, 256])
    w_in_d = din("w_in", [D, 2312]); b_i_d = din("b_igate", [4]); b_f_d = din("b_fgate", [4])
    sinks_d = din("attn_sinks", [8]); ghead_d = din("g_mlstm_head", [512]); w_out_d = din("w_out", [D, D])
    g_mix_d = din("g_mix", [D]); g_cross_d = din("g_cross", [D]); g_mem_d = din("g_mem", [D])
    w_cq_d = din("w_cq", [D, 256]); w_ck_d = din("w_ck", [D, 256]); w_cv_d = din("w_cv", [D, 256])
    w_co_d = din("w_co", [256, D]); g_ffn_d = din("g_ffn", [D])
    w_gate_d = din("w_gate", [D, FH]); w_up_d = din("w_up", [D, FH]); w_down_d = din("w_down", [FH, D])
    g_final_d = din("g_final", [D])
    ident_d = din("c_ident", [128, 128]); mb_band_d = din("c_mb_band", [128, 256]); mb_first_d = din("c_mb_first", [128, 256])
    mb_caus_d = din("c_mb_caus", [128, 128]); mb_causs_d = din("c_mb_causs", [128, 128])
    sel_d = din("c_sel", [4, 1024]); pmask_d = din("c_pmask", [4, 2])
    smc_d = din("c_smc", [32, 128]); smn_d = din("c_smn", [32, 16, 128]); sinkcol_d = din("c_sinkcol", [32, 2])
    bt_d = din("c_bt", [128, 128]); eseq_d = din("c_eseq", [128, 16])

    yp_o = dout("yp", [2048, D]); ys_o = dout("ys", [128, D])
    swak_o = dout("swak", [128, 128]); swav_o = dout("swav", [128, 128])
    Cp_o = dout("Cp", [4, 128, 64]); np_o = dout("np", [4, 64]); mp_o = dout("mp", [4, 1])
    memk_o = dout("memk", [256, 256]); memv_o = dout("memv", [256, 256])
    sks_o = dout("sks", [16, 128, 128]); svs_o = dout("svs", [16, 128, 128])
    Cs_o = dout("Cs", [16, 4, 128, 64]); ns_o = dout("ns", [16, 4, 64]); ms_o = dout("ms", [16, 4])

    st = contextlib.ExitStack()
    with st:
        def sb(name, shape, dt):
            return st.enter_context(nc.sbuf_tensor(name, list(shape), dt))

        def ps(name, shape, dt):
            return st.enter_context(nc.psum_tensor(name, list(shape), dt))

        banks = [ps(f"bk{i}", [128, 512], F32) for i in range(7)]
        bres = [P.res(f"bk{i}") for i in range(7)]
        for r_ in bres:
            r_.excl = True
        tb = ps("tb", [128, 1024], BF16)
        tbh = [tb[:, 0:512], tb[:, 512:1024]]
        tbhr = [P.res("tbA"), P.res("tbB")]
        for r_ in tbhr:
            r_.excl = True
        bki = [0]

        bset = [[0, 1, 2, 3, 4]]
        bcnt = {}

        def bank():
            key = tuple(bset[0])
            c = bcnt.get(key, 0)
            bcnt[key] = c + 1
            i = bset[0][c % len(key)]
            return banks[i], bres[i]

        Y = sb("Y", [128, NT, D], F32)
        Yr = [P.res(f"Y{t}") for t in range(NT)]
        identb = sb("identb", [128, 128], BF16); identr = P.res("identb")
        identf = sb("identf", [128, 128], F32); identfr = P.res("identf")
        onesb = sb("onesb", [128, 128], BF16); onesbr = P.res("onesb")
        onesf = sb("onesf", [128, 256], F32); onesfr = P.res("onesf")
        SEL = sb("SEL", [4, 1024], F32); selr = P.res("SEL")
        gcols = sb("gcols", [128, 4, 8], F32); gcolsr = P.res("gcols")
        gheadc = sb("gheadc", [128, 4], F32); gheadr = P.res("ghead")
        sinkb = sb("sinkb", [128, 16], F32); sinkbr = P.res("sinkb")
        gb4 = sb("gb4", [4, 4], F32); gb4r = P.res("gb4")
        mbband = sb("mbband", [128, 256], BF16); mbbandr = P.res("mbband")
        mbfirst = sb("mbfirst", [128, 256], BF16); mbfirstr = P.res("mbfirst")
        mbcaus = sb("mbcaus", [128, 128], BF16); mbcausr = P.res("mbcaus")
        stat = sb("stat", [128, 8, 4], F32)
        statr = [P.res(f"stat{i}") for i in range(8)]
        stati = [0]
        USE_SQRT = [False]
        xsb = sb("xsb", [128, 2, D], BF16); xsbr = [P.res("xsb0"), P.res("xsb1")]
        Cst = sb("Cst", [64, 4, 129], F32); Cstr = [P.res(f"Cst{h}") for h in range(4)]
        ARN = 64400
        arena = sb("arena", [128, ARN], BF16)
        aoff = [0]

        def A(shape, dt, parts=128, name=None):
            n = int(np.prod(shape))
            nb = n * (4 if dt == F32 else 2)
            n16 = (nb + 1) // 2
            n16 = (n16 + 15) // 16 * 16
            assert aoff[0] + n16 <= ARN, f"arena overflow {aoff[0]}+{n16} ({name})"
            v = arena[0:parts, aoff[0]:aoff[0] + n16]
            aoff[0] += n16
            if dt == F32:
                v = v.bitcast(F32)
            v = v[:, 0:n]
            if len(shape) == 2:
                v = v.rearrange("p (a b) -> p a b", a=shape[0])
            elif len(shape) == 3:
                v = v.rearrange("p (a b c) -> p a b c", a=shape[0], b=shape[1])
            return v

        def new_phase():
            P.new_phase()
            aoff[0] = 0

        def AR(name):
            return P.res(name, arena=True)

        def A_at(off, shape, dt, parts=128):
            n = int(np.prod(shape))
            nb = n * (4 if dt == F32 else 2)
            n16 = ((nb + 1) // 2 + 15) // 16 * 16
            v = arena[0:parts, off:off + n16]
            if dt == F32:
                v = v.bitcast(F32)
            v = v[:, 0:n]
            if len(shape) == 2:
                v = v.rearrange("p (a b) -> p a b", a=shape[0])
            elif len(shape) == 3:
                v = v.rearrange("p (a b c) -> p a b c", a=shape[0], b=shape[1])
            return v, off + n16

        def ARalias(name, olds):
            r = P.res(name, arena=True)
            dd = set(r.readers)
            for o_ in olds:
                if o_.last_w is not None:
                    dd.add(o_.last_w)
                dd.update(o_.readers)
            r.readers = sorted(dd)
            return r

        I = P.I

        def mm(out, lhsT, rhs, start, stop, reads, wres):
            I("pe", "matmul", reads, [wres], out=out, lhsT=lhsT, rhs=rhs, start=start, stop=stop)

        P.dma("pool", identb[:], ident_d, identr, writes=[identr])
        P.dma("sp", identf[:], ident_d, identfr, writes=[identfr])
        I("dve", "memset", [], [onesbr], ap=onesb[:], constant=1.0)
        I("dve", "memset", [], [onesfr], ap=onesf[:], constant=1.0)
        P.dma("sp", SEL[:], sel_d, selr, writes=[selr])
        for i, g in enumerate((g_mix_d, g_cross_d, g_mem_d, g_ffn_d)):
            P.dma("sp", gcols[:, i, :], g.rearrange("(k p) -> p k", p=128), gcolsr, writes=[gcolsr], group=True, allow_slow_non_contiguous=True)
        P.dma("sp", gheadc[:], ghead_d.rearrange("(h p) -> p h", p=128), gheadr, writes=[gheadr], allow_slow_non_contiguous=True)
        P.dma("sp", gb4[:, 0:1], b_i_d.rearrange("(h o) -> h o", o=1), gb4r, writes=[gb4r], group=True, allow_slow_non_contiguous=True)
        P.dma("sp", gb4[:, 1:2], b_f_d.rearrange("(h o) -> h o", o=1), gb4r, writes=[gb4r], group=True, allow_slow_non_contiguous=True)
        P.dma("sp", gb4[:, 2:4], pmask_d, gb4r, writes=[gb4r], group=True, allow_slow_non_contiguous=True)
        P.dma("sp", sinkb[:, 0:8], sinks_d.partition_broadcast(128), sinkbr, writes=[sinkbr])
        I("dve", "tensor_scalar", [sinkbr], [sinkbr], out=sinkb[:, 8:16], in0=sinkb[:, 0:8], scalar1=-1.0, scalar2=None,
          op0=ALU.mult)
        I("dve", "tensor_scalar", [gb4r], [gb4r], out=gb4[:, 1:2], in0=gb4[:, 1:2], scalar1=-1.0, scalar2=None, op0=ALU.mult)
        P.dma("pool", mbband[:], mb_band_d, mbbandr, writes=[mbbandr])
        P.dma("pool", mbfirst[:], mb_first_d, mbfirstr, writes=[mbfirstr])
        P.dma("pool", mbcaus[:], mb_caus_d, mbcausr, writes=[mbcausr])
        for h in range(4):
            I("dve", "memset", [], [Cstr[h]], ap=Cst[:, h, :], constant=0.0)

        def SELh(h, n=128):
            return SEL[:, h * 128:h * 128 + n]

        def NSELh(h, n=128):
            return SEL[:, 512 + h * 128:512 + h * 128 + n]

        def norm_stats(src, sres, jb=0):
            i = stati[0] % 8
            stati[0] += 1
            sr = statr[i]
            I("act", "activation", [sres], [xsbr[jb], sr], out=xsb[:, jb, :], in_=src, func=AF.Square, accum_out=stat[:, i, 0:1])
            I("dve", "tensor_scalar", [sr], [sr], out=stat[:, i, 1:2], in0=stat[:, i, 0:1], scalar1=1.0 / D, scalar2=EPS,
              op0=ALU.mult, op1=ALU.add)
            if USE_SQRT[0]:
                I("act", "activation", [sr], [sr], out=stat[:, i, 2:3], in_=stat[:, i, 1:2], func=AF.Sqrt)
                I("dve", "reciprocal", [sr], [sr], out=stat[:, i, 3:4], in_=stat[:, i, 2:3])
            else:
                I("act", "activation", [sr], [sr], out=stat[:, i, 2:3], in_=stat[:, i, 1:2], func=AF.Ln)
                I("act", "activation", [sr], [sr], out=stat[:, i, 3:4], in_=stat[:, i, 2:3], func=AF.Exp, scale=-0.5)
            return stat[:, i, 3:4], sr

        xsi = [0]

        def norm_T(src, sres, gi, dst, dres, half=None, tsel=None):
            b = xsi[0] % 2 if half is None else half
            xsi[0] += 1
            rstd, sr = norm_stats(src, sres, b)
            I("dve", "tensor_scalar", [sres, sr], [xsbr[b]], out=xsb[:, b, :], in0=src, scalar1=rstd, scalar2=None, op0=ALU.mult)
            if half is None:
                for k in range(8):
                    I("pe", "transpose", [xsbr[b], identr], [*tbhr], out=tb[:, k * 128:(k + 1) * 128],
                      in_=xsb[:, b, k * 128:(k + 1) * 128], identity=identb[:])
                for k in range(8):
                    I("act", "activation", [*tbhr, gcolsr], dres, out=dst[:, k, :], in_=tb[:, k * 128:(k + 1) * 128],
                      func=AF.Copy, scale=gcols[:, gi, k:k + 1])
            else:
                tq, tqr = (tbh[half], tbhr[half]) if tsel is None else tsel
                for kb in range(2):
                    for k4 in range(4):
                        k = kb * 4 + k4
                        I("pe", "transpose", [xsbr[b], identr], [tqr], out=tq[:, k4 * 128:(k4 + 1) * 128],
                          in_=xsb[:, b, k * 128:(k + 1) * 128], identity=identb[:])
                    for k4 in range(4):
                        k = kb * 4 + k4
                        I("act", "activation", [tqr, gcolsr], dres, out=dst[:, k, :],
                          in_=tq[:, k4 * 128:(k4 + 1) * 128], func=AF.Copy, scale=gcols[:, gi, k:k + 1])

        class NS:
            pass

        def alloc_mixer(gt, nkt, nvt):
            M = NS()
            M.WQ = A([8, 512], BF16); M.WQr = AR("WQ")
            M.WTOK = A([8, 1024], BF16); M.WTOKr = AR("WTOK")
            M.WK = M.WTOK[:, :, 0:128]; M.WKr = M.WTOKr
            M.WMQ = A([8, 256], BF16); M.WMQr = AR("WMQ")
            M.WMK = M.WTOK[:, :, 256:512]; M.WMKr = M.WTOKr
            M.WOG = A([8, 512], BF16); M.WOGr = AR("WOG")
            M.WGT = A([8, 8], BF16); M.WGTr = AR("WGT")
            M.WOA = A([4, 1024], BF16); M.WOAr = AR("WOA")
            M.WOM = A([4, 1024], BF16); M.WOMr = AR("WOM")

            def wload(dst, res, src, **kw):
                P.dma("pool", dst, src, res, writes=[res], **kw)

            def wcols(a_, b_):
                return w_in_d[:, a_:b_].rearrange("(k p) n -> p k n", p=128)
            wload(M.WTOK[:, :, 0:256], M.WTOKr, wcols(512, 768), group=True)
            wload(M.WTOK[:, :, 256:1024], M.WTOKr, wcols(1024, 1792), group=True)
            wload(M.WGT[:], M.WGTr, wcols(2304, 2312), allow_slow_non_contiguous=True)
            wload(M.WQ[:], M.WQr, wcols(0, 512))
            wload(M.WMQ[:], M.WMQr, wcols(768, 1024))
            wload(M.WOG[:], M.WOGr, wcols(1792, 2304))
            wload(M.WOA[:], M.WOAr, w_out_d[0:512, :].rearrange("(c p) n -> p c n", p=128))
            wload(M.WOM[:], M.WOMr, w_out_d[512:1024, :].rearrange("(h p) n -> p h n", p=128))
            M.KT = A([2, nkt * 128], BF16, parts=64); M.KTr = [AR(f"KT{i}") for i in range(nkt)]
            M.Vt = A([nvt, 128], BF16); M.Vtr = [AR(f"Vt{i}") for i in range(nvt)]
            M.XNTg = A([8, gt * 128], BF16); M.XNTgr = [AR(f"XNTg{i}") for i in range(gt)]
            M.QT = A([8, gt * 128], BF16, parts=64); M.QTr = AR("QT")
            M.MQT = A([4, gt * 128], BF16, parts=64); M.MQTr = AR("MQT")
            M.MKT = A([4, gt * 128], BF16, parts=64); M.MKTr = AR("MKT")
            M.SGT = A([4, gt * 128], BF16); M.SGTr = AR("SGT")
            M.MKtok = A([gt, 256], BF16); M.MKtokr = [AR(f"MKtok{i}") for i in range(gt)]
            M.MVaug = A([gt, 4, 129], BF16); M.MVaugr = [AR(f"MVaug{i}") for i in range(gt)]
            M.ATTT = A([4, gt * 128], BF16); M.ATTTr = [AR(f"ATTT{i}") for i in range(gt)]
            M.HMT = A([4, gt * 128], BF16); M.HMTr = [AR(f"HMT{i}") for i in range(gt)]
            M.NG = gt * 128
            NG_ = M.NG
            M.G_IG = A([1, NG_ + 1], F32, parts=4)[:, 0, :]; M.G_E = A([1, NG_], F32, parts=4)[:, 0, :]
            M.G_L1 = A([1, NG_], F32, parts=4)[:, 0, :]; M.G_B = A([1, NG_ + 1], F32, parts=4)[:, 0, :]
            M.G_A = A([1, NG_], F32, parts=4)[:, 0, :]; M.G_M = A([1, NG_ + 1], F32, parts=4)[:, 0, :]
            M.G_BM = A([1, NG_], F32, parts=4)[:, 0, :]; M.G_DM = A([1, NG_], F32, parts=4)[:, 0, :]
            M.Gr = AR("G_IG"); M.G_Br = AR("G_B"); M.G_Ar = AR("G_A"); M.G_Mr = AR("G_M"); M.G_BMr = AR("G_BM"); M.G_DMr = AR("G_DM")
            M.SKV = A([1, 256], F32)[:, 0, :]; M.SKVr = AR("SKV")
            M.Ebuf = A([4, 256], BF16); M.Er = AR("E")
            M.PTs = A([1, 1024], BF16); M.PTsr = [AR("PTs0")] * 2
            M.sm_st = A([1, 32], F32)[:, 0, :]; M.smr = AR("sm_st")
            M.WKC = A([1, 8], F32)[:, 0, :]; M.WKCr = AR("WKC")
            M.DG = A([1, 8], F32, parts=4)[:, 0, :]; M.DGr = AR("DG")
            M.VW = A([4, 129], BF16); M.VWr = [AR(f"VW{h}") for h in range(4)]
            M.Cb = A([4, 257], BF16, parts=64); M.Cbr = [AR(f"Cb{h}") for h in range(4)]
            M.WT = A([4, 128], BF16); M.WTr = AR("WT")
            M.ST = A([4, 128], BF16); M.STr = AR("ST")
            M.WI = A([4, 128], BF16); M.WIr = AR("WI")
            M.QW = A([4, 128], BF16, parts=64); M.QWr = AR("QW")
            M.LOWB = A([4, 128], F32); M.LOWBr = AR("LOWB")
            M.T1 = A([4, 128], F32); M.T1r = AR("T1")
            M.T2 = A([4, 128], F32); M.T2r = AR("T2")
            M.USQ = A([4, 128], BF16); M.USQr = AR("USQ")
            for i in range(gt):
                I("dve", "memset", [], [M.MVaugr[i]], ap=M.MVaug[:, i, :, 128:129], constant=1.0)
            M.PB = [(M.XNTg, M.XNTgr, M.MKtok, M.MKtokr, M.MVaug, M.MVaugr)]
            if gt > 1:
                x2 = A([8, gt * 128], BF16); x2r = [AR(f"XNTh{i}") for i in range(gt)]
                k2 = A([gt, 256], BF16); k2r = [AR(f"MKtoh{i}") for i in range(gt)]
                v2 = A([gt, 4, 129], BF16); v2r = [AR(f"MVauh{i}") for i in range(gt)]
                for i in range(gt):
                    I("dve", "memset", [], [v2r[i]], ap=v2[:, i, :, 128:129], constant=1.0)
                M.PB.append((x2, x2r, k2, k2r, v2, v2r))
            return M

        def use(pb):
            M.XNTg, M.XNTgr, M.MKtok, M.MKtokr, M.MVaug, M.MVaugr = M.PB[pb]

        M = alloc_mixer(GT, NTP + 1, NTP + 1)
        I("dve", "memset", [], [M.G_Br], ap=M.G_B[:, 0:1], constant=0.0)
        I("dve", "memset", [], [M.G_Mr], ap=M.G_M[:, 0:1], constant=0.0)

        def tok_major(ti, xcols, xres, vslot, want_kv_out=None):
            b0, b0r = bank()
            for k in range(8):
                mm(b0[:, :], M.XNTg[:, k, xcols], M.WTOK[:, k, 0:512], k == 0, k == 7, [xres, M.WTOKr], b0r)
            I("act", "activation", [b0r], [M.Vtr[vslot]], out=M.Vt[:, vslot, :], in_=b0[:, 128:256], func=AF.Copy)
            I("act", "activation", [b0r], [M.MKtokr[ti]], out=M.MKtok[:, ti, :], in_=b0[:, 256:512], func=AF.Copy, scale=0.125)
            if want_kv_out is not None:
                I("dve", "tensor_copy", [b0r], [M.SKVr], out=M.SKV[:, :], in_=b0[:, 0:256])
                if want_kv_out == "sample":
                    P.dma("sp", sks_o[:, 120:128, :], M.SKV[:, 0:128], M.SKVr, reads=[M.SKVr], group=True)
                    P.dma("sp", svs_o[:, 120:128, :], M.SKV[:, 128:256], M.SKVr, reads=[M.SKVr], group=True)
                else:
                    P.dma("sp", swak_o, M.SKV[:, 0:128], M.SKVr, reads=[M.SKVr], group=True)
                    P.dma("sp", swav_o, M.SKV[:, 128:256], M.SKVr, reads=[M.SKVr], group=True)
            b1, b1r = bank()
            for k in range(8):
                mm(b1[:, :], M.XNTg[:, k, xcols], M.WTOK[:, k, 512:1024], k == 0, k == 7, [xres, M.WTOKr], b1r)
            I("dve", "tensor_copy", [b1r], [M.MVaugr[ti]], out=M.MVaug[:, ti, :, 0:128],
              in_=b1[:, :].rearrange("p (h d) -> p h d", h=4))

        def feat64(W, Wr, nh, dst, dres, ntok, xres, scale=None, dcol0=0):
            for h0 in range(0, nh, 2):
                bk, bkr = bank()
                for hh in range(2):
                    h = h0 + hh
                    for k in range(8):
                        mm(bk[0:64, hh * 256:hh * 256 + ntok], W[:, k, h * 64:(h + 1) * 64], M.XNTg[:, k, 0:ntok],
                           k == 0, k == 7, [Wr] + xres, bkr)
                src = bk[0:64, :].rearrange("p (a b) -> p a b", a=2)[:, :, 0:ntok]
                kw = {} if scale is None else {"scale": scale}
                I("act", "activation", [bkr], dres, out=dst[:, h0:h0 + 2, dcol0:dcol0 + ntok], in_=src, func=AF.Copy, **kw)

        def gates(ntok, xres, prefix):
            pg, pgr = bank()
            for k in range(8):
                mm(pg[0:4, 0:ntok], M.WGT[:, k, 0:4], M.XNTg[:, k, 0:ntok], k == 0, k == 7, [M.WGTr] + xres, pgr)
            for k in range(8):
                mm(pg[0:4, 256:256 + ntok], M.WGT[:, k, 4:8], M.XNTg[:, k, 0:ntok], k == 0, k == 7, [M.WGTr] + xres, pgr)
            I("act", "activation", [pgr, gb4r], [M.Gr], out=M.G_IG[:, 1:ntok + 1], in_=pg[0:4, 0:ntok], func=AF.Identity,
              bias=gb4[:, 0:1])
            I("act", "activation", [pgr, gb4r], [M.Gr], out=M.G_E[:, 0:ntok], in_=pg[0:4, 256:256 + ntok], func=AF.Exp,
              bias=gb4[:, 1:2], scale=-1.0)
            I("act", "activation", [M.Gr], [M.Gr], out=M.G_L1[:, 0:ntok], in_=M.G_E[:, 0:ntok], func=AF.Ln, bias=1.0)
            if prefix == "sample":
                return
            if prefix:
                I("dve", "tensor_scalar", [M.Gr, gb4r], [M.Gr], out=M.G_L1[:, 0:ntok], in0=M.G_L1[:, 0:ntok], scalar1=gb4[:, 2:3],
                  scalar2=None, op0=ALU.mult)
            I("dve", "tensor_tensor_scan", [M.Gr, M.G_Br, onesfr], [M.G_Br], out=M.G_B[:, 1:ntok + 1], data0=onesf[0:4, 0:ntok],
              data1=M.G_L1[:, 0:ntok], initial=M.G_B[:, 0:1], op0=ALU.mult, op1=ALU.subtract)
            I("dve", "scalar_tensor_tensor", [M.Gr, M.G_Br, gb4r], [M.G_Ar], out=M.G_A[:, 0:ntok], in0=M.G_IG[:, 1:ntok + 1],
              scalar=(gb4[:, 3:4] if prefix else 0.0), in1=M.G_B[:, 1:ntok + 1], op0=ALU.add, op1=ALU.subtract)
            I("dve", "tensor_tensor_scan", [M.G_Ar, M.G_Mr, onesfr], [M.G_Mr], out=M.G_M[:, 1:ntok + 1], data0=onesf[0:4, 0:ntok],
              data1=M.G_A[:, 0:ntok], initial=M.G_M[:, 0:1], op0=ALU.mult, op1=ALU.max)
            I("dve", "tensor_tensor", [M.G_Br, M.G_Mr], [M.G_BMr], out=M.G_BM[:, 0:ntok], in0=M.G_B[:, 1:ntok + 1],
              in1=M.G_M[:, 1:ntok + 1], op=ALU.add)
            for ci in range(ntok // 128):
                I("dve", "tensor_scalar", [M.G_Mr], [M.G_DMr], out=M.G_DM[:, ci * 128:(ci + 1) * 128],
                  in0=M.G_M[:, 1 + ci * 128:1 + (ci + 1) * 128], scalar1=M.G_M[:, ci * 128:ci * 128 + 1], scalar2=None,
                  op0=ALU.subtract)

        def gates_carry(ntok):
            I("dve", "tensor_copy", [M.G_Br], [M.G_Br], out=M.G_B[:, 0:1], in_=M.G_B[:, ntok:ntok + 1])
            I("dve", "tensor_copy", [M.G_Mr], [M.G_Mr], out=M.G_M[:, 0:1], in_=M.G_M[:, ntok:ntok + 1])

        def state_update(ti, c0, refresh_cb):
            pw, pwr = bank()
            I4 = SEL[:, 0:512].rearrange("p (h t) -> p h t", t=128)[:, :, 0]
            I("dve", "tensor_scalar", [selr, M.G_Mr], [M.DGr], out=M.DG[:, 0:4], in0=I4, scalar1=M.G_M[:, c0 + 128:c0 + 129],
              scalar2=-1.0, op0=ALU.mult, op1=ALU.mult)
            I("dve", "tensor_scalar", [selr, M.G_DMr], [M.DGr], out=M.DG[:, 4:8], in0=I4, scalar1=M.G_DM[:, c0 + 127:c0 + 128],
              scalar2=-1.0, op0=ALU.mult, op1=ALU.mult)
            mm(pw[:, 0:4], M.G_A[:, c0:c0 + 128], I4, True, False, [M.G_Ar, selr], pwr)
            mm(pw[:, 0:4], onesf[0:4, 0:128], M.DG[:, 0:4], False, True, [onesfr, M.DGr], pwr)
            mm(pw[:, 4:8], onesf[0:4, 0:128], M.DG[:, 4:8], True, True, [onesfr, M.DGr], pwr)
            I("act", "activation", [pwr], [M.WKCr], out=M.WKC[:, 0:8], in_=pw[:, 0:8], func=AF.Exp)
            for h in range(4):
                I("dve", "tensor_scalar", [M.MVaugr[ti], M.WKCr], [M.VWr[h]], out=M.VW[:, h, :], in0=M.MVaug[:, ti, h, :],
                  scalar1=M.WKC[:, h:h + 1], scalar2=None, op0=ALU.mult)
            for h0 in (0, 2):
                dc, dcr = bank()
                for hh in range(2):
                    h = h0 + hh
                    mm(dc[0:64, hh * 129:(hh + 1) * 129], M.MKtok[:, ti, h * 64:(h + 1) * 64], M.VW[:, h, :], True, True,
                       [M.MKtokr[ti], M.VWr[h]], dcr)
                for hh in range(2):
                    h = h0 + hh
                    I("dve", "scalar_tensor_tensor", [Cstr[h], M.WKCr, dcr], [Cstr[h]], out=Cst[:, h, :], in0=Cst[:, h, :],
                      scalar=M.WKC[0:64, 4 + h:5 + h], in1=dc[0:64, hh * 129:(hh + 1) * 129], op0=ALU.mult, op1=ALU.add)
            if refresh_cb:
                for h in range(4):
                    I("act", "activation", [Cstr[h]], [M.Cbr[h]], out=M.Cb[:, h, 0:129], in_=Cst[:, h, :], func=AF.Copy)
                    I("act", "activation", [Cstr[h]], [M.Cbr[h]], out=M.Cb[:, h, 129:257],
                      in_=Cst[:, h, 128:129].broadcast_to([64, 128]), func=AF.Copy)

        def mlstm_chunk(ti, c0, mbias, mbiasr, inter=True, inter_fn=None):
            cs = slice(c0, c0 + 128)
            pwt, pwtr = bank()
            for h in range(4):
                o = pwt[:, h * 128:(h + 1) * 128]
                mm(o, M.G_A[:, cs], SELh(h), True, False, [M.G_Ar, selr], pwtr)
                mm(o, NSELh(h), M.G_M[:, c0 + 1:c0 + 129], False, False, [M.G_Mr, selr], pwtr)
                mm(o, identb[:], mbias, False, True, [identr, mbiasr], pwtr)
            I("act", "activation", [pwtr], [M.WTr], out=M.WT[:, :, :], in_=pwt[:, :].rearrange("p (h t) -> p h t", h=4), func=AF.Exp)
            pqk, pqkr = bank()
            for h in range(4):
                mm(pqk[:, h * 128:(h + 1) * 128], M.MKT[:, h, cs], M.MQT[:, h, cs], True, True, [M.MKTr, M.MQTr], pqkr)
            I("dve", "tensor_tensor", [pqkr, M.WTr], [M.STr], out=M.ST[:, :, :], in0=pqk[:, :].rearrange("p (h t) -> p h t", h=4),
              in1=M.WT[:, :, :], op=ALU.mult)
            pwi, pwir = bank()
            for h in range(4):
                mm(pwi[:, h * 128:(h + 1) * 128], NSELh(h), M.G_DM[:, cs], True, True, [M.G_DMr, selr], pwir)
            I("act", "activation", [pwir], [M.WIr], out=M.WI[:, :, :], in_=pwi[:, :].rearrange("p (h t) -> p h t", h=4), func=AF.Exp)
            I("dve", "tensor_tensor", [M.MQTr, M.WIr], [M.QWr], out=M.QW[:, :, :], in0=M.MQT[:, :, cs], in1=M.WI[0:64, :, :], op=ALU.mult)
            plb, plbr = bank()
            for h in range(4):
                mm(plb[:, h * 128:(h + 1) * 128], NSELh(h), M.G_BM[:, cs], True, True, [M.G_BMr, selr], plbr)
            I("act", "activation", [plbr], [M.LOWBr], out=M.LOWB[:, :, :], in_=plb[:, :].rearrange("p (h t) -> p h t", h=4), func=AF.Exp)
            pnum, pnumr = banks[5], bres[5]
            pden, pdenr = banks[6], bres[6]
            if inter_fn is not None:
                inter_fn("pre")
            for h in range(4):
                o = pnum[:, h * 128:(h + 1) * 128]
                mm(o, M.MVaug[:, ti, h, 0:128], M.ST[:, h, :], True, False, [M.MVaugr[ti], M.STr], pnumr)
                if inter_fn is not None:
                    inter_fn("num", h, pnum, pnumr)
                else:
                    mm(o, M.Cb[:, h, 0:128], M.QW[:, h, :], False, True, [M.Cbr[h], M.QWr], pnumr)
            for h in range(4):
                o = pden[:, h * 128:(h + 1) * 128]
                mm(o, onesb[:], M.ST[:, h, :], True, False, [onesbr, M.STr], pdenr)
                if inter_fn is not None:
                    inter_fn("den", h, pden, pdenr)
                else:
                    mm(o, M.Cb[:, h, 129:257], M.QW[:, h, :], False, True, [M.Cbr[h], M.QWr], pdenr)
            return pnum, pnumr, pden, pdenr

        def mlstm_finish(pnum, pnumr, pden, pdenr, c0, hres):
            cs = slice(c0, c0 + 128)
            v4 = lambda b: b[:, :].rearrange("p (h t) -> p h t", h=4)
            I("act", "activation", [pdenr], [M.T1r], out=M.T1[:, :, :], in_=v4(pden), func=AF.Abs)
            I("dve", "tensor_tensor", [M.T1r, M.LOWBr], [M.T1r], out=M.T1[:, :, :], in0=M.T1[:, :, :], in1=M.LOWB[:, :, :], op=ALU.max)
            I("act", "activation", [M.T1r], [M.T1r], out=M.T1[:, :, :], in_=M.T1[:, :, :], func=AF.Square, scale=float(np.sqrt(EPS)))
            I("act", "activation", [pnumr], [M.USQr], out=M.USQ[:, :, :], in_=v4(pnum), func=AF.Square)
            pss, pssr = bank()
            mm(pss[:, :], onesb[:], M.USQ[:, :, :], True, True, [onesbr, M.USQr], pssr)
            I("dve", "scalar_tensor_tensor", [pssr, M.T1r], [M.T2r], out=M.T2[:, :, :], in0=v4(pss), scalar=1.0 / 128, in1=M.T1[:, :, :],
              op0=ALU.mult, op1=ALU.add)
            I("act", "activation", [M.T2r], [M.T2r], out=M.T2[:, :, :], in_=M.T2[:, :, :], func=AF.Ln)
            I("act", "activation", [M.T2r], [M.T2r], out=M.T2[:, :, :], in_=M.T2[:, :, :], func=AF.Exp, scale=-0.5)
            I("dve", "tensor_tensor", [pnumr, M.T2r], [M.T1r], out=M.T1[:, :, :], in0=v4(pnum), in1=M.T2[:, :, :], op=ALU.mult)
            for h in range(4):
                I("dve", "scalar_tensor_tensor", [M.T1r, gheadr, M.SGTr], [hres], out=M.HMT[:, h, cs], in0=M.T1[:, h, :],
                  scalar=gheadc[:, h:h + 1], in1=M.SGT[:, h, cs], op0=ALU.mult, op1=ALU.mult)

        def swa_tile(ti, kcol0, vslots, mb, mbr, ktres):
            qs = slice(ti * 128, (ti + 1) * 128)
            for h in range(2):
                bks = [bank(), bank()]
                for g in range(4):
                    bk, bkr = bks[g // 2]
                    o = bk[:, (g % 2) * 256:(g % 2 + 1) * 256]
                    mm(o, M.QT[:, 4 * h + g, qs], M.KT[:, h, kcol0:kcol0 + 256], True, False, [M.QTr] + ktres, bkr)
                    mm(o, identb[:], mb, False, True, [identr, mbr], bkr)
                for j in range(2):
                    I("dve", "reduce_max", [bks[j][1]], [M.smr], out=M.sm_st[:, 2 * j:2 * j + 2],
                      in_=bks[j][0][:, :].rearrange("p (a b) -> p a b", a=2), axis=AX.X)
                I("dve", "tensor_scalar", [M.smr], [M.smr], out=M.sm_st[:, 0:4], in0=M.sm_st[:, 0:4], scalar1=-0.125, scalar2=None,
                  op0=ALU.mult)
                I("dve", "tensor_tensor", [M.smr, sinkbr], [M.smr], out=M.sm_st[:, 0:4], in0=M.sm_st[:, 0:4],
                  in1=sinkb[:, 8 + 4 * h:12 + 4 * h], op=ALU.min)
                for g in range(4):
                    bk, bkr = bks[g // 2]
                    I("act", "activation", [bkr, M.smr], [M.Er, M.smr], out=M.Ebuf[:, g, :], in_=bk[:, (g % 2) * 256:(g % 2 + 1) * 256],
                      func=AF.Exp, bias=M.sm_st[:, g:g + 1], scale=0.125, accum_out=M.sm_st[:, 4 + g:5 + g])
                I("dve", "tensor_tensor", [M.smr, sinkbr], [M.smr], out=M.sm_st[:, 8:12], in0=M.sm_st[:, 0:4],
                  in1=sinkb[:, 4 * h:4 * h + 4], op=ALU.add)
                I("act", "activation", [M.smr], [M.smr], out=M.sm_st[:, 8:12], in_=M.sm_st[:, 8:12], func=AF.Exp)
                I("dve", "tensor_tensor", [M.smr], [M.smr], out=M.sm_st[:, 8:12], in0=M.sm_st[:, 8:12], in1=M.sm_st[:, 4:8], op=ALU.add)
                I("dve", "reciprocal", [M.smr], [M.smr], out=M.sm_st[:, 12:16], in_=M.sm_st[:, 8:12])
                for g in range(4):
                    if g % 2 == 0:
                        I("act", "activation", [M.Er, M.smr], [M.Er], out=M.Ebuf[:, g, :], in_=M.Ebuf[:, g, :], func=AF.Copy,
                          scale=M.sm_st[:, 12 + g:13 + g])
                    else:
                        I("dve", "tensor_scalar", [M.Er, M.smr], [M.Er], out=M.Ebuf[:, g, :], in0=M.Ebuf[:, g, :],
                          scalar1=M.sm_st[:, 12 + g:13 + g], scalar2=None, op0=ALU.mult)
                for kb in range(2):
                    for g in range(4):
                        blk = kb * 4 + (g % 2) * 2 + g // 2
                        I("pe", "transpose", [M.Er, identr], [*tbhr], out=tb[:, blk * 128:(blk + 1) * 128],
                          in_=M.Ebuf[:, g, kb * 128:(kb + 1) * 128], identity=identb[:])
                pb = 0
                if h == 0:
                    I("dve", "tensor_copy", [*tbhr], [M.PTsr[pb]], out=M.PTs[:, pb, :], in_=tb[:, :])
                else:
                    I("act", "activation", [*tbhr], [M.PTsr[pb]], out=M.PTs[:, pb, :], in_=tb[:, :], func=AF.Copy)
                po, por = bank()
                for par in range(2):
                    for kb in range(2):
                        mm(po[par * 64:(par + 1) * 64, 0:256], M.Vt[:, vslots[kb], h * 64:(h + 1) * 64],
                           M.PTs[:, pb, kb * 512 + par * 256:kb * 512 + (par + 1) * 256], kb == 0, kb == 1,
                           [M.Vtr[vslots[kb]], M.PTsr[pb]], por)
                I("act", "activation", [por], [M.ATTTr[ti]], out=M.ATTT[:, 2 * h:2 * h + 2, qs],
                  in_=po[:, 0:256].rearrange("p (g q) -> p g q", g=2), func=AF.Copy)

        def wout_tile(ti, t):
            qs = slice(ti * 128, (ti + 1) * 128)
            for c in range(2):
                bk, bkr = bank()
                cc = slice(c * 512, (c + 1) * 512)
                for hg in range(4):
                    mm(bk[:, :], M.ATTT[:, hg, qs], M.WOA[:, hg, cc], hg == 0, False, [M.ATTTr[ti], M.WOAr], bkr)
                for h in range(4):
                    mm(bk[:, :], M.HMT[:, h, qs], M.WOM[:, h, cc], False, h == 3, [M.HMTr[ti], M.WOMr], bkr)
                I("dve", "tensor_tensor", [Yr[t], bkr], [Yr[t]], out=Y[:, t, cc], in0=Y[:, t, cc], in1=bk[:, :], op=ALU.add)

        xpre_t = xpre_d.rearrange("(t p) d -> t p d", p=128)
        xp_t = xp_d.rearrange("(t p) d -> t p d", p=128)
        for t in range(NTP):
            P.dma("sp", Y[:, t, :], xpre_t[t], Yr[t], writes=[Yr[t]])

        tb4sel = (banks[4][:, :].bitcast(BF16)[:, 0:512], bres[4])

        def prep_prefix(g0, pb):
            use(pb)
            for ti in range(GT):
                t = g0 + ti
                norm_T(Y[:, t, :], Yr[t], 0, M.XNTg[:, :, ti * 128:(ti + 1) * 128], [M.XNTgr[ti]], half=1, tsel=tb4sel)
            for ti in range(GT):
                tok_major(ti, slice(ti * 128, (ti + 1) * 128), M.XNTgr[ti], 0)

        prep_prefix(0, 0)
        for gi, g0 in enumerate(range(0, NTP, GT)):
            pb = gi % 2
            use(pb)
            P.rec_begin(); bset[0] = [0, 1, 2, 3]
            gates(GT * 128, M.XNTgr, True)
            if g0 + GT == NTP:
                bk, bkr = bank()
                for h in range(2):
                    for k in range(8):
                        mm(bk[0:64, h * 128:(h + 1) * 128], M.WK[:, k, h * 64:(h + 1) * 64], M.XNTg[:, k, (GT - 1) * 128:GT * 128],
                           k == 0, k == 7, [M.WKr, M.XNTgr[GT - 1]], bkr)
                I("act", "activation", [bkr], [M.KTr[0]], out=M.KT[:, :, 0:128],
                  in_=bk[0:64, 0:256].rearrange("p (a b) -> p a b", a=2), func=AF.Copy)
            for ti in range(GT):
                last = (g0 + ti == NTP - 1)
                state_update(ti, ti * 128, last)
            gates_carry(GT * 128)
            sA = P.rec_end()
            strs = [sA]
            if g0 + GT < NTP:
                P.rec_begin(); bset[0] = [4]
                prep_prefix(g0 + GT, pb ^ 1)
                strs.append(P.rec_end())
                use(pb)
            P.merge(strs)
            bset[0] = [0, 1, 2, 3, 4]

        for t in range(NTP):
            P.dma("sp", Y[:, t, :], xp_t[t], Yr[t], writes=[Yr[t]])

        def prep_main(g0, pb, merged):
            use(pb)
            if merged:
                for ti in range(GT):
                    t = g0 + ti
                    norm_T(Y[:, t, :], Yr[t], 0, M.XNTg[:, :, ti * 128:(ti + 1) * 128], [M.XNTgr[ti]], half=1, tsel=tb4sel)
            else:
                strs_ = []
                for ti in range(GT):
                    t = g0 + ti
                    P.rec_begin()
                    norm_T(Y[:, t, :], Yr[t], 0, M.XNTg[:, :, ti * 128:(ti + 1) * 128], [M.XNTgr[ti]], half=ti)
                    strs_.append(P.rec_end())
                P.merge(strs_)
            for ti in range(GT):
                t = g0 + ti
                tok_major(ti, slice(ti * 128, (ti + 1) * 128), M.XNTgr[ti], 1 + t, want_kv_out=(True if t == NTP - 1 else None))

        prep_main(0, 0, False)
        for gi, g0 in enumerate(range(0, NTP, GT)):
            pb = gi % 2
            use(pb)
            gates(M.NG, M.XNTgr, False)
            feat64(M.WK, M.WKr, 2, M.KT, [M.KTr[1 + g0 + i] for i in range(GT)], M.NG, M.XNTgr, dcol0=128 + g0 * 128)
            feat64(M.WQ, M.WQr, 8, M.QT, [M.QTr], M.NG, M.XNTgr)
            feat64(M.WMQ, M.WMQr, 4, M.MQT, [M.MQTr], M.NG, M.XNTgr)
            feat64(M.WMK, M.WMKr, 4, M.MKT, [M.MKTr], M.NG, M.XNTgr, scale=0.125)
            for h0 in (0, 2):
                bk, bkr = bank()
                for hh in range(2):
                    h = h0 + hh
                    for k in range(8):
                        mm(bk[:, hh * 256:hh * 256 + M.NG], M.WOG[:, k, h * 128:(h + 1) * 128], M.XNTg[:, k, 0:M.NG], k == 0, k == 7,
                           [M.WOGr] + M.XNTgr, bkr)
                sgv = M.SGT[:, h0:h0 + 2, :]
                I("act", "activation", [bkr], [M.SGTr], out=sgv, in_=bk[:, :].rearrange("p (a b) -> p a b", a=2)[:, :, 0:M.NG],
                  func=AF.Exp, scale=-1.0)
                I("act", "activation", [M.SGTr], [M.SGTr], out=sgv, in_=sgv, func=AF.Ln, bias=1.0)
                I("act", "activation", [M.SGTr], [M.SGTr], out=sgv, in_=sgv, func=AF.Exp, scale=-1.0)
            pend = None
            for ti in range(GT):
                t = g0 + ti
                P.rec_begin(); bset[0] = [2, 3]
                pn = mlstm_chunk(ti, ti * 128, mbcaus[:], mbcausr)
                mlstm_finish(*pn, ti * 128, M.HMTr[ti])
                state_update(ti, ti * 128, True)
                s_ml = P.rec_end()
                P.rec_begin(); bset[0] = [0, 1]
                swa_tile(ti, t * 128, (t, t + 1), (mbfirst[:] if t == 0 else mbband[:]), (mbfirstr if t == 0 else mbbandr),
                         [M.KTr[t], M.KTr[t + 1]])
                s_sw = P.rec_end()
                strs = [s_ml, s_sw]
                P.rec_begin(); bset[0] = [4]
                if pend is not None:
                    wout_tile(*pend)
                if ti == 0 and g0 + GT < NTP:
                    prep_main(g0 + GT, pb ^ 1, True)
                    use(pb)
                strs.append(P.rec_end())
                P.merge(strs)
                pend = (ti, t)
            bset[0] = [0, 1, 2, 3, 4]
            wout_tile(*pend)
            if debug and g0 == DBG_G0:
                dA = dout("dbg_att", [128, 4, M.NG]); dH = dout("dbg_hm", [128, 4, M.NG])
                P.dma("pool", dA, M.ATTT[:, :, :], M.ATTTr[0], reads=M.ATTTr)
                P.dma("pool", dH, M.HMT[:, :, :], M.HMTr[0], reads=M.HMTr)
            gates_carry(M.NG)

        CO = A([4, 64], F32); COr = AR("CO")
        for h in range(4):
            bk, bkr = bank()
            mm(bk[:, 0:64], Cst[:, h, 0:128], identf[0:64, 0:64], True, True, [Cstr[h], identfr], bkr)
            I("act", "activation", [bkr], [COr], out=CO[:, h, :], in_=bk[:, 0:64], func=AF.Copy)
        P.dma("sp", Cp_o.rearrange("h p k -> p h k"), CO[:, :, :], COr, reads=[COr])
        for h in range(4):
            P.dma("sp", np_o[h, :].rearrange("(k o) -> k o", o=1), Cst[:, h, 128:129], Cstr[h], reads=[Cstr[h]], allow_slow_non_contiguous=True)
        P.dma("sp", mp_o, M.G_BM[:, M.NG - 1:M.NG], M.G_BMr, reads=[M.G_BMr], allow_slow_non_contiguous=True)

        new_phase()
        MA = M
        M = alloc_mixer(1, 1, 1)
        R0_olds = [M.WQr, M.WTOKr, M.WMQr, M.WOGr, M.WGTr]
        TS = NTP
        P.dma("sp", Y[:, TS, :], xs_d, Yr[TS], writes=[Yr[TS]])
        shk = P.res("shk"); shv = P.res("shv")
        P.dma("sp", sks_o[:, 0:120, :], csk_d[:, 8:128, :], shk, writes=[shk])
        P.dma("sp", svs_o[:, 0:120, :], csv_d[:, 8:128, :], shv, writes=[shv])
        CKn = A([16, 128], BF16); CKnr = AR("CKn")
        CV = A([16, 128], BF16); CVr = AR("CV")
        CKT = A([16, 128], BF16, parts=64); CKTr = AR("CKT")
        SMC = A([1, 128], BF16, parts=32)[:, 0, :]; SMCr = AR("SMC")
        SMN = A([16, 128], BF16, parts=32); SMNr = AR("SMN")
        SINKC = A([1, 4], F32, parts=32)[:, 0, :]; SINKCr = AR("SINKC")
        mbcs = A([1, 128], BF16)[:, 0, :]; mbcsr = AR("mbcs")
        PNs = A([4, 256], BF16, parts=32); PNsr = [AR(f"PNs{i}") for i in range(4)]
        sms = A([4, 8], F32, parts=32); smsr = [AR(f"sms{i}") for i in range(4)]
        PTS = A([1, 1024], BF16)[:, 0, :]; PTSr = AR("PTS")
        M0 = A([1, 16], F32, parts=4)[:, 0, :]; M0r = AR("M0")
        MTe = A([1, 128], F32, parts=4)[:, 0, :]; MTer = AR("MTe")
        DMT = A([1, 16], F32, parts=4)[:, 0, :]; DMTr = AR("DMT")
        E16 = A([1, 16], F32)[:, 0, :]; E16r = AR("E16")
        EW = A([4, 16], BF16); EWr = AR("EW")
        WCB = A([4, 16], F32); WCBr = AR("WCB")
        SNn = A([1, 64], F32, parts=64)[:, 0, :]; SNnr = AR("SNn")
        SNT = A([1, 64], F32, parts=64)[:, 0, :]; SNTr = AR("SNT")
        NNT = A([1, 64], F32, parts=64)[:, 0, :]; NNTr = AR("NNT")
        NNo = A([1, 64], F32, parts=64)[:, 0, :]; NNor = AR("NNo")
        BTf = A([1, 128], F32)[:, 0, :]; BTfr = AR("BTf")
        P.dma("pool", CKn[:, :, :], csk_d.rearrange("j p c -> p j c"), CKnr, writes=[CKnr])
        P.dma("pool", CV[:, :, :], csv_d.rearrange("j p c -> p j c"), CVr, writes=[CVr])
        P.dma("pool", SMC, smc_d, SMCr, writes=[SMCr])
        P.dma("pool", SMN[:, :, :], smn_d, SMNr, writes=[SMNr])
        P.dma("sp", SINKC[:, 0:2], sinkcol_d, SINKCr, writes=[SINKCr])
        I("dve", "tensor_scalar", [SINKCr], [SINKCr], out=SINKC[:, 2:4], in0=SINKC[:, 0:2], scalar1=-1.0, scalar2=None, op0=ALU.mult)
        P.dma("pool", mbcs, mb_causs_d, mbcsr, writes=[mbcsr])
        P.dma("sp", M0, sm_d.rearrange("j h -> h j"), M0r, writes=[M0r], allow_slow_non_contiguous=True)
        P.dma("sp", E16, eseq_d, E16r, writes=[E16r])
        P.dma("sp", SNn, sn_d.rearrange("j h k -> (j h) k"), SNnr, writes=[SNnr])

        norm_T(Y[:, TS, :], Yr[TS], 0, M.XNTg[:, :, 0:128], [M.XNTgr[0]])
        tok_major(0, slice(0, 128), M.XNTgr[0], 0, want_kv_out="sample")
        gates(128, M.XNTgr, "sample")
        feat64(M.WK, M.WKr, 2, M.KT, [M.KTr[0]], 128, M.XNTgr, dcol0=0)
        feat64(M.WQ, M.WQr, 8, M.QT, [M.QTr], 128, M.XNTgr)
        feat64(M.WMQ, M.WMQr, 4, M.MQT, [M.MQTr], 128, M.XNTgr)
        feat64(M.WMK, M.WMKr, 4, M.MKT, [M.MKTr], 128, M.XNTgr, scale=0.125)
        for h0 in (0, 2):
            bk, bkr = bank()
            for hh in range(2):
                h = h0 + hh
                for k in range(8):
                    mm(bk[:, hh * 256:hh * 256 + 128], M.WOG[:, k, h * 128:(h + 1) * 128], M.XNTg[:, k, 0:128], k == 0, k == 7,
                       [M.WOGr] + M.XNTgr, bkr)
            sgv = M.SGT[:, h0:h0 + 2, :]
            I("act", "activation", [bkr], [M.SGTr], out=sgv, in_=bk[:, :].rearrange("p (a b) -> p a b", a=2)[:, :, 0:128],
              func=AF.Exp, scale=-1.0)
            I("act", "activation", [M.SGTr], [M.SGTr], out=sgv, in_=sgv, func=AF.Ln, bias=1.0)
            I("act", "activation", [M.SGTr], [M.SGTr], out=sgv, in_=sgv, func=AF.Exp, scale=-1.0)
        for j in range(16):
            I("dve", "tensor_tensor_scan", [M.Gr, M.G_Br, onesfr], [M.G_Br], out=M.G_B[:, 1 + 8 * j:9 + 8 * j],
              data0=onesf[0:4, 0:8], data1=M.G_L1[:, 8 * j:8 * j + 8], initial=0.0, op0=ALU.mult, op1=ALU.subtract)
        I("dve", "tensor_tensor", [M.Gr, M.G_Br], [M.G_Ar], out=M.G_A[:, 0:128], in0=M.G_IG[:, 1:129], in1=M.G_B[:, 1:129],
          op=ALU.subtract)
        for j in range(16):
            I("dve", "tensor_tensor_scan", [M.G_Ar, M.G_Mr, onesfr, M0r], [M.G_Mr], out=M.G_M[:, 1 + 8 * j:9 + 8 * j],
              data0=onesf[0:4, 0:8], data1=M.G_A[:, 8 * j:8 * j + 8], initial=M0[:, j:j + 1], op0=ALU.mult, op1=ALU.max)
        I("dve", "tensor_tensor", [M.G_Br, M.G_Mr], [M.G_BMr], out=M.G_BM[:, 0:128], in0=M.G_B[:, 1:129], in1=M.G_M[:, 1:129],
          op=ALU.add)
        GM3 = M.G_M[:, 1:129].rearrange("p (j i) -> p j i", i=8)
        I("dve", "tensor_tensor", [M.G_Mr, M0r], [M.G_DMr], out=M.G_DM[:, 0:128].rearrange("p (j i) -> p j i", i=8), in0=GM3,
          in1=M0[:, :].unsqueeze(2).broadcast_to([4, 16, 8]), op=ALU.subtract)
        I("dve", "tensor_copy", [M.G_Mr], [MTer], out=MTe[:, :].rearrange("p (j i) -> p j i", i=8),
          in_=GM3[:, :, 7:8].broadcast_to([4, 16, 8]))
        I("dve", "tensor_tensor", [M.G_Mr, M0r], [DMTr], out=DMT[:, :].unsqueeze(2), in0=GM3[:, :, 7:8], in1=M0[:, :].unsqueeze(2),
          op=ALU.subtract)
        P.dma("sp", ms_o.rearrange("j h -> h j"), M.G_BM[:, 0:128].rearrange("p (j i) -> p j i", i=8)[:, :, 7], M.G_BMr,
              reads=[M.G_BMr], allow_slow_non_contiguous=True)

        pair_i = [0]
        QS = A([2, 16, 32], BF16, parts=64); QSr = AR("QS")
        for h in range(2):
            for par in range(2):
                I("act", "activation", [M.QTr], [QSr],
                  out=QS[:, h, :, par * 16:(par + 1) * 16].rearrange("p j (gp i) -> p j gp i", i=8),
                  in_=M.QT[:, 4 * h:4 * h + 4, :].rearrange("p (gp two) t -> p two gp t", two=2)[:, par].rearrange(
                      "p gp (j i) -> p j gp i", i=8), func=AF.Copy)
        for h in range(2):
            for q4 in range(2):
                for jj in range(8):
                    j = q4 * 8 + jj
                    I("pe", "transpose", [CKnr, identr], [*tbhr], out=tb[0:64, jj * 128:(jj + 1) * 128],
                      in_=CKn[:, j, h * 64:(h + 1) * 64], identity=identb[:])
                I("act", "activation", [*tbhr], [CKTr], out=CKT[:, q4 * 8:(q4 + 1) * 8, :],
                  in_=tb[0:64, :].rearrange("p (a b) -> p a b", a=8), func=AF.Copy)
            for half in range(2):
                po, por = banks[4], bres[4]
                strs = []
                for sk in range(4):
                    P.rec_begin(); bset[0] = [sk]
                    for jj in (sk, sk + 4):
                        j = half * 8 + jj
                        b = sk
                        bk, bkr = bank()
                        lq = QS[:, h, j, :]
                        mm(bk[0:32, 0:128], lq, CKT[:, j, :], True, False, [QSr, CKTr], bkr)
                        mm(bk[0:32, 0:128], identb[0:32, 0:32], SMC, False, True, [identr, SMCr], bkr)
                        mm(bk[0:32, 128:256], lq, M.KT[:, h, 0:128], True, False, [QSr, M.KTr[0]], bkr)
                        mm(bk[0:32, 128:256], identb[0:32, 0:32], SMN[:, j, :], False, True, [identr, SMNr], bkr)
                        st_ = sms[:, b, :]
                        I("dve", "reduce_max", [bkr], [smsr[b]], out=st_[:, 0:1], in_=bk[0:32, 0:256], axis=AX.X)
                        I("dve", "tensor_scalar", [smsr[b]], [smsr[b]], out=st_[:, 0:1], in0=st_[:, 0:1], scalar1=-0.125,
                          scalar2=None, op0=ALU.mult)
                        I("dve", "tensor_tensor", [smsr[b], SINKCr], [smsr[b]], out=st_[:, 0:1], in0=st_[:, 0:1],
                          in1=SINKC[:, 2 + h:3 + h], op=ALU.min)
                        I("act", "activation", [bkr, smsr[b]], [PNsr[b], smsr[b]], out=PNs[:, b, :], in_=bk[0:32, 0:256],
                          func=AF.Exp, bias=st_[:, 0:1], scale=0.125, accum_out=st_[:, 1:2])
                        I("act", "activation", [SINKCr, smsr[b]], [smsr[b]], out=st_[:, 2:3], in_=SINKC[:, h:h + 1], func=AF.Exp,
                          bias=st_[:, 0:1])
                        I("dve", "tensor_tensor", [smsr[b]], [smsr[b]], out=st_[:, 2:3], in0=st_[:, 2:3], in1=st_[:, 1:2],
                          op=ALU.add)
                        I("dve", "reciprocal", [smsr[b]], [smsr[b]], out=st_[:, 3:4], in_=st_[:, 2:3])
                        I("dve", "tensor_scalar", [PNsr[b], smsr[b]], [PNsr[b]], out=PNs[:, b, :], in0=PNs[:, b, :],
                          scalar1=st_[:, 3:4], scalar2=None, op0=ALU.mult)
                        for c2 in range(2):
                            I("pe", "transpose", [PNsr[b], identr], [*tbhr], out=tb[:, jj * 64 + c2 * 32:jj * 64 + (c2 + 1) * 32],
                              in_=PNs[:, b, c2 * 128:(c2 + 1) * 128], identity=identb[0:32, 0:32])
                    strs.append(P.rec_end())
                P.merge(strs)
                bset[0] = [0, 1, 2, 3]
                I("dve", "tensor_copy", [*tbhr], [PTSr], out=PTS[:, 0:512], in_=tb[:, 0:512])
                for jj in range(8):
                    j = half * 8 + jj
                    for par in range(2):
                        o = po[par * 64:(par + 1) * 64, jj * 16:(jj + 1) * 16]
                        mm(o, CV[:, j, h * 64:(h + 1) * 64], PTS[:, jj * 64 + par * 16:jj * 64 + par * 16 + 16], True, False,
                           [CVr, PTSr], por)
                        mm(o, M.Vt[:, 0, h * 64:(h + 1) * 64], PTS[:, jj * 64 + 32 + par * 16:jj * 64 + 32 + par * 16 + 16], False,
                           True, [M.Vtr[0], PTSr], por)
                I("act", "activation", [por], [M.ATTTr[0]],
                  out=M.ATTT[:, 2 * h:2 * h + 2, half * 64:(half + 1) * 64].rearrange("p c (j i) -> p j c i", i=8),
                  in_=po[:, 0:128].rearrange("p (j c i) -> p j c i", c=2, i=8), func=AF.Copy)

        bset[0] = [0, 1, 2, 3, 4]
        off = 0
        SCf, off = A_at(off, [64, 64], F32); SCfr = ARalias("SCf", R0_olds)
        SCT, off = A_at(off, [64, 128], BF16, parts=64); SCTr = ARalias("SCT", R0_olds)
        QN, off = A_at(off, [4, 128], BF16, parts=64); QNr = ARalias("QN", R0_olds)
        KJ, off = A_at(off, [16, 64], BF16); KJr = ARalias("KJ", R0_olds)
        assert off <= 18496
        P.dma("sp", SCf[:, :, :], sC_d.rearrange("j h p k -> p (j h) k"), SCfr, writes=[SCfr])
        for p4 in range(16):
            bk, bkr = bank()
            for q_ in range(4):
                pr = p4 * 4 + q_
                mm(bk[0:64, q_ * 128:(q_ + 1) * 128], SCf[:, pr, :], identf[:], True, True, [SCfr, identfr], bkr)
            I("act", "activation", [bkr], [SCTr], out=SCT[:, p4 * 4:(p4 + 1) * 4, :],
              in_=bk[0:64, :].rearrange("p (a b) -> p a b", a=4), func=AF.Copy)
        bk, bkr = bank()
        mm(bk[0:64, 0:64], SNn, identf[0:64, 0:64], True, True, [SNnr, identfr], bkr)
        I("act", "activation", [bkr], [SNTr], out=SNT, in_=bk[0:64, 0:64], func=AF.Copy)

        def sample_inter(kind, h=None, pb=None, pbr=None):
            if kind == "pre":
                I("dve", "tensor_tensor", [M.QWr, SNTr], [QNr], out=QN[:, :, :].rearrange("p h (j i) -> p h j i", i=8),
                  in0=M.QW[:, :, :].rearrange("p h (j i) -> p h j i", i=8),
                  in1=SNT.rearrange("p (j h) -> p h j", h=4).unsqueeze(3).broadcast_to([64, 4, 16, 8]), op=ALU.mult)
            elif kind == "num":
                for j in range(16):
                    mm(pb[:, h * 128 + 8 * j:h * 128 + 8 * j + 8], SCT[:, j * 4 + h, :], M.QW[:, h, 8 * j:8 * j + 8], False, j == 15,
                       [SCTr, M.QWr], pbr)
            else:
                mm(pb[:, h * 128:(h + 1) * 128], onesb[0:64, :], QN[:, h, :], False, True, [onesbr, QNr], pbr)

        pn = mlstm_chunk(0, 0, mbcs, mbcsr, inter_fn=sample_inter)
        mlstm_finish(*pn, 0, M.HMTr[0])
        wout_tile(0, TS)

        pw, pwr = bank()
        for h in range(4):
            mm(pw[:, h:h + 1], M.G_A[:, 0:128], SELh(h, 1), True, False, [M.G_Ar, selr], pwr)
            mm(pw[:, h:h + 1], MTe, SEL[:, 512 + h * 128:512 + h * 128 + 1], False, True, [MTer, selr], pwr)
        for h in range(4):
            mm(pw[:, 8 + 16 * h:8 + 16 * (h + 1)], NSELh(h), DMT, True, True, [DMTr, selr], pwr)
        I("act", "activation", [pwr], [M.WKCr], out=M.WKC[:, 0:4], in_=pw[:, 0:4], func=AF.Exp)
        I("act", "activation", [pwr], [WCBr], out=WCB[:, :, :], in_=pw[:, 8:72].rearrange("p (h j) -> p h j", h=4), func=AF.Exp)
        for h in range(4):
            I("dve", "tensor_scalar", [M.MVaugr[0], M.WKCr], [M.VWr[h]], out=M.VW[:, h, :], in0=M.MVaug[:, 0, h, :],
              scalar1=M.WKC[:, h:h + 1], scalar2=None, op0=ALU.mult)
            I("dve", "tensor_scalar", [E16r, M.WKCr], [EWr], out=EW[:, h, :], in0=E16, scalar1=M.WKC[:, h:h + 1], scalar2=None,
              op0=ALU.mult)
        bk, bkr = bank()
        for h in range(4):
            mm(bk[0:64, h * 16:(h + 1) * 16], M.MKtok[:, 0, h * 64:(h + 1) * 64], EW[:, h, :], True, True, [M.MKtokr[0], EWr], bkr)
        I("dve", "tensor_tensor", [SNTr, WCBr], [NNTr], out=NNT.rearrange("p (j h) -> p h j", h=4),
          in0=SNT.rearrange("p (j h) -> p h j", h=4), in1=WCB[0:64, :, :], op=ALU.mult)
        I("dve", "tensor_tensor", [NNTr, bkr], [NNTr], out=NNT.rearrange("p (j h) -> p h j", h=4),
          in0=NNT.rearrange("p (j h) -> p h j", h=4), in1=bk[0:64, 0:64].rearrange("p (h j) -> p h j", h=4), op=ALU.add)
        bk2, bk2r = bank()
        mm(bk2[0:64, 0:64], NNT, identf[0:64, 0:64], True, True, [NNTr, identfr], bk2r)
        I("act", "activation", [bk2r], [NNor], out=NNo, in_=bk2[0:64, 0:64], func=AF.Copy)
        P.dma("sp", ns_o.rearrange("j h k -> (j h) k"), NNo, NNor, reads=[NNor])
        for h in range(4):
            I("dve", "tensor_tensor", [M.MKtokr[0], E16r], [KJr], out=KJ[:, :, :],
              in0=M.MKtok[:, 0, h * 64:(h + 1) * 64].unsqueeze(1).broadcast_to([128, 16, 64]),
              in1=E16.unsqueeze(2).broadcast_to([128, 16, 64]), op=ALU.mult)
            for half in range(2):
                bk, bkr = bank()
                mm(bk[:, :], M.VW[:, h, 0:128], KJ[:, half * 8:(half + 1) * 8, :], True, True, [M.VWr[h], KJr], bkr)
                scv = SCf[:, :, :].rearrange("p (j h) k -> p h j k", h=4)[:, h, half * 8:(half + 1) * 8, :]
                I("dve", "tensor_tensor", [SCfr, WCBr], [SCfr], out=scv, in0=scv,
                  in1=WCB[:, h, half * 8:(half + 1) * 8].unsqueeze(2).broadcast_to([128, 8, 64]), op=ALU.mult)
                I("dve", "tensor_tensor", [SCfr, bkr], [SCfr], out=scv, in0=scv,
                  in1=bk[:, :].rearrange("p (j k) -> p j k", k=64), op=ALU.add)
        P.dma("sp", Cs_o.rearrange("j h p k -> p (j h) k"), SCf[:, :, :], SCfr, reads=[SCfr])

        def dump_y(tiles):
            yo = yp_o.rearrange("(t p) d -> t p d", p=128)
            for t in tiles:
                if Yr[t].last_w is None:
                    continue
                if t < NTP:
                    P.dma("sp", yo[t], Y[:, t, :], Yr[t], reads=[Yr[t]])
                else:
                    P.dma("sp", ys_o, Y[:, t, :], Yr[t], reads=[Yr[t]])

        if stage <= 1:
            dump_y(range(NT))
            P.emit()
            return nc, P

        new_phase()
        WCQ = A([8, 256], BF16); WCQr = AR("WCQ")
        WCKV = A([8, 512], BF16); WCKVr = AR("WCKV")
        WCO = A([2, 1024], BF16); WCOr = AR("WCO")
        P.dma("pool", WCQ[:], w_cq_d.rearrange("(k p) n -> p k n", p=128), WCQr, writes=[WCQr])
        P.dma("pool", WCKV[:, :, 0:256], w_ck_d.rearrange("(k p) n -> p k n", p=128), WCKVr, writes=[WCKVr], group=True)
        P.dma("pool", WCKV[:, :, 256:512], w_cv_d.rearrange("(k p) n -> p k n", p=128), WCKVr, writes=[WCKVr], group=True)
        P.dma("pool", WCO[:], w_co_d.rearrange("(c p) n -> p c n", p=128), WCOr, writes=[WCOr])
        MEMX = A([2, D], F32); MEMXr = [AR("MEMX0"), AR("MEMX1")]
        MNT = A([8, 256], BF16); MNTr = [AR("MNT0"), AR("MNT1")]
        MKTm = A([4, 256], BF16, parts=64); MKTmr = AR("MKTm")
        MVm = A([2, 256], BF16); MVmr = AR("MVm")
        MKVo = A([2, 512], F32); MKVor = [AR("MKVo0"), AR("MKVo1")]
        GB = 4
        XNTb = A([8, GB * 128], BF16); XNTbr = [AR(f"XNTb{i}") for i in range(GB)]
        QcT = A([4, GB * 128], BF16, parts=64); QcTr = AR("QcT")
        OcT = A([2, GB * 128], BF16); OcTr = [AR(f"OcT{i}") for i in range(GB)]
        Eb2s = [A([4, 256], BF16) for _ in range(2)]; Eb2rs = [AR("Eb2a"), AR("Eb2b")]
        PT2s = [A([1, 1024], BF16) for _ in range(2)]; PT2rs = [AR("PT2a"), AR("PT2b")]
        sm2s = [A([1, 32], F32)[:, 0, :] for _ in range(2)]; sm2rs = [AR("sm2a"), AR("sm2b")]

        mem_t = mem_d.rearrange("(t p) d -> t p d", p=128)
        import os
        SK = os.environ.get("SKIP", "")
        for mt in range(2):
            P.dma("sp", MEMX[:, mt, :], mem_t[mt], MEMXr[mt], writes=[MEMXr[mt]])
        for mt in (range(2) if "noBnorm" not in SK else []):
            norm_T(MEMX[:, mt, :], MEMXr[mt], 2, MNT[:, :, mt * 128:(mt + 1) * 128], [MNTr[mt]])
        for mt in (range(2) if "noBkv" not in SK else []):
            bk, bkr = bank()
            for k in range(8):
                mm(bk[:, :], MNT[:, k, mt * 128:(mt + 1) * 128], WCKV[:, k, :], k == 0, k == 7, [MNTr[mt], WCKVr], bkr)
            if "noBcp1" not in SK:
                I("act", "activation", [bkr], [MKVor[mt]], out=MKVo[:, mt, :], in_=bk[:, :], func=AF.Copy)
            if "noBcp2" not in SK:
                I("act", "activation", [bkr], [MVmr], out=MVm[:, mt, :], in_=bk[:, 256:512], func=AF.Copy)
            if "noBdma" not in SK:
                P.dma("sp", memk_o[mt * 128:(mt + 1) * 128, :], MKVo[:, mt, 0:256], MKVor[mt], reads=[MKVor[mt]], group=True)
                P.dma("sp", memv_o[mt * 128:(mt + 1) * 128, :], MKVo[:, mt, 256:512], MKVor[mt], reads=[MKVor[mt]], group=True)
        for h0 in ((0, 2) if "noBkt" not in SK else []):
            bk, bkr = bank()
            for hh in range(2):
                h = h0 + hh
                for k in range(8):
                    mm(bk[0:64, hh * 256:(hh + 1) * 256], WCKV[:, k, h * 64:(h + 1) * 64], MNT[:, k, :], k == 0, k == 7,
                       [WCKVr] + MNTr, bkr)
            I("act", "activation", [bkr], [MKTmr], out=MKTm[:, h0:h0 + 2, :],
              in_=bk[0:64, :].rearrange("p (a b) -> p a b", a=2), func=AF.Copy)

        def cross_q(ntok, xres):
            for h in range(4):
                bk, bkr = bank()
                for k in range(8):
                    mm(bk[0:64, 0:ntok], WCQ[:, k, h * 64:(h + 1) * 64], XNTb[:, k, 0:ntok], k == 0, k == 7, [WCQr] + xres, bkr)
                I("act", "activation", [bkr], [QcTr], out=QcT[:, h, 0:ntok], in_=bk[0:64, 0:ntok], func=AF.Copy)

        def cross_tile_prompt(ti, sx):
            Eb2, Eb2r, PT2, PT2r, sm2, sm2r, tbx, tbxr = Eb2s[sx], Eb2rs[sx], PT2s[sx], PT2rs[sx], sm2s[sx], sm2rs[sx], tbh[sx], tbhr[sx]
            qs = slice(ti * 128, (ti + 1) * 128)
            bks = [bank(), bank()]
            for h in range(4):
                bk, bkr = bks[h // 2]
                mm(bk[:, (h % 2) * 256:(h % 2 + 1) * 256], QcT[:, h, qs], MKTm[:, h, :], True, True, [QcTr, MKTmr], bkr)
            for j in range(2):
                I("dve", "reduce_max", [bks[j][1]], [sm2r], out=sm2[:, 2 * j:2 * j + 2],
                  in_=bks[j][0][:, :].rearrange("p (a b) -> p a b", a=2), axis=AX.X)
            I("dve", "tensor_scalar", [sm2r], [sm2r], out=sm2[:, 0:4], in0=sm2[:, 0:4], scalar1=-0.125, scalar2=None, op0=ALU.mult)
            for h in range(4):
                bk, bkr = bks[h // 2]
                I("act", "activation", [bkr, sm2r], [Eb2r, sm2r], out=Eb2[:, h, :], in_=bk[:, (h % 2) * 256:(h % 2 + 1) * 256],
                  func=AF.Exp, bias=sm2[:, h:h + 1], scale=0.125, accum_out=sm2[:, 4 + h:5 + h])
            I("dve", "reciprocal", [sm2r], [sm2r], out=sm2[:, 8:12], in_=sm2[:, 4:8])
            for h in range(4):
                if h % 2 == 0:
                    I("act", "activation", [Eb2r, sm2r], [Eb2r], out=Eb2[:, h, :], in_=Eb2[:, h, :], func=AF.Copy,
                      scale=sm2[:, 8 + h:9 + h])
                else:
                    I("dve", "tensor_scalar", [Eb2r, sm2r], [Eb2r], out=Eb2[:, h, :], in0=Eb2[:, h, :],
                      scalar1=sm2[:, 8 + h:9 + h], scalar2=None, op0=ALU.mult)
            po, por = bank()
            for mc in range(2):
                for h in range(4):
                    I("pe", "transpose", [Eb2r, identr], [tbxr], out=tbx[:, h * 128:(h + 1) * 128],
                      in_=Eb2[:, h, mc * 128:(mc + 1) * 128], identity=identb[:])
                if mc == 0:
                    I("dve", "tensor_copy", [tbxr], [PT2r], out=PT2[:, 0, 0:512], in_=tbx)
                else:
                    I("act", "activation", [tbxr], [PT2r], out=PT2[:, 0, 512:1024], in_=tbx, func=AF.Copy)
            for h in range(4):
                for mc in range(2):
                    mm(po[(h % 2) * 64:(h % 2 + 1) * 64, (h // 2) * 128:(h // 2 + 1) * 128], MVm[:, mc, h * 64:(h + 1) * 64],
                       PT2[:, 0, (mc * 4 + h) * 128:(mc * 4 + h + 1) * 128], mc == 0, mc == 1, [MVmr, PT2r], por)
            I("act", "activation", [por], [OcTr[ti]], out=OcT[:, :, qs], in_=po[:, 0:256].rearrange("p (h q) -> p h q", h=2),
              func=AF.Copy)

        def wco_tile(ti, t):
            qs = slice(ti * 128, (ti + 1) * 128)
            for c in range(2):
                bk, bkr = bank()
                cc = slice(c * 512, (c + 1) * 512)
                for h in range(2):
                    mm(bk[:, :], OcT[:, h, qs], WCO[:, h, cc], h == 0, h == 1, [OcTr[ti], WCOr], bkr)
                I("dve", "tensor_tensor", [Yr[t], bkr], [Yr[t]], out=Y[:, t, cc], in0=Y[:, t, cc], in1=bk[:, :], op=ALU.add)

        import os
        for g0 in (range(0, NTP, GB) if "noBloop" not in os.environ.get("SKIP", "") else []):
            for tp_ in range(0, GB, 2):
                strs = []
                for sx in range(2):
                    ti = tp_ + sx
                    P.rec_begin()
                    norm_T(Y[:, g0 + ti, :], Yr[g0 + ti], 1, XNTb[:, :, ti * 128:(ti + 1) * 128], [XNTbr[ti]], half=sx)
                    strs.append(P.rec_end())
                P.merge(strs)
            cross_q(GB * 128, XNTbr)
            for tp_ in range(0, GB, 2):
                strs = []
                for sx in range(2):
                    P.rec_begin(); bset[0] = [0, 1, 2] if sx == 0 else [3, 4, 5]
                    cross_tile_prompt(tp_ + sx, sx)
                    wco_tile(tp_ + sx, g0 + tp_ + sx)
                    strs.append(P.rec_end())
                P.merge(strs)
            bset[0] = [0, 1, 2, 3, 4]
        TS = NTP
        CMn = A([8, 2, 256], BF16); CMnr = AR("CMn")
        CMV = A([16, 2, 256], BF16); CMVr = AR("CMV")
        CMKT = A([8, 4, 256], BF16, parts=64); CMKTr = AR("CMKT")
        Es = A([2, 4, 256], BF16, parts=8); Esr = [AR("Es0"), AR("Es1")]
        sm3 = A([2, 16], F32, parts=8); sm3r = [AR("sm30"), AR("sm31")]
        PT3 = A([1, 1024], BF16)[:, 0, :]; PT3r = AR("PT3")
        P.dma("pool", CMV[:, :, :, :], cmv_d.rearrange("j (c p) f -> p j c f", p=128), CMVr, writes=[CMVr])
        norm_T(Y[:, TS, :], Yr[TS], 1, XNTb[:, :, 0:128], [XNTbr[0]])
        cross_q(128, [XNTbr[0]])
        po3, po3r = banks[5], bres[5]
        for half in range(2):
            P.dma("pool", CMn[:, :, :, :], cmk_d[half * 8:(half + 1) * 8].rearrange("j (c p) f -> p j c f", p=128), CMnr,
                  writes=[CMnr])
            for jj in range(8):
                for h in range(4):
                    for mc in range(2):
                        I("pe", "transpose", [CMnr, identr], [*tbhr], out=tb[0:64, (h * 2 + mc) * 128:(h * 2 + mc + 1) * 128],
                          in_=CMn[:, jj, mc, h * 64:(h + 1) * 64], identity=identb[:])
                I("act", "activation", [*tbhr], [CMKTr], out=CMKT[:, jj, :, :],
                  in_=tb[0:64, :].rearrange("p (h m) -> p h m", h=4), func=AF.Copy)
            strs = []
            for sx in range(2):
                P.rec_begin(); bset[0] = [0, 1] if sx == 0 else [2, 3]
                for jj in range(sx, 8, 2):
                    j = half * 8 + jj
                    b = sx
                    bks = [bank(), bank()]
                    for h in range(4):
                        bk, bkr = bks[h // 2]
                        mm(bk[0:8, (h % 2) * 256:(h % 2 + 1) * 256], QcT[:, h, 8 * j:8 * j + 8], CMKT[:, jj, h, :], True, True,
                           [QcTr, CMKTr], bkr)
                    st_ = sm3[:, b, :]
                    for q_ in range(2):
                        I("dve", "reduce_max", [bks[q_][1]], [sm3r[b]], out=st_[:, 2 * q_:2 * q_ + 2],
                          in_=bks[q_][0][0:8, :].rearrange("p (a b) -> p a b", a=2), axis=AX.X)
                    I("dve", "tensor_scalar", [sm3r[b]], [sm3r[b]], out=st_[:, 0:4], in0=st_[:, 0:4], scalar1=-0.125, scalar2=None,
                      op0=ALU.mult)
                    for h in range(4):
                        bk, bkr = bks[h // 2]
                        I("act", "activation", [bkr, sm3r[b]], [Esr[b], sm3r[b]], out=Es[:, b, h, :],
                          in_=bk[0:8, (h % 2) * 256:(h % 2 + 1) * 256], func=AF.Exp, bias=st_[:, h:h + 1], scale=0.125,
                          accum_out=st_[:, 4 + h:5 + h])
                    I("dve", "reciprocal", [sm3r[b]], [sm3r[b]], out=st_[:, 8:12], in_=st_[:, 4:8])
                    I("dve", "tensor_tensor", [Esr[b], sm3r[b]], [Esr[b]], out=Es[:, b, :, :], in0=Es[:, b, :, :],
                      in1=st_[:, 8:12].unsqueeze(2).broadcast_to([8, 4, 256]), op=ALU.mult)
                    for mc in range(2):
                        for h in range(4):
                            c0_ = j * 64 + (mc * 4 + h) * 8
                            I("pe", "transpose", [Esr[b], identr], [*tbhr], out=tb[:, c0_:c0_ + 8],
                              in_=Es[:, b, h, mc * 128:(mc + 1) * 128], identity=identb[0:8, 0:8])

                strs.append(P.rec_end())
            P.merge(strs)
            bset[0] = [0, 1, 2, 3, 4]
            I("dve", "tensor_copy", [*tbhr], [PT3r], out=PT3[:, half * 512:(half + 1) * 512], in_=tb[:, half * 512:(half + 1) * 512])
        for j in range(16):
            for h in range(4):
                for mc in range(2):
                    c0_ = j * 64 + (mc * 4 + h) * 8
                    mm(po3[(h % 2) * 64:(h % 2 + 1) * 64, (j * 2 + h // 2) * 8:(j * 2 + h // 2) * 8 + 8],
                       CMV[:, j, mc, h * 64:(h + 1) * 64], PT3[:, c0_:c0_ + 8], mc == 0, mc == 1, [CMVr, PT3r], po3r)
        I("act", "activation", [po3r], [OcTr[0]], out=OcT[:, :, 0:128].rearrange("p c (j i) -> p j c i", i=8),
          in_=po3[:, 0:256].rearrange("p (j c i) -> p j c i", c=2, i=8), func=AF.Copy)
        wco_tile(0, TS)

        if stage <= 2:
            dump_y(range(NT))
            P.emit()
            return nc, P

        new_phase()
        USE_SQRT[0] = True
        NF = FH // 128
        XNTa = A([8, NT * 128], BF16); XNTar = [AR(f"XNTa{t}") for t in range(NT)]
        NSLOT = 12
        WG = A([NSLOT, 8, 128], BF16); WU = A([NSLOT, 8, 128], BF16); WD = A([NSLOT, D], BF16)
        Wsr = [AR(f"Ws{s_}") for s_ in range(NSLOT)]
        Hh = A([6, 512], BF16); Hr = [AR(f"H{j}") for j in range(6)]
        SG = A([2, 512], BF16); SGr = [AR("SG0"), AR("SG1")]
        OUT = A([1, D], F32)[:, 0, :]; OUTr = AR("OUT")
        gfin = A([1, D], F32)[:, 0, :]; gfinr = AR("gfin")
        P.dma("sp", gfin, g_final_d.partition_broadcast(128), gfinr, writes=[gfinr])
        passes = [list(range(0, 6)), list(range(6, 12)), list(range(12, 17)), list(range(17, 22))]
        groups = [(0, 4), (4, 4), (8, 4), (12, 4), (16, 1)]
        wd_v = w_down_d.rearrange("(f p) n -> f p n", p=128)
        wslot = {}
        nload = [0]

        def load_w(f):
            s_ = nload[0] % NSLOT
            nload[0] += 1
            wslot[f] = s_
            P.dma("pool", WG[:, s_], w_gate_d[:, f * 128:(f + 1) * 128].rearrange("(k p) n -> p k n", p=128), Wsr[s_],
                  writes=[Wsr[s_]], group=True)
            P.dma("pool", WU[:, s_], w_up_d[:, f * 128:(f + 1) * 128].rearrange("(k p) n -> p k n", p=128), Wsr[s_],
                  writes=[Wsr[s_]], group=True)
            P.dma("pool", WD[:, s_], wd_v[f], Wsr[s_], writes=[Wsr[s_]], group=True)

        for f in passes[0]:
            load_w(f)
        gcnt = [0]
        def ffn_norm_group(gi_):
            t0_, n_ = groups[gi_]
            for t in range(t0_, t0_ + n_):
                norm_T(Y[:, t, :], Yr[t], 3, XNTa[:, :, t * 128:(t + 1) * 128], [XNTar[t]], half=t % 2)

        ffn_norm_group(0)
        for pi, fl in enumerate(passes):
            for gi, (t0, n) in enumerate(groups):
                if pi + 1 < len(passes) and gi == 0:
                    for f in passes[pi + 1]:
                        load_w(f)
                merging = (pi == 0 and gi + 1 < len(groups))
                if merging:
                    P.rec_begin()
                    ffn_norm_group(gi + 1)
                    s_norm = P.rec_end()
                    P.rec_begin()
                ntok = n * 128
                tok = slice(t0 * 128, t0 * 128 + ntok)
                xr = [XNTar[t] for t in range(t0, t0 + n)]
                for j, f in enumerate(fl):
                    s_ = wslot[f]
                    b = gcnt[0] % 2
                    gcnt[0] += 1
                    pg, pgr = bank()
                    pu, pur = bank()
                    for k in range(8):
                        mm(pg[:, 0:ntok], WG[:, s_, k, :], XNTa[:, k, tok], k == 0, k == 7, [Wsr[s_]] + xr, pgr)
                    for k in range(8):
                        mm(pu[:, 0:ntok], WU[:, s_, k, :], XNTa[:, k, tok], k == 0, k == 7, [Wsr[s_]] + xr, pur)
                    I("act", "activation", [pgr], [SGr[b]], out=SG[:, b, 0:ntok], in_=pg[:, 0:ntok], func=AF.Silu)
                    I("dve", "tensor_tensor", [SGr[b], pur], [Hr[j]], out=Hh[:, j, 0:ntok], in0=SG[:, b, 0:ntok],
                      in1=pu[:, 0:ntok], op=ALU.mult)
                for ti in range(n):
                    t = t0 + ti
                    for c in range(2):
                        pd, pdr = bank()
                        for j, f in enumerate(fl):
                            s_ = wslot[f]
                            mm(pd[:, :], Hh[:, j, ti * 128:(ti + 1) * 128], WD[:, s_, c * 512:(c + 1) * 512], j == 0,
                               j == len(fl) - 1, [Hr[j], Wsr[s_]], pdr)
                        I("dve", "tensor_tensor", [Yr[t], pdr], [Yr[t]], out=Y[:, t, c * 512:(c + 1) * 512],
                          in0=Y[:, t, c * 512:(c + 1) * 512], in1=pd[:, :], op=ALU.add)
                if merging:
                    s_ffn = P.rec_end()
                    P.merge([s_ffn, s_norm])
                if pi == len(passes) - 1:
                    yo = yp_o.rearrange("(t p) d -> t p d", p=128)
                    for t in range(t0, t0 + n):
                        rstd, sr = norm_stats(Y[:, t, :], Yr[t], 0)
                        I("dve", "scalar_tensor_tensor", [Yr[t], sr, gfinr], [OUTr], out=OUT, in0=Y[:, t, :], scalar=rstd,
                          in1=gfin, op0=ALU.mult, op1=ALU.mult)
                        P.dma("sp", (yo[t] if t < NTP else ys_o), OUT, OUTr, reads=[OUTr])
        P.emit()
        return nc, P


def make_consts(hf):
    c = {}
    c["c_ident"] = np.eye(128, dtype=np.float32)
    i = np.arange(128)[:, None]; j = np.arange(256)[None, :]
    band = np.where((j >= i) & (j <= i + 128), 0.0, NEG).astype(np.float32)
    first = band.copy()
    if hf == 0:
        first[:, :128] = NEG
    c["c_mb_band"] = band; c["c_mb_first"] = first
    s = np.arange(128)[:, None]; t = np.arange(128)[None, :]
    c["c_mb_caus"] = np.where(s <= t, 0.0, NEG).astype(np.float32)
    c["c_mb_causs"] = np.where((s <= t) & (s // 8 == t // 8), 0.0, NEG).astype(np.float32)
    sel = np.zeros((4, 1024), np.float32)
    for h in range(4):
        sel[h, h * 128:(h + 1) * 128] = 1.0
        sel[h, 512 + h * 128:512 + (h + 1) * 128] = -1.0
    c["c_sel"] = sel
    pm = np.zeros((4, 2), np.float32)
    pm[:, 0] = 1.0 if hf else 0.0
    pm[:, 1] = 0.0 if hf else NEG
    c["c_pmask"] = pm
    r = np.arange(32)[:, None] % 8
    p = np.arange(128)[None, :]
    c["c_smc"] = np.where(p >= r, 0.0, NEG).astype(np.float32)
    smn = np.full((32, 16, 128), NEG, np.float32)
    for jq in range(16):
        for ii in range(8):
            smn[(np.arange(32) % 8) >= ii, jq, jq * 8 + ii] = 0.0
    c["c_smn"] = smn
    c["c_bt"] = np.where((s <= t) & (s // 8 == t // 8), 1.0, 0.0).astype(np.float32)
    e = np.zeros((128, 16), np.float32); e[np.arange(128), np.arange(128) // 8] = 1.0
    c["c_eseq"] = e
    return c

def shard_inputs(inp):
    maps = []
    W = ["w_in", "b_igate", "b_fgate", "attn_sinks", "g_mlstm_head", "w_out", "g_mix", "g_cross", "g_mem",
         "w_cq", "w_ck", "w_cv", "w_co", "g_ffn", "w_gate", "w_up", "w_down"]
    wd = {k: np.ascontiguousarray(np.asarray(inp[k], np.float32)[0]) for k in W}
    wd["g_final"] = np.ascontiguousarray(np.asarray(inp["g_final"], np.float32))
    xp = np.asarray(inp["x_prompt"], np.float32); xs = np.asarray(inp["x_sample"], np.float32)
    for c in range(8):
        b, hf = c // 2, c % 2
        m = dict(wd)
        m["xp"] = np.ascontiguousarray(xp[b, hf * 2048:(hf + 1) * 2048])
        m["xpre"] = np.ascontiguousarray(xp[b, 0:2048]) if hf else np.zeros((2048, 1024), np.float32)
        m["xs"] = np.ascontiguousarray(xs[16 * c:16 * c + 16].reshape(128, 1024))
        m["mem"] = np.ascontiguousarray(np.asarray(inp["mem_prompt"], np.float32)[b])
        sl = slice(16 * c, 16 * c + 16)
        m["csk"] = np.ascontiguousarray(np.asarray(inp["cache_swa_k"], np.float32)[0, sl].reshape(16, 128, 128))
        m["csv"] = np.ascontiguousarray(np.asarray(inp["cache_swa_v"], np.float32)[0, sl].reshape(16, 128, 128))
        m["sC"] = np.ascontiguousarray(np.asarray(inp["state_mlstm_C"], np.float32)[0, sl])
        m["sn"] = np.ascontiguousarray(np.asarray(inp["state_mlstm_n"], np.float32)[0, sl])
        m["sm"] = np.ascontiguousarray(np.asarray(inp["state_mlstm_m"], np.float32)[0, sl])
        m["cmk"] = np.ascontiguousarray(np.asarray(inp["cache_mem_k"], np.float32)[0, sl].reshape(16, 256, 256))
        m["cmv"] = np.ascontiguousarray(np.asarray(inp["cache_mem_v"], np.float32)[0, sl].reshape(16, 256, 256))
        m.update(make_consts(hf))
        sk = wd["attn_sinks"]
        sc = np.zeros((32, 2), np.float32)
        rr = np.arange(32)
        for h in range(2):
            sc[:, h] = sk[4 * h + 2 * ((rr % 16) // 8) + rr // 16]
        m["c_sinkcol"] = sc
        maps.append(m)
    return maps

def gather(res):
    f = np.float32
    yp = np.zeros((4, 4096, 1024), f); ys = np.zeros((128, 8, 1024), f)
    skp = np.zeros((1, 4, 128, 2, 64), f); svp = np.zeros_like(skp)
    Cp = np.zeros((1, 4, 4, 128, 64), f); npp = np.zeros((1, 4, 4, 64), f); mp = np.zeros((1, 4, 4), f)
    mkp = np.zeros((1, 4, 256, 4, 64), f); mvp = np.zeros_like(mkp)
    sks = np.zeros((1, 128, 128, 2, 64), f); svs = np.zeros_like(sks)
    Cs = np.zeros((1, 128, 4, 128, 64), f); ns = np.zeros((1, 128, 4, 64), f); ms = np.zeros((1, 128, 4), f)
    for c in range(8):
        r = res[c]; b, hf = c // 2, c % 2
        yp[b, hf * 2048:(hf + 1) * 2048] = r["yp"]
        ys[16 * c:16 * c + 16] = r["ys"].reshape(16, 8, 1024)
        if hf == 1:
            skp[0, b] = r["swak"].reshape(128, 2, 64); svp[0, b] = r["swav"].reshape(128, 2, 64)
            Cp[0, b] = r["Cp"]; npp[0, b] = r["np"]; mp[0, b] = r["mp"].reshape(4)
        else:
            mkp[0, b] = r["memk"].reshape(256, 4, 64); mvp[0, b] = r["memv"].reshape(256, 4, 64)
        sl = slice(16 * c, 16 * c + 16)
        sks[0, sl] = r["sks"].reshape(16, 128, 2, 64); svs[0, sl] = r["svs"].reshape(16, 128, 2, 64)
        Cs[0, sl] = r["Cs"]; ns[0, sl] = r["ns"]; ms[0, sl] = r["ms"]
    return (yp, ys, skp, svp, Cp, npp, mp, mkp, mvp, sks, svs, Cs, ns, ms)


_CACHE = {}


def kernel(**inputs):
    if "nc" not in _CACHE:
        _CACHE["nc"] = build_program(3)[0]
    nc = _CACHE["nc"]
    maps = shard_inputs(inputs)
    res = run_bass_kernel_spmd(nc, maps, core_ids=list(range(8)))
    return gather(res.results)
```

```python
import contextlib
from concourse.bass_utils import run_bass_kernel_spmd
import numpy as np
import concourse.bass as bass
import concourse.mybir as mybir

F32 = mybir.dt.float32
BF16 = mybir.dt.bfloat16
I32 = mybir.dt.int32
AF = mybir.ActivationFunctionType
ALU = mybir.AluOpType
AX = mybir.AxisListType

ENGS = ("pe", "act", "dve", "pool", "sp")


class Res:
    __slots__ = ("name", "last_w", "readers", "sem", "dcount", "excl")

    def __init__(self, name):
        self.name = name
        self.last_w = None
        self.readers = []
        self.sem = None
        self.dcount = 0
        self.excl = False


class Op:
    __slots__ = ("eng", "fn", "deps", "dma_res", "sig", "cnt", "k", "group")

    def __init__(self, eng, fn, dma_res):
        self.eng = eng
        self.fn = fn
        self.deps = set()
        self.dma_res = dma_res
        self.sig = False
        self.cnt = 0
        self.k = 0


class Prog:
    def __init__(self, nc):
        self.nc = nc
        self.ops = []
        self.nres = 0
        self.inherit = []
        self.phase_res = []

    def res(self, name=None, arena=False):
        self.nres += 1
        r = Res(name or f"r{self.nres}")
        if arena:
            r.readers = list(self.inherit)
            self.phase_res.append(r)
        return r

    def new_phase(self):
        inh = set(self.inherit)
        for r in self.phase_res:
            if r.last_w is not None:
                inh.add(r.last_w)
            inh.update(r.readers)
        self.inherit = sorted(inh)
        self.phase_res = []

    def rec_begin(self):
        self._rec = []

    def rec_end(self):
        r = self._rec
        self._rec = None
        return r

    def merge(self, streams):
        streams = [s_ for s_ in streams if s_]
        pos = [0] * len(streams)
        while True:
            best = None
            for k, s_ in enumerate(streams):
                if pos[k] < len(s_):
                    f = pos[k] / len(s_)
                    if best is None or f < best[0]:
                        best = (f, k)
            if best is None:
                break
            k = best[1]
            a, kw = streams[k][pos[k]]
            pos[k] += 1
            self.op(*a, **kw)

    def op(self, eng, fn, reads=(), writes=(), dma_res=None, accum=False, group=False):
        if getattr(self, "_rec", None) is not None:
            self._rec.append(((eng, fn, tuple(reads), tuple(writes)), dict(dma_res=dma_res, accum=accum, group=group)))
            return None
        i = len(self.ops)
        o = Op(eng, fn, dma_res)
        for r in reads:
            if r.last_w is not None:
                o.deps.add(r.last_w)
            if r.excl:
                for q in r.readers:
                    if self.ops[q].eng != eng:
                        o.deps.add(q)
            r.readers.append(i)
        for r in writes:
            if r.last_w is not None:
                lw = self.ops[r.last_w]
                if group and lw.dma_res is not None and lw.dma_res is dma_res:
                    o.deps |= lw.deps
                elif not (accum and lw.eng == "pe" and eng == "pe"):
                    o.deps.add(r.last_w)
            latest = {}
            for q in r.readers:
                if q == i:
                    continue
                oq = self.ops[q]
                if oq.dma_res is not None:
                    o.deps.add(q)
                elif latest.get(oq.eng, -1) < q:
                    latest[oq.eng] = q
            o.deps.update(latest.values())
            r.last_w = i
            r.readers = []
        if eng == "pe":
            o.deps = {d for d in o.deps if self.ops[d].eng != "pe" or self.ops[d].dma_res is not None}
        self.ops.append(o)
        return i

    def dma(self, eng, out, in_, res, reads=(), writes=(), group=False, **kw):
        kw = dict(kw); kw["out"] = out; kw["in_"] = in_
        return self.op(eng, ("dma_start", kw), reads=reads, writes=writes, dma_res=res, group=group)

    def I(self, eng, name, reads=(), writes=(), **kw):
        return self.op(eng, (name, kw), reads=reads, writes=writes)

    def emit(self, final_wait_all=True):
        nc = self.nc
        ops = self.ops
        for o in ops:
            for d in o.deps:
                ops[d].sig = True
        per_eng = {e: [] for e in ENGS}
        for i, o in enumerate(ops):
            per_eng[o.eng].append(i)
        import contextlib
        with contextlib.ExitStack() as st:
            esem = {e: st.enter_context(nc.semaphore(f"s_{e}")) for e in ENGS}
            ecount = {e: 0 for e in ENGS}
            dma_sems = []
            for i, o in enumerate(ops):
                if o.dma_res is not None:
                    r = o.dma_res
                    if r.sem is None:
                        r.sem = st.enter_context(nc.semaphore(f"d{len(dma_sems)}_{r.name}"))
                        dma_sems.append(r)
                    r.dcount += 1
                    o.cnt = 16 * r.dcount
                elif o.sig:
                    ecount[o.eng] += 1
                    o.cnt = ecount[o.eng]
            self.n_dma_sems = len(dma_sems)
            know = {e: {} for e in ENGS}
            know_issue = [None] * len(ops)

            def key_of(o):
                return ("d", id(o.dma_res)) if o.dma_res is not None else ("e", o.eng)

            block = st.enter_context(nc.Block())
            handles = {}

            plan = [None] * len(ops)
            for i, o in enumerate(ops):
                kn = know[o.eng]
                need = {}
                for d in o.deps:
                    p = ops[d]
                    k = key_of(p)
                    if kn.get(k, 0) >= p.cnt:
                        continue
                    if need.get(k, (0, None))[0] < p.cnt:
                        need[k] = (p.cnt, d)
                waits = []
                for k, (cnt, d) in need.items():
                    p = ops[d]
                    sem = p.dma_res.sem if p.dma_res is not None else esem[p.eng]
                    waits.append((sem, cnt))
                    kn[k] = max(kn.get(k, 0), cnt)
                    ki = know_issue[d]
                    for kk, vv in ki.items():
                        if kn.get(kk, 0) < vv:
                            kn[kk] = vv
                know_issue[i] = dict(kn)
                plan[i] = waits
            self.n_waits = sum(len(w) for w in plan)

            def make(ename):
                def body(eh):
                    for i in per_eng[ename]:
                        o = ops[i]
                        for sem, cnt in plan[i]:
                            eh.wait_ge(sem, cnt)
                        ins = getattr(eh, o.fn[0])(**o.fn[1])
                        if o.dma_res is not None:
                            ins.then_inc(o.dma_res.sem, 16)
                        elif o.sig:
                            ins.then_inc(esem[o.eng], 1)
                    if ename == "sp" and final_wait_all:
                        for r in dma_sems:
                            eh.wait_ge(r.sem, 16 * r.dcount)
                        for e in ("pe", "act", "dve", "pool"):
                            if ecount[e]:
                                eh.wait_ge(esem[e], ecount[e])
                return body

            block.tensor(make("pe"))
            block.scalar(make("act"))
            block.vector(make("dve"))
            block.gpsimd(make("pool"))
            block.sync(make("sp"))


D = 1024
FH = 2816
EPS = 1e-6
NTP = 16
NT = 17
GT = 2
NEG = -30000.0
DBG_G0 = 2


def build_program(stage=3, debug=False):
    nc = bass.Bass("TRN2", target_bir_lowering=False)
    P = Prog(nc)

    def din(name, shape, dt=F32):
        return nc.dram_tensor(name, list(shape), dt, kind="ExternalInput").ap()

    def dout(name, shape):
        return nc.dram_tensor(name, list(shape), F32, kind="ExternalOutput").ap()

    xp_d = din("xp", [2048, D]); xpre_d = din("xpre", [2048, D]); xs_d = din("xs", [128, D])
    mem_d = din("mem", [256, D])
    csk_d = din("csk", [16, 128, 128]); csv_d = din("csv", [16, 128, 128])
    sC_d = din("sC", [16, 4, 128, 64]); sn_d = din("sn", [16, 4, 64]); sm_d = din("sm", [16, 4])
    cmk_d = din("cmk", [16, 256, 256]); cmv_d = din("cmv", [16, 256, 256])
    w_in_d = din("w_in", [D, 2312]); b_i_d = din("b_igate", [4]); b_f_d = din("b_fgate", [4])
    sinks_d = din("attn_sinks", [8]); ghead_d = din("g_mlstm_head", [512]); w_out_d = din("w_out", [D, D])
    g_mix_d = din("g_mix", [D]); g_cross_d = din("g_cross", [D]); g_mem_d = din("g_mem", [D])
    w_cq_d = din("w_cq", [D, 256]); w_ck_d = din("w_ck", [D, 256]); w_cv_d = din("w_cv", [D, 256])
    w_co_d = din("w_co", [256, D]); g_ffn_d = din("g_ffn", [D])
    w_gate_d = din("w_gate", [D, FH]); w_up_d = din("w_up", [D, FH]); w_down_d = din("w_down", [FH, D])
    g_final_d = din("g_final", [D])
    ident_d = din("c_ident", [128, 128]); mb_band_d = din("c_mb_band", [128, 256]); mb_first_d = din("c_mb_first", [128, 256])
    mb_caus_d = din("c_mb_caus", [128, 128]); mb_causs_d = din("c_mb_causs", [128, 128])
    sel_d = din("c_sel", [4, 1024]); pmask_d = din("c_pmask", [4, 2])
    smc_d = din("c_smc", [32, 128]); smn_d = din("c_smn", [32, 16, 128]); sinkcol_d = din("c_sinkcol", [32, 2])
    bt_d = din("c_bt", [128, 128]); eseq_d = din("c_eseq", [128, 16])

    yp_o = dout("yp", [2048, D]); ys_o = dout("ys", [128, D])
    swak_o = dout("swak", [128, 128]); swav_o = dout("swav", [128, 128])
    Cp_o = dout("Cp", [4, 128, 64]); np_o = dout("np", [4, 64]); mp_o = dout("mp", [4, 1])
    memk_o = dout("memk", [256, 256]); memv_o = dout("memv", [256, 256])
    sks_o = dout("sks", [16, 128, 128]); svs_o = dout("svs", [16, 128, 128])
    Cs_o = dout("Cs", [16, 4, 128, 64]); ns_o = dout("ns", [16, 4, 64]); ms_o = dout("ms", [16, 4])

    st = contextlib.ExitStack()
    with st:
        def sb(name, shape, dt):
            return st.enter_context(nc.sbuf_tensor(name, list(shape), dt))

        def ps(name, shape, dt):
            return st.enter_context(nc.psum_tensor(name, list(shape), dt))

        banks = [ps(f"bk{i}", [128, 512], F32) for i in range(7)]
        bres = [P.res(f"bk{i}") for i in range(7)]
        for r_ in bres:
            r_.excl = True
        tb = ps("tb", [128, 1024], BF16)
        tbh = [tb[:, 0:512], tb[:, 512:1024]]
        tbhr = [P.res("tbA"), P.res("tbB")]
        for r_ in tbhr:
            r_.excl = True
        tb2 = banks[6][:, :].bitcast(BF16)
        TSEL = [(tb[:, 0:512], tbhr[0]), (tb2[:, 0:512], bres[6])]
        bki = [0]

        bset = [[0, 1, 2, 3, 4]]
        bcnt = {}

        def bank():
            key = tuple(bset[0])
            c = bcnt.get(key, 0)
            bcnt[key] = c + 1
            i = bset[0][c % len(key)]
            return banks[i], bres[i]

        Y = sb("Y", [128, NT, D], F32)
        Yr = [P.res(f"Y{t}") for t in range(NT)]
        identb = sb("identb", [128, 128], BF16); identr = P.res("identb")
        identf = sb("identf", [128, 128], F32); identfr = P.res("identf")
        onesb = sb("onesb", [128, 128], BF16); onesbr = P.res("onesb")
        onesf = sb("onesf", [128, 256], F32); onesfr = P.res("onesf")
        SEL = sb("SEL", [4, 1024], F32); selr = P.res("SEL")
        gcols = sb("gcols", [128, 4, 8], F32); gcolsr = P.res("gcols")
        gheadc = sb("gheadc", [128, 4], F32); gheadr = P.res("ghead")
        sinkb = sb("sinkb", [128, 16], F32); sinkbr = P.res("sinkb")
        gb4 = sb("gb4", [4, 4], F32); gb4r = P.res("gb4")
        mbband = sb("mbband", [128, 256], BF16); mbbandr = P.res("mbband")
        mbfirst = sb("mbfirst", [128, 256], BF16); mbfirstr = P.res("mbfirst")
        mbcaus = sb("mbcaus", [128, 128], BF16); mbcausr = P.res("mbcaus")
        stat = sb("stat", [128, 8, 4], F32)
        statr = [P.res(f"stat{i}") for i in range(8)]
        stati = [0]
        USE_SQRT = [False]
        xsb = sb("xsb", [128, 2, D], BF16); xsbr = [P.res("xsb0"), P.res("xsb1")]
        Cst = sb("Cst", [64, 4, 129], F32); Cstr = [P.res(f"Cst{h}") for h in range(4)]
        ARN = 64400
        arena = sb("arena", [128, ARN], BF16)
        aoff = [0]

        def A(shape, dt, parts=128, name=None):
            n = int(np.prod(shape))
            nb = n * (4 if dt == F32 else 2)
            n16 = (nb + 1) // 2
            n16 = (n16 + 15) // 16 * 16
            assert aoff[0] + n16 <= ARN, f"arena overflow {aoff[0]}+{n16} ({name})"
            v = arena[0:parts, aoff[0]:aoff[0] + n16]
            aoff[0] += n16
            if dt == F32:
                v = v.bitcast(F32)
            v = v[:, 0:n]
            if len(shape) == 2:
                v = v.rearrange("p (a b) -> p a b", a=shape[0])
            elif len(shape) == 3:
                v = v.rearrange("p (a b c) -> p a b c", a=shape[0], b=shape[1])
            return v

        def new_phase():
            P.new_phase()
            aoff[0] = 0

        def AR(name):
            return P.res(name, arena=True)

        def A_at(off, shape, dt, parts=128):
            n = int(np.prod(shape))
            nb = n * (4 if dt == F32 else 2)
            n16 = ((nb + 1) // 2 + 15) // 16 * 16
            v = arena[0:parts, off:off + n16]
            if dt == F32:
                v = v.bitcast(F32)
            v = v[:, 0:n]
            if len(shape) == 2:
                v = v.rearrange("p (a b) -> p a b", a=shape[0])
            elif len(shape) == 3:
                v = v.rearrange("p (a b c) -> p a b c", a=shape[0], b=shape[1])
            return v, off + n16

        def ARalias(name, olds):
            r = P.res(name, arena=True)
            dd = set(r.readers)
            for o_ in olds:
                if o_.last_w is not None:
                    dd.add(o_.last_w)
                dd.update(o_.readers)
            r.readers = sorted(dd)
            return r

        I = P.I

        def mm(out, lhsT, rhs, start, stop, reads, wres):
            I("pe", "matmul", reads, [wres], out=out, lhsT=lhsT, rhs=rhs, start=start, stop=stop)

        P.dma("pool", identb[:], ident_d, identr, writes=[identr])
        P.dma("sp", identf[:], ident_d, identfr, writes=[identfr])
        I("dve", "memset", [], [onesbr], ap=onesb[:], constant=1.0)
        I("dve", "memset", [], [onesfr], ap=onesf[:], constant=1.0)
        P.dma("sp", SEL[:], sel_d, selr, writes=[selr])
        for i, g in enumerate((g_mix_d, g_cross_d, g_mem_d, g_ffn_d)):
            P.dma("sp", gcols[:, i, :], g.rearrange("(k p) -> p k", p=128), gcolsr, writes=[gcolsr], group=True, allow_slow_non_contiguous=True)
        P.dma("sp", gheadc[:], ghead_d.rearrange("(h p) -> p h", p=128), gheadr, writes=[gheadr], allow_slow_non_contiguous=True)
        P.dma("sp", gb4[:, 0:1], b_i_d.rearrange("(h o) -> h o", o=1), gb4r, writes=[gb4r], group=True, allow_slow_non_contiguous=True)
        P.dma("sp", gb4[:, 1:2], b_f_d.rearrange("(h o) -> h o", o=1), gb4r, writes=[gb4r], group=True, allow_slow_non_contiguous=True)
        P.dma("sp", gb4[:, 2:4], pmask_d, gb4r, writes=[gb4r], group=True, allow_slow_non_contiguous=True)
        P.dma("sp", sinkb[:, 0:8], sinks_d.partition_broadcast(128), sinkbr, writes=[sinkbr])
        I("dve", "tensor_scalar", [sinkbr], [sinkbr], out=sinkb[:, 8:16], in0=sinkb[:, 0:8], scalar1=-1.0, scalar2=None,
          op0=ALU.mult)
        I("dve", "tensor_scalar", [gb4r], [gb4r], out=gb4[:, 1:2], in0=gb4[:, 1:2], scalar1=-1.0, scalar2=None, op0=ALU.mult)
        P.dma("pool", mbband[:], mb_band_d, mbbandr, writes=[mbbandr])
        P.dma("pool", mbfirst[:], mb_first_d, mbfirstr, writes=[mbfirstr])
        P.dma("pool", mbcaus[:], mb_caus_d, mbcausr, writes=[mbcausr])
        for h in range(4):
            I("dve", "memset", [], [Cstr[h]], ap=Cst[:, h, :], constant=0.0)

        def SELh(h, n=128):
            return SEL[:, h * 128:h * 128 + n]

        def NSELh(h, n=128):
            return SEL[:, 512 + h * 128:512 + h * 128 + n]

        def norm_stats(src, sres, jb=0):
            i = stati[0] % 8
            stati[0] += 1
            sr = statr[i]
            I("act", "activation", [sres], [xsbr[jb], sr], out=xsb[:, jb, :], in_=src, func=AF.Square, accum_out=stat[:, i, 0:1])
            I("dve", "tensor_scalar", [sr], [sr], out=stat[:, i, 1:2], in0=stat[:, i, 0:1], scalar1=1.0 / D, scalar2=EPS,
              op0=ALU.mult, op1=ALU.add)
            if USE_SQRT[0]:
                I("act", "activation", [sr], [sr], out=stat[:, i, 2:3], in_=stat[:, i, 1:2], func=AF.Sqrt)
                I("dve", "reciprocal", [sr], [sr], out=stat[:, i, 3:4], in_=stat[:, i, 2:3])
            else:
                I("act", "activation", [sr], [sr], out=stat[:, i, 2:3], in_=stat[:, i, 1:2], func=AF.Ln)
                I("act", "activation", [sr], [sr], out=stat[:, i, 3:4], in_=stat[:, i, 2:3], func=AF.Exp, scale=-0.5)
            return stat[:, i, 3:4], sr

        xsi = [0]

        def norm_T(src, sres, gi, dst, dres, half=None, tsel=None):
            b = xsi[0] % 2 if half is None else half
            xsi[0] += 1
            rstd, sr = norm_stats(src, sres, b)
            I("dve", "tensor_scalar", [sres, sr], [xsbr[b]], out=xsb[:, b, :], in0=src, scalar1=rstd, scalar2=None, op0=ALU.mult)
            if half is None:
                for k in range(8):
                    I("pe", "transpose", [xsbr[b], identr], [*tbhr], out=tb[:, k * 128:(k + 1) * 128],
                      in_=xsb[:, b, k * 128:(k + 1) * 128], identity=identb[:])
                for k in range(8):
                    I("act", "activation", [*tbhr, gcolsr], dres, out=dst[:, k, :], in_=tb[:, k * 128:(k + 1) * 128],
                      func=AF.Copy, scale=gcols[:, gi, k:k + 1])
            else:
                tq, tqr = (tbh[half], tbhr[half]) if tsel is None else tsel
                for kb in range(2):
                    for k4 in range(4):
                        k = kb * 4 + k4
                        I("pe", "transpose", [xsbr[b], identr], [tqr], out=tq[:, k4 * 128:(k4 + 1) * 128],
                          in_=xsb[:, b, k * 128:(k + 1) * 128], identity=identb[:])
                    for k4 in range(4):
                        k = kb * 4 + k4
                        I("act", "activation", [tqr, gcolsr], dres, out=dst[:, k, :],
                          in_=tq[:, k4 * 128:(k4 + 1) * 128], func=AF.Copy, scale=gcols[:, gi, k:k + 1])

        class NS:
            pass

        def alloc_mixer(gt, nkt, nvt):
            M = NS()
            M.WQ = A([8, 512], BF16); M.WQr = AR("WQ")
            M.WTOK = A([8, 1024], BF16); M.WTOKr = AR("WTOK")
            M.WK = M.WTOK[:, :, 0:128]; M.WKr = M.WTOKr
            M.WMQ = A([8, 256], BF16); M.WMQr = AR("WMQ")
            M.WMK = M.WTOK[:, :, 256:512]; M.WMKr = M.WTOKr
            M.WOG = A([8, 512], BF16); M.WOGr = AR("WOG")
            M.WGT = A([8, 8], BF16); M.WGTr = AR("WGT")
            M.WOA = A([4, 1024], BF16); M.WOAr = AR("WOA")
            M.WOM = A([4, 1024], BF16); M.WOMr = AR("WOM")

            def wload(dst, res, src, **kw):
                P.dma("pool", dst, src, res, writes=[res], **kw)

            def wcols(a_, b_):
                return w_in_d[:, a_:b_].rearrange("(k p) n -> p k n", p=128)
            wload(M.WTOK[:, :, 0:256], M.WTOKr, wcols(512, 768), group=True)
            wload(M.WTOK[:, :, 256:1024], M.WTOKr, wcols(1024, 1792), group=True)
            wload(M.WGT[:], M.WGTr, wcols(2304, 2312), allow_slow_non_contiguous=True)
            wload(M.WQ[:], M.WQr, wcols(0, 512))
            wload(M.WMQ[:], M.WMQr, wcols(768, 1024))
            wload(M.WOG[:], M.WOGr, wcols(1792, 2304))
            wload(M.WOA[:], M.WOAr, w_out_d[0:512, :].rearrange("(c p) n -> p c n", p=128))
            wload(M.WOM[:], M.WOMr, w_out_d[512:1024, :].rearrange("(h p) n -> p h n", p=128))
            M.KT = A([2, nkt * 128], BF16, parts=64); M.KTr = [AR(f"KT{i}") for i in range(nkt)]
            M.Vt = A([nvt, 128], BF16); M.Vtr = [AR(f"Vt{i}") for i in range(nvt)]
            M.XNTg = A([8, gt * 128], BF16); M.XNTgr = [AR(f"XNTg{i}") for i in range(gt)]
            M.QT = A([8, gt * 128], BF16, parts=64); M.QTr = AR("QT")
            M.MQT = A([4, gt * 128], BF16, parts=64); M.MQTr = AR("MQT")
            M.MKT = A([4, gt * 128], BF16, parts=64); M.MKTr = AR("MKT")
            M.SGT = A([4, gt * 128], BF16); M.SGTr = AR("SGT")
            M.MKtok = A([gt, 256], BF16); M.MKtokr = [AR(f"MKtok{i}") for i in range(gt)]
            M.MVaug = A([gt, 4, 129], BF16); M.MVaugr = [AR(f"MVaug{i}") for i in range(gt)]
            M.ATTT = A([4, gt * 128], BF16); M.ATTTr = [AR(f"ATTT{i}") for i in range(gt)]
            M.HMT = A([4, gt * 128], BF16); M.HMTr = [AR(f"HMT{i}") for i in range(gt)]
            M.NG = gt * 128
            NG_ = M.NG
            M.G_IG = A([1, NG_ + 1], F32, parts=4)[:, 0, :]; M.G_E = A([1, NG_], F32, parts=4)[:, 0, :]
            M.G_L1 = A([1, NG_], F32, parts=4)[:, 0, :]; M.G_B = A([1, NG_ + 1], F32, parts=4)[:, 0, :]
            M.G_A = A([1, NG_], F32, parts=4)[:, 0, :]; M.G_M = A([1, NG_ + 1], F32, parts=4)[:, 0, :]
            M.G_BM = A([1, NG_], F32, parts=4)[:, 0, :]; M.G_DM = A([1, NG_], F32, parts=4)[:, 0, :]
            M.Gr = AR("G_IG"); M.G_Br = AR("G_B"); M.G_Ar = AR("G_A"); M.G_Mr = AR("G_M"); M.G_BMr = AR("G_BM"); M.G_DMr = AR("G_DM")
            M.SKV = A([1, 256], F32)[:, 0, :]; M.SKVr = AR("SKV")
            M.Ebuf = A([4, 256], BF16); M.Er = AR("E")
            M.PTs = A([1, 1024], BF16); M.PTsr = [AR("PTs0")] * 2
            M.sm_st = A([1, 32], F32)[:, 0, :]; M.smr = AR("sm_st")
            M.WKC = A([1, 8], F32)[:, 0, :]; M.WKCr = AR("WKC")
            M.DG = A([1, 8], F32, parts=4)[:, 0, :]; M.DGr = AR("DG")
            M.VW = A([4, 129], BF16); M.VWr = [AR(f"VW{h}") for h in range(4)]
            M.Cb = A([4, 257], BF16, parts=64); M.Cbr = [AR(f"Cb{h}") for h in range(4)]
            M.WT = A([4, 128], BF16); M.WTr = AR("WT")
            M.ST = A([4, 128], BF16); M.STr = AR("ST")
            M.WI = A([4, 128], BF16); M.WIr = AR("WI")
            M.QW = A([4, 128], BF16, parts=64); M.QWr = AR("QW")
            M.LOWB = A([4, 128], F32); M.LOWBr = AR("LOWB")
            M.T1 = A([4, 128], F32); M.T1r = AR("T1")
            M.T2 = A([4, 128], F32); M.T2r = AR("T2")
            M.USQ = A([4, 128], BF16); M.USQr = AR("USQ")
            for i in range(gt):
                I("dve", "memset", [], [M.MVaugr[i]], ap=M.MVaug[:, i, :, 128:129], constant=1.0)
            M.PB = [(M.XNTg, M.XNTgr, M.MKtok, M.MKtokr, M.MVaug, M.MVaugr)]
            if gt > 1:
                x2 = A([8, gt * 128], BF16); x2r = [AR(f"XNTh{i}") for i in range(gt)]
                k2 = A([gt, 256], BF16); k2r = [AR(f"MKtoh{i}") for i in range(gt)]
                v2 = A([gt, 4, 129], BF16); v2r = [AR(f"MVauh{i}") for i in range(gt)]
                for i in range(gt):
                    I("dve", "memset", [], [v2r[i]], ap=v2[:, i, :, 128:129], constant=1.0)
                M.PB.append((x2, x2r, k2, k2r, v2, v2r))
            return M

        def use(pb):
            M.XNTg, M.XNTgr, M.MKtok, M.MKtokr, M.MVaug, M.MVaugr = M.PB[pb]

        M = alloc_mixer(GT, NTP + 1, NTP + 1)
        I("dve", "memset", [], [M.G_Br], ap=M.G_B[:, 0:1], constant=0.0)
        I("dve", "memset", [], [M.G_Mr], ap=M.G_M[:, 0:1], constant=0.0)

        def tok_major(ti, xcols, xres, vslot, want_kv_out=None):
            b0, b0r = bank()
            for k in range(8):
                mm(b0[:, :], M.XNTg[:, k, xcols], M.WTOK[:, k, 0:512], k == 0, k == 7, [xres, M.WTOKr], b0r)
            I("act", "activation", [b0r], [M.Vtr[vslot]], out=M.Vt[:, vslot, :], in_=b0[:, 128:256], func=AF.Copy)
            I("act", "activation", [b0r], [M.MKtokr[ti]], out=M.MKtok[:, ti, :], in_=b0[:, 256:512], func=AF.Copy, scale=0.125)
            if want_kv_out is not None:
                I("dve", "tensor_copy", [b0r], [M.SKVr], out=M.SKV[:, :], in_=b0[:, 0:256])
                if want_kv_out == "sample":
                    P.dma("sp", sks_o[:, 120:128, :], M.SKV[:, 0:128], M.SKVr, reads=[M.SKVr], group=True)
                    P.dma("sp", svs_o[:, 120:128, :], M.SKV[:, 128:256], M.SKVr, reads=[M.SKVr], group=True)
                else:
                    P.dma("sp", swak_o, M.SKV[:, 0:128], M.SKVr, reads=[M.SKVr], group=True)
                    P.dma("sp", swav_o, M.SKV[:, 128:256], M.SKVr, reads=[M.SKVr], group=True)
            b1, b1r = bank()
            for k in range(8):
                mm(b1[:, :], M.XNTg[:, k, xcols], M.WTOK[:, k, 512:1024], k == 0, k == 7, [xres, M.WTOKr], b1r)
            I("dve", "tensor_copy", [b1r], [M.MVaugr[ti]], out=M.MVaug[:, ti, :, 0:128],
              in_=b1[:, :].rearrange("p (h d) -> p h d", h=4))

        def feat64(W, Wr, nh, dst, dres, ntok, xres, scale=None, dcol0=0):
            for h0 in range(0, nh, 2):
                bk, bkr = bank()
                for hh in range(2):
                    h = h0 + hh
                    for k in range(8):
                        mm(bk[0:64, hh * 256:hh * 256 + ntok], W[:, k, h * 64:(h + 1) * 64], M.XNTg[:, k, 0:ntok],
                           k == 0, k == 7, [Wr] + xres, bkr)
                src = bk[0:64, :].rearrange("p (a b) -> p a b", a=2)[:, :, 0:ntok]
                kw = {} if scale is None else {"scale": scale}
                I("act", "activation", [bkr], dres, out=dst[:, h0:h0 + 2, dcol0:dcol0 + ntok], in_=src, func=AF.Copy, **kw)

        def gates(ntok, xres, prefix):
            pg, pgr = bank()
            for k in range(8):
                mm(pg[0:4, 0:ntok], M.WGT[:, k, 0:4], M.XNTg[:, k, 0:ntok], k == 0, k == 7, [M.WGTr] + xres, pgr)
            for k in range(8):
                mm(pg[0:4, 256:256 + ntok], M.WGT[:, k, 4:8], M.XNTg[:, k, 0:ntok], k == 0, k == 7, [M.WGTr] + xres, pgr)
            I("act", "activation", [pgr, gb4r], [M.Gr], out=M.G_IG[:, 1:ntok + 1], in_=pg[0:4, 0:ntok], func=AF.Identity,
              bias=gb4[:, 0:1])
            I("act", "activation", [pgr, gb4r], [M.Gr], out=M.G_E[:, 0:ntok], in_=pg[0:4, 256:256 + ntok], func=AF.Exp,
              bias=gb4[:, 1:2], scale=-1.0)
            I("act", "activation", [M.Gr], [M.Gr], out=M.G_L1[:, 0:ntok], in_=M.G_E[:, 0:ntok], func=AF.Ln, bias=1.0)
            if prefix == "sample":
                return
            if prefix:
                I("dve", "tensor_scalar", [M.Gr, gb4r], [M.Gr], out=M.G_L1[:, 0:ntok], in0=M.G_L1[:, 0:ntok], scalar1=gb4[:, 2:3],
                  scalar2=None, op0=ALU.mult)
            I("dve", "tensor_tensor_scan", [M.Gr, M.G_Br, onesfr], [M.G_Br], out=M.G_B[:, 1:ntok + 1], data0=onesf[0:4, 0:ntok],
              data1=M.G_L1[:, 0:ntok], initial=M.G_B[:, 0:1], op0=ALU.mult, op1=ALU.subtract)
            I("dve", "scalar_tensor_tensor", [M.Gr, M.G_Br, gb4r], [M.G_Ar], out=M.G_A[:, 0:ntok], in0=M.G_IG[:, 1:ntok + 1],
              scalar=(gb4[:, 3:4] if prefix else 0.0), in1=M.G_B[:, 1:ntok + 1], op0=ALU.add, op1=ALU.subtract)
            I("dve", "tensor_tensor_scan", [M.G_Ar, M.G_Mr, onesfr], [M.G_Mr], out=M.G_M[:, 1:ntok + 1], data0=onesf[0:4, 0:ntok],
              data1=M.G_A[:, 0:ntok], initial=M.G_M[:, 0:1], op0=ALU.mult, op1=ALU.max)
            I("dve", "tensor_tensor", [M.G_Br, M.G_Mr], [M.G_BMr], out=M.G_BM[:, 0:ntok], in0=M.G_B[:, 1:ntok + 1],
              in1=M.G_M[:, 1:ntok + 1], op=ALU.add)
            for ci in range(ntok // 128):
                I("dve", "tensor_scalar", [M.G_Mr], [M.G_DMr], out=M.G_DM[:, ci * 128:(ci + 1) * 128],
                  in0=M.G_M[:, 1 + ci * 128:1 + (ci + 1) * 128], scalar1=M.G_M[:, ci * 128:ci * 128 + 1], scalar2=None,
                  op0=ALU.subtract)

        def gates_carry(ntok):
            I("dve", "tensor_copy", [M.G_Br], [M.G_Br], out=M.G_B[:, 0:1], in_=M.G_B[:, ntok:ntok + 1])
            I("dve", "tensor_copy", [M.G_Mr], [M.G_Mr], out=M.G_M[:, 0:1], in_=M.G_M[:, ntok:ntok + 1])

        def state_update(ti, c0, refresh_cb):
            pw, pwr = bank()
            I4 = SEL[:, 0:512].rearrange("p (h t) -> p h t", t=128)[:, :, 0]
            I("dve", "tensor_scalar", [selr, M.G_Mr], [M.DGr], out=M.DG[:, 0:4], in0=I4, scalar1=M.G_M[:, c0 + 128:c0 + 129],
              scalar2=-1.0, op0=ALU.mult, op1=ALU.mult)
            I("dve", "tensor_scalar", [selr, M.G_DMr], [M.DGr], out=M.DG[:, 4:8], in0=I4, scalar1=M.G_DM[:, c0 + 127:c0 + 128],
              scalar2=-1.0, op0=ALU.mult, op1=ALU.mult)
            mm(pw[:, 0:4], M.G_A[:, c0:c0 + 128], I4, True, False, [M.G_Ar, selr], pwr)
            mm(pw[:, 0:4], onesf[0:4, 0:128], M.DG[:, 0:4], False, True, [onesfr, M.DGr], pwr)
            mm(pw[:, 4:8], onesf[0:4, 0:128], M.DG[:, 4:8], True, True, [onesfr, M.DGr], pwr)
            I("act", "activation", [pwr], [M.WKCr], out=M.WKC[:, 0:8], in_=pw[:, 0:8], func=AF.Exp)
            for h in range(4):
                I("dve", "tensor_scalar", [M.MVaugr[ti], M.WKCr], [M.VWr[h]], out=M.VW[:, h, :], in0=M.MVaug[:, ti, h, :],
                  scalar1=M.WKC[:, h:h + 1], scalar2=None, op0=ALU.mult)
            for h0 in (0, 2):
                dc, dcr = bank()
                for hh in range(2):
                    h = h0 + hh
                    mm(dc[0:64, hh * 129:(hh + 1) * 129], M.MKtok[:, ti, h * 64:(h + 1) * 64], M.VW[:, h, :], True, True,
                       [M.MKtokr[ti], M.VWr[h]], dcr)
                for hh in range(2):
                    h = h0 + hh
                    I("dve", "scalar_tensor_tensor", [Cstr[h], M.WKCr, dcr], [Cstr[h]], out=Cst[:, h, :], in0=Cst[:, h, :],
                      scalar=M.WKC[0:64, 4 + h:5 + h], in1=dc[0:64, hh * 129:(hh + 1) * 129], op0=ALU.mult, op1=ALU.add)
            if refresh_cb:
                for h in range(4):
                    I("act", "activation", [Cstr[h]], [M.Cbr[h]], out=M.Cb[:, h, 0:129], in_=Cst[:, h, :], func=AF.Copy)
                    I("act", "activation", [Cstr[h]], [M.Cbr[h]], out=M.Cb[:, h, 129:257],
                      in_=Cst[:, h, 128:129].broadcast_to([64, 128]), func=AF.Copy)

        def mlstm_chunk(ti, c0, mbias, mbiasr, inter=True, inter_fn=None):
            cs = slice(c0, c0 + 128)
            pwt, pwtr = bank()
            for h in range(4):
                o = pwt[:, h * 128:(h + 1) * 128]
                mm(o, M.G_A[:, cs], SELh(h), True, False, [M.G_Ar, selr], pwtr)
                mm(o, NSELh(h), M.G_M[:, c0 + 1:c0 + 129], False, False, [M.G_Mr, selr], pwtr)
                mm(o, identb[:], mbias, False, True, [identr, mbiasr], pwtr)
            I("act", "activation", [pwtr], [M.WTr], out=M.WT[:, :, :], in_=pwt[:, :].rearrange("p (h t) -> p h t", h=4), func=AF.Exp)
            pqk, pqkr = bank()
            for h in range(4):
                mm(pqk[:, h * 128:(h + 1) * 128], M.MKT[:, h, cs], M.MQT[:, h, cs], True, True, [M.MKTr, M.MQTr], pqkr)
            I("dve", "tensor_tensor", [pqkr, M.WTr], [M.STr], out=M.ST[:, :, :], in0=pqk[:, :].rearrange("p (h t) -> p h t", h=4),
              in1=M.WT[:, :, :], op=ALU.mult)
            pwi, pwir = bank()
            for h in range(4):
                mm(pwi[:, h * 128:(h + 1) * 128], NSELh(h), M.G_DM[:, cs], True, True, [M.G_DMr, selr], pwir)
            I("act", "activation", [pwir], [M.WIr], out=M.WI[:, :, :], in_=pwi[:, :].rearrange("p (h t) -> p h t", h=4), func=AF.Exp)
            I("dve", "tensor_tensor", [M.MQTr, M.WIr], [M.QWr], out=M.QW[:, :, :], in0=M.MQT[:, :, cs], in1=M.WI[0:64, :, :], op=ALU.mult)
            plb, plbr = bank()
            for h in range(4):
                mm(plb[:, h * 128:(h + 1) * 128], NSELh(h), M.G_BM[:, cs], True, True, [M.G_BMr, selr], plbr)
            I("act", "activation", [plbr], [M.LOWBr], out=M.LOWB[:, :, :], in_=plb[:, :].rearrange("p (h t) -> p h t", h=4), func=AF.Exp)
            pnum, pnumr = banks[5], bres[5]
            pden, pdenr = banks[6], bres[6]
            if inter_fn is not None:
                inter_fn("pre")
            for h in range(4):
                o = pnum[:, h * 128:(h + 1) * 128]
                mm(o, M.MVaug[:, ti, h, 0:128], M.ST[:, h, :], True, False, [M.MVaugr[ti], M.STr], pnumr)
                if inter_fn is not None:
                    inter_fn("num", h, pnum, pnumr)
                else:
                    mm(o, M.Cb[:, h, 0:128], M.QW[:, h, :], False, True, [M.Cbr[h], M.QWr], pnumr)
            for h in range(4):
                o = pden[:, h * 128:(h + 1) * 128]
                mm(o, onesb[:], M.ST[:, h, :], True, False, [onesbr, M.STr], pdenr)
                if inter_fn is not None:
                    inter_fn("den", h, pden, pdenr)
                else:
                    mm(o, M.Cb[:, h, 129:257], M.QW[:, h, :], False, True, [M.Cbr[h], M.QWr], pdenr)
            return pnum, pnumr, pden, pdenr

        def mlstm_finish(pnum, pnumr, pden, pdenr, c0, hres):
            cs = slice(c0, c0 + 128)
            v4 = lambda b: b[:, :].rearrange("p (h t) -> p h t", h=4)
            I("act", "activation", [pdenr], [M.T1r], out=M.T1[:, :, :], in_=v4(pden), func=AF.Abs)
            I("dve", "tensor_tensor", [M.T1r, M.LOWBr], [M.T1r], out=M.T1[:, :, :], in0=M.T1[:, :, :], in1=M.LOWB[:, :, :], op=ALU.max)
            I("act", "activation", [M.T1r], [M.T1r], out=M.T1[:, :, :], in_=M.T1[:, :, :], func=AF.Square, scale=float(np.sqrt(EPS)))
            I("act", "activation", [pnumr], [M.USQr], out=M.USQ[:, :, :], in_=v4(pnum), func=AF.Square)
            pss, pssr = bank()
            mm(pss[:, :], onesb[:], M.USQ[:, :, :], True, True, [onesbr, M.USQr], pssr)
            I("dve", "scalar_tensor_tensor", [pssr, M.T1r], [M.T2r], out=M.T2[:, :, :], in0=v4(pss), scalar=1.0 / 128, in1=M.T1[:, :, :],
              op0=ALU.mult, op1=ALU.add)
            I("act", "activation", [M.T2r], [M.T2r], out=M.T2[:, :, :], in_=M.T2[:, :, :], func=AF.Ln)
            I("act", "activation", [M.T2r], [M.T2r], out=M.T2[:, :, :], in_=M.T2[:, :, :], func=AF.Exp, scale=-0.5)
            I("dve", "tensor_tensor", [pnumr, M.T2r], [M.T1r], out=M.T1[:, :, :], in0=v4(pnum), in1=M.T2[:, :, :], op=ALU.mult)
            for h in range(4):
                I("dve", "scalar_tensor_tensor", [M.T1r, gheadr, M.SGTr], [hres], out=M.HMT[:, h, cs], in0=M.T1[:, h, :],
                  scalar=gheadc[:, h:h + 1], in1=M.SGT[:, h, cs], op0=ALU.mult, op1=ALU.mult)

        def swa_tile(ti, kcol0, vslots, mb, mbr, ktres):
            qs = slice(ti * 128, (ti + 1) * 128)
            for h in range(2):
                bks = [bank(), bank()]
                for g in range(4):
                    bk, bkr = bks[g // 2]
                    o = bk[:, (g % 2) * 256:(g % 2 + 1) * 256]
                    mm(o, M.QT[:, 4 * h + g, qs], M.KT[:, h, kcol0:kcol0 + 256], True, False, [M.QTr] + ktres, bkr)
                    mm(o, identb[:], mb, False, True, [identr, mbr], bkr)
                for j in range(2):
                    I("dve", "reduce_max", [bks[j][1]], [M.smr], out=M.sm_st[:, 2 * j:2 * j + 2],
                      in_=bks[j][0][:, :].rearrange("p (a b) -> p a b", a=2), axis=AX.X)
                I("dve", "tensor_scalar", [M.smr], [M.smr], out=M.sm_st[:, 0:4], in0=M.sm_st[:, 0:4], scalar1=-0.125, scalar2=None,
                  op0=ALU.mult)
                I("dve", "tensor_tensor", [M.smr, sinkbr], [M.smr], out=M.sm_st[:, 0:4], in0=M.sm_st[:, 0:4],
                  in1=sinkb[:, 8 + 4 * h:12 + 4 * h], op=ALU.min)
                for g in range(4):
                    bk, bkr = bks[g // 2]
                    I("act", "activation", [bkr, M.smr], [M.Er, M.smr], out=M.Ebuf[:, g, :], in_=bk[:, (g % 2) * 256:(g % 2 + 1) * 256],
                      func=AF.Exp, bias=M.sm_st[:, g:g + 1], scale=0.125, accum_out=M.sm_st[:, 4 + g:5 + g])
                I("dve", "tensor_tensor", [M.smr, sinkbr], [M.smr], out=M.sm_st[:, 8:12], in0=M.sm_st[:, 0:4],
                  in1=sinkb[:, 4 * h:4 * h + 4], op=ALU.add)
                I("act", "activation", [M.smr], [M.smr], out=M.sm_st[:, 8:12], in_=M.sm_st[:, 8:12], func=AF.Exp)
                I("dve", "tensor_tensor", [M.smr], [M.smr], out=M.sm_st[:, 8:12], in0=M.sm_st[:, 8:12], in1=M.sm_st[:, 4:8], op=ALU.add)
                I("dve", "reciprocal", [M.smr], [M.smr], out=M.sm_st[:, 12:16], in_=M.sm_st[:, 8:12])
                for g in range(4):
                    if g % 2 == 0:
                        I("act", "activation", [M.Er, M.smr], [M.Er], out=M.Ebuf[:, g, :], in_=M.Ebuf[:, g, :], func=AF.Copy,
                          scale=M.sm_st[:, 12 + g:13 + g])
                    else:
                        I("dve", "tensor_scalar", [M.Er, M.smr], [M.Er], out=M.Ebuf[:, g, :], in0=M.Ebuf[:, g, :],
                          scalar1=M.sm_st[:, 12 + g:13 + g], scalar2=None, op0=ALU.mult)
                for kb in range(2):
                    for g in range(4):
                        blk = kb * 4 + (g % 2) * 2 + g // 2
                        I("pe", "transpose", [M.Er, identr], [*tbhr], out=tb[:, blk * 128:(blk + 1) * 128],
                          in_=M.Ebuf[:, g, kb * 128:(kb + 1) * 128], identity=identb[:])
                pb = 0
                if h == 0:
                    I("dve", "tensor_copy", [*tbhr], [M.PTsr[pb]], out=M.PTs[:, pb, :], in_=tb[:, :])
                else:
                    I("act", "activation", [*tbhr], [M.PTsr[pb]], out=M.PTs[:, pb, :], in_=tb[:, :], func=AF.Copy)
                po, por = bank()
                for par in range(2):
                    for kb in range(2):
                        mm(po[par * 64:(par + 1) * 64, 0:256], M.Vt[:, vslots[kb], h * 64:(h + 1) * 64],
                           M.PTs[:, pb, kb * 512 + par * 256:kb * 512 + (par + 1) * 256], kb == 0, kb == 1,
                           [M.Vtr[vslots[kb]], M.PTsr[pb]], por)
                I("act", "activation", [por], [M.ATTTr[ti]], out=M.ATTT[:, 2 * h:2 * h + 2, qs],
                  in_=po[:, 0:256].rearrange("p (g q) -> p g q", g=2), func=AF.Copy)

        def wout_tile(ti, t):
            qs = slice(ti * 128, (ti + 1) * 128)
            for c in range(2):
                bk, bkr = bank()
                cc = slice(c * 512, (c + 1) * 512)
                for hg in range(4):
                    mm(bk[:, :], M.ATTT[:, hg, qs], M.WOA[:, hg, cc], hg == 0, False, [M.ATTTr[ti], M.WOAr], bkr)
                for h in range(4):
                    mm(bk[:, :], M.HMT[:, h, qs], M.WOM[:, h, cc], False, h == 3, [M.HMTr[ti], M.WOMr], bkr)
                I("dve", "tensor_tensor", [Yr[t], bkr], [Yr[t]], out=Y[:, t, cc], in0=Y[:, t, cc], in1=bk[:, :], op=ALU.add)

        xpre_t = xpre_d.rearrange("(t p) d -> t p d", p=128)
        xp_t = xp_d.rearrange("(t p) d -> t p d", p=128)
        for t in range(NTP):
            P.dma("sp", Y[:, t, :], xpre_t[t], Yr[t], writes=[Yr[t]])

        tb4sel = (banks[4][:, :].bitcast(BF16)[:, 0:512], bres[4])

        def prep_prefix(g0, pb):
            use(pb)
            for ti in range(GT):
                t = g0 + ti
                norm_T(Y[:, t, :], Yr[t], 0, M.XNTg[:, :, ti * 128:(ti + 1) * 128], [M.XNTgr[ti]], half=1, tsel=tb4sel)
            for ti in range(GT):
                tok_major(ti, slice(ti * 128, (ti + 1) * 128), M.XNTgr[ti], 0)

        prep_prefix(0, 0)
        for gi, g0 in enumerate(range(0, NTP, GT)):
            pb = gi % 2
            use(pb)
            P.rec_begin(); bset[0] = [0, 1, 2, 3]
            gates(GT * 128, M.XNTgr, True)
            if g0 + GT == NTP:
                bk, bkr = bank()
                for h in range(2):
                    for k in range(8):
                        mm(bk[0:64, h * 128:(h + 1) * 128], M.WK[:, k, h * 64:(h + 1) * 64], M.XNTg[:, k, (GT - 1) * 128:GT * 128],
                           k == 0, k == 7, [M.WKr, M.XNTgr[GT - 1]], bkr)
                I("act", "activation", [bkr], [M.KTr[0]], out=M.KT[:, :, 0:128],
                  in_=bk[0:64, 0:256].rearrange("p (a b) -> p a b", a=2), func=AF.Copy)
            for ti in range(GT):
                last = (g0 + ti == NTP - 1)
                state_update(ti, ti * 128, last)
            gates_carry(GT * 128)
            sA = P.rec_end()
            strs = [sA]
            if g0 + GT < NTP:
                P.rec_begin(); bset[0] = [4]
                prep_prefix(g0 + GT, pb ^ 1)
                strs.append(P.rec_end())
                use(pb)
            P.merge(strs)
            bset[0] = [0, 1, 2, 3, 4]

        for t in range(NTP):
            P.dma("sp", Y[:, t, :], xp_t[t], Yr[t], writes=[Yr[t]])

        def prep_main(g0, pb, merged):
            use(pb)
            if merged:
                for ti in range(GT):
                    t = g0 + ti
                    norm_T(Y[:, t, :], Yr[t], 0, M.XNTg[:, :, ti * 128:(ti + 1) * 128], [M.XNTgr[ti]], half=1, tsel=tb4sel)
            else:
                for ti in range(GT):
                    t = g0 + ti
                    norm_T(Y[:, t, :], Yr[t], 0, M.XNTg[:, :, ti * 128:(ti + 1) * 128], [M.XNTgr[ti]])
            for ti in range(GT):
                t = g0 + ti
                tok_major(ti, slice(ti * 128, (ti + 1) * 128), M.XNTgr[ti], 1 + t, want_kv_out=(True if t == NTP - 1 else None))

        prep_main(0, 0, False)
        for gi, g0 in enumerate(range(0, NTP, GT)):
            pb = gi % 2
            use(pb)
            gates(M.NG, M.XNTgr, False)
            feat64(M.WK, M.WKr, 2, M.KT, [M.KTr[1 + g0 + i] for i in range(GT)], M.NG, M.XNTgr, dcol0=128 + g0 * 128)
            feat64(M.WQ, M.WQr, 8, M.QT, [M.QTr], M.NG, M.XNTgr)
            feat64(M.WMQ, M.WMQr, 4, M.MQT, [M.MQTr], M.NG, M.XNTgr)
            feat64(M.WMK, M.WMKr, 4, M.MKT, [M.MKTr], M.NG, M.XNTgr, scale=0.125)
            for h0 in (0, 2):
                bk, bkr = bank()
                for hh in range(2):
                    h = h0 + hh
                    for k in range(8):
                        mm(bk[:, hh * 256:hh * 256 + M.NG], M.WOG[:, k, h * 128:(h + 1) * 128], M.XNTg[:, k, 0:M.NG], k == 0, k == 7,
                           [M.WOGr] + M.XNTgr, bkr)
                sgv = M.SGT[:, h0:h0 + 2, :]
                I("act", "activation", [bkr], [M.SGTr], out=sgv, in_=bk[:, :].rearrange("p (a b) -> p a b", a=2)[:, :, 0:M.NG],
                  func=AF.Exp, scale=-1.0)
                I("act", "activation", [M.SGTr], [M.SGTr], out=sgv, in_=sgv, func=AF.Ln, bias=1.0)
                I("act", "activation", [M.SGTr], [M.SGTr], out=sgv, in_=sgv, func=AF.Exp, scale=-1.0)
            pend = None
            for ti in range(GT):
                t = g0 + ti
                P.rec_begin(); bset[0] = [2, 3]
                pn = mlstm_chunk(ti, ti * 128, mbcaus[:], mbcausr)
                mlstm_finish(*pn, ti * 128, M.HMTr[ti])
                state_update(ti, ti * 128, True)
                s_ml = P.rec_end()
                P.rec_begin(); bset[0] = [0, 1]
                swa_tile(ti, t * 128, (t, t + 1), (mbfirst[:] if t == 0 else mbband[:]), (mbfirstr if t == 0 else mbbandr),
                         [M.KTr[t], M.KTr[t + 1]])
                s_sw = P.rec_end()
                strs = [s_ml, s_sw]
                P.rec_begin(); bset[0] = [4]
                if pend is not None:
                    wout_tile(*pend)
                if ti == 0 and g0 + GT < NTP:
                    prep_main(g0 + GT, pb ^ 1, True)
                    use(pb)
                strs.append(P.rec_end())
                P.merge(strs)
                pend = (ti, t)
            bset[0] = [0, 1, 2, 3, 4]
            wout_tile(*pend)
            if debug and g0 == DBG_G0:
                dA = dout("dbg_att", [128, 4, M.NG]); dH = dout("dbg_hm", [128, 4, M.NG])
                P.dma("pool", dA, M.ATTT[:, :, :], M.ATTTr[0], reads=M.ATTTr)
                P.dma("pool", dH, M.HMT[:, :, :], M.HMTr[0], reads=M.HMTr)
            gates_carry(M.NG)

        CO = A([4, 64], F32); COr = AR("CO")
        for h in range(4):
            bk, bkr = bank()
            mm(bk[:, 0:64], Cst[:, h, 0:128], identf[0:64, 0:64], True, True, [Cstr[h], identfr], bkr)
            I("act", "activation", [bkr], [COr], out=CO[:, h, :], in_=bk[:, 0:64], func=AF.Copy)
        P.dma("sp", Cp_o.rearrange("h p k -> p h k"), CO[:, :, :], COr, reads=[COr])
        for h in range(4):
            P.dma("sp", np_o[h, :].rearrange("(k o) -> k o", o=1), Cst[:, h, 128:129], Cstr[h], reads=[Cstr[h]], allow_slow_non_contiguous=True)
        P.dma("sp", mp_o, M.G_BM[:, M.NG - 1:M.NG], M.G_BMr, reads=[M.G_BMr], allow_slow_non_contiguous=True)

        new_phase()
        MA = M
        M = alloc_mixer(1, 1, 1)
        R0_olds = [M.WQr, M.WTOKr, M.WMQr, M.WOGr, M.WGTr]
        TS = NTP
        P.dma("sp", Y[:, TS, :], xs_d, Yr[TS], writes=[Yr[TS]])
        shk = P.res("shk"); shv = P.res("shv")
        P.dma("sp", sks_o[:, 0:120, :], csk_d[:, 8:128, :], shk, writes=[shk])
        P.dma("sp", svs_o[:, 0:120, :], csv_d[:, 8:128, :], shv, writes=[shv])
        CKn = A([16, 128], BF16); CKnr = AR("CKn")
        CV = A([16, 128], BF16); CVr = AR("CV")
        CKT = A([16, 128], BF16, parts=64); CKTr = AR("CKT")
        SMC = A([1, 128], BF16, parts=32)[:, 0, :]; SMCr = AR("SMC")
        SMN = A([16, 128], BF16, parts=32); SMNr = AR("SMN")
        SINKC = A([1, 4], F32, parts=32)[:, 0, :]; SINKCr = AR("SINKC")
        mbcs = A([1, 128], BF16)[:, 0, :]; mbcsr = AR("mbcs")
        PNs = A([4, 256], BF16, parts=32); PNsr = [AR(f"PNs{i}") for i in range(4)]
        sms = A([4, 8], F32, parts=32); smsr = [AR(f"sms{i}") for i in range(4)]
        PTS = A([1, 1024], BF16)[:, 0, :]; PTSr = AR("PTS")
        M0 = A([1, 16], F32, parts=4)[:, 0, :]; M0r = AR("M0")
        MTe = A([1, 128], F32, parts=4)[:, 0, :]; MTer = AR("MTe")
        DMT = A([1, 16], F32, parts=4)[:, 0, :]; DMTr = AR("DMT")
        E16 = A([1, 16], F32)[:, 0, :]; E16r = AR("E16")
        EW = A([4, 16], BF16); EWr = AR("EW")
        WCB = A([4, 16], F32); WCBr = AR("WCB")
        SNn = A([1, 64], F32, parts=64)[:, 0, :]; SNnr = AR("SNn")
        SNT = A([1, 64], F32, parts=64)[:, 0, :]; SNTr = AR("SNT")
        NNT = A([1, 64], F32, parts=64)[:, 0, :]; NNTr = AR("NNT")
        NNo = A([1, 64], F32, parts=64)[:, 0, :]; NNor = AR("NNo")
        BTf = A([1, 128], F32)[:, 0, :]; BTfr = AR("BTf")
        P.dma("pool", CKn[:, :, :], csk_d.rearrange("j p c -> p j c"), CKnr, writes=[CKnr])
        P.dma("pool", CV[:, :, :], csv_d.rearrange("j p c -> p j c"), CVr, writes=[CVr])
        P.dma("pool", SMC, smc_d, SMCr, writes=[SMCr])
        P.dma("pool", SMN[:, :, :], smn_d, SMNr, writes=[SMNr])
        P.dma("sp", SINKC[:, 0:2], sinkcol_d, SINKCr, writes=[SINKCr])
        I("dve", "tensor_scalar", [SINKCr], [SINKCr], out=SINKC[:, 2:4], in0=SINKC[:, 0:2], scalar1=-1.0, scalar2=None, op0=ALU.mult)
        P.dma("pool", mbcs, mb_causs_d, mbcsr, writes=[mbcsr])
        P.dma("sp", M0, sm_d.rearrange("j h -> h j"), M0r, writes=[M0r], allow_slow_non_contiguous=True)
        P.dma("sp", E16, eseq_d, E16r, writes=[E16r])
        P.dma("sp", SNn, sn_d.rearrange("j h k -> (j h) k"), SNnr, writes=[SNnr])

        norm_T(Y[:, TS, :], Yr[TS], 0, M.XNTg[:, :, 0:128], [M.XNTgr[0]])
        tok_major(0, slice(0, 128), M.XNTgr[0], 0, want_kv_out="sample")
        gates(128, M.XNTgr, "sample")
        feat64(M.WK, M.WKr, 2, M.KT, [M.KTr[0]], 128, M.XNTgr, dcol0=0)
        feat64(M.WQ, M.WQr, 8, M.QT, [M.QTr], 128, M.XNTgr)
        feat64(M.WMQ, M.WMQr, 4, M.MQT, [M.MQTr], 128, M.XNTgr)
        feat64(M.WMK, M.WMKr, 4, M.MKT, [M.MKTr], 128, M.XNTgr, scale=0.125)
        for h0 in (0, 2):
            bk, bkr = bank()
            for hh in range(2):
                h = h0 + hh
                for k in range(8):
                    mm(bk[:, hh * 256:hh * 256 + 128], M.WOG[:, k, h * 128:(h + 1) * 128], M.XNTg[:, k, 0:128], k == 0, k == 7,
                       [M.WOGr] + M.XNTgr, bkr)
            sgv = M.SGT[:, h0:h0 + 2, :]
            I("act", "activation", [bkr], [M.SGTr], out=sgv, in_=bk[:, :].rearrange("p (a b) -> p a b", a=2)[:, :, 0:128],
              func=AF.Exp, scale=-1.0)
            I("act", "activation", [M.SGTr], [M.SGTr], out=sgv, in_=sgv, func=AF.Ln, bias=1.0)
            I("act", "activation", [M.SGTr], [M.SGTr], out=sgv, in_=sgv, func=AF.Exp, scale=-1.0)
        for j in range(16):
            I("dve", "tensor_tensor_scan", [M.Gr, M.G_Br, onesfr], [M.G_Br], out=M.G_B[:, 1 + 8 * j:9 + 8 * j],
              data0=onesf[0:4, 0:8], data1=M.G_L1[:, 8 * j:8 * j + 8], initial=0.0, op0=ALU.mult, op1=ALU.subtract)
        I("dve", "tensor_tensor", [M.Gr, M.G_Br], [M.G_Ar], out=M.G_A[:, 0:128], in0=M.G_IG[:, 1:129], in1=M.G_B[:, 1:129],
          op=ALU.subtract)
        for j in range(16):
            I("dve", "tensor_tensor_scan", [M.G_Ar, M.G_Mr, onesfr, M0r], [M.G_Mr], out=M.G_M[:, 1 + 8 * j:9 + 8 * j],
              data0=onesf[0:4, 0:8], data1=M.G_A[:, 8 * j:8 * j + 8], initial=M0[:, j:j + 1], op0=ALU.mult, op1=ALU.max)
        I("dve", "tensor_tensor", [M.G_Br, M.G_Mr], [M.G_BMr], out=M.G_BM[:, 0:128], in0=M.G_B[:, 1:129], in1=M.G_M[:, 1:129],
          op=ALU.add)
        GM3 = M.G_M[:, 1:129].rearrange("p (j i) -> p j i", i=8)
        I("dve", "tensor_tensor", [M.G_Mr, M0r], [M.G_DMr], out=M.G_DM[:, 0:128].rearrange("p (j i) -> p j i", i=8), in0=GM3,
          in1=M0[:, :].unsqueeze(2).broadcast_to([4, 16, 8]), op=ALU.subtract)
        I("dve", "tensor_copy", [M.G_Mr], [MTer], out=MTe[:, :].rearrange("p (j i) -> p j i", i=8),
          in_=GM3[:, :, 7:8].broadcast_to([4, 16, 8]))
        I("dve", "tensor_tensor", [M.G_Mr, M0r], [DMTr], out=DMT[:, :].unsqueeze(2), in0=GM3[:, :, 7:8], in1=M0[:, :].unsqueeze(2),
          op=ALU.subtract)
        P.dma("sp", ms_o.rearrange("j h -> h j"), M.G_BM[:, 0:128].rearrange("p (j i) -> p j i", i=8)[:, :, 7], M.G_BMr,
              reads=[M.G_BMr], allow_slow_non_contiguous=True)

        pair_i = [0]
        QS = A([2, 16, 32], BF16, parts=64); QSr = AR("QS")
        for h in range(2):
            for par in range(2):
                I("act", "activation", [M.QTr], [QSr],
                  out=QS[:, h, :, par * 16:(par + 1) * 16].rearrange("p j (gp i) -> p j gp i", i=8),
                  in_=M.QT[:, 4 * h:4 * h + 4, :].rearrange("p (gp two) t -> p two gp t", two=2)[:, par].rearrange(
                      "p gp (j i) -> p j gp i", i=8), func=AF.Copy)
        for h in range(2):
            for q4 in range(2):
                for jj in range(8):
                    j = q4 * 8 + jj
                    I("pe", "transpose", [CKnr, identr], [*tbhr], out=tb[0:64, jj * 128:(jj + 1) * 128],
                      in_=CKn[:, j, h * 64:(h + 1) * 64], identity=identb[:])
                I("act", "activation", [*tbhr], [CKTr], out=CKT[:, q4 * 8:(q4 + 1) * 8, :],
                  in_=tb[0:64, :].rearrange("p (a b) -> p a b", a=8), func=AF.Copy)
            for half in range(2):
                po, por = banks[4], bres[4]
                strs = []
                for sk in range(4):
                    P.rec_begin(); bset[0] = [sk]
                    for jj in (sk, sk + 4):
                        j = half * 8 + jj
                        b = sk
                        bk, bkr = bank()
                        lq = QS[:, h, j, :]
                        mm(bk[0:32, 0:128], lq, CKT[:, j, :], True, False, [QSr, CKTr], bkr)
                        mm(bk[0:32, 0:128], identb[0:32, 0:32], SMC, False, True, [identr, SMCr], bkr)
                        mm(bk[0:32, 128:256], lq, M.KT[:, h, 0:128], True, False, [QSr, M.KTr[0]], bkr)
                        mm(bk[0:32, 128:256], identb[0:32, 0:32], SMN[:, j, :], False, True, [identr, SMNr], bkr)
                        st_ = sms[:, b, :]
                        I("dve", "reduce_max", [bkr], [smsr[b]], out=st_[:, 0:1], in_=bk[0:32, 0:256], axis=AX.X)
                        I("dve", "tensor_scalar", [smsr[b]], [smsr[b]], out=st_[:, 0:1], in0=st_[:, 0:1], scalar1=-0.125,
                          scalar2=None, op0=ALU.mult)
                        I("dve", "tensor_tensor", [smsr[b], SINKCr], [smsr[b]], out=st_[:, 0:1], in0=st_[:, 0:1],
                          in1=SINKC[:, 2 + h:3 + h], op=ALU.min)
                        I("act", "activation", [bkr, smsr[b]], [PNsr[b], smsr[b]], out=PNs[:, b, :], in_=bk[0:32, 0:256],
                          func=AF.Exp, bias=st_[:, 0:1], scale=0.125, accum_out=st_[:, 1:2])
                        I("act", "activation", [SINKCr, smsr[b]], [smsr[b]], out=st_[:, 2:3], in_=SINKC[:, h:h + 1], func=AF.Exp,
                          bias=st_[:, 0:1])
                        I("dve", "tensor_tensor", [smsr[b]], [smsr[b]], out=st_[:, 2:3], in0=st_[:, 2:3], in1=st_[:, 1:2],
                          op=ALU.add)
                        I("dve", "reciprocal", [smsr[b]], [smsr[b]], out=st_[:, 3:4], in_=st_[:, 2:3])
                        I("dve", "tensor_scalar", [PNsr[b], smsr[b]], [PNsr[b]], out=PNs[:, b, :], in0=PNs[:, b, :],
                          scalar1=st_[:, 3:4], scalar2=None, op0=ALU.mult)
                        for c2 in range(2):
                            I("pe", "transpose", [PNsr[b], identr], [*tbhr], out=tb[:, jj * 64 + c2 * 32:jj * 64 + (c2 + 1) * 32],
                              in_=PNs[:, b, c2 * 128:(c2 + 1) * 128], identity=identb[0:32, 0:32])
                    strs.append(P.rec_end())
                P.merge(strs)
                bset[0] = [0, 1, 2, 3]
                I("dve", "tensor_copy", [*tbhr], [PTSr], out=PTS[:, 0:512], in_=tb[:, 0:512])
                for jj in range(8):
                    j = half * 8 + jj
                    for par in range(2):
                        o = po[par * 64:(par + 1) * 64, jj * 16:(jj + 1) * 16]
                        mm(o, CV[:, j, h * 64:(h + 1) * 64], PTS[:, jj * 64 + par * 16:jj * 64 + par * 16 + 16], True, False,
                           [CVr, PTSr], por)
                        mm(o, M.Vt[:, 0, h * 64:(h + 1) * 64], PTS[:, jj * 64 + 32 + par * 16:jj * 64 + 32 + par * 16 + 16], False,
                           True, [M.Vtr[0], PTSr], por)
                I("act", "activation", [por], [M.ATTTr[0]],
                  out=M.ATTT[:, 2 * h:2 * h + 2, half * 64:(half + 1) * 64].rearrange("p c (j i) -> p j c i", i=8),
                  in_=po[:, 0:128].rearrange("p (j c i) -> p j c i", c=2, i=8), func=AF.Copy)

        bset[0] = [0, 1, 2, 3, 4]
        off = 0
        SCf, off = A_at(off, [64, 64], F32); SCfr = ARalias("SCf", R0_olds)
        SCT, off = A_at(off, [64, 128], BF16, parts=64); SCTr = ARalias("SCT", R0_olds)
        QN, off = A_at(off, [4, 128], BF16, parts=64); QNr = ARalias("QN", R0_olds)
        KJ, off = A_at(off, [16, 64], BF16); KJr = ARalias("KJ", R0_olds)
        assert off <= 18496
        P.dma("sp", SCf[:, :, :], sC_d.rearrange("j h p k -> p (j h) k"), SCfr, writes=[SCfr])
        for p4 in range(16):
            bk, bkr = bank()
            for q_ in range(4):
                pr = p4 * 4 + q_
                mm(bk[0:64, q_ * 128:(q_ + 1) * 128], SCf[:, pr, :], identf[:], True, True, [SCfr, identfr], bkr)
            I("act", "activation", [bkr], [SCTr], out=SCT[:, p4 * 4:(p4 + 1) * 4, :],
              in_=bk[0:64, :].rearrange("p (a b) -> p a b", a=4), func=AF.Copy)
        bk, bkr = bank()
        mm(bk[0:64, 0:64], SNn, identf[0:64, 0:64], True, True, [SNnr, identfr], bkr)
        I("act", "activation", [bkr], [SNTr], out=SNT, in_=bk[0:64, 0:64], func=AF.Copy)

        def sample_inter(kind, h=None, pb=None, pbr=None):
            if kind == "pre":
                I("dve", "tensor_tensor", [M.QWr, SNTr], [QNr], out=QN[:, :, :].rearrange("p h (j i) -> p h j i", i=8),
                  in0=M.QW[:, :, :].rearrange("p h (j i) -> p h j i", i=8),
                  in1=SNT.rearrange("p (j h) -> p h j", h=4).unsqueeze(3).broadcast_to([64, 4, 16, 8]), op=ALU.mult)
            elif kind == "num":
                for j in range(16):
                    mm(pb[:, h * 128 + 8 * j:h * 128 + 8 * j + 8], SCT[:, j * 4 + h, :], M.QW[:, h, 8 * j:8 * j + 8], False, j == 15,
                       [SCTr, M.QWr], pbr)
            else:
                mm(pb[:, h * 128:(h + 1) * 128], onesb[0:64, :], QN[:, h, :], False, True, [onesbr, QNr], pbr)

        pn = mlstm_chunk(0, 0, mbcs, mbcsr, inter_fn=sample_inter)
        mlstm_finish(*pn, 0, M.HMTr[0])
        wout_tile(0, TS)

        pw, pwr = bank()
        for h in range(4):
            mm(pw[:, h:h + 1], M.G_A[:, 0:128], SELh(h, 1), True, False, [M.G_Ar, selr], pwr)
            mm(pw[:, h:h + 1], MTe, SEL[:, 512 + h * 128:512 + h * 128 + 1], False, True, [MTer, selr], pwr)
        for h in range(4):
            mm(pw[:, 8 + 16 * h:8 + 16 * (h + 1)], NSELh(h), DMT, True, True, [DMTr, selr], pwr)
        I("act", "activation", [pwr], [M.WKCr], out=M.WKC[:, 0:4], in_=pw[:, 0:4], func=AF.Exp)
        I("act", "activation", [pwr], [WCBr], out=WCB[:, :, :], in_=pw[:, 8:72].rearrange("p (h j) -> p h j", h=4), func=AF.Exp)
        for h in range(4):
            I("dve", "tensor_scalar", [M.MVaugr[0], M.WKCr], [M.VWr[h]], out=M.VW[:, h, :], in0=M.MVaug[:, 0, h, :],
              scalar1=M.WKC[:, h:h + 1], scalar2=None, op0=ALU.mult)
            I("dve", "tensor_scalar", [E16r, M.WKCr], [EWr], out=EW[:, h, :], in0=E16, scalar1=M.WKC[:, h:h + 1], scalar2=None,
              op0=ALU.mult)
        bk, bkr = bank()
        for h in range(4):
            mm(bk[0:64, h * 16:(h + 1) * 16], M.MKtok[:, 0, h * 64:(h + 1) * 64], EW[:, h, :], True, True, [M.MKtokr[0], EWr], bkr)
        I("dve", "tensor_tensor", [SNTr, WCBr], [NNTr], out=NNT.rearrange("p (j h) -> p h j", h=4),
          in0=SNT.rearrange("p (j h) -> p h j", h=4), in1=WCB[0:64, :, :], op=ALU.mult)
        I("dve", "tensor_tensor", [NNTr, bkr], [NNTr], out=NNT.rearrange("p (j h) -> p h j", h=4),
          in0=NNT.rearrange("p (j h) -> p h j", h=4), in1=bk[0:64, 0:64].rearrange("p (h j) -> p h j", h=4), op=ALU.add)
        bk2, bk2r = bank()
        mm(bk2[0:64, 0:64], NNT, identf[0:64, 0:64], True, True, [NNTr, identfr], bk2r)
        I("act", "activation", [bk2r], [NNor], out=NNo, in_=bk2[0:64, 0:64], func=AF.Copy)
        P.dma("sp", ns_o.rearrange("j h k -> (j h) k"), NNo, NNor, reads=[NNor])
        for h in range(4):
            I("dve", "tensor_tensor", [M.MKtokr[0], E16r], [KJr], out=KJ[:, :, :],
              in0=M.MKtok[:, 0, h * 64:(h + 1) * 64].unsqueeze(1).broadcast_to([128, 16, 64]),
              in1=E16.unsqueeze(2).broadcast_to([128, 16, 64]), op=ALU.mult)
            for half in range(2):
                bk, bkr = bank()
                mm(bk[:, :], M.VW[:, h, 0:128], KJ[:, half * 8:(half + 1) * 8, :], True, True, [M.VWr[h], KJr], bkr)
                scv = SCf[:, :, :].rearrange("p (j h) k -> p h j k", h=4)[:, h, half * 8:(half + 1) * 8, :]
                I("dve", "tensor_tensor", [SCfr, WCBr], [SCfr], out=scv, in0=scv,
                  in1=WCB[:, h, half * 8:(half + 1) * 8].unsqueeze(2).broadcast_to([128, 8, 64]), op=ALU.mult)
                I("dve", "tensor_tensor", [SCfr, bkr], [SCfr], out=scv, in0=scv,
                  in1=bk[:, :].rearrange("p (j k) -> p j k", k=64), op=ALU.add)
        P.dma("sp", Cs_o.rearrange("j h p k -> p (j h) k"), SCf[:, :, :], SCfr, reads=[SCfr])

        def dump_y(tiles):
            yo = yp_o.rearrange("(t p) d -> t p d", p=128)
            for t in tiles:
                if Yr[t].last_w is None:
                    continue
                if t < NTP:
                    P.dma("sp", yo[t], Y[:, t, :], Yr[t], reads=[Yr[t]])
                else:
                    P.dma("sp", ys_o, Y[:, t, :], Yr[t], reads=[Yr[t]])

        if stage <= 1:
            dump_y(range(NT))
            P.emit()
            return nc, P

        new_phase()
        WCQ = A([8, 256], BF16); WCQr = AR("WCQ")
        WCKV = A([8, 512], BF16); WCKVr = AR("WCKV")
        WCO = A([2, 1024], BF16); WCOr = AR("WCO")
        P.dma("pool", WCQ[:], w_cq_d.rearrange("(k p) n -> p k n", p=128), WCQr, writes=[WCQr])
        P.dma("pool", WCKV[:, :, 0:256], w_ck_d.rearrange("(k p) n -> p k n", p=128), WCKVr, writes=[WCKVr], group=True)
        P.dma("pool", WCKV[:, :, 256:512], w_cv_d.rearrange("(k p) n -> p k n", p=128), WCKVr, writes=[WCKVr], group=True)
        P.dma("pool", WCO[:], w_co_d.rearrange("(c p) n -> p c n", p=128), WCOr, writes=[WCOr])
        MEMX = A([2, D], F32); MEMXr = [AR("MEMX0"), AR("MEMX1")]
        MNT = A([8, 256], BF16); MNTr = [AR("MNT0"), AR("MNT1")]
        MKTm = A([4, 256], BF16, parts=64); MKTmr = AR("MKTm")
        MVm = A([2, 256], BF16); MVmr = AR("MVm")
        MKVo = A([2, 512], F32); MKVor = [AR("MKVo0"), AR("MKVo1")]
        GB = 4
        XNTb = A([8, GB * 128], BF16); XNTbr = [AR(f"XNTb{i}") for i in range(GB)]
        QcT = A([4, GB * 128], BF16, parts=64); QcTr = AR("QcT")
        OcT = A([2, GB * 128], BF16); OcTr = [AR(f"OcT{i}") for i in range(GB)]
        Eb2s = [A([4, 256], BF16) for _ in range(2)]; Eb2rs = [AR("Eb2a"), AR("Eb2b")]
        PT2s = [A([1, 1024], BF16) for _ in range(2)]; PT2rs = [AR("PT2a"), AR("PT2b")]
        sm2s = [A([1, 32], F32)[:, 0, :] for _ in range(2)]; sm2rs = [AR("sm2a"), AR("sm2b")]

        mem_t = mem_d.rearrange("(t p) d -> t p d", p=128)
        import os
        SK = os.environ.get("SKIP", "")
        for mt in range(2):
            P.dma("sp", MEMX[:, mt, :], mem_t[mt], MEMXr[mt], writes=[MEMXr[mt]])
        for mt in (range(2) if "noBnorm" not in SK else []):
            norm_T(MEMX[:, mt, :], MEMXr[mt], 2, MNT[:, :, mt * 128:(mt + 1) * 128], [MNTr[mt]])
        for mt in (range(2) if "noBkv" not in SK else []):
            bk, bkr = bank()
            for k in range(8):
                mm(bk[:, :], MNT[:, k, mt * 128:(mt + 1) * 128], WCKV[:, k, :], k == 0, k == 7, [MNTr[mt], WCKVr], bkr)
            if "noBcp1" not in SK:
                I("act", "activation", [bkr], [MKVor[mt]], out=MKVo[:, mt, :], in_=bk[:, :], func=AF.Copy)
            if "noBcp2" not in SK:
                I("act", "activation", [bkr], [MVmr], out=MVm[:, mt, :], in_=bk[:, 256:512], func=AF.Copy)
            if "noBdma" not in SK:
                P.dma("sp", memk_o[mt * 128:(mt + 1) * 128, :], MKVo[:, mt, 0:256], MKVor[mt], reads=[MKVor[mt]], group=True)
                P.dma("sp", memv_o[mt * 128:(mt + 1) * 128, :], MKVo[:, mt, 256:512], MKVor[mt], reads=[MKVor[mt]], group=True)
        for h0 in ((0, 2) if "noBkt" not in SK else []):
            bk, bkr = bank()
            for hh in range(2):
                h = h0 + hh
                for k in range(8):
                    mm(bk[0:64, hh * 256:(hh + 1) * 256], WCKV[:, k, h * 64:(h + 1) * 64], MNT[:, k, :], k == 0, k == 7,
                       [WCKVr] + MNTr, bkr)
            I("act", "activation", [bkr], [MKTmr], out=MKTm[:, h0:h0 + 2, :],
              in_=bk[0:64, :].rearrange("p (a b) -> p a b", a=2), func=AF.Copy)

        def cross_q(ntok, xres):
            for h in range(4):
                bk, bkr = bank()
                for k in range(8):
                    mm(bk[0:64, 0:ntok], WCQ[:, k, h * 64:(h + 1) * 64], XNTb[:, k, 0:ntok], k == 0, k == 7, [WCQr] + xres, bkr)
                I("act", "activation", [bkr], [QcTr], out=QcT[:, h, 0:ntok], in_=bk[0:64, 0:ntok], func=AF.Copy)

        def cross_tile_prompt(ti, sx):
            Eb2, Eb2r, PT2, PT2r, sm2, sm2r = Eb2s[sx], Eb2rs[sx], PT2s[sx], PT2rs[sx], sm2s[sx], sm2rs[sx]
            tbx, tbxr = TSEL[sx]
            qs = slice(ti * 128, (ti + 1) * 128)
            bks = [bank(), bank()]
            for h in range(4):
                bk, bkr = bks[h // 2]
                mm(bk[:, (h % 2) * 256:(h % 2 + 1) * 256], QcT[:, h, qs], MKTm[:, h, :], True, True, [QcTr, MKTmr], bkr)
            for j in range(2):
                I("dve", "reduce_max", [bks[j][1]], [sm2r], out=sm2[:, 2 * j:2 * j + 2],
                  in_=bks[j][0][:, :].rearrange("p (a b) -> p a b", a=2), axis=AX.X)
            I("dve", "tensor_scalar", [sm2r], [sm2r], out=sm2[:, 0:4], in0=sm2[:, 0:4], scalar1=-0.125, scalar2=None, op0=ALU.mult)
            for h in range(4):
                bk, bkr = bks[h // 2]
                I("act", "activation", [bkr, sm2r], [Eb2r, sm2r], out=Eb2[:, h, :], in_=bk[:, (h % 2) * 256:(h % 2 + 1) * 256],
                  func=AF.Exp, bias=sm2[:, h:h + 1], scale=0.125, accum_out=sm2[:, 4 + h:5 + h])
            I("dve", "reciprocal", [sm2r], [sm2r], out=sm2[:, 8:12], in_=sm2[:, 4:8])
            for h in range(4):
                if h % 2 == 0:
                    I("act", "activation", [Eb2r, sm2r], [Eb2r], out=Eb2[:, h, :], in_=Eb2[:, h, :], func=AF.Copy,
                      scale=sm2[:, 8 + h:9 + h])
                else:
                    I("dve", "tensor_scalar", [Eb2r, sm2r], [Eb2r], out=Eb2[:, h, :], in0=Eb2[:, h, :],
                      scalar1=sm2[:, 8 + h:9 + h], scalar2=None, op0=ALU.mult)
            po, por = bank()
            for mc in range(2):
                for h in range(4):
                    I("pe", "transpose", [Eb2r, identr], [tbxr], out=tbx[:, h * 128:(h + 1) * 128],
                      in_=Eb2[:, h, mc * 128:(mc + 1) * 128], identity=identb[:])
                if mc == 0:
                    I("dve", "tensor_copy", [tbxr], [PT2r], out=PT2[:, 0, 0:512], in_=tbx)
                else:
                    I("act", "activation", [tbxr], [PT2r], out=PT2[:, 0, 512:1024], in_=tbx, func=AF.Copy)
            for h in range(4):
                for mc in range(2):
                    mm(po[(h % 2) * 64:(h % 2 + 1) * 64, (h // 2) * 128:(h // 2 + 1) * 128], MVm[:, mc, h * 64:(h + 1) * 64],
                       PT2[:, 0, (mc * 4 + h) * 128:(mc * 4 + h + 1) * 128], mc == 0, mc == 1, [MVmr, PT2r], por)
            I("act", "activation", [por], [OcTr[ti]], out=OcT[:, :, qs], in_=po[:, 0:256].rearrange("p (h q) -> p h q", h=2),
              func=AF.Copy)

        def wco_tile(ti, t):
            qs = slice(ti * 128, (ti + 1) * 128)
            for c in range(2):
                bk, bkr = bank()
                cc = slice(c * 512, (c + 1) * 512)
                for h in range(2):
                    mm(bk[:, :], OcT[:, h, qs], WCO[:, h, cc], h == 0, h == 1, [OcTr[ti], WCOr], bkr)
                I("dve", "tensor_tensor", [Yr[t], bkr], [Yr[t]], out=Y[:, t, cc], in0=Y[:, t, cc], in1=bk[:, :], op=ALU.add)

        import os
        for g0 in (range(0, NTP, GB) if "noBloop" not in os.environ.get("SKIP", "") else []):
            for tp_ in range(0, GB, 2):
                strs = []
                for sx in range(2):
                    ti = tp_ + sx
                    P.rec_begin()
                    norm_T(Y[:, g0 + ti, :], Yr[g0 + ti], 1, XNTb[:, :, ti * 128:(ti + 1) * 128], [XNTbr[ti]], half=sx,
                           tsel=TSEL[sx])
                    strs.append(P.rec_end())
                P.merge(strs)
            cross_q(GB * 128, XNTbr)
            for tp_ in range(0, GB, 2):
                strs = []
                for sx in range(2):
                    P.rec_begin(); bset[0] = [0, 1, 2] if sx == 0 else [3, 4, 5]
                    cross_tile_prompt(tp_ + sx, sx)
                    wco_tile(tp_ + sx, g0 + tp_ + sx)
                    strs.append(P.rec_end())
                P.merge(strs)
            bset[0] = [0, 1, 2, 3, 4]
        TS = NTP
        CMn = A([8, 2, 256], BF16); CMnr = AR("CMn")
        CMV = A([16, 2, 256], BF16); CMVr = AR("CMV")
        CMKT = A([8, 4, 256], BF16, parts=64); CMKTr = AR("CMKT")
        Es = A([2, 4, 256], BF16, parts=8); Esr = [AR("Es0"), AR("Es1")]
        sm3 = A([2, 16], F32, parts=8); sm3r = [AR("sm30"), AR("sm31")]
        PT3 = A([1, 1024], BF16)[:, 0, :]; PT3r = AR("PT3")
        P.dma("pool", CMV[:, :, :, :], cmv_d.rearrange("j (c p) f -> p j c f", p=128), CMVr, writes=[CMVr])
        norm_T(Y[:, TS, :], Yr[TS], 1, XNTb[:, :, 0:128], [XNTbr[0]])
        cross_q(128, [XNTbr[0]])
        po3, po3r = banks[5], bres[5]
        for half in range(2):
            P.dma("pool", CMn[:, :, :, :], cmk_d[half * 8:(half + 1) * 8].rearrange("j (c p) f -> p j c f", p=128), CMnr,
                  writes=[CMnr])
            for jj in range(8):
                for h in range(4):
                    for mc in range(2):
                        I("pe", "transpose", [CMnr, identr], [*tbhr], out=tb[0:64, (h * 2 + mc) * 128:(h * 2 + mc + 1) * 128],
                          in_=CMn[:, jj, mc, h * 64:(h + 1) * 64], identity=identb[:])
                I("act", "activation", [*tbhr], [CMKTr], out=CMKT[:, jj, :, :],
                  in_=tb[0:64, :].rearrange("p (h m) -> p h m", h=4), func=AF.Copy)
            strs = []
            for sx in range(2):
                P.rec_begin(); bset[0] = [0, 1] if sx == 0 else [2, 3]
                for jj in range(sx, 8, 2):
                    j = half * 8 + jj
                    b = sx
                    bks = [bank(), bank()]
                    for h in range(4):
                        bk, bkr = bks[h // 2]
                        mm(bk[0:8, (h % 2) * 256:(h % 2 + 1) * 256], QcT[:, h, 8 * j:8 * j + 8], CMKT[:, jj, h, :], True, True,
                           [QcTr, CMKTr], bkr)
                    st_ = sm3[:, b, :]
                    for q_ in range(2):
                        I("dve", "reduce_max", [bks[q_][1]], [sm3r[b]], out=st_[:, 2 * q_:2 * q_ + 2],
                          in_=bks[q_][0][0:8, :].rearrange("p (a b) -> p a b", a=2), axis=AX.X)
                    I("dve", "tensor_scalar", [sm3r[b]], [sm3r[b]], out=st_[:, 0:4], in0=st_[:, 0:4], scalar1=-0.125, scalar2=None,
                      op0=ALU.mult)
                    for h in range(4):
                        bk, bkr = bks[h // 2]
                        I("act", "activation", [bkr, sm3r[b]], [Esr[b], sm3r[b]], out=Es[:, b, h, :],
                          in_=bk[0:8, (h % 2) * 256:(h % 2 + 1) * 256], func=AF.Exp, bias=st_[:, h:h + 1], scale=0.125,
                          accum_out=st_[:, 4 + h:5 + h])
                    I("dve", "reciprocal", [sm3r[b]], [sm3r[b]], out=st_[:, 8:12], in_=st_[:, 4:8])
                    I("dve", "tensor_tensor", [Esr[b], sm3r[b]], [Esr[b]], out=Es[:, b, :, :], in0=Es[:, b, :, :],
                      in1=st_[:, 8:12].unsqueeze(2).broadcast_to([8, 4, 256]), op=ALU.mult)
                    for mc in range(2):
                        for h in range(4):
                            c0_ = j * 64 + (mc * 4 + h) * 8
                            I("pe", "transpose", [Esr[b], identr], [*tbhr], out=tb[:, c0_:c0_ + 8],
                              in_=Es[:, b, h, mc * 128:(mc + 1) * 128], identity=identb[0:8, 0:8])

                strs.append(P.rec_end())
            P.merge(strs)
            bset[0] = [0, 1, 2, 3, 4]
            I("dve", "tensor_copy", [*tbhr], [PT3r], out=PT3[:, half * 512:(half + 1) * 512], in_=tb[:, half * 512:(half + 1) * 512])
        for j in range(16):
            for h in range(4):
                for mc in range(2):
                    c0_ = j * 64 + (mc * 4 + h) * 8
                    mm(po3[(h % 2) * 64:(h % 2 + 1) * 64, (j * 2 + h // 2) * 8:(j * 2 + h // 2) * 8 + 8],
                       CMV[:, j, mc, h * 64:(h + 1) * 64], PT3[:, c0_:c0_ + 8], mc == 0, mc == 1, [CMVr, PT3r], po3r)
        I("act", "activation", [po3r], [OcTr[0]], out=OcT[:, :, 0:128].rearrange("p c (j i) -> p j c i", i=8),
          in_=po3[:, 0:256].rearrange("p (j c i) -> p j c i", c=2, i=8), func=AF.Copy)
        wco_tile(0, TS)

        if stage <= 2:
            dump_y(range(NT))
            P.emit()
            return nc, P

        new_phase()
        USE_SQRT[0] = True
        NF = FH // 128
        XNTa = A([8, NT * 128], BF16); XNTar = [AR(f"XNTa{t}") for t in range(NT)]
        NSLOT = 12
        WG = A([NSLOT, 8, 128], BF16); WU = A([NSLOT, 8, 128], BF16); WD = A([NSLOT, D], BF16)
        Wsr = [AR(f"Ws{s_}") for s_ in range(NSLOT)]
        Hh = A([6, 512], BF16); Hr = [AR(f"H{j}") for j in range(6)]
        SG = A([2, 512], BF16); SGr = [AR("SG0"), AR("SG1")]
        OUT = A([1, D], F32)[:, 0, :]; OUTr = AR("OUT")
        gfin = A([1, D], F32)[:, 0, :]; gfinr = AR("gfin")
        P.dma("sp", gfin, g_final_d.partition_broadcast(128), gfinr, writes=[gfinr])
        passes = [list(range(0, 6)), list(range(6, 12)), list(range(12, 17)), list(range(17, 22))]
        groups = [(0, 4), (4, 4), (8, 4), (12, 4), (16, 1)]
        wd_v = w_down_d.rearrange("(f p) n -> f p n", p=128)
        wslot = {}
        nload = [0]

        def load_w(f):
            s_ = nload[0] % NSLOT
            nload[0] += 1
            wslot[f] = s_
            P.dma("pool", WG[:, s_], w_gate_d[:, f * 128:(f + 1) * 128].rearrange("(k p) n -> p k n", p=128), Wsr[s_],
                  writes=[Wsr[s_]], group=True)
            P.dma("pool", WU[:, s_], w_up_d[:, f * 128:(f + 1) * 128].rearrange("(k p) n -> p k n", p=128), Wsr[s_],
                  writes=[Wsr[s_]], group=True)
            P.dma("pool", WD[:, s_], wd_v[f], Wsr[s_], writes=[Wsr[s_]], group=True)

        for f in passes[0]:
            load_w(f)
        gcnt = [0]
        def ffn_norm_group(gi_):
            t0_, n_ = groups[gi_]
            for t in range(t0_, t0_ + n_):
                norm_T(Y[:, t, :], Yr[t], 3, XNTa[:, :, t * 128:(t + 1) * 128], [XNTar[t]], half=t % 2, tsel=TSEL[t % 2])

        ffn_norm_group(0)
        for pi, fl in enumerate(passes):
            for gi, (t0, n) in enumerate(groups):
                if pi + 1 < len(passes) and gi == 0:
                    for f in passes[pi + 1]:
                        load_w(f)
                merging = (pi == 0 and gi + 1 < len(groups))
                if merging:
                    P.rec_begin()
                    ffn_norm_group(gi + 1)
                    s_norm = P.rec_end()
                    P.rec_begin()
                ntok = n * 128
                tok = slice(t0 * 128, t0 * 128 + ntok)
                xr = [XNTar[t] for t in range(t0, t0 + n)]
                for j, f in enumerate(fl):
                    s_ = wslot[f]
                    b = gcnt[0] % 2
                    gcnt[0] += 1
                    pg, pgr = bank()
                    pu, pur = bank()
                    for k in range(8):
                        mm(pg[:, 0:ntok], WG[:, s_, k, :], XNTa[:, k, tok], k == 0, k == 7, [Wsr[s_]] + xr, pgr)
                    for k in range(8):
                        mm(pu[:, 0:ntok], WU[:, s_, k, :], XNTa[:, k, tok], k == 0, k == 7, [Wsr[s_]] + xr, pur)
                    I("act", "activation", [pgr], [SGr[b]], out=SG[:, b, 0:ntok], in_=pg[:, 0:ntok], func=AF.Silu)
                    I("dve", "tensor_tensor", [SGr[b], pur], [Hr[j]], out=Hh[:, j, 0:ntok], in0=SG[:, b, 0:ntok],
                      in1=pu[:, 0:ntok], op=ALU.mult)
                for ti in range(n):
                    t = t0 + ti
                    for c in range(2):
                        pd, pdr = bank()
                        for j, f in enumerate(fl):
                            s_ = wslot[f]
                            mm(pd[:, :], Hh[:, j, ti * 128:(ti + 1) * 128], WD[:, s_, c * 512:(c + 1) * 512], j == 0,
                               j == len(fl) - 1, [Hr[j], Wsr[s_]], pdr)
                        I("dve", "tensor_tensor", [Yr[t], pdr], [Yr[t]], out=Y[:, t, c * 512:(c + 1) * 512],
                          in0=Y[:, t, c * 512:(c + 1) * 512], in1=pd[:, :], op=ALU.add)
                if merging:
                    s_ffn = P.rec_end()
                    P.merge([s_ffn, s_norm])
                if pi == len(passes) - 1:
                    yo = yp_o.rearrange("(t p) d -> t p d", p=128)
                    for t in range(t0, t0 + n):
                        rstd, sr = norm_stats(Y[:, t, :], Yr[t], 0)
                        I("dve", "scalar_tensor_tensor", [Yr[t], sr, gfinr], [OUTr], out=OUT, in0=Y[:, t, :], scalar=rstd,
                          in1=gfin, op0=ALU.mult, op1=ALU.mult)
                        P.dma("sp", (yo[t] if t < NTP else ys_o), OUT, OUTr, reads=[OUTr])
        P.emit()
        return nc, P


def make_consts(hf):
    c = {}
    c["c_ident"] = np.eye(128, dtype=np.float32)
    i = np.arange(128)[:, None]; j = np.arange(256)[None, :]
    band = np.where((j >= i) & (j <= i + 128), 0.0, NEG).astype(np.float32)
    first = band.copy()
    if hf == 0:
        first[:, :128] = NEG
    c["c_mb_band"] = band; c["c_mb_first"] = first
    s = np.arange(128)[:, None]; t = np.arange(128)[None, :]
    c["c_mb_caus"] = np.where(s <= t, 0.0, NEG).astype(np.float32)
    c["c_mb_causs"] = np.where((s <= t) & (s // 8 == t // 8), 0.0, NEG).astype(np.float32)
    sel = np.zeros((4, 1024), np.float32)
    for h in range(4):
        sel[h, h * 128:(h + 1) * 128] = 1.0
        sel[h, 512 + h * 128:512 + (h + 1) * 128] = -1.0
    c["c_sel"] = sel
    pm = np.zeros((4, 2), np.float32)
    pm[:, 0] = 1.0 if hf else 0.0
    pm[:, 1] = 0.0 if hf else NEG
    c["c_pmask"] = pm
    r = np.arange(32)[:, None] % 8
    p = np.arange(128)[None, :]
    c["c_smc"] = np.where(p >= r, 0.0, NEG).astype(np.float32)
    smn = np.full((32, 16, 128), NEG, np.float32)
    for jq in range(16):
        for ii in range(8):
            smn[(np.arange(32) % 8) >= ii, jq, jq * 8 + ii] = 0.0
    c["c_smn"] = smn
    c["c_bt"] = np.where((s <= t) & (s // 8 == t // 8), 1.0, 0.0).astype(np.float32)
    e = np.zeros((128, 16), np.float32); e[np.arange(128), np.arange(128) // 8] = 1.0
    c["c_eseq"] = e
    return c

def shard_inputs(inp):
    maps = []
    W = ["w_in", "b_igate", "b_fgate", "attn_sinks", "g_mlstm_head", "w_out", "g_mix", "g_cross", "g_mem",
         "w_cq", "w_ck", "w_cv", "w_co", "g_ffn", "w_gate", "w_up", "w_down"]
    wd = {k: np.ascontiguousarray(np.asarray(inp[k], np.float32)[0]) for k in W}
    wd["g_final"] = np.ascontiguousarray(np.asarray(inp["g_final"], np.float32))
    xp = np.asarray(inp["x_prompt"], np.float32); xs = np.asarray(inp["x_sample"], np.float32)
    for c in range(8):
        b, hf = c // 2, c % 2
        m = dict(wd)
        m["xp"] = np.ascontiguousarray(xp[b, hf * 2048:(hf + 1) * 2048])
        m["xpre"] = np.ascontiguousarray(xp[b, 0:2048]) if hf else np.zeros((2048, 1024), np.float32)
        m["xs"] = np.ascontiguousarray(xs[16 * c:16 * c + 16].reshape(128, 1024))
        m["mem"] = np.ascontiguousarray(np.asarray(inp["mem_prompt"], np.float32)[b])
        sl = slice(16 * c, 16 * c + 16)
        m["csk"] = np.ascontiguousarray(np.asarray(inp["cache_swa_k"], np.float32)[0, sl].reshape(16, 128, 128))
        m["csv"] = np.ascontiguousarray(np.asarray(inp["cache_swa_v"], np.float32)[0, sl].reshape(16, 128, 128))
        m["sC"] = np.ascontiguousarray(np.asarray(inp["state_mlstm_C"], np.float32)[0, sl])
        m["sn"] = np.ascontiguousarray(np.asarray(inp["state_mlstm_n"], np.float32)[0, sl])
        m["sm"] = np.ascontiguousarray(np.asarray(inp["state_mlstm_m"], np.float32)[0, sl])
        m["cmk"] = np.ascontiguousarray(np.asarray(inp["cache_mem_k"], np.float32)[0, sl].reshape(16, 256, 256))
        m["cmv"] = np.ascontiguousarray(np.asarray(inp["cache_mem_v"], np.float32)[0, sl].reshape(16, 256, 256))
        m.update(make_consts(hf))
        sk = wd["attn_sinks"]
        sc = np.zeros((32, 2), np.float32)
        rr = np.arange(32)
        for h in range(2):
            sc[:, h] = sk[4 * h + 2 * ((rr % 16) // 8) + rr // 16]
        m["c_sinkcol"] = sc
        maps.append(m)
    return maps

def gather(res):
    f = np.float32
    yp = np.zeros((4, 4096, 1024), f); ys = np.zeros((128, 8, 1024), f)
    skp = np.zeros((1, 4, 128, 2, 64), f); svp = np.zeros_like(skp)
    Cp = np.zeros((1, 4, 4, 128, 64), f); npp = np.zeros((1, 4, 4, 64), f); mp = np.zeros((1, 4, 4), f)
    mkp = np.zeros((1, 4, 256, 4, 64), f); mvp = np.zeros_like(mkp)
    sks = np.zeros((1, 128, 128, 2, 64), f); svs = np.zeros_like(sks)
    Cs = np.zeros((1, 128, 4, 128, 64), f); ns = np.zeros((1, 128, 4, 64), f); ms = np.zeros((1, 128, 4), f)
    for c in range(8):
        r = res[c]; b, hf = c // 2, c % 2
        yp[b, hf * 2048:(hf + 1) * 2048] = r["yp"]
        ys[16 * c:16 * c + 16] = r["ys"].reshape(16, 8, 1024)
        if hf == 1:
            skp[0, b] = r["swak"].reshape(128, 2, 64); svp[0, b] = r["swav"].reshape(128, 2, 64)
            Cp[0, b] = r["Cp"]; npp[0, b] = r["np"]; mp[0, b] = r["mp"].reshape(4)
        else:
            mkp[0, b] = r["memk"].reshape(256, 4, 64); mvp[0, b] = r["memv"].reshape(256, 4, 64)
        sl = slice(16 * c, 16 * c + 16)
        sks[0, sl] = r["sks"].reshape(16, 128, 2, 64); svs[0, sl] = r["svs"].reshape(16, 128, 2, 64)
        Cs[0, sl] = r["Cs"]; ns[0, sl] = r["ns"]; ms[0, sl] = r["ms"]
    return (yp, ys, skp, svp, Cp, npp, mp, mkp, mvp, sks, svs, Cs, ns, ms)


_CACHE = {}


def kernel(**inputs):
    if "nc" not in _CACHE:
        _CACHE["nc"] = build_program(3)[0]
    nc = _CACHE["nc"]
    maps = shard_inputs(inputs)
    res = run_bass_kernel_spmd(nc, maps, core_ids=list(range(8)))
    return gather(res.results)
```

```python
import contextlib
from concourse.bass_utils import run_bass_kernel_spmd
import numpy as np
import concourse.bass as bass
import concourse.mybir as mybir

F32 = mybir.dt.float32
BF16 = mybir.dt.bfloat16
I32 = mybir.dt.int32
AF = mybir.ActivationFunctionType
ALU = mybir.AluOpType
AX = mybir.AxisListType

ENGS = ("pe", "act", "dve", "pool", "sp")


class Res:
    __slots__ = ("name", "last_w", "readers", "sem", "dcount", "excl")

    def __init__(self, name):
        self.name = name
        self.last_w = None
        self.readers = []
        self.sem = None
        self.dcount = 0
        self.excl = False


class Op:
    __slots__ = ("eng", "fn", "deps", "dma_res", "sig", "cnt", "k", "group")

    def __init__(self, eng, fn, dma_res):
        self.eng = eng
        self.fn = fn
        self.deps = set()
        self.dma_res = dma_res
        self.sig = False
        self.cnt = 0
        self.k = 0


class Prog:
    def __init__(self, nc):
        self.nc = nc
        self.ops = []
        self.nres = 0
        self.inherit = []
        self.phase_res = []

    def res(self, name=None, arena=False):
        self.nres += 1
        r = Res(name or f"r{self.nres}")
        if arena:
            r.readers = list(self.inherit)
            self.phase_res.append(r)
        return r

    def new_phase(self):
        inh = set(self.inherit)
        for r in self.phase_res:
            if r.last_w is not None:
                inh.add(r.last_w)
            inh.update(r.readers)
        self.inherit = sorted(inh)
        self.phase_res = []

    def rec_begin(self):
        self._rec = []

    def rec_end(self):
        r = self._rec
        self._rec = None
        return r

    def merge(self, streams):
        streams = [s_ for s_ in streams if s_]
        pos = [0] * len(streams)
        while True:
            best = None
            for k, s_ in enumerate(streams):
                if pos[k] < len(s_):
                    f = pos[k] / len(s_)
                    if best is None or f < best[0]:
                        best = (f, k)
            if best is None:
                break
            k = best[1]
            a, kw = streams[k][pos[k]]
            pos[k] += 1
            self.op(*a, **kw)

    def op(self, eng, fn, reads=(), writes=(), dma_res=None, accum=False, group=False):
        if getattr(self, "_rec", None) is not None:
            self._rec.append(((eng, fn, tuple(reads), tuple(writes)), dict(dma_res=dma_res, accum=accum, group=group)))
            return None
        i = len(self.ops)
        o = Op(eng, fn, dma_res)
        for r in reads:
            if r.last_w is not None:
                o.deps.add(r.last_w)
            if r.excl:
                for q in r.readers:
                    if self.ops[q].eng != eng:
                        o.deps.add(q)
            r.readers.append(i)
        for r in writes:
            if r.last_w is not None:
                lw = self.ops[r.last_w]
                if group and lw.dma_res is not None and lw.dma_res is dma_res:
                    o.deps |= lw.deps
                elif not (accum and lw.eng == "pe" and eng == "pe"):
                    o.deps.add(r.last_w)
            latest = {}
            for q in r.readers:
                if q == i:
                    continue
                oq = self.ops[q]
                if oq.dma_res is not None:
                    o.deps.add(q)
                elif latest.get(oq.eng, -1) < q:
                    latest[oq.eng] = q
            o.deps.update(latest.values())
            r.last_w = i
            r.readers = []
        if eng == "pe":
            o.deps = {d for d in o.deps if self.ops[d].eng != "pe" or self.ops[d].dma_res is not None}
        self.ops.append(o)
        return i

    def dma(self, eng, out, in_, res, reads=(), writes=(), group=False, **kw):
        kw = dict(kw); kw["out"] = out; kw["in_"] = in_
        return self.op(eng, ("dma_start", kw), reads=reads, writes=writes, dma_res=res, group=group)

    def I(self, eng, name, reads=(), writes=(), **kw):
        return self.op(eng, (name, kw), reads=reads, writes=writes)

    def emit(self, final_wait_all=True):
        nc = self.nc
        ops = self.ops
        for o in ops:
            for d in o.deps:
                ops[d].sig = True
        per_eng = {e: [] for e in ENGS}
        for i, o in enumerate(ops):
            per_eng[o.eng].append(i)
        import contextlib
        with contextlib.ExitStack() as st:
            esem = {e: st.enter_context(nc.semaphore(f"s_{e}")) for e in ENGS}
            ecount = {e: 0 for e in ENGS}
            dma_sems = []
            for i, o in enumerate(ops):
                if o.dma_res is not None:
                    r = o.dma_res
                    if r.sem is None:
                        r.sem = st.enter_context(nc.semaphore(f"d{len(dma_sems)}_{r.name}"))
                        dma_sems.append(r)
                    r.dcount += 1
                    o.cnt = 16 * r.dcount
                elif o.sig:
                    ecount[o.eng] += 1
                    o.cnt = ecount[o.eng]
            self.n_dma_sems = len(dma_sems)
            know = {e: {} for e in ENGS}
            know_issue = [None] * len(ops)

            def key_of(o):
                return ("d", id(o.dma_res)) if o.dma_res is not None else ("e", o.eng)

            block = st.enter_context(nc.Block())
            handles = {}

            plan = [None] * len(ops)
            for i, o in enumerate(ops):
                kn = know[o.eng]
                need = {}
                for d in o.deps:
                    p = ops[d]
                    k = key_of(p)
                    if kn.get(k, 0) >= p.cnt:
                        continue
                    if need.get(k, (0, None))[0] < p.cnt:
                        need[k] = (p.cnt, d)
                waits = []
                for k, (cnt, d) in need.items():
                    p = ops[d]
                    sem = p.dma_res.sem if p.dma_res is not None else esem[p.eng]
                    waits.append((sem, cnt))
                    kn[k] = max(kn.get(k, 0), cnt)
                    ki = know_issue[d]
                    for kk, vv in ki.items():
                        if kn.get(kk, 0) < vv:
                            kn[kk] = vv
                know_issue[i] = dict(kn)
                plan[i] = waits
            self.n_waits = sum(len(w) for w in plan)

            def make(ename):
                def body(eh):
                    for i in per_eng[ename]:
                        o = ops[i]
                        for sem, cnt in plan[i]:
                            eh.wait_ge(sem, cnt)
                        ins = getattr(eh, o.fn[0])(**o.fn[1])
                        if o.dma_res is not None:
                            ins.then_inc(o.dma_res.sem, 16)
                        elif o.sig:
                            ins.then_inc(esem[o.eng], 1)
                    if ename == "sp" and final_wait_all:
                        for r in dma_sems:
                            eh.wait_ge(r.sem, 16 * r.dcount)
                        for e in ("pe", "act", "dve", "pool"):
                            if ecount[e]:
                                eh.wait_ge(esem[e], ecount[e])
                return body

            block.tensor(make("pe"))
            block.scalar(make("act"))
            block.vector(make("dve"))
            block.gpsimd(make("pool"))
            block.sync(make("sp"))


D = 1024
FH = 2816
EPS = 1e-6
NTP = 16
NT = 17
GT = 2
NEG = -30000.0
DBG_G0 = 2


def build_program(stage=3, debug=False):
    nc = bass.Bass("TRN2", target_bir_lowering=False)
    P = Prog(nc)

    def din(name, shape, dt=F32):
        return nc.dram_tensor(name, list(shape), dt, kind="ExternalInput").ap()

    def dout(name, shape):
        return nc.dram_tensor(name, list(shape), F32, kind="ExternalOutput").ap()

    xp_d = din("xp", [2048, D]); xpre_d = din("xpre", [2048, D]); xs_d = din("xs", [128, D])
    mem_d = din("mem", [256, D])
    csk_d = din("csk", [16, 128, 128]); csv_d = din("csv", [16, 128, 128])
    sC_d = din("sC", [16, 4, 128, 64]); sn_d = din("sn", [16, 4, 64]); sm_d = din("sm", [16, 4])
    cmk_d = din("cmk", [16, 256, 256]); cmv_d = din("cmv", [16, 256, 256])
    w_in_d = din("w_in", [D, 2312]); b_i_d = din("b_igate", [4]); b_f_d = din("b_fgate", [4])
    sinks_d = din("attn_sinks", [8]); ghead_d = din("g_mlstm_head", [512]); w_out_d = din("w_out", [D, D])
    g_mix_d = din("g_mix", [D]); g_cross_d = din("g_cross", [D]); g_mem_d = din("g_mem", [D])
    w_cq_d = din("w_cq", [D, 256]); w_ck_d = din("w_ck", [D, 256]); w_cv_d = din("w_cv", [D, 256])
    w_co_d = din("w_co", [256, D]); g_ffn_d = din("g_ffn", [D])
    w_gate_d = din("w_gate", [D, FH]); w_up_d = din("w_up", [D, FH]); w_down_d = din("w_down", [FH, D])
    g_final_d = din("g_final", [D])
    ident_d = din("c_ident", [128, 128]); mb_band_d = din("c_mb_band", [128, 256]); mb_first_d = din("c_mb_first", [128, 256])
    mb_caus_d = din("c_mb_caus", [128, 128]); mb_causs_d = din("c_mb_causs", [128, 128])
    sel_d = din("c_sel", [4, 1024]); pmask_d = din("c_pmask", [4, 2])
    smc_d = din("c_smc", [32, 128]); smn_d = din("c_smn", [32, 16, 128]); sinkcol_d = din("c_sinkcol", [32, 2])
    bt_d = din("c_bt", [128, 128]); eseq_d = din("c_eseq", [128, 16])

    yp_o = dout("yp", [2048, D]); ys_o = dout("ys", [128, D])
    swak_o = dout("swak", [128, 128]); swav_o = dout("swav", [128, 128])
    Cp_o = dout("Cp", [4, 128, 64]); np_o = dout("np", [4, 64]); mp_o = dout("mp", [4, 1])
    memk_o = dout("memk", [256, 256]); memv_o = dout("memv", [256, 256])
    sks_o = dout("sks", [16, 128, 128]); svs_o = dout("svs", [16, 128, 128])
    Cs_o = dout("Cs", [16, 4, 128, 64]); ns_o = dout("ns", [16, 4, 64]); ms_o = dout("ms", [16, 4])

    st = contextlib.ExitStack()
    with st:
        def sb(name, shape, dt):
            return st.enter_context(nc.sbuf_tensor(name, list(shape), dt))

        def ps(name, shape, dt):
            return st.enter_context(nc.psum_tensor(name, list(shape), dt))

        banks = [ps(f"bk{i}", [128, 512], F32) for i in range(7)]
        bres = [P.res(f"bk{i}") for i in range(7)]
        for r_ in bres:
            r_.excl = True
        tb = ps("tb", [128, 1024], BF16)
        tbh = [tb[:, 0:512], tb[:, 512:1024]]
        tbhr = [P.res("tbA"), P.res("tbB")]
        for r_ in tbhr:
            r_.excl = True
        tb2 = banks[6][:, :].bitcast(BF16)
        TSEL = [(tb[:, 0:512], tbhr[0]), (tb2[:, 0:512], bres[6])]
        bki = [0]

        bset = [[0, 1, 2, 3, 4]]
        bcnt = {}

        def bank():
            key = tuple(bset[0])
            c = bcnt.get(key, 0)
            bcnt[key] = c + 1
            i = bset[0][c % len(key)]
            return banks[i], bres[i]

        Y = sb("Y", [128, NT, D], F32)
        Yr = [P.res(f"Y{t}") for t in range(NT)]
        identb = sb("identb", [128, 128], BF16); identr = P.res("identb")
        identf = sb("identf", [128, 128], F32); identfr = P.res("identf")
        onesb = sb("onesb", [128, 128], BF16); onesbr = P.res("onesb")
        onesf = sb("onesf", [128, 256], F32); onesfr = P.res("onesf")
        SEL = sb("SEL", [4, 1024], F32); selr = P.res("SEL")
        gcols = sb("gcols", [128, 4, 8], F32); gcolsr = P.res("gcols")
        gheadc = sb("gheadc", [128, 4], F32); gheadr = P.res("ghead")
        sinkb = sb("sinkb", [128, 16], F32); sinkbr = P.res("sinkb")
        gb4 = sb("gb4", [4, 4], F32); gb4r = P.res("gb4")
        mbband = sb("mbband", [128, 256], BF16); mbbandr = P.res("mbband")
        mbfirst = sb("mbfirst", [128, 256], BF16); mbfirstr = P.res("mbfirst")
        mbcaus = sb("mbcaus", [128, 128], BF16); mbcausr = P.res("mbcaus")
        stat = sb("stat", [128, 8, 4], F32)
        statr = [P.res(f"stat{i}") for i in range(8)]
        stati = [0]
        USE_SQRT = [False]
        xsb = sb("xsb", [128, 2, D], BF16); xsbr = [P.res("xsb0"), P.res("xsb1")]
        Cst = sb("Cst", [64, 4, 129], F32); Cstr = [P.res(f"Cst{h}") for h in range(4)]
        ARN = 64400
        arena = sb("arena", [128, ARN], BF16)
        aoff = [0]

        def A(shape, dt, parts=128, name=None):
            n = int(np.prod(shape))
            nb = n * (4 if dt == F32 else 2)
            n16 = (nb + 1) // 2
            n16 = (n16 + 15) // 16 * 16
            assert aoff[0] + n16 <= ARN, f"arena overflow {aoff[0]}+{n16} ({name})"
            v = arena[0:parts, aoff[0]:aoff[0] + n16]
            aoff[0] += n16
            if dt == F32:
                v = v.bitcast(F32)
            v = v[:, 0:n]
            if len(shape) == 2:
                v = v.rearrange("p (a b) -> p a b", a=shape[0])
            elif len(shape) == 3:
                v = v.rearrange("p (a b c) -> p a b c", a=shape[0], b=shape[1])
            return v

        def new_phase():
            P.new_phase()
            aoff[0] = 0

        def AR(name):
            return P.res(name, arena=True)

        def A_at(off, shape, dt, parts=128):
            n = int(np.prod(shape))
            nb = n * (4 if dt == F32 else 2)
            n16 = ((nb + 1) // 2 + 15) // 16 * 16
            v = arena[0:parts, off:off + n16]
            if dt == F32:
                v = v.bitcast(F32)
            v = v[:, 0:n]
            if len(shape) == 2:
                v = v.rearrange("p (a b) -> p a b", a=shape[0])
            elif len(shape) == 3:
                v = v.rearrange("p (a b c) -> p a b c", a=shape[0], b=shape[1])
            return v, off + n16

        def ARalias(name, olds):
            r = P.res(name, arena=True)
            dd = set(r.readers)
            for o_ in olds:
                if o_.last_w is not None:
                    dd.add(o_.last_w)
                dd.update(o_.readers)
            r.readers = sorted(dd)
            return r

        I = P.I

        def mm(out, lhsT, rhs, start, stop, reads, wres):
            I("pe", "matmul", reads, [wres], out=out, lhsT=lhsT, rhs=rhs, start=start, stop=stop)

        P.dma("pool", identb[:], ident_d, identr, writes=[identr])
        P.dma("sp", identf[:], ident_d, identfr, writes=[identfr])
        I("dve", "memset", [], [onesbr], ap=onesb[:], constant=1.0)
        I("dve", "memset", [], [onesfr], ap=onesf[:], constant=1.0)
        P.dma("sp", SEL[:], sel_d, selr, writes=[selr])
        for i, g in enumerate((g_mix_d, g_cross_d, g_mem_d, g_ffn_d)):
            P.dma("sp", gcols[:, i, :], g.rearrange("(k p) -> p k", p=128), gcolsr, writes=[gcolsr], group=True, allow_slow_non_contiguous=True)
        P.dma("sp", gheadc[:], ghead_d.rearrange("(h p) -> p h", p=128), gheadr, writes=[gheadr], allow_slow_non_contiguous=True)
        P.dma("sp", gb4[:, 0:1], b_i_d.rearrange("(h o) -> h o", o=1), gb4r, writes=[gb4r], group=True, allow_slow_non_contiguous=True)
        P.dma("sp", gb4[:, 1:2], b_f_d.rearrange("(h o) -> h o", o=1), gb4r, writes=[gb4r], group=True, allow_slow_non_contiguous=True)
        P.dma("sp", gb4[:, 2:4], pmask_d, gb4r, writes=[gb4r], group=True, allow_slow_non_contiguous=True)
        P.dma("sp", sinkb[:, 0:8], sinks_d.partition_broadcast(128), sinkbr, writes=[sinkbr])
        I("dve", "tensor_scalar", [sinkbr], [sinkbr], out=sinkb[:, 8:16], in0=sinkb[:, 0:8], scalar1=-1.0, scalar2=None,
          op0=ALU.mult)
        I("dve", "tensor_scalar", [gb4r], [gb4r], out=gb4[:, 1:2], in0=gb4[:, 1:2], scalar1=-1.0, scalar2=None, op0=ALU.mult)
        P.dma("pool", mbband[:], mb_band_d, mbbandr, writes=[mbbandr])
        P.dma("pool", mbfirst[:], mb_first_d, mbfirstr, writes=[mbfirstr])
        P.dma("pool", mbcaus[:], mb_caus_d, mbcausr, writes=[mbcausr])
        for h in range(4):
            I("dve", "memset", [], [Cstr[h]], ap=Cst[:, h, :], constant=0.0)

        def SELh(h, n=128):
            return SEL[:, h * 128:h * 128 + n]

        def NSELh(h, n=128):
            return SEL[:, 512 + h * 128:512 + h * 128 + n]

        def norm_stats(src, sres, jb=0):
            i = stati[0] % 8
            stati[0] += 1
            sr = statr[i]
            I("act", "activation", [sres], [xsbr[jb], sr], out=xsb[:, jb, :], in_=src, func=AF.Square, accum_out=stat[:, i, 0:1])
            I("dve", "tensor_scalar", [sr], [sr], out=stat[:, i, 1:2], in0=stat[:, i, 0:1], scalar1=1.0 / D, scalar2=EPS,
              op0=ALU.mult, op1=ALU.add)
            if USE_SQRT[0]:
                I("act", "activation", [sr], [sr], out=stat[:, i, 2:3], in_=stat[:, i, 1:2], func=AF.Sqrt)
                I("dve", "reciprocal", [sr], [sr], out=stat[:, i, 3:4], in_=stat[:, i, 2:3])
            else:
                I("act", "activation", [sr], [sr], out=stat[:, i, 2:3], in_=stat[:, i, 1:2], func=AF.Ln)
                I("act", "activation", [sr], [sr], out=stat[:, i, 3:4], in_=stat[:, i, 2:3], func=AF.Exp, scale=-0.5)
            return stat[:, i, 3:4], sr

        xsi = [0]

        def norm_T(src, sres, gi, dst, dres, half=None, tsel=None):
            b = xsi[0] % 2 if half is None else half
            xsi[0] += 1
            rstd, sr = norm_stats(src, sres, b)
            I("dve", "tensor_scalar", [sres, sr], [xsbr[b]], out=xsb[:, b, :], in0=src, scalar1=rstd, scalar2=None, op0=ALU.mult)
            if half is None:
                for k in range(8):
                    I("pe", "transpose", [xsbr[b], identr], [*tbhr], out=tb[:, k * 128:(k + 1) * 128],
                      in_=xsb[:, b, k * 128:(k + 1) * 128], identity=identb[:])
                for k in range(8):
                    I("act", "activation", [*tbhr, gcolsr], dres, out=dst[:, k, :], in_=tb[:, k * 128:(k + 1) * 128],
                      func=AF.Copy, scale=gcols[:, gi, k:k + 1])
            else:
                tq, tqr = (tbh[half], tbhr[half]) if tsel is None else tsel
                for kb in range(2):
                    for k4 in range(4):
                        k = kb * 4 + k4
                        I("pe", "transpose", [xsbr[b], identr], [tqr], out=tq[:, k4 * 128:(k4 + 1) * 128],
                          in_=xsb[:, b, k * 128:(k + 1) * 128], identity=identb[:])
                    for k4 in range(4):
                        k = kb * 4 + k4
                        if k4 % 2 == 0:
                            I("act", "activation", [tqr, gcolsr], dres, out=dst[:, k, :],
                              in_=tq[:, k4 * 128:(k4 + 1) * 128], func=AF.Copy, scale=gcols[:, gi, k:k + 1])
                        else:
                            I("dve", "tensor_scalar", [tqr, gcolsr], dres, out=dst[:, k, :],
                              in0=tq[:, k4 * 128:(k4 + 1) * 128], scalar1=gcols[:, gi, k:k + 1], scalar2=None, op0=ALU.mult)

        class NS:
            pass

        def alloc_mixer(gt, nkt, nvt):
            M = NS()
            M.WQ = A([8, 512], BF16); M.WQr = AR("WQ")
            M.WTOK = A([8, 1024], BF16); M.WTOKr = AR("WTOK")
            M.WK = M.WTOK[:, :, 0:128]; M.WKr = M.WTOKr
            M.WMQ = A([8, 256], BF16); M.WMQr = AR("WMQ")
            M.WMK = M.WTOK[:, :, 256:512]; M.WMKr = M.WTOKr
            M.WOG = A([8, 512], BF16); M.WOGr = AR("WOG")
            M.WGT = A([8, 8], BF16); M.WGTr = AR("WGT")
            M.WOA = A([4, 1024], BF16); M.WOAr = AR("WOA")
            M.WOM = A([4, 1024], BF16); M.WOMr = AR("WOM")

            def wload(dst, res, src, **kw):
                P.dma("pool", dst, src, res, writes=[res], **kw)

            def wcols(a_, b_):
                return w_in_d[:, a_:b_].rearrange("(k p) n -> p k n", p=128)
            wload(M.WTOK[:, :, 0:256], M.WTOKr, wcols(512, 768), group=True)
            wload(M.WTOK[:, :, 256:1024], M.WTOKr, wcols(1024, 1792), group=True)
            wload(M.WGT[:], M.WGTr, wcols(2304, 2312), allow_slow_non_contiguous=True)
            wload(M.WQ[:], M.WQr, wcols(0, 512))
            wload(M.WMQ[:], M.WMQr, wcols(768, 1024))
            wload(M.WOG[:], M.WOGr, wcols(1792, 2304))
            wload(M.WOA[:], M.WOAr, w_out_d[0:512, :].rearrange("(c p) n -> p c n", p=128))
            wload(M.WOM[:], M.WOMr, w_out_d[512:1024, :].rearrange("(h p) n -> p h n", p=128))
            M.KT = A([2, nkt * 128], BF16, parts=64); M.KTr = [AR(f"KT{i}") for i in range(nkt)]
            M.Vt = A([nvt, 128], BF16); M.Vtr = [AR(f"Vt{i}") for i in range(nvt)]
            M.XNTg = A([8, gt * 128], BF16); M.XNTgr = [AR(f"XNTg{i}") for i in range(gt)]
            M.QT = A([8, gt * 128], BF16, parts=64); M.QTr = AR("QT")
            M.MQT = A([4, gt * 128], BF16, parts=64); M.MQTr = AR("MQT")
            M.MKT = A([4, gt * 128], BF16, parts=64); M.MKTr = AR("MKT")
            M.SGT = A([4, gt * 128], BF16); M.SGTr = AR("SGT")
            M.MKtok = A([gt, 256], BF16); M.MKtokr = [AR(f"MKtok{i}") for i in range(gt)]
            M.MVaug = A([gt, 4, 129], BF16); M.MVaugr = [AR(f"MVaug{i}") for i in range(gt)]
            M.ATTT = A([4, gt * 128], BF16); M.ATTTr = [AR(f"ATTT{i}") for i in range(gt)]
            M.HMT = A([4, gt * 128], BF16); M.HMTr = [AR(f"HMT{i}") for i in range(gt)]
            M.NG = gt * 128
            NG_ = M.NG
            M.G_IG = A([1, NG_ + 1], F32, parts=4)[:, 0, :]; M.G_E = A([1, NG_], F32, parts=4)[:, 0, :]
            M.G_L1 = A([1, NG_], F32, parts=4)[:, 0, :]; M.G_B = A([1, NG_ + 1], F32, parts=4)[:, 0, :]
            M.G_A = A([1, NG_], F32, parts=4)[:, 0, :]; M.G_M = A([1, NG_ + 1], F32, parts=4)[:, 0, :]
            M.G_BM = A([1, NG_], F32, parts=4)[:, 0, :]; M.G_DM = A([1, NG_], F32, parts=4)[:, 0, :]
            M.Gr = AR("G_IG"); M.G_Br = AR("G_B"); M.G_Ar = AR("G_A"); M.G_Mr = AR("G_M"); M.G_BMr = AR("G_BM"); M.G_DMr = AR("G_DM")
            M.SKV = A([1, 256], F32)[:, 0, :]; M.SKVr = AR("SKV")
            M.Ebuf = A([4, 256], BF16); M.Er = AR("E")
            M.PTs = A([1, 1024], BF16); M.PTsr = [AR("PTs0")] * 2
            M.sm_st = A([1, 32], F32)[:, 0, :]; M.smr = AR("sm_st")
            M.WKC = A([1, 8], F32)[:, 0, :]; M.WKCr = AR("WKC")
            M.DG = A([1, 8], F32, parts=4)[:, 0, :]; M.DGr = AR("DG")
            M.VW = A([4, 129], BF16); M.VWr = [AR(f"VW{h}") for h in range(4)]
            M.Cb = A([4, 257], BF16, parts=64); M.Cbr = [AR(f"Cb{h}") for h in range(4)]
            M.WT = A([4, 128], BF16); M.WTr = AR("WT")
            M.ST = A([4, 128], BF16); M.STr = AR("ST")
            M.WI = A([4, 128], BF16); M.WIr = AR("WI")
            M.QW = A([4, 128], BF16, parts=64); M.QWr = AR("QW")
            M.LOWB = A([4, 128], F32); M.LOWBr = AR("LOWB")
            M.T1 = A([4, 128], F32); M.T1r = AR("T1")
            M.T2 = A([4, 128], F32); M.T2r = AR("T2")
            M.USQ = A([4, 128], BF16); M.USQr = AR("USQ")
            for i in range(gt):
                I("dve", "memset", [], [M.MVaugr[i]], ap=M.MVaug[:, i, :, 128:129], constant=1.0)
            M.PB = [(M.XNTg, M.XNTgr, M.MKtok, M.MKtokr, M.MVaug, M.MVaugr)]
            if gt > 1:
                x2 = A([8, gt * 128], BF16); x2r = [AR(f"XNTh{i}") for i in range(gt)]
                k2 = A([gt, 256], BF16); k2r = [AR(f"MKtoh{i}") for i in range(gt)]
                v2 = A([gt, 4, 129], BF16); v2r = [AR(f"MVauh{i}") for i in range(gt)]
                for i in range(gt):
                    I("dve", "memset", [], [v2r[i]], ap=v2[:, i, :, 128:129], constant=1.0)
                M.PB.append((x2, x2r, k2, k2r, v2, v2r))
            return M

        def use(pb):
            M.XNTg, M.XNTgr, M.MKtok, M.MKtokr, M.MVaug, M.MVaugr = M.PB[pb]

        M = alloc_mixer(GT, NTP + 1, NTP + 1)
        I("dve", "memset", [], [M.G_Br], ap=M.G_B[:, 0:1], constant=0.0)
        I("dve", "memset", [], [M.G_Mr], ap=M.G_M[:, 0:1], constant=0.0)

        def tok_major(ti, xcols, xres, vslot, want_kv_out=None):
            b0, b0r = bank()
            for k in range(8):
                mm(b0[:, :], M.XNTg[:, k, xcols], M.WTOK[:, k, 0:512], k == 0, k == 7, [xres, M.WTOKr], b0r)
            I("act", "activation", [b0r], [M.Vtr[vslot]], out=M.Vt[:, vslot, :], in_=b0[:, 128:256], func=AF.Copy)
            I("act", "activation", [b0r], [M.MKtokr[ti]], out=M.MKtok[:, ti, :], in_=b0[:, 256:512], func=AF.Copy, scale=0.125)
            if want_kv_out is not None:
                I("dve", "tensor_copy", [b0r], [M.SKVr], out=M.SKV[:, :], in_=b0[:, 0:256])
                if want_kv_out == "sample":
                    P.dma("sp", sks_o[:, 120:128, :], M.SKV[:, 0:128], M.SKVr, reads=[M.SKVr], group=True)
                    P.dma("sp", svs_o[:, 120:128, :], M.SKV[:, 128:256], M.SKVr, reads=[M.SKVr], group=True)
                else:
                    P.dma("sp", swak_o, M.SKV[:, 0:128], M.SKVr, reads=[M.SKVr], group=True)
                    P.dma("sp", swav_o, M.SKV[:, 128:256], M.SKVr, reads=[M.SKVr], group=True)
            b1, b1r = bank()
            for k in range(8):
                mm(b1[:, :], M.XNTg[:, k, xcols], M.WTOK[:, k, 512:1024], k == 0, k == 7, [xres, M.WTOKr], b1r)
            I("dve", "tensor_copy", [b1r], [M.MVaugr[ti]], out=M.MVaug[:, ti, :, 0:128],
              in_=b1[:, :].rearrange("p (h d) -> p h d", h=4))

        def feat64(W, Wr, nh, dst, dres, ntok, xres, scale=None, dcol0=0):
            for h0 in range(0, nh, 2):
                bk, bkr = bank()
                for hh in range(2):
                    h = h0 + hh
                    for k in range(8):
                        mm(bk[0:64, hh * 256:hh * 256 + ntok], W[:, k, h * 64:(h + 1) * 64], M.XNTg[:, k, 0:ntok],
                           k == 0, k == 7, [Wr] + xres, bkr)
                src = bk[0:64, :].rearrange("p (a b) -> p a b", a=2)[:, :, 0:ntok]
                kw = {} if scale is None else {"scale": scale}
                if scale is None and (h0 // 2) % 2 == 1:
                    I("dve", "tensor_copy", [bkr], dres, out=dst[:, h0:h0 + 2, dcol0:dcol0 + ntok], in_=src)
                else:
                    I("act", "activation", [bkr], dres, out=dst[:, h0:h0 + 2, dcol0:dcol0 + ntok], in_=src, func=AF.Copy, **kw)

        def gates(ntok, xres, prefix):
            pg, pgr = bank()
            for k in range(8):
                mm(pg[0:4, 0:ntok], M.WGT[:, k, 0:4], M.XNTg[:, k, 0:ntok], k == 0, k == 7, [M.WGTr] + xres, pgr)
            for k in range(8):
                mm(pg[0:4, 256:256 + ntok], M.WGT[:, k, 4:8], M.XNTg[:, k, 0:ntok], k == 0, k == 7, [M.WGTr] + xres, pgr)
            I("act", "activation", [pgr, gb4r], [M.Gr], out=M.G_IG[:, 1:ntok + 1], in_=pg[0:4, 0:ntok], func=AF.Identity,
              bias=gb4[:, 0:1])
            I("act", "activation", [pgr, gb4r], [M.Gr], out=M.G_E[:, 0:ntok], in_=pg[0:4, 256:256 + ntok], func=AF.Exp,
              bias=gb4[:, 1:2], scale=-1.0)
            I("act", "activation", [M.Gr], [M.Gr], out=M.G_L1[:, 0:ntok], in_=M.G_E[:, 0:ntok], func=AF.Ln, bias=1.0)
            if prefix == "sample":
                return
            if prefix:
                I("dve", "tensor_scalar", [M.Gr, gb4r], [M.Gr], out=M.G_L1[:, 0:ntok], in0=M.G_L1[:, 0:ntok], scalar1=gb4[:, 2:3],
                  scalar2=None, op0=ALU.mult)
            I("dve", "tensor_tensor_scan", [M.Gr, M.G_Br, onesfr], [M.G_Br], out=M.G_B[:, 1:ntok + 1], data0=onesf[0:4, 0:ntok],
              data1=M.G_L1[:, 0:ntok], initial=M.G_B[:, 0:1], op0=ALU.mult, op1=ALU.subtract)
            I("dve", "scalar_tensor_tensor", [M.Gr, M.G_Br, gb4r], [M.G_Ar], out=M.G_A[:, 0:ntok], in0=M.G_IG[:, 1:ntok + 1],
              scalar=(gb4[:, 3:4] if prefix else 0.0), in1=M.G_B[:, 1:ntok + 1], op0=ALU.add, op1=ALU.subtract)
            I("dve", "tensor_tensor_scan", [M.G_Ar, M.G_Mr, onesfr], [M.G_Mr], out=M.G_M[:, 1:ntok + 1], data0=onesf[0:4, 0:ntok],
              data1=M.G_A[:, 0:ntok], initial=M.G_M[:, 0:1], op0=ALU.mult, op1=ALU.max)
            I("dve", "tensor_tensor", [M.G_Br, M.G_Mr], [M.G_BMr], out=M.G_BM[:, 0:ntok], in0=M.G_B[:, 1:ntok + 1],
              in1=M.G_M[:, 1:ntok + 1], op=ALU.add)
            for ci in range(ntok // 128):
                I("dve", "tensor_scalar", [M.G_Mr], [M.G_DMr], out=M.G_DM[:, ci * 128:(ci + 1) * 128],
                  in0=M.G_M[:, 1 + ci * 128:1 + (ci + 1) * 128], scalar1=M.G_M[:, ci * 128:ci * 128 + 1], scalar2=None,
                  op0=ALU.subtract)

        def gates_carry(ntok):
            I("dve", "tensor_copy", [M.G_Br], [M.G_Br], out=M.G_B[:, 0:1], in_=M.G_B[:, ntok:ntok + 1])
            I("dve", "tensor_copy", [M.G_Mr], [M.G_Mr], out=M.G_M[:, 0:1], in_=M.G_M[:, ntok:ntok + 1])

        def state_update(ti, c0, refresh_cb):
            pw, pwr = bank()
            I4 = SEL[:, 0:512].rearrange("p (h t) -> p h t", t=128)[:, :, 0]
            I("dve", "tensor_scalar", [selr, M.G_Mr], [M.DGr], out=M.DG[:, 0:4], in0=I4, scalar1=M.G_M[:, c0 + 128:c0 + 129],
              scalar2=-1.0, op0=ALU.mult, op1=ALU.mult)
            I("dve", "tensor_scalar", [selr, M.G_DMr], [M.DGr], out=M.DG[:, 4:8], in0=I4, scalar1=M.G_DM[:, c0 + 127:c0 + 128],
              scalar2=-1.0, op0=ALU.mult, op1=ALU.mult)
            mm(pw[:, 0:4], M.G_A[:, c0:c0 + 128], I4, True, False, [M.G_Ar, selr], pwr)
            mm(pw[:, 0:4], onesf[0:4, 0:128], M.DG[:, 0:4], False, True, [onesfr, M.DGr], pwr)
            mm(pw[:, 4:8], onesf[0:4, 0:128], M.DG[:, 4:8], True, True, [onesfr, M.DGr], pwr)
            I("act", "activation", [pwr], [M.WKCr], out=M.WKC[:, 0:8], in_=pw[:, 0:8], func=AF.Exp)
            for h in range(4):
                I("dve", "tensor_scalar", [M.MVaugr[ti], M.WKCr], [M.VWr[h]], out=M.VW[:, h, :], in0=M.MVaug[:, ti, h, :],
                  scalar1=M.WKC[:, h:h + 1], scalar2=None, op0=ALU.mult)
            for h0 in (0, 2):
                dc, dcr = bank()
                for hh in range(2):
                    h = h0 + hh
                    mm(dc[0:64, hh * 129:(hh + 1) * 129], M.MKtok[:, ti, h * 64:(h + 1) * 64], M.VW[:, h, :], True, True,
                       [M.MKtokr[ti], M.VWr[h]], dcr)
                for hh in range(2):
                    h = h0 + hh
                    I("dve", "scalar_tensor_tensor", [Cstr[h], M.WKCr, dcr], [Cstr[h]], out=Cst[:, h, :], in0=Cst[:, h, :],
                      scalar=M.WKC[0:64, 4 + h:5 + h], in1=dc[0:64, hh * 129:(hh + 1) * 129], op0=ALU.mult, op1=ALU.add)
            if refresh_cb:
                for h in range(4):
                    I("act", "activation", [Cstr[h]], [M.Cbr[h]], out=M.Cb[:, h, 0:129], in_=Cst[:, h, :], func=AF.Copy)
                    I("act", "activation", [Cstr[h]], [M.Cbr[h]], out=M.Cb[:, h, 129:257],
                      in_=Cst[:, h, 128:129].broadcast_to([64, 128]), func=AF.Copy)

        def mlstm_chunk(ti, c0, mbias, mbiasr, inter=True, inter_fn=None):
            cs = slice(c0, c0 + 128)
            pwt, pwtr = bank()
            for h in range(4):
                o = pwt[:, h * 128:(h + 1) * 128]
                mm(o, M.G_A[:, cs], SELh(h), True, False, [M.G_Ar, selr], pwtr)
                mm(o, NSELh(h), M.G_M[:, c0 + 1:c0 + 129], False, False, [M.G_Mr, selr], pwtr)
                mm(o, identb[:], mbias, False, True, [identr, mbiasr], pwtr)
            I("act", "activation", [pwtr], [M.WTr], out=M.WT[:, :, :], in_=pwt[:, :].rearrange("p (h t) -> p h t", h=4), func=AF.Exp)
            pqk, pqkr = bank()
            for h in range(4):
                mm(pqk[:, h * 128:(h + 1) * 128], M.MKT[:, h, cs], M.MQT[:, h, cs], True, True, [M.MKTr, M.MQTr], pqkr)
            I("dve", "tensor_tensor", [pqkr, M.WTr], [M.STr], out=M.ST[:, :, :], in0=pqk[:, :].rearrange("p (h t) -> p h t", h=4),
              in1=M.WT[:, :, :], op=ALU.mult)
            pwi, pwir = bank()
            for h in range(4):
                mm(pwi[:, h * 128:(h + 1) * 128], NSELh(h), M.G_DM[:, cs], True, True, [M.G_DMr, selr], pwir)
            I("act", "activation", [pwir], [M.WIr], out=M.WI[:, :, :], in_=pwi[:, :].rearrange("p (h t) -> p h t", h=4), func=AF.Exp)
            I("dve", "tensor_tensor", [M.MQTr, M.WIr], [M.QWr], out=M.QW[:, :, :], in0=M.MQT[:, :, cs], in1=M.WI[0:64, :, :], op=ALU.mult)
            plb, plbr = bank()
            for h in range(4):
                mm(plb[:, h * 128:(h + 1) * 128], NSELh(h), M.G_BM[:, cs], True, True, [M.G_BMr, selr], plbr)
            I("act", "activation", [plbr], [M.LOWBr], out=M.LOWB[:, :, :], in_=plb[:, :].rearrange("p (h t) -> p h t", h=4), func=AF.Exp)
            pnum, pnumr = banks[5], bres[5]
            pden, pdenr = banks[6], bres[6]
            if inter_fn is not None:
                inter_fn("pre")
            for h in range(4):
                o = pnum[:, h * 128:(h + 1) * 128]
                mm(o, M.MVaug[:, ti, h, 0:128], M.ST[:, h, :], True, False, [M.MVaugr[ti], M.STr], pnumr)
                if inter_fn is not None:
                    inter_fn("num", h, pnum, pnumr)
                else:
                    mm(o, M.Cb[:, h, 0:128], M.QW[:, h, :], False, True, [M.Cbr[h], M.QWr], pnumr)
            for h in range(4):
                o = pden[:, h * 128:(h + 1) * 128]
                mm(o, onesb[:], M.ST[:, h, :], True, False, [onesbr, M.STr], pdenr)
                if inter_fn is not None:
                    inter_fn("den", h, pden, pdenr)
                else:
                    mm(o, M.Cb[:, h, 129:257], M.QW[:, h, :], False, True, [M.Cbr[h], M.QWr], pdenr)
            return pnum, pnumr, pden, pdenr

        def mlstm_finish(pnum, pnumr, pden, pdenr, c0, hres):
            cs = slice(c0, c0 + 128)
            v4 = lambda b: b[:, :].rearrange("p (h t) -> p h t", h=4)
            I("act", "activation", [pdenr], [M.T1r], out=M.T1[:, :, :], in_=v4(pden), func=AF.Abs)
            I("dve", "tensor_tensor", [M.T1r, M.LOWBr], [M.T1r], out=M.T1[:, :, :], in0=M.T1[:, :, :], in1=M.LOWB[:, :, :], op=ALU.max)
            I("act", "activation", [M.T1r], [M.T1r], out=M.T1[:, :, :], in_=M.T1[:, :, :], func=AF.Square, scale=float(np.sqrt(EPS)))
            I("act", "activation", [pnumr], [M.USQr], out=M.USQ[:, :, :], in_=v4(pnum), func=AF.Square)
            pss, pssr = bank()
            mm(pss[:, :], onesb[:], M.USQ[:, :, :], True, True, [onesbr, M.USQr], pssr)
            I("dve", "scalar_tensor_tensor", [pssr, M.T1r], [M.T2r], out=M.T2[:, :, :], in0=v4(pss), scalar=1.0 / 128, in1=M.T1[:, :, :],
              op0=ALU.mult, op1=ALU.add)
            I("act", "activation", [M.T2r], [M.T2r], out=M.T2[:, :, :], in_=M.T2[:, :, :], func=AF.Ln)
            I("act", "activation", [M.T2r], [M.T2r], out=M.T2[:, :, :], in_=M.T2[:, :, :], func=AF.Exp, scale=-0.5)
            I("dve", "tensor_tensor", [pnumr, M.T2r], [M.T1r], out=M.T1[:, :, :], in0=v4(pnum), in1=M.T2[:, :, :], op=ALU.mult)
            for h in range(4):
                I("dve", "scalar_tensor_tensor", [M.T1r, gheadr, M.SGTr], [hres], out=M.HMT[:, h, cs], in0=M.T1[:, h, :],
                  scalar=gheadc[:, h:h + 1], in1=M.SGT[:, h, cs], op0=ALU.mult, op1=ALU.mult)

        def swa_tile(ti, kcol0, vslots, mb, mbr, ktres):
            qs = slice(ti * 128, (ti + 1) * 128)
            for h in range(2):
                bks = [bank(), bank()]
                for g in range(4):
                    bk, bkr = bks[g // 2]
                    o = bk[:, (g % 2) * 256:(g % 2 + 1) * 256]
                    mm(o, M.QT[:, 4 * h + g, qs], M.KT[:, h, kcol0:kcol0 + 256], True, False, [M.QTr] + ktres, bkr)
                    mm(o, identb[:], mb, False, True, [identr, mbr], bkr)
                for j in range(2):
                    I("dve", "reduce_max", [bks[j][1]], [M.smr], out=M.sm_st[:, 2 * j:2 * j + 2],
                      in_=bks[j][0][:, :].rearrange("p (a b) -> p a b", a=2), axis=AX.X)
                I("dve", "tensor_scalar", [M.smr], [M.smr], out=M.sm_st[:, 0:4], in0=M.sm_st[:, 0:4], scalar1=-0.125, scalar2=None,
                  op0=ALU.mult)
                I("dve", "tensor_tensor", [M.smr, sinkbr], [M.smr], out=M.sm_st[:, 0:4], in0=M.sm_st[:, 0:4],
                  in1=sinkb[:, 8 + 4 * h:12 + 4 * h], op=ALU.min)
                for g in range(4):
                    bk, bkr = bks[g // 2]
                    I("act", "activation", [bkr, M.smr], [M.Er, M.smr], out=M.Ebuf[:, g, :], in_=bk[:, (g % 2) * 256:(g % 2 + 1) * 256],
                      func=AF.Exp, bias=M.sm_st[:, g:g + 1], scale=0.125, accum_out=M.sm_st[:, 4 + g:5 + g])
                I("dve", "tensor_tensor", [M.smr, sinkbr], [M.smr], out=M.sm_st[:, 8:12], in0=M.sm_st[:, 0:4],
                  in1=sinkb[:, 4 * h:4 * h + 4], op=ALU.add)
                I("act", "activation", [M.smr], [M.smr], out=M.sm_st[:, 8:12], in_=M.sm_st[:, 8:12], func=AF.Exp)
                I("dve", "tensor_tensor", [M.smr], [M.smr], out=M.sm_st[:, 8:12], in0=M.sm_st[:, 8:12], in1=M.sm_st[:, 4:8], op=ALU.add)
                I("dve", "reciprocal", [M.smr], [M.smr], out=M.sm_st[:, 12:16], in_=M.sm_st[:, 8:12])
                for g in range(4):
                    if g % 2 == 0:
                        I("act", "activation", [M.Er, M.smr], [M.Er], out=M.Ebuf[:, g, :], in_=M.Ebuf[:, g, :], func=AF.Copy,
                          scale=M.sm_st[:, 12 + g:13 + g])
                    else:
                        I("dve", "tensor_scalar", [M.Er, M.smr], [M.Er], out=M.Ebuf[:, g, :], in0=M.Ebuf[:, g, :],
                          scalar1=M.sm_st[:, 12 + g:13 + g], scalar2=None, op0=ALU.mult)
                for kb in range(2):
                    for g in range(4):
                        blk = kb * 4 + (g % 2) * 2 + g // 2
                        I("pe", "transpose", [M.Er, identr], [*tbhr], out=tb[:, blk * 128:(blk + 1) * 128],
                          in_=M.Ebuf[:, g, kb * 128:(kb + 1) * 128], identity=identb[:])
                pb = 0
                if h == 0:
                    I("dve", "tensor_copy", [*tbhr], [M.PTsr[pb]], out=M.PTs[:, pb, :], in_=tb[:, :])
                else:
                    I("act", "activation", [*tbhr], [M.PTsr[pb]], out=M.PTs[:, pb, :], in_=tb[:, :], func=AF.Copy)
                po, por = bank()
                for par in range(2):
                    for kb in range(2):
                        mm(po[par * 64:(par + 1) * 64, 0:256], M.Vt[:, vslots[kb], h * 64:(h + 1) * 64],
                           M.PTs[:, pb, kb * 512 + par * 256:kb * 512 + (par + 1) * 256], kb == 0, kb == 1,
                           [M.Vtr[vslots[kb]], M.PTsr[pb]], por)
                I("act", "activation", [por], [M.ATTTr[ti]], out=M.ATTT[:, 2 * h:2 * h + 2, qs],
                  in_=po[:, 0:256].rearrange("p (g q) -> p g q", g=2), func=AF.Copy)

        def wout_tile(ti, t):
            qs = slice(ti * 128, (ti + 1) * 128)
            for c in range(2):
                bk, bkr = bank()
                cc = slice(c * 512, (c + 1) * 512)
                for hg in range(4):
                    mm(bk[:, :], M.ATTT[:, hg, qs], M.WOA[:, hg, cc], hg == 0, False, [M.ATTTr[ti], M.WOAr], bkr)
                for h in range(4):
                    mm(bk[:, :], M.HMT[:, h, qs], M.WOM[:, h, cc], False, h == 3, [M.HMTr[ti], M.WOMr], bkr)
                I("dve", "tensor_tensor", [Yr[t], bkr], [Yr[t]], out=Y[:, t, cc], in0=Y[:, t, cc], in1=bk[:, :], op=ALU.add)

        xpre_t = xpre_d.rearrange("(t p) d -> t p d", p=128)
        xp_t = xp_d.rearrange("(t p) d -> t p d", p=128)
        for t in range(NTP):
            P.dma("sp", Y[:, t, :], xpre_t[t], Yr[t], writes=[Yr[t]])

        tb4sel = (banks[4][:, :].bitcast(BF16)[:, 0:512], bres[4])

        def prep_prefix(g0, pb):
            use(pb)
            for ti in range(GT):
                t = g0 + ti
                norm_T(Y[:, t, :], Yr[t], 0, M.XNTg[:, :, ti * 128:(ti + 1) * 128], [M.XNTgr[ti]], half=1, tsel=tb4sel)
            for ti in range(GT):
                tok_major(ti, slice(ti * 128, (ti + 1) * 128), M.XNTgr[ti], 0)

        prep_prefix(0, 0)
        for gi, g0 in enumerate(range(0, NTP, GT)):
            pb = gi % 2
            use(pb)
            P.rec_begin(); bset[0] = [0, 1, 2, 3]
            gates(GT * 128, M.XNTgr, True)
            if g0 + GT == NTP:
                bk, bkr = bank()
                for h in range(2):
                    for k in range(8):
                        mm(bk[0:64, h * 128:(h + 1) * 128], M.WK[:, k, h * 64:(h + 1) * 64], M.XNTg[:, k, (GT - 1) * 128:GT * 128],
                           k == 0, k == 7, [M.WKr, M.XNTgr[GT - 1]], bkr)
                I("act", "activation", [bkr], [M.KTr[0]], out=M.KT[:, :, 0:128],
                  in_=bk[0:64, 0:256].rearrange("p (a b) -> p a b", a=2), func=AF.Copy)
            for ti in range(GT):
                last = (g0 + ti == NTP - 1)
                state_update(ti, ti * 128, last)
            gates_carry(GT * 128)
            sA = P.rec_end()
            strs = [sA]
            if g0 + GT < NTP:
                P.rec_begin(); bset[0] = [4]
                prep_prefix(g0 + GT, pb ^ 1)
                strs.append(P.rec_end())
                use(pb)
            P.merge(strs)
            bset[0] = [0, 1, 2, 3, 4]

        for t in range(NTP):
            P.dma("sp", Y[:, t, :], xp_t[t], Yr[t], writes=[Yr[t]])

        def prep_main(g0, pb, merged):
            use(pb)
            if merged:
                for ti in range(GT):
                    t = g0 + ti
                    norm_T(Y[:, t, :], Yr[t], 0, M.XNTg[:, :, ti * 128:(ti + 1) * 128], [M.XNTgr[ti]], half=1, tsel=tb4sel)
            else:
                for ti in range(GT):
                    t = g0 + ti
                    norm_T(Y[:, t, :], Yr[t], 0, M.XNTg[:, :, ti * 128:(ti + 1) * 128], [M.XNTgr[ti]])
            for ti in range(GT):
                t = g0 + ti
                tok_major(ti, slice(ti * 128, (ti + 1) * 128), M.XNTgr[ti], 1 + t, want_kv_out=(True if t == NTP - 1 else None))

        prep_main(0, 0, False)
        for gi, g0 in enumerate(range(0, NTP, GT)):
            pb = gi % 2
            use(pb)
            gates(M.NG, M.XNTgr, False)
            feat64(M.WK, M.WKr, 2, M.KT, [M.KTr[1 + g0 + i] for i in range(GT)], M.NG, M.XNTgr, dcol0=128 + g0 * 128)
            feat64(M.WQ, M.WQr, 8, M.QT, [M.QTr], M.NG, M.XNTgr)
            feat64(M.WMQ, M.WMQr, 4, M.MQT, [M.MQTr], M.NG, M.XNTgr)
            feat64(M.WMK, M.WMKr, 4, M.MKT, [M.MKTr], M.NG, M.XNTgr, scale=0.125)
            for h0 in (0, 2):
                bk, bkr = bank()
                for hh in range(2):
                    h = h0 + hh
                    for k in range(8):
                        mm(bk[:, hh * 256:hh * 256 + M.NG], M.WOG[:, k, h * 128:(h + 1) * 128], M.XNTg[:, k, 0:M.NG], k == 0, k == 7,
                           [M.WOGr] + M.XNTgr, bkr)
                sgv = M.SGT[:, h0:h0 + 2, :]
                I("act", "activation", [bkr], [M.SGTr], out=sgv, in_=bk[:, :].rearrange("p (a b) -> p a b", a=2)[:, :, 0:M.NG],
                  func=AF.Exp, scale=-1.0)
                I("act", "activation", [M.SGTr], [M.SGTr], out=sgv, in_=sgv, func=AF.Ln, bias=1.0)
                I("act", "activation", [M.SGTr], [M.SGTr], out=sgv, in_=sgv, func=AF.Exp, scale=-1.0)
            pend = None
            for ti in range(GT):
                t = g0 + ti
                P.rec_begin(); bset[0] = [2, 3]
                pn = mlstm_chunk(ti, ti * 128, mbcaus[:], mbcausr)
                mlstm_finish(*pn, ti * 128, M.HMTr[ti])
                state_update(ti, ti * 128, True)
                s_ml = P.rec_end()
                P.rec_begin(); bset[0] = [0, 1]
                swa_tile(ti, t * 128, (t, t + 1), (mbfirst[:] if t == 0 else mbband[:]), (mbfirstr if t == 0 else mbbandr),
                         [M.KTr[t], M.KTr[t + 1]])
                s_sw = P.rec_end()
                strs = [s_ml, s_sw]
                P.rec_begin(); bset[0] = [4]
                if pend is not None:
                    wout_tile(*pend)
                if ti == 0 and g0 + GT < NTP:
                    prep_main(g0 + GT, pb ^ 1, True)
                    use(pb)
                strs.append(P.rec_end())
                P.merge(strs)
                pend = (ti, t)
            bset[0] = [0, 1, 2, 3, 4]
            wout_tile(*pend)
            if debug and g0 == DBG_G0:
                dA = dout("dbg_att", [128, 4, M.NG]); dH = dout("dbg_hm", [128, 4, M.NG])
                P.dma("pool", dA, M.ATTT[:, :, :], M.ATTTr[0], reads=M.ATTTr)
                P.dma("pool", dH, M.HMT[:, :, :], M.HMTr[0], reads=M.HMTr)
            gates_carry(M.NG)

        CO = A([4, 64], F32); COr = AR("CO")
        for h in range(4):
            bk, bkr = bank()
            mm(bk[:, 0:64], Cst[:, h, 0:128], identf[0:64, 0:64], True, True, [Cstr[h], identfr], bkr)
            I("act", "activation", [bkr], [COr], out=CO[:, h, :], in_=bk[:, 0:64], func=AF.Copy)
        P.dma("sp", Cp_o.rearrange("h p k -> p h k"), CO[:, :, :], COr, reads=[COr])
        for h in range(4):
            P.dma("sp", np_o[h, :].rearrange("(k o) -> k o", o=1), Cst[:, h, 128:129], Cstr[h], reads=[Cstr[h]], allow_slow_non_contiguous=True)
        P.dma("sp", mp_o, M.G_BM[:, M.NG - 1:M.NG], M.G_BMr, reads=[M.G_BMr], allow_slow_non_contiguous=True)

        new_phase()
        MA = M
        M = alloc_mixer(1, 1, 1)
        R0_olds = [M.WQr, M.WTOKr, M.WMQr, M.WOGr, M.WGTr]
        TS = NTP
        P.dma("sp", Y[:, TS, :], xs_d, Yr[TS], writes=[Yr[TS]])
        shk = P.res("shk"); shv = P.res("shv")
        P.dma("sp", sks_o[:, 0:120, :], csk_d[:, 8:128, :], shk, writes=[shk])
        P.dma("sp", svs_o[:, 0:120, :], csv_d[:, 8:128, :], shv, writes=[shv])
        CKn = A([16, 128], BF16); CKnr = AR("CKn")
        CV = A([16, 128], BF16); CVr = AR("CV")
        CKT = A([16, 128], BF16, parts=64); CKTr = AR("CKT")
        SMC = A([1, 128], BF16, parts=32)[:, 0, :]; SMCr = AR("SMC")
        SMN = A([16, 128], BF16, parts=32); SMNr = AR("SMN")
        SINKC = A([1, 4], F32, parts=32)[:, 0, :]; SINKCr = AR("SINKC")
        mbcs = A([1, 128], BF16)[:, 0, :]; mbcsr = AR("mbcs")
        PNs = A([4, 256], BF16, parts=32); PNsr = [AR(f"PNs{i}") for i in range(4)]
        sms = A([4, 8], F32, parts=32); smsr = [AR(f"sms{i}") for i in range(4)]
        PTS = A([1, 1024], BF16)[:, 0, :]; PTSr = AR("PTS")
        M0 = A([1, 16], F32, parts=4)[:, 0, :]; M0r = AR("M0")
        MTe = A([1, 128], F32, parts=4)[:, 0, :]; MTer = AR("MTe")
        DMT = A([1, 16], F32, parts=4)[:, 0, :]; DMTr = AR("DMT")
        E16 = A([1, 16], F32)[:, 0, :]; E16r = AR("E16")
        EW = A([4, 16], BF16); EWr = AR("EW")
        WCB = A([4, 16], F32); WCBr = AR("WCB")
        SNn = A([1, 64], F32, parts=64)[:, 0, :]; SNnr = AR("SNn")
        SNT = A([1, 64], F32, parts=64)[:, 0, :]; SNTr = AR("SNT")
        NNT = A([1, 64], F32, parts=64)[:, 0, :]; NNTr = AR("NNT")
        NNo = A([1, 64], F32, parts=64)[:, 0, :]; NNor = AR("NNo")
        BTf = A([1, 128], F32)[:, 0, :]; BTfr = AR("BTf")
        P.dma("pool", CKn[:, :, :], csk_d.rearrange("j p c -> p j c"), CKnr, writes=[CKnr])
        P.dma("pool", CV[:, :, :], csv_d.rearrange("j p c -> p j c"), CVr, writes=[CVr])
        P.dma("pool", SMC, smc_d, SMCr, writes=[SMCr])
        P.dma("pool", SMN[:, :, :], smn_d, SMNr, writes=[SMNr])
        P.dma("sp", SINKC[:, 0:2], sinkcol_d, SINKCr, writes=[SINKCr])
        I("dve", "tensor_scalar", [SINKCr], [SINKCr], out=SINKC[:, 2:4], in0=SINKC[:, 0:2], scalar1=-1.0, scalar2=None, op0=ALU.mult)
        P.dma("pool", mbcs, mb_causs_d, mbcsr, writes=[mbcsr])
        P.dma("sp", M0, sm_d.rearrange("j h -> h j"), M0r, writes=[M0r], allow_slow_non_contiguous=True)
        P.dma("sp", E16, eseq_d, E16r, writes=[E16r])
        P.dma("sp", SNn, sn_d.rearrange("j h k -> (j h) k"), SNnr, writes=[SNnr])

        norm_T(Y[:, TS, :], Yr[TS], 0, M.XNTg[:, :, 0:128], [M.XNTgr[0]])
        tok_major(0, slice(0, 128), M.XNTgr[0], 0, want_kv_out="sample")
        gates(128, M.XNTgr, "sample")
        feat64(M.WK, M.WKr, 2, M.KT, [M.KTr[0]], 128, M.XNTgr, dcol0=0)
        feat64(M.WQ, M.WQr, 8, M.QT, [M.QTr], 128, M.XNTgr)
        feat64(M.WMQ, M.WMQr, 4, M.MQT, [M.MQTr], 128, M.XNTgr)
        feat64(M.WMK, M.WMKr, 4, M.MKT, [M.MKTr], 128, M.XNTgr, scale=0.125)
        for h0 in (0, 2):
            bk, bkr = bank()
            for hh in range(2):
                h = h0 + hh
                for k in range(8):
                    mm(bk[:, hh * 256:hh * 256 + 128], M.WOG[:, k, h * 128:(h + 1) * 128], M.XNTg[:, k, 0:128], k == 0, k == 7,
                       [M.WOGr] + M.XNTgr, bkr)
            sgv = M.SGT[:, h0:h0 + 2, :]
            I("act", "activation", [bkr], [M.SGTr], out=sgv, in_=bk[:, :].rearrange("p (a b) -> p a b", a=2)[:, :, 0:128],
              func=AF.Exp, scale=-1.0)
            I("act", "activation", [M.SGTr], [M.SGTr], out=sgv, in_=sgv, func=AF.Ln, bias=1.0)
            I("act", "activation", [M.SGTr], [M.SGTr], out=sgv, in_=sgv, func=AF.Exp, scale=-1.0)
        for j in range(16):
            I("dve", "tensor_tensor_scan", [M.Gr, M.G_Br, onesfr], [M.G_Br], out=M.G_B[:, 1 + 8 * j:9 + 8 * j],
              data0=onesf[0:4, 0:8], data1=M.G_L1[:, 8 * j:8 * j + 8], initial=0.0, op0=ALU.mult, op1=ALU.subtract)
        I("dve", "tensor_tensor", [M.Gr, M.G_Br], [M.G_Ar], out=M.G_A[:, 0:128], in0=M.G_IG[:, 1:129], in1=M.G_B[:, 1:129],
          op=ALU.subtract)
        for j in range(16):
            I("dve", "tensor_tensor_scan", [M.G_Ar, M.G_Mr, onesfr, M0r], [M.G_Mr], out=M.G_M[:, 1 + 8 * j:9 + 8 * j],
              data0=onesf[0:4, 0:8], data1=M.G_A[:, 8 * j:8 * j + 8], initial=M0[:, j:j + 1], op0=ALU.mult, op1=ALU.max)
        I("dve", "tensor_tensor", [M.G_Br, M.G_Mr], [M.G_BMr], out=M.G_BM[:, 0:128], in0=M.G_B[:, 1:129], in1=M.G_M[:, 1:129],
          op=ALU.add)
        GM3 = M.G_M[:, 1:129].rearrange("p (j i) -> p j i", i=8)
        I("dve", "tensor_tensor", [M.G_Mr, M0r], [M.G_DMr], out=M.G_DM[:, 0:128].rearrange("p (j i) -> p j i", i=8), in0=GM3,
          in1=M0[:, :].unsqueeze(2).broadcast_to([4, 16, 8]), op=ALU.subtract)
        I("dve", "tensor_copy", [M.G_Mr], [MTer], out=MTe[:, :].rearrange("p (j i) -> p j i", i=8),
          in_=GM3[:, :, 7:8].broadcast_to([4, 16, 8]))
        I("dve", "tensor_tensor", [M.G_Mr, M0r], [DMTr], out=DMT[:, :].unsqueeze(2), in0=GM3[:, :, 7:8], in1=M0[:, :].unsqueeze(2),
          op=ALU.subtract)
        P.dma("sp", ms_o.rearrange("j h -> h j"), M.G_BM[:, 0:128].rearrange("p (j i) -> p j i", i=8)[:, :, 7], M.G_BMr,
              reads=[M.G_BMr], allow_slow_non_contiguous=True)

        pair_i = [0]
        QS = A([2, 16, 32], BF16, parts=64); QSr = AR("QS")
        for h in range(2):
            for par in range(2):
                I("act", "activation", [M.QTr], [QSr],
                  out=QS[:, h, :, par * 16:(par + 1) * 16].rearrange("p j (gp i) -> p j gp i", i=8),
                  in_=M.QT[:, 4 * h:4 * h + 4, :].rearrange("p (gp two) t -> p two gp t", two=2)[:, par].rearrange(
                      "p gp (j i) -> p j gp i", i=8), func=AF.Copy)
        for h in range(2):
            for q4 in range(2):
                for jj in range(8):
                    j = q4 * 8 + jj
                    I("pe", "transpose", [CKnr, identr], [*tbhr], out=tb[0:64, jj * 128:(jj + 1) * 128],
                      in_=CKn[:, j, h * 64:(h + 1) * 64], identity=identb[:])
                I("act", "activation", [*tbhr], [CKTr], out=CKT[:, q4 * 8:(q4 + 1) * 8, :],
                  in_=tb[0:64, :].rearrange("p (a b) -> p a b", a=8), func=AF.Copy)
            for half in range(2):
                po, por = banks[4], bres[4]
                strs = []
                for sk in range(4):
                    P.rec_begin(); bset[0] = [sk]
                    for jj in (sk, sk + 4):
                        j = half * 8 + jj
                        b = sk
                        bk, bkr = bank()
                        lq = QS[:, h, j, :]
                        mm(bk[0:32, 0:128], lq, CKT[:, j, :], True, False, [QSr, CKTr], bkr)
                        mm(bk[0:32, 0:128], identb[0:32, 0:32], SMC, False, True, [identr, SMCr], bkr)
                        mm(bk[0:32, 128:256], lq, M.KT[:, h, 0:128], True, False, [QSr, M.KTr[0]], bkr)
                        mm(bk[0:32, 128:256], identb[0:32, 0:32], SMN[:, j, :], False, True, [identr, SMNr], bkr)
                        st_ = sms[:, b, :]
                        I("dve", "reduce_max", [bkr], [smsr[b]], out=st_[:, 0:1], in_=bk[0:32, 0:256], axis=AX.X)
                        I("dve", "tensor_scalar", [smsr[b]], [smsr[b]], out=st_[:, 0:1], in0=st_[:, 0:1], scalar1=-0.125,
                          scalar2=None, op0=ALU.mult)
                        I("dve", "tensor_tensor", [smsr[b], SINKCr], [smsr[b]], out=st_[:, 0:1], in0=st_[:, 0:1],
                          in1=SINKC[:, 2 + h:3 + h], op=ALU.min)
                        I("act", "activation", [bkr, smsr[b]], [PNsr[b], smsr[b]], out=PNs[:, b, :], in_=bk[0:32, 0:256],
                          func=AF.Exp, bias=st_[:, 0:1], scale=0.125, accum_out=st_[:, 1:2])
                        I("act", "activation", [SINKCr, smsr[b]], [smsr[b]], out=st_[:, 2:3], in_=SINKC[:, h:h + 1], func=AF.Exp,
                          bias=st_[:, 0:1])
                        I("dve", "tensor_tensor", [smsr[b]], [smsr[b]], out=st_[:, 2:3], in0=st_[:, 2:3], in1=st_[:, 1:2],
                          op=ALU.add)
                        I("dve", "reciprocal", [smsr[b]], [smsr[b]], out=st_[:, 3:4], in_=st_[:, 2:3])
                        I("dve", "tensor_scalar", [PNsr[b], smsr[b]], [PNsr[b]], out=PNs[:, b, :], in0=PNs[:, b, :],
                          scalar1=st_[:, 3:4], scalar2=None, op0=ALU.mult)
                        for c2 in range(2):
                            I("pe", "transpose", [PNsr[b], identr], [*tbhr], out=tb[:, jj * 64 + c2 * 32:jj * 64 + (c2 + 1) * 32],
                              in_=PNs[:, b, c2 * 128:(c2 + 1) * 128], identity=identb[0:32, 0:32])
                    strs.append(P.rec_end())
                P.merge(strs)
                bset[0] = [0, 1, 2, 3]
                I("dve", "tensor_copy", [*tbhr], [PTSr], out=PTS[:, 0:512], in_=tb[:, 0:512])
                for jj in range(8):
                    j = half * 8 + jj
                    for par in range(2):
                        o = po[par * 64:(par + 1) * 64, jj * 16:(jj + 1) * 16]
                        mm(o, CV[:, j, h * 64:(h + 1) * 64], PTS[:, jj * 64 + par * 16:jj * 64 + par * 16 + 16], True, False,
                           [CVr, PTSr], por)
                        mm(o, M.Vt[:, 0, h * 64:(h + 1) * 64], PTS[:, jj * 64 + 32 + par * 16:jj * 64 + 32 + par * 16 + 16], False,
                           True, [M.Vtr[0], PTSr], por)
                I("act", "activation", [por], [M.ATTTr[0]],
                  out=M.ATTT[:, 2 * h:2 * h + 2, half * 64:(half + 1) * 64].rearrange("p c (j i) -> p j c i", i=8),
                  in_=po[:, 0:128].rearrange("p (j c i) -> p j c i", c=2, i=8), func=AF.Copy)

        bset[0] = [0, 1, 2, 3, 4]
        off = 0
        SCf, off = A_at(off, [64, 64], F32); SCfr = ARalias("SCf", R0_olds)
        SCT, off = A_at(off, [64, 128], BF16, parts=64); SCTr = ARalias("SCT", R0_olds)
        QN, off = A_at(off, [4, 128], BF16, parts=64); QNr = ARalias("QN", R0_olds)
        KJ, off = A_at(off, [16, 64], BF16); KJr = ARalias("KJ", R0_olds)
        assert off <= 18496
        P.dma("sp", SCf[:, :, :], sC_d.rearrange("j h p k -> p (j h) k"), SCfr, writes=[SCfr])
        for p4 in range(16):
            bk, bkr = bank()
            for q_ in range(4):
                pr = p4 * 4 + q_
                mm(bk[0:64, q_ * 128:(q_ + 1) * 128], SCf[:, pr, :], identf[:], True, True, [SCfr, identfr], bkr)
            I("act", "activation", [bkr], [SCTr], out=SCT[:, p4 * 4:(p4 + 1) * 4, :],
              in_=bk[0:64, :].rearrange("p (a b) -> p a b", a=4), func=AF.Copy)
        bk, bkr = bank()
        mm(bk[0:64, 0:64], SNn, identf[0:64, 0:64], True, True, [SNnr, identfr], bkr)
        I("act", "activation", [bkr], [SNTr], out=SNT, in_=bk[0:64, 0:64], func=AF.Copy)

        def sample_inter(kind, h=None, pb=None, pbr=None):
            if kind == "pre":
                I("dve", "tensor_tensor", [M.QWr, SNTr], [QNr], out=QN[:, :, :].rearrange("p h (j i) -> p h j i", i=8),
                  in0=M.QW[:, :, :].rearrange("p h (j i) -> p h j i", i=8),
                  in1=SNT.rearrange("p (j h) -> p h j", h=4).unsqueeze(3).broadcast_to([64, 4, 16, 8]), op=ALU.mult)
            elif kind == "num":
                for j in range(16):
                    mm(pb[:, h * 128 + 8 * j:h * 128 + 8 * j + 8], SCT[:, j * 4 + h, :], M.QW[:, h, 8 * j:8 * j + 8], False, j == 15,
                       [SCTr, M.QWr], pbr)
            else:
                mm(pb[:, h * 128:(h + 1) * 128], onesb[0:64, :], QN[:, h, :], False, True, [onesbr, QNr], pbr)

        pn = mlstm_chunk(0, 0, mbcs, mbcsr, inter_fn=sample_inter)
        mlstm_finish(*pn, 0, M.HMTr[0])
        wout_tile(0, TS)

        pw, pwr = bank()
        for h in range(4):
            mm(pw[:, h:h + 1], M.G_A[:, 0:128], SELh(h, 1), True, False, [M.G_Ar, selr], pwr)
            mm(pw[:, h:h + 1], MTe, SEL[:, 512 + h * 128:512 + h * 128 + 1], False, True, [MTer, selr], pwr)
        for h in range(4):
            mm(pw[:, 8 + 16 * h:8 + 16 * (h + 1)], NSELh(h), DMT, True, True, [DMTr, selr], pwr)
        I("act", "activation", [pwr], [M.WKCr], out=M.WKC[:, 0:4], in_=pw[:, 0:4], func=AF.Exp)
        I("act", "activation", [pwr], [WCBr], out=WCB[:, :, :], in_=pw[:, 8:72].rearrange("p (h j) -> p h j", h=4), func=AF.Exp)
        for h in range(4):
            I("dve", "tensor_scalar", [M.MVaugr[0], M.WKCr], [M.VWr[h]], out=M.VW[:, h, :], in0=M.MVaug[:, 0, h, :],
              scalar1=M.WKC[:, h:h + 1], scalar2=None, op0=ALU.mult)
            I("dve", "tensor_scalar", [E16r, M.WKCr], [EWr], out=EW[:, h, :], in0=E16, scalar1=M.WKC[:, h:h + 1], scalar2=None,
              op0=ALU.mult)
        bk, bkr = bank()
        for h in range(4):
            mm(bk[0:64, h * 16:(h + 1) * 16], M.MKtok[:, 0, h * 64:(h + 1) * 64], EW[:, h, :], True, True, [M.MKtokr[0], EWr], bkr)
        I("dve", "tensor_tensor", [SNTr, WCBr], [NNTr], out=NNT.rearrange("p (j h) -> p h j", h=4),
          in0=SNT.rearrange("p (j h) -> p h j", h=4), in1=WCB[0:64, :, :], op=ALU.mult)
        I("dve", "tensor_tensor", [NNTr, bkr], [NNTr], out=NNT.rearrange("p (j h) -> p h j", h=4),
          in0=NNT.rearrange("p (j h) -> p h j", h=4), in1=bk[0:64, 0:64].rearrange("p (h j) -> p h j", h=4), op=ALU.add)
        bk2, bk2r = bank()
        mm(bk2[0:64, 0:64], NNT, identf[0:64, 0:64], True, True, [NNTr, identfr], bk2r)
        I("act", "activation", [bk2r], [NNor], out=NNo, in_=bk2[0:64, 0:64], func=AF.Copy)
        P.dma("sp", ns_o.rearrange("j h k -> (j h) k"), NNo, NNor, reads=[NNor])
        for h in range(4):
            I("dve", "tensor_tensor", [M.MKtokr[0], E16r], [KJr], out=KJ[:, :, :],
              in0=M.MKtok[:, 0, h * 64:(h + 1) * 64].unsqueeze(1).broadcast_to([128, 16, 64]),
              in1=E16.unsqueeze(2).broadcast_to([128, 16, 64]), op=ALU.mult)
            for half in range(2):
                bk, bkr = bank()
                mm(bk[:, :], M.VW[:, h, 0:128], KJ[:, half * 8:(half + 1) * 8, :], True, True, [M.VWr[h], KJr], bkr)
                scv = SCf[:, :, :].rearrange("p (j h) k -> p h j k", h=4)[:, h, half * 8:(half + 1) * 8, :]
                I("dve", "tensor_tensor", [SCfr, WCBr], [SCfr], out=scv, in0=scv,
                  in1=WCB[:, h, half * 8:(half + 1) * 8].unsqueeze(2).broadcast_to([128, 8, 64]), op=ALU.mult)
                I("dve", "tensor_tensor", [SCfr, bkr], [SCfr], out=scv, in0=scv,
                  in1=bk[:, :].rearrange("p (j k) -> p j k", k=64), op=ALU.add)
        P.dma("sp", Cs_o.rearrange("j h p k -> p (j h) k"), SCf[:, :, :], SCfr, reads=[SCfr])

        def dump_y(tiles):
            yo = yp_o.rearrange("(t p) d -> t p d", p=128)
            for t in tiles:
                if Yr[t].last_w is None:
                    continue
                if t < NTP:
                    P.dma("sp", yo[t], Y[:, t, :], Yr[t], reads=[Yr[t]])
                else:
                    P.dma("sp", ys_o, Y[:, t, :], Yr[t], reads=[Yr[t]])

        if stage <= 1:
            dump_y(range(NT))
            P.emit()
            return nc, P

        new_phase()
        WCQ = A([8, 256], BF16); WCQr = AR("WCQ")
        WCKV = A([8, 512], BF16); WCKVr = AR("WCKV")
        WCO = A([2, 1024], BF16); WCOr = AR("WCO")
        P.dma("pool", WCQ[:], w_cq_d.rearrange("(k p) n -> p k n", p=128), WCQr, writes=[WCQr])
        P.dma("pool", WCKV[:, :, 0:256], w_ck_d.rearrange("(k p) n -> p k n", p=128), WCKVr, writes=[WCKVr], group=True)
        P.dma("pool", WCKV[:, :, 256:512], w_cv_d.rearrange("(k p) n -> p k n", p=128), WCKVr, writes=[WCKVr], group=True)
        P.dma("pool", WCO[:], w_co_d.rearrange("(c p) n -> p c n", p=128), WCOr, writes=[WCOr])
        MEMX = A([2, D], F32); MEMXr = [AR("MEMX0"), AR("MEMX1")]
        MNT = A([8, 256], BF16); MNTr = [AR("MNT0"), AR("MNT1")]
        MKTm = A([4, 256], BF16, parts=64); MKTmr = AR("MKTm")
        MVm = A([2, 256], BF16); MVmr = AR("MVm")
        MKVo = A([2, 512], F32); MKVor = [AR("MKVo0"), AR("MKVo1")]
        GB = 4
        XNTb = A([8, GB * 128], BF16); XNTbr = [AR(f"XNTb{i}") for i in range(GB)]
        QcT = A([4, GB * 128], BF16, parts=64); QcTr = AR("QcT")
        OcT = A([2, GB * 128], BF16); OcTr = [AR(f"OcT{i}") for i in range(GB)]
        Eb2s = [A([4, 256], BF16) for _ in range(2)]; Eb2rs = [AR("Eb2a"), AR("Eb2b")]
        PT2s = [A([1, 1024], BF16) for _ in range(2)]; PT2rs = [AR("PT2a"), AR("PT2b")]
        sm2s = [A([1, 32], F32)[:, 0, :] for _ in range(2)]; sm2rs = [AR("sm2a"), AR("sm2b")]

        mem_t = mem_d.rearrange("(t p) d -> t p d", p=128)
        import os
        SK = os.environ.get("SKIP", "")
        for mt in range(2):
            P.dma("sp", MEMX[:, mt, :], mem_t[mt], MEMXr[mt], writes=[MEMXr[mt]])
        for mt in (range(2) if "noBnorm" not in SK else []):
            norm_T(MEMX[:, mt, :], MEMXr[mt], 2, MNT[:, :, mt * 128:(mt + 1) * 128], [MNTr[mt]])
        for mt in (range(2) if "noBkv" not in SK else []):
            bk, bkr = bank()
            for k in range(8):
                mm(bk[:, :], MNT[:, k, mt * 128:(mt + 1) * 128], WCKV[:, k, :], k == 0, k == 7, [MNTr[mt], WCKVr], bkr)
            if "noBcp1" not in SK:
                I("act", "activation", [bkr], [MKVor[mt]], out=MKVo[:, mt, :], in_=bk[:, :], func=AF.Copy)
            if "noBcp2" not in SK:
                I("act", "activation", [bkr], [MVmr], out=MVm[:, mt, :], in_=bk[:, 256:512], func=AF.Copy)
            if "noBdma" not in SK:
                P.dma("sp", memk_o[mt * 128:(mt + 1) * 128, :], MKVo[:, mt, 0:256], MKVor[mt], reads=[MKVor[mt]], group=True)
                P.dma("sp", memv_o[mt * 128:(mt + 1) * 128, :], MKVo[:, mt, 256:512], MKVor[mt], reads=[MKVor[mt]], group=True)
        for h0 in ((0, 2) if "noBkt" not in SK else []):
            bk, bkr = bank()
            for hh in range(2):
                h = h0 + hh
                for k in range(8):
                    mm(bk[0:64, hh * 256:(hh + 1) * 256], WCKV[:, k, h * 64:(h + 1) * 64], MNT[:, k, :], k == 0, k == 7,
                       [WCKVr] + MNTr, bkr)
            I("act", "activation", [bkr], [MKTmr], out=MKTm[:, h0:h0 + 2, :],
              in_=bk[0:64, :].rearrange("p (a b) -> p a b", a=2), func=AF.Copy)

        def cross_q(ntok, xres):
            for h in range(4):
                bk, bkr = bank()
                for k in range(8):
                    mm(bk[0:64, 0:ntok], WCQ[:, k, h * 64:(h + 1) * 64], XNTb[:, k, 0:ntok], k == 0, k == 7, [WCQr] + xres, bkr)
                I("act", "activation", [bkr], [QcTr], out=QcT[:, h, 0:ntok], in_=bk[0:64, 0:ntok], func=AF.Copy)

        def cross_tile_prompt(ti, sx):
            Eb2, Eb2r, PT2, PT2r, sm2, sm2r = Eb2s[sx], Eb2rs[sx], PT2s[sx], PT2rs[sx], sm2s[sx], sm2rs[sx]
            tbx, tbxr = TSEL[sx]
            qs = slice(ti * 128, (ti + 1) * 128)
            bks = [bank(), bank()]
            for h in range(4):
                bk, bkr = bks[h // 2]
                mm(bk[:, (h % 2) * 256:(h % 2 + 1) * 256], QcT[:, h, qs], MKTm[:, h, :], True, True, [QcTr, MKTmr], bkr)
            for j in range(2):
                I("dve", "reduce_max", [bks[j][1]], [sm2r], out=sm2[:, 2 * j:2 * j + 2],
                  in_=bks[j][0][:, :].rearrange("p (a b) -> p a b", a=2), axis=AX.X)
            I("dve", "tensor_scalar", [sm2r], [sm2r], out=sm2[:, 0:4], in0=sm2[:, 0:4], scalar1=-0.125, scalar2=None, op0=ALU.mult)
            for h in range(4):
                bk, bkr = bks[h // 2]
                I("act", "activation", [bkr, sm2r], [Eb2r, sm2r], out=Eb2[:, h, :], in_=bk[:, (h % 2) * 256:(h % 2 + 1) * 256],
                  func=AF.Exp, bias=sm2[:, h:h + 1], scale=0.125, accum_out=sm2[:, 4 + h:5 + h])
            I("dve", "reciprocal", [sm2r], [sm2r], out=sm2[:, 8:12], in_=sm2[:, 4:8])
            for h in range(4):
                if h % 2 == 0:
                    I("act", "activation", [Eb2r, sm2r], [Eb2r], out=Eb2[:, h, :], in_=Eb2[:, h, :], func=AF.Copy,
                      scale=sm2[:, 8 + h:9 + h])
                else:
                    I("dve", "tensor_scalar", [Eb2r, sm2r], [Eb2r], out=Eb2[:, h, :], in0=Eb2[:, h, :],
                      scalar1=sm2[:, 8 + h:9 + h], scalar2=None, op0=ALU.mult)
            po, por = bank()
            for mc in range(2):
                for h in range(4):
                    I("pe", "transpose", [Eb2r, identr], [tbxr], out=tbx[:, h * 128:(h + 1) * 128],
                      in_=Eb2[:, h, mc * 128:(mc + 1) * 128], identity=identb[:])
                if mc == 0:
                    I("dve", "tensor_copy", [tbxr], [PT2r], out=PT2[:, 0, 0:512], in_=tbx)
                else:
                    I("act", "activation", [tbxr], [PT2r], out=PT2[:, 0, 512:1024], in_=tbx, func=AF.Copy)
            for h in range(4):
                for mc in range(2):
                    mm(po[(h % 2) * 64:(h % 2 + 1) * 64, (h // 2) * 128:(h // 2 + 1) * 128], MVm[:, mc, h * 64:(h + 1) * 64],
                       PT2[:, 0, (mc * 4 + h) * 128:(mc * 4 + h + 1) * 128], mc == 0, mc == 1, [MVmr, PT2r], por)
            I("act", "activation", [por], [OcTr[ti]], out=OcT[:, :, qs], in_=po[:, 0:256].rearrange("p (h q) -> p h q", h=2),
              func=AF.Copy)

        def wco_tile(ti, t):
            qs = slice(ti * 128, (ti + 1) * 128)
            for c in range(2):
                bk, bkr = bank()
                cc = slice(c * 512, (c + 1) * 512)
                for h in range(2):
                    mm(bk[:, :], OcT[:, h, qs], WCO[:, h, cc], h == 0, h == 1, [OcTr[ti], WCOr], bkr)
                I("dve", "tensor_tensor", [Yr[t], bkr], [Yr[t]], out=Y[:, t, cc], in0=Y[:, t, cc], in1=bk[:, :], op=ALU.add)

        import os
        for g0 in (range(0, NTP, GB) if "noBloop" not in os.environ.get("SKIP", "") else []):
            for tp_ in range(0, GB, 2):
                strs = []
                for sx in range(2):
                    ti = tp_ + sx
                    P.rec_begin()
                    norm_T(Y[:, g0 + ti, :], Yr[g0 + ti], 1, XNTb[:, :, ti * 128:(ti + 1) * 128], [XNTbr[ti]], half=sx,
                           tsel=TSEL[sx])
                    strs.append(P.rec_end())
                P.merge(strs)
            cross_q(GB * 128, XNTbr)
            for tp_ in range(0, GB, 2):
                strs = []
                for sx in range(2):
                    P.rec_begin(); bset[0] = [0, 1, 2] if sx == 0 else [3, 4, 5]
                    cross_tile_prompt(tp_ + sx, sx)
                    wco_tile(tp_ + sx, g0 + tp_ + sx)
                    strs.append(P.rec_end())
                P.merge(strs)
            bset[0] = [0, 1, 2, 3, 4]
        TS = NTP
        CMn = A([8, 2, 256], BF16); CMnr = AR("CMn")
        CMV = A([16, 2, 256], BF16); CMVr = AR("CMV")
        CMKT = A([8, 4, 256], BF16, parts=64); CMKTr = AR("CMKT")
        Es = A([2, 4, 256], BF16, parts=8); Esr = [AR("Es0"), AR("Es1")]
        sm3 = A([2, 16], F32, parts=8); sm3r = [AR("sm30"), AR("sm31")]
        PT3 = A([1, 1024], BF16)[:, 0, :]; PT3r = AR("PT3")
        P.dma("pool", CMV[:, :, :, :], cmv_d.rearrange("j (c p) f -> p j c f", p=128), CMVr, writes=[CMVr])
        norm_T(Y[:, TS, :], Yr[TS], 1, XNTb[:, :, 0:128], [XNTbr[0]])
        cross_q(128, [XNTbr[0]])
        po3, po3r = banks[5], bres[5]
        for half in range(2):
            P.dma("pool", CMn[:, :, :, :], cmk_d[half * 8:(half + 1) * 8].rearrange("j (c p) f -> p j c f", p=128), CMnr,
                  writes=[CMnr])
            for jj in range(8):
                for h in range(4):
                    for mc in range(2):
                        I("pe", "transpose", [CMnr, identr], [*tbhr], out=tb[0:64, (h * 2 + mc) * 128:(h * 2 + mc + 1) * 128],
                          in_=CMn[:, jj, mc, h * 64:(h + 1) * 64], identity=identb[:])
                I("act", "activation", [*tbhr], [CMKTr], out=CMKT[:, jj, :, :],
                  in_=tb[0:64, :].rearrange("p (h m) -> p h m", h=4), func=AF.Copy)
            strs = []
            for sx in range(2):
                P.rec_begin(); bset[0] = [0, 1] if sx == 0 else [2, 3]
                for jj in range(sx, 8, 2):
                    j = half * 8 + jj
                    b = sx
                    bks = [bank(), bank()]
                    for h in range(4):
                        bk, bkr = bks[h // 2]
                        mm(bk[0:8, (h % 2) * 256:(h % 2 + 1) * 256], QcT[:, h, 8 * j:8 * j + 8], CMKT[:, jj, h, :], True, True,
                           [QcTr, CMKTr], bkr)
                    st_ = sm3[:, b, :]
                    for q_ in range(2):
                        I("dve", "reduce_max", [bks[q_][1]], [sm3r[b]], out=st_[:, 2 * q_:2 * q_ + 2],
                          in_=bks[q_][0][0:8, :].rearrange("p (a b) -> p a b", a=2), axis=AX.X)
                    I("dve", "tensor_scalar", [sm3r[b]], [sm3r[b]], out=st_[:, 0:4], in0=st_[:, 0:4], scalar1=-0.125, scalar2=None,
                      op0=ALU.mult)
                    for h in range(4):
                        bk, bkr = bks[h // 2]
                        I("act", "activation", [bkr, sm3r[b]], [Esr[b], sm3r[b]], out=Es[:, b, h, :],
                          in_=bk[0:8, (h % 2) * 256:(h % 2 + 1) * 256], func=AF.Exp, bias=st_[:, h:h + 1], scale=0.125,
                          accum_out=st_[:, 4 + h:5 + h])
                    I("dve", "reciprocal", [sm3r[b]], [sm3r[b]], out=st_[:, 8:12], in_=st_[:, 4:8])
                    I("dve", "tensor_tensor", [Esr[b], sm3r[b]], [Esr[b]], out=Es[:, b, :, :], in0=Es[:, b, :, :],
                      in1=st_[:, 8:12].unsqueeze(2).broadcast_to([8, 4, 256]), op=ALU.mult)
                    for mc in range(2):
                        for h in range(4):
                            c0_ = j * 64 + (mc * 4 + h) * 8
                            I("pe", "transpose", [Esr[b], identr], [*tbhr], out=tb[:, c0_:c0_ + 8],
                              in_=Es[:, b, h, mc * 128:(mc + 1) * 128], identity=identb[0:8, 0:8])

                strs.append(P.rec_end())
            P.merge(strs)
            bset[0] = [0, 1, 2, 3, 4]
            I("dve", "tensor_copy", [*tbhr], [PT3r], out=PT3[:, half * 512:(half + 1) * 512], in_=tb[:, half * 512:(half + 1) * 512])
        for j in range(16):
            for h in range(4):
                for mc in range(2):
                    c0_ = j * 64 + (mc * 4 + h) * 8
                    mm(po3[(h % 2) * 64:(h % 2 + 1) * 64, (j * 2 + h // 2) * 8:(j * 2 + h // 2) * 8 + 8],
                       CMV[:, j, mc, h * 64:(h + 1) * 64], PT3[:, c0_:c0_ + 8], mc == 0, mc == 1, [CMVr, PT3r], po3r)
        I("act", "activation", [po3r], [OcTr[0]], out=OcT[:, :, 0:128].rearrange("p c (j i) -> p j c i", i=8),
          in_=po3[:, 0:256].rearrange("p (j c i) -> p j c i", c=2, i=8), func=AF.Copy)
        wco_tile(0, TS)

        if stage <= 2:
            dump_y(range(NT))
            P.emit()
            return nc, P

        new_phase()
        USE_SQRT[0] = True
        NF = FH // 128
        XNTa = A([8, NT * 128], BF16); XNTar = [AR(f"XNTa{t}") for t in range(NT)]
        NSLOT = 12
        WG = A([NSLOT, 8, 128], BF16); WU = A([NSLOT, 8, 128], BF16); WD = A([NSLOT, D], BF16)
        Wsr = [AR(f"Ws{s_}") for s_ in range(NSLOT)]
        Hh = A([6, 512], BF16); Hr = [AR(f"H{j}") for j in range(6)]
        SG = A([2, 512], BF16); SGr = [AR("SG0"), AR("SG1")]
        OUT = A([1, D], F32)[:, 0, :]; OUTr = AR("OUT")
        gfin = A([1, D], F32)[:, 0, :]; gfinr = AR("gfin")
        P.dma("sp", gfin, g_final_d.partition_broadcast(128), gfinr, writes=[gfinr])
        passes = [list(range(0, 6)), list(range(6, 12)), list(range(12, 17)), list(range(17, 22))]
        groups = [(0, 4), (4, 4), (8, 4), (12, 4), (16, 1)]
        wd_v = w_down_d.rearrange("(f p) n -> f p n", p=128)
        wslot = {}
        nload = [0]

        def load_w(f):
            s_ = nload[0] % NSLOT
            nload[0] += 1
            wslot[f] = s_
            P.dma("pool", WG[:, s_], w_gate_d[:, f * 128:(f + 1) * 128].rearrange("(k p) n -> p k n", p=128), Wsr[s_],
                  writes=[Wsr[s_]], group=True)
            P.dma("pool", WU[:, s_], w_up_d[:, f * 128:(f + 1) * 128].rearrange("(k p) n -> p k n", p=128), Wsr[s_],
                  writes=[Wsr[s_]], group=True)
            P.dma("pool", WD[:, s_], wd_v[f], Wsr[s_], writes=[Wsr[s_]], group=True)

        for f in passes[0]:
            load_w(f)
        gcnt = [0]
        def ffn_norm_group(gi_):
            t0_, n_ = groups[gi_]
            for t in range(t0_, t0_ + n_):
                norm_T(Y[:, t, :], Yr[t], 3, XNTa[:, :, t * 128:(t + 1) * 128], [XNTar[t]], half=t % 2, tsel=TSEL[t % 2])

        ffn_norm_group(0)
        for pi, fl in enumerate(passes):
            for gi, (t0, n) in enumerate(groups):
                if pi + 1 < len(passes) and gi == 0:
                    for f in passes[pi + 1]:
                        load_w(f)
                merging = (pi == 0 and gi + 1 < len(groups))
                if merging:
                    P.rec_begin()
                    ffn_norm_group(gi + 1)
                    s_norm = P.rec_end()
                    P.rec_begin()
                ntok = n * 128
                tok = slice(t0 * 128, t0 * 128 + ntok)
                xr = [XNTar[t] for t in range(t0, t0 + n)]
                for j, f in enumerate(fl):
                    s_ = wslot[f]
                    b = gcnt[0] % 2
                    gcnt[0] += 1
                    pg, pgr = bank()
                    pu, pur = bank()
                    for k in range(8):
                        mm(pg[:, 0:ntok], WG[:, s_, k, :], XNTa[:, k, tok], k == 0, k == 7, [Wsr[s_]] + xr, pgr)
                    for k in range(8):
                        mm(pu[:, 0:ntok], WU[:, s_, k, :], XNTa[:, k, tok], k == 0, k == 7, [Wsr[s_]] + xr, pur)
                    I("act", "activation", [pgr], [SGr[b]], out=SG[:, b, 0:ntok], in_=pg[:, 0:ntok], func=AF.Silu)
                    I("dve", "tensor_tensor", [SGr[b], pur], [Hr[j]], out=Hh[:, j, 0:ntok], in0=SG[:, b, 0:ntok],
                      in1=pu[:, 0:ntok], op=ALU.mult)
                for ti in range(n):
                    t = t0 + ti
                    for c in range(2):
                        pd, pdr = bank()
                        for j, f in enumerate(fl):
                            s_ = wslot[f]
                            mm(pd[:, :], Hh[:, j, ti * 128:(ti + 1) * 128], WD[:, s_, c * 512:(c + 1) * 512], j == 0,
                               j == len(fl) - 1, [Hr[j], Wsr[s_]], pdr)
                        I("dve", "tensor_tensor", [Yr[t], pdr], [Yr[t]], out=Y[:, t, c * 512:(c + 1) * 512],
                          in0=Y[:, t, c * 512:(c + 1) * 512], in1=pd[:, :], op=ALU.add)
                if merging:
                    s_ffn = P.rec_end()
                    P.merge([s_ffn, s_norm])
                if pi == len(passes) - 1:
                    yo = yp_o.rearrange("(t p) d -> t p d", p=128)
                    for t in range(t0, t0 + n):
                        rstd, sr = norm_stats(Y[:, t, :], Yr[t], 0)
                        I("dve", "scalar_tensor_tensor", [Yr[t], sr, gfinr], [OUTr], out=OUT, in0=Y[:, t, :], scalar=rstd,
                          in1=gfin, op0=ALU.mult, op1=ALU.mult)
                        P.dma("sp", (yo[t] if t < NTP else ys_o), OUT, OUTr, reads=[OUTr])
        P.emit()
        return nc, P


def make_consts(hf):
    c = {}
    c["c_ident"] = np.eye(128, dtype=np.float32)
    i = np.arange(128)[:, None]; j = np.arange(256)[None, :]
    band = np.where((j >= i) & (j <= i + 128), 0.0, NEG).astype(np.float32)
    first = band.copy()
    if hf == 0:
        first[:, :128] = NEG
    c["c_mb_band"] = band; c["c_mb_first"] = first
    s = np.arange(128)[:, None]; t = np.arange(128)[None, :]
    c["c_mb_caus"] = np.where(s <= t, 0.0, NEG).astype(np.float32)
    c["c_mb_causs"] = np.where((s <= t) & (s // 8 == t // 8), 0.0, NEG).astype(np.float32)
    sel = np.zeros((4, 1024), np.float32)
    for h in range(4):
        sel[h, h * 128:(h + 1) * 128] = 1.0
        sel[h, 512 + h * 128:512 + (h + 1) * 128] = -1.0
    c["c_sel"] = sel
    pm = np.zeros((4, 2), np.float32)
    pm[:, 0] = 1.0 if hf else 0.0
    pm[:, 1] = 0.0 if hf else NEG
    c["c_pmask"] = pm
    r = np.arange(32)[:, None] % 8
    p = np.arange(128)[None, :]
    c["c_smc"] = np.where(p >= r, 0.0, NEG).astype(np.float32)
    smn = np.full((32, 16, 128), NEG, np.float32)
    for jq in range(16):
        for ii in range(8):
            smn[(np.arange(32) % 8) >= ii, jq, jq * 8 + ii] = 0.0
    c["c_smn"] = smn
    c["c_bt"] = np.where((s <= t) & (s // 8 == t // 8), 1.0, 0.0).astype(np.float32)
    e = np.zeros((128, 16), np.float32); e[np.arange(128), np.arange(128) // 8] = 1.0
    c["c_eseq"] = e
    return c

def shard_inputs(inp):
    maps = []
    W = ["w_in", "b_igate", "b_fgate", "attn_sinks", "g_mlstm_head", "w_out", "g_mix", "g_cross", "g_mem",
         "w_cq", "w_ck", "w_cv", "w_co", "g_ffn", "w_gate", "w_up", "w_down"]
    wd = {k: np.ascontiguousarray(np.asarray(inp[k], np.float32)[0]) for k in W}
    wd["g_final"] = np.ascontiguousarray(np.asarray(inp["g_final"], np.float32))
    xp = np.asarray(inp["x_prompt"], np.float32); xs = np.asarray(inp["x_sample"], np.float32)
    for c in range(8):
        b, hf = c // 2, c % 2
        m = dict(wd)
        m["xp"] = np.ascontiguousarray(xp[b, hf * 2048:(hf + 1) * 2048])
        m["xpre"] = np.ascontiguousarray(xp[b, 0:2048]) if hf else np.zeros((2048, 1024), np.float32)
        m["xs"] = np.ascontiguousarray(xs[16 * c:16 * c + 16].reshape(128, 1024))
        m["mem"] = np.ascontiguousarray(np.asarray(inp["mem_prompt"], np.float32)[b])
        sl = slice(16 * c, 16 * c + 16)
        m["csk"] = np.ascontiguousarray(np.asarray(inp["cache_swa_k"], np.float32)[0, sl].reshape(16, 128, 128))
        m["csv"] = np.ascontiguousarray(np.asarray(inp["cache_swa_v"], np.float32)[0, sl].reshape(16, 128, 128))
        m["sC"] = np.ascontiguousarray(np.asarray(inp["state_mlstm_C"], np.float32)[0, sl])
        m["sn"] = np.ascontiguousarray(np.asarray(inp["state_mlstm_n"], np.float32)[0, sl])
        m["sm"] = np.ascontiguousarray(np.asarray(inp["state_mlstm_m"], np.float32)[0, sl])
        m["cmk"] = np.ascontiguousarray(np.asarray(inp["cache_mem_k"], np.float32)[0, sl].reshape(16, 256, 256))
        m["cmv"] = np.ascontiguousarray(np.asarray(inp["cache_mem_v"], np.float32)[0, sl].reshape(16, 256, 256))
        m.update(make_consts(hf))
        sk = wd["attn_sinks"]
        sc = np.zeros((32, 2), np.float32)
        rr = np.arange(32)
        for h in range(2):
            sc[:, h] = sk[4 * h + 2 * ((rr % 16) // 8) + rr // 16]
        m["c_sinkcol"] = sc
        maps.append(m)
    return maps

def gather(res):
    f = np.float32
    yp = np.zeros((4, 4096, 1024), f); ys = np.zeros((128, 8, 1024), f)
    skp = np.zeros((1, 4, 128, 2, 64), f); svp = np.zeros_like(skp)
    Cp = np.zeros((1, 4, 4, 128, 64), f); npp = np.zeros((1, 4, 4, 64), f); mp = np.zeros((1, 4, 4), f)
    mkp = np.zeros((1, 4, 256, 4, 64), f); mvp = np.zeros_like(mkp)
    sks = np.zeros((1, 128, 128, 2, 64), f); svs = np.zeros_like(sks)
    Cs = np.zeros((1, 128, 4, 128, 64), f); ns = np.zeros((1, 128, 4, 64), f); ms = np.zeros((1, 128, 4), f)
    for c in range(8):
        r = res[c]; b, hf = c // 2, c % 2
        yp[b, hf * 2048:(hf + 1) * 2048] = r["yp"]
        ys[16 * c:16 * c + 16] = r["ys"].reshape(16, 8, 1024)
        if hf == 1:
            skp[0, b] = r["swak"].reshape(128, 2, 64); svp[0, b] = r["swav"].reshape(128, 2, 64)
            Cp[0, b] = r["Cp"]; npp[0, b] = r["np"]; mp[0, b] = r["mp"].reshape(4)
        else:
            mkp[0, b] = r["memk"].reshape(256, 4, 64); mvp[0, b] = r["memv"].reshape(256, 4, 64)
        sl = slice(16 * c, 16 * c + 16)
        sks[0, sl] = r["sks"].reshape(16, 128, 2, 64); svs[0, sl] = r["svs"].reshape(16, 128, 2, 64)
        Cs[0, sl] = r["Cs"]; ns[0, sl] = r["ns"]; ms[0, sl] = r["ms"]
    return (yp, ys, skp, svp, Cp, npp, mp, mkp, mvp, sks, svs, Cs, ns, ms)


_CACHE = {}


def kernel(**inputs):
    if "nc" not in _CACHE:
        _CACHE["nc"] = build_program(3)[0]
    nc = _CACHE["nc"]
    maps = shard_inputs(inputs)
    res = run_bass_kernel_spmd(nc, maps, core_ids=list(range(8)))
    return gather(res.results)
```

```python
import contextlib
from concourse.bass_utils import run_bass_kernel_spmd
import numpy as np
import concourse.bass as bass
import concourse.mybir as mybir

F32 = mybir.dt.float32
BF16 = mybir.dt.bfloat16
I32 = mybir.dt.int32
AF = mybir.ActivationFunctionType
ALU = mybir.AluOpType
AX = mybir.AxisListType

ENGS = ("pe", "act", "dve", "pool", "sp")


class Res:
    __slots__ = ("name", "last_w", "readers", "sem", "dcount", "excl")

    def __init__(self, name):
        self.name = name
        self.last_w = None
        self.readers = []
        self.sem = None
        self.dcount = 0
        self.excl = False


class Op:
    __slots__ = ("eng", "fn", "deps", "dma_res", "sig", "cnt", "k", "group")

    def __init__(self, eng, fn, dma_res):
        self.eng = eng
        self.fn = fn
        self.deps = set()
        self.dma_res = dma_res
        self.sig = False
        self.cnt = 0
        self.k = 0


class Prog:
    def __init__(self, nc):
        self.nc = nc
        self.ops = []
        self.nres = 0
        self.inherit = []
        self.phase_res = []

    def res(self, name=None, arena=False):
        self.nres += 1
        r = Res(name or f"r{self.nres}")
        if arena:
            r.readers = list(self.inherit)
            self.phase_res.append(r)
        return r

    def new_phase(self):
        inh = set(self.inherit)
        for r in self.phase_res:
            if r.last_w is not None:
                inh.add(r.last_w)
            inh.update(r.readers)
        self.inherit = sorted(inh)
        self.phase_res = []

    def rec_begin(self):
        self._rec = []

    def rec_end(self):
        r = self._rec
        self._rec = None
        return r

    def merge(self, streams):
        streams = [s_ for s_ in streams if s_]
        pos = [0] * len(streams)
        while True:
            best = None
            for k, s_ in enumerate(streams):
                if pos[k] < len(s_):
                    f = pos[k] / len(s_)
                    if best is None or f < best[0]:
                        best = (f, k)
            if best is None:
                break
            k = best[1]
            a, kw = streams[k][pos[k]]
            pos[k] += 1
            self.op(*a, **kw)

    def op(self, eng, fn, reads=(), writes=(), dma_res=None, accum=False, group=False):
        if getattr(self, "_rec", None) is not None:
            self._rec.append(((eng, fn, tuple(reads), tuple(writes)), dict(dma_res=dma_res, accum=accum, group=group)))
            return None
        i = len(self.ops)
        o = Op(eng, fn, dma_res)
        for r in reads:
            if r.last_w is not None:
                o.deps.add(r.last_w)
            if r.excl:
                for q in r.readers:
                    if self.ops[q].eng != eng:
                        o.deps.add(q)
            r.readers.append(i)
        for r in writes:
            if r.last_w is not None:
                lw = self.ops[r.last_w]
                if group and lw.dma_res is not None and lw.dma_res is dma_res:
                    o.deps |= lw.deps
                elif not (accum and lw.eng == "pe" and eng == "pe"):
                    o.deps.add(r.last_w)
            latest = {}
            for q in r.readers:
                if q == i:
                    continue
                oq = self.ops[q]
                if oq.dma_res is not None:
                    o.deps.add(q)
                elif latest.get(oq.eng, -1) < q:
                    latest[oq.eng] = q
            o.deps.update(latest.values())
            r.last_w = i
            r.readers = []
        if eng == "pe":
            o.deps = {d for d in o.deps if self.ops[d].eng != "pe" or self.ops[d].dma_res is not None}
        self.ops.append(o)
        return i

    def dma(self, eng, out, in_, res, reads=(), writes=(), group=False, **kw):
        kw = dict(kw); kw["out"] = out; kw["in_"] = in_
        return self.op(eng, ("dma_start", kw), reads=reads, writes=writes, dma_res=res, group=group)

    def I(self, eng, name, reads=(), writes=(), **kw):
        return self.op(eng, (name, kw), reads=reads, writes=writes)

    def emit(self, final_wait_all=True):
        nc = self.nc
        ops = self.ops
        for o in ops:
            for d in o.deps:
                ops[d].sig = True
        per_eng = {e: [] for e in ENGS}
        for i, o in enumerate(ops):
            per_eng[o.eng].append(i)
        import contextlib
        with contextlib.ExitStack() as st:
            esem = {e: st.enter_context(nc.semaphore(f"s_{e}")) for e in ENGS}
            ecount = {e: 0 for e in ENGS}
            dma_sems = []
            for i, o in enumerate(ops):
                if o.dma_res is not None:
                    r = o.dma_res
                    if r.sem is None:
                        r.sem = st.enter_context(nc.semaphore(f"d{len(dma_sems)}_{r.name}"))
                        dma_sems.append(r)
                    r.dcount += 1
                    o.cnt = 16 * r.dcount
                elif o.sig:
                    ecount[o.eng] += 1
                    o.cnt = ecount[o.eng]
            self.n_dma_sems = len(dma_sems)
            know = {e: {} for e in ENGS}
            know_issue = [None] * len(ops)

            def key_of(o):
                return ("d", id(o.dma_res)) if o.dma_res is not None else ("e", o.eng)

            block = st.enter_context(nc.Block())
            handles = {}

            plan = [None] * len(ops)
            for i, o in enumerate(ops):
                kn = know[o.eng]
                need = {}
                for d in o.deps:
                    p = ops[d]
                    k = key_of(p)
                    if kn.get(k, 0) >= p.cnt:
                        continue
                    if need.get(k, (0, None))[0] < p.cnt:
                        need[k] = (p.cnt, d)
                waits = []
                for k, (cnt, d) in need.items():
                    p = ops[d]
                    sem = p.dma_res.sem if p.dma_res is not None else esem[p.eng]
                    waits.append((sem, cnt))
                    kn[k] = max(kn.get(k, 0), cnt)
                    ki = know_issue[d]
                    for kk, vv in ki.items():
                        if kn.get(kk, 0) < vv:
                            kn[kk] = vv
                know_issue[i] = dict(kn)
                plan[i] = waits
            self.n_waits = sum(len(w) for w in plan)

            def make(ename):
                def body(eh):
                    for i in per_eng[ename]:
                        o = ops[i]
                        for sem, cnt in plan[i]:
                            eh.wait_ge(sem, cnt)
                        ins = getattr(eh, o.fn[0])(**o.fn[1])
                        if o.dma_res is not None:
                            ins.then_inc(o.dma_res.sem, 16)
                        elif o.sig:
                            ins.then_inc(esem[o.eng], 1)
                    if ename == "sp" and final_wait_all:
                        for r in dma_sems:
                            eh.wait_ge(r.sem, 16 * r.dcount)
                        for e in ("pe", "act", "dve", "pool"):
                            if ecount[e]:
                                eh.wait_ge(esem[e], ecount[e])
                return body

            block.tensor(make("pe"))
            block.scalar(make("act"))
            block.vector(make("dve"))
            block.gpsimd(make("pool"))
            block.sync(make("sp"))


D = 1024
FH = 2816
EPS = 1e-6
NTP = 16
NT = 17
GT = 2
NEG = -30000.0
DBG_G0 = 2


def build_program(stage=3, debug=False):
    nc = bass.Bass("TRN2", target_bir_lowering=False)
    P = Prog(nc)

    def din(name, shape, dt=F32):
        return nc.dram_tensor(name, list(shape), dt, kind="ExternalInput").ap()

    def dout(name, shape):
        return nc.dram_tensor(name, list(shape), F32, kind="ExternalOutput").ap()

    xp_d = din("xp", [2048, D]); xpre_d = din("xpre", [2048, D]); xs_d = din("xs", [128, D])
    mem_d = din("mem", [256, D])
    csk_d = din("csk", [16, 128, 128]); csv_d = din("csv", [16, 128, 128])
    sC_d = din("sC", [16, 4, 128, 64]); sn_d = din("sn", [16, 4, 64]); sm_d = din("sm", [16, 4])
    cmk_d = din("cmk", [16, 256, 256]); cmv_d = din("cmv", [16, 256, 256])
    w_in_d = din("w_in", [D, 2312]); b_i_d = din("b_igate", [4]); b_f_d = din("b_fgate", [4])
    sinks_d = din("attn_sinks", [8]); ghead_d = din("g_mlstm_head", [512]); w_out_d = din("w_out", [D, D])
    g_mix_d = din("g_mix", [D]); g_cross_d = din("g_cross", [D]); g_mem_d = din("g_mem", [D])
    w_cq_d = din("w_cq", [D, 256]); w_ck_d = din("w_ck", [D, 256]); w_cv_d = din("w_cv", [D, 256])
    w_co_d = din("w_co", [256, D]); g_ffn_d = din("g_ffn", [D])
    w_gate_d = din("w_gate", [D, FH]); w_up_d = din("w_up", [D, FH]); w_down_d = din("w_down", [FH, D])
    g_final_d = din("g_final", [D])
    ident_d = din("c_ident", [128, 128]); mb_band_d = din("c_mb_band", [128, 256]); mb_first_d = din("c_mb_first", [128, 256])
    mb_caus_d = din("c_mb_caus", [128, 128]); mb_causs_d = din("c_mb_causs", [128, 128])
    sel_d = din("c_sel", [4, 1024]); pmask_d = din("c_pmask", [4, 2])
    smc_d = din("c_smc", [32, 128]); smn_d = din("c_smn", [32, 16, 128]); sinkcol_d = din("c_sinkcol", [32, 2])
    bt_d = din("c_bt", [128, 128]); eseq_d = din("c_eseq", [128, 16])

    yp_o = dout("yp", [2048, D]); ys_o = dout("ys", [128, D])
    swak_o = dout("swak", [128, 128]); swav_o = dout("swav", [128, 128])
    Cp_o = dout("Cp", [4, 128, 64]); np_o = dout("np", [4, 64]); mp_o = dout("mp", [4, 1])
    memk_o = dout("memk", [256, 256]); memv_o = dout("memv", [256, 256])
    sks_o = dout("sks", [16, 128, 128]); svs_o = dout("svs", [16, 128, 128])
    Cs_o = dout("Cs", [16, 4, 128, 64]); ns_o = dout("ns", [16, 4, 64]); ms_o = dout("ms", [16, 4])

    st = contextlib.ExitStack()
    with st:
        def sb(name, shape, dt):
            return st.enter_context(nc.sbuf_tensor(name, list(shape), dt))

        def ps(name, shape, dt):
            return st.enter_context(nc.psum_tensor(name, list(shape), dt))

        banks = [ps(f"bk{i}", [128, 512], F32) for i in range(7)]
        bres = [P.res(f"bk{i}") for i in range(7)]
        for r_ in bres:
            r_.excl = True
        tb = ps("tb", [128, 1024], BF16)
        tbh = [tb[:, 0:512], tb[:, 512:1024]]
        tbhr = [P.res("tbA"), P.res("tbB")]
        for r_ in tbhr:
            r_.excl = True
        tb2 = banks[6][:, :].bitcast(BF16)
        TSEL = [(tb[:, 0:512], tbhr[0]), (tb2[:, 0:512], bres[6])]
        bki = [0]

        bset = [[0, 1, 2, 3, 4]]
        bcnt = {}

        def bank():
            key = tuple(bset[0])
            c = bcnt.get(key, 0)
            bcnt[key] = c + 1
            i = bset[0][c % len(key)]
            return banks[i], bres[i]

        Y = sb("Y", [128, NT, D], F32)
        Yr = [P.res(f"Y{t}") for t in range(NT)]
        identb = sb("identb", [128, 128], BF16); identr = P.res("identb")
        identf = sb("identf", [128, 128], F32); identfr = P.res("identf")
        onesb = sb("onesb", [128, 128], BF16); onesbr = P.res("onesb")
        onesf = sb("onesf", [128, 256], F32); onesfr = P.res("onesf")
        SEL = sb("SEL", [4, 1024], F32); selr = P.res("SEL")
        gcols = sb("gcols", [128, 4, 8], F32); gcolsr = P.res("gcols")
        gheadc = sb("gheadc", [128, 4], F32); gheadr = P.res("ghead")
        sinkb = sb("sinkb", [128, 16], F32); sinkbr = P.res("sinkb")
        gb4 = sb("gb4", [4, 4], F32); gb4r = P.res("gb4")
        mbband = sb("mbband", [128, 256], BF16); mbbandr = P.res("mbband")
        mbfirst = sb("mbfirst", [128, 256], BF16); mbfirstr = P.res("mbfirst")
        mbcaus = sb("mbcaus", [128, 128], BF16); mbcausr = P.res("mbcaus")
        stat = sb("stat", [128, 8, 4], F32)
        statr = [P.res(f"stat{i}") for i in range(8)]
        stati = [0]
        USE_SQRT = [False]
        xsb = sb("xsb", [128, 2, D], BF16); xsbr = [P.res("xsb0"), P.res("xsb1")]
        Cst = sb("Cst", [64, 4, 129], F32); Cstr = [P.res(f"Cst{h}") for h in range(4)]
        ARN = 64400
        arena = sb("arena", [128, ARN], BF16)
        aoff = [0]

        def A(shape, dt, parts=128, name=None):
            n = int(np.prod(shape))
            nb = n * (4 if dt == F32 else 2)
            n16 = (nb + 1) // 2
            n16 = (n16 + 15) // 16 * 16
            assert aoff[0] + n16 <= ARN, f"arena overflow {aoff[0]}+{n16} ({name})"
            v = arena[0:parts, aoff[0]:aoff[0] + n16]
            aoff[0] += n16
            if dt == F32:
                v = v.bitcast(F32)
            v = v[:, 0:n]
            if len(shape) == 2:
                v = v.rearrange("p (a b) -> p a b", a=shape[0])
            elif len(shape) == 3:
                v = v.rearrange("p (a b c) -> p a b c", a=shape[0], b=shape[1])
            return v

        def new_phase():
            P.new_phase()
            aoff[0] = 0

        def AR(name):
            return P.res(name, arena=True)

        def A_at(off, shape, dt, parts=128):
            n = int(np.prod(shape))
            nb = n * (4 if dt == F32 else 2)
            n16 = ((nb + 1) // 2 + 15) // 16 * 16
            v = arena[0:parts, off:off + n16]
            if dt == F32:
                v = v.bitcast(F32)
            v = v[:, 0:n]
            if len(shape) == 2:
                v = v.rearrange("p (a b) -> p a b", a=shape[0])
            elif len(shape) == 3:
                v = v.rearrange("p (a b c) -> p a b c", a=shape[0], b=shape[1])
            return v, off + n16

        def ARalias(name, olds):
            r = P.res(name, arena=True)
            dd = set(r.readers)
            for o_ in olds:
                if o_.last_w is not None:
                    dd.add(o_.last_w)
                dd.update(o_.readers)
            r.readers = sorted(dd)
            return r

        I = P.I

        def mm(out, lhsT, rhs, start, stop, reads, wres):
            I("pe", "matmul", reads, [wres], out=out, lhsT=lhsT, rhs=rhs, start=start, stop=stop)

        P.dma("pool", identb[:], ident_d, identr, writes=[identr])
        P.dma("sp", identf[:], ident_d, identfr, writes=[identfr])
        I("dve", "memset", [], [onesbr], ap=onesb[:], constant=1.0)
        I("dve", "memset", [], [onesfr], ap=onesf[:], constant=1.0)
        P.dma("sp", SEL[:], sel_d, selr, writes=[selr])
        for i, g in enumerate((g_mix_d, g_cross_d, g_mem_d, g_ffn_d)):
            P.dma("sp", gcols[:, i, :], g.rearrange("(k p) -> p k", p=128), gcolsr, writes=[gcolsr], group=True, allow_slow_non_contiguous=True)
        P.dma("sp", gheadc[:], ghead_d.rearrange("(h p) -> p h", p=128), gheadr, writes=[gheadr], allow_slow_non_contiguous=True)
        P.dma("sp", gb4[:, 0:1], b_i_d.rearrange("(h o) -> h o", o=1), gb4r, writes=[gb4r], group=True, allow_slow_non_contiguous=True)
        P.dma("sp", gb4[:, 1:2], b_f_d.rearrange("(h o) -> h o", o=1), gb4r, writes=[gb4r], group=True, allow_slow_non_contiguous=True)
        P.dma("sp", gb4[:, 2:4], pmask_d, gb4r, writes=[gb4r], group=True, allow_slow_non_contiguous=True)
        P.dma("sp", sinkb[:, 0:8], sinks_d.partition_broadcast(128), sinkbr, writes=[sinkbr])
        I("dve", "tensor_scalar", [sinkbr], [sinkbr], out=sinkb[:, 8:16], in0=sinkb[:, 0:8], scalar1=-1.0, scalar2=None,
          op0=ALU.mult)
        I("dve", "tensor_scalar", [gb4r], [gb4r], out=gb4[:, 1:2], in0=gb4[:, 1:2], scalar1=-1.0, scalar2=None, op0=ALU.mult)
        P.dma("pool", mbband[:], mb_band_d, mbbandr, writes=[mbbandr])
        P.dma("pool", mbfirst[:], mb_first_d, mbfirstr, writes=[mbfirstr])
        P.dma("pool", mbcaus[:], mb_caus_d, mbcausr, writes=[mbcausr])
        for h in range(4):
            I("dve", "memset", [], [Cstr[h]], ap=Cst[:, h, :], constant=0.0)

        def SELh(h, n=128):
            return SEL[:, h * 128:h * 128 + n]

        def NSELh(h, n=128):
            return SEL[:, 512 + h * 128:512 + h * 128 + n]

        def norm_stats(src, sres, jb=0):
            i = stati[0] % 8
            stati[0] += 1
            sr = statr[i]
            I("act", "activation", [sres], [xsbr[jb], sr], out=xsb[:, jb, :], in_=src, func=AF.Square, accum_out=stat[:, i, 0:1])
            I("dve", "tensor_scalar", [sr], [sr], out=stat[:, i, 1:2], in0=stat[:, i, 0:1], scalar1=1.0 / D, scalar2=EPS,
              op0=ALU.mult, op1=ALU.add)
            if USE_SQRT[0]:
                I("act", "activation", [sr], [sr], out=stat[:, i, 2:3], in_=stat[:, i, 1:2], func=AF.Sqrt)
                I("dve", "reciprocal", [sr], [sr], out=stat[:, i, 3:4], in_=stat[:, i, 2:3])
            else:
                I("act", "activation", [sr], [sr], out=stat[:, i, 2:3], in_=stat[:, i, 1:2], func=AF.Ln)
                I("act", "activation", [sr], [sr], out=stat[:, i, 3:4], in_=stat[:, i, 2:3], func=AF.Exp, scale=-0.5)
            return stat[:, i, 3:4], sr

        xsi = [0]

        def norm_T(src, sres, gi, dst, dres, half=None, tsel=None):
            b = xsi[0] % 2 if half is None else half
            xsi[0] += 1
            rstd, sr = norm_stats(src, sres, b)
            I("dve", "tensor_scalar", [sres, sr], [xsbr[b]], out=xsb[:, b, :], in0=src, scalar1=rstd, scalar2=None, op0=ALU.mult)
            if half is None:
                for k in range(8):
                    I("pe", "transpose", [xsbr[b], identr], [*tbhr], out=tb[:, k * 128:(k + 1) * 128],
                      in_=xsb[:, b, k * 128:(k + 1) * 128], identity=identb[:])
                for k in range(8):
                    I("act", "activation", [*tbhr, gcolsr], dres, out=dst[:, k, :], in_=tb[:, k * 128:(k + 1) * 128],
                      func=AF.Copy, scale=gcols[:, gi, k:k + 1])
            else:
                tq, tqr = (tbh[half], tbhr[half]) if tsel is None else tsel
                for kb in range(2):
                    for k4 in range(4):
                        k = kb * 4 + k4
                        I("pe", "transpose", [xsbr[b], identr], [tqr], out=tq[:, k4 * 128:(k4 + 1) * 128],
                          in_=xsb[:, b, k * 128:(k + 1) * 128], identity=identb[:])
                    for k4 in range(4):
                        k = kb * 4 + k4
                        if k4 % 2 == 0:
                            I("act", "activation", [tqr, gcolsr], dres, out=dst[:, k, :],
                              in_=tq[:, k4 * 128:(k4 + 1) * 128], func=AF.Copy, scale=gcols[:, gi, k:k + 1])
                        else:
                            I("dve", "tensor_scalar", [tqr, gcolsr], dres, out=dst[:, k, :],
                              in0=tq[:, k4 * 128:(k4 + 1) * 128], scalar1=gcols[:, gi, k:k + 1], scalar2=None, op0=ALU.mult)

        class NS:
            pass

        def alloc_mixer(gt, nkt, nvt, reuse=None):
            M = NS()
            M.WQ = A([8, 512], BF16); M.WQr = AR("WQ")
            M.WTOK = A([8, 1024], BF16); M.WTOKr = AR("WTOK")
            M.WK = M.WTOK[:, :, 0:128]; M.WKr = M.WTOKr
            M.WMQ = A([8, 256], BF16); M.WMQr = AR("WMQ")
            M.WMK = M.WTOK[:, :, 256:512]; M.WMKr = M.WTOKr
            M.WOG = A([8, 512], BF16); M.WOGr = AR("WOG")
            M.WGT = A([8, 8], BF16); M.WGTr = AR("WGT")
            M.WOA = A([4, 1024], BF16); M.WOAr = AR("WOA")
            M.WOM = A([4, 1024], BF16); M.WOMr = AR("WOM")

            def wload(dst, res, src, **kw):
                P.dma("pool", dst, src, res, writes=[res], **kw)

            def wcols(a_, b_):
                return w_in_d[:, a_:b_].rearrange("(k p) n -> p k n", p=128)
            if reuse is None:
                wload(M.WTOK[:, :, 0:256], M.WTOKr, wcols(512, 768), group=True)
                wload(M.WTOK[:, :, 256:1024], M.WTOKr, wcols(1024, 1792), group=True)
                wload(M.WGT[:], M.WGTr, wcols(2304, 2312), allow_slow_non_contiguous=True)
                wload(M.WQ[:], M.WQr, wcols(0, 512))
                wload(M.WMQ[:], M.WMQr, wcols(768, 1024))
                wload(M.WOG[:], M.WOGr, wcols(1792, 2304))
                wload(M.WOA[:], M.WOAr, w_out_d[0:512, :].rearrange("(c p) n -> p c n", p=128))
                wload(M.WOM[:], M.WOMr, w_out_d[512:1024, :].rearrange("(h p) n -> p h n", p=128))
            else:
                for nm_ in ("WQr", "WTOKr", "WMQr", "WOGr", "WGTr", "WOAr", "WOMr"):
                    setattr(M, nm_, getattr(reuse, nm_))
                M.WKr = M.WTOKr; M.WMKr = M.WTOKr
                P.phase_res.extend([M.WQr, M.WTOKr, M.WMQr, M.WOGr, M.WGTr, M.WOAr, M.WOMr])
            M.KT = A([2, nkt * 128], BF16, parts=64); M.KTr = [AR(f"KT{i}") for i in range(nkt)]
            M.Vt = A([nvt, 128], BF16); M.Vtr = [AR(f"Vt{i}") for i in range(nvt)]
            M.XNTg = A([8, gt * 128], BF16); M.XNTgr = [AR(f"XNTg{i}") for i in range(gt)]
            M.QT = A([8, gt * 128], BF16, parts=64); M.QTr = AR("QT")
            M.MQT = A([4, gt * 128], BF16, parts=64); M.MQTr = AR("MQT")
            M.MKT = A([4, gt * 128], BF16, parts=64); M.MKTr = AR("MKT")
            M.SGT = A([4, gt * 128], BF16); M.SGTr = AR("SGT")
            M.MKtok = A([gt, 256], BF16); M.MKtokr = [AR(f"MKtok{i}") for i in range(gt)]
            M.MVaug = A([gt, 4, 129], BF16); M.MVaugr = [AR(f"MVaug{i}") for i in range(gt)]
            M.ATTT = A([4, gt * 128], BF16); M.ATTTr = [AR(f"ATTT{i}") for i in range(gt)]
            M.HMT = A([4, gt * 128], BF16); M.HMTr = [AR(f"HMT{i}") for i in range(gt)]
            M.NG = gt * 128
            NG_ = M.NG
            M.G_IG = A([1, NG_ + 1], F32, parts=4)[:, 0, :]; M.G_E = A([1, NG_], F32, parts=4)[:, 0, :]
            M.G_L1 = A([1, NG_], F32, parts=4)[:, 0, :]; M.G_B = A([1, NG_ + 1], F32, parts=4)[:, 0, :]
            M.G_A = A([1, NG_], F32, parts=4)[:, 0, :]; M.G_M = A([1, NG_ + 1], F32, parts=4)[:, 0, :]
            M.G_BM = A([1, NG_], F32, parts=4)[:, 0, :]; M.G_DM = A([1, NG_], F32, parts=4)[:, 0, :]
            M.Gr = AR("G_IG"); M.G_Br = AR("G_B"); M.G_Ar = AR("G_A"); M.G_Mr = AR("G_M"); M.G_BMr = AR("G_BM"); M.G_DMr = AR("G_DM")
            M.SKV = A([1, 256], F32)[:, 0, :]; M.SKVr = AR("SKV")
            M.Ebuf = A([4, 256], BF16); M.Er = AR("E")
            M.PTs = A([1, 1024], BF16); M.PTsr = [AR("PTs0")] * 2
            M.sm_st = A([1, 32], F32)[:, 0, :]; M.smr = AR("sm_st")
            M.WKC = A([1, 8], F32)[:, 0, :]; M.WKCr = AR("WKC")
            M.DG = A([1, 8], F32, parts=4)[:, 0, :]; M.DGr = AR("DG")
            M.VW = A([4, 129], BF16); M.VWr = [AR(f"VW{h}") for h in range(4)]
            M.Cb = A([4, 257], BF16, parts=64); M.Cbr = [AR(f"Cb{h}") for h in range(4)]
            M.WT = A([4, 128], BF16); M.WTr = AR("WT")
            M.ST = A([4, 128], BF16); M.STr = AR("ST")
            M.WI = A([4, 128], BF16); M.WIr = AR("WI")
            M.QW = A([4, 128], BF16, parts=64); M.QWr = AR("QW")
            M.LOWB = A([4, 128], F32); M.LOWBr = AR("LOWB")
            M.T1 = A([4, 128], F32); M.T1r = AR("T1")
            M.T2 = A([4, 128], F32); M.T2r = AR("T2")
            M.USQ = A([4, 128], BF16); M.USQr = AR("USQ")
            for i in range(gt):
                I("dve", "memset", [], [M.MVaugr[i]], ap=M.MVaug[:, i, :, 128:129], constant=1.0)
            M.PB = [(M.XNTg, M.XNTgr, M.MKtok, M.MKtokr, M.MVaug, M.MVaugr)]
            if gt > 1:
                x2 = A([8, gt * 128], BF16); x2r = [AR(f"XNTh{i}") for i in range(gt)]
                k2 = A([gt, 256], BF16); k2r = [AR(f"MKtoh{i}") for i in range(gt)]
                v2 = A([gt, 4, 129], BF16); v2r = [AR(f"MVauh{i}") for i in range(gt)]
                for i in range(gt):
                    I("dve", "memset", [], [v2r[i]], ap=v2[:, i, :, 128:129], constant=1.0)
                M.PB.append((x2, x2r, k2, k2r, v2, v2r))
            return M

        def use(pb):
            M.XNTg, M.XNTgr, M.MKtok, M.MKtokr, M.MVaug, M.MVaugr = M.PB[pb]

        M = alloc_mixer(GT, NTP + 1, NTP + 1)
        I("dve", "memset", [], [M.G_Br], ap=M.G_B[:, 0:1], constant=0.0)
        I("dve", "memset", [], [M.G_Mr], ap=M.G_M[:, 0:1], constant=0.0)

        def tok_major(ti, xcols, xres, vslot, want_kv_out=None):
            b0, b0r = bank()
            for k in range(8):
                mm(b0[:, :], M.XNTg[:, k, xcols], M.WTOK[:, k, 0:512], k == 0, k == 7, [xres, M.WTOKr], b0r)
            I("act", "activation", [b0r], [M.Vtr[vslot]], out=M.Vt[:, vslot, :], in_=b0[:, 128:256], func=AF.Copy)
            I("act", "activation", [b0r], [M.MKtokr[ti]], out=M.MKtok[:, ti, :], in_=b0[:, 256:512], func=AF.Copy, scale=0.125)
            if want_kv_out is not None:
                I("dve", "tensor_copy", [b0r], [M.SKVr], out=M.SKV[:, :], in_=b0[:, 0:256])
                if want_kv_out == "sample":
                    P.dma("sp", sks_o[:, 120:128, :], M.SKV[:, 0:128], M.SKVr, reads=[M.SKVr], group=True)
                    P.dma("sp", svs_o[:, 120:128, :], M.SKV[:, 128:256], M.SKVr, reads=[M.SKVr], group=True)
                else:
                    P.dma("sp", swak_o, M.SKV[:, 0:128], M.SKVr, reads=[M.SKVr], group=True)
                    P.dma("sp", swav_o, M.SKV[:, 128:256], M.SKVr, reads=[M.SKVr], group=True)
            b1, b1r = bank()
            for k in range(8):
                mm(b1[:, :], M.XNTg[:, k, xcols], M.WTOK[:, k, 512:1024], k == 0, k == 7, [xres, M.WTOKr], b1r)
            I("dve", "tensor_copy", [b1r], [M.MVaugr[ti]], out=M.MVaug[:, ti, :, 0:128],
              in_=b1[:, :].rearrange("p (h d) -> p h d", h=4))

        def feat64(W, Wr, nh, dst, dres, ntok, xres, scale=None, dcol0=0):
            for h0 in range(0, nh, 2):
                bk, bkr = bank()
                for hh in range(2):
                    h = h0 + hh
                    for k in range(8):
                        mm(bk[0:64, hh * 256:hh * 256 + ntok], W[:, k, h * 64:(h + 1) * 64], M.XNTg[:, k, 0:ntok],
                           k == 0, k == 7, [Wr] + xres, bkr)
                src = bk[0:64, :].rearrange("p (a b) -> p a b", a=2)[:, :, 0:ntok]
                kw = {} if scale is None else {"scale": scale}
                if scale is None and (h0 // 2) % 2 == 1:
                    I("dve", "tensor_copy", [bkr], dres, out=dst[:, h0:h0 + 2, dcol0:dcol0 + ntok], in_=src)
                else:
                    I("act", "activation", [bkr], dres, out=dst[:, h0:h0 + 2, dcol0:dcol0 + ntok], in_=src, func=AF.Copy, **kw)

        def gates(ntok, xres, prefix):
            pg, pgr = bank()
            for k in range(8):
                mm(pg[0:4, 0:ntok], M.WGT[:, k, 0:4], M.XNTg[:, k, 0:ntok], k == 0, k == 7, [M.WGTr] + xres, pgr)
            for k in range(8):
                mm(pg[0:4, 256:256 + ntok], M.WGT[:, k, 4:8], M.XNTg[:, k, 0:ntok], k == 0, k == 7, [M.WGTr] + xres, pgr)
            I("act", "activation", [pgr, gb4r], [M.Gr], out=M.G_IG[:, 1:ntok + 1], in_=pg[0:4, 0:ntok], func=AF.Identity,
              bias=gb4[:, 0:1])
            I("act", "activation", [pgr, gb4r], [M.Gr], out=M.G_E[:, 0:ntok], in_=pg[0:4, 256:256 + ntok], func=AF.Exp,
              bias=gb4[:, 1:2], scale=-1.0)
            I("act", "activation", [M.Gr], [M.Gr], out=M.G_L1[:, 0:ntok], in_=M.G_E[:, 0:ntok], func=AF.Ln, bias=1.0)
            if prefix == "sample":
                return
            if prefix:
                I("dve", "tensor_scalar", [M.Gr, gb4r], [M.Gr], out=M.G_L1[:, 0:ntok], in0=M.G_L1[:, 0:ntok], scalar1=gb4[:, 2:3],
                  scalar2=None, op0=ALU.mult)
            I("dve", "tensor_tensor_scan", [M.Gr, M.G_Br, onesfr], [M.G_Br], out=M.G_B[:, 1:ntok + 1], data0=onesf[0:4, 0:ntok],
              data1=M.G_L1[:, 0:ntok], initial=M.G_B[:, 0:1], op0=ALU.mult, op1=ALU.subtract)
            I("dve", "scalar_tensor_tensor", [M.Gr, M.G_Br, gb4r], [M.G_Ar], out=M.G_A[:, 0:ntok], in0=M.G_IG[:, 1:ntok + 1],
              scalar=(gb4[:, 3:4] if prefix else 0.0), in1=M.G_B[:, 1:ntok + 1], op0=ALU.add, op1=ALU.subtract)
            I("dve", "tensor_tensor_scan", [M.G_Ar, M.G_Mr, onesfr], [M.G_Mr], out=M.G_M[:, 1:ntok + 1], data0=onesf[0:4, 0:ntok],
              data1=M.G_A[:, 0:ntok], initial=M.G_M[:, 0:1], op0=ALU.mult, op1=ALU.max)
            I("dve", "tensor_tensor", [M.G_Br, M.G_Mr], [M.G_BMr], out=M.G_BM[:, 0:ntok], in0=M.G_B[:, 1:ntok + 1],
              in1=M.G_M[:, 1:ntok + 1], op=ALU.add)
            for ci in range(ntok // 128):
                I("dve", "tensor_scalar", [M.G_Mr], [M.G_DMr], out=M.G_DM[:, ci * 128:(ci + 1) * 128],
                  in0=M.G_M[:, 1 + ci * 128:1 + (ci + 1) * 128], scalar1=M.G_M[:, ci * 128:ci * 128 + 1], scalar2=None,
                  op0=ALU.subtract)

        def gates_carry(ntok):
            I("dve", "tensor_copy", [M.G_Br], [M.G_Br], out=M.G_B[:, 0:1], in_=M.G_B[:, ntok:ntok + 1])
            I("dve", "tensor_copy", [M.G_Mr], [M.G_Mr], out=M.G_M[:, 0:1], in_=M.G_M[:, ntok:ntok + 1])

        def state_update(ti, c0, refresh_cb):
            pw, pwr = bank()
            I4 = SEL[:, 0:512].rearrange("p (h t) -> p h t", t=128)[:, :, 0]
            I("dve", "tensor_scalar", [selr, M.G_Mr], [M.DGr], out=M.DG[:, 0:4], in0=I4, scalar1=M.G_M[:, c0 + 128:c0 + 129],
              scalar2=-1.0, op0=ALU.mult, op1=ALU.mult)
            I("dve", "tensor_scalar", [selr, M.G_DMr], [M.DGr], out=M.DG[:, 4:8], in0=I4, scalar1=M.G_DM[:, c0 + 127:c0 + 128],
              scalar2=-1.0, op0=ALU.mult, op1=ALU.mult)
            mm(pw[:, 0:4], M.G_A[:, c0:c0 + 128], I4, True, False, [M.G_Ar, selr], pwr)
            mm(pw[:, 0:4], onesf[0:4, 0:128], M.DG[:, 0:4], False, True, [onesfr, M.DGr], pwr)
            mm(pw[:, 4:8], onesf[0:4, 0:128], M.DG[:, 4:8], True, True, [onesfr, M.DGr], pwr)
            I("act", "activation", [pwr], [M.WKCr], out=M.WKC[:, 0:8], in_=pw[:, 0:8], func=AF.Exp)
            for h in range(4):
                I("dve", "tensor_scalar", [M.MVaugr[ti], M.WKCr], [M.VWr[h]], out=M.VW[:, h, :], in0=M.MVaug[:, ti, h, :],
                  scalar1=M.WKC[:, h:h + 1], scalar2=None, op0=ALU.mult)
            for h0 in (0, 2):
                dc, dcr = bank()
                for hh in range(2):
                    h = h0 + hh
                    mm(dc[0:64, hh * 129:(hh + 1) * 129], M.MKtok[:, ti, h * 64:(h + 1) * 64], M.VW[:, h, :], True, True,
                       [M.MKtokr[ti], M.VWr[h]], dcr)
                for hh in range(2):
                    h = h0 + hh
                    I("dve", "scalar_tensor_tensor", [Cstr[h], M.WKCr, dcr], [Cstr[h]], out=Cst[:, h, :], in0=Cst[:, h, :],
                      scalar=M.WKC[0:64, 4 + h:5 + h], in1=dc[0:64, hh * 129:(hh + 1) * 129], op0=ALU.mult, op1=ALU.add)
            if refresh_cb:
                for h in range(4):
                    I("act", "activation", [Cstr[h]], [M.Cbr[h]], out=M.Cb[:, h, 0:129], in_=Cst[:, h, :], func=AF.Copy)
                    I("act", "activation", [Cstr[h]], [M.Cbr[h]], out=M.Cb[:, h, 129:257],
                      in_=Cst[:, h, 128:129].broadcast_to([64, 128]), func=AF.Copy)

        def mlstm_chunk(ti, c0, mbias, mbiasr, inter=True, inter_fn=None):
            cs = slice(c0, c0 + 128)
            pwt, pwtr = bank()
            for h in range(4):
                o = pwt[:, h * 128:(h + 1) * 128]
                mm(o, M.G_A[:, cs], SELh(h), True, False, [M.G_Ar, selr], pwtr)
                mm(o, NSELh(h), M.G_M[:, c0 + 1:c0 + 129], False, False, [M.G_Mr, selr], pwtr)
                mm(o, identb[:], mbias, False, True, [identr, mbiasr], pwtr)
            I("act", "activation", [pwtr], [M.WTr], out=M.WT[:, :, :], in_=pwt[:, :].rearrange("p (h t) -> p h t", h=4), func=AF.Exp)
            pqk, pqkr = bank()
            for h in range(4):
                mm(pqk[:, h * 128:(h + 1) * 128], M.MKT[:, h, cs], M.MQT[:, h, cs], True, True, [M.MKTr, M.MQTr], pqkr)
            I("dve", "tensor_tensor", [pqkr, M.WTr], [M.STr], out=M.ST[:, :, :], in0=pqk[:, :].rearrange("p (h t) -> p h t", h=4),
              in1=M.WT[:, :, :], op=ALU.mult)
            pwi, pwir = bank()
            for h in range(4):
                mm(pwi[:, h * 128:(h + 1) * 128], NSELh(h), M.G_DM[:, cs], True, True, [M.G_DMr, selr], pwir)
            I("act", "activation", [pwir], [M.WIr], out=M.WI[:, :, :], in_=pwi[:, :].rearrange("p (h t) -> p h t", h=4), func=AF.Exp)
            I("dve", "tensor_tensor", [M.MQTr, M.WIr], [M.QWr], out=M.QW[:, :, :], in0=M.MQT[:, :, cs], in1=M.WI[0:64, :, :], op=ALU.mult)
            plb, plbr = bank()
            for h in range(4):
                mm(plb[:, h * 128:(h + 1) * 128], NSELh(h), M.G_BM[:, cs], True, True, [M.G_BMr, selr], plbr)
            I("act", "activation", [plbr], [M.LOWBr], out=M.LOWB[:, :, :], in_=plb[:, :].rearrange("p (h t) -> p h t", h=4), func=AF.Exp)
            pnum, pnumr = banks[5], bres[5]
            pden, pdenr = banks[6], bres[6]
            if inter_fn is not None:
                inter_fn("pre")
            for h in range(4):
                o = pnum[:, h * 128:(h + 1) * 128]
                mm(o, M.MVaug[:, ti, h, 0:128], M.ST[:, h, :], True, False, [M.MVaugr[ti], M.STr], pnumr)
                if inter_fn is not None:
                    inter_fn("num", h, pnum, pnumr)
                else:
                    mm(o, M.Cb[:, h, 0:128], M.QW[:, h, :], False, True, [M.Cbr[h], M.QWr], pnumr)
            for h in range(4):
                o = pden[:, h * 128:(h + 1) * 128]
                mm(o, onesb[:], M.ST[:, h, :], True, False, [onesbr, M.STr], pdenr)
                if inter_fn is not None:
                    inter_fn("den", h, pden, pdenr)
                else:
                    mm(o, M.Cb[:, h, 129:257], M.QW[:, h, :], False, True, [M.Cbr[h], M.QWr], pdenr)
            return pnum, pnumr, pden, pdenr

        def mlstm_finish(pnum, pnumr, pden, pdenr, c0, hres):
            cs = slice(c0, c0 + 128)
            v4 = lambda b: b[:, :].rearrange("p (h t) -> p h t", h=4)
            I("act", "activation", [pdenr], [M.T1r], out=M.T1[:, :, :], in_=v4(pden), func=AF.Abs)
            I("dve", "tensor_tensor", [M.T1r, M.LOWBr], [M.T1r], out=M.T1[:, :, :], in0=M.T1[:, :, :], in1=M.LOWB[:, :, :], op=ALU.max)
            I("act", "activation", [M.T1r], [M.T1r], out=M.T1[:, :, :], in_=M.T1[:, :, :], func=AF.Square, scale=float(np.sqrt(EPS)))
            I("act", "activation", [pnumr], [M.USQr], out=M.USQ[:, :, :], in_=v4(pnum), func=AF.Square)
            pss, pssr = bank()
            mm(pss[:, :], onesb[:], M.USQ[:, :, :], True, True, [onesbr, M.USQr], pssr)
            I("dve", "scalar_tensor_tensor", [pssr, M.T1r], [M.T2r], out=M.T2[:, :, :], in0=v4(pss), scalar=1.0 / 128, in1=M.T1[:, :, :],
              op0=ALU.mult, op1=ALU.add)
            I("act", "activation", [M.T2r], [M.T2r], out=M.T2[:, :, :], in_=M.T2[:, :, :], func=AF.Ln)
            I("act", "activation", [M.T2r], [M.T2r], out=M.T2[:, :, :], in_=M.T2[:, :, :], func=AF.Exp, scale=-0.5)
            I("dve", "tensor_tensor", [pnumr, M.T2r], [M.T1r], out=M.T1[:, :, :], in0=v4(pnum), in1=M.T2[:, :, :], op=ALU.mult)
            for h in range(4):
                I("dve", "scalar_tensor_tensor", [M.T1r, gheadr, M.SGTr], [hres], out=M.HMT[:, h, cs], in0=M.T1[:, h, :],
                  scalar=gheadc[:, h:h + 1], in1=M.SGT[:, h, cs], op0=ALU.mult, op1=ALU.mult)

        def swa_tile(ti, kcol0, vslots, mb, mbr, ktres):
            qs = slice(ti * 128, (ti + 1) * 128)
            for h in range(2):
                bks = [bank(), bank()]
                for g in range(4):
                    bk, bkr = bks[g // 2]
                    o = bk[:, (g % 2) * 256:(g % 2 + 1) * 256]
                    mm(o, M.QT[:, 4 * h + g, qs], M.KT[:, h, kcol0:kcol0 + 256], True, False, [M.QTr] + ktres, bkr)
                    mm(o, identb[:], mb, False, True, [identr, mbr], bkr)
                for j in range(2):
                    I("dve", "reduce_max", [bks[j][1]], [M.smr], out=M.sm_st[:, 2 * j:2 * j + 2],
                      in_=bks[j][0][:, :].rearrange("p (a b) -> p a b", a=2), axis=AX.X)
                I("dve", "tensor_scalar", [M.smr], [M.smr], out=M.sm_st[:, 0:4], in0=M.sm_st[:, 0:4], scalar1=-0.125, scalar2=None,
                  op0=ALU.mult)
                I("dve", "tensor_tensor", [M.smr, sinkbr], [M.smr], out=M.sm_st[:, 0:4], in0=M.sm_st[:, 0:4],
                  in1=sinkb[:, 8 + 4 * h:12 + 4 * h], op=ALU.min)
                for g in range(4):
                    bk, bkr = bks[g // 2]
                    I("act", "activation", [bkr, M.smr], [M.Er, M.smr], out=M.Ebuf[:, g, :], in_=bk[:, (g % 2) * 256:(g % 2 + 1) * 256],
                      func=AF.Exp, bias=M.sm_st[:, g:g + 1], scale=0.125, accum_out=M.sm_st[:, 4 + g:5 + g])
                I("dve", "tensor_tensor", [M.smr, sinkbr], [M.smr], out=M.sm_st[:, 8:12], in0=M.sm_st[:, 0:4],
                  in1=sinkb[:, 4 * h:4 * h + 4], op=ALU.add)
                I("act", "activation", [M.smr], [M.smr], out=M.sm_st[:, 8:12], in_=M.sm_st[:, 8:12], func=AF.Exp)
                I("dve", "tensor_tensor", [M.smr], [M.smr], out=M.sm_st[:, 8:12], in0=M.sm_st[:, 8:12], in1=M.sm_st[:, 4:8], op=ALU.add)
                I("dve", "reciprocal", [M.smr], [M.smr], out=M.sm_st[:, 12:16], in_=M.sm_st[:, 8:12])
                for g in range(4):
                    if g % 2 == 0:
                        I("act", "activation", [M.Er, M.smr], [M.Er], out=M.Ebuf[:, g, :], in_=M.Ebuf[:, g, :], func=AF.Copy,
                          scale=M.sm_st[:, 12 + g:13 + g])
                    else:
                        I("dve", "tensor_scalar", [M.Er, M.smr], [M.Er], out=M.Ebuf[:, g, :], in0=M.Ebuf[:, g, :],
                          scalar1=M.sm_st[:, 12 + g:13 + g], scalar2=None, op0=ALU.mult)
                for kb in range(2):
                    for g in range(4):
                        blk = kb * 4 + (g % 2) * 2 + g // 2
                        I("pe", "transpose", [M.Er, identr], [*tbhr], out=tb[:, blk * 128:(blk + 1) * 128],
                          in_=M.Ebuf[:, g, kb * 128:(kb + 1) * 128], identity=identb[:])
                pb = 0
                if h == 0:
                    I("dve", "tensor_copy", [*tbhr], [M.PTsr[pb]], out=M.PTs[:, pb, :], in_=tb[:, :])
                else:
                    I("act", "activation", [*tbhr], [M.PTsr[pb]], out=M.PTs[:, pb, :], in_=tb[:, :], func=AF.Copy)
                po, por = bank()
                for par in range(2):
                    for kb in range(2):
                        mm(po[par * 64:(par + 1) * 64, 0:256], M.Vt[:, vslots[kb], h * 64:(h + 1) * 64],
                           M.PTs[:, pb, kb * 512 + par * 256:kb * 512 + (par + 1) * 256], kb == 0, kb == 1,
                           [M.Vtr[vslots[kb]], M.PTsr[pb]], por)
                I("act", "activation", [por], [M.ATTTr[ti]], out=M.ATTT[:, 2 * h:2 * h + 2, qs],
                  in_=po[:, 0:256].rearrange("p (g q) -> p g q", g=2), func=AF.Copy)

        def wout_tile(ti, t):
            qs = slice(ti * 128, (ti + 1) * 128)
            for c in range(2):
                bk, bkr = bank()
                cc = slice(c * 512, (c + 1) * 512)
                for hg in range(4):
                    mm(bk[:, :], M.ATTT[:, hg, qs], M.WOA[:, hg, cc], hg == 0, False, [M.ATTTr[ti], M.WOAr], bkr)
                for h in range(4):
                    mm(bk[:, :], M.HMT[:, h, qs], M.WOM[:, h, cc], False, h == 3, [M.HMTr[ti], M.WOMr], bkr)
                I("dve", "tensor_tensor", [Yr[t], bkr], [Yr[t]], out=Y[:, t, cc], in0=Y[:, t, cc], in1=bk[:, :], op=ALU.add)

        xpre_t = xpre_d.rearrange("(t p) d -> t p d", p=128)
        xp_t = xp_d.rearrange("(t p) d -> t p d", p=128)
        for t in range(NTP):
            P.dma("sp", Y[:, t, :], xpre_t[t], Yr[t], writes=[Yr[t]])

        tb4sel = (banks[4][:, :].bitcast(BF16)[:, 0:512], bres[4])

        def prep_prefix(g0, pb):
            use(pb)
            for ti in range(GT):
                t = g0 + ti
                norm_T(Y[:, t, :], Yr[t], 0, M.XNTg[:, :, ti * 128:(ti + 1) * 128], [M.XNTgr[ti]], half=1, tsel=tb4sel)
            for ti in range(GT):
                tok_major(ti, slice(ti * 128, (ti + 1) * 128), M.XNTgr[ti], 0)

        prep_prefix(0, 0)
        for gi, g0 in enumerate(range(0, NTP, GT)):
            pb = gi % 2
            use(pb)
            P.rec_begin(); bset[0] = [0, 1, 2, 3]
            gates(GT * 128, M.XNTgr, True)
            if g0 + GT == NTP:
                bk, bkr = bank()
                for h in range(2):
                    for k in range(8):
                        mm(bk[0:64, h * 128:(h + 1) * 128], M.WK[:, k, h * 64:(h + 1) * 64], M.XNTg[:, k, (GT - 1) * 128:GT * 128],
                           k == 0, k == 7, [M.WKr, M.XNTgr[GT - 1]], bkr)
                I("act", "activation", [bkr], [M.KTr[0]], out=M.KT[:, :, 0:128],
                  in_=bk[0:64, 0:256].rearrange("p (a b) -> p a b", a=2), func=AF.Copy)
            for ti in range(GT):
                last = (g0 + ti == NTP - 1)
                state_update(ti, ti * 128, last)
            gates_carry(GT * 128)
            sA = P.rec_end()
            strs = [sA]
            if g0 + GT < NTP:
                P.rec_begin(); bset[0] = [4]
                prep_prefix(g0 + GT, pb ^ 1)
                strs.append(P.rec_end())
                use(pb)
            P.merge(strs)
            bset[0] = [0, 1, 2, 3, 4]

        for t in range(NTP):
            P.dma("sp", Y[:, t, :], xp_t[t], Yr[t], writes=[Yr[t]])

        def prep_main(g0, pb, merged):
            use(pb)
            if merged:
                for ti in range(GT):
                    t = g0 + ti
                    norm_T(Y[:, t, :], Yr[t], 0, M.XNTg[:, :, ti * 128:(ti + 1) * 128], [M.XNTgr[ti]], half=1, tsel=tb4sel)
            else:
                for ti in range(GT):
                    t = g0 + ti
                    norm_T(Y[:, t, :], Yr[t], 0, M.XNTg[:, :, ti * 128:(ti + 1) * 128], [M.XNTgr[ti]])
            for ti in range(GT):
                t = g0 + ti
                tok_major(ti, slice(ti * 128, (ti + 1) * 128), M.XNTgr[ti], 1 + t, want_kv_out=(True if t == NTP - 1 else None))

        prep_main(0, 0, False)
        for gi, g0 in enumerate(range(0, NTP, GT)):
            pb = gi % 2
            use(pb)
            gates(M.NG, M.XNTgr, False)
            feat64(M.WK, M.WKr, 2, M.KT, [M.KTr[1 + g0 + i] for i in range(GT)], M.NG, M.XNTgr, dcol0=128 + g0 * 128)
            feat64(M.WQ, M.WQr, 8, M.QT, [M.QTr], M.NG, M.XNTgr)
            feat64(M.WMQ, M.WMQr, 4, M.MQT, [M.MQTr], M.NG, M.XNTgr)
            feat64(M.WMK, M.WMKr, 4, M.MKT, [M.MKTr], M.NG, M.XNTgr, scale=0.125)
            for h0 in (0, 2):
                bk, bkr = bank()
                for hh in range(2):
                    h = h0 + hh
                    for k in range(8):
                        mm(bk[:, hh * 256:hh * 256 + M.NG], M.WOG[:, k, h * 128:(h + 1) * 128], M.XNTg[:, k, 0:M.NG], k == 0, k == 7,
                           [M.WOGr] + M.XNTgr, bkr)
                sgv = M.SGT[:, h0:h0 + 2, :]
                I("act", "activation", [bkr], [M.SGTr], out=sgv, in_=bk[:, :].rearrange("p (a b) -> p a b", a=2)[:, :, 0:M.NG],
                  func=AF.Exp, scale=-1.0)
                I("act", "activation", [M.SGTr], [M.SGTr], out=sgv, in_=sgv, func=AF.Ln, bias=1.0)
                I("act", "activation", [M.SGTr], [M.SGTr], out=sgv, in_=sgv, func=AF.Exp, scale=-1.0)
            pend = None
            for ti in range(GT):
                t = g0 + ti
                P.rec_begin(); bset[0] = [2, 3]
                pn = mlstm_chunk(ti, ti * 128, mbcaus[:], mbcausr)
                mlstm_finish(*pn, ti * 128, M.HMTr[ti])
                state_update(ti, ti * 128, True)
                s_ml = P.rec_end()
                P.rec_begin(); bset[0] = [0, 1]
                swa_tile(ti, t * 128, (t, t + 1), (mbfirst[:] if t == 0 else mbband[:]), (mbfirstr if t == 0 else mbbandr),
                         [M.KTr[t], M.KTr[t + 1]])
                s_sw = P.rec_end()
                strs = [s_ml, s_sw]
                P.rec_begin(); bset[0] = [4]
                if pend is not None:
                    wout_tile(*pend)
                if ti == 0 and g0 + GT < NTP:
                    prep_main(g0 + GT, pb ^ 1, True)
                    use(pb)
                strs.append(P.rec_end())
                P.merge(strs)
                pend = (ti, t)
            bset[0] = [0, 1, 2, 3, 4]
            wout_tile(*pend)
            if debug and g0 == DBG_G0:
                dA = dout("dbg_att", [128, 4, M.NG]); dH = dout("dbg_hm", [128, 4, M.NG])
                P.dma("pool", dA, M.ATTT[:, :, :], M.ATTTr[0], reads=M.ATTTr)
                P.dma("pool", dH, M.HMT[:, :, :], M.HMTr[0], reads=M.HMTr)
            gates_carry(M.NG)

        CO = A([4, 64], F32); COr = AR("CO")
        for h in range(4):
            bk, bkr = bank()
            mm(bk[:, 0:64], Cst[:, h, 0:128], identf[0:64, 0:64], True, True, [Cstr[h], identfr], bkr)
            I("act", "activation", [bkr], [COr], out=CO[:, h, :], in_=bk[:, 0:64], func=AF.Copy)
        P.dma("sp", Cp_o.rearrange("h p k -> p h k"), CO[:, :, :], COr, reads=[COr])
        for h in range(4):
            P.dma("sp", np_o[h, :].rearrange("(k o) -> k o", o=1), Cst[:, h, 128:129], Cstr[h], reads=[Cstr[h]], allow_slow_non_contiguous=True)
        P.dma("sp", mp_o, M.G_BM[:, M.NG - 1:M.NG], M.G_BMr, reads=[M.G_BMr], allow_slow_non_contiguous=True)

        new_phase()
        MA = M
        M = alloc_mixer(1, 1, 1, reuse=MA)
        R0_olds = [M.WQr, M.WTOKr, M.WMQr, M.WOGr, M.WGTr]
        TS = NTP
        P.dma("sp", Y[:, TS, :], xs_d, Yr[TS], writes=[Yr[TS]])
        shk = P.res("shk"); shv = P.res("shv")
        P.dma("sp", sks_o[:, 0:120, :], csk_d[:, 8:128, :], shk, writes=[shk])
        P.dma("sp", svs_o[:, 0:120, :], csv_d[:, 8:128, :], shv, writes=[shv])
        CKn = A([16, 128], BF16); CKnr = AR("CKn")
        CV = A([16, 128], BF16); CVr = AR("CV")
        CKT = A([16, 128], BF16, parts=64); CKTr = AR("CKT")
        SMC = A([1, 128], BF16, parts=32)[:, 0, :]; SMCr = AR("SMC")
        SMN = A([16, 128], BF16, parts=32); SMNr = AR("SMN")
        SINKC = A([1, 4], F32, parts=32)[:, 0, :]; SINKCr = AR("SINKC")
        mbcs = A([1, 128], BF16)[:, 0, :]; mbcsr = AR("mbcs")
        PNs = A([4, 256], BF16, parts=32); PNsr = [AR(f"PNs{i}") for i in range(4)]
        sms = A([4, 8], F32, parts=32); smsr = [AR(f"sms{i}") for i in range(4)]
        PTS = A([1, 1024], BF16)[:, 0, :]; PTSr = AR("PTS")
        M0 = A([1, 16], F32, parts=4)[:, 0, :]; M0r = AR("M0")
        MTe = A([1, 128], F32, parts=4)[:, 0, :]; MTer = AR("MTe")
        DMT = A([1, 16], F32, parts=4)[:, 0, :]; DMTr = AR("DMT")
        E16 = A([1, 16], F32)[:, 0, :]; E16r = AR("E16")
        EW = A([4, 16], BF16); EWr = AR("EW")
        WCB = A([4, 16], F32); WCBr = AR("WCB")
        SNn = A([1, 64], F32, parts=64)[:, 0, :]; SNnr = AR("SNn")
        SNT = A([1, 64], F32, parts=64)[:, 0, :]; SNTr = AR("SNT")
        NNT = A([1, 64], F32, parts=64)[:, 0, :]; NNTr = AR("NNT")
        NNo = A([1, 64], F32, parts=64)[:, 0, :]; NNor = AR("NNo")
        BTf = A([1, 128], F32)[:, 0, :]; BTfr = AR("BTf")
        P.dma("pool", CKn[:, :, :], csk_d.rearrange("j p c -> p j c"), CKnr, writes=[CKnr])
        P.dma("pool", CV[:, :, :], csv_d.rearrange("j p c -> p j c"), CVr, writes=[CVr])
        P.dma("pool", SMC, smc_d, SMCr, writes=[SMCr])
        P.dma("pool", SMN[:, :, :], smn_d, SMNr, writes=[SMNr])
        P.dma("sp", SINKC[:, 0:2], sinkcol_d, SINKCr, writes=[SINKCr])
        I("dve", "tensor_scalar", [SINKCr], [SINKCr], out=SINKC[:, 2:4], in0=SINKC[:, 0:2], scalar1=-1.0, scalar2=None, op0=ALU.mult)
        P.dma("pool", mbcs, mb_causs_d, mbcsr, writes=[mbcsr])
        P.dma("sp", M0, sm_d.rearrange("j h -> h j"), M0r, writes=[M0r], allow_slow_non_contiguous=True)
        P.dma("sp", E16, eseq_d, E16r, writes=[E16r])
        P.dma("sp", SNn, sn_d.rearrange("j h k -> (j h) k"), SNnr, writes=[SNnr])

        norm_T(Y[:, TS, :], Yr[TS], 0, M.XNTg[:, :, 0:128], [M.XNTgr[0]])
        tok_major(0, slice(0, 128), M.XNTgr[0], 0, want_kv_out="sample")
        gates(128, M.XNTgr, "sample")
        feat64(M.WK, M.WKr, 2, M.KT, [M.KTr[0]], 128, M.XNTgr, dcol0=0)
        feat64(M.WQ, M.WQr, 8, M.QT, [M.QTr], 128, M.XNTgr)
        feat64(M.WMQ, M.WMQr, 4, M.MQT, [M.MQTr], 128, M.XNTgr)
        feat64(M.WMK, M.WMKr, 4, M.MKT, [M.MKTr], 128, M.XNTgr, scale=0.125)
        for h0 in (0, 2):
            bk, bkr = bank()
            for hh in range(2):
                h = h0 + hh
                for k in range(8):
                    mm(bk[:, hh * 256:hh * 256 + 128], M.WOG[:, k, h * 128:(h + 1) * 128], M.XNTg[:, k, 0:128], k == 0, k == 7,
                       [M.WOGr] + M.XNTgr, bkr)
            sgv = M.SGT[:, h0:h0 + 2, :]
            I("act", "activation", [bkr], [M.SGTr], out=sgv, in_=bk[:, :].rearrange("p (a b) -> p a b", a=2)[:, :, 0:128],
              func=AF.Exp, scale=-1.0)
            I("act", "activation", [M.SGTr], [M.SGTr], out=sgv, in_=sgv, func=AF.Ln, bias=1.0)
            I("act", "activation", [M.SGTr], [M.SGTr], out=sgv, in_=sgv, func=AF.Exp, scale=-1.0)
        for j in range(16):
            I("dve", "tensor_tensor_scan", [M.Gr, M.G_Br, onesfr], [M.G_Br], out=M.G_B[:, 1 + 8 * j:9 + 8 * j],
              data0=onesf[0:4, 0:8], data1=M.G_L1[:, 8 * j:8 * j + 8], initial=0.0, op0=ALU.mult, op1=ALU.subtract)
        I("dve", "tensor_tensor", [M.Gr, M.G_Br], [M.G_Ar], out=M.G_A[:, 0:128], in0=M.G_IG[:, 1:129], in1=M.G_B[:, 1:129],
          op=ALU.subtract)
        for j in range(16):
            I("dve", "tensor_tensor_scan", [M.G_Ar, M.G_Mr, onesfr, M0r], [M.G_Mr], out=M.G_M[:, 1 + 8 * j:9 + 8 * j],
              data0=onesf[0:4, 0:8], data1=M.G_A[:, 8 * j:8 * j + 8], initial=M0[:, j:j + 1], op0=ALU.mult, op1=ALU.max)
        I("dve", "tensor_tensor", [M.G_Br, M.G_Mr], [M.G_BMr], out=M.G_BM[:, 0:128], in0=M.G_B[:, 1:129], in1=M.G_M[:, 1:129],
          op=ALU.add)
        GM3 = M.G_M[:, 1:129].rearrange("p (j i) -> p j i", i=8)
        I("dve", "tensor_tensor", [M.G_Mr, M0r], [M.G_DMr], out=M.G_DM[:, 0:128].rearrange("p (j i) -> p j i", i=8), in0=GM3,
          in1=M0[:, :].unsqueeze(2).broadcast_to([4, 16, 8]), op=ALU.subtract)
        I("dve", "tensor_copy", [M.G_Mr], [MTer], out=MTe[:, :].rearrange("p (j i) -> p j i", i=8),
          in_=GM3[:, :, 7:8].broadcast_to([4, 16, 8]))
        I("dve", "tensor_tensor", [M.G_Mr, M0r], [DMTr], out=DMT[:, :].unsqueeze(2), in0=GM3[:, :, 7:8], in1=M0[:, :].unsqueeze(2),
          op=ALU.subtract)
        P.dma("sp", ms_o.rearrange("j h -> h j"), M.G_BM[:, 0:128].rearrange("p (j i) -> p j i", i=8)[:, :, 7], M.G_BMr,
              reads=[M.G_BMr], allow_slow_non_contiguous=True)

        pair_i = [0]
        QS = A([2, 16, 32], BF16, parts=64); QSr = AR("QS")
        for h in range(2):
            for par in range(2):
                I("act", "activation", [M.QTr], [QSr],
                  out=QS[:, h, :, par * 16:(par + 1) * 16].rearrange("p j (gp i) -> p j gp i", i=8),
                  in_=M.QT[:, 4 * h:4 * h + 4, :].rearrange("p (gp two) t -> p two gp t", two=2)[:, par].rearrange(
                      "p gp (j i) -> p j gp i", i=8), func=AF.Copy)
        for h in range(2):
            for q4 in range(2):
                for jj in range(8):
                    j = q4 * 8 + jj
                    I("pe", "transpose", [CKnr, identr], [*tbhr], out=tb[0:64, jj * 128:(jj + 1) * 128],
                      in_=CKn[:, j, h * 64:(h + 1) * 64], identity=identb[:])
                I("act", "activation", [*tbhr], [CKTr], out=CKT[:, q4 * 8:(q4 + 1) * 8, :],
                  in_=tb[0:64, :].rearrange("p (a b) -> p a b", a=8), func=AF.Copy)
            for half in range(2):
                po, por = banks[4], bres[4]
                strs = []
                for sk in range(4):
                    P.rec_begin(); bset[0] = [sk]
                    for jj in (sk, sk + 4):
                        j = half * 8 + jj
                        b = sk
                        bk, bkr = bank()
                        lq = QS[:, h, j, :]
                        mm(bk[0:32, 0:128], lq, CKT[:, j, :], True, False, [QSr, CKTr], bkr)
                        mm(bk[0:32, 0:128], identb[0:32, 0:32], SMC, False, True, [identr, SMCr], bkr)
                        mm(bk[0:32, 128:256], lq, M.KT[:, h, 0:128], True, False, [QSr, M.KTr[0]], bkr)
                        mm(bk[0:32, 128:256], identb[0:32, 0:32], SMN[:, j, :], False, True, [identr, SMNr], bkr)
                        st_ = sms[:, b, :]
                        I("dve", "reduce_max", [bkr], [smsr[b]], out=st_[:, 0:1], in_=bk[0:32, 0:256], axis=AX.X)
                        I("dve", "tensor_scalar", [smsr[b]], [smsr[b]], out=st_[:, 0:1], in0=st_[:, 0:1], scalar1=-0.125,
                          scalar2=None, op0=ALU.mult)
                        I("dve", "tensor_tensor", [smsr[b], SINKCr], [smsr[b]], out=st_[:, 0:1], in0=st_[:, 0:1],
                          in1=SINKC[:, 2 + h:3 + h], op=ALU.min)
                        I("act", "activation", [bkr, smsr[b]], [PNsr[b], smsr[b]], out=PNs[:, b, :], in_=bk[0:32, 0:256],
                          func=AF.Exp, bias=st_[:, 0:1], scale=0.125, accum_out=st_[:, 1:2])
                        I("act", "activation", [SINKCr, smsr[b]], [smsr[b]], out=st_[:, 2:3], in_=SINKC[:, h:h + 1], func=AF.Exp,
                          bias=st_[:, 0:1])
                        I("dve", "tensor_tensor", [smsr[b]], [smsr[b]], out=st_[:, 2:3], in0=st_[:, 2:3], in1=st_[:, 1:2],
                          op=ALU.add)
                        I("dve", "reciprocal", [smsr[b]], [smsr[b]], out=st_[:, 3:4], in_=st_[:, 2:3])
                        I("dve", "tensor_scalar", [PNsr[b], smsr[b]], [PNsr[b]], out=PNs[:, b, :], in0=PNs[:, b, :],
                          scalar1=st_[:, 3:4], scalar2=None, op0=ALU.mult)
                        for c2 in range(2):
                            I("pe", "transpose", [PNsr[b], identr], [*tbhr], out=tb[:, jj * 64 + c2 * 32:jj * 64 + (c2 + 1) * 32],
                              in_=PNs[:, b, c2 * 128:(c2 + 1) * 128], identity=identb[0:32, 0:32])
                    strs.append(P.rec_end())
                P.merge(strs)
                bset[0] = [0, 1, 2, 3]
                I("dve", "tensor_copy", [*tbhr], [PTSr], out=PTS[:, 0:512], in_=tb[:, 0:512])
                for jj in range(8):
                    j = half * 8 + jj
                    for par in range(2):
                        o = po[par * 64:(par + 1) * 64, jj * 16:(jj + 1) * 16]
                        mm(o, CV[:, j, h * 64:(h + 1) * 64], PTS[:, jj * 64 + par * 16:jj * 64 + par * 16 + 16], True, False,
                           [CVr, PTSr], por)
                        mm(o, M.Vt[:, 0, h * 64:(h + 1) * 64], PTS[:, jj * 64 + 32 + par * 16:jj * 64 + 32 + par * 16 + 16], False,
                           True, [M.Vtr[0], PTSr], por)
                I("act", "activation", [por], [M.ATTTr[0]],
                  out=M.ATTT[:, 2 * h:2 * h + 2, half * 64:(half + 1) * 64].rearrange("p c (j i) -> p j c i", i=8),
                  in_=po[:, 0:128].rearrange("p (j c i) -> p j c i", c=2, i=8), func=AF.Copy)

        bset[0] = [0, 1, 2, 3, 4]
        off = 0
        SCf, off = A_at(off, [64, 64], F32); SCfr = ARalias("SCf", R0_olds)
        SCT, off = A_at(off, [64, 128], BF16, parts=64); SCTr = ARalias("SCT", R0_olds)
        QN, off = A_at(off, [4, 128], BF16, parts=64); QNr = ARalias("QN", R0_olds)
        KJ, off = A_at(off, [16, 64], BF16); KJr = ARalias("KJ", R0_olds)
        assert off <= 18496
        P.dma("sp", SCf[:, :, :], sC_d.rearrange("j h p k -> p (j h) k"), SCfr, writes=[SCfr])
        for p4 in range(16):
            bk, bkr = bank()
            for q_ in range(4):
                pr = p4 * 4 + q_
                mm(bk[0:64, q_ * 128:(q_ + 1) * 128], SCf[:, pr, :], identf[:], True, True, [SCfr, identfr], bkr)
            I("act", "activation", [bkr], [SCTr], out=SCT[:, p4 * 4:(p4 + 1) * 4, :],
              in_=bk[0:64, :].rearrange("p (a b) -> p a b", a=4), func=AF.Copy)
        bk, bkr = bank()
        mm(bk[0:64, 0:64], SNn, identf[0:64, 0:64], True, True, [SNnr, identfr], bkr)
        I("act", "activation", [bkr], [SNTr], out=SNT, in_=bk[0:64, 0:64], func=AF.Copy)

        def sample_inter(kind, h=None, pb=None, pbr=None):
            if kind == "pre":
                I("dve", "tensor_tensor", [M.QWr, SNTr], [QNr], out=QN[:, :, :].rearrange("p h (j i) -> p h j i", i=8),
                  in0=M.QW[:, :, :].rearrange("p h (j i) -> p h j i", i=8),
                  in1=SNT.rearrange("p (j h) -> p h j", h=4).unsqueeze(3).broadcast_to([64, 4, 16, 8]), op=ALU.mult)
            elif kind == "num":
                for j in range(16):
                    mm(pb[:, h * 128 + 8 * j:h * 128 + 8 * j + 8], SCT[:, j * 4 + h, :], M.QW[:, h, 8 * j:8 * j + 8], False, j == 15,
                       [SCTr, M.QWr], pbr)
            else:
                mm(pb[:, h * 128:(h + 1) * 128], onesb[0:64, :], QN[:, h, :], False, True, [onesbr, QNr], pbr)

        pn = mlstm_chunk(0, 0, mbcs, mbcsr, inter_fn=sample_inter)
        mlstm_finish(*pn, 0, M.HMTr[0])
        wout_tile(0, TS)

        pw, pwr = bank()
        for h in range(4):
            mm(pw[:, h:h + 1], M.G_A[:, 0:128], SELh(h, 1), True, False, [M.G_Ar, selr], pwr)
            mm(pw[:, h:h + 1], MTe, SEL[:, 512 + h * 128:512 + h * 128 + 1], False, True, [MTer, selr], pwr)
        for h in range(4):
            mm(pw[:, 8 + 16 * h:8 + 16 * (h + 1)], NSELh(h), DMT, True, True, [DMTr, selr], pwr)
        I("act", "activation", [pwr], [M.WKCr], out=M.WKC[:, 0:4], in_=pw[:, 0:4], func=AF.Exp)
        I("act", "activation", [pwr], [WCBr], out=WCB[:, :, :], in_=pw[:, 8:72].rearrange("p (h j) -> p h j", h=4), func=AF.Exp)
        for h in range(4):
            I("dve", "tensor_scalar", [M.MVaugr[0], M.WKCr], [M.VWr[h]], out=M.VW[:, h, :], in0=M.MVaug[:, 0, h, :],
              scalar1=M.WKC[:, h:h + 1], scalar2=None, op0=ALU.mult)
            I("dve", "tensor_scalar", [E16r, M.WKCr], [EWr], out=EW[:, h, :], in0=E16, scalar1=M.WKC[:, h:h + 1], scalar2=None,
              op0=ALU.mult)
        bk, bkr = bank()
        for h in range(4):
            mm(bk[0:64, h * 16:(h + 1) * 16], M.MKtok[:, 0, h * 64:(h + 1) * 64], EW[:, h, :], True, True, [M.MKtokr[0], EWr], bkr)
        I("dve", "tensor_tensor", [SNTr, WCBr], [NNTr], out=NNT.rearrange("p (j h) -> p h j", h=4),
          in0=SNT.rearrange("p (j h) -> p h j", h=4), in1=WCB[0:64, :, :], op=ALU.mult)
        I("dve", "tensor_tensor", [NNTr, bkr], [NNTr], out=NNT.rearrange("p (j h) -> p h j", h=4),
          in0=NNT.rearrange("p (j h) -> p h j", h=4), in1=bk[0:64, 0:64].rearrange("p (h j) -> p h j", h=4), op=ALU.add)
        bk2, bk2r = bank()
        mm(bk2[0:64, 0:64], NNT, identf[0:64, 0:64], True, True, [NNTr, identfr], bk2r)
        I("act", "activation", [bk2r], [NNor], out=NNo, in_=bk2[0:64, 0:64], func=AF.Copy)
        P.dma("sp", ns_o.rearrange("j h k -> (j h) k"), NNo, NNor, reads=[NNor])
        for h in range(4):
            I("dve", "tensor_tensor", [M.MKtokr[0], E16r], [KJr], out=KJ[:, :, :],
              in0=M.MKtok[:, 0, h * 64:(h + 1) * 64].unsqueeze(1).broadcast_to([128, 16, 64]),
              in1=E16.unsqueeze(2).broadcast_to([128, 16, 64]), op=ALU.mult)
            for half in range(2):
                bk, bkr = bank()
                mm(bk[:, :], M.VW[:, h, 0:128], KJ[:, half * 8:(half + 1) * 8, :], True, True, [M.VWr[h], KJr], bkr)
                scv = SCf[:, :, :].rearrange("p (j h) k -> p h j k", h=4)[:, h, half * 8:(half + 1) * 8, :]
                I("dve", "tensor_tensor", [SCfr, WCBr], [SCfr], out=scv, in0=scv,
                  in1=WCB[:, h, half * 8:(half + 1) * 8].unsqueeze(2).broadcast_to([128, 8, 64]), op=ALU.mult)
                I("dve", "tensor_tensor", [SCfr, bkr], [SCfr], out=scv, in0=scv,
                  in1=bk[:, :].rearrange("p (j k) -> p j k", k=64), op=ALU.add)
        P.dma("sp", Cs_o.rearrange("j h p k -> p (j h) k"), SCf[:, :, :], SCfr, reads=[SCfr])

        def dump_y(tiles):
            yo = yp_o.rearrange("(t p) d -> t p d", p=128)
            for t in tiles:
                if Yr[t].last_w is None:
                    continue
                if t < NTP:
                    P.dma("sp", yo[t], Y[:, t, :], Yr[t], reads=[Yr[t]])
                else:
                    P.dma("sp", ys_o, Y[:, t, :], Yr[t], reads=[Yr[t]])

        if stage <= 1:
            dump_y(range(NT))
            P.emit()
            return nc, P

        new_phase()
        WCQ = A([8, 256], BF16); WCQr = AR("WCQ")
        WCKV = A([8, 512], BF16); WCKVr = AR("WCKV")
        WCO = A([2, 1024], BF16); WCOr = AR("WCO")
        P.dma("pool", WCQ[:], w_cq_d.rearrange("(k p) n -> p k n", p=128), WCQr, writes=[WCQr])
        P.dma("pool", WCKV[:, :, 0:256], w_ck_d.rearrange("(k p) n -> p k n", p=128), WCKVr, writes=[WCKVr], group=True)
        P.dma("pool", WCKV[:, :, 256:512], w_cv_d.rearrange("(k p) n -> p k n", p=128), WCKVr, writes=[WCKVr], group=True)
        P.dma("pool", WCO[:], w_co_d.rearrange("(c p) n -> p c n", p=128), WCOr, writes=[WCOr])
        MEMX = A([2, D], F32); MEMXr = [AR("MEMX0"), AR("MEMX1")]
        MNT = A([8, 256], BF16); MNTr = [AR("MNT0"), AR("MNT1")]
        MKTm = A([4, 256], BF16, parts=64); MKTmr = AR("MKTm")
        MVm = A([2, 256], BF16); MVmr = AR("MVm")
        MKVo = A([2, 512], F32); MKVor = [AR("MKVo0"), AR("MKVo1")]
        GB = 4
        XNTb = A([8, GB * 128], BF16); XNTbr = [AR(f"XNTb{i}") for i in range(GB)]
        QcT = A([4, GB * 128], BF16, parts=64); QcTr = AR("QcT")
        OcT = A([2, GB * 128], BF16); OcTr = [AR(f"OcT{i}") for i in range(GB)]
        Eb2s = [A([4, 256], BF16) for _ in range(2)]; Eb2rs = [AR("Eb2a"), AR("Eb2b")]
        PT2s = [A([1, 1024], BF16) for _ in range(2)]; PT2rs = [AR("PT2a"), AR("PT2b")]
        sm2s = [A([1, 32], F32)[:, 0, :] for _ in range(2)]; sm2rs = [AR("sm2a"), AR("sm2b")]

        mem_t = mem_d.rearrange("(t p) d -> t p d", p=128)
        import os
        SK = os.environ.get("SKIP", "")
        for mt in range(2):
            P.dma("sp", MEMX[:, mt, :], mem_t[mt], MEMXr[mt], writes=[MEMXr[mt]])
        for mt in (range(2) if "noBnorm" not in SK else []):
            norm_T(MEMX[:, mt, :], MEMXr[mt], 2, MNT[:, :, mt * 128:(mt + 1) * 128], [MNTr[mt]])
        for mt in (range(2) if "noBkv" not in SK else []):
            bk, bkr = bank()
            for k in range(8):
                mm(bk[:, :], MNT[:, k, mt * 128:(mt + 1) * 128], WCKV[:, k, :], k == 0, k == 7, [MNTr[mt], WCKVr], bkr)
            if "noBcp1" not in SK:
                I("act", "activation", [bkr], [MKVor[mt]], out=MKVo[:, mt, :], in_=bk[:, :], func=AF.Copy)
            if "noBcp2" not in SK:
                I("act", "activation", [bkr], [MVmr], out=MVm[:, mt, :], in_=bk[:, 256:512], func=AF.Copy)
            if "noBdma" not in SK:
                P.dma("sp", memk_o[mt * 128:(mt + 1) * 128, :], MKVo[:, mt, 0:256], MKVor[mt], reads=[MKVor[mt]], group=True)
                P.dma("sp", memv_o[mt * 128:(mt + 1) * 128, :], MKVo[:, mt, 256:512], MKVor[mt], reads=[MKVor[mt]], group=True)
        for h0 in ((0, 2) if "noBkt" not in SK else []):
            bk, bkr = bank()
            for hh in range(2):
                h = h0 + hh
                for k in range(8):
                    mm(bk[0:64, hh * 256:(hh + 1) * 256], WCKV[:, k, h * 64:(h + 1) * 64], MNT[:, k, :], k == 0, k == 7,
                       [WCKVr] + MNTr, bkr)
            I("act", "activation", [bkr], [MKTmr], out=MKTm[:, h0:h0 + 2, :],
              in_=bk[0:64, :].rearrange("p (a b) -> p a b", a=2), func=AF.Copy)

        def cross_q(ntok, xres):
            for h in range(4):
                bk, bkr = bank()
                for k in range(8):
                    mm(bk[0:64, 0:ntok], WCQ[:, k, h * 64:(h + 1) * 64], XNTb[:, k, 0:ntok], k == 0, k == 7, [WCQr] + xres, bkr)
                I("act", "activation", [bkr], [QcTr], out=QcT[:, h, 0:ntok], in_=bk[0:64, 0:ntok], func=AF.Copy)

        def cross_tile_prompt(ti, sx):
            Eb2, Eb2r, PT2, PT2r, sm2, sm2r = Eb2s[sx], Eb2rs[sx], PT2s[sx], PT2rs[sx], sm2s[sx], sm2rs[sx]
            tbx, tbxr = TSEL[sx]
            qs = slice(ti * 128, (ti + 1) * 128)
            bks = [bank(), bank()]
            for h in range(4):
                bk, bkr = bks[h // 2]
                mm(bk[:, (h % 2) * 256:(h % 2 + 1) * 256], QcT[:, h, qs], MKTm[:, h, :], True, True, [QcTr, MKTmr], bkr)
            for j in range(2):
                I("dve", "reduce_max", [bks[j][1]], [sm2r], out=sm2[:, 2 * j:2 * j + 2],
                  in_=bks[j][0][:, :].rearrange("p (a b) -> p a b", a=2), axis=AX.X)
            I("dve", "tensor_scalar", [sm2r], [sm2r], out=sm2[:, 0:4], in0=sm2[:, 0:4], scalar1=-0.125, scalar2=None, op0=ALU.mult)
            for h in range(4):
                bk, bkr = bks[h // 2]
                I("act", "activation", [bkr, sm2r], [Eb2r, sm2r], out=Eb2[:, h, :], in_=bk[:, (h % 2) * 256:(h % 2 + 1) * 256],
                  func=AF.Exp, bias=sm2[:, h:h + 1], scale=0.125, accum_out=sm2[:, 4 + h:5 + h])
            I("dve", "reciprocal", [sm2r], [sm2r], out=sm2[:, 8:12], in_=sm2[:, 4:8])
            for h in range(4):
                if h % 2 == 0:
                    I("act", "activation", [Eb2r, sm2r], [Eb2r], out=Eb2[:, h, :], in_=Eb2[:, h, :], func=AF.Copy,
                      scale=sm2[:, 8 + h:9 + h])
                else:
                    I("dve", "tensor_scalar", [Eb2r, sm2r], [Eb2r], out=Eb2[:, h, :], in0=Eb2[:, h, :],
                      scalar1=sm2[:, 8 + h:9 + h], scalar2=None, op0=ALU.mult)
            po, por = bank()
            for mc in range(2):
                for h in range(4):
                    I("pe", "transpose", [Eb2r, identr], [tbxr], out=tbx[:, h * 128:(h + 1) * 128],
                      in_=Eb2[:, h, mc * 128:(mc + 1) * 128], identity=identb[:])
                if mc == 0:
                    I("dve", "tensor_copy", [tbxr], [PT2r], out=PT2[:, 0, 0:512], in_=tbx)
                else:
                    I("act", "activation", [tbxr], [PT2r], out=PT2[:, 0, 512:1024], in_=tbx, func=AF.Copy)
            for h in range(4):
                for mc in range(2):
                    mm(po[(h % 2) * 64:(h % 2 + 1) * 64, (h // 2) * 128:(h // 2 + 1) * 128], MVm[:, mc, h * 64:(h + 1) * 64],
                       PT2[:, 0, (mc * 4 + h) * 128:(mc * 4 + h + 1) * 128], mc == 0, mc == 1, [MVmr, PT2r], por)
            I("act", "activation", [por], [OcTr[ti]], out=OcT[:, :, qs], in_=po[:, 0:256].rearrange("p (h q) -> p h q", h=2),
              func=AF.Copy)

        def wco_tile(ti, t):
            qs = slice(ti * 128, (ti + 1) * 128)
            for c in range(2):
                bk, bkr = bank()
                cc = slice(c * 512, (c + 1) * 512)
                for h in range(2):
                    mm(bk[:, :], OcT[:, h, qs], WCO[:, h, cc], h == 0, h == 1, [OcTr[ti], WCOr], bkr)
                I("dve", "tensor_tensor", [Yr[t], bkr], [Yr[t]], out=Y[:, t, cc], in0=Y[:, t, cc], in1=bk[:, :], op=ALU.add)

        import os
        for g0 in (range(0, NTP, GB) if "noBloop" not in os.environ.get("SKIP", "") else []):
            for tp_ in range(0, GB, 2):
                strs = []
                for sx in range(2):
                    ti = tp_ + sx
                    P.rec_begin()
                    norm_T(Y[:, g0 + ti, :], Yr[g0 + ti], 1, XNTb[:, :, ti * 128:(ti + 1) * 128], [XNTbr[ti]], half=sx,
                           tsel=TSEL[sx])
                    strs.append(P.rec_end())
                P.merge(strs)
            cross_q(GB * 128, XNTbr)
            for tp_ in range(0, GB, 2):
                strs = []
                for sx in range(2):
                    P.rec_begin(); bset[0] = [0, 1, 2] if sx == 0 else [3, 4, 5]
                    cross_tile_prompt(tp_ + sx, sx)
                    wco_tile(tp_ + sx, g0 + tp_ + sx)
                    strs.append(P.rec_end())
                P.merge(strs)
            bset[0] = [0, 1, 2, 3, 4]
        TS = NTP
        CMn = A([8, 2, 256], BF16); CMnr = AR("CMn")
        CMV = A([16, 2, 256], BF16); CMVr = AR("CMV")
        CMKT = A([8, 4, 256], BF16, parts=64); CMKTr = AR("CMKT")
        Es = A([2, 4, 256], BF16, parts=8); Esr = [AR("Es0"), AR("Es1")]
        sm3 = A([2, 16], F32, parts=8); sm3r = [AR("sm30"), AR("sm31")]
        PT3 = A([1, 1024], BF16)[:, 0, :]; PT3r = AR("PT3")
        P.dma("pool", CMV[:, :, :, :], cmv_d.rearrange("j (c p) f -> p j c f", p=128), CMVr, writes=[CMVr])
        norm_T(Y[:, TS, :], Yr[TS], 1, XNTb[:, :, 0:128], [XNTbr[0]])
        cross_q(128, [XNTbr[0]])
        po3, po3r = banks[5], bres[5]
        for half in range(2):
            P.dma("pool", CMn[:, :, :, :], cmk_d[half * 8:(half + 1) * 8].rearrange("j (c p) f -> p j c f", p=128), CMnr,
                  writes=[CMnr])
            for jj in range(8):
                for h in range(4):
                    for mc in range(2):
                        I("pe", "transpose", [CMnr, identr], [*tbhr], out=tb[0:64, (h * 2 + mc) * 128:(h * 2 + mc + 1) * 128],
                          in_=CMn[:, jj, mc, h * 64:(h + 1) * 64], identity=identb[:])
                I("act", "activation", [*tbhr], [CMKTr], out=CMKT[:, jj, :, :],
                  in_=tb[0:64, :].rearrange("p (h m) -> p h m", h=4), func=AF.Copy)
            strs = []
            for sx in range(2):
                P.rec_begin(); bset[0] = [0, 1] if sx == 0 else [2, 3]
                for jj in range(sx, 8, 2):
                    j = half * 8 + jj
                    b = sx
                    bks = [bank(), bank()]
                    for h in range(4):
                        bk, bkr = bks[h // 2]
                        mm(bk[0:8, (h % 2) * 256:(h % 2 + 1) * 256], QcT[:, h, 8 * j:8 * j + 8], CMKT[:, jj, h, :], True, True,
                           [QcTr, CMKTr], bkr)
                    st_ = sm3[:, b, :]
                    for q_ in range(2):
                        I("dve", "reduce_max", [bks[q_][1]], [sm3r[b]], out=st_[:, 2 * q_:2 * q_ + 2],
                          in_=bks[q_][0][0:8, :].rearrange("p (a b) -> p a b", a=2), axis=AX.X)
                    I("dve", "tensor_scalar", [sm3r[b]], [sm3r[b]], out=st_[:, 0:4], in0=st_[:, 0:4], scalar1=-0.125, scalar2=None,
                      op0=ALU.mult)
                    for h in range(4):
                        bk, bkr = bks[h // 2]
                        I("act", "activation", [bkr, sm3r[b]], [Esr[b], sm3r[b]], out=Es[:, b, h, :],
                          in_=bk[0:8, (h % 2) * 256:(h % 2 + 1) * 256], func=AF.Exp, bias=st_[:, h:h + 1], scale=0.125,
                          accum_out=st_[:, 4 + h:5 + h])
                    I("dve", "reciprocal", [sm3r[b]], [sm3r[b]], out=st_[:, 8:12], in_=st_[:, 4:8])
                    I("dve", "tensor_tensor", [Esr[b], sm3r[b]], [Esr[b]], out=Es[:, b, :, :], in0=Es[:, b, :, :],
                      in1=st_[:, 8:12].unsqueeze(2).broadcast_to([8, 4, 256]), op=ALU.mult)
                    for mc in range(2):
                        for h in range(4):
                            c0_ = j * 64 + (mc * 4 + h) * 8
                            I("pe", "transpose", [Esr[b], identr], [*tbhr], out=tb[:, c0_:c0_ + 8],
                              in_=Es[:, b, h, mc * 128:(mc + 1) * 128], identity=identb[0:8, 0:8])

                strs.append(P.rec_end())
            P.merge(strs)
            bset[0] = [0, 1, 2, 3, 4]
            I("dve", "tensor_copy", [*tbhr], [PT3r], out=PT3[:, half * 512:(half + 1) * 512], in_=tb[:, half * 512:(half + 1) * 512])
        for j in range(16):
            for h in range(4):
                for mc in range(2):
                    c0_ = j * 64 + (mc * 4 + h) * 8
                    mm(po3[(h % 2) * 64:(h % 2 + 1) * 64, (j * 2 + h // 2) * 8:(j * 2 + h // 2) * 8 + 8],
                       CMV[:, j, mc, h * 64:(h + 1) * 64], PT3[:, c0_:c0_ + 8], mc == 0, mc == 1, [CMVr, PT3r], po3r)
        I("act", "activation", [po3r], [OcTr[0]], out=OcT[:, :, 0:128].rearrange("p c (j i) -> p j c i", i=8),
          in_=po3[:, 0:256].rearrange("p (j c i) -> p j c i", c=2, i=8), func=AF.Copy)
        wco_tile(0, TS)

        if stage <= 2:
            dump_y(range(NT))
            P.emit()
            return nc, P

        new_phase()
        USE_SQRT[0] = True
        NF = FH // 128
        XNTa = A([8, NT * 128], BF16); XNTar = [AR(f"XNTa{t}") for t in range(NT)]
        NSLOT = 12
        WG = A([NSLOT, 8, 128], BF16); WU = A([NSLOT, 8, 128], BF16); WD = A([NSLOT, D], BF16)
        Wsr = [AR(f"Ws{s_}") for s_ in range(NSLOT)]
        Hh = A([6, 512], BF16); Hr = [AR(f"H{j}") for j in range(6)]
        SG = A([2, 512], BF16); SGr = [AR("SG0"), AR("SG1")]
        OUT = A([1, D], F32)[:, 0, :]; OUTr = AR("OUT")
        gfin = A([1, D], F32)[:, 0, :]; gfinr = AR("gfin")
        P.dma("sp", gfin, g_final_d.partition_broadcast(128), gfinr, writes=[gfinr])
        passes = [list(range(0, 6)), list(range(6, 12)), list(range(12, 17)), list(range(17, 22))]
        groups = [(0, 4), (4, 4), (8, 4), (12, 4), (16, 1)]
        wd_v = w_down_d.rearrange("(f p) n -> f p n", p=128)
        wslot = {}
        nload = [0]

        def load_w(f):
            s_ = nload[0] % NSLOT
            nload[0] += 1
            wslot[f] = s_
            P.dma("pool", WG[:, s_], w_gate_d[:, f * 128:(f + 1) * 128].rearrange("(k p) n -> p k n", p=128), Wsr[s_],
                  writes=[Wsr[s_]], group=True)
            P.dma("pool", WU[:, s_], w_up_d[:, f * 128:(f + 1) * 128].rearrange("(k p) n -> p k n", p=128), Wsr[s_],
                  writes=[Wsr[s_]], group=True)
            P.dma("pool", WD[:, s_], wd_v[f], Wsr[s_], writes=[Wsr[s_]], group=True)

        for f in passes[0]:
            load_w(f)
        gcnt = [0]
        def ffn_norm_group(gi_):
            t0_, n_ = groups[gi_]
            for t in range(t0_, t0_ + n_):
                norm_T(Y[:, t, :], Yr[t], 3, XNTa[:, :, t * 128:(t + 1) * 128], [XNTar[t]], half=t % 2, tsel=TSEL[t % 2])

        ffn_norm_group(0)
        for pi, fl in enumerate(passes):
            for gi, (t0, n) in enumerate(groups):
                if pi + 1 < len(passes) and gi == 0:
                    for f in passes[pi + 1]:
                        load_w(f)
                merging = (pi == 0 and gi + 1 < len(groups))
                if merging:
                    P.rec_begin()
                    ffn_norm_group(gi + 1)
                    s_norm = P.rec_end()
                    P.rec_begin()
                ntok = n * 128
                tok = slice(t0 * 128, t0 * 128 + ntok)
                xr = [XNTar[t] for t in range(t0, t0 + n)]
                for j, f in enumerate(fl):
                    s_ = wslot[f]
                    b = gcnt[0] % 2
                    gcnt[0] += 1
                    pg, pgr = bank()
                    pu, pur = bank()
                    for k in range(8):
                        mm(pg[:, 0:ntok], WG[:, s_, k, :], XNTa[:, k, tok], k == 0, k == 7, [Wsr[s_]] + xr, pgr)
                    for k in range(8):
                        mm(pu[:, 0:ntok], WU[:, s_, k, :], XNTa[:, k, tok], k == 0, k == 7, [Wsr[s_]] + xr, pur)
                    I("act", "activation", [pgr], [SGr[b]], out=SG[:, b, 0:ntok], in_=pg[:, 0:ntok], func=AF.Silu)
                    I("dve", "tensor_tensor", [SGr[b], pur], [Hr[j]], out=Hh[:, j, 0:ntok], in0=SG[:, b, 0:ntok],
                      in1=pu[:, 0:ntok], op=ALU.mult)
                for ti in range(n):
                    t = t0 + ti
                    for c in range(2):
                        pd, pdr = bank()
                        for j, f in enumerate(fl):
                            s_ = wslot[f]
                            mm(pd[:, :], Hh[:, j, ti * 128:(ti + 1) * 128], WD[:, s_, c * 512:(c + 1) * 512], j == 0,
                               j == len(fl) - 1, [Hr[j], Wsr[s_]], pdr)
                        I("dve", "tensor_tensor", [Yr[t], pdr], [Yr[t]], out=Y[:, t, c * 512:(c + 1) * 512],
                          in0=Y[:, t, c * 512:(c + 1) * 512], in1=pd[:, :], op=ALU.add)
                if merging:
                    s_ffn = P.rec_end()
                    P.merge([s_ffn, s_norm])
                if pi == len(passes) - 1:
                    yo = yp_o.rearrange("(t p) d -> t p d", p=128)
                    for t in range(t0, t0 + n):
                        rstd, sr = norm_stats(Y[:, t, :], Yr[t], 0)
                        I("dve", "scalar_tensor_tensor", [Yr[t], sr, gfinr], [OUTr], out=OUT, in0=Y[:, t, :], scalar=rstd,
                          in1=gfin, op0=ALU.mult, op1=ALU.mult)
                        P.dma("sp", (yo[t] if t < NTP else ys_o), OUT, OUTr, reads=[OUTr])
        P.emit()
        return nc, P


def make_consts(hf):
    c = {}
    c["c_ident"] = np.eye(128, dtype=np.float32)
    i = np.arange(128)[:, None]; j = np.arange(256)[None, :]
    band = np.where((j >= i) & (j <= i + 128), 0.0, NEG).astype(np.float32)
    first = band.copy()
    if hf == 0:
        first[:, :128] = NEG
    c["c_mb_band"] = band; c["c_mb_first"] = first
    s = np.arange(128)[:, None]; t = np.arange(128)[None, :]
    c["c_mb_caus"] = np.where(s <= t, 0.0, NEG).astype(np.float32)
    c["c_mb_causs"] = np.where((s <= t) & (s // 8 == t // 8), 0.0, NEG).astype(np.float32)
    sel = np.zeros((4, 1024), np.float32)
    for h in range(4):
        sel[h, h * 128:(h + 1) * 128] = 1.0
        sel[h, 512 + h * 128:512 + (h + 1) * 128] = -1.0
    c["c_sel"] = sel
    pm = np.zeros((4, 2), np.float32)
    pm[:, 0] = 1.0 if hf else 0.0
    pm[:, 1] = 0.0 if hf else NEG
    c["c_pmask"] = pm
    r = np.arange(32)[:, None] % 8
    p = np.arange(128)[None, :]
    c["c_smc"] = np.where(p >= r, 0.0, NEG).astype(np.float32)
    smn = np.full((32, 16, 128), NEG, np.float32)
    for jq in range(16):
        for ii in range(8):
            smn[(np.arange(32) % 8) >= ii, jq, jq * 8 + ii] = 0.0
    c["c_smn"] = smn
    c["c_bt"] = np.where((s <= t) & (s // 8 == t // 8), 1.0, 0.0).astype(np.float32)
    e = np.zeros((128, 16), np.float32); e[np.arange(128), np.arange(128) // 8] = 1.0
    c["c_eseq"] = e
    return c

def shard_inputs(inp):
    maps = []
    W = ["w_in", "b_igate", "b_fgate", "attn_sinks", "g_mlstm_head", "w_out", "g_mix", "g_cross", "g_mem",
         "w_cq", "w_ck", "w_cv", "w_co", "g_ffn", "w_gate", "w_up", "w_down"]
    wd = {k: np.ascontiguousarray(np.asarray(inp[k], np.float32)[0]) for k in W}
    wd["g_final"] = np.ascontiguousarray(np.asarray(inp["g_final"], np.float32))
    xp = np.asarray(inp["x_prompt"], np.float32); xs = np.asarray(inp["x_sample"], np.float32)
    for c in range(8):
        b, hf = c // 2, c % 2
        m = dict(wd)
        m["xp"] = np.ascontiguousarray(xp[b, hf * 2048:(hf + 1) * 2048])
        m["xpre"] = np.ascontiguousarray(xp[b, 0:2048]) if hf else np.zeros((2048, 1024), np.float32)
        m["xs"] = np.ascontiguousarray(xs[16 * c:16 * c + 16].reshape(128, 1024))
        m["mem"] = np.ascontiguousarray(np.asarray(inp["mem_prompt"], np.float32)[b])
        sl = slice(16 * c, 16 * c + 16)
        m["csk"] = np.ascontiguousarray(np.asarray(inp["cache_swa_k"], np.float32)[0, sl].reshape(16, 128, 128))
        m["csv"] = np.ascontiguousarray(np.asarray(inp["cache_swa_v"], np.float32)[0, sl].reshape(16, 128, 128))
        m["sC"] = np.ascontiguousarray(np.asarray(inp["state_mlstm_C"], np.float32)[0, sl])
        m["sn"] = np.ascontiguousarray(np.asarray(inp["state_mlstm_n"], np.float32)[0, sl])
        m["sm"] = np.ascontiguousarray(np.asarray(inp["state_mlstm_m"], np.float32)[0, sl])
        m["cmk"] = np.ascontiguousarray(np.asarray(inp["cache_mem_k"], np.float32)[0, sl].reshape(16, 256, 256))
        m["cmv"] = np.ascontiguousarray(np.asarray(inp["cache_mem_v"], np.float32)[0, sl].reshape(16, 256, 256))
        m.update(make_consts(hf))
        sk = wd["attn_sinks"]
        sc = np.zeros((32, 2), np.float32)
        rr = np.arange(32)
        for h in range(2):
            sc[:, h] = sk[4 * h + 2 * ((rr % 16) // 8) + rr // 16]
        m["c_sinkcol"] = sc
        maps.append(m)
    return maps

def gather(res):
    f = np.float32
    yp = np.zeros((4, 4096, 1024), f); ys = np.zeros((128, 8, 1024), f)
    skp = np.zeros((1, 4, 128, 2, 64), f); svp = np.zeros_like(skp)
    Cp = np.zeros((1, 4, 4, 128, 64), f); npp = np.zeros((1, 4, 4, 64), f); mp = np.zeros((1, 4, 4), f)
    mkp = np.zeros((1, 4, 256, 4, 64), f); mvp = np.zeros_like(mkp)
    sks = np.zeros((1, 128, 128, 2, 64), f); svs = np.zeros_like(sks)
    Cs = np.zeros((1, 128, 4, 128, 64), f); ns = np.zeros((1, 128, 4, 64), f); ms = np.zeros((1, 128, 4), f)
    for c in range(8):
        r = res[c]; b, hf = c // 2, c % 2
        yp[b, hf * 2048:(hf + 1) * 2048] = r["yp"]
        ys[16 * c:16 * c + 16] = r["ys"].reshape(16, 8, 1024)
        if hf == 1:
            skp[0, b] = r["swak"].reshape(128, 2, 64); svp[0, b] = r["swav"].reshape(128, 2, 64)
            Cp[0, b] = r["Cp"]; npp[0, b] = r["np"]; mp[0, b] = r["mp"].reshape(4)
        else:
            mkp[0, b] = r["memk"].reshape(256, 4, 64); mvp[0, b] = r["memv"].reshape(256, 4, 64)
        sl = slice(16 * c, 16 * c + 16)
        sks[0, sl] = r["sks"].reshape(16, 128, 2, 64); svs[0, sl] = r["svs"].reshape(16, 128, 2, 64)
        Cs[0, sl] = r["Cs"]; ns[0, sl] = r["ns"]; ms[0, sl] = r["ms"]
    return (yp, ys, skp, svp, Cp, npp, mp, mkp, mvp, sks, svs, Cs, ns, ms)


_CACHE = {}


def kernel(**inputs):
    if "nc" not in _CACHE:
        _CACHE["nc"] = build_program(3)[0]
    nc = _CACHE["nc"]
    maps = shard_inputs(inputs)
    res = run_bass_kernel_spmd(nc, maps, core_ids=list(range(8)))
    return gather(res.results)
```

```python
import contextlib
from concourse.bass_utils import run_bass_kernel_spmd
import numpy as np
import concourse.bass as bass
import concourse.mybir as mybir

F32 = mybir.dt.float32
BF16 = mybir.dt.bfloat16
I32 = mybir.dt.int32
AF = mybir.ActivationFunctionType
ALU = mybir.AluOpType
AX = mybir.AxisListType

ENGS = ("pe", "act", "dve", "pool", "sp")


class Res:
    __slots__ = ("name", "last_w", "readers", "sem", "dcount", "excl")

    def __init__(self, name):
        self.name = name
        self.last_w = None
        self.readers = []
        self.sem = None
        self.dcount = 0
        self.excl = False


class Op:
    __slots__ = ("eng", "fn", "deps", "dma_res", "sig", "cnt", "k", "group")

    def __init__(self, eng, fn, dma_res):
        self.eng = eng
        self.fn = fn
        self.deps = set()
        self.dma_res = dma_res
        self.sig = False
        self.cnt = 0
        self.k = 0


class Prog:
    def __init__(self, nc):
        self.nc = nc
        self.ops = []
        self.nres = 0
        self.inherit = []
        self.phase_res = []

    def res(self, name=None, arena=False):
        self.nres += 1
        r = Res(name or f"r{self.nres}")
        if arena:
            r.readers = list(self.inherit)
            self.phase_res.append(r)
        return r

    def new_phase(self):
        inh = set(self.inherit)
        for r in self.phase_res:
            if r.last_w is not None:
                inh.add(r.last_w)
            inh.update(r.readers)
        self.inherit = sorted(inh)
        self.phase_res = []

    def rec_begin(self):
        self._rec = []

    def rec_end(self):
        r = self._rec
        self._rec = None
        return r

    def merge(self, streams):
        streams = [s_ for s_ in streams if s_]
        pos = [0] * len(streams)
        while True:
            best = None
            for k, s_ in enumerate(streams):
                if pos[k] < len(s_):
                    f = pos[k] / len(s_)
                    if best is None or f < best[0]:
                        best = (f, k)
            if best is None:
                break
            k = best[1]
            a, kw = streams[k][pos[k]]
            pos[k] += 1
            self.op(*a, **kw)

    def op(self, eng, fn, reads=(), writes=(), dma_res=None, accum=False, group=False):
        if getattr(self, "_rec", None) is not None:
            self._rec.append(((eng, fn, tuple(reads), tuple(writes)), dict(dma_res=dma_res, accum=accum, group=group)))
            return None
        i = len(self.ops)
        o = Op(eng, fn, dma_res)
        for r in reads:
            if r.last_w is not None:
                o.deps.add(r.last_w)
            if r.excl:
                for q in r.readers:
                    if self.ops[q].eng != eng:
                        o.deps.add(q)
            r.readers.append(i)
        for r in writes:
            if r.last_w is not None:
                lw = self.ops[r.last_w]
                if group and lw.dma_res is not None and lw.dma_res is dma_res:
                    o.deps |= lw.deps
                elif not (accum and lw.eng == "pe" and eng == "pe"):
                    o.deps.add(r.last_w)
            latest = {}
            for q in r.readers:
                if q == i:
                    continue
                oq = self.ops[q]
                if oq.dma_res is not None:
                    o.deps.add(q)
                elif latest.get(oq.eng, -1) < q:
                    latest[oq.eng] = q
            o.deps.update(latest.values())
            r.last_w = i
            r.readers = []
        if eng == "pe":
            o.deps = {d for d in o.deps if self.ops[d].eng != "pe" or self.ops[d].dma_res is not None}
        self.ops.append(o)
        return i

    def dma(self, eng, out, in_, res, reads=(), writes=(), group=False, **kw):
        kw = dict(kw); kw["out"] = out; kw["in_"] = in_
        return self.op(eng, ("dma_start", kw), reads=reads, writes=writes, dma_res=res, group=group)

    def I(self, eng, name, reads=(), writes=(), **kw):
        return self.op(eng, (name, kw), reads=reads, writes=writes)

    def emit(self, final_wait_all=True):
        nc = self.nc
        ops = self.ops
        for o in ops:
            for d in o.deps:
                ops[d].sig = True
        per_eng = {e: [] for e in ENGS}
        for i, o in enumerate(ops):
            per_eng[o.eng].append(i)
        import contextlib
        with contextlib.ExitStack() as st:
            esem = {e: st.enter_context(nc.semaphore(f"s_{e}")) for e in ENGS}
            ecount = {e: 0 for e in ENGS}
            dma_sems = []
            for i, o in enumerate(ops):
                if o.dma_res is not None:
                    r = o.dma_res
                    if r.sem is None:
                        r.sem = st.enter_context(nc.semaphore(f"d{len(dma_sems)}_{r.name}"))
                        dma_sems.append(r)
                    r.dcount += 1
                    o.cnt = 16 * r.dcount
                elif o.sig:
                    ecount[o.eng] += 1
                    o.cnt = ecount[o.eng]
            self.n_dma_sems = len(dma_sems)
            know = {e: {} for e in ENGS}
            know_issue = [None] * len(ops)

            def key_of(o):
                return ("d", id(o.dma_res)) if o.dma_res is not None else ("e", o.eng)

            block = st.enter_context(nc.Block())
            handles = {}

            plan = [None] * len(ops)
            for i, o in enumerate(ops):
                kn = know[o.eng]
                need = {}
                for d in o.deps:
                    p = ops[d]
                    k = key_of(p)
                    if kn.get(k, 0) >= p.cnt:
                        continue
                    if need.get(k, (0, None))[0] < p.cnt:
                        need[k] = (p.cnt, d)
                waits = []
                for k, (cnt, d) in need.items():
                    p = ops[d]
                    sem = p.dma_res.sem if p.dma_res is not None else esem[p.eng]
                    waits.append((sem, cnt))
                    kn[k] = max(kn.get(k, 0), cnt)
                    ki = know_issue[d]
                    for kk, vv in ki.items():
                        if kn.get(kk, 0) < vv:
                            kn[kk] = vv
                know_issue[i] = dict(kn)
                plan[i] = waits
            self.n_waits = sum(len(w) for w in plan)

            def make(ename):
                def body(eh):
                    for i in per_eng[ename]:
                        o = ops[i]
                        for sem, cnt in plan[i]:
                            eh.wait_ge(sem, cnt)
                        ins = getattr(eh, o.fn[0])(**o.fn[1])
                        if o.dma_res is not None:
                            ins.then_inc(o.dma_res.sem, 16)
                        elif o.sig:
                            ins.then_inc(esem[o.eng], 1)
                    if ename == "sp" and final_wait_all:
                        for r in dma_sems:
                            eh.wait_ge(r.sem, 16 * r.dcount)
                        for e in ("pe", "act", "dve", "pool"):
                            if ecount[e]:
                                eh.wait_ge(esem[e], ecount[e])
                return body

            block.tensor(make("pe"))
            block.scalar(make("act"))
            block.vector(make("dve"))
            block.gpsimd(make("pool"))
            block.sync(make("sp"))


D = 1024
FH = 2816
EPS = 1e-6
NTP = 16
NT = 17
GT = 2
NEG = -30000.0
DBG_G0 = 2


def build_program(stage=3, debug=False):
    nc = bass.Bass("TRN2", target_bir_lowering=False)
    P = Prog(nc)

    def din(name, shape, dt=F32):
        return nc.dram_tensor(name, list(shape), dt, kind="ExternalInput").ap()

    def dout(name, shape):
        return nc.dram_tensor(name, list(shape), F32, kind="ExternalOutput").ap()

    xp_d = din("xp", [2048, D]); xpre_d = din("xpre", [2048, D]); xs_d = din("xs", [128, D])
    mem_d = din("mem", [256, D])
    csk_d = din("csk", [16, 128, 128]); csv_d = din("csv", [16, 128, 128])
    sC_d = din("sC", [16, 4, 128, 64]); sn_d = din("sn", [16, 4, 64]); sm_d = din("sm", [16, 4])
    cmk_d = din("cmk", [16, 256, 256]); cmv_d = din("cmv", [16, 256, 256])
    w_in_d = din("w_in", [D, 2312]); b_i_d = din("b_igate", [4]); b_f_d = din("b_fgate", [4])
    sinks_d = din("attn_sinks", [8]); ghead_d = din("g_mlstm_head", [512]); w_out_d = din("w_out", [D, D])
    g_mix_d = din("g_mix", [D]); g_cross_d = din("g_cross", [D]); g_mem_d = din("g_mem", [D])
    w_cq_d = din("w_cq", [D, 256]); w_ck_d = din("w_ck", [D, 256]); w_cv_d = din("w_cv", [D, 256])
    w_co_d = din("w_co", [256, D]); g_ffn_d = din("g_ffn", [D])
    w_gate_d = din("w_gate", [D, FH]); w_up_d = din("w_up", [D, FH]); w_down_d = din("w_down", [FH, D])
    g_final_d = din("g_final", [D])
    ident_d = din("c_ident", [128, 128]); mb_band_d = din("c_mb_band", [128, 256]); mb_first_d = din("c_mb_first", [128, 256])
    mb_caus_d = din("c_mb_caus", [128, 128]); mb_causs_d = din("c_mb_causs", [128, 128])
    sel_d = din("c_sel", [4, 1024]); pmask_d = din("c_pmask", [4, 2])
    smc_d = din("c_smc", [32, 128]); smn_d = din("c_smn", [32, 16, 128]); sinkcol_d = din("c_sinkcol", [32, 2])
    bt_d = din("c_bt", [128, 128]); eseq_d = din("c_eseq", [128, 16])

    yp_o = dout("yp", [2048, D]); ys_o = dout("ys", [128, D])
    swak_o = dout("swak", [128, 128]); swav_o = dout("swav", [128, 128])
    Cp_o = dout("Cp", [4, 128, 64]); np_o = dout("np", [4, 64]); mp_o = dout("mp", [4, 1])
    memk_o = dout("memk", [256, 256]); memv_o = dout("memv", [256, 256])
    sks_o = dout("sks", [16, 128, 128]); svs_o = dout("svs", [16, 128, 128])
    Cs_o = dout("Cs", [16, 4, 128, 64]); ns_o = dout("ns", [16, 4, 64]); ms_o = dout("ms", [16, 4])

    st = contextlib.ExitStack()
    with st:
        def sb(name, shape, dt):
            return st.enter_context(nc.sbuf_tensor(name, list(shape), dt))

        def ps(name, shape, dt):
            return st.enter_context(nc.psum_tensor(name, list(shape), dt))

        banks = [ps(f"bk{i}", [128, 512], F32) for i in range(7)]
        bres = [P.res(f"bk{i}") for i in range(7)]
        for r_ in bres:
            r_.excl = True
        tb = ps("tb", [128, 1024], BF16)
        tbh = [tb[:, 0:512], tb[:, 512:1024]]
        tbhr = [P.res("tbA"), P.res("tbB")]
        for r_ in tbhr:
            r_.excl = True
        tb2 = banks[6][:, :].bitcast(BF16)
        TSEL = [(tb[:, 0:512], tbhr[0]), (tb2[:, 0:512], bres[6])]
        bki = [0]

        bset = [[0, 1, 2, 3, 4]]
        bcnt = {}

        def bank():
            key = tuple(bset[0])
            c = bcnt.get(key, 0)
            bcnt[key] = c + 1
            i = bset[0][c % len(key)]
            return banks[i], bres[i]

        Y = sb("Y", [128, NT, D], F32)
        Yr = [P.res(f"Y{t}") for t in range(NT)]
        identb = sb("identb", [128, 128], BF16); identr = P.res("identb")
        identf = sb("identf", [128, 128], F32); identfr = P.res("identf")
        onesb = sb("onesb", [128, 128], BF16); onesbr = P.res("onesb")
        onesf = sb("onesf", [128, 256], F32); onesfr = P.res("onesf")
        SEL = sb("SEL", [4, 1024], F32); selr = P.res("SEL")
        gcols = sb("gcols", [128, 4, 8], F32); gcolsr = P.res("gcols")
        gheadc = sb("gheadc", [128, 4], F32); gheadr = P.res("ghead")
        sinkb = sb("sinkb", [128, 16], F32); sinkbr = P.res("sinkb")
        gb4 = sb("gb4", [4, 4], F32); gb4r = P.res("gb4")
        mbband = sb("mbband", [128, 256], BF16); mbbandr = P.res("mbband")
        mbfirst = sb("mbfirst", [128, 256], BF16); mbfirstr = P.res("mbfirst")
        mbcaus = sb("mbcaus", [128, 128], BF16); mbcausr = P.res("mbcaus")
        stat = sb("stat", [128, 8, 4], F32)
        statr = [P.res(f"stat{i}") for i in range(8)]
        stati = [0]
        USE_SQRT = [False]
        xsb = sb("xsb", [128, 2, D], BF16); xsbr = [P.res("xsb0"), P.res("xsb1")]
        Cst = sb("Cst", [64, 4, 129], F32); Cstr = [P.res(f"Cst{h}") for h in range(4)]
        ARN = 64400
        arena = sb("arena", [128, ARN], BF16)
        aoff = [0]

        def A(shape, dt, parts=128, name=None):
            n = int(np.prod(shape))
            nb = n * (4 if dt == F32 else 2)
            n16 = (nb + 1) // 2
            n16 = (n16 + 15) // 16 * 16
            assert aoff[0] + n16 <= ARN, f"arena overflow {aoff[0]}+{n16} ({name})"
            v = arena[0:parts, aoff[0]:aoff[0] + n16]
            aoff[0] += n16
            if dt == F32:
                v = v.bitcast(F32)
            v = v[:, 0:n]
            if len(shape) == 2:
                v = v.rearrange("p (a b) -> p a b", a=shape[0])
            elif len(shape) == 3:
                v = v.rearrange("p (a b c) -> p a b c", a=shape[0], b=shape[1])
            return v

        def new_phase():
            P.new_phase()
            aoff[0] = 0

        def AR(name):
            return P.res(name, arena=True)

        def A_at(off, shape, dt, parts=128):
            n = int(np.prod(shape))
            nb = n * (4 if dt == F32 else 2)
            n16 = ((nb + 1) // 2 + 15) // 16 * 16
            v = arena[0:parts, off:off + n16]
            if dt == F32:
                v = v.bitcast(F32)
            v = v[:, 0:n]
            if len(shape) == 2:
                v = v.rearrange("p (a b) -> p a b", a=shape[0])
            elif len(shape) == 3:
                v = v.rearrange("p (a b c) -> p a b c", a=shape[0], b=shape[1])
            return v, off + n16

        def ARalias(name, olds):
            r = P.res(name, arena=True)
            dd = set(r.readers)
            for o_ in olds:
                if o_.last_w is not None:
                    dd.add(o_.last_w)
                dd.update(o_.readers)
            r.readers = sorted(dd)
            return r

        I = P.I

        def mm(out, lhsT, rhs, start, stop, reads, wres):
            I("pe", "matmul", reads, [wres], out=out, lhsT=lhsT, rhs=rhs, start=start, stop=stop)

        P.dma("pool", identb[:], ident_d, identr, writes=[identr])
        P.dma("sp", identf[:], ident_d, identfr, writes=[identfr])
        I("dve", "memset", [], [onesbr], ap=onesb[:], constant=1.0)
        I("dve", "memset", [], [onesfr], ap=onesf[:], constant=1.0)
        P.dma("sp", SEL[:], sel_d, selr, writes=[selr])
        for i, g in enumerate((g_mix_d, g_cross_d, g_mem_d, g_ffn_d)):
            P.dma("sp", gcols[:, i, :], g.rearrange("(k p) -> p k", p=128), gcolsr, writes=[gcolsr], group=True, allow_slow_non_contiguous=True)
        P.dma("sp", gheadc[:], ghead_d.rearrange("(h p) -> p h", p=128), gheadr, writes=[gheadr], allow_slow_non_contiguous=True)
        P.dma("sp", gb4[:, 0:1], b_i_d.rearrange("(h o) -> h o", o=1), gb4r, writes=[gb4r], group=True, allow_slow_non_contiguous=True)
        P.dma("sp", gb4[:, 1:2], b_f_d.rearrange("(h o) -> h o", o=1), gb4r, writes=[gb4r], group=True, allow_slow_non_contiguous=True)
        P.dma("sp", gb4[:, 2:4], pmask_d, gb4r, writes=[gb4r], group=True, allow_slow_non_contiguous=True)
        P.dma("sp", sinkb[:, 0:8], sinks_d.partition_broadcast(128), sinkbr, writes=[sinkbr])
        I("dve", "tensor_scalar", [sinkbr], [sinkbr], out=sinkb[:, 8:16], in0=sinkb[:, 0:8], scalar1=-1.0, scalar2=None,
          op0=ALU.mult)
        I("dve", "tensor_scalar", [gb4r], [gb4r], out=gb4[:, 1:2], in0=gb4[:, 1:2], scalar1=-1.0, scalar2=None, op0=ALU.mult)
        P.dma("pool", mbband[:], mb_band_d, mbbandr, writes=[mbbandr])
        P.dma("pool", mbfirst[:], mb_first_d, mbfirstr, writes=[mbfirstr])
        P.dma("pool", mbcaus[:], mb_caus_d, mbcausr, writes=[mbcausr])
        for h in range(4):
            I("dve", "memset", [], [Cstr[h]], ap=Cst[:, h, :], constant=0.0)

        def SELh(h, n=128):
            return SEL[:, h * 128:h * 128 + n]

        def NSELh(h, n=128):
            return SEL[:, 512 + h * 128:512 + h * 128 + n]

        def norm_stats(src, sres, jb=0):
            i = stati[0] % 8
            stati[0] += 1
            sr = statr[i]
            I("act", "activation", [sres], [xsbr[jb], sr], out=xsb[:, jb, :], in_=src, func=AF.Square, accum_out=stat[:, i, 0:1])
            I("dve", "tensor_scalar", [sr], [sr], out=stat[:, i, 1:2], in0=stat[:, i, 0:1], scalar1=1.0 / D, scalar2=EPS,
              op0=ALU.mult, op1=ALU.add)
            if USE_SQRT[0]:
                I("act", "activation", [sr], [sr], out=stat[:, i, 2:3], in_=stat[:, i, 1:2], func=AF.Sqrt)
                I("dve", "reciprocal", [sr], [sr], out=stat[:, i, 3:4], in_=stat[:, i, 2:3])
            else:
                I("act", "activation", [sr], [sr], out=stat[:, i, 2:3], in_=stat[:, i, 1:2], func=AF.Ln)
                I("act", "activation", [sr], [sr], out=stat[:, i, 3:4], in_=stat[:, i, 2:3], func=AF.Exp, scale=-0.5)
            return stat[:, i, 3:4], sr

        xsi = [0]

        def norm_T(src, sres, gi, dst, dres, half=None, tsel=None):
            b = xsi[0] % 2 if half is None else half
            xsi[0] += 1
            rstd, sr = norm_stats(src, sres, b)
            I("dve", "tensor_scalar", [sres, sr], [xsbr[b]], out=xsb[:, b, :], in0=src, scalar1=rstd, scalar2=None, op0=ALU.mult)
            if half is None:
                for k in range(8):
                    I("pe", "transpose", [xsbr[b], identr], [*tbhr], out=tb[:, k * 128:(k + 1) * 128],
                      in_=xsb[:, b, k * 128:(k + 1) * 128], identity=identb[:])
                for k in range(8):
                    I("act", "activation", [*tbhr, gcolsr], dres, out=dst[:, k, :], in_=tb[:, k * 128:(k + 1) * 128],
                      func=AF.Copy, scale=gcols[:, gi, k:k + 1])
            else:
                tq, tqr = (tbh[half], tbhr[half]) if tsel is None else tsel
                for kb in range(2):
                    for k4 in range(4):
                        k = kb * 4 + k4
                        I("pe", "transpose", [xsbr[b], identr], [tqr], out=tq[:, k4 * 128:(k4 + 1) * 128],
                          in_=xsb[:, b, k * 128:(k + 1) * 128], identity=identb[:])
                    for k4 in range(4):
                        k = kb * 4 + k4
                        if k4 % 2 == 0:
                            I("act", "activation", [tqr, gcolsr], dres, out=dst[:, k, :],
                              in_=tq[:, k4 * 128:(k4 + 1) * 128], func=AF.Copy, scale=gcols[:, gi, k:k + 1])
                        else:
                            I("dve", "tensor_scalar", [tqr, gcolsr], dres, out=dst[:, k, :],
                              in0=tq[:, k4 * 128:(k4 + 1) * 128], scalar1=gcols[:, gi, k:k + 1], scalar2=None, op0=ALU.mult)

        class NS:
            pass

        def alloc_mixer(gt, nkt, nvt, reuse=None):
            M = NS()
            M.WQ = A([8, 512], BF16); M.WQr = AR("WQ")
            M.WTOK = A([8, 1024], BF16); M.WTOKr = AR("WTOK")
            M.WK = M.WTOK[:, :, 0:128]; M.WKr = M.WTOKr
            M.WMQ = A([8, 256], BF16); M.WMQr = AR("WMQ")
            M.WMK = M.WTOK[:, :, 256:512]; M.WMKr = M.WTOKr
            M.WOG = A([8, 512], BF16); M.WOGr = AR("WOG")
            M.WGT = A([8, 8], BF16); M.WGTr = AR("WGT")
            M.WOA = A([4, 1024], BF16); M.WOAr = AR("WOA")
            M.WOM = A([4, 1024], BF16); M.WOMr = AR("WOM")

            def wload(dst, res, src, **kw):
                P.dma("pool", dst, src, res, writes=[res], **kw)

            def wcols(a_, b_):
                return w_in_d[:, a_:b_].rearrange("(k p) n -> p k n", p=128)
            if reuse is None:
                wload(M.WTOK[:, :, 0:256], M.WTOKr, wcols(512, 768), group=True)
                wload(M.WTOK[:, :, 256:1024], M.WTOKr, wcols(1024, 1792), group=True)
                wload(M.WGT[:], M.WGTr, wcols(2304, 2312), allow_slow_non_contiguous=True)
                wload(M.WQ[:], M.WQr, wcols(0, 512))
                wload(M.WMQ[:], M.WMQr, wcols(768, 1024))
                wload(M.WOG[:], M.WOGr, wcols(1792, 2304))
                wload(M.WOA[:], M.WOAr, w_out_d[0:512, :].rearrange("(c p) n -> p c n", p=128))
                wload(M.WOM[:], M.WOMr, w_out_d[512:1024, :].rearrange("(h p) n -> p h n", p=128))
            else:
                for nm_ in ("WQr", "WTOKr", "WMQr", "WOGr", "WGTr", "WOAr", "WOMr"):
                    setattr(M, nm_, getattr(reuse, nm_))
                M.WKr = M.WTOKr; M.WMKr = M.WTOKr
                P.phase_res.extend([M.WQr, M.WTOKr, M.WMQr, M.WOGr, M.WGTr, M.WOAr, M.WOMr])
            M.KT = A([2, nkt * 128], BF16, parts=64); M.KTr = [AR(f"KT{i}") for i in range(nkt)]
            M.Vt = A([nvt, 128], BF16); M.Vtr = [AR(f"Vt{i}") for i in range(nvt)]
            M.XNTg = A([8, gt * 128], BF16); M.XNTgr = [AR(f"XNTg{i}") for i in range(gt)]
            M.QT = A([8, gt * 128], BF16, parts=64); M.QTr = AR("QT")
            M.MQT = A([4, gt * 128], BF16, parts=64); M.MQTr = AR("MQT")
            M.MKT = A([4, gt * 128], BF16, parts=64); M.MKTr = AR("MKT")
            M.SGT = A([4, gt * 128], BF16); M.SGTr = AR("SGT")
            M.MKtok = A([gt, 256], BF16); M.MKtokr = [AR(f"MKtok{i}") for i in range(gt)]
            M.MVaug = A([gt, 4, 129], BF16); M.MVaugr = [AR(f"MVaug{i}") for i in range(gt)]
            M.ATTT = A([4, gt * 128], BF16); M.ATTTr = [AR(f"ATTT{i}") for i in range(gt)]
            M.HMT = A([4, gt * 128], BF16); M.HMTr = [AR(f"HMT{i}") for i in range(gt)]
            M.NG = gt * 128
            NG_ = M.NG
            M.G_IG = A([1, NG_ + 1], F32, parts=4)[:, 0, :]; M.G_E = A([1, NG_], F32, parts=4)[:, 0, :]
            M.G_L1 = A([1, NG_], F32, parts=4)[:, 0, :]; M.G_B = A([1, NG_ + 1], F32, parts=4)[:, 0, :]
            M.G_A = A([1, NG_], F32, parts=4)[:, 0, :]; M.G_M = A([1, NG_ + 1], F32, parts=4)[:, 0, :]
            M.G_BM = A([1, NG_], F32, parts=4)[:, 0, :]; M.G_DM = A([1, NG_], F32, parts=4)[:, 0, :]
            M.Gr = AR("G_IG"); M.G_Br = AR("G_B"); M.G_Ar = AR("G_A"); M.G_Mr = AR("G_M"); M.G_BMr = AR("G_BM"); M.G_DMr = AR("G_DM")
            M.SKV = A([1, 256], F32)[:, 0, :]; M.SKVr = AR("SKV")
            M.Ebuf = A([4, 256], BF16); M.Er = AR("E")
            M.PTs = A([1, 1024], BF16); M.PTsr = [AR("PTs0")] * 2
            M.sm_st = A([1, 32], F32)[:, 0, :]; M.smr = AR("sm_st")
            M.WKC = A([1, 8], F32)[:, 0, :]; M.WKCr = AR("WKC")
            M.DG = A([1, 8], F32, parts=4)[:, 0, :]; M.DGr = AR("DG")
            M.VW = A([4, 129], BF16); M.VWr = [AR(f"VW{h}") for h in range(4)]
            M.Cb = A([4, 257], BF16, parts=64); M.Cbr = [AR(f"Cb{h}") for h in range(4)]
            M.WT = A([4, 128], BF16); M.WTr = AR("WT")
            M.ST = A([4, 128], BF16); M.STr = AR("ST")
            M.WI = A([4, 128], BF16); M.WIr = AR("WI")
            M.QW = A([4, 128], BF16, parts=64); M.QWr = AR("QW")
            M.LOWB = A([4, 128], F32); M.LOWBr = AR("LOWB")
            M.T1 = A([4, 128], F32); M.T1r = AR("T1")
            M.T2 = A([4, 128], F32); M.T2r = AR("T2")
            M.USQ = A([4, 128], BF16); M.USQr = AR("USQ")
            for i in range(gt):
                I("dve", "memset", [], [M.MVaugr[i]], ap=M.MVaug[:, i, :, 128:129], constant=1.0)
            M.PB = [(M.XNTg, M.XNTgr, M.MKtok, M.MKtokr, M.MVaug, M.MVaugr)]
            if gt > 1:
                x2 = A([8, gt * 128], BF16); x2r = [AR(f"XNTh{i}") for i in range(gt)]
                k2 = A([gt, 256], BF16); k2r = [AR(f"MKtoh{i}") for i in range(gt)]
                v2 = A([gt, 4, 129], BF16); v2r = [AR(f"MVauh{i}") for i in range(gt)]
                for i in range(gt):
                    I("dve", "memset", [], [v2r[i]], ap=v2[:, i, :, 128:129], constant=1.0)
                M.PB.append((x2, x2r, k2, k2r, v2, v2r))
            return M

        def use(pb):
            M.XNTg, M.XNTgr, M.MKtok, M.MKtokr, M.MVaug, M.MVaugr = M.PB[pb]

        M = alloc_mixer(GT, NTP + 1, NTP + 1)
        I("dve", "memset", [], [M.G_Br], ap=M.G_B[:, 0:1], constant=0.0)
        I("dve", "memset", [], [M.G_Mr], ap=M.G_M[:, 0:1], constant=0.0)

        def tok_major(ti, xcols, xres, vslot, want_kv_out=None, light=False):
            b0, b0r = bank()
            if light:
                for k in range(8):
                    mm(b0[:, 0:256], M.XNTg[:, k, xcols], M.WTOK[:, k, 256:512], k == 0, k == 7, [xres, M.WTOKr], b0r)
                I("act", "activation", [b0r], [M.MKtokr[ti]], out=M.MKtok[:, ti, :], in_=b0[:, 0:256], func=AF.Copy, scale=0.125)
            else:
                for k in range(8):
                    mm(b0[:, :], M.XNTg[:, k, xcols], M.WTOK[:, k, 0:512], k == 0, k == 7, [xres, M.WTOKr], b0r)
                I("act", "activation", [b0r], [M.Vtr[vslot]], out=M.Vt[:, vslot, :], in_=b0[:, 128:256], func=AF.Copy)
                I("act", "activation", [b0r], [M.MKtokr[ti]], out=M.MKtok[:, ti, :], in_=b0[:, 256:512], func=AF.Copy, scale=0.125)
            if want_kv_out is not None:
                I("dve", "tensor_copy", [b0r], [M.SKVr], out=M.SKV[:, :], in_=b0[:, 0:256])
                if want_kv_out == "sample":
                    P.dma("sp", sks_o[:, 120:128, :], M.SKV[:, 0:128], M.SKVr, reads=[M.SKVr], group=True)
                    P.dma("sp", svs_o[:, 120:128, :], M.SKV[:, 128:256], M.SKVr, reads=[M.SKVr], group=True)
                else:
                    P.dma("sp", swak_o, M.SKV[:, 0:128], M.SKVr, reads=[M.SKVr], group=True)
                    P.dma("sp", swav_o, M.SKV[:, 128:256], M.SKVr, reads=[M.SKVr], group=True)
            b1, b1r = bank()
            for k in range(8):
                mm(b1[:, :], M.XNTg[:, k, xcols], M.WTOK[:, k, 512:1024], k == 0, k == 7, [xres, M.WTOKr], b1r)
            I("dve", "tensor_copy", [b1r], [M.MVaugr[ti]], out=M.MVaug[:, ti, :, 0:128],
              in_=b1[:, :].rearrange("p (h d) -> p h d", h=4))

        def feat64(W, Wr, nh, dst, dres, ntok, xres, scale=None, dcol0=0):
            for h0 in range(0, nh, 2):
                bk, bkr = bank()
                for hh in range(2):
                    h = h0 + hh
                    for k in range(8):
                        mm(bk[0:64, hh * 256:hh * 256 + ntok], W[:, k, h * 64:(h + 1) * 64], M.XNTg[:, k, 0:ntok],
                           k == 0, k == 7, [Wr] + xres, bkr)
                src = bk[0:64, :].rearrange("p (a b) -> p a b", a=2)[:, :, 0:ntok]
                kw = {} if scale is None else {"scale": scale}
                if scale is None and (h0 // 2) % 2 == 1:
                    I("dve", "tensor_copy", [bkr], dres, out=dst[:, h0:h0 + 2, dcol0:dcol0 + ntok], in_=src)
                else:
                    I("act", "activation", [bkr], dres, out=dst[:, h0:h0 + 2, dcol0:dcol0 + ntok], in_=src, func=AF.Copy, **kw)

        def gates(ntok, xres, prefix):
            pg, pgr = bank()
            for k in range(8):
                mm(pg[0:4, 0:ntok], M.WGT[:, k, 0:4], M.XNTg[:, k, 0:ntok], k == 0, k == 7, [M.WGTr] + xres, pgr)
            for k in range(8):
                mm(pg[0:4, 256:256 + ntok], M.WGT[:, k, 4:8], M.XNTg[:, k, 0:ntok], k == 0, k == 7, [M.WGTr] + xres, pgr)
            I("act", "activation", [pgr, gb4r], [M.Gr], out=M.G_IG[:, 1:ntok + 1], in_=pg[0:4, 0:ntok], func=AF.Identity,
              bias=gb4[:, 0:1])
            I("act", "activation", [pgr, gb4r], [M.Gr], out=M.G_E[:, 0:ntok], in_=pg[0:4, 256:256 + ntok], func=AF.Exp,
              bias=gb4[:, 1:2], scale=-1.0)
            I("act", "activation", [M.Gr], [M.Gr], out=M.G_L1[:, 0:ntok], in_=M.G_E[:, 0:ntok], func=AF.Ln, bias=1.0)
            if prefix == "sample":
                return
            if prefix:
                I("dve", "tensor_scalar", [M.Gr, gb4r], [M.Gr], out=M.G_L1[:, 0:ntok], in0=M.G_L1[:, 0:ntok], scalar1=gb4[:, 2:3],
                  scalar2=None, op0=ALU.mult)
            I("dve", "tensor_tensor_scan", [M.Gr, M.G_Br, onesfr], [M.G_Br], out=M.G_B[:, 1:ntok + 1], data0=onesf[0:4, 0:ntok],
              data1=M.G_L1[:, 0:ntok], initial=M.G_B[:, 0:1], op0=ALU.mult, op1=ALU.subtract)
            I("dve", "scalar_tensor_tensor", [M.Gr, M.G_Br, gb4r], [M.G_Ar], out=M.G_A[:, 0:ntok], in0=M.G_IG[:, 1:ntok + 1],
              scalar=(gb4[:, 3:4] if prefix else 0.0), in1=M.G_B[:, 1:ntok + 1], op0=ALU.add, op1=ALU.subtract)
            I("dve", "tensor_tensor_scan", [M.G_Ar, M.G_Mr, onesfr], [M.G_Mr], out=M.G_M[:, 1:ntok + 1], data0=onesf[0:4, 0:ntok],
              data1=M.G_A[:, 0:ntok], initial=M.G_M[:, 0:1], op0=ALU.mult, op1=ALU.max)
            I("dve", "tensor_tensor", [M.G_Br, M.G_Mr], [M.G_BMr], out=M.G_BM[:, 0:ntok], in0=M.G_B[:, 1:ntok + 1],
              in1=M.G_M[:, 1:ntok + 1], op=ALU.add)
            for ci in range(ntok // 128):
                I("dve", "tensor_scalar", [M.G_Mr], [M.G_DMr], out=M.G_DM[:, ci * 128:(ci + 1) * 128],
                  in0=M.G_M[:, 1 + ci * 128:1 + (ci + 1) * 128], scalar1=M.G_M[:, ci * 128:ci * 128 + 1], scalar2=None,
                  op0=ALU.subtract)

        def gates_carry(ntok):
            I("dve", "tensor_copy", [M.G_Br], [M.G_Br], out=M.G_B[:, 0:1], in_=M.G_B[:, ntok:ntok + 1])
            I("dve", "tensor_copy", [M.G_Mr], [M.G_Mr], out=M.G_M[:, 0:1], in_=M.G_M[:, ntok:ntok + 1])

        def state_update(ti, c0, refresh_cb):
            pw, pwr = bank()
            I4 = SEL[:, 0:512].rearrange("p (h t) -> p h t", t=128)[:, :, 0]
            I("dve", "tensor_scalar", [selr, M.G_Mr], [M.DGr], out=M.DG[:, 0:4], in0=I4, scalar1=M.G_M[:, c0 + 128:c0 + 129],
              scalar2=-1.0, op0=ALU.mult, op1=ALU.mult)
            I("dve", "tensor_scalar", [selr, M.G_DMr], [M.DGr], out=M.DG[:, 4:8], in0=I4, scalar1=M.G_DM[:, c0 + 127:c0 + 128],
              scalar2=-1.0, op0=ALU.mult, op1=ALU.mult)
            mm(pw[:, 0:4], M.G_A[:, c0:c0 + 128], I4, True, False, [M.G_Ar, selr], pwr)
            mm(pw[:, 0:4], onesf[0:4, 0:128], M.DG[:, 0:4], False, True, [onesfr, M.DGr], pwr)
            mm(pw[:, 4:8], onesf[0:4, 0:128], M.DG[:, 4:8], True, True, [onesfr, M.DGr], pwr)
            I("act", "activation", [pwr], [M.WKCr], out=M.WKC[:, 0:8], in_=pw[:, 0:8], func=AF.Exp)
            for h in range(4):
                I("dve", "tensor_scalar", [M.MVaugr[ti], M.WKCr], [M.VWr[h]], out=M.VW[:, h, :], in0=M.MVaug[:, ti, h, :],
                  scalar1=M.WKC[:, h:h + 1], scalar2=None, op0=ALU.mult)
            for h0 in (0, 2):
                dc, dcr = bank()
                for hh in range(2):
                    h = h0 + hh
                    mm(dc[0:64, hh * 129:(hh + 1) * 129], M.MKtok[:, ti, h * 64:(h + 1) * 64], M.VW[:, h, :], True, True,
                       [M.MKtokr[ti], M.VWr[h]], dcr)
                for hh in range(2):
                    h = h0 + hh
                    I("dve", "scalar_tensor_tensor", [Cstr[h], M.WKCr, dcr], [Cstr[h]], out=Cst[:, h, :], in0=Cst[:, h, :],
                      scalar=M.WKC[0:64, 4 + h:5 + h], in1=dc[0:64, hh * 129:(hh + 1) * 129], op0=ALU.mult, op1=ALU.add)
            if refresh_cb:
                for h in range(4):
                    I("act", "activation", [Cstr[h]], [M.Cbr[h]], out=M.Cb[:, h, 0:129], in_=Cst[:, h, :], func=AF.Copy)
                    I("act", "activation", [Cstr[h]], [M.Cbr[h]], out=M.Cb[:, h, 129:257],
                      in_=Cst[:, h, 128:129].broadcast_to([64, 128]), func=AF.Copy)

        def mlstm_chunk(ti, c0, mbias, mbiasr, inter=True, inter_fn=None):
            cs = slice(c0, c0 + 128)
            pwt, pwtr = bank()
            for h in range(4):
                o = pwt[:, h * 128:(h + 1) * 128]
                mm(o, M.G_A[:, cs], SELh(h), True, False, [M.G_Ar, selr], pwtr)
                mm(o, NSELh(h), M.G_M[:, c0 + 1:c0 + 129], False, False, [M.G_Mr, selr], pwtr)
                mm(o, identb[:], mbias, False, True, [identr, mbiasr], pwtr)
            I("act", "activation", [pwtr], [M.WTr], out=M.WT[:, :, :], in_=pwt[:, :].rearrange("p (h t) -> p h t", h=4), func=AF.Exp)
            pqk, pqkr = bank()
            for h in range(4):
                mm(pqk[:, h * 128:(h + 1) * 128], M.MKT[:, h, cs], M.MQT[:, h, cs], True, True, [M.MKTr, M.MQTr], pqkr)
            I("dve", "tensor_tensor", [pqkr, M.WTr], [M.STr], out=M.ST[:, :, :], in0=pqk[:, :].rearrange("p (h t) -> p h t", h=4),
              in1=M.WT[:, :, :], op=ALU.mult)
            pwi, pwir = bank()
            for h in range(4):
                mm(pwi[:, h * 128:(h + 1) * 128], NSELh(h), M.G_DM[:, cs], True, True, [M.G_DMr, selr], pwir)
            I("act", "activation", [pwir], [M.WIr], out=M.WI[:, :, :], in_=pwi[:, :].rearrange("p (h t) -> p h t", h=4), func=AF.Exp)
            I("dve", "tensor_tensor", [M.MQTr, M.WIr], [M.QWr], out=M.QW[:, :, :], in0=M.MQT[:, :, cs], in1=M.WI[0:64, :, :], op=ALU.mult)
            plb, plbr = bank()
            for h in range(4):
                mm(plb[:, h * 128:(h + 1) * 128], NSELh(h), M.G_BM[:, cs], True, True, [M.G_BMr, selr], plbr)
            I("act", "activation", [plbr], [M.LOWBr], out=M.LOWB[:, :, :], in_=plb[:, :].rearrange("p (h t) -> p h t", h=4), func=AF.Exp)
            pnum, pnumr = banks[5], bres[5]
            pden, pdenr = banks[6], bres[6]
            if inter_fn is not None:
                inter_fn("pre")
            for h in range(4):
                o = pnum[:, h * 128:(h + 1) * 128]
                mm(o, M.MVaug[:, ti, h, 0:128], M.ST[:, h, :], True, False, [M.MVaugr[ti], M.STr], pnumr)
                if inter_fn is not None:
                    inter_fn("num", h, pnum, pnumr)
                else:
                    mm(o, M.Cb[:, h, 0:128], M.QW[:, h, :], False, True, [M.Cbr[h], M.QWr], pnumr)
            for h in range(4):
                o = pden[:, h * 128:(h + 1) * 128]
                mm(o, onesb[:], M.ST[:, h, :], True, False, [onesbr, M.STr], pdenr)
                if inter_fn is not None:
                    inter_fn("den", h, pden, pdenr)
                else:
                    mm(o, M.Cb[:, h, 129:257], M.QW[:, h, :], False, True, [M.Cbr[h], M.QWr], pdenr)
            return pnum, pnumr, pden, pdenr

        def mlstm_finish(pnum, pnumr, pden, pdenr, c0, hres):
            cs = slice(c0, c0 + 128)
            v4 = lambda b: b[:, :].rearrange("p (h t) -> p h t", h=4)
            I("act", "activation", [pdenr], [M.T1r], out=M.T1[:, :, :], in_=v4(pden), func=AF.Abs)
            I("dve", "tensor_tensor", [M.T1r, M.LOWBr], [M.T1r], out=M.T1[:, :, :], in0=M.T1[:, :, :], in1=M.LOWB[:, :, :], op=ALU.max)
            I("act", "activation", [M.T1r], [M.T1r], out=M.T1[:, :, :], in_=M.T1[:, :, :], func=AF.Square, scale=float(np.sqrt(EPS)))
            I("act", "activation", [pnumr], [M.USQr], out=M.USQ[:, :, :], in_=v4(pnum), func=AF.Square)
            pss, pssr = bank()
            mm(pss[:, :], onesb[:], M.USQ[:, :, :], True, True, [onesbr, M.USQr], pssr)
            I("dve", "scalar_tensor_tensor", [pssr, M.T1r], [M.T2r], out=M.T2[:, :, :], in0=v4(pss), scalar=1.0 / 128, in1=M.T1[:, :, :],
              op0=ALU.mult, op1=ALU.add)
            I("act", "activation", [M.T2r], [M.T2r], out=M.T2[:, :, :], in_=M.T2[:, :, :], func=AF.Ln)
            I("act", "activation", [M.T2r], [M.T2r], out=M.T2[:, :, :], in_=M.T2[:, :, :], func=AF.Exp, scale=-0.5)
            I("dve", "tensor_tensor", [pnumr, M.T2r], [M.T1r], out=M.T1[:, :, :], in0=v4(pnum), in1=M.T2[:, :, :], op=ALU.mult)
            for h in range(4):
                I("dve", "scalar_tensor_tensor", [M.T1r, gheadr, M.SGTr], [hres], out=M.HMT[:, h, cs], in0=M.T1[:, h, :],
                  scalar=gheadc[:, h:h + 1], in1=M.SGT[:, h, cs], op0=ALU.mult, op1=ALU.mult)

        def swa_tile(ti, kcol0, vslots, mb, mbr, ktres):
            qs = slice(ti * 128, (ti + 1) * 128)
            for h in range(2):
                bks = [bank(), bank()]
                for g in range(4):
                    bk, bkr = bks[g // 2]
                    o = bk[:, (g % 2) * 256:(g % 2 + 1) * 256]
                    mm(o, M.QT[:, 4 * h + g, qs], M.KT[:, h, kcol0:kcol0 + 256], True, False, [M.QTr] + ktres, bkr)
                    mm(o, identb[:], mb, False, True, [identr, mbr], bkr)
                for j in range(2):
                    I("dve", "reduce_max", [bks[j][1]], [M.smr], out=M.sm_st[:, 2 * j:2 * j + 2],
                      in_=bks[j][0][:, :].rearrange("p (a b) -> p a b", a=2), axis=AX.X)
                I("dve", "tensor_scalar", [M.smr], [M.smr], out=M.sm_st[:, 0:4], in0=M.sm_st[:, 0:4], scalar1=-0.125, scalar2=None,
                  op0=ALU.mult)
                I("dve", "tensor_tensor", [M.smr, sinkbr], [M.smr], out=M.sm_st[:, 0:4], in0=M.sm_st[:, 0:4],
                  in1=sinkb[:, 8 + 4 * h:12 + 4 * h], op=ALU.min)
                for g in range(4):
                    bk, bkr = bks[g // 2]
                    I("act", "activation", [bkr, M.smr], [M.Er, M.smr], out=M.Ebuf[:, g, :], in_=bk[:, (g % 2) * 256:(g % 2 + 1) * 256],
                      func=AF.Exp, bias=M.sm_st[:, g:g + 1], scale=0.125, accum_out=M.sm_st[:, 4 + g:5 + g])
                I("dve", "tensor_tensor", [M.smr, sinkbr], [M.smr], out=M.sm_st[:, 8:12], in0=M.sm_st[:, 0:4],
                  in1=sinkb[:, 4 * h:4 * h + 4], op=ALU.add)
                I("act", "activation", [M.smr], [M.smr], out=M.sm_st[:, 8:12], in_=M.sm_st[:, 8:12], func=AF.Exp)
                I("dve", "tensor_tensor", [M.smr], [M.smr], out=M.sm_st[:, 8:12], in0=M.sm_st[:, 8:12], in1=M.sm_st[:, 4:8], op=ALU.add)
                I("dve", "reciprocal", [M.smr], [M.smr], out=M.sm_st[:, 12:16], in_=M.sm_st[:, 8:12])
                for g in range(4):
                    if g % 2 == 0:
                        I("act", "activation", [M.Er, M.smr], [M.Er], out=M.Ebuf[:, g, :], in_=M.Ebuf[:, g, :], func=AF.Copy,
                          scale=M.sm_st[:, 12 + g:13 + g])
                    else:
                        I("dve", "tensor_scalar", [M.Er, M.smr], [M.Er], out=M.Ebuf[:, g, :], in0=M.Ebuf[:, g, :],
                          scalar1=M.sm_st[:, 12 + g:13 + g], scalar2=None, op0=ALU.mult)
                for kb in range(2):
                    for g in range(4):
                        blk = kb * 4 + (g % 2) * 2 + g // 2
                        I("pe", "transpose", [M.Er, identr], [*tbhr], out=tb[:, blk * 128:(blk + 1) * 128],
                          in_=M.Ebuf[:, g, kb * 128:(kb + 1) * 128], identity=identb[:])
                pb = 0
                if h == 0:
                    I("dve", "tensor_copy", [*tbhr], [M.PTsr[pb]], out=M.PTs[:, pb, :], in_=tb[:, :])
                else:
                    I("act", "activation", [*tbhr], [M.PTsr[pb]], out=M.PTs[:, pb, :], in_=tb[:, :], func=AF.Copy)
                po, por = bank()
                for par in range(2):
                    for kb in range(2):
                        mm(po[par * 64:(par + 1) * 64, 0:256], M.Vt[:, vslots[kb], h * 64:(h + 1) * 64],
                           M.PTs[:, pb, kb * 512 + par * 256:kb * 512 + (par + 1) * 256], kb == 0, kb == 1,
                           [M.Vtr[vslots[kb]], M.PTsr[pb]], por)
                I("act", "activation", [por], [M.ATTTr[ti]], out=M.ATTT[:, 2 * h:2 * h + 2, qs],
                  in_=po[:, 0:256].rearrange("p (g q) -> p g q", g=2), func=AF.Copy)

        def wout_tile(ti, t):
            qs = slice(ti * 128, (ti + 1) * 128)
            for c in range(2):
                bk, bkr = bank()
                cc = slice(c * 512, (c + 1) * 512)
                for hg in range(4):
                    mm(bk[:, :], M.ATTT[:, hg, qs], M.WOA[:, hg, cc], hg == 0, False, [M.ATTTr[ti], M.WOAr], bkr)
                for h in range(4):
                    mm(bk[:, :], M.HMT[:, h, qs], M.WOM[:, h, cc], False, h == 3, [M.HMTr[ti], M.WOMr], bkr)
                I("dve", "tensor_tensor", [Yr[t], bkr], [Yr[t]], out=Y[:, t, cc], in0=Y[:, t, cc], in1=bk[:, :], op=ALU.add)

        xpre_t = xpre_d.rearrange("(t p) d -> t p d", p=128)
        xp_t = xp_d.rearrange("(t p) d -> t p d", p=128)
        for t in range(NTP):
            P.dma("sp", Y[:, t, :], xpre_t[t], Yr[t], writes=[Yr[t]])

        tb4sel = (banks[4][:, :].bitcast(BF16)[:, 0:512], bres[4])

        def prep_prefix(g0, pb):
            use(pb)
            for ti in range(GT):
                t = g0 + ti
                norm_T(Y[:, t, :], Yr[t], 0, M.XNTg[:, :, ti * 128:(ti + 1) * 128], [M.XNTgr[ti]], half=1, tsel=tb4sel)
            for ti in range(GT):
                tok_major(ti, slice(ti * 128, (ti + 1) * 128), M.XNTgr[ti], 0, light=(g0 + ti != NTP - 1))

        prep_prefix(0, 0)
        for gi, g0 in enumerate(range(0, NTP, GT)):
            pb = gi % 2
            use(pb)
            P.rec_begin(); bset[0] = [0, 1, 2]
            gates(GT * 128, M.XNTgr, True)
            if g0 + GT == NTP:
                bk, bkr = bank()
                for h in range(2):
                    for k in range(8):
                        mm(bk[0:64, h * 128:(h + 1) * 128], M.WK[:, k, h * 64:(h + 1) * 64], M.XNTg[:, k, (GT - 1) * 128:GT * 128],
                           k == 0, k == 7, [M.WKr, M.XNTgr[GT - 1]], bkr)
                I("act", "activation", [bkr], [M.KTr[0]], out=M.KT[:, :, 0:128],
                  in_=bk[0:64, 0:256].rearrange("p (a b) -> p a b", a=2), func=AF.Copy)
            for ti in range(GT):
                last = (g0 + ti == NTP - 1)
                state_update(ti, ti * 128, last)
            gates_carry(GT * 128)
            sA = P.rec_end()
            strs = [sA]
            if g0 + GT < NTP:
                P.rec_begin(); bset[0] = [3, 4]
                prep_prefix(g0 + GT, pb ^ 1)
                strs.append(P.rec_end())
                use(pb)
            P.merge(strs)
            bset[0] = [0, 1, 2, 3, 4]

        for t in range(NTP):
            P.dma("sp", Y[:, t, :], xp_t[t], Yr[t], writes=[Yr[t]])

        def prep_main(g0, pb, merged):
            use(pb)
            if merged:
                for ti in range(GT):
                    t = g0 + ti
                    norm_T(Y[:, t, :], Yr[t], 0, M.XNTg[:, :, ti * 128:(ti + 1) * 128], [M.XNTgr[ti]], half=1, tsel=tb4sel)
            else:
                for ti in range(GT):
                    t = g0 + ti
                    norm_T(Y[:, t, :], Yr[t], 0, M.XNTg[:, :, ti * 128:(ti + 1) * 128], [M.XNTgr[ti]])
            for ti in range(GT):
                t = g0 + ti
                tok_major(ti, slice(ti * 128, (ti + 1) * 128), M.XNTgr[ti], 1 + t, want_kv_out=(True if t == NTP - 1 else None))

        prep_main(0, 0, False)
        for gi, g0 in enumerate(range(0, NTP, GT)):
            pb = gi % 2
            use(pb)
            gates(M.NG, M.XNTgr, False)
            feat64(M.WK, M.WKr, 2, M.KT, [M.KTr[1 + g0 + i] for i in range(GT)], M.NG, M.XNTgr, dcol0=128 + g0 * 128)
            feat64(M.WQ, M.WQr, 8, M.QT, [M.QTr], M.NG, M.XNTgr)
            feat64(M.WMQ, M.WMQr, 4, M.MQT, [M.MQTr], M.NG, M.XNTgr)
            feat64(M.WMK, M.WMKr, 4, M.MKT, [M.MKTr], M.NG, M.XNTgr, scale=0.125)
            for h0 in (0, 2):
                bk, bkr = bank()
                for hh in range(2):
                    h = h0 + hh
                    for k in range(8):
                        mm(bk[:, hh * 256:hh * 256 + M.NG], M.WOG[:, k, h * 128:(h + 1) * 128], M.XNTg[:, k, 0:M.NG], k == 0, k == 7,
                           [M.WOGr] + M.XNTgr, bkr)
                sgv = M.SGT[:, h0:h0 + 2, :]
                I("act", "activation", [bkr], [M.SGTr], out=sgv, in_=bk[:, :].rearrange("p (a b) -> p a b", a=2)[:, :, 0:M.NG],
                  func=AF.Exp, scale=-1.0)
                I("act", "activation", [M.SGTr], [M.SGTr], out=sgv, in_=sgv, func=AF.Ln, bias=1.0)
                I("act", "activation", [M.SGTr], [M.SGTr], out=sgv, in_=sgv, func=AF.Exp, scale=-1.0)
            pend = None
            for ti in range(GT):
                t = g0 + ti
                P.rec_begin(); bset[0] = [2, 3]
                pn = mlstm_chunk(ti, ti * 128, mbcaus[:], mbcausr)
                mlstm_finish(*pn, ti * 128, M.HMTr[ti])
                state_update(ti, ti * 128, True)
                s_ml = P.rec_end()
                P.rec_begin(); bset[0] = [0, 1]
                swa_tile(ti, t * 128, (t, t + 1), (mbfirst[:] if t == 0 else mbband[:]), (mbfirstr if t == 0 else mbbandr),
                         [M.KTr[t], M.KTr[t + 1]])
                s_sw = P.rec_end()
                strs = [s_ml, s_sw]
                P.rec_begin(); bset[0] = [4]
                if pend is not None:
                    wout_tile(*pend)
                if ti == 0 and g0 + GT < NTP:
                    prep_main(g0 + GT, pb ^ 1, True)
                    use(pb)
                strs.append(P.rec_end())
                P.merge(strs)
                pend = (ti, t)
            bset[0] = [0, 1, 2, 3, 4]
            wout_tile(*pend)
            if debug and g0 == DBG_G0:
                dA = dout("dbg_att", [128, 4, M.NG]); dH = dout("dbg_hm", [128, 4, M.NG])
                P.dma("pool", dA, M.ATTT[:, :, :], M.ATTTr[0], reads=M.ATTTr)
                P.dma("pool", dH, M.HMT[:, :, :], M.HMTr[0], reads=M.HMTr)
            gates_carry(M.NG)

        CO = A([4, 64], F32); COr = AR("CO")
        for h in range(4):
            bk, bkr = bank()
            mm(bk[:, 0:64], Cst[:, h, 0:128], identf[0:64, 0:64], True, True, [Cstr[h], identfr], bkr)
            I("act", "activation", [bkr], [COr], out=CO[:, h, :], in_=bk[:, 0:64], func=AF.Copy)
        P.dma("sp", Cp_o.rearrange("h p k -> p h k"), CO[:, :, :], COr, reads=[COr])
        for h in range(4):
            P.dma("sp", np_o[h, :].rearrange("(k o) -> k o", o=1), Cst[:, h, 128:129], Cstr[h], reads=[Cstr[h]], allow_slow_non_contiguous=True)
        P.dma("sp", mp_o, M.G_BM[:, M.NG - 1:M.NG], M.G_BMr, reads=[M.G_BMr], allow_slow_non_contiguous=True)

        new_phase()
        MA = M
        M = alloc_mixer(1, 1, 1, reuse=MA)
        R0_olds = [M.WQr, M.WTOKr, M.WMQr, M.WOGr, M.WGTr]
        TS = NTP
        P.dma("sp", Y[:, TS, :], xs_d, Yr[TS], writes=[Yr[TS]])
        shk = P.res("shk"); shv = P.res("shv")
        P.dma("sp", sks_o[:, 0:120, :], csk_d[:, 8:128, :], shk, writes=[shk])
        P.dma("sp", svs_o[:, 0:120, :], csv_d[:, 8:128, :], shv, writes=[shv])
        CKn = A([16, 128], BF16); CKnr = AR("CKn")
        CV = A([16, 128], BF16); CVr = AR("CV")
        CKT = A([16, 128], BF16, parts=64); CKTr = AR("CKT")
        SMC = A([1, 128], BF16, parts=32)[:, 0, :]; SMCr = AR("SMC")
        SMN = A([16, 128], BF16, parts=32); SMNr = AR("SMN")
        SINKC = A([1, 4], F32, parts=32)[:, 0, :]; SINKCr = AR("SINKC")
        mbcs = A([1, 128], BF16)[:, 0, :]; mbcsr = AR("mbcs")
        PNs = A([4, 256], BF16, parts=32); PNsr = [AR(f"PNs{i}") for i in range(4)]
        sms = A([4, 8], F32, parts=32); smsr = [AR(f"sms{i}") for i in range(4)]
        PTS = A([1, 1024], BF16)[:, 0, :]; PTSr = AR("PTS")
        M0 = A([1, 16], F32, parts=4)[:, 0, :]; M0r = AR("M0")
        MTe = A([1, 128], F32, parts=4)[:, 0, :]; MTer = AR("MTe")
        DMT = A([1, 16], F32, parts=4)[:, 0, :]; DMTr = AR("DMT")
        E16 = A([1, 16], F32)[:, 0, :]; E16r = AR("E16")
        EW = A([4, 16], BF16); EWr = AR("EW")
        WCB = A([4, 16], F32); WCBr = AR("WCB")
        SNn = A([1, 64], F32, parts=64)[:, 0, :]; SNnr = AR("SNn")
        SNT = A([1, 64], F32, parts=64)[:, 0, :]; SNTr = AR("SNT")
        NNT = A([1, 64], F32, parts=64)[:, 0, :]; NNTr = AR("NNT")
        NNo = A([1, 64], F32, parts=64)[:, 0, :]; NNor = AR("NNo")
        BTf = A([1, 128], F32)[:, 0, :]; BTfr = AR("BTf")
        P.dma("pool", CKn[:, :, :], csk_d.rearrange("j p c -> p j c"), CKnr, writes=[CKnr])
        P.dma("pool", CV[:, :, :], csv_d.rearrange("j p c -> p j c"), CVr, writes=[CVr])
        P.dma("pool", SMC, smc_d, SMCr, writes=[SMCr])
        P.dma("pool", SMN[:, :, :], smn_d, SMNr, writes=[SMNr])
        P.dma("sp", SINKC[:, 0:2], sinkcol_d, SINKCr, writes=[SINKCr])
        I("dve", "tensor_scalar", [SINKCr], [SINKCr], out=SINKC[:, 2:4], in0=SINKC[:, 0:2], scalar1=-1.0, scalar2=None, op0=ALU.mult)
        P.dma("pool", mbcs, mb_causs_d, mbcsr, writes=[mbcsr])
        P.dma("sp", M0, sm_d.rearrange("j h -> h j"), M0r, writes=[M0r], allow_slow_non_contiguous=True)
        P.dma("sp", E16, eseq_d, E16r, writes=[E16r])
        P.dma("sp", SNn, sn_d.rearrange("j h k -> (j h) k"), SNnr, writes=[SNnr])

        norm_T(Y[:, TS, :], Yr[TS], 0, M.XNTg[:, :, 0:128], [M.XNTgr[0]])
        tok_major(0, slice(0, 128), M.XNTgr[0], 0, want_kv_out="sample")
        gates(128, M.XNTgr, "sample")
        feat64(M.WK, M.WKr, 2, M.KT, [M.KTr[0]], 128, M.XNTgr, dcol0=0)
        feat64(M.WQ, M.WQr, 8, M.QT, [M.QTr], 128, M.XNTgr)
        feat64(M.WMQ, M.WMQr, 4, M.MQT, [M.MQTr], 128, M.XNTgr)
        feat64(M.WMK, M.WMKr, 4, M.MKT, [M.MKTr], 128, M.XNTgr, scale=0.125)
        for h0 in (0, 2):
            bk, bkr = bank()
            for hh in range(2):
                h = h0 + hh
                for k in range(8):
                    mm(bk[:, hh * 256:hh * 256 + 128], M.WOG[:, k, h * 128:(h + 1) * 128], M.XNTg[:, k, 0:128], k == 0, k == 7,
                       [M.WOGr] + M.XNTgr, bkr)
            sgv = M.SGT[:, h0:h0 + 2, :]
            I("act", "activation", [bkr], [M.SGTr], out=sgv, in_=bk[:, :].rearrange("p (a b) -> p a b", a=2)[:, :, 0:128],
              func=AF.Exp, scale=-1.0)
            I("act", "activation", [M.SGTr], [M.SGTr], out=sgv, in_=sgv, func=AF.Ln, bias=1.0)
            I("act", "activation", [M.SGTr], [M.SGTr], out=sgv, in_=sgv, func=AF.Exp, scale=-1.0)
        for j in range(16):
            I("dve", "tensor_tensor_scan", [M.Gr, M.G_Br, onesfr], [M.G_Br], out=M.G_B[:, 1 + 8 * j:9 + 8 * j],
              data0=onesf[0:4, 0:8], data1=M.G_L1[:, 8 * j:8 * j + 8], initial=0.0, op0=ALU.mult, op1=ALU.subtract)
        I("dve", "tensor_tensor", [M.Gr, M.G_Br], [M.G_Ar], out=M.G_A[:, 0:128], in0=M.G_IG[:, 1:129], in1=M.G_B[:, 1:129],
          op=ALU.subtract)
        for j in range(16):
            I("dve", "tensor_tensor_scan", [M.G_Ar, M.G_Mr, onesfr, M0r], [M.G_Mr], out=M.G_M[:, 1 + 8 * j:9 + 8 * j],
              data0=onesf[0:4, 0:8], data1=M.G_A[:, 8 * j:8 * j + 8], initial=M0[:, j:j + 1], op0=ALU.mult, op1=ALU.max)
        I("dve", "tensor_tensor", [M.G_Br, M.G_Mr], [M.G_BMr], out=M.G_BM[:, 0:128], in0=M.G_B[:, 1:129], in1=M.G_M[:, 1:129],
          op=ALU.add)
        GM3 = M.G_M[:, 1:129].rearrange("p (j i) -> p j i", i=8)
        I("dve", "tensor_tensor", [M.G_Mr, M0r], [M.G_DMr], out=M.G_DM[:, 0:128].rearrange("p (j i) -> p j i", i=8), in0=GM3,
          in1=M0[:, :].unsqueeze(2).broadcast_to([4, 16, 8]), op=ALU.subtract)
        I("dve", "tensor_copy", [M.G_Mr], [MTer], out=MTe[:, :].rearrange("p (j i) -> p j i", i=8),
          in_=GM3[:, :, 7:8].broadcast_to([4, 16, 8]))
        I("dve", "tensor_tensor", [M.G_Mr, M0r], [DMTr], out=DMT[:, :].unsqueeze(2), in0=GM3[:, :, 7:8], in1=M0[:, :].unsqueeze(2),
          op=ALU.subtract)
        P.dma("sp", ms_o.rearrange("j h -> h j"), M.G_BM[:, 0:128].rearrange("p (j i) -> p j i", i=8)[:, :, 7], M.G_BMr,
              reads=[M.G_BMr], allow_slow_non_contiguous=True)

        pair_i = [0]
        QS = A([2, 16, 32], BF16, parts=64); QSr = AR("QS")
        for h in range(2):
            for par in range(2):
                I("act", "activation", [M.QTr], [QSr],
                  out=QS[:, h, :, par * 16:(par + 1) * 16].rearrange("p j (gp i) -> p j gp i", i=8),
                  in_=M.QT[:, 4 * h:4 * h + 4, :].rearrange("p (gp two) t -> p two gp t", two=2)[:, par].rearrange(
                      "p gp (j i) -> p j gp i", i=8), func=AF.Copy)
        for h in range(2):
            for q4 in range(2):
                for jj in range(8):
                    j = q4 * 8 + jj
                    I("pe", "transpose", [CKnr, identr], [*tbhr], out=tb[0:64, jj * 128:(jj + 1) * 128],
                      in_=CKn[:, j, h * 64:(h + 1) * 64], identity=identb[:])
                I("act", "activation", [*tbhr], [CKTr], out=CKT[:, q4 * 8:(q4 + 1) * 8, :],
                  in_=tb[0:64, :].rearrange("p (a b) -> p a b", a=8), func=AF.Copy)
            for half in range(2):
                po, por = banks[4], bres[4]
                strs = []
                for sk in range(4):
                    P.rec_begin(); bset[0] = [sk]
                    for jj in (sk, sk + 4):
                        j = half * 8 + jj
                        b = sk
                        bk, bkr = bank()
                        lq = QS[:, h, j, :]
                        mm(bk[0:32, 0:128], lq, CKT[:, j, :], True, False, [QSr, CKTr], bkr)
                        mm(bk[0:32, 0:128], identb[0:32, 0:32], SMC, False, True, [identr, SMCr], bkr)
                        mm(bk[0:32, 128:256], lq, M.KT[:, h, 0:128], True, False, [QSr, M.KTr[0]], bkr)
                        mm(bk[0:32, 128:256], identb[0:32, 0:32], SMN[:, j, :], False, True, [identr, SMNr], bkr)
                        st_ = sms[:, b, :]
                        I("dve", "reduce_max", [bkr], [smsr[b]], out=st_[:, 0:1], in_=bk[0:32, 0:256], axis=AX.X)
                        I("dve", "tensor_scalar", [smsr[b]], [smsr[b]], out=st_[:, 0:1], in0=st_[:, 0:1], scalar1=-0.125,
                          scalar2=None, op0=ALU.mult)
                        I("dve", "tensor_tensor", [smsr[b], SINKCr], [smsr[b]], out=st_[:, 0:1], in0=st_[:, 0:1],
                          in1=SINKC[:, 2 + h:3 + h], op=ALU.min)
                        I("act", "activation", [bkr, smsr[b]], [PNsr[b], smsr[b]], out=PNs[:, b, :], in_=bk[0:32, 0:256],
                          func=AF.Exp, bias=st_[:, 0:1], scale=0.125, accum_out=st_[:, 1:2])
                        I("act", "activation", [SINKCr, smsr[b]], [smsr[b]], out=st_[:, 2:3], in_=SINKC[:, h:h + 1], func=AF.Exp,
                          bias=st_[:, 0:1])
                        I("dve", "tensor_tensor", [smsr[b]], [smsr[b]], out=st_[:, 2:3], in0=st_[:, 2:3], in1=st_[:, 1:2],
                          op=ALU.add)
                        I("dve", "reciprocal", [smsr[b]], [smsr[b]], out=st_[:, 3:4], in_=st_[:, 2:3])
                        I("dve", "tensor_scalar", [PNsr[b], smsr[b]], [PNsr[b]], out=PNs[:, b, :], in0=PNs[:, b, :],
                          scalar1=st_[:, 3:4], scalar2=None, op0=ALU.mult)
                        for c2 in range(2):
                            I("pe", "transpose", [PNsr[b], identr], [*tbhr], out=tb[:, jj * 64 + c2 * 32:jj * 64 + (c2 + 1) * 32],
                              in_=PNs[:, b, c2 * 128:(c2 + 1) * 128], identity=identb[0:32, 0:32])
                    strs.append(P.rec_end())
                P.merge(strs)
                bset[0] = [0, 1, 2, 3]
                I("dve", "tensor_copy", [*tbhr], [PTSr], out=PTS[:, 0:512], in_=tb[:, 0:512])
                for jj in range(8):
                    j = half * 8 + jj
                    for par in range(2):
                        o = po[par * 64:(par + 1) * 64, jj * 16:(jj + 1) * 16]
                        mm(o, CV[:, j, h * 64:(h + 1) * 64], PTS[:, jj * 64 + par * 16:jj * 64 + par * 16 + 16], True, False,
                           [CVr, PTSr], por)
                        mm(o, M.Vt[:, 0, h * 64:(h + 1) * 64], PTS[:, jj * 64 + 32 + par * 16:jj * 64 + 32 + par * 16 + 16], False,
                           True, [M.Vtr[0], PTSr], por)
                I("act", "activation", [por], [M.ATTTr[0]],
                  out=M.ATTT[:, 2 * h:2 * h + 2, half * 64:(half + 1) * 64].rearrange("p c (j i) -> p j c i", i=8),
                  in_=po[:, 0:128].rearrange("p (j c i) -> p j c i", c=2, i=8), func=AF.Copy)

        bset[0] = [0, 1, 2, 3, 4]
        off = 0
        SCf, off = A_at(off, [64, 64], F32); SCfr = ARalias("SCf", R0_olds)
        SCT, off = A_at(off, [64, 128], BF16, parts=64); SCTr = ARalias("SCT", R0_olds)
        QN, off = A_at(off, [4, 128], BF16, parts=64); QNr = ARalias("QN", R0_olds)
        KJ, off = A_at(off, [16, 64], BF16); KJr = ARalias("KJ", R0_olds)
        assert off <= 18496
        P.dma("sp", SCf[:, :, :], sC_d.rearrange("j h p k -> p (j h) k"), SCfr, writes=[SCfr])
        for p4 in range(16):
            bk, bkr = bank()
            for q_ in range(4):
                pr = p4 * 4 + q_
                mm(bk[0:64, q_ * 128:(q_ + 1) * 128], SCf[:, pr, :], identf[:], True, True, [SCfr, identfr], bkr)
            I("act", "activation", [bkr], [SCTr], out=SCT[:, p4 * 4:(p4 + 1) * 4, :],
              in_=bk[0:64, :].rearrange("p (a b) -> p a b", a=4), func=AF.Copy)
        bk, bkr = bank()
        mm(bk[0:64, 0:64], SNn, identf[0:64, 0:64], True, True, [SNnr, identfr], bkr)
        I("act", "activation", [bkr], [SNTr], out=SNT, in_=bk[0:64, 0:64], func=AF.Copy)

        def sample_inter(kind, h=None, pb=None, pbr=None):
            if kind == "pre":
                I("dve", "tensor_tensor", [M.QWr, SNTr], [QNr], out=QN[:, :, :].rearrange("p h (j i) -> p h j i", i=8),
                  in0=M.QW[:, :, :].rearrange("p h (j i) -> p h j i", i=8),
                  in1=SNT.rearrange("p (j h) -> p h j", h=4).unsqueeze(3).broadcast_to([64, 4, 16, 8]), op=ALU.mult)
            elif kind == "num":
                for j in range(16):
                    mm(pb[:, h * 128 + 8 * j:h * 128 + 8 * j + 8], SCT[:, j * 4 + h, :], M.QW[:, h, 8 * j:8 * j + 8], False, j == 15,
                       [SCTr, M.QWr], pbr)
            else:
                mm(pb[:, h * 128:(h + 1) * 128], onesb[0:64, :], QN[:, h, :], False, True, [onesbr, QNr], pbr)

        pn = mlstm_chunk(0, 0, mbcs, mbcsr, inter_fn=sample_inter)
        mlstm_finish(*pn, 0, M.HMTr[0])
        wout_tile(0, TS)

        pw, pwr = bank()
        for h in range(4):
            mm(pw[:, h:h + 1], M.G_A[:, 0:128], SELh(h, 1), True, False, [M.G_Ar, selr], pwr)
            mm(pw[:, h:h + 1], MTe, SEL[:, 512 + h * 128:512 + h * 128 + 1], False, True, [MTer, selr], pwr)
        for h in range(4):
            mm(pw[:, 8 + 16 * h:8 + 16 * (h + 1)], NSELh(h), DMT, True, True, [DMTr, selr], pwr)
        I("act", "activation", [pwr], [M.WKCr], out=M.WKC[:, 0:4], in_=pw[:, 0:4], func=AF.Exp)
        I("act", "activation", [pwr], [WCBr], out=WCB[:, :, :], in_=pw[:, 8:72].rearrange("p (h j) -> p h j", h=4), func=AF.Exp)
        for h in range(4):
            I("dve", "tensor_scalar", [M.MVaugr[0], M.WKCr], [M.VWr[h]], out=M.VW[:, h, :], in0=M.MVaug[:, 0, h, :],
              scalar1=M.WKC[:, h:h + 1], scalar2=None, op0=ALU.mult)
            I("dve", "tensor_scalar", [E16r, M.WKCr], [EWr], out=EW[:, h, :], in0=E16, scalar1=M.WKC[:, h:h + 1], scalar2=None,
              op0=ALU.mult)
        bk, bkr = bank()
        for h in range(4):
            mm(bk[0:64, h * 16:(h + 1) * 16], M.MKtok[:, 0, h * 64:(h + 1) * 64], EW[:, h, :], True, True, [M.MKtokr[0], EWr], bkr)
        I("dve", "tensor_tensor", [SNTr, WCBr], [NNTr], out=NNT.rearrange("p (j h) -> p h j", h=4),
          in0=SNT.rearrange("p (j h) -> p h j", h=4), in1=WCB[0:64, :, :], op=ALU.mult)
        I("dve", "tensor_tensor", [NNTr, bkr], [NNTr], out=NNT.rearrange("p (j h) -> p h j", h=4),
          in0=NNT.rearrange("p (j h) -> p h j", h=4), in1=bk[0:64, 0:64].rearrange("p (h j) -> p h j", h=4), op=ALU.add)
        bk2, bk2r = bank()
        mm(bk2[0:64, 0:64], NNT, identf[0:64, 0:64], True, True, [NNTr, identfr], bk2r)
        I("act", "activation", [bk2r], [NNor], out=NNo, in_=bk2[0:64, 0:64], func=AF.Copy)
        P.dma("sp", ns_o.rearrange("j h k -> (j h) k"), NNo, NNor, reads=[NNor])
        for h in range(4):
            I("dve", "tensor_tensor", [M.MKtokr[0], E16r], [KJr], out=KJ[:, :, :],
              in0=M.MKtok[:, 0, h * 64:(h + 1) * 64].unsqueeze(1).broadcast_to([128, 16, 64]),
              in1=E16.unsqueeze(2).broadcast_to([128, 16, 64]), op=ALU.mult)
            for half in range(2):
                bk, bkr = bank()
                mm(bk[:, :], M.VW[:, h, 0:128], KJ[:, half * 8:(half + 1) * 8, :], True, True, [M.VWr[h], KJr], bkr)
                scv = SCf[:, :, :].rearrange("p (j h) k -> p h j k", h=4)[:, h, half * 8:(half + 1) * 8, :]
                I("dve", "tensor_tensor", [SCfr, WCBr], [SCfr], out=scv, in0=scv,
                  in1=WCB[:, h, half * 8:(half + 1) * 8].unsqueeze(2).broadcast_to([128, 8, 64]), op=ALU.mult)
                I("dve", "tensor_tensor", [SCfr, bkr], [SCfr], out=scv, in0=scv,
                  in1=bk[:, :].rearrange("p (j k) -> p j k", k=64), op=ALU.add)
        P.dma("sp", Cs_o.rearrange("j h p k -> p (j h) k"), SCf[:, :, :], SCfr, reads=[SCfr])

        def dump_y(tiles):
            yo = yp_o.rearrange("(t p) d -> t p d", p=128)
            for t in tiles:
                if Yr[t].last_w is None:
                    continue
                if t < NTP:
                    P.dma("sp", yo[t], Y[:, t, :], Yr[t], reads=[Yr[t]])
                else:
                    P.dma("sp", ys_o, Y[:, t, :], Yr[t], reads=[Yr[t]])

        if stage <= 1:
            dump_y(range(NT))
            P.emit()
            return nc, P

        new_phase()
        WCQ = A([8, 256], BF16); WCQr = AR("WCQ")
        WCKV = A([8, 512], BF16); WCKVr = AR("WCKV")
        WCO = A([2, 1024], BF16); WCOr = AR("WCO")
        P.dma("pool", WCQ[:], w_cq_d.rearrange("(k p) n -> p k n", p=128), WCQr, writes=[WCQr])
        P.dma("pool", WCKV[:, :, 0:256], w_ck_d.rearrange("(k p) n -> p k n", p=128), WCKVr, writes=[WCKVr], group=True)
        P.dma("pool", WCKV[:, :, 256:512], w_cv_d.rearrange("(k p) n -> p k n", p=128), WCKVr, writes=[WCKVr], group=True)
        P.dma("pool", WCO[:], w_co_d.rearrange("(c p) n -> p c n", p=128), WCOr, writes=[WCOr])
        MEMX = A([2, D], F32); MEMXr = [AR("MEMX0"), AR("MEMX1")]
        MNT = A([8, 256], BF16); MNTr = [AR("MNT0"), AR("MNT1")]
        MKTm = A([4, 256], BF16, parts=64); MKTmr = AR("MKTm")
        MVm = A([2, 256], BF16); MVmr = AR("MVm")
        MKVo = A([2, 512], F32); MKVor = [AR("MKVo0"), AR("MKVo1")]
        GB = 4
        XNTb = A([8, GB * 128], BF16); XNTbr = [AR(f"XNTb{i}") for i in range(GB)]
        QcT = A([4, GB * 128], BF16, parts=64); QcTr = AR("QcT")
        OcT = A([2, GB * 128], BF16); OcTr = [AR(f"OcT{i}") for i in range(GB)]
        Eb2s = [A([4, 256], BF16) for _ in range(2)]; Eb2rs = [AR("Eb2a"), AR("Eb2b")]
        PT2s = [A([1, 1024], BF16) for _ in range(2)]; PT2rs = [AR("PT2a"), AR("PT2b")]
        sm2s = [A([1, 32], F32)[:, 0, :] for _ in range(2)]; sm2rs = [AR("sm2a"), AR("sm2b")]

        mem_t = mem_d.rearrange("(t p) d -> t p d", p=128)
        import os
        SK = os.environ.get("SKIP", "")
        for mt in range(2):
            P.dma("sp", MEMX[:, mt, :], mem_t[mt], MEMXr[mt], writes=[MEMXr[mt]])
        for mt in (range(2) if "noBnorm" not in SK else []):
            norm_T(MEMX[:, mt, :], MEMXr[mt], 2, MNT[:, :, mt * 128:(mt + 1) * 128], [MNTr[mt]])
        for mt in (range(2) if "noBkv" not in SK else []):
            bk, bkr = bank()
            for k in range(8):
                mm(bk[:, :], MNT[:, k, mt * 128:(mt + 1) * 128], WCKV[:, k, :], k == 0, k == 7, [MNTr[mt], WCKVr], bkr)
            if "noBcp1" not in SK:
                I("act", "activation", [bkr], [MKVor[mt]], out=MKVo[:, mt, :], in_=bk[:, :], func=AF.Copy)
            if "noBcp2" not in SK:
                I("act", "activation", [bkr], [MVmr], out=MVm[:, mt, :], in_=bk[:, 256:512], func=AF.Copy)
            if "noBdma" not in SK:
                P.dma("sp", memk_o[mt * 128:(mt + 1) * 128, :], MKVo[:, mt, 0:256], MKVor[mt], reads=[MKVor[mt]], group=True)
                P.dma("sp", memv_o[mt * 128:(mt + 1) * 128, :], MKVo[:, mt, 256:512], MKVor[mt], reads=[MKVor[mt]], group=True)
        for h0 in ((0, 2) if "noBkt" not in SK else []):
            bk, bkr = bank()
            for hh in range(2):
                h = h0 + hh
                for k in range(8):
                    mm(bk[0:64, hh * 256:(hh + 1) * 256], WCKV[:, k, h * 64:(h + 1) * 64], MNT[:, k, :], k == 0, k == 7,
                       [WCKVr] + MNTr, bkr)
            I("act", "activation", [bkr], [MKTmr], out=MKTm[:, h0:h0 + 2, :],
              in_=bk[0:64, :].rearrange("p (a b) -> p a b", a=2), func=AF.Copy)

        def cross_q(ntok, xres):
            for h in range(4):
                bk, bkr = bank()
                for k in range(8):
                    mm(bk[0:64, 0:ntok], WCQ[:, k, h * 64:(h + 1) * 64], XNTb[:, k, 0:ntok], k == 0, k == 7, [WCQr] + xres, bkr)
                I("act", "activation", [bkr], [QcTr], out=QcT[:, h, 0:ntok], in_=bk[0:64, 0:ntok], func=AF.Copy)

        def cross_tile_prompt(ti, sx):
            Eb2, Eb2r, PT2, PT2r, sm2, sm2r = Eb2s[sx], Eb2rs[sx], PT2s[sx], PT2rs[sx], sm2s[sx], sm2rs[sx]
            tbx, tbxr = TSEL[sx]
            qs = slice(ti * 128, (ti + 1) * 128)
            bks = [bank(), bank()]
            for h in range(4):
                bk, bkr = bks[h // 2]
                mm(bk[:, (h % 2) * 256:(h % 2 + 1) * 256], QcT[:, h, qs], MKTm[:, h, :], True, True, [QcTr, MKTmr], bkr)
            for j in range(2):
                I("dve", "reduce_max", [bks[j][1]], [sm2r], out=sm2[:, 2 * j:2 * j + 2],
                  in_=bks[j][0][:, :].rearrange("p (a b) -> p a b", a=2), axis=AX.X)
            I("dve", "tensor_scalar", [sm2r], [sm2r], out=sm2[:, 0:4], in0=sm2[:, 0:4], scalar1=-0.125, scalar2=None, op0=ALU.mult)
            for h in range(4):
                bk, bkr = bks[h // 2]
                I("act", "activation", [bkr, sm2r], [Eb2r, sm2r], out=Eb2[:, h, :], in_=bk[:, (h % 2) * 256:(h % 2 + 1) * 256],
                  func=AF.Exp, bias=sm2[:, h:h + 1], scale=0.125, accum_out=sm2[:, 4 + h:5 + h])
            I("dve", "reciprocal", [sm2r], [sm2r], out=sm2[:, 8:12], in_=sm2[:, 4:8])
            for h in range(4):
                if h % 2 == 0:
                    I("act", "activation", [Eb2r, sm2r], [Eb2r], out=Eb2[:, h, :], in_=Eb2[:, h, :], func=AF.Copy,
                      scale=sm2[:, 8 + h:9 + h])
                else:
                    I("dve", "tensor_scalar", [Eb2r, sm2r], [Eb2r], out=Eb2[:, h, :], in0=Eb2[:, h, :],
                      scalar1=sm2[:, 8 + h:9 + h], scalar2=None, op0=ALU.mult)
            po, por = bank()
            for mc in range(2):
                for h in range(4):
                    I("pe", "transpose", [Eb2r, identr], [tbxr], out=tbx[:, h * 128:(h + 1) * 128],
                      in_=Eb2[:, h, mc * 128:(mc + 1) * 128], identity=identb[:])
                if mc == 0:
                    I("dve", "tensor_copy", [tbxr], [PT2r], out=PT2[:, 0, 0:512], in_=tbx)
                else:
                    I("act", "activation", [tbxr], [PT2r], out=PT2[:, 0, 512:1024], in_=tbx, func=AF.Copy)
            for h in range(4):
                for mc in range(2):
                    mm(po[(h % 2) * 64:(h % 2 + 1) * 64, (h // 2) * 128:(h // 2 + 1) * 128], MVm[:, mc, h * 64:(h + 1) * 64],
                       PT2[:, 0, (mc * 4 + h) * 128:(mc * 4 + h + 1) * 128], mc == 0, mc == 1, [MVmr, PT2r], por)
            I("act", "activation", [por], [OcTr[ti]], out=OcT[:, :, qs], in_=po[:, 0:256].rearrange("p (h q) -> p h q", h=2),
              func=AF.Copy)

        def wco_tile(ti, t):
            qs = slice(ti * 128, (ti + 1) * 128)
            for c in range(2):
                bk, bkr = bank()
                cc = slice(c * 512, (c + 1) * 512)
                for h in range(2):
                    mm(bk[:, :], OcT[:, h, qs], WCO[:, h, cc], h == 0, h == 1, [OcTr[ti], WCOr], bkr)
                I("dve", "tensor_tensor", [Yr[t], bkr], [Yr[t]], out=Y[:, t, cc], in0=Y[:, t, cc], in1=bk[:, :], op=ALU.add)

        import os
        for g0 in (range(0, NTP, GB) if "noBloop" not in os.environ.get("SKIP", "") else []):
            for tp_ in range(0, GB, 2):
                strs = []
                for sx in range(2):
                    ti = tp_ + sx
                    P.rec_begin()
                    norm_T(Y[:, g0 + ti, :], Yr[g0 + ti], 1, XNTb[:, :, ti * 128:(ti + 1) * 128], [XNTbr[ti]], half=sx,
                           tsel=TSEL[sx])
                    strs.append(P.rec_end())
                P.merge(strs)
            cross_q(GB * 128, XNTbr)
            for tp_ in range(0, GB, 2):
                strs = []
                for sx in range(2):
                    P.rec_begin(); bset[0] = [0, 1, 2] if sx == 0 else [3, 4, 5]
                    cross_tile_prompt(tp_ + sx, sx)
                    wco_tile(tp_ + sx, g0 + tp_ + sx)
                    strs.append(P.rec_end())
                P.merge(strs)
            bset[0] = [0, 1, 2, 3, 4]
        TS = NTP
        CMn = A([8, 2, 256], BF16); CMnr = AR("CMn")
        CMV = A([16, 2, 256], BF16); CMVr = AR("CMV")
        CMKT = A([8, 4, 256], BF16, parts=64); CMKTr = AR("CMKT")
        Es = A([2, 4, 256], BF16, parts=8); Esr = [AR("Es0"), AR("Es1")]
        sm3 = A([2, 16], F32, parts=8); sm3r = [AR("sm30"), AR("sm31")]
        PT3 = A([1, 1024], BF16)[:, 0, :]; PT3r = AR("PT3")
        P.dma("pool", CMV[:, :, :, :], cmv_d.rearrange("j (c p) f -> p j c f", p=128), CMVr, writes=[CMVr])
        norm_T(Y[:, TS, :], Yr[TS], 1, XNTb[:, :, 0:128], [XNTbr[0]])
        cross_q(128, [XNTbr[0]])
        po3, po3r = banks[5], bres[5]
        for half in range(2):
            P.dma("pool", CMn[:, :, :, :], cmk_d[half * 8:(half + 1) * 8].rearrange("j (c p) f -> p j c f", p=128), CMnr,
                  writes=[CMnr])
            for jj in range(8):
                for h in range(4):
                    for mc in range(2):
                        I("pe", "transpose", [CMnr, identr], [*tbhr], out=tb[0:64, (h * 2 + mc) * 128:(h * 2 + mc + 1) * 128],
                          in_=CMn[:, jj, mc, h * 64:(h + 1) * 64], identity=identb[:])
                I("act", "activation", [*tbhr], [CMKTr], out=CMKT[:, jj, :, :],
                  in_=tb[0:64, :].rearrange("p (h m) -> p h m", h=4), func=AF.Copy)
            strs = []
            for sx in range(2):
                P.rec_begin(); bset[0] = [0, 1] if sx == 0 else [2, 3]
                for jj in range(sx, 8, 2):
                    j = half * 8 + jj
                    b = sx
                    bks = [bank(), bank()]
                    for h in range(4):
                        bk, bkr = bks[h // 2]
                        mm(bk[0:8, (h % 2) * 256:(h % 2 + 1) * 256], QcT[:, h, 8 * j:8 * j + 8], CMKT[:, jj, h, :], True, True,
                           [QcTr, CMKTr], bkr)
                    st_ = sm3[:, b, :]
                    for q_ in range(2):
                        I("dve", "reduce_max", [bks[q_][1]], [sm3r[b]], out=st_[:, 2 * q_:2 * q_ + 2],
                          in_=bks[q_][0][0:8, :].rearrange("p (a b) -> p a b", a=2), axis=AX.X)
                    I("dve", "tensor_scalar", [sm3r[b]], [sm3r[b]], out=st_[:, 0:4], in0=st_[:, 0:4], scalar1=-0.125, scalar2=None,
                      op0=ALU.mult)
                    for h in range(4):
                        bk, bkr = bks[h // 2]
                        I("act", "activation", [bkr, sm3r[b]], [Esr[b], sm3r[b]], out=Es[:, b, h, :],
                          in_=bk[0:8, (h % 2) * 256:(h % 2 + 1) * 256], func=AF.Exp, bias=st_[:, h:h + 1], scale=0.125,
                          accum_out=st_[:, 4 + h:5 + h])
                    I("dve", "reciprocal", [sm3r[b]], [sm3r[b]], out=st_[:, 8:12], in_=st_[:, 4:8])
                    I("dve", "tensor_tensor", [Esr[b], sm3r[b]], [Esr[b]], out=Es[:, b, :, :], in0=Es[:, b, :, :],
                      in1=st_[:, 8:12].unsqueeze(2).broadcast_to([8, 4, 256]), op=ALU.mult)
                    for mc in range(2):
                        for h in range(4):
                            c0_ = j * 64 + (mc * 4 + h) * 8
                            I("pe", "transpose", [Esr[b], identr], [*tbhr], out=tb[:, c0_:c0_ + 8],
                              in_=Es[:, b, h, mc * 128:(mc + 1) * 128], identity=identb[0:8, 0:8])

                strs.append(P.rec_end())
            P.merge(strs)
            bset[0] = [0, 1, 2, 3, 4]
            I("dve", "tensor_copy", [*tbhr], [PT3r], out=PT3[:, half * 512:(half + 1) * 512], in_=tb[:, half * 512:(half + 1) * 512])
        for j in range(16):
            for h in range(4):
                for mc in range(2):
                    c0_ = j * 64 + (mc * 4 + h) * 8
                    mm(po3[(h % 2) * 64:(h % 2 + 1) * 64, (j * 2 + h // 2) * 8:(j * 2 + h // 2) * 8 + 8],
                       CMV[:, j, mc, h * 64:(h + 1) * 64], PT3[:, c0_:c0_ + 8], mc == 0, mc == 1, [CMVr, PT3r], po3r)
        I("act", "activation", [po3r], [OcTr[0]], out=OcT[:, :, 0:128].rearrange("p c (j i) -> p j c i", i=8),
          in_=po3[:, 0:256].rearrange("p (j c i) -> p j c i", c=2, i=8), func=AF.Copy)
        wco_tile(0, TS)

        if stage <= 2:
            dump_y(range(NT))
            P.emit()
            return nc, P

        new_phase()
        USE_SQRT[0] = True
        NF = FH // 128
        XNTa = A([8, NT * 128], BF16); XNTar = [AR(f"XNTa{t}") for t in range(NT)]
        NSLOT = 12
        WG = A([NSLOT, 8, 128], BF16); WU = A([NSLOT, 8, 128], BF16); WD = A([NSLOT, D], BF16)
        Wsr = [AR(f"Ws{s_}") for s_ in range(NSLOT)]
        Hh = A([6, 512], BF16); Hr = [AR(f"H{j}") for j in range(6)]
        SG = A([2, 512], BF16); SGr = [AR("SG0"), AR("SG1")]
        OUT = A([1, D], F32)[:, 0, :]; OUTr = AR("OUT")
        gfin = A([1, D], F32)[:, 0, :]; gfinr = AR("gfin")
        P.dma("sp", gfin, g_final_d.partition_broadcast(128), gfinr, writes=[gfinr])
        passes = [list(range(0, 6)), list(range(6, 12)), list(range(12, 17)), list(range(17, 22))]
        groups = [(0, 4), (4, 4), (8, 4), (12, 4), (16, 1)]
        wd_v = w_down_d.rearrange("(f p) n -> f p n", p=128)
        wslot = {}
        nload = [0]

        def load_w(f):
            s_ = nload[0] % NSLOT
            nload[0] += 1
            wslot[f] = s_
            P.dma("pool", WG[:, s_], w_gate_d[:, f * 128:(f + 1) * 128].rearrange("(k p) n -> p k n", p=128), Wsr[s_],
                  writes=[Wsr[s_]], group=True)
            P.dma("pool", WU[:, s_], w_up_d[:, f * 128:(f + 1) * 128].rearrange("(k p) n -> p k n", p=128), Wsr[s_],
                  writes=[Wsr[s_]], group=True)
            P.dma("pool", WD[:, s_], wd_v[f], Wsr[s_], writes=[Wsr[s_]], group=True)

        for f in passes[0]:
            load_w(f)
        gcnt = [0]
        def ffn_norm_group(gi_):
            t0_, n_ = groups[gi_]
            for t in range(t0_, t0_ + n_):
                norm_T(Y[:, t, :], Yr[t], 3, XNTa[:, :, t * 128:(t + 1) * 128], [XNTar[t]], half=t % 2, tsel=TSEL[t % 2])

        ffn_norm_group(0)
        for pi, fl in enumerate(passes):
            for gi, (t0, n) in enumerate(groups):
                if pi + 1 < len(passes) and gi == 0:
                    for f in passes[pi + 1]:
                        load_w(f)
                merging = (pi == 0 and gi + 1 < len(groups))
                if merging:
                    P.rec_begin()
                    ffn_norm_group(gi + 1)
                    s_norm = P.rec_end()
                    P.rec_begin()
                ntok = n * 128
                tok = slice(t0 * 128, t0 * 128 + ntok)
                xr = [XNTar[t] for t in range(t0, t0 + n)]
                for j, f in enumerate(fl):
                    s_ = wslot[f]
                    b = gcnt[0] % 2
                    gcnt[0] += 1
                    pg, pgr = bank()
                    pu, pur = bank()
                    for k in range(8):
                        mm(pg[:, 0:ntok], WG[:, s_, k, :], XNTa[:, k, tok], k == 0, k == 7, [Wsr[s_]] + xr, pgr)
                    for k in range(8):
                        mm(pu[:, 0:ntok], WU[:, s_, k, :], XNTa[:, k, tok], k == 0, k == 7, [Wsr[s_]] + xr, pur)
                    I("act", "activation", [pgr], [SGr[b]], out=SG[:, b, 0:ntok], in_=pg[:, 0:ntok], func=AF.Silu)
                    I("dve", "tensor_tensor", [SGr[b], pur], [Hr[j]], out=Hh[:, j, 0:ntok], in0=SG[:, b, 0:ntok],
                      in1=pu[:, 0:ntok], op=ALU.mult)
                for ti in range(n):
                    t = t0 + ti
                    for c in range(2):
                        pd, pdr = bank()
                        for j, f in enumerate(fl):
                            s_ = wslot[f]
                            mm(pd[:, :], Hh[:, j, ti * 128:(ti + 1) * 128], WD[:, s_, c * 512:(c + 1) * 512], j == 0,
                               j == len(fl) - 1, [Hr[j], Wsr[s_]], pdr)
                        I("dve", "tensor_tensor", [Yr[t], pdr], [Yr[t]], out=Y[:, t, c * 512:(c + 1) * 512],
                          in0=Y[:, t, c * 512:(c + 1) * 512], in1=pd[:, :], op=ALU.add)
                if merging:
                    s_ffn = P.rec_end()
                    P.merge([s_ffn, s_norm])
                if pi == len(passes) - 1:
                    yo = yp_o.rearrange("(t p) d -> t p d", p=128)
                    for t in range(t0, t0 + n):
                        rstd, sr = norm_stats(Y[:, t, :], Yr[t], 0)
                        I("dve", "scalar_tensor_tensor", [Yr[t], sr, gfinr], [OUTr], out=OUT, in0=Y[:, t, :], scalar=rstd,
                          in1=gfin, op0=ALU.mult, op1=ALU.mult)
                        P.dma("sp", (yo[t] if t < NTP else ys_o), OUT, OUTr, reads=[OUTr])
        P.emit()
        return nc, P


def make_consts(hf):
    c = {}
    c["c_ident"] = np.eye(128, dtype=np.float32)
    i = np.arange(128)[:, None]; j = np.arange(256)[None, :]
    band = np.where((j >= i) & (j <= i + 128), 0.0, NEG).astype(np.float32)
    first = band.copy()
    if hf == 0:
        first[:, :128] = NEG
    c["c_mb_band"] = band; c["c_mb_first"] = first
    s = np.arange(128)[:, None]; t = np.arange(128)[None, :]
    c["c_mb_caus"] = np.where(s <= t, 0.0, NEG).astype(np.float32)
    c["c_mb_causs"] = np.where((s <= t) & (s // 8 == t // 8), 0.0, NEG).astype(np.float32)
    sel = np.zeros((4, 1024), np.float32)
    for h in range(4):
        sel[h, h * 128:(h + 1) * 128] = 1.0
        sel[h, 512 + h * 128:512 + (h + 1) * 128] = -1.0
    c["c_sel"] = sel
    pm = np.zeros((4, 2), np.float32)
    pm[:, 0] = 1.0 if hf else 0.0
    pm[:, 1] = 0.0 if hf else NEG
    c["c_pmask"] = pm
    r = np.arange(32)[:, None] % 8
    p = np.arange(128)[None, :]
    c["c_smc"] = np.where(p >= r, 0.0, NEG).astype(np.float32)
    smn = np.full((32, 16, 128), NEG, np.float32)
    for jq in range(16):
        for ii in range(8):
            smn[(np.arange(32) % 8) >= ii, jq, jq * 8 + ii] = 0.0
    c["c_smn"] = smn
    c["c_bt"] = np.where((s <= t) & (s // 8 == t // 8), 1.0, 0.0).astype(np.float32)
    e = np.zeros((128, 16), np.float32); e[np.arange(128), np.arange(128) // 8] = 1.0
    c["c_eseq"] = e
    return c

def shard_inputs(inp):
    maps = []
    W = ["w_in", "b_igate", "b_fgate", "attn_sinks", "g_mlstm_head", "w_out", "g_mix", "g_cross", "g_mem",
         "w_cq", "w_ck", "w_cv", "w_co", "g_ffn", "w_gate", "w_up", "w_down"]
    wd = {k: np.ascontiguousarray(np.asarray(inp[k], np.float32)[0]) for k in W}
    wd["g_final"] = np.ascontiguousarray(np.asarray(inp["g_final"], np.float32))
    xp = np.asarray(inp["x_prompt"], np.float32); xs = np.asarray(inp["x_sample"], np.float32)
    for c in range(8):
        b, hf = c // 2, c % 2
        m = dict(wd)
        m["xp"] = np.ascontiguousarray(xp[b, hf * 2048:(hf + 1) * 2048])
        m["xpre"] = np.ascontiguousarray(xp[b, 0:2048]) if hf else np.zeros((2048, 1024), np.float32)
        m["xs"] = np.ascontiguousarray(xs[16 * c:16 * c + 16].reshape(128, 1024))
        m["mem"] = np.ascontiguousarray(np.asarray(inp["mem_prompt"], np.float32)[b])
        sl = slice(16 * c, 16 * c + 16)
        m["csk"] = np.ascontiguousarray(np.asarray(inp["cache_swa_k"], np.float32)[0, sl].reshape(16, 128, 128))
        m["csv"] = np.ascontiguousarray(np.asarray(inp["cache_swa_v"], np.float32)[0, sl].reshape(16, 128, 128))
        m["sC"] = np.ascontiguousarray(np.asarray(inp["state_mlstm_C"], np.float32)[0, sl])
        m["sn"] = np.ascontiguousarray(np.asarray(inp["state_mlstm_n"], np.float32)[0, sl])
        m["sm"] = np.ascontiguousarray(np.asarray(inp["state_mlstm_m"], np.float32)[0, sl])
        m["cmk"] = np.ascontiguousarray(np.asarray(inp["cache_mem_k"], np.float32)[0, sl].reshape(16, 256, 256))
        m["cmv"] = np.ascontiguousarray(np.asarray(inp["cache_mem_v"], np.float32)[0, sl].reshape(16, 256, 256))
        m.update(make_consts(hf))
        sk = wd["attn_sinks"]
        sc = np.zeros((32, 2), np.float32)
        rr = np.arange(32)
        for h in range(2):
            sc[:, h] = sk[4 * h + 2 * ((rr % 16) // 8) + rr // 16]
        m["c_sinkcol"] = sc
        maps.append(m)
    return maps

def gather(res):
    f = np.float32
    yp = np.zeros((4, 4096, 1024), f); ys = np.zeros((128, 8, 1024), f)
    skp = np.zeros((1, 4, 128, 2, 64), f); svp = np.zeros_like(skp)
    Cp = np.zeros((1, 4, 4, 128, 64), f); npp = np.zeros((1, 4, 4, 64), f); mp = np.zeros((1, 4, 4), f)
    mkp = np.zeros((1, 4, 256, 4, 64), f); mvp = np.zeros_like(mkp)
    sks = np.zeros((1, 128, 128, 2, 64), f); svs = np.zeros_like(sks)
    Cs = np.zeros((1, 128, 4, 128, 64), f); ns = np.zeros((1, 128, 4, 64), f); ms = np.zeros((1, 128, 4), f)
    for c in range(8):
        r = res[c]; b, hf = c // 2, c % 2
        yp[b, hf * 2048:(hf + 1) * 2048] = r["yp"]
        ys[16 * c:16 * c + 16] = r["ys"].reshape(16, 8, 1024)
        if hf == 1:
            skp[0, b] = r["swak"].reshape(128, 2, 64); svp[0, b] = r["swav"].reshape(128, 2, 64)
            Cp[0, b] = r["Cp"]; npp[0, b] = r["np"]; mp[0, b] = r["mp"].reshape(4)
        else:
            mkp[0, b] = r["memk"].reshape(256, 4, 64); mvp[0, b] = r["memv"].reshape(256, 4, 64)
        sl = slice(16 * c, 16 * c + 16)
        sks[0, sl] = r["sks"].reshape(16, 128, 2, 64); svs[0, sl] = r["svs"].reshape(16, 128, 2, 64)
        Cs[0, sl] = r["Cs"]; ns[0, sl] = r["ns"]; ms[0, sl] = r["ms"]
    return (yp, ys, skp, svp, Cp, npp, mp, mkp, mvp, sks, svs, Cs, ns, ms)


_CACHE = {}


def kernel(**inputs):
    if "nc" not in _CACHE:
        _CACHE["nc"] = build_program(3)[0]
    nc = _CACHE["nc"]
    maps = shard_inputs(inputs)
    res = run_bass_kernel_spmd(nc, maps, core_ids=list(range(8)))
    return gather(res.results)
```
